# Optimizing a Trainium2 kernel written in Bass

```python
import jax
import jax.numpy as jnp
from jax import lax
import numpy as np

D_MODEL = 1024
BATCH = 32
SEQ = 256
DEPTH = 2
DEC_BATCH = 2
DEC_SEQ = 2048
PAST_LEN = 512

GRID_W = 64
Q_BLOCK = 128
ROPE_BASE = 10000.0
NORM_EPS = 1e-6
MASK_VALUE = -1e30
LOG_FLOOR = 1e-30
N_MOD = 6
N_BRANCH = 4
BRANCH_W = 512
D_FF = 4 * D_MODEL

HGRN_HEADS = 4
HGRN_DK = 128
HGRN_DV = 128
HGRN_CHUNK = 32

SWA_HEADS = 8
SWA_KV_HEADS = 2
SWA_HEAD_DIM = 64
SWA_WINDOW = 128

MLA_HEADS = 8
MLA_Q_RANK = 256
MLA_KV_RANK = 128
MLA_NOPE = 64
MLA_ROPE = 32
MLA_V = 64

GQA_HEADS = 8
GQA_KV_HEADS = 2
GQA_HEAD_DIM = 64

IN_SPLITS = (
    ('hgrn_q', HGRN_HEADS * HGRN_DK),
    ('hgrn_f_fwd', HGRN_HEADS * HGRN_DK),
    ('hgrn_f_bwd', HGRN_HEADS * HGRN_DK),
    ('hgrn_i', HGRN_HEADS * HGRN_DV),
    ('hgrn_g', HGRN_HEADS * HGRN_DV),
    ('swa_q', SWA_HEADS * SWA_HEAD_DIM),
    ('swa_k', SWA_KV_HEADS * SWA_HEAD_DIM),
    ('swa_v', SWA_KV_HEADS * SWA_HEAD_DIM),
    ('mla_cq', MLA_Q_RANK),
    ('mla_ckv', MLA_KV_RANK),
    ('mla_kr', MLA_ROPE),
    ('gqa_q', GQA_HEADS * GQA_HEAD_DIM),
    ('gqa_k', GQA_KV_HEADS * GQA_HEAD_DIM),
    ('gqa_v', GQA_KV_HEADS * GQA_HEAD_DIM),
    ('gates', N_BRANCH * D_MODEL),
)
D_IN = sum(w for _, w in IN_SPLITS)

kernel_name = 'hybrid_diffusion_prefix_trunk_step'


def _rmsnorm(x, g):
    xf = x.astype(jnp.float32)
    y = xf * lax.rsqrt(jnp.mean(xf * xf, axis=-1, keepdims=True) + NORM_EPS)
    return (y * g.astype(jnp.float32)).astype(x.dtype)


def _grid_positions(n_tokens):
    rows = n_tokens // GRID_W
    row = jnp.repeat(jnp.arange(rows, dtype=jnp.float32), GRID_W)
    col = jnp.tile(jnp.arange(GRID_W, dtype=jnp.float32), rows)
    return row, col


def _axial_rope(x, row, col):
    B, T, H, d = x.shape
    nf = d // 4
    inv = ROPE_BASE ** (-jnp.arange(nf, dtype=jnp.float32) / nf)
    ang = jnp.stack([row[:, None] * inv, col[:, None] * inv], axis=1)[:, None]
    cos, sin = jnp.cos(ang), jnp.sin(ang)
    xs = x.astype(jnp.float32).reshape(B, T, H, 2, 2, nf)
    x1, x2 = xs[..., 0, :], xs[..., 1, :]
    out = jnp.stack([x1 * cos - x2 * sin, x2 * cos + x1 * sin], axis=-2)
    return out.reshape(B, T, H, d).astype(x.dtype)


def _split_cols(u):
    parts, off = {}, 0
    for name, width in IN_SPLITS:
        parts[name] = u[..., off:off + width]
        off += width
    return parts


def _joint_attention(q, segments, scale, sink=None):
    n_kv, n_grp = q.shape[3], q.shape[4]
    scores = []
    for k, v, mask in segments:
        spec = 'bnqhgd,bnkhd->bnhgqk' if k.ndim == 5 else 'bnqhgd,bkhd->bnhgqk'
        s = jnp.einsum(spec, q, k).astype(jnp.float32) * scale
        if mask is not None:
            s = jnp.where(mask, s, MASK_VALUE)
        scores.append(s)
    s = jnp.concatenate(scores, axis=-1)
    m = jnp.max(s, axis=-1, keepdims=True)
    if sink is not None:
        sink_logit = sink.astype(jnp.float32).reshape(1, 1, n_kv, n_grp, 1, 1)
        m = jnp.maximum(m, sink_logit)
    e = jnp.exp(s - m)
    den = jnp.sum(e, axis=-1, keepdims=True)
    if sink is not None:
        den = den + jnp.exp(sink_logit - m)
    p = e / den
    out, off = None, 0
    for k, v, _ in segments:
        n_k = k.shape[-3]
        spec = 'bnhgqk,bnkhd->bnqhgd' if v.ndim == 5 else 'bnhgqk,bkhd->bnqhgd'
        o = jnp.einsum(spec, p[..., off:off + n_k].astype(v.dtype), v)
        out = o if out is None else out + o
        off += n_k
    return out


def _dense_blocked(q, segments, scale, sink=None):
    B, T = q.shape[:2]
    nb = T // Q_BLOCK
    qb = jnp.moveaxis(q.reshape(B, nb, Q_BLOCK, *q.shape[2:]), 1, 0)
    out = lax.map(lambda qi: _joint_attention(qi[:, None], segments, scale, sink)[:, 0], qb)
    return jnp.moveaxis(out, 0, 1).reshape(B, T, *out.shape[3:])


def _window_attention(q, k, v, ctx_k, ctx_v, scale, sink):
    B, T = q.shape[:2]
    nb = T // Q_BLOCK
    pad = ((0, 0), (Q_BLOCK, Q_BLOCK), (0, 0), (0, 0))

    def band(a):
        ab = jnp.pad(a, pad).reshape(B, nb + 2, Q_BLOCK, *a.shape[2:])
        return jnp.concatenate([ab[:, :-2], ab[:, 1:-1], ab[:, 2:]], axis=2)

    qpos = jnp.arange(T).reshape(nb, Q_BLOCK)
    kpos = (jnp.arange(nb)[:, None] - 1) * Q_BLOCK + jnp.arange(3 * Q_BLOCK)[None, :]
    valid = ((kpos >= 0) & (kpos < T))[:, None, :]
    mask = (jnp.abs(qpos[:, :, None] - kpos[:, None, :]) <= SWA_WINDOW) & valid
    mask = mask[None, :, None, None]
    qb = q.reshape(B, nb, Q_BLOCK, *q.shape[2:])
    out = _joint_attention(qb, [(ctx_k, ctx_v, None), (band(k), band(v), mask)], scale, sink)
    return out.reshape(B, T, *out.shape[3:])


def _lower_bounds(p):
    s = jax.nn.softmax(p.astype(jnp.float32), axis=0)
    return jnp.cumsum(s, axis=0) - s[0]


def _hgrn_forget(f_pre, lb):
    f_pre = f_pre.astype(jnp.float32)
    lb = lb.astype(jnp.float32)
    f = lb + (1.0 - lb) * jax.nn.sigmoid(f_pre)
    log_f = jnp.log(jnp.maximum(f, LOG_FLOOR))
    k = (1.0 - lb) * jax.nn.sigmoid(-f_pre)
    return k, log_f


def _gla_scan(q, k, v, log_f, s0):
    B, T, H, _ = q.shape
    C = HGRN_CHUNK
    n = T // C

    def chunks(a):
        return jnp.moveaxis(a.reshape(B, n, C, *a.shape[2:]), 1, 0)

    tri = jnp.tril(jnp.ones((C, C), dtype=bool))[None, :, :, None, None]

    def step(S, inp):
        qc, kc, vc, gc = inp
        b = jnp.cumsum(gc, axis=1)
        diff = jnp.where(tri, b[:, :, None] - b[:, None, :], MASK_VALUE)
        A = jnp.einsum('bthk,bshk,btshk->bhts', qc, kc, jnp.exp(diff))
        o = jnp.einsum('bhts,bshv->bthv', A, vc) + jnp.einsum('bthk,bhkv->bthv', qc * jnp.exp(b), S)
        b_last = b[:, -1]
        S = jnp.exp(b_last)[..., None] * S + jnp.einsum('bshk,bshv->bhkv', kc * jnp.exp(b_last[:, None] - b), vc)
        return S, o

    S, o = lax.scan(step, s0, (chunks(q), chunks(k), chunks(v), chunks(log_f)))
    return jnp.moveaxis(o, 0, 1).reshape(B, T, H, v.shape[-1]), S


def _hgrn_mixer(u, P, lb_f, lb_b, ctx):
    dtype = u['hgrn_q'].dtype
    B, T, _ = u['hgrn_q'].shape
    kshape = (B, T, HGRN_HEADS, HGRN_DK)
    vshape = (B, T, HGRN_HEADS, HGRN_DV)
    q = jax.nn.silu(u['hgrn_q'].astype(jnp.float32)).reshape(kshape)
    v = u['hgrn_i'].astype(jnp.float32).reshape(vshape)
    k_f, logf_f = _hgrn_forget(u['hgrn_f_fwd'].reshape(kshape), lb_f.reshape(HGRN_HEADS, HGRN_DK))
    k_b, logf_b = _hgrn_forget(u['hgrn_f_bwd'].reshape(kshape), lb_b.reshape(HGRN_HEADS, HGRN_DK))
    if ctx is None:
        s_f = jnp.zeros((B, HGRN_HEADS, HGRN_DK, HGRN_DV), jnp.float32)
        s_b = s_f
    else:
        s_f = ctx['hgrn'][:, 0].astype(jnp.float32)
        s_b = ctx['hgrn'][:, 1].astype(jnp.float32)
    o_f, fin_f = _gla_scan(q, k_f, v, logf_f, s_f)
    o_b, fin_b = _gla_scan(q[:, ::-1], k_b[:, ::-1], v[:, ::-1], logf_b[:, ::-1], s_b)
    o = _rmsnorm(o_f + o_b[:, ::-1], P['hgrn_norm'].reshape(HGRN_HEADS, HGRN_DV))
    o = o * jax.nn.silu(u['hgrn_g'].astype(jnp.float32).reshape(vshape))
    return o.reshape(B, T, HGRN_HEADS * HGRN_DV).astype(dtype), jnp.stack([fin_f, fin_b], axis=1)


def _swa_mixer(u, P, pos, ctx):
    B, T, _ = u['swa_q'].shape
    grp = SWA_HEADS // SWA_KV_HEADS
    q = u['swa_q'].reshape(B, T, SWA_HEADS, SWA_HEAD_DIM)
    k = u['swa_k'].reshape(B, T, SWA_KV_HEADS, SWA_HEAD_DIM)
    v = u['swa_v'].reshape(B, T, SWA_KV_HEADS, SWA_HEAD_DIM)
    scale = SWA_HEAD_DIM ** -0.5
    if ctx is None:
        o = _dense_blocked(q.reshape(B, T, SWA_KV_HEADS, grp, SWA_HEAD_DIM), [(k, v, None)], scale, P['swa_sink'])
    else:
        qr = _axial_rope(q, *pos).reshape(B, T, SWA_KV_HEADS, grp, SWA_HEAD_DIM)
        kr = _axial_rope(k, *pos)
        o = _window_attention(qr, kr, v, ctx['swa_k'], ctx['swa_v'], scale, P['swa_sink'])
    return o.reshape(B, T, SWA_HEADS * SWA_HEAD_DIM), (k, v)


def _mla_keys(c_kv, k_rope, w_ukv):
    B, T, _ = c_kv.shape
    kv = (c_kv @ w_ukv).reshape(B, T, MLA_HEADS, MLA_NOPE + MLA_V)
    k_r = jnp.broadcast_to(k_rope[:, :, None, :], (B, T, MLA_HEADS, MLA_ROPE)).astype(kv.dtype)
    return jnp.concatenate([kv[..., :MLA_NOPE], k_r], axis=-1), kv[..., MLA_NOPE:]


def _mla_mixer(u, P, pos, ctx):
    B, T, _ = u['mla_cq'].shape
    q = (_rmsnorm(u['mla_cq'], P['mla_q_norm']) @ P['mla_w_uq']).reshape(B, T, MLA_HEADS, MLA_NOPE + MLA_ROPE)
    q_nope, q_rope = q[..., :MLA_NOPE], q[..., MLA_NOPE:]
    c_kv = _rmsnorm(u['mla_ckv'], P['mla_kv_norm'])
    k_rope = u['mla_kr']
    if pos is not None:
        q_rope = _axial_rope(q_rope, *pos)
        k_rope_used = _axial_rope(k_rope[:, :, None, :], *pos)[:, :, 0]
    else:
        k_rope_used = k_rope
    q = jnp.concatenate([q_nope, q_rope], axis=-1)[:, :, :, None, :]
    k, v = _mla_keys(c_kv, k_rope_used, P['mla_w_ukv'])
    segments = [(k, v, None)]
    if ctx is not None:
        ck, cv = _mla_keys(ctx['mla_ckv'], ctx['mla_kr'], P['mla_w_ukv'])
        segments = [(ck, cv, None)] + segments
    o = _dense_blocked(q, segments, (MLA_NOPE + MLA_ROPE) ** -0.5)
    return o.reshape(B, T, MLA_HEADS * MLA_V), (c_kv, k_rope)


def _gqa_mixer(u, P, pos, ctx):
    B, T, _ = u['gqa_q'].shape
    grp = GQA_HEADS // GQA_KV_HEADS
    q = _rmsnorm(u['gqa_q'].reshape(B, T, GQA_HEADS, GQA_HEAD_DIM), P['gqa_q_norm'])
    k = _rmsnorm(u['gqa_k'].reshape(B, T, GQA_KV_HEADS, GQA_HEAD_DIM), P['gqa_k_norm'])
    v = u['gqa_v'].reshape(B, T, GQA_KV_HEADS, GQA_HEAD_DIM)
    scale = GQA_HEAD_DIM ** -0.5
    if ctx is None:
        segments = [(k, v, None)]
    else:
        q = _axial_rope(q, *pos)
        segments = [(ctx['gqa_k'], ctx['gqa_v'], None), (_axial_rope(k, *pos), v, None)]
    o = _dense_blocked(q.reshape(B, T, GQA_KV_HEADS, grp, GQA_HEAD_DIM), segments, scale)
    return o.reshape(B, T, GQA_HEADS * GQA_HEAD_DIM), (k, v)


def _merge(outs, gates, P):
    B, T, _ = gates.shape
    o = jnp.stack(outs, axis=2)
    br = jnp.einsum('btnw,nwd->btnd', o, P['w_branch'])
    g = jax.nn.sigmoid(gates.reshape(B, T, N_BRANCH, D_MODEL).astype(jnp.float32)).astype(br.dtype)
    return jnp.sum(br * g, axis=2) @ P['w_out']


def _layer(x, mod, P, lb_f, lb_b, pos, ctx):
    sh1, sc1, g1, sh2, sc2, g2 = jnp.split(mod.astype(x.dtype), N_MOD, axis=-1)
    h = _rmsnorm(x, P['norm_mix_pre']) * (1 + sc1) + sh1
    u = _split_cols(h @ P['w_in'])
    o_a, st_a = _hgrn_mixer(u, P, lb_f, lb_b, ctx)
    o_b, st_b = _swa_mixer(u, P, pos, ctx)
    o_c, st_c = _mla_mixer(u, P, pos, ctx)
    o_d, st_d = _gqa_mixer(u, P, pos, ctx)
    y = _merge((o_a, o_b, o_c, o_d), u['gates'], P)
    x = x + g1 * _rmsnorm(y, P['norm_mix_post'])
    h = _rmsnorm(x, P['norm_mlp_pre']) * (1 + sc2) + sh2
    y = jnp.square(jax.nn.relu(h @ P['w_mlp_in'])) @ P['w_mlp_out']
    x = x + g2 * _rmsnorm(y, P['norm_mlp_post'])
    return x, (st_a, st_b[0], st_b[1], st_c[0], st_c[1], st_d[0], st_d[1])


def setup_inputs(seed: int = 0) -> dict:
    key = jax.random.key(seed)
    ks = jax.random.split(key, 32)

    def nrm(i, shape, s):
        return s * jax.random.normal(ks[i], shape, jnp.float32)

    def gain(i, shape):
        return 1.0 + 0.05 * jax.random.normal(ks[i], shape, jnp.float32)

    kv_swa = (DEC_BATCH, DEPTH, PAST_LEN, SWA_KV_HEADS, SWA_HEAD_DIM)
    kv_gqa = (DEC_BATCH, DEPTH, PAST_LEN, GQA_KV_HEADS, GQA_HEAD_DIM)
    return {
        'x_prompt': nrm(0, (BATCH, SEQ, D_MODEL), 1.0),
        'x_sample': nrm(1, (DEC_BATCH, DEC_SEQ, D_MODEL), 1.0),
        'state_hgrn': nrm(2, (DEC_BATCH, DEPTH, 2, HGRN_HEADS, HGRN_DK, HGRN_DV), 0.5),
        'cache_swa_k': nrm(3, kv_swa, 1.0),
        'cache_swa_v': nrm(4, kv_swa, 1.0),
        'cache_mla_ckv': nrm(5, (DEC_BATCH, DEPTH, PAST_LEN, MLA_KV_RANK), 1.0),
        'cache_mla_kr': nrm(6, (DEC_BATCH, DEPTH, PAST_LEN, MLA_ROPE), 1.0),
        'cache_gqa_k': nrm(7, kv_gqa, 1.0),
        'cache_gqa_v': nrm(8, kv_gqa, 1.0),
        'c': nrm(9, (DEC_BATCH, D_MODEL), 1.0),
        'c_ctx': nrm(10, (D_MODEL,), 1.0),
        'w_ada': nrm(11, (DEPTH, D_MODEL, N_MOD * D_MODEL), 0.5 * D_MODEL ** -0.5),
        'b_ada': nrm(12, (DEPTH, N_MOD * D_MODEL), 0.01),
        'norm_mix_pre': gain(13, (DEPTH, D_MODEL)),
        'norm_mix_post': gain(14, (DEPTH, D_MODEL)),
        'norm_mlp_pre': gain(15, (DEPTH, D_MODEL)),
        'norm_mlp_post': gain(16, (DEPTH, D_MODEL)),
        'w_in': nrm(17, (DEPTH, D_MODEL, D_IN), D_MODEL ** -0.5),
        'hgrn_lb_fwd': nrm(18, (DEPTH, HGRN_HEADS * HGRN_DK), 0.5),
        'hgrn_lb_bwd': nrm(19, (DEPTH, HGRN_HEADS * HGRN_DK), 0.5),
        'hgrn_norm': gain(20, (DEPTH, HGRN_HEADS * HGRN_DV)),
        'swa_sink': nrm(21, (DEPTH, SWA_HEADS), 0.5),
        'mla_q_norm': gain(22, (DEPTH, MLA_Q_RANK)),
        'mla_kv_norm': gain(23, (DEPTH, MLA_KV_RANK)),
        'mla_w_uq': nrm(24, (DEPTH, MLA_Q_RANK, MLA_HEADS * (MLA_NOPE + MLA_ROPE)), MLA_Q_RANK ** -0.5),
        'mla_w_ukv': nrm(25, (DEPTH, MLA_KV_RANK, MLA_HEADS * (MLA_NOPE + MLA_V)), MLA_KV_RANK ** -0.5),
        'gqa_q_norm': gain(26, (DEPTH, GQA_HEAD_DIM)),
        'gqa_k_norm': gain(27, (DEPTH, GQA_HEAD_DIM)),
        'w_branch': nrm(28, (DEPTH, N_BRANCH, BRANCH_W, D_MODEL), BRANCH_W ** -0.5),
        'w_out': nrm(29, (DEPTH, D_MODEL, D_MODEL), D_MODEL ** -0.5),
        'w_mlp_in': nrm(30, (DEPTH, D_MODEL, D_FF), D_MODEL ** -0.5),
        'w_mlp_out': nrm(31, (DEPTH, D_FF, D_MODEL), D_FF ** -0.5),
    }


def reference(x_prompt, x_sample, state_hgrn, cache_swa_k, cache_swa_v, cache_mla_ckv, cache_mla_kr,
              cache_gqa_k, cache_gqa_v, c, c_ctx, w_ada, b_ada, norm_mix_pre, norm_mix_post, norm_mlp_pre,
              norm_mlp_post, w_in, hgrn_lb_fwd, hgrn_lb_bwd, hgrn_norm, swa_sink, mla_q_norm, mla_kv_norm,
              mla_w_uq, mla_w_ukv, gqa_q_norm, gqa_k_norm, w_branch, w_out, w_mlp_in, w_mlp_out):
    lb_fwd = _lower_bounds(hgrn_lb_fwd)
    lb_bwd = _lower_bounds(hgrn_lb_bwd)

    def layer_params(l):
        return dict(
            w_in=w_in[l], norm_mix_pre=norm_mix_pre[l], norm_mix_post=norm_mix_post[l],
            norm_mlp_pre=norm_mlp_pre[l], norm_mlp_post=norm_mlp_post[l], hgrn_norm=hgrn_norm[l],
            swa_sink=swa_sink[l], mla_q_norm=mla_q_norm[l], mla_kv_norm=mla_kv_norm[l],
            mla_w_uq=mla_w_uq[l], mla_w_ukv=mla_w_ukv[l], gqa_q_norm=gqa_q_norm[l],
            gqa_k_norm=gqa_k_norm[l], w_branch=w_branch[l], w_out=w_out[l],
            w_mlp_in=w_mlp_in[l], w_mlp_out=w_mlp_out[l])

    y_prompt = x_prompt
    ctx_per_layer = []
    for l in range(DEPTH):
        mod = jax.nn.silu(c_ctx) @ w_ada[l] + b_ada[l]
        y_prompt, st = _layer(y_prompt, mod, layer_params(l), lb_fwd[l], lb_bwd[l], None, None)
        ctx_per_layer.append(st)
    stacked = [jnp.stack(t, axis=1) for t in zip(*ctx_per_layer)]
    (new_state_hgrn, new_cache_swa_k, new_cache_swa_v, new_cache_mla_ckv, new_cache_mla_kr,
     new_cache_gqa_k, new_cache_gqa_v) = stacked

    pos = _grid_positions(x_sample.shape[1])
    y_sample = x_sample
    for l in range(DEPTH):
        mod = (jax.nn.silu(c) @ w_ada[l] + b_ada[l])[:, None, :]
        ctx = dict(hgrn=state_hgrn[:, l], swa_k=cache_swa_k[:, l], swa_v=cache_swa_v[:, l],
                   mla_ckv=cache_mla_ckv[:, l], mla_kr=cache_mla_kr[:, l],
                   gqa_k=cache_gqa_k[:, l], gqa_v=cache_gqa_v[:, l])
        y_sample, _ = _layer(y_sample, mod, layer_params(l), lb_fwd[l], lb_bwd[l], pos, ctx)

    return (y_prompt, y_sample, new_state_hgrn, new_cache_swa_k, new_cache_swa_v, new_cache_mla_ckv,
            new_cache_mla_kr, new_cache_gqa_k, new_cache_gqa_v)
```

```python
import numpy as np
from contextlib import ExitStack
import ml_dtypes
import concourse.bass as bass
import concourse.mybir as mybir
from concourse.bass_utils import run_bass_kernel_spmd

F32 = mybir.dt.float32
BF16 = mybir.dt.bfloat16
AF = mybir.ActivationFunctionType
ALU = mybir.AluOpType

DM = 1024
DEPTH = 2
NCORE = 8
EPS = 1e-6
OFF = dict(hq=0, ff=512, fb=1024, hi=1536, hg=2048, sq=2560, sk=3072, sv=3200, cq=3328, ckv=3584, kr=3712,
           gq=3744, gk=4256, gv=4384, gates=4512)
D_IN = 8608
UTM_COLS = 1792


class Sched:
    NDMA = 24

    def __init__(self, nc, es):
        self.nc = nc
        self.eng = {"pe": nc.tensor, "act": nc.scalar, "dve": nc.vector, "pool": nc.gpsimd, "sp": nc.sync}
        self.sem = {k: es.enter_context(nc.semaphore("s_" + k)) for k in ("pe", "act", "dve", "pool")}
        self.cnt = {k: 0 for k in self.sem}
        self.dsem = [es.enter_context(nc.semaphore("d%d" % i)) for i in range(self.NDMA)]
        self.dcnt = [0] * self.NDMA
        self.dnext = 0
        self.seen = {k: {} for k in self.eng}
        self.lastw = {}
        self.reads = {}
        self.n_instr = 0

    def _sem_of(self, key):
        return self.sem[key] if isinstance(key, str) else self.dsem[key[1]]

    def _wait(self, e, tok):
        key, val = tok
        if self.seen[e].get(key, 0) >= val:
            return
        self.eng[e].wait_ge(self._sem_of(key), val)
        self.seen[e][key] = val

    def _deps(self, e, reads, writes, pe_acc=False):
        best = {}
        for r in reads:
            t = self.lastw.get(r)
            if t is not None and best.get(t[0], 0) < t[1]:
                best[t[0]] = t[1]
        for w in writes:
            t = self.lastw.get(w)
            if t is not None and best.get(t[0], 0) < t[1]:
                if not (pe_acc and t[0] == "pe"):
                    best[t[0]] = t[1]
            for t in self.reads.get(w, ()):
                if best.get(t[0], 0) < t[1]:
                    best[t[0]] = t[1]
        for key, val in best.items():
            self._wait(e, (key, val))

    def _record(self, tok, reads, writes):
        for r in reads:
            lst = self.reads.setdefault(r, [])
            lst[:] = [t for t in lst if t[0] != tok[0]]
            lst.append(tok)
        for w in writes:
            self.lastw[w] = tok
            self.reads[w] = []

    def op(self, e, fn, reads=(), writes=(), pe_acc=False):
        self._deps(e, reads, writes, pe_acc)
        ins = fn(self.eng[e])
        self.cnt[e] += 1
        ins.then_inc(self.sem[e], 1)
        self._record((e, self.cnt[e]), reads, writes)
        self.n_instr += 1
        return ins

    def dma(self, q, out, in_, reads=(), writes=()):
        i = self.dnext
        self.dnext = (self.dnext + 1) % self.NDMA
        if self.dcnt[i] > 0:
            self._wait(q, (("d", i), self.dcnt[i]))
        self._deps(q, reads, writes)
        ins = self.eng[q].dma_start(out=out, in_=in_)
        self.dcnt[i] += 16
        ins.then_inc(self.dsem[i], 16)
        self._record((("d", i), self.dcnt[i]), reads, writes)
        self.n_instr += 1
        return ins

    def barrier(self):
        best = {}
        for k in self.cnt:
            if self.cnt[k]:
                best[k] = self.cnt[k]
        for i in range(self.NDMA):
            if self.dcnt[i]:
                best[("d", i)] = self.dcnt[i]
        for e in self.eng:
            for key, val in best.items():
                self._wait(e, (key, val))
        self.lastw = {}
        self.reads = {}


class B:
    def __init__(self, nc, es):
        self.nc, self.es = nc, es
        self.S = Sched(nc, es)
        self.D = {}
        self.psn = 0

    def din(self, name, shape, dt=F32):
        self.D[name] = self.nc.dram_tensor(name, list(shape), dt, kind="ExternalInput").ap()
        return self.D[name]

    def dout(self, name, shape, dt=F32):
        self.D[name] = self.nc.dram_tensor(name, list(shape), dt, kind="ExternalOutput").ap()
        return self.D[name]

    def dscr(self, name, shape, dt=F32):
        self.D[name] = self.nc.dram_tensor(name, list(shape), dt, kind="Internal").ap()
        return self.D[name]

    def sb(self, st, name, shape, dt=F32):
        self.uid = getattr(self, "uid", 0) + 1
        return st.enter_context(self.nc.sbuf_tensor("sb%d_%s" % (self.uid, name), list(shape), dt))

    def ps(self):
        i = self.psn % 6
        self.psn += 1
        return self.psum[i], "ps%d" % i

    def acc(self, i):
        return self.psum[6 + i], "ps%d" % (6 + i)


def build_program():
    nc = bass.Bass("TRN2", target_bir_lowering=False)
    es = ExitStack()
    b = B(nc, es)
    S = b.S
    D = b.D
    GR = {
        "s": dict(T=2048, nseq=1, L=2048, P=512, rope=True, cond=1),
        "p": dict(T=1024, nseq=4, L=256, P=0, rope=False, cond=0),
    }
    for g, G in GR.items():
        T = G["T"]
        b.din("xT_" + g, [DM, T])
        b.dout("yT_" + g, [DM, T])
        b.dscr("X1_" + g, [DM, T])
        for l in range(DEPTH - 1):
            b.dscr("X2_%d_%s" % (l, g), [DM, T])
        b.dscr("UT_" + g, [68 * 128, T])
        b.dscr("UTM_" + g, [T, UTM_COLS])
        b.dscr("OT_" + g, [4, 512, T], BF16)
        b.dscr("YP_" + g, [DM, T], BF16)
    b.din("cT", [128, 8, 2])
    b.din("w_ada", [DEPTH, DM, 6 * DM])
    b.din("b_adaT", [DEPTH, 128, 48])
    b.din("gains", [DEPTH, 128, 4, 8])
    b.din("w_in", [DEPTH, DM, D_IN])
    b.din("lbT", [DEPTH, 2, 512])
    b.din("hgrn_normT", [DEPTH, 128, 4])
    b.din("sink", [DEPTH, 1, 8])
    b.din("mla_qnT", [DEPTH, 128, 2])
    b.din("mla_kvn", [DEPTH, 128, 1])
    b.din("w_uq", [DEPTH, 256, 768])
    b.din("w_uq_sw", [DEPTH, 256, 768])
    b.din("w_ukv", [DEPTH, 128, 1024])
    b.din("gqa_qn", [DEPTH, 64, 2])
    b.din("gqa_kn", [DEPTH, 64, 2])
    b.din("w_branch", [DEPTH, 4, 512, DM])
    b.din("w_out", [DEPTH, DM, DM])
    b.din("w_mlp_in", [DEPTH, DM, 4 * DM])
    b.din("w_mlp_out", [DEPTH, 4 * DM, DM])
    b.din("st_hgrn", [DEPTH, 2, 4, 128, 128])
    b.din("c_swa_kT", [DEPTH, 2, 64, 512])
    b.din("c_swa_v", [DEPTH, 512, 128])
    b.din("c_ckvT", [DEPTH, 128, 512])
    b.din("c_krT", [DEPTH, 32, 512])
    b.din("c_gqa_kT", [DEPTH, 2, 64, 512])
    b.din("c_gqa_v", [DEPTH, 512, 128])
    b.din("rope64", [4, 96, 2048])
    b.din("cmat", [6, 128, 128])
    b.din("swamask", [2, 128, 512], BF16)
    b.dout("o_hgrn", [DEPTH, 4, 2, 4, 128, 128])
    b.dout("o_swa_kT", [DEPTH, 128, 1024])
    b.dout("o_swa_v", [DEPTH, 1024, 128])
    b.dout("o_ckvT", [DEPTH, 128, 1024])
    b.dout("o_krT", [DEPTH, 32, 1024])
    b.dout("o_gqa_kT", [DEPTH, 128, 1024])
    b.dout("o_gqa_v", [DEPTH, 1024, 128])

    b.psum = [es.enter_context(nc.psum_tensor("psb%d" % i, [128, 512], F32)) for i in range(8)]

    cst = ExitStack()
    es.enter_context(cst)
    ones_bf = b.sb(cst, "ones_bf", [128, 128], BF16)
    ones_f = b.sb(cst, "ones_f", [128, 128], F32)
    cm = b.sb(cst, "cm", [128, 6, 128], F32)
    modT = b.sb(cst, "modT", [128, DEPTH, 48, 2], F32)
    gains = b.sb(cst, "gains", [128, DEPTH, 4, 8], F32)
    coef = b.sb(cst, "coef", [128, DEPTH, 2, 6, 8], F32)
    S.op("pool", lambda e: e.memset(ones_bf[:], 1.0), writes=["ones_bf"])
    S.op("pool", lambda e: e.memset(ones_f[:], 1.0), writes=["ones_f"])
    S.dma("sp", cm[:], D["cmat"].rearrange("a p c -> p a c"), writes=["cm"])
    S.dma("sp", gains[:], D["gains"].rearrange("l p a c -> p l a c"), writes=["gains"])

    with ExitStack() as st:
        cT = b.sb(st, "cT", [128, 8, 2])
        scT = b.sb(st, "scT", [128, 8, 2])
        badaT = b.sb(st, "badaT", [128, DEPTH, 48])
        wa = [b.sb(st, "wa%d" % i, [128, 8, 768]) for i in range(2)]
        S.dma("sp", cT[:], D["cT"], writes=["cT"])
        S.dma("sp", badaT[:], D["b_adaT"].rearrange("l p j -> p l j"), writes=["badaT"])
        S.op("act", lambda e: e.activation(scT[:], cT[:], AF.Silu), reads=["cT"], writes=["scT"])
        n = 0
        for l in range(DEPTH):
            pt, pn = b.ps()
            for cg in range(8):
                w = wa[n % 2]
                wn = "wa%d" % (n % 2)
                n += 1
                S.dma("sp" if cg % 2 == 0 else "act", w[:],
                      D["w_ada"][l, :, cg * 768:(cg + 1) * 768].rearrange("(kc p) c -> p kc c", p=128), writes=[wn])
                for jj in range(6):
                    j = cg * 6 + jj
                    for kc in range(8):
                        S.op("pe", lambda e: e.matmul(pt[:, 2 * j:2 * j + 2], w[:, kc, jj * 128:(jj + 1) * 128], scT[:, kc, :],
                                                      start=(kc == 0), stop=(kc == 7)),
                             reads=[wn, "scT"], writes=[pn], pe_acc=True)
            S.op("dve", lambda e: e.tensor_tensor(modT[:, l], pt[:, 0:96].rearrange("p (j c) -> p j c", c=2),
                                                  badaT[:, l].unsqueeze(2).to_broadcast([128, 48, 2]), ALU.add),
                 reads=[pn, "badaT"], writes=["modT"])
        for l in range(DEPTH):
            for c in range(2):
                m = lambda i: modT[:, l, i * 8:(i + 1) * 8, c]
                S.op("dve", lambda e: e.scalar_tensor_tensor(coef[:, l, c, 0], m(1), 1.0, gains[:, l, 0], ALU.add, ALU.mult),
                     reads=["modT", "gains"], writes=["coef"])
                S.op("dve", lambda e: e.tensor_copy(coef[:, l, c, 1], m(0)), reads=["modT"], writes=["coef"])
                S.op("dve", lambda e: e.tensor_tensor(coef[:, l, c, 2], m(2), gains[:, l, 1], ALU.mult), reads=["modT", "gains"], writes=["coef"])
                S.op("dve", lambda e: e.scalar_tensor_tensor(coef[:, l, c, 3], m(4), 1.0, gains[:, l, 2], ALU.add, ALU.mult),
                     reads=["modT", "gains"], writes=["coef"])
                S.op("dve", lambda e: e.tensor_copy(coef[:, l, c, 4], m(3)), reads=["modT"], writes=["coef"])
                S.op("dve", lambda e: e.tensor_tensor(coef[:, l, c, 5], m(5), gains[:, l, 3], ALU.mult), reads=["modT", "gains"], writes=["coef"])
        S.barrier()

    def rstd_from_sumsq(pt, pn, out, outn, npart, ncol, inv_n):
        S.op("act", lambda e: e.activation(out[0:npart, 0:ncol], pt[0:npart, 0:ncol], AF.Sqrt, bias=epsb[0:npart, :], scale=inv_n),
             reads=[pn, "epsb"], writes=[outn])
        S.op("dve", lambda e: e.reciprocal(out[0:npart, 0:ncol], out[0:npart, 0:ncol]), reads=[outn], writes=[outn])

    epsb = b.sb(cst, "epsb", [128, 1], F32)
    S.op("pool", lambda e: e.memset(epsb[:], EPS), writes=["epsb"])

    def norm_mod_tile(st_unused, xt, xn, hT, hn, t0, a_ap, sh_ap, tmp, sq, rs):
        for kc in range(8):
            S.op("act", lambda e: e.activation(sq[:, kc, :], xt[:, kc, :], AF.Square), reads=[xn], writes=["sq"])
        pt, pn = b.ps()
        for kc in range(8):
            S.op("pe", lambda e: e.matmul(pt[:], ones_bf[:], sq[:, kc, :], start=(kc == 0), stop=(kc == 7)),
                 reads=["sq", "ones_bf"], writes=[pn], pe_acc=True)
        rstd_from_sumsq(pt, pn, rs, "rs", 128, 512, 1.0 / DM)
        for kc in range(8):
            S.op("dve", lambda e: e.tensor_tensor(tmp[:], xt[:, kc, :], rs[:], ALU.mult), reads=[xn, "rs"], writes=["tmp"])
            S.op("act", lambda e: e.activation(hT[:, kc, t0:t0 + 512], tmp[:], AF.Identity, bias=sh_ap[:, kc:kc + 1], scale=a_ap[:, kc:kc + 1]),
                 reads=["tmp", "coef"], writes=[hn])

    def stage_h(st, g, l, Xsrc, ia, ish):
        G = GR[g]
        T = G["T"]
        hT = b.sb(st, "hT", [128, 8, T], BF16)
        with ExitStack() as s2:
            xts = [b.sb(s2, "xt%d" % i, [128, 8, 512]) for i in range(2)]
            tmp = b.sb(s2, "tmp", [128, 512])
            sq = b.sb(s2, "sq", [128, 8, 512], BF16)
            rs = b.sb(s2, "rs", [128, 512])
            for tt in range(T // 512):
                xt, xn = xts[tt % 2], "xt%d" % (tt % 2)
                S.dma("sp", xt[:], Xsrc[:, tt * 512:(tt + 1) * 512].rearrange("(kc p) t -> p kc t", p=128), reads=[Xsrc.name], writes=[xn])
                norm_mod_tile(None, xt, xn, hT, "hT", tt * 512, coef[:, l, G["cond"], ia], coef[:, l, G["cond"], ish], tmp, sq, rs)
            S.barrier()
        return hT

    def stage_proj(g, l, hT):
        G = GR[g]
        T = G["T"]
        UT, UTM = D["UT_" + g], D["UTM_" + g]
        tm_groups = {1: [(0, 512, 0)], 2: [(0, 512, 512)], 3: [(0, 512, 1024)], 6: [(128, 128, 1536)], 8: [(288, 128, 1664)]}
        with ExitStack() as st:
            wb = [b.sb(st, "wb%d" % i, [128, 8, 512], BF16) for i in range(2)]
            ev = [b.sb(st, "ev%d" % i, [128, 512]) for i in range(4)]
            nev = 0
            for cg in range(17):
                ncol = 512 if cg < 16 else D_IN - 8192
                w, wn = wb[cg % 2], "wb%d" % (cg % 2)
                S.dma("pool", w[:, :, 0:ncol], D["w_in"][l, :, cg * 512:cg * 512 + ncol].rearrange("(kc p) c -> p kc c", p=128), writes=[wn])
                for tt in range(T // 512):
                    for cc in range((ncol + 127) // 128):
                        m = min(128, ncol - cc * 128)
                        pt, pn = b.ps()
                        for kc in range(8):
                            S.op("pe", lambda e: e.matmul(pt[0:m, :], w[:, kc, cc * 128:cc * 128 + m], hT[:, kc, tt * 512:(tt + 1) * 512],
                                                          start=(kc == 0), stop=(kc == 7)), reads=[wn, "hT"], writes=[pn], pe_acc=True)
                        e_, en = ev[nev % 4], "ev%d" % (nev % 4)
                        eng = "act" if nev % 2 == 0 else "dve"
                        nev += 1
                        if eng == "act":
                            S.op("act", lambda e: e.copy(e_[0:m, :], pt[0:m, :]), reads=[pn], writes=[en])
                        else:
                            S.op("dve", lambda e: e.tensor_copy(e_[0:m, :], pt[0:m, :]), reads=[pn], writes=[en])
                        r0 = cg * 512 + cc * 128
                        S.dma("sp", UT[r0:r0 + m, tt * 512:(tt + 1) * 512], e_[0:m, :], reads=[en], writes=["UT"])
                for (c0, cn, dst) in tm_groups.get(cg, []):
                    for t4 in range(T // 128):
                        pt, pn = b.ps()
                        for kc in range(8):
                            S.op("pe", lambda e: e.matmul(pt[:, 0:cn], hT[:, kc, t4 * 128:(t4 + 1) * 128], w[:, kc, c0:c0 + cn],
                                                          start=(kc == 0), stop=(kc == 7)), reads=[wn, "hT"], writes=[pn], pe_acc=True)
                        e_, en = ev[nev % 4], "ev%d" % (nev % 4)
                        eng = "act" if nev % 2 == 0 else "dve"
                        nev += 1
                        if eng == "act":
                            S.op("act", lambda e: e.copy(e_[:, 0:cn], pt[:, 0:cn]), reads=[pn], writes=[en])
                        else:
                            S.op("dve", lambda e: e.tensor_copy(e_[:, 0:cn], pt[:, 0:cn]), reads=[pn], writes=[en])
                        S.dma("sp", UTM[t4 * 128:(t4 + 1) * 128, dst:dst + cn], e_[:, 0:cn], reads=[en], writes=["UTM"])
            S.barrier()

    def attn_unit(qT, qn, Kd, Nq, chunks, scale, o_out, on, wk, sink=None):
        po, pon = b.acc(0)
        pd, pdn = b.acc(1)
        nch = len(chunks)
        for i, (kT, kn, V, vn, nk, mask) in enumerate(chunks):
            pst, psn_ = b.ps()
            S.op("pe", lambda e: e.matmul(pst[0:nk, 0:Nq], kT, qT, start=True, stop=True), reads=[kn, qn], writes=[psn_])
            E, En = wk["E"][i % 3], "E%d" % (i % 3)
            S.op("act", lambda e: e.activation(E[0:nk, 0:Nq], pst[0:nk, 0:Nq], AF.Exp, scale=scale), reads=[psn_], writes=[En])
            if mask is not None:
                S.op("pool", lambda e: e.tensor_tensor(E[0:nk, 0:Nq], E[0:nk, 0:Nq], mask, ALU.mult), reads=[En, "swamask"], writes=[En])
            last = (i == nch - 1) and sink is None
            S.op("pe", lambda e: e.matmul(po[0:64, 0:Nq], V, E[0:nk, 0:Nq], start=(i == 0), stop=(i == nch - 1)),
                 reads=[vn, En], writes=[pon], pe_acc=True)
            S.op("pe", lambda e: e.matmul(pd[0:64, 0:Nq], ones_bf[0:nk, 0:64], E[0:nk, 0:Nq], start=(i == 0), stop=last),
                 reads=["ones_bf", En], writes=[pdn], pe_acc=True)
        if sink is not None:
            S.op("pe", lambda e: e.matmul(pd[0:64, 0:Nq], ones_bf[0:1, 0:64], sink, start=False, stop=True),
                 reads=["ones_bf", "sinkrow"], writes=[pdn], pe_acc=True)
        rec = wk["rec"]
        S.op("dve", lambda e: e.reciprocal(rec[0:64, 0:Nq], pd[0:64, 0:Nq]), reads=[pdn], writes=["rec"])
        S.op("dve", lambda e: e.tensor_tensor(o_out, po[0:64, 0:Nq], rec[0:64, 0:Nq], ALU.mult), reads=[pon, "rec"], writes=[on])

    def rms_rope_rows(st, src_rows_fn, n_rows, T, gain2, gainn, do_norm, do_rope, rope_idx, out_bf, outn, out32=None, out32n=None,
                      out32_pre_rope=False):
        x = b.sb(st, "rr_x", [64, T])
        xs = b.sb(st, "rr_xs", [64, T]) if do_rope else None
        sqb = b.sb(st, "rr_sq", [64, 512], BF16)
        rs = b.sb(st, "rr_rs", [64, T])
        t1 = b.sb(st, "rr_t1", [64, 512])
        for (r0, nr, ap) in src_rows_fn(False):
            S.dma("sp", x[r0:r0 + nr, :], ap, reads=["UT"], writes=["rr_x"])
        if do_rope:
            for (r0, nr, ap) in src_rows_fn(True):
                S.dma("act", xs[r0:r0 + nr, :], ap, reads=["UT"], writes=["rr_xs"])
        nr = n_rows
        W = min(512, T)
        for c in range(T // W):
            cs = slice(c * W, (c + 1) * W)
            if do_norm:
                S.op("act", lambda e: e.activation(sqb[0:nr, 0:W], x[0:nr, cs], AF.Square), reads=["rr_x"], writes=["rr_sq"])
                pt, pn = b.ps()
                S.op("pe", lambda e: e.matmul(pt[0:nr, 0:W], ones_bf[0:nr, 0:nr], sqb[0:nr, 0:W], start=True, stop=True),
                     reads=["rr_sq", "ones_bf"], writes=[pn])
                rstd_from_sumsq(pt, pn, rs[:, cs], "rr_rs", nr, W, 1.0 / nr)
                S.op("dve", lambda e: e.scalar_tensor_tensor(x[0:nr, cs], x[0:nr, cs], gain2[0:nr, 0:1], rs[0:nr, cs], ALU.mult, ALU.mult),
                     reads=["rr_x", "rr_rs", gainn], writes=["rr_x"])
                if do_rope:
                    S.op("dve", lambda e: e.scalar_tensor_tensor(xs[0:nr, cs], xs[0:nr, cs], gain2[0:nr, 1:2], rs[0:nr, cs], ALU.mult, ALU.mult),
                         reads=["rr_xs", "rr_rs", gainn], writes=["rr_xs"])
            if out32 is not None and out32_pre_rope:
                S.op("pool", lambda e: e.tensor_copy(out32[0:nr, cs], x[0:nr, cs]), reads=["rr_x"], writes=[out32n])
            if do_rope:
                S.op("dve", lambda e: e.tensor_tensor(x[0:nr, cs], x[0:nr, cs], rope[0:nr, rope_idx, cs], ALU.mult), reads=["rr_x", "rope"], writes=["rr_x"])
                S.op("pool", lambda e: e.tensor_tensor(t1[0:nr, 0:W], xs[0:nr, cs], rope[0:nr, rope_idx + 1, cs], ALU.mult), reads=["rr_xs", "rope"], writes=["rr_t1"])
                S.op("dve", lambda e: e.tensor_tensor(out_bf[0:nr, cs], x[0:nr, cs], t1[0:nr, 0:W], ALU.add), reads=["rr_x", "rr_t1"], writes=[outn])
            else:
                S.op("dve", lambda e: e.tensor_copy(out_bf[0:nr, cs], x[0:nr, cs]), reads=["rr_x"], writes=[outn])

    def swap_rows(base, T0, T1, UT, nd):
        q = nd // 4
        def f(swapped):
            if not swapped:
                return [(0, nd, UT[base:base + nd, T0:T1])]
            return [(0, q, UT[base + q:base + 2 * q, T0:T1]), (q, q, UT[base:base + q, T0:T1]),
                    (2 * q, q, UT[base + 3 * q:base + 4 * q, T0:T1]), (3 * q, q, UT[base + 2 * q:base + 3 * q, T0:T1])]
        return f

    rope = b.sb(cst, "rope", [96, 4, 2048], BF16)
    S.dma("pool", rope[:], D["rope64"].rearrange("a p t -> p a t"), writes=["rope"])

    def stage_gqa_like(g, l, kind):
        G = GR[g]
        T, L, P, nseq, do_rope = G["T"], G["L"], G["P"], G["nseq"], G["rope"]
        UT, UTM, OT = D["UT_" + g], D["UTM_" + g], D["OT_" + g]
        qo, ko = (OFF["sq"], OFF["sk"]) if kind == "swa" else (OFF["gq"], OFF["gk"])
        vcol = 1536 if kind == "swa" else 1664
        bi = 1 if kind == "swa" else 3
        do_norm = kind == "gqa"
        scale = 64 ** -0.5
        nkc_ctx = P // 128
        for sq_ in range(nseq):
            T0, T1 = sq_ * L, (sq_ + 1) * L
            with ExitStack() as st:
                kT = b.sb(st, "kT", [64, 2, P + L], BF16)
                Vt = b.sb(st, "Vt", [128, (P + L) // 128, 128], BF16)
                qTh = b.sb(st, "qTh", [64, 8, L], BF16)
                gq2 = b.sb(st, "gq2", [64, 2]); gk2 = b.sb(st, "gk2", [64, 2])
                sinkrow = b.sb(st, "sinkrow", [1, 8, 128], BF16)
                sk32 = b.sb(st, "sk32", [1, 8])
                if do_norm:
                    S.dma("sp", gq2[:], D["gqa_qn"][l], writes=["gq2"])
                    S.dma("sp", gk2[:], D["gqa_kn"][l], writes=["gk2"])
                else:
                    S.dma("sp", sk32[:], D["sink"][l], writes=["sk32"])
                    S.op("act", lambda e: e.activation(sk32[:], sk32[:], AF.Exp), reads=["sk32"], writes=["sk32"])
                    S.op("dve", lambda e: e.tensor_copy(sinkrow[:], sk32[:].unsqueeze(2).to_broadcast([1, 8, 128])), reads=["sk32"], writes=["sinkrow"])
                if P:
                    with ExitStack() as s2:
                        kc32 = b.sb(s2, "kc32", [64, 2, P]); vc32 = b.sb(s2, "vc32", [128, P // 128, 128])
                        src_k = D["c_swa_kT"] if kind == "swa" else D["c_gqa_kT"]
                        src_v = D["c_swa_v"] if kind == "swa" else D["c_gqa_v"]
                        S.dma("sp", kc32[:], src_k[l].rearrange("h d t -> d h t"), writes=["kc32"])
                        S.dma("sp", vc32[:], src_v[l].rearrange("(c p) f -> p c f", p=128), writes=["vc32"])
                        S.op("dve", lambda e: e.tensor_copy(kT[:, :, 0:P], kc32[:]), reads=["kc32"], writes=["kT"])
                        S.op("dve", lambda e: e.tensor_copy(Vt[:, 0:P // 128, :], vc32[:]), reads=["vc32"], writes=["Vt"])
                        S.barrier()
                for kvh in range(2):
                    with ExitStack() as s2:
                        k32 = b.sb(s2, "k32", [64, L]) if g == "p" else None
                        rms_rope_rows(s2, swap_rows(ko + kvh * 64, T0, T1, UT, 64), 64, L, gk2, "gk2", do_norm, do_rope, 0,
                                      kT[:, kvh, P:P + L], "kT", out32=k32, out32n="k32", out32_pre_rope=True)
                        if g == "p":
                            dst = D["o_swa_kT"] if kind == "swa" else D["o_gqa_kT"]
                            S.dma("sp", dst[l, kvh * 64:(kvh + 1) * 64, T0:T1], k32[:], reads=["k32"], writes=["okT"])
                        S.barrier()
                with ExitStack() as s2:
                    v32 = b.sb(s2, "v32", [128, L // 128, 128])
                    S.dma("sp", v32[:], UTM[T0:T1, vcol:vcol + 128].rearrange("(c p) f -> p c f", p=128), reads=["UTM"], writes=["v32"])
                    S.op("dve", lambda e: e.tensor_copy(Vt[:, P // 128:, :], v32[:]), reads=["v32"], writes=["Vt"])
                    if g == "p":
                        dst = D["o_swa_v"] if kind == "swa" else D["o_gqa_v"]
                        S.dma("act", dst[l, T0:T1, :].rearrange("(c p) f -> p c f", p=128), v32[:], reads=["v32"], writes=["ov"])
                    S.barrier()
                for h in range(8):
                    with ExitStack() as s2:
                        rms_rope_rows(s2, swap_rows(qo + h * 64, T0, T1, UT, 64), 64, L, gq2, "gq2", do_norm, do_rope, 0, qTh[:, h, :], "qTh")
                        S.barrier()
                with ExitStack() as s2:
                    wk = dict(E=[b.sb(s2, "E%d" % i, [128, 512], BF16) for i in range(3)], rec=b.sb(s2, "rec", [64, 512]))
                    obs = [b.sb(s2, "ob%d" % i, [64, 512], BF16) for i in range(2)]
                    swm = b.sb(s2, "swamask", [128, 2, 512], BF16)
                    S.dma("sp", swm[:], D["swamask"].rearrange("a p c -> p a c"), writes=["swamask"])
                    nu = 0
                    for kvh in range(2):
                        for qb in range(L // 128):
                            chunks = []
                            for c in range(nkc_ctx):
                                chunks.append((kT[:, kvh, c * 128:(c + 1) * 128], "kT", Vt[:, c, kvh * 64:(kvh + 1) * 64], "Vt", 128, None))
                            if kind == "swa" and P:
                                for (kb, mi) in ((qb - 1, 0), (qb, None), (qb + 1, 1)):
                                    if 0 <= kb < L // 128:
                                        chunks.append((kT[:, kvh, P + kb * 128:P + (kb + 1) * 128], "kT", Vt[:, nkc_ctx + kb, kvh * 64:(kvh + 1) * 64], "Vt",
                                                       128, None if mi is None else swm[:, mi, :]))
                            else:
                                for kb in range(L // 128):
                                    chunks.append((kT[:, kvh, P + kb * 128:P + (kb + 1) * 128], "kT", Vt[:, nkc_ctx + kb, kvh * 64:(kvh + 1) * 64], "Vt", 128, None))
                            ob, obn = obs[nu % 2], "ob%d" % (nu % 2)
                            nu += 1
                            attn_unit(qTh[:, kvh * 4:(kvh + 1) * 4, qb * 128:(qb + 1) * 128], "qTh", 64, 512, chunks, scale,
                                      ob[:], obn, wk, sink=(sinkrow[0:1, kvh * 4:(kvh + 1) * 4, :] if kind == "swa" else None))
                            S.dma("sp", OT[bi, kvh * 256:(kvh + 1) * 256, T0 + qb * 128:T0 + (qb + 1) * 128].rearrange("(h d) t -> d h t", d=64),
                                  ob[:].rearrange("d (h t) -> d h t", h=4), reads=[obn], writes=["OT"])
                    S.barrier()

    def stage_mla(g, l):
        G = GR[g]
        T, L, P, nseq, do_rope = G["T"], G["L"], G["P"], G["nseq"], G["rope"]
        UT, OT = D["UT_" + g], D["OT_" + g]
        scale = 96 ** -0.5
        NK = P + L
        for sq_ in range(nseq):
            T0, T1 = sq_ * L, (sq_ + 1) * L
            with ExitStack() as st:
                ckvT = b.sb(st, "ckvT", [128, NK], BF16)
                krT = b.sb(st, "krT", [96, NK], BF16)
                cqn = b.sb(st, "cqn", [128, 2, L], BF16)
                wuq = b.sb(st, "wuq", [128, 2, 768], BF16); wuqs = b.sb(st, "wuqs", [128, 2, 768], BF16)
                wukv = b.sb(st, "wukv", [128, 1024], BF16)
                g_q = b.sb(st, "g_q", [128, 2]); g_kv = b.sb(st, "g_kv", [128, 1])
                S.dma("pool", wuq[:], D["w_uq"][l].rearrange("(kc p) c -> p kc c", p=128), writes=["wuq"])
                S.dma("pool", wuqs[:], D["w_uq_sw"][l].rearrange("(kc p) c -> p kc c", p=128), writes=["wuqs"])
                S.dma("pool", wukv[:], D["w_ukv"][l], writes=["wukv"])
                S.dma("sp", g_q[:], D["mla_qnT"][l], writes=["g_q"])
                S.dma("sp", g_kv[:], D["mla_kvn"][l], writes=["g_kv"])
                with ExitStack() as s2:
                    x = b.sb(s2, "m_x", [128, 2, L]); sqb = b.sb(s2, "m_sq", [128, 2, 512], BF16); rs = b.sb(s2, "m_rs", [128, 512])
                    xk = b.sb(s2, "m_xk", [128, L]); kr32 = b.sb(s2, "m_kr", [96, L]); krs = b.sb(s2, "m_krs", [96, L]); t1 = b.sb(s2, "m_t1", [96, 512])
                    if P:
                        S.dma("pool", ckvT[:, 0:P], D["c_ckvT"][l], writes=["ckvT"])
                        S.dma("pool", krT[64:96, 0:P], D["c_krT"][l], writes=["krT"])
                    S.dma("sp", x[:], UT[OFF["cq"]:OFF["cq"] + 256, T0:T1].rearrange("(kc p) t -> p kc t", p=128), reads=["UT"], writes=["m_x"])
                    S.dma("sp", xk[:], UT[OFF["ckv"]:OFF["ckv"] + 128, T0:T1], reads=["UT"], writes=["m_xk"])
                    S.dma("sp", kr32[64:96, :], UT[OFF["kr"]:OFF["kr"] + 32, T0:T1], reads=["UT"], writes=["m_kr"])
                    if do_rope:
                        for (r0, nr, ap) in swap_rows(OFF["kr"], T0, T1, UT, 32)(True):
                            S.dma("act", krs[64 + r0:64 + r0 + nr, :], ap, reads=["UT"], writes=["m_krs"])
                    for c in range(L // 512 if L >= 512 else 1):
                        w = min(512, L)
                        cs = slice(c * w, (c + 1) * w)
                        pt, pn = b.ps()
                        for kc in range(2):
                            S.op("act", lambda e: e.activation(sqb[:, kc, 0:w], x[:, kc, cs], AF.Square), reads=["m_x"], writes=["m_sq"])
                        for kc in range(2):
                            S.op("pe", lambda e: e.matmul(pt[:, 0:w], ones_bf[:], sqb[:, kc, 0:w], start=(kc == 0), stop=(kc == 1)),
                                 reads=["m_sq", "ones_bf"], writes=[pn], pe_acc=True)
                        rstd_from_sumsq(pt, pn, rs, "m_rs", 128, w, 1.0 / 256)
                        for kc in range(2):
                            S.op("dve", lambda e: e.scalar_tensor_tensor(cqn[:, kc, cs], x[:, kc, cs], g_q[:, kc:kc + 1], rs[:, 0:w], ALU.mult, ALU.mult),
                                 reads=["m_x", "m_rs", "g_q"], writes=["cqn"])
                        pt, pn = b.ps()
                        S.op("act", lambda e: e.activation(sqb[:, 0, 0:w], xk[:, cs], AF.Square), reads=["m_xk"], writes=["m_sq"])
                        S.op("pe", lambda e: e.matmul(pt[:, 0:w], ones_bf[:], sqb[:, 0, 0:w], start=True, stop=True), reads=["m_sq", "ones_bf"], writes=[pn])
                        rstd_from_sumsq(pt, pn, rs, "m_rs", 128, w, 1.0 / 128)
                        S.op("dve", lambda e: e.scalar_tensor_tensor(xk[:, cs], xk[:, cs], g_kv[:, 0:1], rs[:, 0:w], ALU.mult, ALU.mult),
                             reads=["m_xk", "m_rs", "g_kv"], writes=["m_xk"])
                        S.op("pool", lambda e: e.tensor_copy(ckvT[:, P + c * w:P + (c + 1) * w], xk[:, cs]), reads=["m_xk"], writes=["ckvT"])
                        if do_rope:
                            S.op("dve", lambda e: e.tensor_tensor(t1[64:96, 0:w], kr32[64:96, cs], rope[64:96, 2, cs], ALU.mult), reads=["m_kr", "rope"], writes=["m_t1"])
                            S.op("pool", lambda e: e.tensor_tensor(krs[64:96, cs], krs[64:96, cs], rope[64:96, 3, cs], ALU.mult), reads=["m_krs", "rope"], writes=["m_krs"])
                            S.op("dve", lambda e: e.tensor_tensor(krT[64:96, P + c * w:P + (c + 1) * w], t1[64:96, 0:w], krs[64:96, cs], ALU.add),
                                 reads=["m_t1", "m_krs"], writes=["krT"])
                        else:
                            S.op("dve", lambda e: e.tensor_copy(krT[64:96, P + c * w:P + (c + 1) * w], kr32[64:96, cs]), reads=["m_kr"], writes=["krT"])
                    if g == "p":
                        S.dma("sp", D["o_ckvT"][l, :, T0:T1], xk[:], reads=["m_xk"], writes=["ockv"])
                        S.dma("sp", D["o_krT"][l, :, T0:T1], kr32[64:96, :], reads=["m_kr"], writes=["okr"])
                    S.barrier()
                Vall = b.sb(st, "Vall", [128, NK // 128, 512], BF16)
                for c in range(NK // 128):
                    pt, pn = b.ps()
                    S.op("pe", lambda e: e.matmul(pt[:, :], ckvT[:, c * 128:(c + 1) * 128],
                                                  wukv[:].rearrange("p (h x) -> p h x", x=128)[:, :, 64:128], start=True, stop=True),
                         reads=["ckvT", "wukv"], writes=[pn])
                    if c % 2 == 0:
                        S.op("act", lambda e: e.copy(Vall[:, c, :], pt[:, :]), reads=[pn], writes=["Vall"])
                    else:
                        S.op("dve", lambda e: e.tensor_copy(Vall[:, c, :], pt[:, :]), reads=[pn], writes=["Vall"])
                S.barrier()
                with ExitStack() as s2:
                    wk = dict(E=[b.sb(s2, "E%d" % i, [128, 512], BF16) for i in range(3)], rec=b.sb(s2, "rec", [64, 512]))
                    kTh = [b.sb(s2, "kTh%d" % i, [96, NK], BF16) for i in range(2)]
                    qh = [b.sb(s2, "qh%d" % i, [96, L], BF16) for i in range(2)]
                    obs = [b.sb(s2, "ob%d" % i, [64, 512], BF16) for i in range(2)]
                    t2 = b.sb(s2, "m_t2", [96, 512]); t3 = b.sb(s2, "m_t3", [96, 512])
                    nu = 0
                    W = min(512, L)
                    for h in range(8):
                        kt, ktn = kTh[h % 2], "kTh%d" % (h % 2)
                        q_, q_n = qh[h % 2], "qh%d" % (h % 2)
                        for c in range((NK + 511) // 512):
                            w = min(512, NK - c * 512)
                            pt, pn = b.ps()
                            S.op("pe", lambda e: e.matmul(pt[0:64, 0:w], wukv[:, h * 128:h * 128 + 64], ckvT[:, c * 512:c * 512 + w], start=True, stop=True),
                                 reads=["wukv", "ckvT"], writes=[pn])
                            S.op("act", lambda e: e.copy(kt[0:64, c * 512:c * 512 + w], pt[0:64, 0:w]), reads=[pn], writes=[ktn])
                        S.op("pool", lambda e: e.tensor_copy(kt[64:96, :], krT[64:96, :]), reads=["krT"], writes=[ktn])
                        for c in range(L // W):
                            cs = slice(c * W, (c + 1) * W)
                            pt, pn = b.ps()
                            for kc in range(2):
                                S.op("pe", lambda e: e.matmul(pt[0:96, 0:W], wuq[:, kc, h * 96:(h + 1) * 96], cqn[:, kc, cs], start=(kc == 0), stop=(kc == 1)),
                                     reads=["wuq", "cqn"], writes=[pn], pe_acc=True)
                            S.op("act", lambda e: e.copy(q_[0:64, cs], pt[0:64, 0:W]), reads=[pn], writes=[q_n])
                            if do_rope:
                                pt2, pn2 = b.ps()
                                for kc in range(2):
                                    S.op("pe", lambda e: e.matmul(pt2[0:96, 0:W], wuqs[:, kc, h * 96:(h + 1) * 96], cqn[:, kc, cs], start=(kc == 0), stop=(kc == 1)),
                                         reads=["wuqs", "cqn"], writes=[pn2], pe_acc=True)
                                S.op("dve", lambda e: e.tensor_tensor(t2[64:96, 0:W], pt[64:96, 0:W], rope[64:96, 2, cs], ALU.mult), reads=[pn, "rope"], writes=["m_t2"])
                                S.op("dve", lambda e: e.tensor_tensor(t3[64:96, 0:W], pt2[64:96, 0:W], rope[64:96, 3, cs], ALU.mult), reads=[pn2, "rope"], writes=["m_t3"])
                                S.op("pool", lambda e: e.tensor_tensor(q_[64:96, cs], t2[64:96, 0:W], t3[64:96, 0:W], ALU.add), reads=["m_t2", "m_t3"], writes=[q_n])
                            else:
                                S.op("dve", lambda e: e.tensor_copy(q_[64:96, cs], pt[64:96, 0:W]), reads=[pn], writes=[q_n])
                        for c in range(L // W):
                            chunks = [(kt[:, kc * 128:(kc + 1) * 128], ktn, Vall[:, kc, h * 64:(h + 1) * 64], "Vall", 128, None) for kc in range(NK // 128)]
                            ob, obn = obs[nu % 2], "ob%d" % (nu % 2)
                            nu += 1
                            attn_unit(q_[:, c * W:(c + 1) * W], q_n, 96, W, chunks, scale, ob[:, 0:W], obn, wk)
                            S.dma("sp", OT[2, h * 64:(h + 1) * 64, T0 + c * W:T0 + (c + 1) * W], ob[:, 0:W], reads=[obn], writes=["OT"])
                    S.barrier()

    def stage_hgrn(g, l):
        G = GR[g]
        T, L, P, nseq = G["T"], G["L"], G["P"], G["nseq"]
        UT, UTM, OT = D["UT_" + g], D["UTM_" + g], D["OT_" + g]
        NTL = L // 128
        ident, triU, triL, sL, sU, csel = (cm[:, i, :] for i in range(6))
        with ExitStack() as st:
            lbr = b.sb(st, "lbr", [128, DEPTH, 2, 512])
            lbb = b.sb(st, "lbb", [128, 2, 512]); oml = b.sb(st, "oml", [128, 2, 512]); den = b.sb(st, "lden", [128, 2, 512])
            S.dma("sp", lbr[:], D["lbT"].partition_broadcast(128), writes=["lbr"])
            S.op("act", lambda e: e.activation(lbr[:], lbr[:], AF.Exp), reads=["lbr"], writes=["lbr"])
            S.op("dve", lambda e: e.tensor_tensor(den[:], lbr[:, 0], lbr[:, 1], ALU.add), reads=["lbr"], writes=["lden"])
            S.op("dve", lambda e: e.reciprocal(den[:], den[:]), reads=["lden"], writes=["lden"])
            if l == 0:
                S.op("pool", lambda e: e.memset(lbb[:], 0.0), writes=["lbb"])
            else:
                S.op("dve", lambda e: e.tensor_tensor(lbb[:], lbr[:, 1], den[:], ALU.mult), reads=["lbr", "lden"], writes=["lbb"])
            S.op("dve", lambda e: e.tensor_scalar(oml[:], lbb[:], -1.0, 1.0, ALU.mult, ALU.add), reads=["lbb"], writes=["oml"])
            hn = b.sb(st, "hn", [128, 4]); S.dma("sp", hn[:], D["hgrn_normT"][l], writes=["hn"])
            Sst = b.sb(st, "Sst", [128, 2, 4, 128])
            SSTN = ["Sst"] + ["Sst%d%d" % (d_, h_) for d_ in range(2) for h_ in range(4)]
            for sq_ in range(nseq):
                T0 = sq_ * L
                if P:
                    S.dma("sp", Sst[:], D["st_hgrn"][l].rearrange("d h k v -> k d h v"), writes=SSTN)
                else:
                    S.op("pool", lambda e: e.memset(Sst[:], 0.0), writes=SSTN)
                for d in range(2):
                    order = range(NTL) if d == 0 else range(NTL - 1, -1, -1)
                    M_in, M_ex = (triU, sL) if d == 0 else (triL, sU)
                    for ti in order:
                        t0 = T0 + ti * 128
                        with ExitStack() as s2:
                            fpre = b.sb(s2, "fpre", [128, 512]); vv = b.sb(s2, "vv", [128, 512]); tt_ = b.sb(s2, "h_t", [128, 512])
                            gg = b.sb(s2, "h_g", [128, 512]); kk = b.sb(s2, "h_k", [128, 512]); kt_ = b.sb(s2, "h_kt", [128, 512]); kh = b.sb(s2, "h_kh", [128, 512])
                            khm = b.sb(s2, "h_khm", [128, 4, 128]); qT = b.sb(s2, "h_qT", [128, 4, 128]); qt = b.sb(s2, "h_qt", [128, 4, 128])
                            eb = b.sb(s2, "h_eb", [128, 132]); ktT = b.sb(s2, "h_ktT", [128, 128]); AT = b.sb(s2, "h_AT", [128, 128])
                            ofb = b.sb(s2, "h_of", [128, 4, 128], BF16)
                            osum = b.sb(s2, "h_os", [128, 4, 128]); gT = b.sb(s2, "h_gT", [128, 4, 128])
                            osq = b.sb(s2, "h_osq", [128, 4, 128], BF16); ors = b.sb(s2, "h_ors", [128, 512])
                            if d == 1:
                                S.dma("act", ofb[:], OT[0, :, t0:t0 + 128].rearrange("(h p) t -> p h t", p=128), reads=["OTa"], writes=["h_of"])
                                S.dma("act", gT[:], UT[OFF["hg"]:OFF["hg"] + 512, t0:t0 + 128].rearrange("(h p) t -> p h t", p=128), reads=["UT"], writes=["h_gT"])
                                S.op("act", lambda e: e.activation(gT[:], gT[:], AF.Silu), reads=["h_gT"], writes=["h_gT"])
                            S.dma("sp", fpre[:], UTM[t0:t0 + 128, d * 512:(d + 1) * 512], reads=["UTM"], writes=["fpre"])
                            S.dma("sp", vv[:], UTM[t0:t0 + 128, 1024:1536], reads=["UTM"], writes=["vv"])
                            S.dma("act", qT[:], UT[0:512, t0:t0 + 128].rearrange("(h p) t -> p h t", p=128), reads=["UT"], writes=["h_qT"])
                            S.op("act", lambda e: e.activation(qT[:], qT[:], AF.Silu), reads=["h_qT"], writes=["h_qT"])
                            S.op("act", lambda e: e.activation(tt_[:], fpre[:], AF.Sigmoid), reads=["fpre"], writes=["h_t"])
                            S.op("dve", lambda e: e.tensor_tensor(tt_[:], tt_[:], oml[:, d], ALU.mult), reads=["h_t", "oml"], writes=["h_t"])
                            S.op("dve", lambda e: e.scalar_tensor_tensor(gg[:], tt_[:], 1e-30, lbb[:, d], ALU.max, ALU.add), reads=["h_t", "lbb"], writes=["h_g"])
                            S.op("act", lambda e: e.activation(gg[:], gg[:], AF.Ln), reads=["h_g"], writes=["h_g"])
                            S.op("dve", lambda e: e.tensor_tensor(kk[:], oml[:, d], tt_[:], ALU.subtract), reads=["h_t", "oml"], writes=["h_k"])
                            pb, pbn = b.ps()
                            S.op("pe", lambda e: e.matmul(pb[:], M_in, gg[:], start=True, stop=True), reads=["cm", "h_g"], writes=[pbn])
                            S.op("act", lambda e: e.activation(kt_[:], pb[:], AF.Exp, scale=-1.0), reads=[pbn], writes=["h_kt"])
                            S.op("dve", lambda e: e.tensor_tensor(kt_[:], kt_[:], kk[:], ALU.mult), reads=["h_kt", "h_k"], writes=["h_kt"])
                            pr, prn = b.ps()
                            S.op("pe", lambda e: e.matmul(pr[:], M_ex, gg[:], start=True, stop=True), reads=["cm", "h_g"], writes=[prn])
                            S.op("act", lambda e: e.activation(kh[:], pr[:], AF.Exp), reads=[prn], writes=["h_kh"])
                            S.op("dve", lambda e: e.tensor_tensor(kh[:], kh[:], kk[:], ALU.mult), reads=["h_kh", "h_k"], writes=["h_kh"])
                            for h in range(4):
                                hs = slice(h * 128, (h + 1) * 128)
                                pbt, pbtn = b.ps()
                                S.op("pe", lambda e: e.matmul(pbt[:, 0:128], gg[:, hs], M_in, start=True, stop=True), reads=["h_g", "cm"], writes=[pbtn])
                                S.op("pe", lambda e: e.matmul(pbt[:, 128:132], gg[:, hs], csel[:, 0:4], start=True, stop=True), reads=["h_g", "cm"], writes=[pbtn])
                                S.op("act", lambda e: e.activation(eb[:], pbt[:, 0:132], AF.Exp), reads=[pbtn], writes=["h_eb"])
                                S.op("dve", lambda e: e.tensor_tensor(qt[:, h, :], qT[:, h, :], eb[:, 0:128], ALU.mult), reads=["h_qT", "h_eb"], writes=["h_qt"])
                                pk, pkn = b.ps()
                                S.op("pe", lambda e: e.matmul(pk[:, 0:128], kt_[:, hs], ident, start=True, stop=True), reads=["h_kt", "cm"], writes=[pkn])
                                S.op("act", lambda e: e.copy(ktT[:], pk[:, 0:128]), reads=[pkn], writes=["h_ktT"])
                                pa, pan = b.ps()
                                S.op("pe", lambda e: e.matmul(pa[:, 0:128], ktT[:], qt[:, h, :], start=True, stop=True), reads=["h_ktT", "h_qt"], writes=[pan])
                                S.op("dve", lambda e: e.tensor_tensor(AT[:], pa[:, 0:128], M_in, ALU.mult), reads=[pan, "cm"], writes=["h_AT"])
                                for j in range(4):
                                    S.op("pool", lambda e: e.tensor_scalar(khm[:, j, :], kh[:, hs], csel[:, j:j + 1], None, ALU.mult), reads=["h_kh", "cm"], writes=["h_khm"])
                                po_, pon_ = b.acc(0)
                                S.op("pe", lambda e: e.matmul(po_[:, 0:128], vv[:, hs], AT[:], start=True, stop=False), reads=["vv", "h_AT"], writes=[pon_])
                                corder = range(4) if d == 0 else range(3, -1, -1)
                                for n_, j in enumerate(corder):
                                    js = slice(j * 32, (j + 1) * 32)
                                    S.op("pe", lambda e: e.matmul(po_[:, js], Sst[:, d, h, :], qt[:, h, js], start=False, stop=(n_ == 3)),
                                         reads=["Sst%d%d" % (d, h), "h_qt"], writes=[pon_], pe_acc=True)
                                    ps_, psn2 = b.ps()
                                    S.op("pe", lambda e: e.matmul(ps_[:, 0:128], khm[:, j, :], vv[:, hs], start=True, stop=True), reads=["h_khm", "vv"], writes=[psn2])
                                    S.op("dve", lambda e: e.scalar_tensor_tensor(Sst[:, d, h, :], Sst[:, d, h, :], eb[:, 128 + j:129 + j], ps_[:, 0:128], ALU.mult, ALU.add),
                                         reads=[psn2, "h_eb", "Sst"], writes=["Sst%d%d" % (d, h)])
                                if d == 0:
                                    S.op("act", lambda e: e.copy(ofb[:, h, :], po_[:, 0:128]), reads=[pon_], writes=["h_of"])
                                else:
                                    S.op("dve", lambda e: e.tensor_tensor(osum[:, h, :], ofb[:, h, :], po_[:, 0:128], ALU.add), reads=[pon_, "h_of"], writes=["h_os"])
                            if d == 1:
                                S.op("act", lambda e: e.activation(osq[:], osum[:], AF.Square), reads=["h_os"], writes=["h_osq"])
                                pn_, pnn = b.ps()
                                for h in range(4):
                                    S.op("pe", lambda e: e.matmul(pn_[:, h * 128:(h + 1) * 128], ones_bf[:], osq[:, h, :], start=True, stop=True),
                                         reads=["h_osq", "ones_bf"], writes=[pnn])
                                rstd_from_sumsq(pn_, pnn, ors, "h_ors", 128, 512, 1.0 / 128)
                                for h in range(4):
                                    S.op("dve", lambda e: e.scalar_tensor_tensor(osum[:, h, :], osum[:, h, :], hn[:, h:h + 1], ors[:, h * 128:(h + 1) * 128], ALU.mult, ALU.mult),
                                         reads=["h_os", "hn", "h_ors"], writes=["h_os"])
                                S.op("dve", lambda e: e.tensor_tensor(ofb[:], osum[:], gT[:], ALU.mult), reads=["h_os", "h_gT"], writes=["h_of"])
                            S.dma("sp", OT[0, :, t0:t0 + 128].rearrange("(h p) t -> p h t", p=128), ofb[:], reads=["h_of"], writes=["OTa"])
                            S.barrier()
                if g == "p":
                    S.dma("sp", D["o_hgrn"][l, sq_].rearrange("d h k v -> k d h v"), Sst[:], reads=SSTN, writes=["ohg"])
                    S.barrier()

    def epilogue(st, z, zn, xt, xn, gco, Xdst, c0, wkn):
        sq, rs, tmp = wkn
        for kc in range(8):
            S.op("act", lambda e: e.activation(sq[:, kc, :], z[:, kc, :], AF.Square), reads=[zn], writes=["e_sq"])
        pt, pn = b.ps()
        for kc in range(8):
            S.op("pe", lambda e: e.matmul(pt[:], ones_bf[:], sq[:, kc, :], start=(kc == 0), stop=(kc == 7)), reads=["e_sq", "ones_bf"], writes=[pn], pe_acc=True)
        rstd_from_sumsq(pt, pn, rs, "e_rs", 128, 512, 1.0 / DM)
        for kc in range(8):
            S.op("dve", lambda e: e.tensor_tensor(tmp[:], z[:, kc, :], rs[:], ALU.mult), reads=[zn, "e_rs"], writes=["e_tmp"])
            S.op("dve", lambda e: e.scalar_tensor_tensor(xt[:, kc, :], tmp[:], gco[:, kc:kc + 1], xt[:, kc, :], ALU.mult, ALU.add),
                 reads=["e_tmp", "coef", xn], writes=[xn])
        S.dma("sp", Xdst[:, c0:c0 + 512].rearrange("(kc p) t -> p kc t", p=128), xt[:], reads=[xn], writes=[Xdst.name])

    def stage_merge(g, l, Xsrc, Xdst):
        G = GR[g]
        T = G["T"]
        UT, OT = D["UT_" + g], D["OT_" + g]
        with ExitStack() as st:
            Wb = b.sb(st, "Wb", [128, 4, 4, 1024], BF16)
            Wo = b.sb(st, "Wo", [128, 8, 1024], BF16)
            S.dma("pool", Wb[:], D["w_branch"][l].rearrange("n (kc p) c -> p n kc c", p=128), writes=["Wb"])
            S.dma("pool", Wo[:], D["w_out"][l].rearrange("(kc p) c -> p kc c", p=128), writes=["Wo"])
            ot = b.sb(st, "ot", [128, 4, 4, 512], BF16)
            yp = b.sb(st, "yp", [128, 8, 512], BF16)
            gts = [b.sb(st, "gt%d" % i, [128, 512]) for i in range(3)]
            accf = b.sb(st, "accf", [128, 512]); tm2 = b.sb(st, "tm2", [128, 512])
            z = b.sb(st, "z", [128, 8, 512]); xt = b.sb(st, "xt", [128, 8, 512])
            wkn = (b.sb(st, "e_sq", [128, 8, 512], BF16), b.sb(st, "e_rs", [128, 512]), b.sb(st, "e_tmp", [128, 512]))
            ng = 0
            for tt in range(T // 512):
                cs = slice(tt * 512, (tt + 1) * 512)
                S.dma("sp", ot[:], OT[:, :, cs].rearrange("n (kc p) t -> p n kc t", p=128), reads=["OT", "OTa"], writes=["ot"])
                S.dma("act", xt[:], Xsrc[:, cs].rearrange("(kc p) t -> p kc t", p=128), reads=[Xsrc.name], writes=["xt"])
                for dmc in range(8):
                    for n in range(4):
                        gt, gtn = gts[ng % 3], "gt%d" % (ng % 3)
                        ng += 1
                        r0 = OFF["gates"] + n * 1024 + dmc * 128
                        S.dma("sp", gt[:], UT[r0:r0 + 128, cs], reads=["UT"], writes=[gtn])
                        S.op("act", lambda e: e.activation(gt[:], gt[:], AF.Sigmoid), reads=[gtn], writes=[gtn])
                        pt, pn = b.ps()
                        for kc in range(4):
                            S.op("pe", lambda e: e.matmul(pt[:], Wb[:, n, kc, dmc * 128:(dmc + 1) * 128], ot[:, n, kc, :], start=(kc == 0), stop=(kc == 3)),
                                 reads=["Wb", "ot"], writes=[pn], pe_acc=True)
                        if n == 0:
                            S.op("dve", lambda e: e.tensor_tensor(accf[:], pt[:], gt[:], ALU.mult), reads=[pn, gtn], writes=["accf"])
                        elif n < 3:
                            S.op("dve", lambda e: e.tensor_tensor(tm2[:], pt[:], gt[:], ALU.mult), reads=[pn, gtn], writes=["tm2"])
                            S.op("pool", lambda e: e.tensor_tensor(accf[:], accf[:], tm2[:], ALU.add), reads=["tm2", "accf"], writes=["accf"])
                        else:
                            S.op("dve", lambda e: e.tensor_tensor(tm2[:], pt[:], gt[:], ALU.mult), reads=[pn, gtn], writes=["tm2"])
                            S.op("pool", lambda e: e.tensor_tensor(yp[:, dmc, :], accf[:], tm2[:], ALU.add), reads=["tm2", "accf"], writes=["yp"])
                for oc in range(8):
                    pt, pn = b.ps()
                    for kc in range(8):
                        S.op("pe", lambda e: e.matmul(pt[:], Wo[:, kc, oc * 128:(oc + 1) * 128], yp[:, kc, :], start=(kc == 0), stop=(kc == 7)),
                             reads=["Wo", "yp"], writes=[pn], pe_acc=True)
                    S.op("act", lambda e: e.copy(z[:, oc, :], pt[:]), reads=[pn], writes=["z"])
                epilogue(st, z, "z", xt, "xt", coef[:, l, G["cond"], 2], Xdst, tt * 512, wkn)
            S.barrier()

    def stage_mlp(g, l, Xsrc, Xdst):
        G = GR[g]
        T = G["T"]
        with ExitStack() as st:
            h2 = stage_h(st, g, l, Xsrc, 3, 4)
            W2 = [b.sb(st, "W2_%d" % i, [128, 8, 1024], BF16) for i in range(2)]
            n2 = 0
            w1 = [b.sb(st, "w1_%d" % i, [128, 8, 512], BF16) for i in range(2)]
            hid = b.sb(st, "hid", [128, 32, 512], BF16)
            rl = [b.sb(st, "rl%d" % i, [128, 512]) for i in range(2)]
            z = b.sb(st, "z", [128, 8, 512]); xt = b.sb(st, "xt", [128, 8, 512])
            wkn = (b.sb(st, "e_sq", [128, 8, 512], BF16), b.sb(st, "e_rs", [128, 512]), b.sb(st, "e_tmp", [128, 512]))
            nw = 0
            nr = 0
            for tt in range(T // 512):
                cs = slice(tt * 512, (tt + 1) * 512)
                S.dma("act", xt[:], Xsrc[:, cs].rearrange("(kc p) t -> p kc t", p=128), reads=[Xsrc.name], writes=["xt"])
                for cg in range(8):
                    w, wn = w1[nw % 2], "w1_%d" % (nw % 2)
                    nw += 1
                    S.dma("pool", w[:], D["w_mlp_in"][l, :, cg * 512:(cg + 1) * 512].rearrange("(kc p) c -> p kc c", p=128), writes=[wn])
                    for cc in range(4):
                        pt, pn = b.ps()
                        for kc in range(8):
                            S.op("pe", lambda e: e.matmul(pt[:], w[:, kc, cc * 128:(cc + 1) * 128], h2[:, kc, cs], start=(kc == 0), stop=(kc == 7)),
                                 reads=[wn, "hT"], writes=[pn], pe_acc=True)
                        r, rn = rl[nr % 2], "rl%d" % (nr % 2)
                        nr += 1
                        S.op("act", lambda e: e.activation(r[:], pt[:], AF.Relu), reads=[pn], writes=[rn])
                        S.op("dve", lambda e: e.tensor_tensor(hid[:, cg * 4 + cc, :], r[:], r[:], ALU.mult), reads=[rn], writes=["hid"])
                for q4 in range(4):
                    w2, w2n = W2[n2 % 2], "W2_%d" % (n2 % 2)
                    n2 += 1
                    S.dma("pool", w2[:], D["w_mlp_out"][l, q4 * 1024:(q4 + 1) * 1024, :].rearrange("(kc p) c -> p kc c", p=128), writes=[w2n])
                    for oc in range(8):
                        pt, pn = b.ps()
                        for fc in range(8):
                            S.op("pe", lambda e: e.matmul(pt[:], w2[:, fc, oc * 128:(oc + 1) * 128], hid[:, q4 * 8 + fc, :], start=(fc == 0), stop=(fc == 7)),
                                 reads=[w2n, "hid"], writes=[pn], pe_acc=True)
                        if q4 == 0:
                            S.op("act", lambda e: e.copy(z[:, oc, :], pt[:]), reads=[pn], writes=["z%d" % oc, "z"])
                        else:
                            S.op("dve", lambda e: e.tensor_tensor(z[:, oc, :], z[:, oc, :], pt[:], ALU.add), reads=[pn, "z%d" % oc], writes=["z%d" % oc, "z"])
                epilogue(st, z, "z", xt, "xt", coef[:, l, G["cond"], 5], Xdst, tt * 512, wkn)
            S.barrier()

    for g in ("s", "p"):
        X = D["xT_" + g]
        for l in range(DEPTH):
            with ExitStack() as st:
                hT = stage_h(st, g, l, X, 0, 1)
                stage_proj(g, l, hT)
            stage_hgrn(g, l)
            stage_gqa_like(g, l, "swa")
            stage_mla(g, l)
            stage_gqa_like(g, l, "gqa")
            stage_merge(g, l, X, D["X1_" + g])
            Xn = D["yT_" + g] if l == DEPTH - 1 else D["X2_%d_%s" % (l, g)]
            stage_mlp(g, l, D["X1_" + g], Xn)
            X = Xn
    S.barrier()
    return nc, b


_CACHE = {}


def _consts():
    nf = 16
    t = np.arange(2048)
    row, col = (t // 64).astype(np.float32), (t % 64).astype(np.float32)
    rope = np.zeros((4, 96, 2048), np.float32)
    def fill(ci, si, r0, nf):
        inv = (10000.0 ** (-np.arange(nf, dtype=np.float32) / nf)).astype(np.float32)
        ar = (row[None, :] * inv[:, None]).astype(np.float32)
        ac = (col[None, :] * inv[:, None]).astype(np.float32)
        for k, a in enumerate((ar, ac)):
            b0 = r0 + k * 2 * nf
            rope[ci, b0:b0 + nf] = np.cos(a); rope[ci, b0 + nf:b0 + 2 * nf] = np.cos(a)
            rope[si, b0:b0 + nf] = -np.sin(a); rope[si, b0 + nf:b0 + 2 * nf] = np.sin(a)
    fill(0, 1, 0, 16)
    fill(2, 3, 64, 8)
    s_ = np.arange(128)[:, None]; t_ = np.arange(128)[None, :]
    same = (s_ // 32) == (t_ // 32)
    cm = np.zeros((6, 128, 128), np.float32)
    cm[0] = np.eye(128)
    cm[1] = same & (s_ <= t_)
    cm[2] = same & (s_ >= t_)
    cm[3] = same & (s_ > t_)
    cm[4] = same & (s_ < t_)
    cm[5][:, 0:4] = (s_ // 32) == np.arange(4)[None, :]
    j = np.arange(128)[:, None]; i = (np.arange(512) % 128)[None, :]
    swm = np.stack([(j >= i), (j <= i)]).astype(np.float32).astype(ml_dtypes.bfloat16)
    return rope, cm, swm


def _perm(nd):
    q = nd // 4
    return np.concatenate([np.arange(q, 2 * q), np.arange(0, q), np.arange(3 * q, 4 * q), np.arange(2 * q, 3 * q)])


def kernel(**inp):
    f = lambda a: np.ascontiguousarray(np.asarray(a, dtype=np.float32))
    I = {k: f(v) for k, v in inp.items()}
    if "prog" not in _CACHE:
        _CACHE["prog"] = build_program()
    nc, b = _CACHE["prog"]
    rope, cm, swm = _consts()
    fm = lambda v, n: f(v.reshape(v.shape[0], n, 128).transpose(0, 2, 1))
    shared = dict(
        w_ada=I["w_ada"], b_adaT=fm(I["b_ada"], 48),
        gains=f(np.stack([fm(I[k], 8) for k in ("norm_mix_pre", "norm_mix_post", "norm_mlp_pre", "norm_mlp_post")], axis=2)),
        w_in=I["w_in"], lbT=f(np.stack([I["hgrn_lb_fwd"], I["hgrn_lb_bwd"]], axis=1)),
        hgrn_normT=fm(I["hgrn_norm"], 4), sink=f(I["swa_sink"][:, None, :]),
        mla_qnT=fm(I["mla_q_norm"], 2), mla_kvn=f(I["mla_kv_norm"][:, :, None]),
        w_uq=I["mla_w_uq"], w_ukv=I["mla_w_ukv"],
        gqa_qn=f(np.stack([I["gqa_q_norm"], I["gqa_q_norm"][:, _perm(64)]], axis=2)),
        gqa_kn=f(np.stack([I["gqa_k_norm"], I["gqa_k_norm"][:, _perm(64)]], axis=2)),
        w_branch=I["w_branch"], w_out=I["w_out"], w_mlp_in=I["w_mlp_in"], w_mlp_out=I["w_mlp_out"],
        rope64=rope, cmat=cm, swamask=swm,
    )
    wsw = I["mla_w_uq"].reshape(DEPTH, 256, 8, 96).copy()
    wsw[..., 64:96] = wsw[..., 64:96][..., _perm(32)]
    shared["w_uq_sw"] = f(wsw.reshape(DEPTH, 256, 768))
    in_maps = []
    for i in range(NCORE):
        bb = i % 2
        m = dict(shared)
        m["xT_s"] = f(I["x_sample"][bb].T)
        m["xT_p"] = f(I["x_prompt"][4 * i:4 * i + 4].reshape(1024, DM).T)
        cc = np.stack([I["c_ctx"], I["c"][bb]], axis=1)
        m["cT"] = f(cc.reshape(8, 128, 2).transpose(1, 0, 2))
        m["st_hgrn"] = f(I["state_hgrn"][bb])
        m["c_swa_kT"] = f(I["cache_swa_k"][bb].transpose(0, 2, 3, 1))
        m["c_swa_v"] = f(I["cache_swa_v"][bb].reshape(DEPTH, 512, 128))
        m["c_ckvT"] = f(I["cache_mla_ckv"][bb].transpose(0, 2, 1))
        m["c_krT"] = f(I["cache_mla_kr"][bb].transpose(0, 2, 1))
        m["c_gqa_kT"] = f(I["cache_gqa_k"][bb].transpose(0, 2, 3, 1))
        m["c_gqa_v"] = f(I["cache_gqa_v"][bb].reshape(DEPTH, 512, 128))
        in_maps.append(m)
    res = run_bass_kernel_spmd(nc, in_maps, core_ids=list(range(NCORE)))
    R = res.results
    y_prompt = np.concatenate([R[i]["yT_p"].T.reshape(4, 256, DM) for i in range(NCORE)], axis=0)
    y_sample = np.stack([R[0]["yT_s"].T, R[1]["yT_s"].T], axis=0)
    cat = lambda fn: np.ascontiguousarray(np.concatenate([fn(R[i]) for i in range(NCORE)], axis=0).astype(np.float32))
    n_hgrn = cat(lambda r: r["o_hgrn"].transpose(1, 0, 2, 3, 4, 5))
    kT = lambda a: a.reshape(DEPTH, 2, 64, 4, 256).transpose(3, 0, 4, 1, 2)
    vv = lambda a: a.reshape(DEPTH, 4, 256, 2, 64).transpose(1, 0, 2, 3, 4)
    n_swa_k = cat(lambda r: kT(r["o_swa_kT"]))
    n_swa_v = cat(lambda r: vv(r["o_swa_v"]))
    n_ckv = cat(lambda r: r["o_ckvT"].reshape(DEPTH, 128, 4, 256).transpose(2, 0, 3, 1))
    n_kr = cat(lambda r: r["o_krT"].reshape(DEPTH, 32, 4, 256).transpose(2, 0, 3, 1))
    n_gqa_k = cat(lambda r: kT(r["o_gqa_kT"]))
    n_gqa_v = cat(lambda r: vv(r["o_gqa_v"]))
    return (np.ascontiguousarray(y_prompt.astype(np.float32)), np.ascontiguousarray(y_sample.astype(np.float32)),
            n_hgrn, n_swa_k, n_swa_v, n_ckv, n_kr, n_gqa_k, n_gqa_v)
```

```python
import numpy as np
from contextlib import ExitStack
import ml_dtypes
import concourse.bass as bass
import concourse.mybir as mybir
from concourse.bass_utils import run_bass_kernel_spmd

F32 = mybir.dt.float32
BF16 = mybir.dt.bfloat16
AF = mybir.ActivationFunctionType
ALU = mybir.AluOpType

DM = 1024
DEPTH = 2
NCORE = 8
EPS = 1e-6
OFF = dict(hq=0, ff=512, fb=1024, hi=1536, hg=2048, sq=2560, sk=3072, sv=3200, cq=3328, ckv=3584, kr=3712,
           gq=3744, gk=4256, gv=4384, gates=4512)
D_IN = 8608
UTM_COLS = 1792
STAGES = {"proj", "hgrn", "swa", "mla", "gqa", "merge", "mlp"}


class Sched:
    NDMA = 24

    def __init__(self, nc, es):
        self.nc = nc
        self.eng = {"pe": nc.tensor, "act": nc.scalar, "dve": nc.vector, "pool": nc.gpsimd, "sp": nc.sync}
        self.sem = {k: es.enter_context(nc.semaphore("s_" + k)) for k in ("pe", "act", "dve", "pool")}
        self.cnt = {k: 0 for k in self.sem}
        self.dsem = [es.enter_context(nc.semaphore("d%d" % i)) for i in range(self.NDMA)]
        self.dcnt = [0] * self.NDMA
        self.dnext = 0
        self.seen = {k: {} for k in self.eng}
        self.lastw = {}
        self.reads = {}
        self.n_instr = 0

    def _sem_of(self, key):
        return self.sem[key] if isinstance(key, str) else self.dsem[key[1]]

    def _wait(self, e, tok):
        key, val = tok
        if self.seen[e].get(key, 0) >= val:
            return
        self.eng[e].wait_ge(self._sem_of(key), val)
        self.seen[e][key] = val

    def _deps(self, e, reads, writes, pe_acc=False):
        best = {}
        for r in reads:
            t = self.lastw.get(r)
            if t is not None and best.get(t[0], 0) < t[1]:
                best[t[0]] = t[1]
        for w in writes:
            t = self.lastw.get(w)
            if t is not None and best.get(t[0], 0) < t[1]:
                if not (pe_acc and t[0] == "pe"):
                    best[t[0]] = t[1]
            for t in self.reads.get(w, ()):
                if best.get(t[0], 0) < t[1]:
                    best[t[0]] = t[1]
        for key, val in best.items():
            self._wait(e, (key, val))

    def _record(self, tok, reads, writes):
        for r in reads:
            lst = self.reads.setdefault(r, [])
            lst[:] = [t for t in lst if t[0] != tok[0]]
            lst.append(tok)
        for w in writes:
            self.lastw[w] = tok
            self.reads[w] = []

    def op(self, e, fn, reads=(), writes=(), pe_acc=False):
        self._deps(e, reads, writes, pe_acc)
        ins = fn(self.eng[e])
        self.cnt[e] += 1
        ins.then_inc(self.sem[e], 1)
        self._record((e, self.cnt[e]), reads, writes)
        self.n_instr += 1
        return ins

    def dma(self, q, out, in_, reads=(), writes=()):
        i = self.dnext
        self.dnext = (self.dnext + 1) % self.NDMA
        if self.dcnt[i] > 0:
            self._wait(q, (("d", i), self.dcnt[i]))
        self._deps(q, reads, writes)
        ins = self.eng[q].dma_start(out=out, in_=in_)
        self.dcnt[i] += 16
        ins.then_inc(self.dsem[i], 16)
        self._record((("d", i), self.dcnt[i]), reads, writes)
        self.n_instr += 1
        return ins

    def barrier(self):
        best = {}
        for k in self.cnt:
            if self.cnt[k]:
                best[k] = self.cnt[k]
        for i in range(self.NDMA):
            if self.dcnt[i]:
                best[("d", i)] = self.dcnt[i]
        for e in self.eng:
            for key, val in best.items():
                self._wait(e, (key, val))
        self.lastw = {}
        self.reads = {}


class B:
    def __init__(self, nc, es):
        self.nc, self.es = nc, es
        self.S = Sched(nc, es)
        self.D = {}
        self.psn = 0

    def din(self, name, shape, dt=F32):
        self.D[name] = self.nc.dram_tensor(name, list(shape), dt, kind="ExternalInput").ap()
        return self.D[name]

    def dout(self, name, shape, dt=F32):
        self.D[name] = self.nc.dram_tensor(name, list(shape), dt, kind="ExternalOutput").ap()
        return self.D[name]

    def dscr(self, name, shape, dt=F32):
        self.D[name] = self.nc.dram_tensor(name, list(shape), dt, kind="Internal").ap()
        return self.D[name]

    def sb(self, st, name, shape, dt=F32):
        self.uid = getattr(self, "uid", 0) + 1
        return st.enter_context(self.nc.sbuf_tensor("sb%d_%s" % (self.uid, name), list(shape), dt))

    def ps(self):
        i = self.psn % 6
        self.psn += 1
        return self.psum[i], "ps%d" % i

    def acc(self, i):
        return self.psum[6 + i], "ps%d" % (6 + i)


def build_program():
    nc = bass.Bass("TRN2", target_bir_lowering=False)
    es = ExitStack()
    b = B(nc, es)
    S = b.S
    D = b.D
    GR = {
        "s": dict(T=2048, nseq=1, L=2048, P=512, rope=True, cond=1),
        "p": dict(T=1024, nseq=4, L=256, P=0, rope=False, cond=0),
    }
    for g, G in GR.items():
        T = G["T"]
        b.din("xT_" + g, [DM, T])
        b.dout("yT_" + g, [DM, T])
        b.dscr("X1_" + g, [DM, T])
        for l in range(DEPTH - 1):
            b.dscr("X2_%d_%s" % (l, g), [DM, T])
        b.dscr("UT_" + g, [68 * 128, T])
        b.dscr("UTM_" + g, [T, UTM_COLS])
        b.dscr("OT_" + g, [4, 512, T], BF16)
        b.dscr("YP_" + g, [DM, T], BF16)
    b.din("cT", [128, 8, 2])
    b.din("w_ada", [DEPTH, DM, 6 * DM])
    b.din("b_adaT", [DEPTH, 128, 48])
    b.din("gains", [DEPTH, 128, 4, 8])
    b.din("w_in", [DEPTH, DM, D_IN])
    b.din("lbT", [DEPTH, 2, 512])
    b.din("hgrn_normT", [DEPTH, 128, 4])
    b.din("sink", [DEPTH, 1, 8])
    b.din("mla_qnT", [DEPTH, 128, 2])
    b.din("mla_kvn", [DEPTH, 128, 1])
    b.din("w_uq", [DEPTH, 256, 768])
    b.din("w_uq_sw", [DEPTH, 256, 768])
    b.din("w_ukv", [DEPTH, 128, 1024])
    b.din("gqa_qn", [DEPTH, 64, 2])
    b.din("gqa_kn", [DEPTH, 64, 2])
    b.din("w_branch", [DEPTH, 4, 512, DM])
    b.din("w_out", [DEPTH, DM, DM])
    b.din("w_mlp_in", [DEPTH, DM, 4 * DM])
    b.din("w_mlp_out", [DEPTH, 4 * DM, DM])
    b.din("st_hgrn", [DEPTH, 2, 4, 128, 128])
    b.din("c_swa_kT", [DEPTH, 2, 64, 512])
    b.din("c_swa_v", [DEPTH, 512, 128])
    b.din("c_ckvT", [DEPTH, 128, 512])
    b.din("c_krT", [DEPTH, 32, 512])
    b.din("c_gqa_kT", [DEPTH, 2, 64, 512])
    b.din("c_gqa_v", [DEPTH, 512, 128])
    b.din("rope64", [4, 96, 2048])
    b.din("cmat", [6, 128, 128])
    b.din("swamask", [2, 128, 512], BF16)
    b.dout("o_hgrn", [DEPTH, 4, 2, 4, 128, 128])
    b.dout("o_swa_kT", [DEPTH, 128, 1024])
    b.dout("o_swa_v", [DEPTH, 1024, 128])
    b.dout("o_ckvT", [DEPTH, 128, 1024])
    b.dout("o_krT", [DEPTH, 32, 1024])
    b.dout("o_gqa_kT", [DEPTH, 128, 1024])
    b.dout("o_gqa_v", [DEPTH, 1024, 128])

    b.psum = [es.enter_context(nc.psum_tensor("psb%d" % i, [128, 512], F32)) for i in range(8)]

    cst = ExitStack()
    es.enter_context(cst)
    ones_bf = b.sb(cst, "ones_bf", [128, 128], BF16)
    ones_f = b.sb(cst, "ones_f", [128, 128], F32)
    cm = b.sb(cst, "cm", [128, 6, 128], F32)
    modT = b.sb(cst, "modT", [128, DEPTH, 48, 2], F32)
    gains = b.sb(cst, "gains", [128, DEPTH, 4, 8], F32)
    coef = b.sb(cst, "coef", [128, DEPTH, 2, 6, 8], F32)
    S.op("pool", lambda e: e.memset(ones_bf[:], 1.0), writes=["ones_bf"])
    S.op("pool", lambda e: e.memset(ones_f[:], 1.0), writes=["ones_f"])
    S.dma("sp", cm[:], D["cmat"].rearrange("a p c -> p a c"), writes=["cm"])
    S.dma("sp", gains[:], D["gains"].rearrange("l p a c -> p l a c"), writes=["gains"])

    with ExitStack() as st:
        cT = b.sb(st, "cT", [128, 8, 2])
        scT = b.sb(st, "scT", [128, 8, 2])
        badaT = b.sb(st, "badaT", [128, DEPTH, 48])
        wa = [b.sb(st, "wa%d" % i, [128, 8, 768]) for i in range(2)]
        S.dma("sp", cT[:], D["cT"], writes=["cT"])
        S.dma("sp", badaT[:], D["b_adaT"].rearrange("l p j -> p l j"), writes=["badaT"])
        S.op("act", lambda e: e.activation(scT[:], cT[:], AF.Silu), reads=["cT"], writes=["scT"])
        n = 0
        for l in range(DEPTH):
            pt, pn = b.ps()
            for cg in range(8):
                w = wa[n % 2]
                wn = "wa%d" % (n % 2)
                n += 1
                S.dma("sp" if cg % 2 == 0 else "act", w[:],
                      D["w_ada"][l, :, cg * 768:(cg + 1) * 768].rearrange("(kc p) c -> p kc c", p=128), writes=[wn])
                for jj in range(6):
                    j = cg * 6 + jj
                    for kc in range(8):
                        S.op("pe", lambda e: e.matmul(pt[:, 2 * j:2 * j + 2], w[:, kc, jj * 128:(jj + 1) * 128], scT[:, kc, :],
                                                      start=(kc == 0), stop=(kc == 7)),
                             reads=[wn, "scT"], writes=[pn], pe_acc=True)
            S.op("dve", lambda e: e.tensor_tensor(modT[:, l], pt[:, 0:96].rearrange("p (j c) -> p j c", c=2),
                                                  badaT[:, l].unsqueeze(2).to_broadcast([128, 48, 2]), ALU.add),
                 reads=[pn, "badaT"], writes=["modT"])
        for l in range(DEPTH):
            for c in range(2):
                m = lambda i: modT[:, l, i * 8:(i + 1) * 8, c]
                S.op("dve", lambda e: e.scalar_tensor_tensor(coef[:, l, c, 0], m(1), 1.0, gains[:, l, 0], ALU.add, ALU.mult),
                     reads=["modT", "gains"], writes=["coef"])
                S.op("dve", lambda e: e.tensor_copy(coef[:, l, c, 1], m(0)), reads=["modT"], writes=["coef"])
                S.op("dve", lambda e: e.tensor_tensor(coef[:, l, c, 2], m(2), gains[:, l, 1], ALU.mult), reads=["modT", "gains"], writes=["coef"])
                S.op("dve", lambda e: e.scalar_tensor_tensor(coef[:, l, c, 3], m(4), 1.0, gains[:, l, 2], ALU.add, ALU.mult),
                     reads=["modT", "gains"], writes=["coef"])
                S.op("dve", lambda e: e.tensor_copy(coef[:, l, c, 4], m(3)), reads=["modT"], writes=["coef"])
                S.op("dve", lambda e: e.tensor_tensor(coef[:, l, c, 5], m(5), gains[:, l, 3], ALU.mult), reads=["modT", "gains"], writes=["coef"])
        S.barrier()

    def rstd_from_sumsq(pt, pn, out, outn, npart, ncol, inv_n):
        S.op("act", lambda e: e.activation(out[0:npart, 0:ncol], pt[0:npart, 0:ncol], AF.Sqrt, bias=epsb[0:npart, :], scale=inv_n),
             reads=[pn, "epsb"], writes=[outn])
        S.op("dve", lambda e: e.reciprocal(out[0:npart, 0:ncol], out[0:npart, 0:ncol]), reads=[outn], writes=[outn])

    epsb = b.sb(cst, "epsb", [128, 1], F32)
    S.op("pool", lambda e: e.memset(epsb[:], EPS), writes=["epsb"])

    def norm_mod_tile(st_unused, xt, xn, hT, hn, t0, a_ap, sh_ap, tmp, sq, rs):
        for kc in range(8):
            S.op("act", lambda e: e.activation(sq[:, kc, :], xt[:, kc, :], AF.Square), reads=[xn], writes=["sq"])
        pt, pn = b.ps()
        for kc in range(8):
            S.op("pe", lambda e: e.matmul(pt[:], ones_bf[:], sq[:, kc, :], start=(kc == 0), stop=(kc == 7)),
                 reads=["sq", "ones_bf"], writes=[pn], pe_acc=True)
        rstd_from_sumsq(pt, pn, rs, "rs", 128, 512, 1.0 / DM)
        for kc in range(8):
            S.op("dve", lambda e: e.tensor_tensor(tmp[:], xt[:, kc, :], rs[:], ALU.mult), reads=[xn, "rs"], writes=["tmp"])
            S.op("act", lambda e: e.activation(hT[:, kc, t0:t0 + 512], tmp[:], AF.Identity, bias=sh_ap[:, kc:kc + 1], scale=a_ap[:, kc:kc + 1]),
                 reads=["tmp", "coef"], writes=[hn])

    def stage_h(st, g, l, Xsrc, ia, ish):
        G = GR[g]
        T = G["T"]
        hT = b.sb(st, "hT", [128, 8, T], BF16)
        with ExitStack() as s2:
            xts = [b.sb(s2, "xt%d" % i, [128, 8, 512]) for i in range(2)]
            tmp = b.sb(s2, "tmp", [128, 512])
            sq = b.sb(s2, "sq", [128, 8, 512], BF16)
            rs = b.sb(s2, "rs", [128, 512])
            for tt in range(T // 512):
                xt, xn = xts[tt % 2], "xt%d" % (tt % 2)
                S.dma("sp", xt[:], Xsrc[:, tt * 512:(tt + 1) * 512].rearrange("(kc p) t -> p kc t", p=128), writes=[xn])
                norm_mod_tile(None, xt, xn, hT, "hT", tt * 512, coef[:, l, G["cond"], ia], coef[:, l, G["cond"], ish], tmp, sq, rs)
            S.barrier()
        return hT

    def stage_proj(g, l, hT):
        G = GR[g]
        T = G["T"]
        UT, UTM = D["UT_" + g], D["UTM_" + g]
        tm_groups = {1: [(0, 512, 0)], 2: [(0, 512, 512)], 3: [(0, 512, 1024)], 6: [(128, 128, 1536)], 8: [(288, 128, 1664)]}
        with ExitStack() as st:
            wb = [b.sb(st, "wb%d" % i, [128, 8, 512], BF16) for i in range(2)]
            ev = [b.sb(st, "ev%d" % i, [128, 512]) for i in range(4)]
            nev = 0
            for cg in range(17):
                ncol = 512 if cg < 16 else D_IN - 8192
                w, wn = wb[cg % 2], "wb%d" % (cg % 2)
                S.dma("pool", w[:, :, 0:ncol], D["w_in"][l, :, cg * 512:cg * 512 + ncol].rearrange("(kc p) c -> p kc c", p=128), writes=[wn])
                for tt in range(T // 512):
                    for cc in range((ncol + 127) // 128):
                        m = min(128, ncol - cc * 128)
                        pt, pn = b.ps()
                        for kc in range(8):
                            S.op("pe", lambda e: e.matmul(pt[0:m, :], w[:, kc, cc * 128:cc * 128 + m], hT[:, kc, tt * 512:(tt + 1) * 512],
                                                          start=(kc == 0), stop=(kc == 7)), reads=[wn, "hT"], writes=[pn], pe_acc=True)
                        e_, en = ev[nev % 4], "ev%d" % (nev % 4)
                        eng = "act" if nev % 2 == 0 else "dve"
                        nev += 1
                        if eng == "act":
                            S.op("act", lambda e: e.copy(e_[0:m, :], pt[0:m, :]), reads=[pn], writes=[en])
                        else:
                            S.op("dve", lambda e: e.tensor_copy(e_[0:m, :], pt[0:m, :]), reads=[pn], writes=[en])
                        r0 = cg * 512 + cc * 128
                        S.dma("sp", UT[r0:r0 + m, tt * 512:(tt + 1) * 512], e_[0:m, :], reads=[en], writes=[])
                for (c0, cn, dst) in tm_groups.get(cg, []):
                    for t4 in range(T // 128):
                        pt, pn = b.ps()
                        for kc in range(8):
                            S.op("pe", lambda e: e.matmul(pt[:, 0:cn], hT[:, kc, t4 * 128:(t4 + 1) * 128], w[:, kc, c0:c0 + cn],
                                                          start=(kc == 0), stop=(kc == 7)), reads=[wn, "hT"], writes=[pn], pe_acc=True)
                        e_, en = ev[nev % 4], "ev%d" % (nev % 4)
                        eng = "act" if nev % 2 == 0 else "dve"
                        nev += 1
                        if eng == "act":
                            S.op("act", lambda e: e.copy(e_[:, 0:cn], pt[:, 0:cn]), reads=[pn], writes=[en])
                        else:
                            S.op("dve", lambda e: e.tensor_copy(e_[:, 0:cn], pt[:, 0:cn]), reads=[pn], writes=[en])
                        S.dma("sp", UTM[t4 * 128:(t4 + 1) * 128, dst:dst + cn], e_[:, 0:cn], reads=[en], writes=[])
            S.barrier()

    def attn_unit(qT, qn, Kd, Nq, chunks, scale, o_out, on, wk, sink=None):
        po, pon = b.acc(0)
        pd, pdn = b.acc(1)
        nch = len(chunks)
        for i, (kT, kn, V, vn, nk, mask) in enumerate(chunks):
            pst, psn_ = b.ps()
            S.op("pe", lambda e: e.matmul(pst[0:nk, 0:Nq], kT, qT, start=True, stop=True), reads=[kn, qn], writes=[psn_])
            E, En = wk["E"][i % 3], "E%d" % (i % 3)
            S.op("act", lambda e: e.activation(E[0:nk, 0:Nq], pst[0:nk, 0:Nq], AF.Exp, scale=scale), reads=[psn_], writes=[En])
            if mask is not None:
                S.op("pool", lambda e: e.tensor_tensor(E[0:nk, 0:Nq], E[0:nk, 0:Nq], mask, ALU.mult), reads=[En, "swamask"], writes=[En])
            last = (i == nch - 1) and sink is None
            S.op("pe", lambda e: e.matmul(po[0:64, 0:Nq], V, E[0:nk, 0:Nq], start=(i == 0), stop=(i == nch - 1)),
                 reads=[vn, En], writes=[pon], pe_acc=True)
            S.op("pe", lambda e: e.matmul(pd[0:64, 0:Nq], ones_bf[0:nk, 0:64], E[0:nk, 0:Nq], start=(i == 0), stop=last),
                 reads=["ones_bf", En], writes=[pdn], pe_acc=True)
        if sink is not None:
            S.op("pe", lambda e: e.matmul(pd[0:64, 0:Nq], ones_bf[0:1, 0:64], sink, start=False, stop=True),
                 reads=["ones_bf", "sinkrow"], writes=[pdn], pe_acc=True)
        rec = wk["rec"]
        S.op("dve", lambda e: e.reciprocal(rec[0:64, 0:Nq], pd[0:64, 0:Nq]), reads=[pdn], writes=["rec"])
        S.op("dve", lambda e: e.tensor_tensor(o_out, po[0:64, 0:Nq], rec[0:64, 0:Nq], ALU.mult), reads=[pon, "rec"], writes=[on])

    def rms_rope_rows(st, src_rows_fn, n_rows, T, gain2, gainn, do_norm, do_rope, rope_idx, out_bf, outn, out32=None, out32n=None,
                      out32_pre_rope=False):
        x = b.sb(st, "rr_x", [64, T])
        xs = b.sb(st, "rr_xs", [64, T]) if do_rope else None
        sqb = b.sb(st, "rr_sq", [64, 512], BF16)
        rs = b.sb(st, "rr_rs", [64, T])
        t1 = b.sb(st, "rr_t1", [64, 512])
        for (r0, nr, ap) in src_rows_fn(False):
            S.dma("sp", x[r0:r0 + nr, :], ap, writes=["rr_x"])
        if do_rope:
            for (r0, nr, ap) in src_rows_fn(True):
                S.dma("act", xs[r0:r0 + nr, :], ap, writes=["rr_xs"])
        nr = n_rows
        W = min(512, T)
        for c in range(T // W):
            cs = slice(c * W, (c + 1) * W)
            if do_norm:
                S.op("act", lambda e: e.activation(sqb[0:nr, 0:W], x[0:nr, cs], AF.Square), reads=["rr_x"], writes=["rr_sq"])
                pt, pn = b.ps()
                S.op("pe", lambda e: e.matmul(pt[0:nr, 0:W], ones_bf[0:nr, 0:nr], sqb[0:nr, 0:W], start=True, stop=True),
                     reads=["rr_sq", "ones_bf"], writes=[pn])
                rstd_from_sumsq(pt, pn, rs[:, cs], "rr_rs", nr, W, 1.0 / nr)
                S.op("dve", lambda e: e.scalar_tensor_tensor(x[0:nr, cs], x[0:nr, cs], gain2[0:nr, 0:1], rs[0:nr, cs], ALU.mult, ALU.mult),
                     reads=["rr_x", "rr_rs", gainn], writes=["rr_x"])
                if do_rope:
                    S.op("dve", lambda e: e.scalar_tensor_tensor(xs[0:nr, cs], xs[0:nr, cs], gain2[0:nr, 1:2], rs[0:nr, cs], ALU.mult, ALU.mult),
                         reads=["rr_xs", "rr_rs", gainn], writes=["rr_xs"])
            if out32 is not None and out32_pre_rope:
                S.op("pool", lambda e: e.tensor_copy(out32[0:nr, cs], x[0:nr, cs]), reads=["rr_x"], writes=[out32n])
            if do_rope:
                S.op("dve", lambda e: e.tensor_tensor(x[0:nr, cs], x[0:nr, cs], rope[0:nr, rope_idx, cs], ALU.mult), reads=["rr_x", "rope"], writes=["rr_x"])
                S.op("pool", lambda e: e.tensor_tensor(t1[0:nr, 0:W], xs[0:nr, cs], rope[0:nr, rope_idx + 1, cs], ALU.mult), reads=["rr_xs", "rope"], writes=["rr_t1"])
                S.op("dve", lambda e: e.tensor_tensor(out_bf[0:nr, cs], x[0:nr, cs], t1[0:nr, 0:W], ALU.add), reads=["rr_x", "rr_t1"], writes=[outn])
            else:
                S.op("dve", lambda e: e.tensor_copy(out_bf[0:nr, cs], x[0:nr, cs]), reads=["rr_x"], writes=[outn])

    def swap_rows(base, T0, T1, UT, nd):
        q = nd // 4
        def f(swapped):
            if not swapped:
                return [(0, nd, UT[base:base + nd, T0:T1])]
            return [(0, q, UT[base + q:base + 2 * q, T0:T1]), (q, q, UT[base:base + q, T0:T1]),
                    (2 * q, q, UT[base + 3 * q:base + 4 * q, T0:T1]), (3 * q, q, UT[base + 2 * q:base + 3 * q, T0:T1])]
        return f

    rope = b.sb(cst, "rope", [96, 4, 2048], BF16)
    S.dma("pool", rope[:], D["rope64"].rearrange("a p t -> p a t"), writes=["rope"])

    def stage_gqa_like(g, l, kind):
        G = GR[g]
        T, L, P, nseq, do_rope = G["T"], G["L"], G["P"], G["nseq"], G["rope"]
        UT, UTM, OT = D["UT_" + g], D["UTM_" + g], D["OT_" + g]
        qo, ko = (OFF["sq"], OFF["sk"]) if kind == "swa" else (OFF["gq"], OFF["gk"])
        vcol = 1536 if kind == "swa" else 1664
        bi = 1 if kind == "swa" else 3
        do_norm = kind == "gqa"
        scale = 64 ** -0.5
        nkc_ctx = P // 128
        for sq_ in range(nseq):
            T0, T1 = sq_ * L, (sq_ + 1) * L
            with ExitStack() as st:
                kT = b.sb(st, "kT", [64, 2, P + L], BF16)
                Vt = b.sb(st, "Vt", [128, (P + L) // 128, 128], BF16)
                qTh = b.sb(st, "qTh", [64, 8, L], BF16)
                gq2 = b.sb(st, "gq2", [64, 2]); gk2 = b.sb(st, "gk2", [64, 2])
                sinkrow = b.sb(st, "sinkrow", [1, 8, 128], BF16)
                sk32 = b.sb(st, "sk32", [1, 8])
                if do_norm:
                    S.dma("sp", gq2[:], D["gqa_qn"][l], writes=["gq2"])
                    S.dma("sp", gk2[:], D["gqa_kn"][l], writes=["gk2"])
                else:
                    S.dma("sp", sk32[:], D["sink"][l], writes=["sk32"])
                    S.op("act", lambda e: e.activation(sk32[:], sk32[:], AF.Exp), reads=["sk32"], writes=["sk32"])
                    S.op("dve", lambda e: e.tensor_copy(sinkrow[:], sk32[:].unsqueeze(2).to_broadcast([1, 8, 128])), reads=["sk32"], writes=["sinkrow"])
                if P:
                    with ExitStack() as s2:
                        kc32 = b.sb(s2, "kc32", [64, 2, P]); vc32 = b.sb(s2, "vc32", [128, P // 128, 128])
                        src_k = D["c_swa_kT"] if kind == "swa" else D["c_gqa_kT"]
                        src_v = D["c_swa_v"] if kind == "swa" else D["c_gqa_v"]
                        S.dma("sp", kc32[:], src_k[l].rearrange("h d t -> d h t"), writes=["kc32"])
                        S.dma("sp", vc32[:], src_v[l].rearrange("(c p) f -> p c f", p=128), writes=["vc32"])
                        S.op("dve", lambda e: e.tensor_copy(kT[:, :, 0:P], kc32[:]), reads=["kc32"], writes=["kT"])
                        S.op("dve", lambda e: e.tensor_copy(Vt[:, 0:P // 128, :], vc32[:]), reads=["vc32"], writes=["Vt"])
                        S.barrier()
                for kvh in range(2):
                    with ExitStack() as s2:
                        k32 = b.sb(s2, "k32", [64, L]) if g == "p" else None
                        rms_rope_rows(s2, swap_rows(ko + kvh * 64, T0, T1, UT, 64), 64, L, gk2, "gk2", do_norm, do_rope, 0,
                                      kT[:, kvh, P:P + L], "kT", out32=k32, out32n="k32", out32_pre_rope=True)
                        if g == "p":
                            dst = D["o_swa_kT"] if kind == "swa" else D["o_gqa_kT"]
                            S.dma("sp", dst[l, kvh * 64:(kvh + 1) * 64, T0:T1], k32[:], reads=["k32"], writes=[])
                        S.barrier()
                with ExitStack() as s2:
                    v32 = b.sb(s2, "v32", [128, L // 128, 128])
                    S.dma("sp", v32[:], UTM[T0:T1, vcol:vcol + 128].rearrange("(c p) f -> p c f", p=128), writes=["v32"])
                    S.op("dve", lambda e: e.tensor_copy(Vt[:, P // 128:, :], v32[:]), reads=["v32"], writes=["Vt"])
                    if g == "p":
                        dst = D["o_swa_v"] if kind == "swa" else D["o_gqa_v"]
                        S.dma("act", dst[l, T0:T1, :].rearrange("(c p) f -> p c f", p=128), v32[:], reads=["v32"], writes=[])
                    S.barrier()
                for h in range(8):
                    with ExitStack() as s2:
                        rms_rope_rows(s2, swap_rows(qo + h * 64, T0, T1, UT, 64), 64, L, gq2, "gq2", do_norm, do_rope, 0, qTh[:, h, :], "qTh")
                        S.barrier()
                with ExitStack() as s2:
                    wk = dict(E=[b.sb(s2, "E%d" % i, [128, 512], BF16) for i in range(3)], rec=b.sb(s2, "rec", [64, 512]))
                    obs = [b.sb(s2, "ob%d" % i, [64, 512], BF16) for i in range(2)]
                    swm = b.sb(s2, "swamask", [128, 2, 512], BF16)
                    S.dma("sp", swm[:], D["swamask"].rearrange("a p c -> p a c"), writes=["swamask"])
                    nu = 0
                    for kvh in range(2):
                        for qb in range(L // 128):
                            chunks = []
                            for c in range(nkc_ctx):
                                chunks.append((kT[:, kvh, c * 128:(c + 1) * 128], "kT", Vt[:, c, kvh * 64:(kvh + 1) * 64], "Vt", 128, None))
                            if kind == "swa" and P:
                                for (kb, mi) in ((qb - 1, 0), (qb, None), (qb + 1, 1)):
                                    if 0 <= kb < L // 128:
                                        chunks.append((kT[:, kvh, P + kb * 128:P + (kb + 1) * 128], "kT", Vt[:, nkc_ctx + kb, kvh * 64:(kvh + 1) * 64], "Vt",
                                                       128, None if mi is None else swm[:, mi, :]))
                            else:
                                for kb in range(L // 128):
                                    chunks.append((kT[:, kvh, P + kb * 128:P + (kb + 1) * 128], "kT", Vt[:, nkc_ctx + kb, kvh * 64:(kvh + 1) * 64], "Vt", 128, None))
                            ob, obn = obs[nu % 2], "ob%d" % (nu % 2)
                            nu += 1
                            attn_unit(qTh[:, kvh * 4:(kvh + 1) * 4, qb * 128:(qb + 1) * 128], "qTh", 64, 512, chunks, scale,
                                      ob[:], obn, wk, sink=(sinkrow[0:1, kvh * 4:(kvh + 1) * 4, :] if kind == "swa" else None))
                            S.dma("sp", OT[bi, kvh * 256:(kvh + 1) * 256, T0 + qb * 128:T0 + (qb + 1) * 128].rearrange("(h d) t -> d h t", d=64),
                                  ob[:].rearrange("d (h t) -> d h t", h=4), reads=[obn], writes=[])
                    S.barrier()

    def stage_mla(g, l):
        G = GR[g]
        T, L, P, nseq, do_rope = G["T"], G["L"], G["P"], G["nseq"], G["rope"]
        UT, OT = D["UT_" + g], D["OT_" + g]
        scale = 96 ** -0.5
        NK = P + L
        for sq_ in range(nseq):
            T0, T1 = sq_ * L, (sq_ + 1) * L
            with ExitStack() as st:
                ckvT = b.sb(st, "ckvT", [128, NK], BF16)
                krT = b.sb(st, "krT", [96, NK], BF16)
                cqn = b.sb(st, "cqn", [128, 2, L], BF16)
                wuq = b.sb(st, "wuq", [128, 2, 768], BF16); wuqs = b.sb(st, "wuqs", [128, 2, 768], BF16)
                wukv = b.sb(st, "wukv", [128, 1024], BF16)
                g_q = b.sb(st, "g_q", [128, 2]); g_kv = b.sb(st, "g_kv", [128, 1])
                S.dma("pool", wuq[:], D["w_uq"][l].rearrange("(kc p) c -> p kc c", p=128), writes=["wuq"])
                S.dma("pool", wuqs[:], D["w_uq_sw"][l].rearrange("(kc p) c -> p kc c", p=128), writes=["wuqs"])
                S.dma("pool", wukv[:], D["w_ukv"][l], writes=["wukv"])
                S.dma("sp", g_q[:], D["mla_qnT"][l], writes=["g_q"])
                S.dma("sp", g_kv[:], D["mla_kvn"][l], writes=["g_kv"])
                with ExitStack() as s2:
                    x = b.sb(s2, "m_x", [128, 2, L]); sqb = b.sb(s2, "m_sq", [128, 2, 512], BF16); rs = b.sb(s2, "m_rs", [128, 512])
                    xk = b.sb(s2, "m_xk", [128, L]); kr32 = b.sb(s2, "m_kr", [96, L]); krs = b.sb(s2, "m_krs", [96, L]); t1 = b.sb(s2, "m_t1", [96, 512])
                    if P:
                        S.dma("pool", ckvT[:, 0:P], D["c_ckvT"][l], writes=["ckvT"])
                        S.dma("pool", krT[64:96, 0:P], D["c_krT"][l], writes=["krT"])
                    S.dma("sp", x[:], UT[OFF["cq"]:OFF["cq"] + 256, T0:T1].rearrange("(kc p) t -> p kc t", p=128), writes=["m_x"])
                    S.dma("sp", xk[:], UT[OFF["ckv"]:OFF["ckv"] + 128, T0:T1], writes=["m_xk"])
                    S.dma("sp", kr32[64:96, :], UT[OFF["kr"]:OFF["kr"] + 32, T0:T1], writes=["m_kr"])
                    if do_rope:
                        for (r0, nr, ap) in swap_rows(OFF["kr"], T0, T1, UT, 32)(True):
                            S.dma("act", krs[64 + r0:64 + r0 + nr, :], ap, writes=["m_krs"])
                    for c in range(L // 512 if L >= 512 else 1):
                        w = min(512, L)
                        cs = slice(c * w, (c + 1) * w)
                        pt, pn = b.ps()
                        for kc in range(2):
                            S.op("act", lambda e: e.activation(sqb[:, kc, 0:w], x[:, kc, cs], AF.Square), reads=["m_x"], writes=["m_sq"])
                        for kc in range(2):
                            S.op("pe", lambda e: e.matmul(pt[:, 0:w], ones_bf[:], sqb[:, kc, 0:w], start=(kc == 0), stop=(kc == 1)),
                                 reads=["m_sq", "ones_bf"], writes=[pn], pe_acc=True)
                        rstd_from_sumsq(pt, pn, rs, "m_rs", 128, w, 1.0 / 256)
                        for kc in range(2):
                            S.op("dve", lambda e: e.scalar_tensor_tensor(cqn[:, kc, cs], x[:, kc, cs], g_q[:, kc:kc + 1], rs[:, 0:w], ALU.mult, ALU.mult),
                                 reads=["m_x", "m_rs", "g_q"], writes=["cqn"])
                        pt, pn = b.ps()
                        S.op("act", lambda e: e.activation(sqb[:, 0, 0:w], xk[:, cs], AF.Square), reads=["m_xk"], writes=["m_sq"])
                        S.op("pe", lambda e: e.matmul(pt[:, 0:w], ones_bf[:], sqb[:, 0, 0:w], start=True, stop=True), reads=["m_sq", "ones_bf"], writes=[pn])
                        rstd_from_sumsq(pt, pn, rs, "m_rs", 128, w, 1.0 / 128)
                        S.op("dve", lambda e: e.scalar_tensor_tensor(xk[:, cs], xk[:, cs], g_kv[:, 0:1], rs[:, 0:w], ALU.mult, ALU.mult),
                             reads=["m_xk", "m_rs", "g_kv"], writes=["m_xk"])
                        S.op("pool", lambda e: e.tensor_copy(ckvT[:, P + c * w:P + (c + 1) * w], xk[:, cs]), reads=["m_xk"], writes=["ckvT"])
                        if do_rope:
                            S.op("dve", lambda e: e.tensor_tensor(t1[64:96, 0:w], kr32[64:96, cs], rope[64:96, 2, cs], ALU.mult), reads=["m_kr", "rope"], writes=["m_t1"])
                            S.op("pool", lambda e: e.tensor_tensor(krs[64:96, cs], krs[64:96, cs], rope[64:96, 3, cs], ALU.mult), reads=["m_krs", "rope"], writes=["m_krs"])
                            S.op("dve", lambda e: e.tensor_tensor(krT[64:96, P + c * w:P + (c + 1) * w], t1[64:96, 0:w], krs[64:96, cs], ALU.add),
                                 reads=["m_t1", "m_krs"], writes=["krT"])
                        else:
                            S.op("dve", lambda e: e.tensor_copy(krT[64:96, P + c * w:P + (c + 1) * w], kr32[64:96, cs]), reads=["m_kr"], writes=["krT"])
                    if g == "p":
                        S.dma("sp", D["o_ckvT"][l, :, T0:T1], xk[:], reads=["m_xk"], writes=[])
                        S.dma("sp", D["o_krT"][l, :, T0:T1], kr32[64:96, :], reads=["m_kr"], writes=[])
                    S.barrier()
                Vall = b.sb(st, "Vall", [128, NK // 128, 512], BF16)
                for c in range(NK // 128):
                    pt, pn = b.ps()
                    S.op("pe", lambda e: e.matmul(pt[:, :], ckvT[:, c * 128:(c + 1) * 128],
                                                  wukv[:].rearrange("p (h x) -> p h x", x=128)[:, :, 64:128], start=True, stop=True),
                         reads=["ckvT", "wukv"], writes=[pn])
                    if c % 2 == 0:
                        S.op("act", lambda e: e.copy(Vall[:, c, :], pt[:, :]), reads=[pn], writes=["Vall"])
                    else:
                        S.op("dve", lambda e: e.tensor_copy(Vall[:, c, :], pt[:, :]), reads=[pn], writes=["Vall"])
                S.barrier()
                with ExitStack() as s2:
                    wk = dict(E=[b.sb(s2, "E%d" % i, [128, 512], BF16) for i in range(3)], rec=b.sb(s2, "rec", [64, 512]))
                    kTh = [b.sb(s2, "kTh%d" % i, [96, NK], BF16) for i in range(2)]
                    qh = [b.sb(s2, "qh%d" % i, [96, L], BF16) for i in range(2)]
                    obs = [b.sb(s2, "ob%d" % i, [64, 512], BF16) for i in range(2)]
                    t2 = b.sb(s2, "m_t2", [96, 512]); t3 = b.sb(s2, "m_t3", [96, 512])
                    nu = 0
                    W = min(512, L)
                    for h in range(8):
                        kt, ktn = kTh[h % 2], "kTh%d" % (h % 2)
                        q_, q_n = qh[h % 2], "qh%d" % (h % 2)
                        for c in range((NK + 511) // 512):
                            w = min(512, NK - c * 512)
                            pt, pn = b.ps()
                            S.op("pe", lambda e: e.matmul(pt[0:64, 0:w], wukv[:, h * 128:h * 128 + 64], ckvT[:, c * 512:c * 512 + w], start=True, stop=True),
                                 reads=["wukv", "ckvT"], writes=[pn])
                            S.op("act", lambda e: e.copy(kt[0:64, c * 512:c * 512 + w], pt[0:64, 0:w]), reads=[pn], writes=[ktn])
                        S.op("pool", lambda e: e.tensor_copy(kt[64:96, :], krT[64:96, :]), reads=["krT"], writes=[ktn])
                        for c in range(L // W):
                            cs = slice(c * W, (c + 1) * W)
                            pt, pn = b.ps()
                            for kc in range(2):
                                S.op("pe", lambda e: e.matmul(pt[0:96, 0:W], wuq[:, kc, h * 96:(h + 1) * 96], cqn[:, kc, cs], start=(kc == 0), stop=(kc == 1)),
                                     reads=["wuq", "cqn"], writes=[pn], pe_acc=True)
                            S.op("act", lambda e: e.copy(q_[0:64, cs], pt[0:64, 0:W]), reads=[pn], writes=[q_n])
                            if do_rope:
                                pt2, pn2 = b.ps()
                                for kc in range(2):
                                    S.op("pe", lambda e: e.matmul(pt2[0:96, 0:W], wuqs[:, kc, h * 96:(h + 1) * 96], cqn[:, kc, cs], start=(kc == 0), stop=(kc == 1)),
                                         reads=["wuqs", "cqn"], writes=[pn2], pe_acc=True)
                                S.op("dve", lambda e: e.tensor_tensor(t2[64:96, 0:W], pt[64:96, 0:W], rope[64:96, 2, cs], ALU.mult), reads=[pn, "rope"], writes=["m_t2"])
                                S.op("dve", lambda e: e.tensor_tensor(t3[64:96, 0:W], pt2[64:96, 0:W], rope[64:96, 3, cs], ALU.mult), reads=[pn2, "rope"], writes=["m_t3"])
                                S.op("pool", lambda e: e.tensor_tensor(q_[64:96, cs], t2[64:96, 0:W], t3[64:96, 0:W], ALU.add), reads=["m_t2", "m_t3"], writes=[q_n])
                            else:
                                S.op("dve", lambda e: e.tensor_copy(q_[64:96, cs], pt[64:96, 0:W]), reads=[pn], writes=[q_n])
                        for c in range(L // W):
                            chunks = [(kt[:, kc * 128:(kc + 1) * 128], ktn, Vall[:, kc, h * 64:(h + 1) * 64], "Vall", 128, None) for kc in range(NK // 128)]
                            ob, obn = obs[nu % 2], "ob%d" % (nu % 2)
                            nu += 1
                            attn_unit(q_[:, c * W:(c + 1) * W], q_n, 96, W, chunks, scale, ob[:, 0:W], obn, wk)
                            S.dma("sp", OT[2, h * 64:(h + 1) * 64, T0 + c * W:T0 + (c + 1) * W], ob[:, 0:W], reads=[obn], writes=[])
                    S.barrier()

    def stage_hgrn(g, l):
        G = GR[g]
        T, L, P, nseq = G["T"], G["L"], G["P"], G["nseq"]
        UT, UTM, OT = D["UT_" + g], D["UTM_" + g], D["OT_" + g]
        NTL = L // 128
        ident, triU, triL, sL, sU, csel = (cm[:, i, :] for i in range(6))
        with ExitStack() as st:
            lbr = b.sb(st, "lbr", [128, DEPTH, 2, 512])
            lbb = b.sb(st, "lbb", [128, 2, 512]); oml = b.sb(st, "oml", [128, 2, 512]); den = b.sb(st, "lden", [128, 2, 512])
            S.dma("sp", lbr[:], D["lbT"].partition_broadcast(128), writes=["lbr"])
            S.op("act", lambda e: e.activation(lbr[:], lbr[:], AF.Exp), reads=["lbr"], writes=["lbr"])
            S.op("dve", lambda e: e.tensor_tensor(den[:], lbr[:, 0], lbr[:, 1], ALU.add), reads=["lbr"], writes=["lden"])
            S.op("dve", lambda e: e.reciprocal(den[:], den[:]), reads=["lden"], writes=["lden"])
            if l == 0:
                S.op("pool", lambda e: e.memset(lbb[:], 0.0), writes=["lbb"])
            else:
                S.op("dve", lambda e: e.tensor_tensor(lbb[:], lbr[:, 1], den[:], ALU.mult), reads=["lbr", "lden"], writes=["lbb"])
            S.op("dve", lambda e: e.tensor_scalar(oml[:], lbb[:], -1.0, 1.0, ALU.mult, ALU.add), reads=["lbb"], writes=["oml"])
            hn = b.sb(st, "hn", [128, 4]); S.dma("sp", hn[:], D["hgrn_normT"][l], writes=["hn"])
            Sst = b.sb(st, "Sst", [128, 2, 4, 128])
            SSTN = ["Sst"] + ["Sst%d%d" % (d_, h_) for d_ in range(2) for h_ in range(4)]
            for sq_ in range(nseq):
                T0 = sq_ * L
                if P:
                    S.dma("sp", Sst[:], D["st_hgrn"][l].rearrange("d h k v -> k d h v"), writes=SSTN)
                else:
                    S.op("pool", lambda e: e.memset(Sst[:], 0.0), writes=SSTN)
                for d in range(2):
                    order = range(NTL) if d == 0 else range(NTL - 1, -1, -1)
                    M_in, M_ex = (triU, sL) if d == 0 else (triL, sU)
                    for ti in order:
                        t0 = T0 + ti * 128
                        with ExitStack() as s2:
                            fpre = b.sb(s2, "fpre", [128, 512]); vv = b.sb(s2, "vv", [128, 512]); tt_ = b.sb(s2, "h_t", [128, 512])
                            gg = b.sb(s2, "h_g", [128, 512]); kk = b.sb(s2, "h_k", [128, 512]); kt_ = b.sb(s2, "h_kt", [128, 512]); kh = b.sb(s2, "h_kh", [128, 512])
                            khm = b.sb(s2, "h_khm", [128, 4, 128]); qT = b.sb(s2, "h_qT", [128, 4, 128]); qt = b.sb(s2, "h_qt", [128, 4, 128])
                            eb = b.sb(s2, "h_eb", [128, 132]); ktT = b.sb(s2, "h_ktT", [128, 128]); AT = b.sb(s2, "h_AT", [128, 128])
                            ofb = b.sb(s2, "h_of", [128, 4, 128], BF16)
                            osum = b.sb(s2, "h_os", [128, 4, 128]); gT = b.sb(s2, "h_gT", [128, 4, 128])
                            osq = b.sb(s2, "h_osq", [128, 4, 128], BF16); ors = b.sb(s2, "h_ors", [128, 512])
                            if d == 1:
                                S.dma("act", ofb[:], OT[0, :, t0:t0 + 128].rearrange("(h p) t -> p h t", p=128), reads=["OTa"], writes=["h_of"])
                                S.dma("act", gT[:], UT[OFF["hg"]:OFF["hg"] + 512, t0:t0 + 128].rearrange("(h p) t -> p h t", p=128), reads=["UT"], writes=["h_gT"])
                                S.op("act", lambda e: e.activation(gT[:], gT[:], AF.Silu), reads=["h_gT"], writes=["h_gT"])
                            S.dma("sp", fpre[:], UTM[t0:t0 + 128, d * 512:(d + 1) * 512], reads=["UTM"], writes=["fpre"])
                            S.dma("sp", vv[:], UTM[t0:t0 + 128, 1024:1536], reads=["UTM"], writes=["vv"])
                            S.dma("act", qT[:], UT[0:512, t0:t0 + 128].rearrange("(h p) t -> p h t", p=128), reads=["UT"], writes=["h_qT"])
                            S.op("act", lambda e: e.activation(qT[:], qT[:], AF.Silu), reads=["h_qT"], writes=["h_qT"])
                            S.op("act", lambda e: e.activation(tt_[:], fpre[:], AF.Sigmoid), reads=["fpre"], writes=["h_t"])
                            S.op("dve", lambda e: e.tensor_tensor(tt_[:], tt_[:], oml[:, d], ALU.mult), reads=["h_t", "oml"], writes=["h_t"])
                            S.op("dve", lambda e: e.scalar_tensor_tensor(gg[:], tt_[:], 1e-30, lbb[:, d], ALU.max, ALU.add), reads=["h_t", "lbb"], writes=["h_g"])
                            S.op("act", lambda e: e.activation(gg[:], gg[:], AF.Ln), reads=["h_g"], writes=["h_g"])
                            S.op("dve", lambda e: e.tensor_tensor(kk[:], oml[:, d], tt_[:], ALU.subtract), reads=["h_t", "oml"], writes=["h_k"])
                            pb, pbn = b.ps()
                            S.op("pe", lambda e: e.matmul(pb[:], M_in, gg[:], start=True, stop=True), reads=["cm", "h_g"], writes=[pbn])
                            S.op("act", lambda e: e.activation(kt_[:], pb[:], AF.Exp, scale=-1.0), reads=[pbn], writes=["h_kt"])
                            S.op("dve", lambda e: e.tensor_tensor(kt_[:], kt_[:], kk[:], ALU.mult), reads=["h_kt", "h_k"], writes=["h_kt"])
                            pr, prn = b.ps()
                            S.op("pe", lambda e: e.matmul(pr[:], M_ex, gg[:], start=True, stop=True), reads=["cm", "h_g"], writes=[prn])
                            S.op("act", lambda e: e.activation(kh[:], pr[:], AF.Exp), reads=[prn], writes=["h_kh"])
                            S.op("dve", lambda e: e.tensor_tensor(kh[:], kh[:], kk[:], ALU.mult), reads=["h_kh", "h_k"], writes=["h_kh"])
                            for h in range(4):
                                hs = slice(h * 128, (h + 1) * 128)
                                pbt, pbtn = b.ps()
                                S.op("pe", lambda e: e.matmul(pbt[:, 0:128], gg[:, hs], M_in, start=True, stop=True), reads=["h_g", "cm"], writes=[pbtn])
                                S.op("pe", lambda e: e.matmul(pbt[:, 128:132], gg[:, hs], csel[:, 0:4], start=True, stop=True), reads=["h_g", "cm"], writes=[pbtn])
                                S.op("act", lambda e: e.activation(eb[:], pbt[:, 0:132], AF.Exp), reads=[pbtn], writes=["h_eb"])
                                S.op("dve", lambda e: e.tensor_tensor(qt[:, h, :], qT[:, h, :], eb[:, 0:128], ALU.mult), reads=["h_qT", "h_eb"], writes=["h_qt"])
                                pk, pkn = b.ps()
                                S.op("pe", lambda e: e.matmul(pk[:, 0:128], kt_[:, hs], ident, start=True, stop=True), reads=["h_kt", "cm"], writes=[pkn])
                                S.op("act", lambda e: e.copy(ktT[:], pk[:, 0:128]), reads=[pkn], writes=["h_ktT"])
                                pa, pan = b.ps()
                                S.op("pe", lambda e: e.matmul(pa[:, 0:128], ktT[:], qt[:, h, :], start=True, stop=True), reads=["h_ktT", "h_qt"], writes=[pan])
                                S.op("dve", lambda e: e.tensor_tensor(AT[:], pa[:, 0:128], M_in, ALU.mult), reads=[pan, "cm"], writes=["h_AT"])
                                for j in range(4):
                                    S.op("pool", lambda e: e.tensor_scalar(khm[:, j, :], kh[:, hs], csel[:, j:j + 1], None, ALU.mult), reads=["h_kh", "cm"], writes=["h_khm"])
                                po_, pon_ = b.acc(0)
                                S.op("pe", lambda e: e.matmul(po_[:, 0:128], vv[:, hs], AT[:], start=True, stop=False), reads=["vv", "h_AT"], writes=[pon_])
                                corder = range(4) if d == 0 else range(3, -1, -1)
                                for n_, j in enumerate(corder):
                                    js = slice(j * 32, (j + 1) * 32)
                                    S.op("pe", lambda e: e.matmul(po_[:, js], Sst[:, d, h, :], qt[:, h, js], start=False, stop=(n_ == 3)),
                                         reads=["Sst%d%d" % (d, h), "h_qt"], writes=[pon_], pe_acc=True)
                                    ps_, psn2 = b.ps()
                                    S.op("pe", lambda e: e.matmul(ps_[:, 0:128], khm[:, j, :], vv[:, hs], start=True, stop=True), reads=["h_khm", "vv"], writes=[psn2])
                                    S.op("dve", lambda e: e.scalar_tensor_tensor(Sst[:, d, h, :], Sst[:, d, h, :], eb[:, 128 + j:129 + j], ps_[:, 0:128], ALU.mult, ALU.add),
                                         reads=[psn2, "h_eb", "Sst"], writes=["Sst%d%d" % (d, h)])
                                if d == 0:
                                    S.op("act", lambda e: e.copy(ofb[:, h, :], po_[:, 0:128]), reads=[pon_], writes=["h_of"])
                                else:
                                    S.op("dve", lambda e: e.tensor_tensor(osum[:, h, :], ofb[:, h, :], po_[:, 0:128], ALU.add), reads=[pon_, "h_of"], writes=["h_os"])
                            if d == 1:
                                S.op("act", lambda e: e.activation(osq[:], osum[:], AF.Square), reads=["h_os"], writes=["h_osq"])
                                pn_, pnn = b.ps()
                                for h in range(4):
                                    S.op("pe", lambda e: e.matmul(pn_[:, h * 128:(h + 1) * 128], ones_bf[:], osq[:, h, :], start=True, stop=True),
                                         reads=["h_osq", "ones_bf"], writes=[pnn])
                                rstd_from_sumsq(pn_, pnn, ors, "h_ors", 128, 512, 1.0 / 128)
                                for h in range(4):
                                    S.op("dve", lambda e: e.scalar_tensor_tensor(osum[:, h, :], osum[:, h, :], hn[:, h:h + 1], ors[:, h * 128:(h + 1) * 128], ALU.mult, ALU.mult),
                                         reads=["h_os", "hn", "h_ors"], writes=["h_os"])
                                S.op("dve", lambda e: e.tensor_tensor(ofb[:], osum[:], gT[:], ALU.mult), reads=["h_os", "h_gT"], writes=["h_of"])
                            S.dma("sp", OT[0, :, t0:t0 + 128].rearrange("(h p) t -> p h t", p=128), ofb[:], reads=["h_of"], writes=["OTa"])
                            S.barrier()
                if g == "p":
                    S.dma("sp", D["o_hgrn"][l, sq_].rearrange("d h k v -> k d h v"), Sst[:], reads=SSTN, writes=["ohg"])
                    S.barrier()

    def epilogue(st, z, zn, xt, xn, gco, Xdst, c0, wkn):
        sq, rs, tmp = wkn
        for kc in range(8):
            S.op("act", lambda e: e.activation(sq[:, kc, :], z[:, kc, :], AF.Square), reads=[zn], writes=["e_sq"])
        pt, pn = b.ps()
        for kc in range(8):
            S.op("pe", lambda e: e.matmul(pt[:], ones_bf[:], sq[:, kc, :], start=(kc == 0), stop=(kc == 7)), reads=["e_sq", "ones_bf"], writes=[pn], pe_acc=True)
        rstd_from_sumsq(pt, pn, rs, "e_rs", 128, 512, 1.0 / DM)
        for kc in range(8):
            S.op("dve", lambda e: e.tensor_tensor(tmp[:], z[:, kc, :], rs[:], ALU.mult), reads=[zn, "e_rs"], writes=["e_tmp"])
            S.op("dve", lambda e: e.scalar_tensor_tensor(xt[:, kc, :], tmp[:], gco[:, kc:kc + 1], xt[:, kc, :], ALU.mult, ALU.add),
                 reads=["e_tmp", "coef", xn], writes=[xn])
        S.dma("sp", Xdst[:, c0:c0 + 512].rearrange("(kc p) t -> p kc t", p=128), xt[:], reads=[xn], writes=[])

    def stage_merge(g, l, Xsrc, Xdst):
        G = GR[g]
        T = G["T"]
        UT, OT = D["UT_" + g], D["OT_" + g]
        with ExitStack() as st:
            Wb = b.sb(st, "Wb", [128, 4, 4, 1024], BF16)
            Wo = b.sb(st, "Wo", [128, 8, 1024], BF16)
            S.dma("pool", Wb[:], D["w_branch"][l].rearrange("n (kc p) c -> p n kc c", p=128), writes=["Wb"])
            S.dma("pool", Wo[:], D["w_out"][l].rearrange("(kc p) c -> p kc c", p=128), writes=["Wo"])
            ot = b.sb(st, "ot", [128, 4, 4, 512], BF16)
            yp = b.sb(st, "yp", [128, 8, 512], BF16)
            gts = [b.sb(st, "gt%d" % i, [128, 512]) for i in range(3)]
            accf = b.sb(st, "accf", [128, 512]); tm2 = b.sb(st, "tm2", [128, 512])
            z = b.sb(st, "z", [128, 8, 512]); xt = b.sb(st, "xt", [128, 8, 512])
            wkn = (b.sb(st, "e_sq", [128, 8, 512], BF16), b.sb(st, "e_rs", [128, 512]), b.sb(st, "e_tmp", [128, 512]))
            ng = 0
            for tt in range(T // 512):
                cs = slice(tt * 512, (tt + 1) * 512)
                S.dma("sp", ot[:], OT[:, :, cs].rearrange("n (kc p) t -> p n kc t", p=128), writes=["ot"])
                S.dma("act", xt[:], Xsrc[:, cs].rearrange("(kc p) t -> p kc t", p=128), writes=["xt"])
                for dmc in range(8):
                    for n in range(4):
                        gt, gtn = gts[ng % 3], "gt%d" % (ng % 3)
                        ng += 1
                        r0 = OFF["gates"] + n * 1024 + dmc * 128
                        S.dma("sp", gt[:], UT[r0:r0 + 128, cs], writes=[gtn])
                        S.op("act", lambda e: e.activation(gt[:], gt[:], AF.Sigmoid), reads=[gtn], writes=[gtn])
                        pt, pn = b.ps()
                        for kc in range(4):
                            S.op("pe", lambda e: e.matmul(pt[:], Wb[:, n, kc, dmc * 128:(dmc + 1) * 128], ot[:, n, kc, :], start=(kc == 0), stop=(kc == 3)),
                                 reads=["Wb", "ot"], writes=[pn], pe_acc=True)
                        if n == 0:
                            S.op("dve", lambda e: e.tensor_tensor(accf[:], pt[:], gt[:], ALU.mult), reads=[pn, gtn], writes=["accf"])
                        elif n < 3:
                            S.op("dve", lambda e: e.tensor_tensor(tm2[:], pt[:], gt[:], ALU.mult), reads=[pn, gtn], writes=["tm2"])
                            S.op("pool", lambda e: e.tensor_tensor(accf[:], accf[:], tm2[:], ALU.add), reads=["tm2", "accf"], writes=["accf"])
                        else:
                            S.op("dve", lambda e: e.tensor_tensor(tm2[:], pt[:], gt[:], ALU.mult), reads=[pn, gtn], writes=["tm2"])
                            S.op("pool", lambda e: e.tensor_tensor(yp[:, dmc, :], accf[:], tm2[:], ALU.add), reads=["tm2", "accf"], writes=["yp"])
                for oc in range(8):
                    pt, pn = b.ps()
                    for kc in range(8):
                        S.op("pe", lambda e: e.matmul(pt[:], Wo[:, kc, oc * 128:(oc + 1) * 128], yp[:, kc, :], start=(kc == 0), stop=(kc == 7)),
                             reads=["Wo", "yp"], writes=[pn], pe_acc=True)
                    S.op("act", lambda e: e.copy(z[:, oc, :], pt[:]), reads=[pn], writes=["z"])
                epilogue(st, z, "z", xt, "xt", coef[:, l, G["cond"], 2], Xdst, tt * 512, wkn)
            S.barrier()

    def stage_mlp(g, l, Xsrc, Xdst):
        G = GR[g]
        T = G["T"]
        with ExitStack() as st:
            h2 = stage_h(st, g, l, Xsrc, 3, 4)
            W2 = [b.sb(st, "W2_%d" % i, [128, 8, 1024], BF16) for i in range(2)]
            n2 = 0
            w1 = [b.sb(st, "w1_%d" % i, [128, 8, 512], BF16) for i in range(2)]
            hid = b.sb(st, "hid", [128, 32, 512], BF16)
            rl = [b.sb(st, "rl%d" % i, [128, 512]) for i in range(2)]
            z = b.sb(st, "z", [128, 8, 512]); xt = b.sb(st, "xt", [128, 8, 512])
            wkn = (b.sb(st, "e_sq", [128, 8, 512], BF16), b.sb(st, "e_rs", [128, 512]), b.sb(st, "e_tmp", [128, 512]))
            nw = 0
            nr = 0
            for tt in range(T // 512):
                cs = slice(tt * 512, (tt + 1) * 512)
                S.dma("act", xt[:], Xsrc[:, cs].rearrange("(kc p) t -> p kc t", p=128), writes=["xt"])
                for cg in range(8):
                    w, wn = w1[nw % 2], "w1_%d" % (nw % 2)
                    nw += 1
                    S.dma("pool", w[:], D["w_mlp_in"][l, :, cg * 512:(cg + 1) * 512].rearrange("(kc p) c -> p kc c", p=128), writes=[wn])
                    for cc in range(4):
                        pt, pn = b.ps()
                        for kc in range(8):
                            S.op("pe", lambda e: e.matmul(pt[:], w[:, kc, cc * 128:(cc + 1) * 128], h2[:, kc, cs], start=(kc == 0), stop=(kc == 7)),
                                 reads=[wn, "hT"], writes=[pn], pe_acc=True)
                        r, rn = rl[nr % 2], "rl%d" % (nr % 2)
                        nr += 1
                        S.op("act", lambda e: e.activation(r[:], pt[:], AF.Relu), reads=[pn], writes=[rn])
                        S.op("dve", lambda e: e.tensor_tensor(hid[:, cg * 4 + cc, :], r[:], r[:], ALU.mult), reads=[rn], writes=["hid"])
                for q4 in range(4):
                    w2, w2n = W2[n2 % 2], "W2_%d" % (n2 % 2)
                    n2 += 1
                    S.dma("pool", w2[:], D["w_mlp_out"][l, q4 * 1024:(q4 + 1) * 1024, :].rearrange("(kc p) c -> p kc c", p=128), writes=[w2n])
                    for oc in range(8):
                        pt, pn = b.ps()
                        for fc in range(8):
                            S.op("pe", lambda e: e.matmul(pt[:], w2[:, fc, oc * 128:(oc + 1) * 128], hid[:, q4 * 8 + fc, :], start=(fc == 0), stop=(fc == 7)),
                                 reads=[w2n, "hid"], writes=[pn], pe_acc=True)
                        if q4 == 0:
                            S.op("act", lambda e: e.copy(z[:, oc, :], pt[:]), reads=[pn], writes=["z%d" % oc, "z"])
                        else:
                            S.op("dve", lambda e: e.tensor_tensor(z[:, oc, :], z[:, oc, :], pt[:], ALU.add), reads=[pn, "z%d" % oc], writes=["z%d" % oc, "z"])
                epilogue(st, z, "z", xt, "xt", coef[:, l, G["cond"], 5], Xdst, tt * 512, wkn)
            S.barrier()

    for g in ("s", "p"):
        X = D["xT_" + g]
        for l in range(DEPTH):
            if "proj" in STAGES:
                with ExitStack() as st:
                    hT = stage_h(st, g, l, X, 0, 1)
                    stage_proj(g, l, hT)
            if "hgrn" in STAGES:
                stage_hgrn(g, l)
            if "swa" in STAGES:
                stage_gqa_like(g, l, "swa")
            if "mla" in STAGES:
                stage_mla(g, l)
            if "gqa" in STAGES:
                stage_gqa_like(g, l, "gqa")
            if "merge" in STAGES:
                stage_merge(g, l, X, D["X1_" + g])
            Xn = D["yT_" + g] if l == DEPTH - 1 else D["X2_%d_%s" % (l, g)]
            if "mlp" in STAGES:
                stage_mlp(g, l, D["X1_" + g], Xn)
            X = Xn
    S.barrier()
    return nc, b


_CACHE = {}


def _consts():
    nf = 16
    t = np.arange(2048)
    row, col = (t // 64).astype(np.float32), (t % 64).astype(np.float32)
    rope = np.zeros((4, 96, 2048), np.float32)
    def fill(ci, si, r0, nf):
        inv = (10000.0 ** (-np.arange(nf, dtype=np.float32) / nf)).astype(np.float32)
        ar = (row[None, :] * inv[:, None]).astype(np.float32)
        ac = (col[None, :] * inv[:, None]).astype(np.float32)
        for k, a in enumerate((ar, ac)):
            b0 = r0 + k * 2 * nf
            rope[ci, b0:b0 + nf] = np.cos(a); rope[ci, b0 + nf:b0 + 2 * nf] = np.cos(a)
            rope[si, b0:b0 + nf] = -np.sin(a); rope[si, b0 + nf:b0 + 2 * nf] = np.sin(a)
    fill(0, 1, 0, 16)
    fill(2, 3, 64, 8)
    s_ = np.arange(128)[:, None]; t_ = np.arange(128)[None, :]
    same = (s_ // 32) == (t_ // 32)
    cm = np.zeros((6, 128, 128), np.float32)
    cm[0] = np.eye(128)
    cm[1] = same & (s_ <= t_)
    cm[2] = same & (s_ >= t_)
    cm[3] = same & (s_ > t_)
    cm[4] = same & (s_ < t_)
    cm[5][:, 0:4] = (s_ // 32) == np.arange(4)[None, :]
    j = np.arange(128)[:, None]; i = (np.arange(512) % 128)[None, :]
    swm = np.stack([(j >= i), (j <= i)]).astype(np.float32).astype(ml_dtypes.bfloat16)
    return rope, cm, swm


def _perm(nd):
    q = nd // 4
    return np.concatenate([np.arange(q, 2 * q), np.arange(0, q), np.arange(3 * q, 4 * q), np.arange(2 * q, 3 * q)])


def kernel(**inp):
    f = lambda a: np.ascontiguousarray(np.asarray(a, dtype=np.float32))
    I = {k: f(v) for k, v in inp.items()}
    if "prog" not in _CACHE:
        _CACHE["prog"] = build_program()
    nc, b = _CACHE["prog"]
    rope, cm, swm = _consts()
    fm = lambda v, n: f(v.reshape(v.shape[0], n, 128).transpose(0, 2, 1))
    shared = dict(
        w_ada=I["w_ada"], b_adaT=fm(I["b_ada"], 48),
        gains=f(np.stack([fm(I[k], 8) for k in ("norm_mix_pre", "norm_mix_post", "norm_mlp_pre", "norm_mlp_post")], axis=2)),
        w_in=I["w_in"], lbT=f(np.stack([I["hgrn_lb_fwd"], I["hgrn_lb_bwd"]], axis=1)),
        hgrn_normT=fm(I["hgrn_norm"], 4), sink=f(I["swa_sink"][:, None, :]),
        mla_qnT=fm(I["mla_q_norm"], 2), mla_kvn=f(I["mla_kv_norm"][:, :, None]),
        w_uq=I["mla_w_uq"], w_ukv=I["mla_w_ukv"],
        gqa_qn=f(np.stack([I["gqa_q_norm"], I["gqa_q_norm"][:, _perm(64)]], axis=2)),
        gqa_kn=f(np.stack([I["gqa_k_norm"], I["gqa_k_norm"][:, _perm(64)]], axis=2)),
        w_branch=I["w_branch"], w_out=I["w_out"], w_mlp_in=I["w_mlp_in"], w_mlp_out=I["w_mlp_out"],
        rope64=rope, cmat=cm, swamask=swm,
    )
    wsw = I["mla_w_uq"].reshape(DEPTH, 256, 8, 96).copy()
    wsw[..., 64:96] = wsw[..., 64:96][..., _perm(32)]
    shared["w_uq_sw"] = f(wsw.reshape(DEPTH, 256, 768))
    in_maps = []
    for i in range(NCORE):
        bb = i % 2
        m = dict(shared)
        m["xT_s"] = f(I["x_sample"][bb].T)
        m["xT_p"] = f(I["x_prompt"][4 * i:4 * i + 4].reshape(1024, DM).T)
        cc = np.stack([I["c_ctx"], I["c"][bb]], axis=1)
        m["cT"] = f(cc.reshape(8, 128, 2).transpose(1, 0, 2))
        m["st_hgrn"] = f(I["state_hgrn"][bb])
        m["c_swa_kT"] = f(I["cache_swa_k"][bb].transpose(0, 2, 3, 1))
        m["c_swa_v"] = f(I["cache_swa_v"][bb].reshape(DEPTH, 512, 128))
        m["c_ckvT"] = f(I["cache_mla_ckv"][bb].transpose(0, 2, 1))
        m["c_krT"] = f(I["cache_mla_kr"][bb].transpose(0, 2, 1))
        m["c_gqa_kT"] = f(I["cache_gqa_k"][bb].transpose(0, 2, 3, 1))
        m["c_gqa_v"] = f(I["cache_gqa_v"][bb].reshape(DEPTH, 512, 128))
        in_maps.append(m)
    res = run_bass_kernel_spmd(nc, in_maps, core_ids=list(range(NCORE)))
    R = res.results
    y_prompt = np.concatenate([R[i]["yT_p"].T.reshape(4, 256, DM) for i in range(NCORE)], axis=0)
    y_sample = np.stack([R[0]["yT_s"].T, R[1]["yT_s"].T], axis=0)
    cat = lambda fn: np.ascontiguousarray(np.concatenate([fn(R[i]) for i in range(NCORE)], axis=0).astype(np.float32))
    n_hgrn = cat(lambda r: r["o_hgrn"].transpose(1, 0, 2, 3, 4, 5))
    kT = lambda a: a.reshape(DEPTH, 2, 64, 4, 256).transpose(3, 0, 4, 1, 2)
    vv = lambda a: a.reshape(DEPTH, 4, 256, 2, 64).transpose(1, 0, 2, 3, 4)
    n_swa_k = cat(lambda r: kT(r["o_swa_kT"]))
    n_swa_v = cat(lambda r: vv(r["o_swa_v"]))
    n_ckv = cat(lambda r: r["o_ckvT"].reshape(DEPTH, 128, 4, 256).transpose(2, 0, 3, 1))
    n_kr = cat(lambda r: r["o_krT"].reshape(DEPTH, 32, 4, 256).transpose(2, 0, 3, 1))
    n_gqa_k = cat(lambda r: kT(r["o_gqa_kT"]))
    n_gqa_v = cat(lambda r: vv(r["o_gqa_v"]))
    return (np.ascontiguousarray(y_prompt.astype(np.float32)), np.ascontiguousarray(y_sample.astype(np.float32)),
            n_hgrn, n_swa_k, n_swa_v, n_ckv, n_kr, n_gqa_k, n_gqa_v)
```

```python
import numpy as np
from contextlib import ExitStack
import ml_dtypes
import concourse.bass as bass
import concourse.mybir as mybir
from concourse.bass_utils import run_bass_kernel_spmd

F32 = mybir.dt.float32
BF16 = mybir.dt.bfloat16
AF = mybir.ActivationFunctionType
ALU = mybir.AluOpType

DM = 1024
DEPTH = 2
NCORE = 8
EPS = 1e-6
OFF = dict(hq=0, ff=512, fb=1024, hi=1536, hg=2048, sq=2560, sk=3072, sv=3200, cq=3328, ckv=3584, kr=3712,
           gq=3744, gk=4256, gv=4384, gates=4512)
D_IN = 8608
UTM_COLS = 1792
STAGES = {"proj", "hgrn", "swa", "mla", "gqa", "merge", "mlp"}


class Sched:
    NDMA = 24

    def __init__(self, nc, es):
        self.nc = nc
        self.eng = {"pe": nc.tensor, "act": nc.scalar, "dve": nc.vector, "pool": nc.gpsimd, "sp": nc.sync}
        self.sem = {k: es.enter_context(nc.semaphore("s_" + k)) for k in ("pe", "act", "dve", "pool")}
        self.cnt = {k: 0 for k in self.sem}
        self.dsem = [es.enter_context(nc.semaphore("d%d" % i)) for i in range(self.NDMA)]
        self.dcnt = [0] * self.NDMA
        self.dnext = 0
        self.seen = {k: {} for k in self.eng}
        self.lastw = {}
        self.reads = {}
        self.n_instr = 0

    def _sem_of(self, key):
        return self.sem[key] if isinstance(key, str) else self.dsem[key[1]]

    def _wait(self, e, tok):
        key, val = tok
        if self.seen[e].get(key, 0) >= val:
            return
        self.eng[e].wait_ge(self._sem_of(key), val)
        self.seen[e][key] = val

    def _deps(self, e, reads, writes, pe_acc=False):
        best = {}
        for r in reads:
            t = self.lastw.get(r)
            if t is not None and best.get(t[0], 0) < t[1]:
                best[t[0]] = t[1]
        for w in writes:
            t = self.lastw.get(w)
            if t is not None and best.get(t[0], 0) < t[1]:
                if not (pe_acc and t[0] == "pe"):
                    best[t[0]] = t[1]
            for t in self.reads.get(w, ()):
                if best.get(t[0], 0) < t[1]:
                    best[t[0]] = t[1]
        for key, val in best.items():
            self._wait(e, (key, val))

    def _record(self, tok, reads, writes):
        for r in reads:
            lst = self.reads.setdefault(r, [])
            lst[:] = [t for t in lst if t[0] != tok[0]]
            lst.append(tok)
        for w in writes:
            self.lastw[w] = tok
            self.reads[w] = []

    def op(self, e, fn, reads=(), writes=(), pe_acc=False):
        self._deps(e, reads, writes, pe_acc)
        ins = fn(self.eng[e])
        self.cnt[e] += 1
        ins.then_inc(self.sem[e], 1)
        self._record((e, self.cnt[e]), reads, writes)
        self.n_instr += 1
        return ins

    def dma(self, q, out, in_, reads=(), writes=()):
        i = self.dnext
        self.dnext = (self.dnext + 1) % self.NDMA
        if self.dcnt[i] > 0:
            self._wait(q, (("d", i), self.dcnt[i]))
        self._deps(q, reads, writes)
        ins = self.eng[q].dma_start(out=out, in_=in_)
        self.dcnt[i] += 16
        ins.then_inc(self.dsem[i], 16)
        self._record((("d", i), self.dcnt[i]), reads, writes)
        self.n_instr += 1
        return ins

    def barrier(self):
        best = {}
        for k in self.cnt:
            if self.cnt[k]:
                best[k] = self.cnt[k]
        for i in range(self.NDMA):
            if self.dcnt[i]:
                best[("d", i)] = self.dcnt[i]
        for e in self.eng:
            for key, val in best.items():
                self._wait(e, (key, val))
        self.lastw = {}
        self.reads = {}


class B:
    def __init__(self, nc, es):
        self.nc, self.es = nc, es
        self.S = Sched(nc, es)
        self.D = {}
        self.psn = 0

    def din(self, name, shape, dt=F32):
        self.D[name] = self.nc.dram_tensor(name, list(shape), dt, kind="ExternalInput").ap()
        return self.D[name]

    def dout(self, name, shape, dt=F32):
        self.D[name] = self.nc.dram_tensor(name, list(shape), dt, kind="ExternalOutput").ap()
        return self.D[name]

    def dscr(self, name, shape, dt=F32):
        self.D[name] = self.nc.dram_tensor(name, list(shape), dt, kind="Internal").ap()
        return self.D[name]

    def sb(self, st, name, shape, dt=F32):
        self.uid = getattr(self, "uid", 0) + 1
        return st.enter_context(self.nc.sbuf_tensor("sb%d_%s" % (self.uid, name), list(shape), dt))

    def ps(self):
        i = self.psn % self.nrot
        self.psn += 1
        return self.psum[i], "ps%d" % i

    nrot = 6

    def acc(self, i):
        k = (6, 7, 4, 5)[i]
        return self.psum[k], "ps%d" % k


def build_program():
    nc = bass.Bass("TRN2", target_bir_lowering=False)
    es = ExitStack()
    b = B(nc, es)
    S = b.S
    D = b.D
    GR = {
        "s": dict(T=2048, nseq=1, L=2048, P=512, rope=True, cond=1),
        "p": dict(T=1024, nseq=4, L=256, P=0, rope=False, cond=0),
    }
    for g, G in GR.items():
        T = G["T"]
        b.din("xT_" + g, [DM, T])
        b.dout("yT_" + g, [DM, T])
        b.dscr("X1_" + g, [DM, T])
        for l in range(DEPTH - 1):
            b.dscr("X2_%d_%s" % (l, g), [DM, T])
        b.dscr("UT_" + g, [68 * 128, T])
        b.dscr("UTM_" + g, [T, UTM_COLS])
        b.dscr("OT_" + g, [4, 512, T], BF16)
        b.dscr("YP_" + g, [DM, T], BF16)
    b.din("cT", [128, 8, 2])
    b.din("w_ada", [DEPTH, DM, 6 * DM])
    b.din("b_adaT", [DEPTH, 128, 48])
    b.din("gains", [DEPTH, 128, 4, 8])
    b.din("w_in", [DEPTH, DM, D_IN])
    b.din("lbT", [DEPTH, 2, 512])
    b.din("hgrn_normT", [DEPTH, 128, 4])
    b.din("sink", [DEPTH, 1, 8])
    b.din("mla_qnT", [DEPTH, 128, 2])
    b.din("mla_kvn", [DEPTH, 128, 1])
    b.din("w_uq", [DEPTH, 256, 768])
    b.din("w_uq_sw", [DEPTH, 256, 768])
    b.din("w_ukv", [DEPTH, 128, 1024])
    b.din("gqa_qn", [DEPTH, 64, 2])
    b.din("gqa_kn", [DEPTH, 64, 2])
    b.din("w_branch", [DEPTH, 4, 512, DM])
    b.din("w_out", [DEPTH, DM, DM])
    b.din("w_mlp_in", [DEPTH, DM, 4 * DM])
    b.din("w_mlp_out", [DEPTH, 4 * DM, DM])
    b.din("st_hgrn", [DEPTH, 2, 4, 128, 128])
    b.din("c_swa_kT", [DEPTH, 2, 64, 512])
    b.din("c_swa_v", [DEPTH, 512, 128])
    b.din("c_ckvT", [DEPTH, 128, 512])
    b.din("c_krT", [DEPTH, 32, 512])
    b.din("c_gqa_kT", [DEPTH, 2, 64, 512])
    b.din("c_gqa_v", [DEPTH, 512, 128])
    b.din("rope64", [4, 96, 2048])
    b.din("cmat", [6, 128, 128])
    b.din("swamask", [2, 128, 512], BF16)
    b.dout("o_hgrn", [DEPTH, 4, 2, 4, 128, 128])
    b.dout("o_swa_kT", [DEPTH, 128, 1024])
    b.dout("o_swa_v", [DEPTH, 1024, 128])
    b.dout("o_ckvT", [DEPTH, 128, 1024])
    b.dout("o_krT", [DEPTH, 32, 1024])
    b.dout("o_gqa_kT", [DEPTH, 128, 1024])
    b.dout("o_gqa_v", [DEPTH, 1024, 128])

    b.psum = [es.enter_context(nc.psum_tensor("psb%d" % i, [128, 512], F32)) for i in range(8)]

    cst = ExitStack()
    es.enter_context(cst)
    ones_bf = b.sb(cst, "ones_bf", [128, 128], BF16)
    ones_f = b.sb(cst, "ones_f", [128, 128], F32)
    cm = b.sb(cst, "cm", [128, 6, 128], F32)
    modT = b.sb(cst, "modT", [128, DEPTH, 48, 2], F32)
    gains = b.sb(cst, "gains", [128, DEPTH, 4, 8], F32)
    coef = b.sb(cst, "coef", [128, DEPTH, 2, 6, 8], F32)
    S.op("pool", lambda e: e.memset(ones_bf[:], 1.0), writes=["ones_bf"])
    S.op("pool", lambda e: e.memset(ones_f[:], 1.0), writes=["ones_f"])
    S.dma("sp", cm[:], D["cmat"].rearrange("a p c -> p a c"), writes=["cm"])
    S.dma("sp", gains[:], D["gains"].rearrange("l p a c -> p l a c"), writes=["gains"])

    with ExitStack() as st:
        cT = b.sb(st, "cT", [128, 8, 2])
        scT = b.sb(st, "scT", [128, 8, 2])
        badaT = b.sb(st, "badaT", [128, DEPTH, 48])
        wa = [b.sb(st, "wa%d" % i, [128, 8, 768]) for i in range(2)]
        S.dma("sp", cT[:], D["cT"], writes=["cT"])
        S.dma("sp", badaT[:], D["b_adaT"].rearrange("l p j -> p l j"), writes=["badaT"])
        S.op("act", lambda e: e.activation(scT[:], cT[:], AF.Silu), reads=["cT"], writes=["scT"])
        n = 0
        for l in range(DEPTH):
            pt, pn = b.ps()
            for cg in range(8):
                w = wa[n % 2]
                wn = "wa%d" % (n % 2)
                n += 1
                S.dma("sp" if cg % 2 == 0 else "act", w[:],
                      D["w_ada"][l, :, cg * 768:(cg + 1) * 768].rearrange("(kc p) c -> p kc c", p=128), writes=[wn])
                for jj in range(6):
                    j = cg * 6 + jj
                    for kc in range(8):
                        S.op("pe", lambda e: e.matmul(pt[:, 2 * j:2 * j + 2], w[:, kc, jj * 128:(jj + 1) * 128], scT[:, kc, :],
                                                      start=(kc == 0), stop=(kc == 7)),
                             reads=[wn, "scT"], writes=[pn], pe_acc=True)
            S.op("dve", lambda e: e.tensor_tensor(modT[:, l], pt[:, 0:96].rearrange("p (j c) -> p j c", c=2),
                                                  badaT[:, l].unsqueeze(2).to_broadcast([128, 48, 2]), ALU.add),
                 reads=[pn, "badaT"], writes=["modT"])
        for l in range(DEPTH):
            for c in range(2):
                m = lambda i: modT[:, l, i * 8:(i + 1) * 8, c]
                S.op("dve", lambda e: e.scalar_tensor_tensor(coef[:, l, c, 0], m(1), 1.0, gains[:, l, 0], ALU.add, ALU.mult),
                     reads=["modT", "gains"], writes=["coef"])
                S.op("dve", lambda e: e.tensor_copy(coef[:, l, c, 1], m(0)), reads=["modT"], writes=["coef"])
                S.op("dve", lambda e: e.tensor_tensor(coef[:, l, c, 2], m(2), gains[:, l, 1], ALU.mult), reads=["modT", "gains"], writes=["coef"])
                S.op("dve", lambda e: e.scalar_tensor_tensor(coef[:, l, c, 3], m(4), 1.0, gains[:, l, 2], ALU.add, ALU.mult),
                     reads=["modT", "gains"], writes=["coef"])
                S.op("dve", lambda e: e.tensor_copy(coef[:, l, c, 4], m(3)), reads=["modT"], writes=["coef"])
                S.op("dve", lambda e: e.tensor_tensor(coef[:, l, c, 5], m(5), gains[:, l, 3], ALU.mult), reads=["modT", "gains"], writes=["coef"])
        S.barrier()

    def rstd_from_sumsq(pt, pn, out, outn, npart, ncol, inv_n):
        S.op("act", lambda e: e.activation(out[0:npart, 0:ncol], pt[0:npart, 0:ncol], AF.Sqrt, bias=epsb[0:npart, :], scale=inv_n),
             reads=[pn, "epsb"], writes=[outn])
        S.op("dve", lambda e: e.reciprocal(out[0:npart, 0:ncol], out[0:npart, 0:ncol]), reads=[outn], writes=[outn])

    epsb = b.sb(cst, "epsb", [128, 1], F32)
    S.op("pool", lambda e: e.memset(epsb[:], EPS), writes=["epsb"])

    def norm_mod_tile(st_unused, xt, xn, hT, hn, t0, a_ap, sh_ap, tmp, sq, rs):
        for kc in range(8):
            S.op("act", lambda e: e.activation(sq[:, kc, :], xt[:, kc, :], AF.Square), reads=[xn], writes=["sq"])
        pt, pn = b.ps()
        for kc in range(8):
            S.op("pe", lambda e: e.matmul(pt[:], ones_bf[:], sq[:, kc, :], start=(kc == 0), stop=(kc == 7)),
                 reads=["sq", "ones_bf"], writes=[pn], pe_acc=True)
        rstd_from_sumsq(pt, pn, rs, "rs", 128, 512, 1.0 / DM)
        for kc in range(8):
            S.op("dve", lambda e: e.tensor_tensor(tmp[:], xt[:, kc, :], rs[:], ALU.mult), reads=[xn, "rs"], writes=["tmp"])
            S.op("act", lambda e: e.activation(hT[:, kc, t0:t0 + 512], tmp[:], AF.Identity, bias=sh_ap[:, kc:kc + 1], scale=a_ap[:, kc:kc + 1]),
                 reads=["tmp", "coef"], writes=[hn])

    def stage_h(st, g, l, Xsrc, ia, ish):
        G = GR[g]
        T = G["T"]
        hT = b.sb(st, "hT", [128, 8, T], BF16)
        with ExitStack() as s2:
            xts = [b.sb(s2, "xt%d" % i, [128, 8, 512]) for i in range(2)]
            tmp = b.sb(s2, "tmp", [128, 512])
            sq = b.sb(s2, "sq", [128, 8, 512], BF16)
            rs = b.sb(s2, "rs", [128, 512])
            for tt in range(T // 512):
                xt, xn = xts[tt % 2], "xt%d" % (tt % 2)
                S.dma("sp", xt[:], Xsrc[:, tt * 512:(tt + 1) * 512].rearrange("(kc p) t -> p kc t", p=128), writes=[xn])
                norm_mod_tile(None, xt, xn, hT, "hT", tt * 512, coef[:, l, G["cond"], ia], coef[:, l, G["cond"], ish], tmp, sq, rs)
            S.barrier()
        return hT

    def stage_proj(g, l, hT):
        G = GR[g]
        T = G["T"]
        UT, UTM = D["UT_" + g], D["UTM_" + g]
        tm_groups = {1: [(0, 512, 0)], 2: [(0, 512, 512)], 3: [(0, 512, 1024)], 6: [(128, 128, 1536)], 8: [(288, 128, 1664)]}
        with ExitStack() as st:
            wb = [b.sb(st, "wb%d" % i, [128, 8, 512], BF16) for i in range(2)]
            ev = [b.sb(st, "ev%d" % i, [128, 512]) for i in range(4)]
            nev = 0
            for cg in range(17):
                ncol = 512 if cg < 16 else D_IN - 8192
                w, wn = wb[cg % 2], "wb%d" % (cg % 2)
                S.dma("pool", w[:, :, 0:ncol], D["w_in"][l, :, cg * 512:cg * 512 + ncol].rearrange("(kc p) c -> p kc c", p=128), writes=[wn])
                for tt in range(T // 512):
                    for cc in range((ncol + 127) // 128):
                        m = min(128, ncol - cc * 128)
                        pt, pn = b.ps()
                        for kc in range(8):
                            S.op("pe", lambda e: e.matmul(pt[0:m, :], w[:, kc, cc * 128:cc * 128 + m], hT[:, kc, tt * 512:(tt + 1) * 512],
                                                          start=(kc == 0), stop=(kc == 7)), reads=[wn, "hT"], writes=[pn], pe_acc=True)
                        e_, en = ev[nev % 4], "ev%d" % (nev % 4)
                        eng = "act" if nev % 2 == 0 else "dve"
                        nev += 1
                        if eng == "act":
                            S.op("act", lambda e: e.copy(e_[0:m, :], pt[0:m, :]), reads=[pn], writes=[en])
                        else:
                            S.op("dve", lambda e: e.tensor_copy(e_[0:m, :], pt[0:m, :]), reads=[pn], writes=[en])
                        r0 = cg * 512 + cc * 128
                        S.dma("sp", UT[r0:r0 + m, tt * 512:(tt + 1) * 512], e_[0:m, :], reads=[en], writes=[])
                for (c0, cn, dst) in tm_groups.get(cg, []):
                    for t4 in range(T // 128):
                        pt, pn = b.ps()
                        for kc in range(8):
                            S.op("pe", lambda e: e.matmul(pt[:, 0:cn], hT[:, kc, t4 * 128:(t4 + 1) * 128], w[:, kc, c0:c0 + cn],
                                                          start=(kc == 0), stop=(kc == 7)), reads=[wn, "hT"], writes=[pn], pe_acc=True)
                        e_, en = ev[nev % 4], "ev%d" % (nev % 4)
                        eng = "act" if nev % 2 == 0 else "dve"
                        nev += 1
                        if eng == "act":
                            S.op("act", lambda e: e.copy(e_[:, 0:cn], pt[:, 0:cn]), reads=[pn], writes=[en])
                        else:
                            S.op("dve", lambda e: e.tensor_copy(e_[:, 0:cn], pt[:, 0:cn]), reads=[pn], writes=[en])
                        S.dma("sp", UTM[t4 * 128:(t4 + 1) * 128, dst:dst + cn], e_[:, 0:cn], reads=[en], writes=[])
            S.barrier()

    def attn_unit(qT, qn, Kd, Nq, chunks, scale, o_out, on, wk, sink=None):
        po, pon = b.acc(0)
        pd, pdn = b.acc(1)
        nch = len(chunks)
        for i, (kT, kn, V, vn, nk, mask) in enumerate(chunks):
            pst, psn_ = b.ps()
            S.op("pe", lambda e: e.matmul(pst[0:nk, 0:Nq], kT, qT, start=True, stop=True), reads=[kn, qn], writes=[psn_])
            E, En = wk["E"][i % 3], "E%d" % (i % 3)
            S.op("act", lambda e: e.activation(E[0:nk, 0:Nq], pst[0:nk, 0:Nq], AF.Exp, scale=scale), reads=[psn_], writes=[En])
            if mask is not None:
                S.op("pool", lambda e: e.tensor_tensor(E[0:nk, 0:Nq], E[0:nk, 0:Nq], mask, ALU.mult), reads=[En, "swamask"], writes=[En])
            last = (i == nch - 1) and sink is None
            S.op("pe", lambda e: e.matmul(po[0:64, 0:Nq], V, E[0:nk, 0:Nq], start=(i == 0), stop=(i == nch - 1)),
                 reads=[vn, En], writes=[pon], pe_acc=True)
            S.op("pe", lambda e: e.matmul(pd[0:64, 0:Nq], ones_bf[0:nk, 0:64], E[0:nk, 0:Nq], start=(i == 0), stop=last),
                 reads=["ones_bf", En], writes=[pdn], pe_acc=True)
        if sink is not None:
            S.op("pe", lambda e: e.matmul(pd[0:64, 0:Nq], ones_bf[0:1, 0:64], sink, start=False, stop=True),
                 reads=["ones_bf", "sinkrow"], writes=[pdn], pe_acc=True)
        rec = wk["rec"]
        S.op("dve", lambda e: e.reciprocal(rec[0:64, 0:Nq], pd[0:64, 0:Nq]), reads=[pdn], writes=["rec"])
        S.op("dve", lambda e: e.tensor_tensor(o_out, po[0:64, 0:Nq], rec[0:64, 0:Nq], ALU.mult), reads=[pon, "rec"], writes=[on])

    def rms_rope_rows(st, src_rows_fn, n_rows, T, gain2, gainn, do_norm, do_rope, rope_idx, out_bf, outn, out32=None, out32n=None,
                      out32_pre_rope=False):
        x = b.sb(st, "rr_x", [64, T])
        xs = b.sb(st, "rr_xs", [64, T]) if do_rope else None
        sqb = b.sb(st, "rr_sq", [64, 512], BF16)
        rs = b.sb(st, "rr_rs", [64, T])
        t1 = b.sb(st, "rr_t1", [64, 512])
        for (r0, nr, ap) in src_rows_fn(False):
            S.dma("sp", x[r0:r0 + nr, :], ap, writes=["rr_x"])
        if do_rope:
            for (r0, nr, ap) in src_rows_fn(True):
                S.dma("act", xs[r0:r0 + nr, :], ap, writes=["rr_xs"])
        nr = n_rows
        W = min(512, T)
        for c in range(T // W):
            cs = slice(c * W, (c + 1) * W)
            if do_norm:
                S.op("act", lambda e: e.activation(sqb[0:nr, 0:W], x[0:nr, cs], AF.Square), reads=["rr_x"], writes=["rr_sq"])
                pt, pn = b.ps()
                S.op("pe", lambda e: e.matmul(pt[0:nr, 0:W], ones_bf[0:nr, 0:nr], sqb[0:nr, 0:W], start=True, stop=True),
                     reads=["rr_sq", "ones_bf"], writes=[pn])
                rstd_from_sumsq(pt, pn, rs[:, cs], "rr_rs", nr, W, 1.0 / nr)
                S.op("dve", lambda e: e.scalar_tensor_tensor(x[0:nr, cs], x[0:nr, cs], gain2[0:nr, 0:1], rs[0:nr, cs], ALU.mult, ALU.mult),
                     reads=["rr_x", "rr_rs", gainn], writes=["rr_x"])
                if do_rope:
                    S.op("dve", lambda e: e.scalar_tensor_tensor(xs[0:nr, cs], xs[0:nr, cs], gain2[0:nr, 1:2], rs[0:nr, cs], ALU.mult, ALU.mult),
                         reads=["rr_xs", "rr_rs", gainn], writes=["rr_xs"])
            if out32 is not None and out32_pre_rope:
                S.op("pool", lambda e: e.tensor_copy(out32[0:nr, cs], x[0:nr, cs]), reads=["rr_x"], writes=[out32n])
            if do_rope:
                S.op("dve", lambda e: e.tensor_tensor(x[0:nr, cs], x[0:nr, cs], rope[0:nr, rope_idx, cs], ALU.mult), reads=["rr_x", "rope"], writes=["rr_x"])
                S.op("pool", lambda e: e.tensor_tensor(t1[0:nr, 0:W], xs[0:nr, cs], rope[0:nr, rope_idx + 1, cs], ALU.mult), reads=["rr_xs", "rope"], writes=["rr_t1"])
                S.op("dve", lambda e: e.tensor_tensor(out_bf[0:nr, cs], x[0:nr, cs], t1[0:nr, 0:W], ALU.add), reads=["rr_x", "rr_t1"], writes=[outn])
            else:
                S.op("dve", lambda e: e.tensor_copy(out_bf[0:nr, cs], x[0:nr, cs]), reads=["rr_x"], writes=[outn])

    def swap_rows(base, T0, T1, UT, nd):
        q = nd // 4
        def f(swapped):
            if not swapped:
                return [(0, nd, UT[base:base + nd, T0:T1])]
            return [(0, q, UT[base + q:base + 2 * q, T0:T1]), (q, q, UT[base:base + q, T0:T1]),
                    (2 * q, q, UT[base + 3 * q:base + 4 * q, T0:T1]), (3 * q, q, UT[base + 2 * q:base + 3 * q, T0:T1])]
        return f

    rope = b.sb(cst, "rope", [96, 4, 2048], BF16)
    S.dma("pool", rope[:], D["rope64"].rearrange("a p t -> p a t"), writes=["rope"])

    def stage_gqa_like(g, l, kind):
        G = GR[g]
        T, L, P, nseq, do_rope = G["T"], G["L"], G["P"], G["nseq"], G["rope"]
        UT, UTM, OT = D["UT_" + g], D["UTM_" + g], D["OT_" + g]
        qo, ko = (OFF["sq"], OFF["sk"]) if kind == "swa" else (OFF["gq"], OFF["gk"])
        vcol = 1536 if kind == "swa" else 1664
        bi = 1 if kind == "swa" else 3
        do_norm = kind == "gqa"
        scale = 64 ** -0.5
        nkc_ctx = P // 128
        for sq_ in range(nseq):
            T0, T1 = sq_ * L, (sq_ + 1) * L
            with ExitStack() as st:
                kT = b.sb(st, "kT", [64, 2, P + L], BF16)
                Vt = b.sb(st, "Vt", [128, (P + L) // 128, 128], BF16)
                qTh = b.sb(st, "qTh", [64, 8, L], BF16)
                gq2 = b.sb(st, "gq2", [64, 2]); gk2 = b.sb(st, "gk2", [64, 2])
                sinkrow = b.sb(st, "sinkrow", [1, 8, 128], BF16)
                sk32 = b.sb(st, "sk32", [1, 8])
                if do_norm:
                    S.dma("sp", gq2[:], D["gqa_qn"][l], writes=["gq2"])
                    S.dma("sp", gk2[:], D["gqa_kn"][l], writes=["gk2"])
                else:
                    S.dma("sp", sk32[:], D["sink"][l], writes=["sk32"])
                    S.op("act", lambda e: e.activation(sk32[:], sk32[:], AF.Exp), reads=["sk32"], writes=["sk32"])
                    S.op("dve", lambda e: e.tensor_copy(sinkrow[:], sk32[:].unsqueeze(2).to_broadcast([1, 8, 128])), reads=["sk32"], writes=["sinkrow"])
                if P:
                    with ExitStack() as s2:
                        kc32 = b.sb(s2, "kc32", [64, 2, P]); vc32 = b.sb(s2, "vc32", [128, P // 128, 128])
                        src_k = D["c_swa_kT"] if kind == "swa" else D["c_gqa_kT"]
                        src_v = D["c_swa_v"] if kind == "swa" else D["c_gqa_v"]
                        S.dma("sp", kc32[:], src_k[l].rearrange("h d t -> d h t"), writes=["kc32"])
                        S.dma("sp", vc32[:], src_v[l].rearrange("(c p) f -> p c f", p=128), writes=["vc32"])
                        S.op("dve", lambda e: e.tensor_copy(kT[:, :, 0:P], kc32[:]), reads=["kc32"], writes=["kT"])
                        S.op("dve", lambda e: e.tensor_copy(Vt[:, 0:P // 128, :], vc32[:]), reads=["vc32"], writes=["Vt"])
                        S.barrier()
                for kvh in range(2):
                    with ExitStack() as s2:
                        k32 = b.sb(s2, "k32", [64, L]) if g == "p" else None
                        rms_rope_rows(s2, swap_rows(ko + kvh * 64, T0, T1, UT, 64), 64, L, gk2, "gk2", do_norm, do_rope, 0,
                                      kT[:, kvh, P:P + L], "kT", out32=k32, out32n="k32", out32_pre_rope=True)
                        if g == "p":
                            dst = D["o_swa_kT"] if kind == "swa" else D["o_gqa_kT"]
                            S.dma("sp", dst[l, kvh * 64:(kvh + 1) * 64, T0:T1], k32[:], reads=["k32"], writes=[])
                        S.barrier()
                with ExitStack() as s2:
                    v32 = b.sb(s2, "v32", [128, L // 128, 128])
                    S.dma("sp", v32[:], UTM[T0:T1, vcol:vcol + 128].rearrange("(c p) f -> p c f", p=128), writes=["v32"])
                    S.op("dve", lambda e: e.tensor_copy(Vt[:, P // 128:, :], v32[:]), reads=["v32"], writes=["Vt"])
                    if g == "p":
                        dst = D["o_swa_v"] if kind == "swa" else D["o_gqa_v"]
                        S.dma("act", dst[l, T0:T1, :].rearrange("(c p) f -> p c f", p=128), v32[:], reads=["v32"], writes=[])
                    S.barrier()
                for h in range(8):
                    with ExitStack() as s2:
                        rms_rope_rows(s2, swap_rows(qo + h * 64, T0, T1, UT, 64), 64, L, gq2, "gq2", do_norm, do_rope, 0, qTh[:, h, :], "qTh")
                        S.barrier()
                with ExitStack() as s2:
                    wk = dict(E=[b.sb(s2, "E%d" % i, [128, 512], BF16) for i in range(3)], rec=b.sb(s2, "rec", [64, 512]))
                    obs = [b.sb(s2, "ob%d" % i, [64, 512], BF16) for i in range(2)]
                    swm = b.sb(s2, "swamask", [128, 2, 512], BF16)
                    S.dma("sp", swm[:], D["swamask"].rearrange("a p c -> p a c"), writes=["swamask"])
                    nu = 0
                    for kvh in range(2):
                        for qb in range(L // 128):
                            chunks = []
                            for c in range(nkc_ctx):
                                chunks.append((kT[:, kvh, c * 128:(c + 1) * 128], "kT", Vt[:, c, kvh * 64:(kvh + 1) * 64], "Vt", 128, None))
                            if kind == "swa" and P:
                                for (kb, mi) in ((qb - 1, 0), (qb, None), (qb + 1, 1)):
                                    if 0 <= kb < L // 128:
                                        chunks.append((kT[:, kvh, P + kb * 128:P + (kb + 1) * 128], "kT", Vt[:, nkc_ctx + kb, kvh * 64:(kvh + 1) * 64], "Vt",
                                                       128, None if mi is None else swm[:, mi, :]))
                            else:
                                for kb in range(L // 128):
                                    chunks.append((kT[:, kvh, P + kb * 128:P + (kb + 1) * 128], "kT", Vt[:, nkc_ctx + kb, kvh * 64:(kvh + 1) * 64], "Vt", 128, None))
                            ob, obn = obs[nu % 2], "ob%d" % (nu % 2)
                            nu += 1
                            attn_unit(qTh[:, kvh * 4:(kvh + 1) * 4, qb * 128:(qb + 1) * 128], "qTh", 64, 512, chunks, scale,
                                      ob[:], obn, wk, sink=(sinkrow[0:1, kvh * 4:(kvh + 1) * 4, :] if kind == "swa" else None))
                            S.dma("sp", OT[bi, kvh * 256:(kvh + 1) * 256, T0 + qb * 128:T0 + (qb + 1) * 128].rearrange("(h d) t -> d h t", d=64),
                                  ob[:].rearrange("d (h t) -> d h t", h=4), reads=[obn], writes=[])
                    S.barrier()

    def stage_mla(g, l):
        G = GR[g]
        T, L, P, nseq, do_rope = G["T"], G["L"], G["P"], G["nseq"], G["rope"]
        UT, OT = D["UT_" + g], D["OT_" + g]
        scale = 96 ** -0.5
        NK = P + L
        for sq_ in range(nseq):
            T0, T1 = sq_ * L, (sq_ + 1) * L
            with ExitStack() as st:
                ckvT = b.sb(st, "ckvT", [128, NK], BF16)
                krT = b.sb(st, "krT", [96, NK], BF16)
                cqn = b.sb(st, "cqn", [128, 2, L], BF16)
                wuq = b.sb(st, "wuq", [128, 2, 768], BF16); wuqs = b.sb(st, "wuqs", [128, 2, 768], BF16)
                wukv = b.sb(st, "wukv", [128, 1024], BF16)
                g_q = b.sb(st, "g_q", [128, 2]); g_kv = b.sb(st, "g_kv", [128, 1])
                S.dma("pool", wuq[:], D["w_uq"][l].rearrange("(kc p) c -> p kc c", p=128), writes=["wuq"])
                S.dma("pool", wuqs[:], D["w_uq_sw"][l].rearrange("(kc p) c -> p kc c", p=128), writes=["wuqs"])
                S.dma("pool", wukv[:], D["w_ukv"][l], writes=["wukv"])
                S.dma("sp", g_q[:], D["mla_qnT"][l], writes=["g_q"])
                S.dma("sp", g_kv[:], D["mla_kvn"][l], writes=["g_kv"])
                with ExitStack() as s2:
                    x = b.sb(s2, "m_x", [128, 2, L]); sqb = b.sb(s2, "m_sq", [128, 2, 512], BF16); rs = b.sb(s2, "m_rs", [128, 512])
                    xk = b.sb(s2, "m_xk", [128, L]); kr32 = b.sb(s2, "m_kr", [96, L]); krs = b.sb(s2, "m_krs", [96, L]); t1 = b.sb(s2, "m_t1", [96, 512])
                    if P:
                        S.dma("pool", ckvT[:, 0:P], D["c_ckvT"][l], writes=["ckvT"])
                        S.dma("pool", krT[64:96, 0:P], D["c_krT"][l], writes=["krT"])
                    S.dma("sp", x[:], UT[OFF["cq"]:OFF["cq"] + 256, T0:T1].rearrange("(kc p) t -> p kc t", p=128), writes=["m_x"])
                    S.dma("sp", xk[:], UT[OFF["ckv"]:OFF["ckv"] + 128, T0:T1], writes=["m_xk"])
                    S.dma("sp", kr32[64:96, :], UT[OFF["kr"]:OFF["kr"] + 32, T0:T1], writes=["m_kr"])
                    if do_rope:
                        for (r0, nr, ap) in swap_rows(OFF["kr"], T0, T1, UT, 32)(True):
                            S.dma("act", krs[64 + r0:64 + r0 + nr, :], ap, writes=["m_krs"])
                    for c in range(L // 512 if L >= 512 else 1):
                        w = min(512, L)
                        cs = slice(c * w, (c + 1) * w)
                        pt, pn = b.ps()
                        for kc in range(2):
                            S.op("act", lambda e: e.activation(sqb[:, kc, 0:w], x[:, kc, cs], AF.Square), reads=["m_x"], writes=["m_sq"])
                        for kc in range(2):
                            S.op("pe", lambda e: e.matmul(pt[:, 0:w], ones_bf[:], sqb[:, kc, 0:w], start=(kc == 0), stop=(kc == 1)),
                                 reads=["m_sq", "ones_bf"], writes=[pn], pe_acc=True)
                        rstd_from_sumsq(pt, pn, rs, "m_rs", 128, w, 1.0 / 256)
                        for kc in range(2):
                            S.op("dve", lambda e: e.scalar_tensor_tensor(cqn[:, kc, cs], x[:, kc, cs], g_q[:, kc:kc + 1], rs[:, 0:w], ALU.mult, ALU.mult),
                                 reads=["m_x", "m_rs", "g_q"], writes=["cqn"])
                        pt, pn = b.ps()
                        S.op("act", lambda e: e.activation(sqb[:, 0, 0:w], xk[:, cs], AF.Square), reads=["m_xk"], writes=["m_sq"])
                        S.op("pe", lambda e: e.matmul(pt[:, 0:w], ones_bf[:], sqb[:, 0, 0:w], start=True, stop=True), reads=["m_sq", "ones_bf"], writes=[pn])
                        rstd_from_sumsq(pt, pn, rs, "m_rs", 128, w, 1.0 / 128)
                        S.op("dve", lambda e: e.scalar_tensor_tensor(xk[:, cs], xk[:, cs], g_kv[:, 0:1], rs[:, 0:w], ALU.mult, ALU.mult),
                             reads=["m_xk", "m_rs", "g_kv"], writes=["m_xk"])
                        S.op("pool", lambda e: e.tensor_copy(ckvT[:, P + c * w:P + (c + 1) * w], xk[:, cs]), reads=["m_xk"], writes=["ckvT"])
                        if do_rope:
                            S.op("dve", lambda e: e.tensor_tensor(t1[64:96, 0:w], kr32[64:96, cs], rope[64:96, 2, cs], ALU.mult), reads=["m_kr", "rope"], writes=["m_t1"])
                            S.op("pool", lambda e: e.tensor_tensor(krs[64:96, cs], krs[64:96, cs], rope[64:96, 3, cs], ALU.mult), reads=["m_krs", "rope"], writes=["m_krs"])
                            S.op("dve", lambda e: e.tensor_tensor(krT[64:96, P + c * w:P + (c + 1) * w], t1[64:96, 0:w], krs[64:96, cs], ALU.add),
                                 reads=["m_t1", "m_krs"], writes=["krT"])
                        else:
                            S.op("dve", lambda e: e.tensor_copy(krT[64:96, P + c * w:P + (c + 1) * w], kr32[64:96, cs]), reads=["m_kr"], writes=["krT"])
                    if g == "p":
                        S.dma("sp", D["o_ckvT"][l, :, T0:T1], xk[:], reads=["m_xk"], writes=[])
                        S.dma("sp", D["o_krT"][l, :, T0:T1], kr32[64:96, :], reads=["m_kr"], writes=[])
                    S.barrier()
                Vall = b.sb(st, "Vall", [128, NK // 128, 512], BF16)
                for c in range(NK // 128):
                    pt, pn = b.ps()
                    S.op("pe", lambda e: e.matmul(pt[:, :], ckvT[:, c * 128:(c + 1) * 128],
                                                  wukv[:].rearrange("p (h x) -> p h x", x=128)[:, :, 64:128], start=True, stop=True),
                         reads=["ckvT", "wukv"], writes=[pn])
                    if c % 2 == 0:
                        S.op("act", lambda e: e.copy(Vall[:, c, :], pt[:, :]), reads=[pn], writes=["Vall"])
                    else:
                        S.op("dve", lambda e: e.tensor_copy(Vall[:, c, :], pt[:, :]), reads=[pn], writes=["Vall"])
                S.barrier()
                with ExitStack() as s2:
                    wk = dict(E=[b.sb(s2, "E%d" % i, [128, 512], BF16) for i in range(3)], rec=b.sb(s2, "rec", [64, 512]))
                    kTh = [b.sb(s2, "kTh%d" % i, [96, NK], BF16) for i in range(2)]
                    qh = [b.sb(s2, "qh%d" % i, [96, L], BF16) for i in range(2)]
                    obs = [b.sb(s2, "ob%d" % i, [64, 512], BF16) for i in range(2)]
                    t2 = b.sb(s2, "m_t2", [96, 512]); t3 = b.sb(s2, "m_t3", [96, 512])
                    nu = 0
                    W = min(512, L)
                    for h in range(8):
                        kt, ktn = kTh[h % 2], "kTh%d" % (h % 2)
                        q_, q_n = qh[h % 2], "qh%d" % (h % 2)
                        for c in range((NK + 511) // 512):
                            w = min(512, NK - c * 512)
                            pt, pn = b.ps()
                            S.op("pe", lambda e: e.matmul(pt[0:64, 0:w], wukv[:, h * 128:h * 128 + 64], ckvT[:, c * 512:c * 512 + w], start=True, stop=True),
                                 reads=["wukv", "ckvT"], writes=[pn])
                            S.op("act", lambda e: e.copy(kt[0:64, c * 512:c * 512 + w], pt[0:64, 0:w]), reads=[pn], writes=[ktn])
                        S.op("pool", lambda e: e.tensor_copy(kt[64:96, :], krT[64:96, :]), reads=["krT"], writes=[ktn])
                        for c in range(L // W):
                            cs = slice(c * W, (c + 1) * W)
                            pt, pn = b.ps()
                            for kc in range(2):
                                S.op("pe", lambda e: e.matmul(pt[0:96, 0:W], wuq[:, kc, h * 96:(h + 1) * 96], cqn[:, kc, cs], start=(kc == 0), stop=(kc == 1)),
                                     reads=["wuq", "cqn"], writes=[pn], pe_acc=True)
                            S.op("act", lambda e: e.copy(q_[0:64, cs], pt[0:64, 0:W]), reads=[pn], writes=[q_n])
                            if do_rope:
                                pt2, pn2 = b.ps()
                                for kc in range(2):
                                    S.op("pe", lambda e: e.matmul(pt2[0:96, 0:W], wuqs[:, kc, h * 96:(h + 1) * 96], cqn[:, kc, cs], start=(kc == 0), stop=(kc == 1)),
                                         reads=["wuqs", "cqn"], writes=[pn2], pe_acc=True)
                                S.op("dve", lambda e: e.tensor_tensor(t2[64:96, 0:W], pt[64:96, 0:W], rope[64:96, 2, cs], ALU.mult), reads=[pn, "rope"], writes=["m_t2"])
                                S.op("dve", lambda e: e.tensor_tensor(t3[64:96, 0:W], pt2[64:96, 0:W], rope[64:96, 3, cs], ALU.mult), reads=[pn2, "rope"], writes=["m_t3"])
                                S.op("pool", lambda e: e.tensor_tensor(q_[64:96, cs], t2[64:96, 0:W], t3[64:96, 0:W], ALU.add), reads=["m_t2", "m_t3"], writes=[q_n])
                            else:
                                S.op("dve", lambda e: e.tensor_copy(q_[64:96, cs], pt[64:96, 0:W]), reads=[pn], writes=[q_n])
                        for c in range(L // W):
                            chunks = [(kt[:, kc * 128:(kc + 1) * 128], ktn, Vall[:, kc, h * 64:(h + 1) * 64], "Vall", 128, None) for kc in range(NK // 128)]
                            ob, obn = obs[nu % 2], "ob%d" % (nu % 2)
                            nu += 1
                            attn_unit(q_[:, c * W:(c + 1) * W], q_n, 96, W, chunks, scale, ob[:, 0:W], obn, wk)
                            S.dma("sp", OT[2, h * 64:(h + 1) * 64, T0 + c * W:T0 + (c + 1) * W], ob[:, 0:W], reads=[obn], writes=[])
                    S.barrier()

    def stage_hgrn(g, l):
        G = GR[g]
        T, L, P, nseq = G["T"], G["L"], G["P"], G["nseq"]
        UT, UTM, OT = D["UT_" + g], D["UTM_" + g], D["OT_" + g]
        NTL = L // 128
        ident, triU, triL, sL, sU, csel = (cm[:, i, :] for i in range(6))
        with ExitStack() as st:
            lbb = b.sb(st, "lbb", [128, 2, 512]); oml = b.sb(st, "oml", [128, 2, 512])
            with ExitStack() as s2:
                lbr = b.sb(s2, "lbr", [128, DEPTH, 2, 512]); den = b.sb(s2, "lden", [128, 2, 512])
                S.dma("sp", lbr[:], D["lbT"].partition_broadcast(128), writes=["lbr"])
                S.op("act", lambda e: e.activation(lbr[:], lbr[:], AF.Exp), reads=["lbr"], writes=["lbr"])
                S.op("dve", lambda e: e.tensor_tensor(den[:], lbr[:, 0], lbr[:, 1], ALU.add), reads=["lbr"], writes=["lden"])
                S.op("dve", lambda e: e.reciprocal(den[:], den[:]), reads=["lden"], writes=["lden"])
                if l == 0:
                    S.op("pool", lambda e: e.memset(lbb[:], 0.0), writes=["lbb"])
                else:
                    S.op("dve", lambda e: e.tensor_tensor(lbb[:], lbr[:, 1], den[:], ALU.mult), reads=["lbr", "lden"], writes=["lbb"])
                S.op("dve", lambda e: e.tensor_scalar(oml[:], lbb[:], -1.0, 1.0, ALU.mult, ALU.add), reads=["lbb"], writes=["oml"])
                S.barrier()
            hn = b.sb(st, "hn", [128, 4]); S.dma("sp", hn[:], D["hgrn_normT"][l], writes=["hn"])
            Sst = b.sb(st, "Sst", [128, 2, 4, 128])
            SSTN = ["Sst%d%d" % (d_, h_) for d_ in range(2) for h_ in range(4)]
            oall = b.sb(st, "oall", [128, 2, 4, L], BF16)
            W_ = {}
            for d in range(2):
                for k in range(2):
                    sfx = "%d%d" % (d, k)
                    W_[d, k] = dict(
                        sfx=sfx,
                        tt=b.sb(st, "h_t" + sfx, [128, 512]), vv=b.sb(st, "h_v" + sfx, [128, 512]), gg=b.sb(st, "h_g" + sfx, [128, 512]),
                        kk=b.sb(st, "h_k" + sfx, [128, 512]), kt=b.sb(st, "h_kt" + sfx, [128, 512]), kh=b.sb(st, "h_kh" + sfx, [128, 512]),
                        khm=b.sb(st, "h_khm" + sfx, [128, 4, 4, 128]), qT=b.sb(st, "h_qT" + sfx, [128, 4, 128]), qt=b.sb(st, "h_qt" + sfx, [128, 4, 128]),
                        eb=b.sb(st, "h_eb" + sfx, [128, 4, 132]), ktT=b.sb(st, "h_ktT" + sfx, [128, 4, 128]), AT=b.sb(st, "h_AT" + sfx, [128, 4, 128]))
            fin = dict(os=b.sb(st, "f_os", [128, 4, 256]), gT=b.sb(st, "f_gT", [128, 4, 256]), sq=b.sb(st, "f_sq", [128, 4, 256], BF16),
                       rs=b.sb(st, "f_rs", [128, 4, 256]), ob=b.sb(st, "f_ob", [128, 4, 256], BF16))
            it = 0
            b.nrot = 4
            tmpo = [b.sb(st, "h_tmpo%d" % d_, [128, 512]) for d_ in range(2)]
            for sq_ in range(nseq):
                T0 = sq_ * L
                if P:
                    S.dma("sp", Sst[:], D["st_hgrn"][l].rearrange("d h k v -> k d h v"), writes=SSTN)
                else:
                    S.op("pool", lambda e: e.memset(Sst[:], 0.0), writes=SSTN)
                for i in range(NTL):
                    k = it % 2
                    it += 1
                    tis = (i, NTL - 1 - i)
                    pos = [b.acc(0), b.acc(1)]
                    pis = [b.acc(2), b.acc(3)]
                    for d in range(2):
                        w = W_[d, k]
                        x = w["sfx"]
                        t0 = T0 + tis[d] * 128
                        M_in, M_ex = (triU, sL) if d == 0 else (triL, sU)
                        tt_, vv, gg, kk, kt_, kh, khm, qT, qt, eb, ktT, AT = (w[n_] for n_ in ("tt", "vv", "gg", "kk", "kt", "kh", "khm", "qT", "qt", "eb", "ktT", "AT"))
                        S.dma("sp", tt_[:], UTM[t0:t0 + 128, d * 512:(d + 1) * 512], writes=["h_t" + x])
                        S.dma("sp", vv[:], UTM[t0:t0 + 128, 1024:1536], writes=["h_v" + x])
                        S.dma("act", qT[:], UT[0:512, t0:t0 + 128].rearrange("(h p) t -> p h t", p=128), writes=["h_qT" + x])
                        S.op("act", lambda e: e.activation(qT[:], qT[:], AF.Silu), reads=["h_qT" + x], writes=["h_qT" + x])
                        S.op("act", lambda e: e.activation(tt_[:], tt_[:], AF.Sigmoid), reads=["h_t" + x], writes=["h_t" + x])
                        S.op("dve", lambda e: e.tensor_tensor(tt_[:], tt_[:], oml[:, d], ALU.mult), reads=["h_t" + x, "oml"], writes=["h_t" + x])
                        S.op("dve", lambda e: e.scalar_tensor_tensor(gg[:], tt_[:], 1e-30, lbb[:, d], ALU.max, ALU.add), reads=["h_t" + x, "lbb"], writes=["h_g" + x])
                        S.op("act", lambda e: e.activation(gg[:], gg[:], AF.Ln), reads=["h_g" + x], writes=["h_g" + x])
                        S.op("dve", lambda e: e.tensor_tensor(kk[:], oml[:, d], tt_[:], ALU.subtract), reads=["h_t" + x, "oml"], writes=["h_k" + x])
                        pb, pbn = b.ps()
                        S.op("pe", lambda e: e.matmul(pb[:], M_in, gg[:], start=True, stop=True), reads=["cm", "h_g" + x], writes=[pbn])
                        S.op("act", lambda e: e.activation(kt_[:], pb[:], AF.Exp, scale=-1.0), reads=[pbn], writes=["h_kt" + x])
                        S.op("dve", lambda e: e.tensor_tensor(kt_[:], kt_[:], kk[:], ALU.mult), reads=["h_kt" + x, "h_k" + x], writes=["h_kt" + x])
                        pr, prn = b.ps()
                        S.op("pe", lambda e: e.matmul(pr[:], M_ex, gg[:], start=True, stop=True), reads=["cm", "h_g" + x], writes=[prn])
                        S.op("act", lambda e: e.activation(kh[:], pr[:], AF.Exp), reads=[prn], writes=["h_kh" + x])
                        S.op("dve", lambda e: e.tensor_tensor(kh[:], kh[:], kk[:], ALU.mult), reads=["h_kh" + x, "h_k" + x], writes=["h_kh" + x])
                        po_, pon_ = pos[d]
                        for h in range(4):
                            hs = slice(h * 128, (h + 1) * 128)
                            pbt, pbtn = b.ps()
                            S.op("pe", lambda e: e.matmul(pbt[:, 0:128], gg[:, hs], M_in, start=True, stop=True), reads=["h_g" + x, "cm"], writes=[pbtn])
                            S.op("pe", lambda e: e.matmul(pbt[:, 128:132], gg[:, hs], csel[:, 0:4], start=True, stop=True), reads=["h_g" + x, "cm"], writes=[pbtn])
                            S.op("act", lambda e: e.activation(eb[:, h, :], pbt[:, 0:132], AF.Exp), reads=[pbtn], writes=["h_eb" + x])
                            S.op("dve", lambda e: e.tensor_tensor(qt[:, h, :], qT[:, h, :], eb[:, h, 0:128], ALU.mult), reads=["h_qT" + x, "h_eb" + x], writes=["h_qt" + x])
                            pk, pkn = b.ps()
                            S.op("pe", lambda e: e.matmul(pk[:, 0:128], kt_[:, hs], ident, start=True, stop=True), reads=["h_kt" + x, "cm"], writes=[pkn])
                            S.op("act", lambda e: e.copy(ktT[:, h, :], pk[:, 0:128]), reads=[pkn], writes=["h_ktT" + x])
                            pa, pan = b.ps()
                            S.op("pe", lambda e: e.matmul(pa[:, 0:128], ktT[:, h, :], qt[:, h, :], start=True, stop=True), reads=["h_ktT" + x, "h_qt" + x], writes=[pan])
                            S.op("dve", lambda e: e.tensor_tensor(AT[:, h, :], pa[:, 0:128], M_in, ALU.mult), reads=[pan, "cm"], writes=["h_AT" + x])
                            for j in range(4):
                                S.op("pool", lambda e: e.tensor_scalar(khm[:, h, j, :], kh[:, hs], csel[:, j:j + 1], None, ALU.mult),
                                     reads=["h_kh" + x, "cm"], writes=["h_khm" + x])
                            S.op("pe", lambda e: e.matmul(po_[:, hs], vv[:, hs], AT[:, h, :], start=True, stop=True), reads=["h_v" + x, "h_AT" + x], writes=[pon_], pe_acc=(h > 0))
                    for n_ in range(4):
                        pss = []
                        for d in range(2):
                            w = W_[d, k]
                            x = w["sfx"]
                            j = n_ if d == 0 else 3 - n_
                            js = slice(j * 32, (j + 1) * 32)
                            po_, pon_ = pis[d]
                            ps_, psn2 = b.ps()
                            pss.append((ps_, psn2, j))
                            for h in range(4):
                                hs = slice(h * 128, (h + 1) * 128)
                                S.op("pe", lambda e: e.matmul(po_[:, h * 128 + j * 32:h * 128 + (j + 1) * 32], Sst[:, d, h, :], w["qt"][:, h, js], start=True, stop=True),
                                     reads=["Sst%d%d" % (d, h), "h_qt" + x], writes=[pon_], pe_acc=(n_ > 0 or h > 0))
                                S.op("pe", lambda e: e.matmul(ps_[:, hs], w["khm"][:, h, j, :], w["vv"][:, hs], start=True, stop=True),
                                     reads=["h_khm" + x, "h_v" + x], writes=[psn2], pe_acc=True)
                        for d in range(2):
                            w = W_[d, k]
                            x = w["sfx"]
                            ps_, psn2, j = pss[d]
                            for h in range(4):
                                hs = slice(h * 128, (h + 1) * 128)
                                S.op("dve", lambda e: e.scalar_tensor_tensor(Sst[:, d, h, :], Sst[:, d, h, :], w["eb"][:, h, 128 + j:129 + j], ps_[:, hs], ALU.mult, ALU.add),
                                     reads=[psn2, "h_eb" + x, "Sst%d%d" % (d, h)], writes=["Sst%d%d" % (d, h)])
                    for d in range(2):
                        po_, pon_ = pos[d]
                        pi_, pin_ = pis[d]
                        tl = tis[d] * 128
                        S.op("act", lambda e: e.copy(tmpo[d][:], po_[:]), reads=[pon_], writes=["h_tmpo%d" % d])
                        S.op("dve", lambda e: e.tensor_tensor(oall[:, d, :, tl:tl + 128], tmpo[d][:].rearrange("p (h t) -> p h t", h=4),
                                                              pi_[:].rearrange("p (h t) -> p h t", h=4), ALU.add),
                             reads=[pin_, "h_tmpo%d" % d], writes=["oall"])
                for c in range(L // 256):
                    cs = slice(c * 256, (c + 1) * 256)
                    os_, gT, sq, rs, ob = fin["os"], fin["gT"], fin["sq"], fin["rs"], fin["ob"]
                    S.dma("act", gT[:], UT[OFF["hg"]:OFF["hg"] + 512, T0 + c * 256:T0 + (c + 1) * 256].rearrange("(h p) t -> p h t", p=128), writes=["f_gT"])
                    S.op("act", lambda e: e.activation(gT[:], gT[:], AF.Silu), reads=["f_gT"], writes=["f_gT"])
                    S.op("dve", lambda e: e.tensor_tensor(os_[:], oall[:, 0, :, cs], oall[:, 1, :, cs], ALU.add), reads=["oall"], writes=["f_os"])
                    S.op("act", lambda e: e.activation(sq[:], os_[:], AF.Square), reads=["f_os"], writes=["f_sq"])
                    for hh in range(2):
                        pn_, pnn = b.ps()
                        for h2 in range(2):
                            h = hh * 2 + h2
                            S.op("pe", lambda e: e.matmul(pn_[:, h2 * 256:(h2 + 1) * 256], ones_bf[:], sq[:, h, :], start=True, stop=True),
                                 reads=["f_sq", "ones_bf"], writes=[pnn])
                        rstd_from_sumsq(pn_, pnn, rs[:, hh * 2:hh * 2 + 2, :].rearrange("p a t -> p (a t)"), "f_rs", 128, 512, 1.0 / 128)
                    for h in range(4):
                        S.op("dve", lambda e: e.scalar_tensor_tensor(os_[:, h, :], os_[:, h, :], hn[:, h:h + 1], rs[:, h, :], ALU.mult, ALU.mult),
                             reads=["f_os", "hn", "f_rs"], writes=["f_os"])
                    S.op("dve", lambda e: e.tensor_tensor(ob[:], os_[:], gT[:], ALU.mult), reads=["f_os", "f_gT"], writes=["f_ob"])
                    S.dma("sp", OT[0, :, T0 + c * 256:T0 + (c + 1) * 256].rearrange("(h p) t -> p h t", p=128), ob[:], reads=["f_ob"], writes=[])
                if g == "p":
                    S.dma("sp", D["o_hgrn"][l, sq_].rearrange("d h k v -> k d h v"), Sst[:], reads=SSTN, writes=[])
            b.nrot = 6
            S.barrier()

    def epilogue(st, z, zn, xt, xn, gco, Xdst, c0, wkn):
        sq, rs, tmp = wkn
        for kc in range(8):
            S.op("act", lambda e: e.activation(sq[:, kc, :], z[:, kc, :], AF.Square), reads=[zn], writes=["e_sq"])
        pt, pn = b.ps()
        for kc in range(8):
            S.op("pe", lambda e: e.matmul(pt[:], ones_bf[:], sq[:, kc, :], start=(kc == 0), stop=(kc == 7)), reads=["e_sq", "ones_bf"], writes=[pn], pe_acc=True)
        rstd_from_sumsq(pt, pn, rs, "e_rs", 128, 512, 1.0 / DM)
        for kc in range(8):
            S.op("dve", lambda e: e.tensor_tensor(tmp[:], z[:, kc, :], rs[:], ALU.mult), reads=[zn, "e_rs"], writes=["e_tmp"])
            S.op("dve", lambda e: e.scalar_tensor_tensor(xt[:, kc, :], tmp[:], gco[:, kc:kc + 1], xt[:, kc, :], ALU.mult, ALU.add),
                 reads=["e_tmp", "coef", xn], writes=[xn])
        S.dma("sp", Xdst[:, c0:c0 + 512].rearrange("(kc p) t -> p kc t", p=128), xt[:], reads=[xn], writes=[])

    def stage_merge(g, l, Xsrc, Xdst):
        G = GR[g]
        T = G["T"]
        UT, OT = D["UT_" + g], D["OT_" + g]
        with ExitStack() as st:
            Wb = b.sb(st, "Wb", [128, 4, 4, 1024], BF16)
            Wo = b.sb(st, "Wo", [128, 8, 1024], BF16)
            S.dma("pool", Wb[:], D["w_branch"][l].rearrange("n (kc p) c -> p n kc c", p=128), writes=["Wb"])
            S.dma("pool", Wo[:], D["w_out"][l].rearrange("(kc p) c -> p kc c", p=128), writes=["Wo"])
            ot = b.sb(st, "ot", [128, 4, 4, 512], BF16)
            yp = b.sb(st, "yp", [128, 8, 512], BF16)
            gts = [b.sb(st, "gt%d" % i, [128, 512]) for i in range(3)]
            accf = b.sb(st, "accf", [128, 512]); tm2 = b.sb(st, "tm2", [128, 512])
            z = b.sb(st, "z", [128, 8, 512]); xt = b.sb(st, "xt", [128, 8, 512])
            wkn = (b.sb(st, "e_sq", [128, 8, 512], BF16), b.sb(st, "e_rs", [128, 512]), b.sb(st, "e_tmp", [128, 512]))
            ng = 0
            for tt in range(T // 512):
                cs = slice(tt * 512, (tt + 1) * 512)
                S.dma("sp", ot[:], OT[:, :, cs].rearrange("n (kc p) t -> p n kc t", p=128), writes=["ot"])
                S.dma("act", xt[:], Xsrc[:, cs].rearrange("(kc p) t -> p kc t", p=128), writes=["xt"])
                for dmc in range(8):
                    for n in range(4):
                        gt, gtn = gts[ng % 3], "gt%d" % (ng % 3)
                        ng += 1
                        r0 = OFF["gates"] + n * 1024 + dmc * 128
                        S.dma("sp", gt[:], UT[r0:r0 + 128, cs], writes=[gtn])
                        S.op("act", lambda e: e.activation(gt[:], gt[:], AF.Sigmoid), reads=[gtn], writes=[gtn])
                        pt, pn = b.ps()
                        for kc in range(4):
                            S.op("pe", lambda e: e.matmul(pt[:], Wb[:, n, kc, dmc * 128:(dmc + 1) * 128], ot[:, n, kc, :], start=(kc == 0), stop=(kc == 3)),
                                 reads=["Wb", "ot"], writes=[pn], pe_acc=True)
                        if n == 0:
                            S.op("dve", lambda e: e.tensor_tensor(accf[:], pt[:], gt[:], ALU.mult), reads=[pn, gtn], writes=["accf"])
                        elif n < 3:
                            S.op("dve", lambda e: e.tensor_tensor(tm2[:], pt[:], gt[:], ALU.mult), reads=[pn, gtn], writes=["tm2"])
                            S.op("pool", lambda e: e.tensor_tensor(accf[:], accf[:], tm2[:], ALU.add), reads=["tm2", "accf"], writes=["accf"])
                        else:
                            S.op("dve", lambda e: e.tensor_tensor(tm2[:], pt[:], gt[:], ALU.mult), reads=[pn, gtn], writes=["tm2"])
                            S.op("pool", lambda e: e.tensor_tensor(yp[:, dmc, :], accf[:], tm2[:], ALU.add), reads=["tm2", "accf"], writes=["yp"])
                for oc in range(8):
                    pt, pn = b.ps()
                    for kc in range(8):
                        S.op("pe", lambda e: e.matmul(pt[:], Wo[:, kc, oc * 128:(oc + 1) * 128], yp[:, kc, :], start=(kc == 0), stop=(kc == 7)),
                             reads=["Wo", "yp"], writes=[pn], pe_acc=True)
                    S.op("act", lambda e: e.copy(z[:, oc, :], pt[:]), reads=[pn], writes=["z"])
                epilogue(st, z, "z", xt, "xt", coef[:, l, G["cond"], 2], Xdst, tt * 512, wkn)
            S.barrier()

    def stage_mlp(g, l, Xsrc, Xdst):
        G = GR[g]
        T = G["T"]
        with ExitStack() as st:
            h2 = stage_h(st, g, l, Xsrc, 3, 4)
            W2 = [b.sb(st, "W2_%d" % i, [128, 8, 1024], BF16) for i in range(2)]
            n2 = 0
            w1 = [b.sb(st, "w1_%d" % i, [128, 8, 512], BF16) for i in range(2)]
            hid = b.sb(st, "hid", [128, 32, 512], BF16)
            rl = [b.sb(st, "rl%d" % i, [128, 512]) for i in range(2)]
            z = b.sb(st, "z", [128, 8, 512]); xt = b.sb(st, "xt", [128, 8, 512])
            wkn = (b.sb(st, "e_sq", [128, 8, 512], BF16), b.sb(st, "e_rs", [128, 512]), b.sb(st, "e_tmp", [128, 512]))
            nw = 0
            nr = 0
            for tt in range(T // 512):
                cs = slice(tt * 512, (tt + 1) * 512)
                S.dma("act", xt[:], Xsrc[:, cs].rearrange("(kc p) t -> p kc t", p=128), writes=["xt"])
                for cg in range(8):
                    w, wn = w1[nw % 2], "w1_%d" % (nw % 2)
                    nw += 1
                    S.dma("pool", w[:], D["w_mlp_in"][l, :, cg * 512:(cg + 1) * 512].rearrange("(kc p) c -> p kc c", p=128), writes=[wn])
                    for cc in range(4):
                        pt, pn = b.ps()
                        for kc in range(8):
                            S.op("pe", lambda e: e.matmul(pt[:], w[:, kc, cc * 128:(cc + 1) * 128], h2[:, kc, cs], start=(kc == 0), stop=(kc == 7)),
                                 reads=[wn, "hT"], writes=[pn], pe_acc=True)
                        r, rn = rl[nr % 2], "rl%d" % (nr % 2)
                        nr += 1
                        S.op("act", lambda e: e.activation(r[:], pt[:], AF.Relu), reads=[pn], writes=[rn])
                        S.op("dve", lambda e: e.tensor_tensor(hid[:, cg * 4 + cc, :], r[:], r[:], ALU.mult), reads=[rn], writes=["hid"])
                for q4 in range(4):
                    w2, w2n = W2[n2 % 2], "W2_%d" % (n2 % 2)
                    n2 += 1
                    S.dma("pool", w2[:], D["w_mlp_out"][l, q4 * 1024:(q4 + 1) * 1024, :].rearrange("(kc p) c -> p kc c", p=128), writes=[w2n])
                    for oc in range(8):
                        pt, pn = b.ps()
                        for fc in range(8):
                            S.op("pe", lambda e: e.matmul(pt[:], w2[:, fc, oc * 128:(oc + 1) * 128], hid[:, q4 * 8 + fc, :], start=(fc == 0), stop=(fc == 7)),
                                 reads=[w2n, "hid"], writes=[pn], pe_acc=True)
                        if q4 == 0:
                            S.op("act", lambda e: e.copy(z[:, oc, :], pt[:]), reads=[pn], writes=["z%d" % oc, "z"])
                        else:
                            S.op("dve", lambda e: e.tensor_tensor(z[:, oc, :], z[:, oc, :], pt[:], ALU.add), reads=[pn, "z%d" % oc], writes=["z%d" % oc, "z"])
                epilogue(st, z, "z", xt, "xt", coef[:, l, G["cond"], 5], Xdst, tt * 512, wkn)
            S.barrier()

    for g in ("s", "p"):
        X = D["xT_" + g]
        for l in range(DEPTH):
            if "proj" in STAGES:
                with ExitStack() as st:
                    hT = stage_h(st, g, l, X, 0, 1)
                    stage_proj(g, l, hT)
            if "hgrn" in STAGES:
                stage_hgrn(g, l)
            if "swa" in STAGES:
                stage_gqa_like(g, l, "swa")
            if "mla" in STAGES:
                stage_mla(g, l)
            if "gqa" in STAGES:
                stage_gqa_like(g, l, "gqa")
            if "merge" in STAGES:
                stage_merge(g, l, X, D["X1_" + g])
            Xn = D["yT_" + g] if l == DEPTH - 1 else D["X2_%d_%s" % (l, g)]
            if "mlp" in STAGES:
                stage_mlp(g, l, D["X1_" + g], Xn)
            X = Xn
    S.barrier()
    return nc, b


_CACHE = {}


def _consts():
    nf = 16
    t = np.arange(2048)
    row, col = (t // 64).astype(np.float32), (t % 64).astype(np.float32)
    rope = np.zeros((4, 96, 2048), np.float32)
    def fill(ci, si, r0, nf):
        inv = (10000.0 ** (-np.arange(nf, dtype=np.float32) / nf)).astype(np.float32)
        ar = (row[None, :] * inv[:, None]).astype(np.float32)
        ac = (col[None, :] * inv[:, None]).astype(np.float32)
        for k, a in enumerate((ar, ac)):
            b0 = r0 + k * 2 * nf
            rope[ci, b0:b0 + nf] = np.cos(a); rope[ci, b0 + nf:b0 + 2 * nf] = np.cos(a)
            rope[si, b0:b0 + nf] = -np.sin(a); rope[si, b0 + nf:b0 + 2 * nf] = np.sin(a)
    fill(0, 1, 0, 16)
    fill(2, 3, 64, 8)
    s_ = np.arange(128)[:, None]; t_ = np.arange(128)[None, :]
    same = (s_ // 32) == (t_ // 32)
    cm = np.zeros((6, 128, 128), np.float32)
    cm[0] = np.eye(128)
    cm[1] = same & (s_ <= t_)
    cm[2] = same & (s_ >= t_)
    cm[3] = same & (s_ > t_)
    cm[4] = same & (s_ < t_)
    cm[5][:, 0:4] = (s_ // 32) == np.arange(4)[None, :]
    j = np.arange(128)[:, None]; i = (np.arange(512) % 128)[None, :]
    swm = np.stack([(j >= i), (j <= i)]).astype(np.float32).astype(ml_dtypes.bfloat16)
    return rope, cm, swm


def _perm(nd):
    q = nd // 4
    return np.concatenate([np.arange(q, 2 * q), np.arange(0, q), np.arange(3 * q, 4 * q), np.arange(2 * q, 3 * q)])


def kernel(**inp):
    f = lambda a: np.ascontiguousarray(np.asarray(a, dtype=np.float32))
    I = {k: f(v) for k, v in inp.items()}
    if "prog" not in _CACHE:
        _CACHE["prog"] = build_program()
    nc, b = _CACHE["prog"]
    rope, cm, swm = _consts()
    fm = lambda v, n: f(v.reshape(v.shape[0], n, 128).transpose(0, 2, 1))
    shared = dict(
        w_ada=I["w_ada"], b_adaT=fm(I["b_ada"], 48),
        gains=f(np.stack([fm(I[k], 8) for k in ("norm_mix_pre", "norm_mix_post", "norm_mlp_pre", "norm_mlp_post")], axis=2)),
        w_in=I["w_in"], lbT=f(np.stack([I["hgrn_lb_fwd"], I["hgrn_lb_bwd"]], axis=1)),
        hgrn_normT=fm(I["hgrn_norm"], 4), sink=f(I["swa_sink"][:, None, :]),
        mla_qnT=fm(I["mla_q_norm"], 2), mla_kvn=f(I["mla_kv_norm"][:, :, None]),
        w_uq=I["mla_w_uq"], w_ukv=I["mla_w_ukv"],
        gqa_qn=f(np.stack([I["gqa_q_norm"], I["gqa_q_norm"][:, _perm(64)]], axis=2)),
        gqa_kn=f(np.stack([I["gqa_k_norm"], I["gqa_k_norm"][:, _perm(64)]], axis=2)),
        w_branch=I["w_branch"], w_out=I["w_out"], w_mlp_in=I["w_mlp_in"], w_mlp_out=I["w_mlp_out"],
        rope64=rope, cmat=cm, swamask=swm,
    )
    wsw = I["mla_w_uq"].reshape(DEPTH, 256, 8, 96).copy()
    wsw[..., 64:96] = wsw[..., 64:96][..., _perm(32)]
    shared["w_uq_sw"] = f(wsw.reshape(DEPTH, 256, 768))
    in_maps = []
    for i in range(NCORE):
        bb = i % 2
        m = dict(shared)
        m["xT_s"] = f(I["x_sample"][bb].T)
        m["xT_p"] = f(I["x_prompt"][4 * i:4 * i + 4].reshape(1024, DM).T)
        cc = np.stack([I["c_ctx"], I["c"][bb]], axis=1)
        m["cT"] = f(cc.reshape(8, 128, 2).transpose(1, 0, 2))
        m["st_hgrn"] = f(I["state_hgrn"][bb])
        m["c_swa_kT"] = f(I["cache_swa_k"][bb].transpose(0, 2, 3, 1))
        m["c_swa_v"] = f(I["cache_swa_v"][bb].reshape(DEPTH, 512, 128))
        m["c_ckvT"] = f(I["cache_mla_ckv"][bb].transpose(0, 2, 1))
        m["c_krT"] = f(I["cache_mla_kr"][bb].transpose(0, 2, 1))
        m["c_gqa_kT"] = f(I["cache_gqa_k"][bb].transpose(0, 2, 3, 1))
        m["c_gqa_v"] = f(I["cache_gqa_v"][bb].reshape(DEPTH, 512, 128))
        in_maps.append(m)
    res = run_bass_kernel_spmd(nc, in_maps, core_ids=list(range(NCORE)))
    R = res.results
    y_prompt = np.concatenate([R[i]["yT_p"].T.reshape(4, 256, DM) for i in range(NCORE)], axis=0)
    y_sample = np.stack([R[0]["yT_s"].T, R[1]["yT_s"].T], axis=0)
    cat = lambda fn: np.ascontiguousarray(np.concatenate([fn(R[i]) for i in range(NCORE)], axis=0).astype(np.float32))
    n_hgrn = cat(lambda r: r["o_hgrn"].transpose(1, 0, 2, 3, 4, 5))
    kT = lambda a: a.reshape(DEPTH, 2, 64, 4, 256).transpose(3, 0, 4, 1, 2)
    vv = lambda a: a.reshape(DEPTH, 4, 256, 2, 64).transpose(1, 0, 2, 3, 4)
    n_swa_k = cat(lambda r: kT(r["o_swa_kT"]))
    n_swa_v = cat(lambda r: vv(r["o_swa_v"]))
    n_ckv = cat(lambda r: r["o_ckvT"].reshape(DEPTH, 128, 4, 256).transpose(2, 0, 3, 1))
    n_kr = cat(lambda r: r["o_krT"].reshape(DEPTH, 32, 4, 256).transpose(2, 0, 3, 1))
    n_gqa_k = cat(lambda r: kT(r["o_gqa_kT"]))
    n_gqa_v = cat(lambda r: vv(r["o_gqa_v"]))
    return (np.ascontiguousarray(y_prompt.astype(np.float32)), np.ascontiguousarray(y_sample.astype(np.float32)),
            n_hgrn, n_swa_k, n_swa_v, n_ckv, n_kr, n_gqa_k, n_gqa_v)
```

```python
import numpy as np
from contextlib import ExitStack
import ml_dtypes
import concourse.bass as bass
import concourse.mybir as mybir
from concourse.bass_utils import run_bass_kernel_spmd

F32 = mybir.dt.float32
BF16 = mybir.dt.bfloat16
AF = mybir.ActivationFunctionType
ALU = mybir.AluOpType

DM = 1024
DEPTH = 2
NCORE = 8
EPS = 1e-6
OFF = dict(hq=0, ff=512, fb=1024, hi=1536, hg=2048, sq=2560, sk=3072, sv=3200, cq=3328, ckv=3584, kr=3712,
           gq=3744, gk=4256, gv=4384, gates=4512)
D_IN = 8608
UTM_COLS = 1792
STAGES = {"proj", "hgrn", "swa", "mla", "gqa", "merge", "mlp"}


class Sched:
    NDMA = 24

    def __init__(self, nc, es):
        self.nc = nc
        self.eng = {"pe": nc.tensor, "act": nc.scalar, "dve": nc.vector, "pool": nc.gpsimd, "sp": nc.sync}
        self.sem = {k: es.enter_context(nc.semaphore("s_" + k)) for k in ("pe", "act", "dve", "pool")}
        self.cnt = {k: 0 for k in self.sem}
        self.dsem = [es.enter_context(nc.semaphore("d%d" % i)) for i in range(self.NDMA)]
        self.dcnt = [0] * self.NDMA
        self.dnext = 0
        self.seen = {k: {} for k in self.eng}
        self.lastw = {}
        self.reads = {}
        self.n_instr = 0

    def _sem_of(self, key):
        return self.sem[key] if isinstance(key, str) else self.dsem[key[1]]

    def _wait(self, e, tok):
        key, val = tok
        if self.seen[e].get(key, 0) >= val:
            return
        self.eng[e].wait_ge(self._sem_of(key), val)
        self.seen[e][key] = val

    def _deps(self, e, reads, writes, pe_acc=False):
        best = {}
        for r in reads:
            t = self.lastw.get(r)
            if t is not None and best.get(t[0], 0) < t[1]:
                best[t[0]] = t[1]
        for w in writes:
            t = self.lastw.get(w)
            if t is not None and best.get(t[0], 0) < t[1]:
                if not (pe_acc and t[0] == "pe"):
                    best[t[0]] = t[1]
            for t in self.reads.get(w, ()):
                if best.get(t[0], 0) < t[1]:
                    best[t[0]] = t[1]
        for key, val in best.items():
            self._wait(e, (key, val))

    def _record(self, tok, reads, writes):
        for r in reads:
            lst = self.reads.setdefault(r, [])
            lst[:] = [t for t in lst if t[0] != tok[0]]
            lst.append(tok)
        for w in writes:
            self.lastw[w] = tok
            self.reads[w] = []

    def op(self, e, fn, reads=(), writes=(), pe_acc=False):
        self._deps(e, reads, writes, pe_acc)
        ins = fn(self.eng[e])
        self.cnt[e] += 1
        ins.then_inc(self.sem[e], 1)
        self._record((e, self.cnt[e]), reads, writes)
        self.n_instr += 1
        return ins

    def dma(self, q, out, in_, reads=(), writes=()):
        i = self.dnext
        self.dnext = (self.dnext + 1) % self.NDMA
        if self.dcnt[i] > 0:
            self._wait(q, (("d", i), self.dcnt[i]))
        self._deps(q, reads, writes)
        ins = self.eng[q].dma_start(out=out, in_=in_)
        self.dcnt[i] += 16
        ins.then_inc(self.dsem[i], 16)
        self._record((("d", i), self.dcnt[i]), reads, writes)
        self.n_instr += 1
        return ins

    def barrier(self):
        best = {}
        for k in self.cnt:
            if self.cnt[k]:
                best[k] = self.cnt[k]
        for i in range(self.NDMA):
            if self.dcnt[i]:
                best[("d", i)] = self.dcnt[i]
        for e in self.eng:
            for key, val in best.items():
                self._wait(e, (key, val))
        self.lastw = {}
        self.reads = {}


class B:
    def __init__(self, nc, es):
        self.nc, self.es = nc, es
        self.S = Sched(nc, es)
        self.D = {}
        self.psn = 0

    def din(self, name, shape, dt=F32):
        self.D[name] = self.nc.dram_tensor(name, list(shape), dt, kind="ExternalInput").ap()
        return self.D[name]

    def dout(self, name, shape, dt=F32):
        self.D[name] = self.nc.dram_tensor(name, list(shape), dt, kind="ExternalOutput").ap()
        return self.D[name]

    def dscr(self, name, shape, dt=F32):
        self.D[name] = self.nc.dram_tensor(name, list(shape), dt, kind="Internal").ap()
        return self.D[name]

    def sb(self, st, name, shape, dt=F32):
        self.uid = getattr(self, "uid", 0) + 1
        return st.enter_context(self.nc.sbuf_tensor("sb%d_%s" % (self.uid, name), list(shape), dt))

    def ps(self):
        i = self.psn % self.nrot
        self.psn += 1
        return self.psum[i], "ps%d" % i

    nrot = 6

    def acc(self, i):
        k = (6, 7, 4, 5)[i]
        return self.psum[k], "ps%d" % k


def build_program():
    nc = bass.Bass("TRN2", target_bir_lowering=False)
    es = ExitStack()
    b = B(nc, es)
    S = b.S
    D = b.D
    GR = {
        "s": dict(T=2048, nseq=1, L=2048, P=512, rope=True, cond=1),
        "p": dict(T=1024, nseq=4, L=256, P=0, rope=False, cond=0),
    }
    for g, G in GR.items():
        T = G["T"]
        b.din("xT_" + g, [DM, T])
        b.dout("yT_" + g, [DM, T])
        b.dscr("X1_" + g, [DM, T])
        for l in range(DEPTH - 1):
            b.dscr("X2_%d_%s" % (l, g), [DM, T])
        b.dscr("UT_" + g, [68 * 128, T])
        b.dscr("UTM_" + g, [T, UTM_COLS])
        b.dscr("OT_" + g, [4, 512, T], BF16)
        b.dscr("YP_" + g, [DM, T], BF16)
    b.din("cT", [128, 8, 2])
    b.din("w_ada", [DEPTH, DM, 6 * DM])
    b.din("b_adaT", [DEPTH, 128, 48])
    b.din("gains", [DEPTH, 128, 4, 8])
    b.din("w_in", [DEPTH, DM, D_IN])
    b.din("lbT", [DEPTH, 2, 512])
    b.din("hgrn_normT", [DEPTH, 128, 4])
    b.din("sink", [DEPTH, 1, 8])
    b.din("mla_qnT", [DEPTH, 128, 2])
    b.din("mla_kvn", [DEPTH, 128, 1])
    b.din("w_uq", [DEPTH, 256, 768])
    b.din("w_uq_sw", [DEPTH, 256, 768])
    b.din("w_ukv", [DEPTH, 128, 1024])
    b.din("gqa_qn", [DEPTH, 64, 2])
    b.din("gqa_kn", [DEPTH, 64, 2])
    b.din("w_branch", [DEPTH, 4, 512, DM])
    b.din("w_out", [DEPTH, DM, DM])
    b.din("w_mlp_in", [DEPTH, DM, 4 * DM])
    b.din("w_mlp_out", [DEPTH, 4 * DM, DM])
    b.din("st_hgrn", [DEPTH, 2, 4, 128, 128])
    b.din("c_swa_kT", [DEPTH, 2, 64, 512])
    b.din("c_swa_v", [DEPTH, 512, 128])
    b.din("c_ckvT", [DEPTH, 128, 512])
    b.din("c_krT", [DEPTH, 32, 512])
    b.din("c_gqa_kT", [DEPTH, 2, 64, 512])
    b.din("c_gqa_v", [DEPTH, 512, 128])
    b.din("rope64", [4, 96, 2048])
    b.din("cmat", [6, 128, 128])
    b.din("swamask", [2, 128, 512], BF16)
    b.dout("o_hgrn", [DEPTH, 4, 2, 4, 128, 128])
    b.dout("o_swa_kT", [DEPTH, 128, 1024])
    b.dout("o_swa_v", [DEPTH, 1024, 128])
    b.dout("o_ckvT", [DEPTH, 128, 1024])
    b.dout("o_krT", [DEPTH, 32, 1024])
    b.dout("o_gqa_kT", [DEPTH, 128, 1024])
    b.dout("o_gqa_v", [DEPTH, 1024, 128])

    b.psum = [es.enter_context(nc.psum_tensor("psb%d" % i, [128, 512], F32)) for i in range(8)]

    cst = ExitStack()
    es.enter_context(cst)
    ones_bf = b.sb(cst, "ones_bf", [128, 128], BF16)
    ones_f = b.sb(cst, "ones_f", [128, 128], F32)
    cm = b.sb(cst, "cm", [128, 6, 128], F32)
    modT = b.sb(cst, "modT", [128, DEPTH, 48, 2], F32)
    gains = b.sb(cst, "gains", [128, DEPTH, 4, 8], F32)
    coef = b.sb(cst, "coef", [128, DEPTH, 2, 6, 8], F32)
    S.op("pool", lambda e: e.memset(ones_bf[:], 1.0), writes=["ones_bf"])
    S.op("pool", lambda e: e.memset(ones_f[:], 1.0), writes=["ones_f"])
    S.dma("sp", cm[:], D["cmat"].rearrange("a p c -> p a c"), writes=["cm"])
    S.dma("sp", gains[:], D["gains"].rearrange("l p a c -> p l a c"), writes=["gains"])

    with ExitStack() as st:
        cT = b.sb(st, "cT", [128, 8, 2])
        scT = b.sb(st, "scT", [128, 8, 2])
        badaT = b.sb(st, "badaT", [128, DEPTH, 48])
        wa = [b.sb(st, "wa%d" % i, [128, 8, 768]) for i in range(2)]
        S.dma("sp", cT[:], D["cT"], writes=["cT"])
        S.dma("sp", badaT[:], D["b_adaT"].rearrange("l p j -> p l j"), writes=["badaT"])
        S.op("act", lambda e: e.activation(scT[:], cT[:], AF.Silu), reads=["cT"], writes=["scT"])
        n = 0
        for l in range(DEPTH):
            pt, pn = b.ps()
            for cg in range(8):
                w = wa[n % 2]
                wn = "wa%d" % (n % 2)
                n += 1
                S.dma("sp" if cg % 2 == 0 else "act", w[:],
                      D["w_ada"][l, :, cg * 768:(cg + 1) * 768].rearrange("(kc p) c -> p kc c", p=128), writes=[wn])
                for jj in range(6):
                    j = cg * 6 + jj
                    for kc in range(8):
                        S.op("pe", lambda e: e.matmul(pt[:, 2 * j:2 * j + 2], w[:, kc, jj * 128:(jj + 1) * 128], scT[:, kc, :],
                                                      start=(kc == 0), stop=(kc == 7)),
                             reads=[wn, "scT"], writes=[pn], pe_acc=True)
            S.op("dve", lambda e: e.tensor_tensor(modT[:, l], pt[:, 0:96].rearrange("p (j c) -> p j c", c=2),
                                                  badaT[:, l].unsqueeze(2).to_broadcast([128, 48, 2]), ALU.add),
                 reads=[pn, "badaT"], writes=["modT"])
        for l in range(DEPTH):
            for c in range(2):
                m = lambda i: modT[:, l, i * 8:(i + 1) * 8, c]
                S.op("dve", lambda e: e.scalar_tensor_tensor(coef[:, l, c, 0], m(1), 1.0, gains[:, l, 0], ALU.add, ALU.mult),
                     reads=["modT", "gains"], writes=["coef"])
                S.op("dve", lambda e: e.tensor_copy(coef[:, l, c, 1], m(0)), reads=["modT"], writes=["coef"])
                S.op("dve", lambda e: e.tensor_tensor(coef[:, l, c, 2], m(2), gains[:, l, 1], ALU.mult), reads=["modT", "gains"], writes=["coef"])
                S.op("dve", lambda e: e.scalar_tensor_tensor(coef[:, l, c, 3], m(4), 1.0, gains[:, l, 2], ALU.add, ALU.mult),
                     reads=["modT", "gains"], writes=["coef"])
                S.op("dve", lambda e: e.tensor_copy(coef[:, l, c, 4], m(3)), reads=["modT"], writes=["coef"])
                S.op("dve", lambda e: e.tensor_tensor(coef[:, l, c, 5], m(5), gains[:, l, 3], ALU.mult), reads=["modT", "gains"], writes=["coef"])
        S.barrier()

    def rstd_from_sumsq(pt, pn, out, outn, npart, ncol, inv_n):
        S.op("act", lambda e: e.activation(out[0:npart, 0:ncol], pt[0:npart, 0:ncol], AF.Sqrt, bias=epsb[0:npart, :], scale=inv_n),
             reads=[pn, "epsb"], writes=[outn])
        S.op("dve", lambda e: e.reciprocal(out[0:npart, 0:ncol], out[0:npart, 0:ncol]), reads=[outn], writes=[outn])

    epsb = b.sb(cst, "epsb", [128, 1], F32)
    S.op("pool", lambda e: e.memset(epsb[:], EPS), writes=["epsb"])

    def norm_mod_tile(st_unused, xt, xn, hT, hn, t0, a_ap, sh_ap, tmp, sq, rs):
        for kc in range(8):
            S.op("act", lambda e: e.activation(sq[:, kc, :], xt[:, kc, :], AF.Square), reads=[xn], writes=["sq"])
        pt, pn = b.ps()
        for kc in range(8):
            S.op("pe", lambda e: e.matmul(pt[:], ones_bf[:], sq[:, kc, :], start=(kc == 0), stop=(kc == 7)),
                 reads=["sq", "ones_bf"], writes=[pn], pe_acc=True)
        rstd_from_sumsq(pt, pn, rs, "rs", 128, 512, 1.0 / DM)
        for kc in range(8):
            S.op("dve", lambda e: e.tensor_tensor(tmp[:], xt[:, kc, :], rs[:], ALU.mult), reads=[xn, "rs"], writes=["tmp"])
            S.op("act", lambda e: e.activation(hT[:, kc, t0:t0 + 512], tmp[:], AF.Identity, bias=sh_ap[:, kc:kc + 1], scale=a_ap[:, kc:kc + 1]),
                 reads=["tmp", "coef"], writes=[hn])

    def stage_h(st, g, l, Xsrc, ia, ish):
        G = GR[g]
        T = G["T"]
        hT = b.sb(st, "hT", [128, 8, T], BF16)
        with ExitStack() as s2:
            xts = [b.sb(s2, "xt%d" % i, [128, 8, 512]) for i in range(2)]
            tmp = b.sb(s2, "tmp", [128, 512])
            sq = b.sb(s2, "sq", [128, 8, 512], BF16)
            rs = b.sb(s2, "rs", [128, 512])
            for tt in range(T // 512):
                xt, xn = xts[tt % 2], "xt%d" % (tt % 2)
                S.dma("sp", xt[:], Xsrc[:, tt * 512:(tt + 1) * 512].rearrange("(kc p) t -> p kc t", p=128), writes=[xn])
                norm_mod_tile(None, xt, xn, hT, "hT", tt * 512, coef[:, l, G["cond"], ia], coef[:, l, G["cond"], ish], tmp, sq, rs)
            S.barrier()
        return hT

    def stage_proj(g, l, hT):
        G = GR[g]
        T = G["T"]
        UT, UTM = D["UT_" + g], D["UTM_" + g]
        tm_groups = {1: [(0, 512, 0)], 2: [(0, 512, 512)], 3: [(0, 512, 1024)], 6: [(128, 128, 1536)], 8: [(288, 128, 1664)]}
        with ExitStack() as st:
            wb = [b.sb(st, "wb%d" % i, [128, 8, 512], BF16) for i in range(2)]
            ev = [b.sb(st, "ev%d" % i, [128, 512]) for i in range(4)]
            nev = 0
            for cg in range(17):
                ncol = 512 if cg < 16 else D_IN - 8192
                w, wn = wb[cg % 2], "wb%d" % (cg % 2)
                S.dma("pool", w[:, :, 0:ncol], D["w_in"][l, :, cg * 512:cg * 512 + ncol].rearrange("(kc p) c -> p kc c", p=128), writes=[wn])
                for tt in range(T // 512):
                    for cc in range((ncol + 127) // 128):
                        m = min(128, ncol - cc * 128)
                        pt, pn = b.ps()
                        for kc in range(8):
                            S.op("pe", lambda e: e.matmul(pt[0:m, :], w[:, kc, cc * 128:cc * 128 + m], hT[:, kc, tt * 512:(tt + 1) * 512],
                                                          start=(kc == 0), stop=(kc == 7)), reads=[wn, "hT"], writes=[pn], pe_acc=True)
                        e_, en = ev[nev % 4], "ev%d" % (nev % 4)
                        eng = "act" if nev % 2 == 0 else "dve"
                        nev += 1
                        if eng == "act":
                            S.op("act", lambda e: e.copy(e_[0:m, :], pt[0:m, :]), reads=[pn], writes=[en])
                        else:
                            S.op("dve", lambda e: e.tensor_copy(e_[0:m, :], pt[0:m, :]), reads=[pn], writes=[en])
                        r0 = cg * 512 + cc * 128
                        S.dma("sp", UT[r0:r0 + m, tt * 512:(tt + 1) * 512], e_[0:m, :], reads=[en], writes=[])
                for (c0, cn, dst) in tm_groups.get(cg, []):
                    for t4 in range(T // 128):
                        pt, pn = b.ps()
                        for kc in range(8):
                            S.op("pe", lambda e: e.matmul(pt[:, 0:cn], hT[:, kc, t4 * 128:(t4 + 1) * 128], w[:, kc, c0:c0 + cn],
                                                          start=(kc == 0), stop=(kc == 7)), reads=[wn, "hT"], writes=[pn], pe_acc=True)
                        e_, en = ev[nev % 4], "ev%d" % (nev % 4)
                        eng = "act" if nev % 2 == 0 else "dve"
                        nev += 1
                        if eng == "act":
                            S.op("act", lambda e: e.copy(e_[:, 0:cn], pt[:, 0:cn]), reads=[pn], writes=[en])
                        else:
                            S.op("dve", lambda e: e.tensor_copy(e_[:, 0:cn], pt[:, 0:cn]), reads=[pn], writes=[en])
                        S.dma("sp", UTM[t4 * 128:(t4 + 1) * 128, dst:dst + cn], e_[:, 0:cn], reads=[en], writes=[])
            S.barrier()

    def attn_unit(qT, qn, Kd, Nq, chunks, scale, o_out, on, wk, sink=None):
        po, pon = b.acc(0)
        pd, pdn = b.acc(1)
        nch = len(chunks)

        def emit_st(i):
            kT, kn, V, vn, nk, mask = chunks[i]
            pst, psn_ = b.ps()
            S.op("pe", lambda e: e.matmul(pst[0:nk, 0:Nq], kT, qT, start=True, stop=True), reads=[kn] + (qn if isinstance(qn, list) else [qn]), writes=[psn_])
            return pst, psn_

        cur = emit_st(0)
        for i, (kT, kn, V, vn, nk, mask) in enumerate(chunks):
            nxt = emit_st(i + 1) if i + 1 < nch else None
            pst, psn_ = cur
            E, En = wk["E"][i % 3], "E%d" % (i % 3)
            S.op("act", lambda e: e.activation(E[0:nk, 0:Nq], pst[0:nk, 0:Nq], AF.Exp, scale=scale), reads=[psn_], writes=[En])
            if mask is not None:
                S.op("pool", lambda e: e.tensor_tensor(E[0:nk, 0:Nq], E[0:nk, 0:Nq], mask, ALU.mult), reads=[En, "swamask"], writes=[En])
            last = (i == nch - 1) and sink is None
            S.op("pe", lambda e: e.matmul(po[0:64, 0:Nq], V, E[0:nk, 0:Nq], start=(i == 0), stop=(i == nch - 1)),
                 reads=[vn, En], writes=[pon], pe_acc=(i > 0))
            S.op("pe", lambda e: e.matmul(pd[0:64, 0:Nq], ones_bf[0:nk, 0:64], E[0:nk, 0:Nq], start=(i == 0), stop=last),
                 reads=["ones_bf", En], writes=[pdn], pe_acc=(i > 0))
            cur = nxt
        if sink is not None:
            S.op("pe", lambda e: e.matmul(pd[0:64, 0:Nq], ones_bf[0:1, 0:64], sink, start=False, stop=True),
                 reads=["ones_bf", "sinkrow"], writes=[pdn], pe_acc=True)
        rec = wk["rec"]
        S.op("dve", lambda e: e.reciprocal(rec[0:64, 0:Nq], pd[0:64, 0:Nq]), reads=[pdn], writes=["rec"])
        S.op("dve", lambda e: e.tensor_tensor(o_out, po[0:64, 0:Nq], rec[0:64, 0:Nq], ALU.mult), reads=[pon, "rec"], writes=[on])

    def rr_alloc(st, T, do_rope):
        sets = []
        for k in range(2):
            sets.append(dict(k=k, x=b.sb(st, "rr_x%d" % k, [64, T]), xs=(b.sb(st, "rr_xs%d" % k, [64, T]) if do_rope else None),
                             sq=b.sb(st, "rr_sq%d" % k, [64, 512], BF16), rs=b.sb(st, "rr_rs%d" % k, [64, T]), t1=b.sb(st, "rr_t1%d" % k, [64, 512])))
        return sets

    def rms_rope_rows(W_, src_rows_fn, n_rows, T, gain2, gainn, do_norm, do_rope, rope_idx, out_bf, outn, out32=None, out32n=None,
                      out32_pre_rope=False):
        k = W_["k"]
        x, xs, sqb, rs, t1 = W_["x"], W_["xs"], W_["sq"], W_["rs"], W_["t1"]
        xn, xsn, sqn, rsn, t1n = ("rr_x%d" % k, "rr_xs%d" % k, "rr_sq%d" % k, "rr_rs%d" % k, "rr_t1%d" % k)
        for (r0, nr, ap) in src_rows_fn(False):
            S.dma("sp", x[r0:r0 + nr, 0:T], ap, writes=[xn])
        if do_rope:
            for (r0, nr, ap) in src_rows_fn(True):
                S.dma("act", xs[r0:r0 + nr, 0:T], ap, writes=[xsn])
        nr = n_rows
        W = min(512, T)
        for c in range(T // W):
            cs = slice(c * W, (c + 1) * W)
            if do_norm:
                S.op("act", lambda e: e.activation(sqb[0:nr, 0:W], x[0:nr, cs], AF.Square), reads=[xn], writes=[sqn])
                pt, pn = b.ps()
                S.op("pe", lambda e: e.matmul(pt[0:nr, 0:W], ones_bf[0:nr, 0:nr], sqb[0:nr, 0:W], start=True, stop=True),
                     reads=[sqn, "ones_bf"], writes=[pn])
                rstd_from_sumsq(pt, pn, rs[:, cs], rsn, nr, W, 1.0 / nr)
                S.op("dve", lambda e: e.scalar_tensor_tensor(x[0:nr, cs], x[0:nr, cs], gain2[0:nr, 0:1], rs[0:nr, cs], ALU.mult, ALU.mult),
                     reads=[xn, rsn, gainn], writes=[xn])
                if do_rope:
                    S.op("dve", lambda e: e.scalar_tensor_tensor(xs[0:nr, cs], xs[0:nr, cs], gain2[0:nr, 1:2], rs[0:nr, cs], ALU.mult, ALU.mult),
                         reads=[xsn, rsn, gainn], writes=[xsn])
            if out32 is not None and out32_pre_rope:
                S.op("pool", lambda e: e.tensor_copy(out32[0:nr, cs], x[0:nr, cs]), reads=[xn], writes=[out32n])
            if do_rope:
                S.op("dve", lambda e: e.tensor_tensor(x[0:nr, cs], x[0:nr, cs], rope[0:nr, rope_idx, cs], ALU.mult), reads=[xn, "rope"], writes=[xn])
                S.op("pool", lambda e: e.tensor_tensor(t1[0:nr, 0:W], xs[0:nr, cs], rope[0:nr, rope_idx + 1, cs], ALU.mult), reads=[xsn, "rope"], writes=[t1n])
                S.op("dve", lambda e: e.tensor_tensor(out_bf[0:nr, cs], x[0:nr, cs], t1[0:nr, 0:W], ALU.add), reads=[xn, t1n], writes=[outn])
            else:
                S.op("act", lambda e: e.copy(out_bf[0:nr, cs], x[0:nr, cs]), reads=[xn], writes=[outn])

    def swap_rows(base, T0, T1, UT, nd):
        q = nd // 4
        def f(swapped):
            if not swapped:
                return [(0, nd, UT[base:base + nd, T0:T1])]
            return [(0, q, UT[base + q:base + 2 * q, T0:T1]), (q, q, UT[base:base + q, T0:T1]),
                    (2 * q, q, UT[base + 3 * q:base + 4 * q, T0:T1]), (3 * q, q, UT[base + 2 * q:base + 3 * q, T0:T1])]
        return f

    rope = b.sb(cst, "rope", [96, 4, 2048], BF16)
    S.dma("pool", rope[:], D["rope64"].rearrange("a p t -> p a t"), writes=["rope"])

    def stage_gqa_like(g, l, kind):
        G = GR[g]
        T, L, P, nseq, do_rope = G["T"], G["L"], G["P"], G["nseq"], G["rope"]
        UT, UTM, OT = D["UT_" + g], D["UTM_" + g], D["OT_" + g]
        qo, ko = (OFF["sq"], OFF["sk"]) if kind == "swa" else (OFF["gq"], OFF["gk"])
        vcol = 1536 if kind == "swa" else 1664
        bi = 1 if kind == "swa" else 3
        do_norm = kind == "gqa"
        scale = 64 ** -0.5
        nkc_ctx = P // 128
        with ExitStack() as st:
            kT = b.sb(st, "kT", [64, 2, P + T], BF16)
            Vt = b.sb(st, "Vt", [128, (P + T) // 128, 128], BF16)
            qTh = b.sb(st, "qTh", [64, 8, T], BF16)
            gq2 = b.sb(st, "gq2", [64, 2]); gk2 = b.sb(st, "gk2", [64, 2])
            sinkrow = b.sb(st, "sinkrow", [1, 8, 128], BF16)
            sk32 = b.sb(st, "sk32", [1, 8])
            wk = dict(E=[b.sb(st, "E%d" % i, [128, 512], BF16) for i in range(3)], rec=b.sb(st, "rec", [64, 512]))
            obs = [b.sb(st, "ob%d" % i, [64, 512], BF16) for i in range(2)]
            swm = b.sb(st, "swamask", [128, 2, 512], BF16)
            S.dma("sp", swm[:], D["swamask"].rearrange("a p c -> p a c"), writes=["swamask"])
            if do_norm:
                S.dma("sp", gq2[:], D["gqa_qn"][l], writes=["gq2"])
                S.dma("sp", gk2[:], D["gqa_kn"][l], writes=["gk2"])
            else:
                S.dma("sp", sk32[:], D["sink"][l], writes=["sk32"])
                S.op("act", lambda e: e.activation(sk32[:], sk32[:], AF.Exp), reads=["sk32"], writes=["sk32"])
                S.op("dve", lambda e: e.tensor_copy(sinkrow[:], sk32[:].unsqueeze(2).to_broadcast([1, 8, 128])), reads=["sk32"], writes=["sinkrow"])
            with ExitStack() as s2:
                RR = rr_alloc(s2, T, do_rope)
                k32 = [b.sb(s2, "k32_%d" % i, [64, T]) for i in range(2)] if g == "p" else None
                v32 = b.sb(s2, "v32", [128, T // 128, 128])
                if P:
                    src_k = D["c_swa_kT"] if kind == "swa" else D["c_gqa_kT"]
                    src_v = D["c_swa_v"] if kind == "swa" else D["c_gqa_v"]
                    S.dma("pool", kT[:, :, 0:P], src_k[l].rearrange("h d t -> d h t"), writes=["kT"])
                    S.dma("pool", Vt[:, 0:P // 128, :], src_v[l].rearrange("(c p) f -> p c f", p=128), writes=["Vt"])
                S.dma("sp", v32[:], UTM[:, vcol:vcol + 128].rearrange("(c p) f -> p c f", p=128), writes=["v32"])
                S.op("dve", lambda e: e.tensor_copy(Vt[:, P // 128:, :], v32[:]), reads=["v32"], writes=["Vt"])
                if g == "p":
                    dst = D["o_swa_v"] if kind == "swa" else D["o_gqa_v"]
                    S.dma("act", dst[l].rearrange("(c p) f -> p c f", p=128), v32[:], reads=["v32"], writes=[])
                nrr = 0
                for kvh in range(2):
                    rms_rope_rows(RR[nrr % 2], swap_rows(ko + kvh * 64, 0, T, UT, 64), 64, T, gk2, "gk2", do_norm, do_rope, 0,
                                  kT[:, kvh, P:P + T], "kT", out32=(k32[kvh] if g == "p" else None), out32n="k32_%d" % kvh, out32_pre_rope=True)
                    nrr += 1
                    if g == "p":
                        dst = D["o_swa_kT"] if kind == "swa" else D["o_gqa_kT"]
                        S.dma("sp", dst[l, kvh * 64:(kvh + 1) * 64, :], k32[kvh][:], reads=["k32_%d" % kvh], writes=[])
                for h in range(8):
                    rms_rope_rows(RR[nrr % 2], swap_rows(qo + h * 64, 0, T, UT, 64), 64, T, gq2, "gq2", do_norm, do_rope, 0, qTh[:, h, :], "qTh%d" % h)
                    nrr += 1
                nu = 0
                for sq_ in range(nseq):
                    T0 = sq_ * L
                    kb0 = P + T0
                    for kvh in range(2):
                        for qb in range(L // 128):
                            chunks = []
                            for c in range(nkc_ctx):
                                chunks.append((kT[:, kvh, c * 128:(c + 1) * 128], "kT", Vt[:, c, kvh * 64:(kvh + 1) * 64], "Vt", 128, None))
                            if kind == "swa" and P:
                                blocks = [(kb, mi) for (kb, mi) in ((qb - 1, 0), (qb, None), (qb + 1, 1)) if 0 <= kb < L // 128]
                            else:
                                blocks = [(kb, None) for kb in range(L // 128)]
                            for (kb, mi) in blocks:
                                c0 = kb0 + kb * 128
                                chunks.append((kT[:, kvh, c0:c0 + 128], "kT", Vt[:, c0 // 128, kvh * 64:(kvh + 1) * 64], "Vt",
                                               128, None if mi is None else swm[:, mi, :]))
                            ob, obn = obs[nu % 2], "ob%d" % (nu % 2)
                            nu += 1
                            q0 = T0 + qb * 128
                            attn_unit(qTh[:, kvh * 4:(kvh + 1) * 4, q0:q0 + 128], ["qTh%d" % hh for hh in range(kvh * 4, kvh * 4 + 4)], 64, 512, chunks, scale,
                                      ob[:], obn, wk, sink=(sinkrow[0:1, kvh * 4:(kvh + 1) * 4, :] if kind == "swa" else None))
                            S.dma("sp", OT[bi, kvh * 256:(kvh + 1) * 256, q0:q0 + 128].rearrange("(h d) t -> d h t", d=64),
                                  ob[:].rearrange("d (h t) -> d h t", h=4), reads=[obn], writes=[])
                S.barrier()

    def stage_mla(g, l):
        G = GR[g]
        T, L, P, nseq, do_rope = G["T"], G["L"], G["P"], G["nseq"], G["rope"]
        UT, OT = D["UT_" + g], D["OT_" + g]
        scale = 96 ** -0.5
        NK = P + L
        for sq_ in range(nseq):
            T0, T1 = sq_ * L, (sq_ + 1) * L
            with ExitStack() as st:
                ckvT = b.sb(st, "ckvT", [128, NK], BF16)
                krT = b.sb(st, "krT", [96, NK], BF16)
                cqn = b.sb(st, "cqn", [128, 2, L], BF16)
                wuq = b.sb(st, "wuq", [128, 2, 768], BF16); wuqs = b.sb(st, "wuqs", [128, 2, 768], BF16)
                wukv = b.sb(st, "wukv", [128, 1024], BF16)
                g_q = b.sb(st, "g_q", [128, 2]); g_kv = b.sb(st, "g_kv", [128, 1])
                S.dma("pool", wuq[:], D["w_uq"][l].rearrange("(kc p) c -> p kc c", p=128), writes=["wuq"])
                S.dma("pool", wuqs[:], D["w_uq_sw"][l].rearrange("(kc p) c -> p kc c", p=128), writes=["wuqs"])
                S.dma("pool", wukv[:], D["w_ukv"][l], writes=["wukv"])
                S.dma("sp", g_q[:], D["mla_qnT"][l], writes=["g_q"])
                S.dma("sp", g_kv[:], D["mla_kvn"][l], writes=["g_kv"])
                with ExitStack() as s2:
                    x = b.sb(s2, "m_x", [128, 2, L]); sqb = b.sb(s2, "m_sq", [128, 2, 512], BF16); rs = b.sb(s2, "m_rs", [128, 512])
                    xk = b.sb(s2, "m_xk", [128, L]); kr32 = b.sb(s2, "m_kr", [96, L]); krs = b.sb(s2, "m_krs", [96, L]); t1 = b.sb(s2, "m_t1", [96, 512])
                    if P:
                        S.dma("pool", ckvT[:, 0:P], D["c_ckvT"][l], writes=["ckvT"])
                        S.dma("pool", krT[64:96, 0:P], D["c_krT"][l], writes=["krT"])
                    S.dma("sp", x[:], UT[OFF["cq"]:OFF["cq"] + 256, T0:T1].rearrange("(kc p) t -> p kc t", p=128), writes=["m_x"])
                    S.dma("sp", xk[:], UT[OFF["ckv"]:OFF["ckv"] + 128, T0:T1], writes=["m_xk"])
                    S.dma("sp", kr32[64:96, :], UT[OFF["kr"]:OFF["kr"] + 32, T0:T1], writes=["m_kr"])
                    if do_rope:
                        for (r0, nr, ap) in swap_rows(OFF["kr"], T0, T1, UT, 32)(True):
                            S.dma("act", krs[64 + r0:64 + r0 + nr, :], ap, writes=["m_krs"])
                    for c in range(L // 512 if L >= 512 else 1):
                        w = min(512, L)
                        cs = slice(c * w, (c + 1) * w)
                        pt, pn = b.ps()
                        for kc in range(2):
                            S.op("act", lambda e: e.activation(sqb[:, kc, 0:w], x[:, kc, cs], AF.Square), reads=["m_x"], writes=["m_sq"])
                        for kc in range(2):
                            S.op("pe", lambda e: e.matmul(pt[:, 0:w], ones_bf[:], sqb[:, kc, 0:w], start=(kc == 0), stop=(kc == 1)),
                                 reads=["m_sq", "ones_bf"], writes=[pn], pe_acc=True)
                        rstd_from_sumsq(pt, pn, rs, "m_rs", 128, w, 1.0 / 256)
                        for kc in range(2):
                            S.op("dve", lambda e: e.scalar_tensor_tensor(cqn[:, kc, cs], x[:, kc, cs], g_q[:, kc:kc + 1], rs[:, 0:w], ALU.mult, ALU.mult),
                                 reads=["m_x", "m_rs", "g_q"], writes=["cqn"])
                        pt, pn = b.ps()
                        S.op("act", lambda e: e.activation(sqb[:, 0, 0:w], xk[:, cs], AF.Square), reads=["m_xk"], writes=["m_sq"])
                        S.op("pe", lambda e: e.matmul(pt[:, 0:w], ones_bf[:], sqb[:, 0, 0:w], start=True, stop=True), reads=["m_sq", "ones_bf"], writes=[pn])
                        rstd_from_sumsq(pt, pn, rs, "m_rs", 128, w, 1.0 / 128)
                        S.op("dve", lambda e: e.scalar_tensor_tensor(xk[:, cs], xk[:, cs], g_kv[:, 0:1], rs[:, 0:w], ALU.mult, ALU.mult),
                             reads=["m_xk", "m_rs", "g_kv"], writes=["m_xk"])
                        S.op("pool", lambda e: e.tensor_copy(ckvT[:, P + c * w:P + (c + 1) * w], xk[:, cs]), reads=["m_xk"], writes=["ckvT"])
                        if do_rope:
                            S.op("dve", lambda e: e.tensor_tensor(t1[64:96, 0:w], kr32[64:96, cs], rope[64:96, 2, cs], ALU.mult), reads=["m_kr", "rope"], writes=["m_t1"])
                            S.op("pool", lambda e: e.tensor_tensor(krs[64:96, cs], krs[64:96, cs], rope[64:96, 3, cs], ALU.mult), reads=["m_krs", "rope"], writes=["m_krs"])
                            S.op("dve", lambda e: e.tensor_tensor(krT[64:96, P + c * w:P + (c + 1) * w], t1[64:96, 0:w], krs[64:96, cs], ALU.add),
                                 reads=["m_t1", "m_krs"], writes=["krT"])
                        else:
                            S.op("dve", lambda e: e.tensor_copy(krT[64:96, P + c * w:P + (c + 1) * w], kr32[64:96, cs]), reads=["m_kr"], writes=["krT"])
                    if g == "p":
                        S.dma("sp", D["o_ckvT"][l, :, T0:T1], xk[:], reads=["m_xk"], writes=[])
                        S.dma("sp", D["o_krT"][l, :, T0:T1], kr32[64:96, :], reads=["m_kr"], writes=[])
                    S.barrier()
                Vall = b.sb(st, "Vall", [128, NK // 128, 512], BF16)
                for c in range(NK // 128):
                    pt, pn = b.ps()
                    S.op("pe", lambda e: e.matmul(pt[:, :], ckvT[:, c * 128:(c + 1) * 128],
                                                  wukv[:].rearrange("p (h x) -> p h x", x=128)[:, :, 64:128], start=True, stop=True),
                         reads=["ckvT", "wukv"], writes=[pn])
                    if c % 2 == 0:
                        S.op("act", lambda e: e.copy(Vall[:, c, :], pt[:, :]), reads=[pn], writes=["Vall"])
                    else:
                        S.op("dve", lambda e: e.tensor_copy(Vall[:, c, :], pt[:, :]), reads=[pn], writes=["Vall"])
                S.barrier()
                with ExitStack() as s2:
                    wk = dict(E=[b.sb(s2, "E%d" % i, [128, 512], BF16) for i in range(3)], rec=b.sb(s2, "rec", [64, 512]))
                    kTh = [b.sb(s2, "kTh%d" % i, [96, NK], BF16) for i in range(2)]
                    qh = [b.sb(s2, "qh%d" % i, [96, L], BF16) for i in range(2)]
                    obs = [b.sb(s2, "ob%d" % i, [64, 512], BF16) for i in range(2)]
                    t2 = b.sb(s2, "m_t2", [96, 512]); t3 = b.sb(s2, "m_t3", [96, 512])
                    nu = 0
                    W = min(512, L)
                    for h in range(8):
                        kt, ktn = kTh[h % 2], "kTh%d" % (h % 2)
                        q_, q_n = qh[h % 2], "qh%d" % (h % 2)
                        for c in range((NK + 511) // 512):
                            w = min(512, NK - c * 512)
                            pt, pn = b.ps()
                            S.op("pe", lambda e: e.matmul(pt[0:64, 0:w], wukv[:, h * 128:h * 128 + 64], ckvT[:, c * 512:c * 512 + w], start=True, stop=True),
                                 reads=["wukv", "ckvT"], writes=[pn])
                            S.op("act", lambda e: e.copy(kt[0:64, c * 512:c * 512 + w], pt[0:64, 0:w]), reads=[pn], writes=[ktn])
                        S.op("pool", lambda e: e.tensor_copy(kt[64:96, :], krT[64:96, :]), reads=["krT"], writes=[ktn])
                        for c in range(L // W):
                            cs = slice(c * W, (c + 1) * W)
                            pt, pn = b.ps()
                            for kc in range(2):
                                S.op("pe", lambda e: e.matmul(pt[0:96, 0:W], wuq[:, kc, h * 96:(h + 1) * 96], cqn[:, kc, cs], start=(kc == 0), stop=(kc == 1)),
                                     reads=["wuq", "cqn"], writes=[pn], pe_acc=True)
                            S.op("act", lambda e: e.copy(q_[0:64, cs], pt[0:64, 0:W]), reads=[pn], writes=[q_n])
                            if do_rope:
                                pt2, pn2 = b.ps()
                                for kc in range(2):
                                    S.op("pe", lambda e: e.matmul(pt2[0:96, 0:W], wuqs[:, kc, h * 96:(h + 1) * 96], cqn[:, kc, cs], start=(kc == 0), stop=(kc == 1)),
                                         reads=["wuqs", "cqn"], writes=[pn2], pe_acc=True)
                                S.op("dve", lambda e: e.tensor_tensor(t2[64:96, 0:W], pt[64:96, 0:W], rope[64:96, 2, cs], ALU.mult), reads=[pn, "rope"], writes=["m_t2"])
                                S.op("dve", lambda e: e.tensor_tensor(t3[64:96, 0:W], pt2[64:96, 0:W], rope[64:96, 3, cs], ALU.mult), reads=[pn2, "rope"], writes=["m_t3"])
                                S.op("pool", lambda e: e.tensor_tensor(q_[64:96, cs], t2[64:96, 0:W], t3[64:96, 0:W], ALU.add), reads=["m_t2", "m_t3"], writes=[q_n])
                            else:
                                S.op("dve", lambda e: e.tensor_copy(q_[64:96, cs], pt[64:96, 0:W]), reads=[pn], writes=[q_n])
                        for c in range(L // W):
                            chunks = [(kt[:, kc * 128:(kc + 1) * 128], ktn, Vall[:, kc, h * 64:(h + 1) * 64], "Vall", 128, None) for kc in range(NK // 128)]
                            ob, obn = obs[nu % 2], "ob%d" % (nu % 2)
                            nu += 1
                            attn_unit(q_[:, c * W:(c + 1) * W], q_n, 96, W, chunks, scale, ob[:, 0:W], obn, wk)
                            S.dma("sp", OT[2, h * 64:(h + 1) * 64, T0 + c * W:T0 + (c + 1) * W], ob[:, 0:W], reads=[obn], writes=[])
                    S.barrier()

    def stage_hgrn(g, l):
        G = GR[g]
        T, L, P, nseq = G["T"], G["L"], G["P"], G["nseq"]
        UT, UTM, OT = D["UT_" + g], D["UTM_" + g], D["OT_" + g]
        NTL = L // 128
        ident, triU, triL, sL, sU, csel = (cm[:, i, :] for i in range(6))
        with ExitStack() as st:
            lbb = b.sb(st, "lbb", [128, 2, 512]); oml = b.sb(st, "oml", [128, 2, 512])
            with ExitStack() as s2:
                lbr = b.sb(s2, "lbr", [128, DEPTH, 2, 512]); den = b.sb(s2, "lden", [128, 2, 512])
                S.dma("sp", lbr[:], D["lbT"].partition_broadcast(128), writes=["lbr"])
                S.op("act", lambda e: e.activation(lbr[:], lbr[:], AF.Exp), reads=["lbr"], writes=["lbr"])
                S.op("dve", lambda e: e.tensor_tensor(den[:], lbr[:, 0], lbr[:, 1], ALU.add), reads=["lbr"], writes=["lden"])
                S.op("dve", lambda e: e.reciprocal(den[:], den[:]), reads=["lden"], writes=["lden"])
                if l == 0:
                    S.op("pool", lambda e: e.memset(lbb[:], 0.0), writes=["lbb"])
                else:
                    S.op("dve", lambda e: e.tensor_tensor(lbb[:], lbr[:, 1], den[:], ALU.mult), reads=["lbr", "lden"], writes=["lbb"])
                S.op("dve", lambda e: e.tensor_scalar(oml[:], lbb[:], -1.0, 1.0, ALU.mult, ALU.add), reads=["lbb"], writes=["oml"])
                S.barrier()
            hn = b.sb(st, "hn", [128, 4]); S.dma("sp", hn[:], D["hgrn_normT"][l], writes=["hn"])
            Sst = b.sb(st, "Sst", [128, 2, 4, 128])
            SSTN = ["Sst%d%d" % (d_, h_) for d_ in range(2) for h_ in range(4)]
            oall = b.sb(st, "oall", [128, 2, 4, L], BF16)
            W_ = {}
            for d in range(2):
                for k in range(2):
                    sfx = "%d%d" % (d, k)
                    W_[d, k] = dict(
                        sfx=sfx,
                        tt=b.sb(st, "h_t" + sfx, [128, 512]), vv=b.sb(st, "h_v" + sfx, [128, 512]), gg=b.sb(st, "h_g" + sfx, [128, 512]),
                        kk=b.sb(st, "h_k" + sfx, [128, 512]), kt=b.sb(st, "h_kt" + sfx, [128, 512]), kh=b.sb(st, "h_kh" + sfx, [128, 512]),
                        khm=b.sb(st, "h_khm" + sfx, [128, 4, 4, 128]), qT=b.sb(st, "h_qT" + sfx, [128, 4, 128]), qt=b.sb(st, "h_qt" + sfx, [128, 4, 128]),
                        eb=b.sb(st, "h_eb" + sfx, [128, 4, 132]), ktT=b.sb(st, "h_ktT" + sfx, [128, 4, 128]), AT=b.sb(st, "h_AT" + sfx, [128, 4, 128]))
            fin = dict(os=b.sb(st, "f_os", [128, 4, 256]), gT=b.sb(st, "f_gT", [128, 4, 256]), sq=b.sb(st, "f_sq", [128, 4, 256], BF16),
                       rs=b.sb(st, "f_rs", [128, 4, 256]), ob=b.sb(st, "f_ob", [128, 4, 256], BF16))
            it = 0
            b.nrot = 4
            tmpo = [b.sb(st, "h_tmpo%d" % d_, [128, 512]) for d_ in range(2)]
            for sq_ in range(nseq):
                T0 = sq_ * L
                if P:
                    S.dma("sp", Sst[:], D["st_hgrn"][l].rearrange("d h k v -> k d h v"), writes=SSTN)
                else:
                    S.op("pool", lambda e: e.memset(Sst[:], 0.0), writes=SSTN)
                for i in range(NTL):
                    k = it % 2
                    it += 1
                    tis = (i, NTL - 1 - i)
                    pos = [b.acc(0), b.acc(1)]
                    pis = [b.acc(2), b.acc(3)]
                    for d in range(2):
                        w = W_[d, k]
                        x = w["sfx"]
                        t0 = T0 + tis[d] * 128
                        M_in, M_ex = (triU, sL) if d == 0 else (triL, sU)
                        tt_, vv, gg, kk, kt_, kh, khm, qT, qt, eb, ktT, AT = (w[n_] for n_ in ("tt", "vv", "gg", "kk", "kt", "kh", "khm", "qT", "qt", "eb", "ktT", "AT"))
                        S.dma("sp", tt_[:], UTM[t0:t0 + 128, d * 512:(d + 1) * 512], writes=["h_t" + x])
                        S.dma("sp", vv[:], UTM[t0:t0 + 128, 1024:1536], writes=["h_v" + x])
                        S.dma("act", qT[:], UT[0:512, t0:t0 + 128].rearrange("(h p) t -> p h t", p=128), writes=["h_qT" + x])
                        S.op("act", lambda e: e.activation(qT[:], qT[:], AF.Silu), reads=["h_qT" + x], writes=["h_qT" + x])
                        S.op("act", lambda e: e.activation(tt_[:], tt_[:], AF.Sigmoid), reads=["h_t" + x], writes=["h_t" + x])
                        S.op("dve", lambda e: e.tensor_tensor(tt_[:], tt_[:], oml[:, d], ALU.mult), reads=["h_t" + x, "oml"], writes=["h_t" + x])
                        S.op("dve", lambda e: e.scalar_tensor_tensor(gg[:], tt_[:], 1e-30, lbb[:, d], ALU.max, ALU.add), reads=["h_t" + x, "lbb"], writes=["h_g" + x])
                        S.op("act", lambda e: e.activation(gg[:], gg[:], AF.Ln), reads=["h_g" + x], writes=["h_g" + x])
                        S.op("dve", lambda e: e.tensor_tensor(kk[:], oml[:, d], tt_[:], ALU.subtract), reads=["h_t" + x, "oml"], writes=["h_k" + x])
                        pb, pbn = b.ps()
                        S.op("pe", lambda e: e.matmul(pb[:], M_in, gg[:], start=True, stop=True), reads=["cm", "h_g" + x], writes=[pbn])
                        S.op("act", lambda e: e.activation(kt_[:], pb[:], AF.Exp, scale=-1.0), reads=[pbn], writes=["h_kt" + x])
                        S.op("dve", lambda e: e.tensor_tensor(kt_[:], kt_[:], kk[:], ALU.mult), reads=["h_kt" + x, "h_k" + x], writes=["h_kt" + x])
                        pr, prn = b.ps()
                        S.op("pe", lambda e: e.matmul(pr[:], M_ex, gg[:], start=True, stop=True), reads=["cm", "h_g" + x], writes=[prn])
                        S.op("act", lambda e: e.activation(kh[:], pr[:], AF.Exp), reads=[prn], writes=["h_kh" + x])
                        S.op("dve", lambda e: e.tensor_tensor(kh[:], kh[:], kk[:], ALU.mult), reads=["h_kh" + x, "h_k" + x], writes=["h_kh" + x])
                        po_, pon_ = pos[d]
                        for h in range(4):
                            hs = slice(h * 128, (h + 1) * 128)
                            pbt, pbtn = b.ps()
                            S.op("pe", lambda e: e.matmul(pbt[:, 0:128], gg[:, hs], M_in, start=True, stop=True), reads=["h_g" + x, "cm"], writes=[pbtn])
                            S.op("pe", lambda e: e.matmul(pbt[:, 128:132], gg[:, hs], csel[:, 0:4], start=True, stop=True), reads=["h_g" + x, "cm"], writes=[pbtn])
                            S.op("act", lambda e: e.activation(eb[:, h, :], pbt[:, 0:132], AF.Exp), reads=[pbtn], writes=["h_eb" + x])
                            S.op("dve", lambda e: e.tensor_tensor(qt[:, h, :], qT[:, h, :], eb[:, h, 0:128], ALU.mult), reads=["h_qT" + x, "h_eb" + x], writes=["h_qt" + x])
                            pk, pkn = b.ps()
                            S.op("pe", lambda e: e.matmul(pk[:, 0:128], kt_[:, hs], ident, start=True, stop=True), reads=["h_kt" + x, "cm"], writes=[pkn])
                            S.op("act", lambda e: e.copy(ktT[:, h, :], pk[:, 0:128]), reads=[pkn], writes=["h_ktT" + x])
                            pa, pan = b.ps()
                            S.op("pe", lambda e: e.matmul(pa[:, 0:128], ktT[:, h, :], qt[:, h, :], start=True, stop=True), reads=["h_ktT" + x, "h_qt" + x], writes=[pan])
                            S.op("dve", lambda e: e.tensor_tensor(AT[:, h, :], pa[:, 0:128], M_in, ALU.mult), reads=[pan, "cm"], writes=["h_AT" + x])
                            for j in range(4):
                                S.op("pool", lambda e: e.tensor_scalar(khm[:, h, j, :], kh[:, hs], csel[:, j:j + 1], None, ALU.mult),
                                     reads=["h_kh" + x, "cm"], writes=["h_khm" + x])
                            S.op("pe", lambda e: e.matmul(po_[:, hs], vv[:, hs], AT[:, h, :], start=True, stop=True), reads=["h_v" + x, "h_AT" + x], writes=[pon_], pe_acc=(h > 0))
                    for n_ in range(4):
                        pss = []
                        for d in range(2):
                            w = W_[d, k]
                            x = w["sfx"]
                            j = n_ if d == 0 else 3 - n_
                            js = slice(j * 32, (j + 1) * 32)
                            po_, pon_ = pis[d]
                            ps_, psn2 = b.ps()
                            pss.append((ps_, psn2, j))
                            for h in range(4):
                                hs = slice(h * 128, (h + 1) * 128)
                                S.op("pe", lambda e: e.matmul(po_[:, h * 128 + j * 32:h * 128 + (j + 1) * 32], Sst[:, d, h, :], w["qt"][:, h, js], start=True, stop=True),
                                     reads=["Sst%d%d" % (d, h), "h_qt" + x], writes=[pon_], pe_acc=(n_ > 0 or h > 0))
                                S.op("pe", lambda e: e.matmul(ps_[:, hs], w["khm"][:, h, j, :], w["vv"][:, hs], start=True, stop=True),
                                     reads=["h_khm" + x, "h_v" + x], writes=[psn2], pe_acc=True)
                        for d in range(2):
                            w = W_[d, k]
                            x = w["sfx"]
                            ps_, psn2, j = pss[d]
                            for h in range(4):
                                hs = slice(h * 128, (h + 1) * 128)
                                S.op("dve", lambda e: e.scalar_tensor_tensor(Sst[:, d, h, :], Sst[:, d, h, :], w["eb"][:, h, 128 + j:129 + j], ps_[:, hs], ALU.mult, ALU.add),
                                     reads=[psn2, "h_eb" + x, "Sst%d%d" % (d, h)], writes=["Sst%d%d" % (d, h)])
                    for d in range(2):
                        po_, pon_ = pos[d]
                        pi_, pin_ = pis[d]
                        tl = tis[d] * 128
                        S.op("act", lambda e: e.copy(tmpo[d][:], po_[:]), reads=[pon_], writes=["h_tmpo%d" % d])
                        S.op("dve", lambda e: e.tensor_tensor(oall[:, d, :, tl:tl + 128], tmpo[d][:].rearrange("p (h t) -> p h t", h=4),
                                                              pi_[:].rearrange("p (h t) -> p h t", h=4), ALU.add),
                             reads=[pin_, "h_tmpo%d" % d], writes=["oall"])
                for c in range(L // 256):
                    cs = slice(c * 256, (c + 1) * 256)
                    os_, gT, sq, rs, ob = fin["os"], fin["gT"], fin["sq"], fin["rs"], fin["ob"]
                    S.dma("act", gT[:], UT[OFF["hg"]:OFF["hg"] + 512, T0 + c * 256:T0 + (c + 1) * 256].rearrange("(h p) t -> p h t", p=128), writes=["f_gT"])
                    S.op("act", lambda e: e.activation(gT[:], gT[:], AF.Silu), reads=["f_gT"], writes=["f_gT"])
                    S.op("dve", lambda e: e.tensor_tensor(os_[:], oall[:, 0, :, cs], oall[:, 1, :, cs], ALU.add), reads=["oall"], writes=["f_os"])
                    S.op("act", lambda e: e.activation(sq[:], os_[:], AF.Square), reads=["f_os"], writes=["f_sq"])
                    for hh in range(2):
                        pn_, pnn = b.ps()
                        for h2 in range(2):
                            h = hh * 2 + h2
                            S.op("pe", lambda e: e.matmul(pn_[:, h2 * 256:(h2 + 1) * 256], ones_bf[:], sq[:, h, :], start=True, stop=True),
                                 reads=["f_sq", "ones_bf"], writes=[pnn])
                        rstd_from_sumsq(pn_, pnn, rs[:, hh * 2:hh * 2 + 2, :].rearrange("p a t -> p (a t)"), "f_rs", 128, 512, 1.0 / 128)
                    for h in range(4):
                        S.op("dve", lambda e: e.scalar_tensor_tensor(os_[:, h, :], os_[:, h, :], hn[:, h:h + 1], rs[:, h, :], ALU.mult, ALU.mult),
                             reads=["f_os", "hn", "f_rs"], writes=["f_os"])
                    S.op("dve", lambda e: e.tensor_tensor(ob[:], os_[:], gT[:], ALU.mult), reads=["f_os", "f_gT"], writes=["f_ob"])
                    S.dma("sp", OT[0, :, T0 + c * 256:T0 + (c + 1) * 256].rearrange("(h p) t -> p h t", p=128), ob[:], reads=["f_ob"], writes=[])
                if g == "p":
                    S.dma("sp", D["o_hgrn"][l, sq_].rearrange("d h k v -> k d h v"), Sst[:], reads=SSTN, writes=[])
            b.nrot = 6
            S.barrier()

    def epilogue(st, z, zn, xt, xn, gco, Xdst, c0, wkn):
        sq, rs, tmp = wkn
        for kc in range(8):
            S.op("act", lambda e: e.activation(sq[:, kc, :], z[:, kc, :], AF.Square), reads=[zn], writes=["e_sq"])
        pt, pn = b.ps()
        for kc in range(8):
            S.op("pe", lambda e: e.matmul(pt[:], ones_bf[:], sq[:, kc, :], start=(kc == 0), stop=(kc == 7)), reads=["e_sq", "ones_bf"], writes=[pn], pe_acc=True)
        rstd_from_sumsq(pt, pn, rs, "e_rs", 128, 512, 1.0 / DM)
        for kc in range(8):
            S.op("dve", lambda e: e.tensor_tensor(tmp[:], z[:, kc, :], rs[:], ALU.mult), reads=[zn, "e_rs"], writes=["e_tmp"])
            S.op("dve", lambda e: e.scalar_tensor_tensor(xt[:, kc, :], tmp[:], gco[:, kc:kc + 1], xt[:, kc, :], ALU.mult, ALU.add),
                 reads=["e_tmp", "coef", xn], writes=[xn])
        S.dma("sp", Xdst[:, c0:c0 + 512].rearrange("(kc p) t -> p kc t", p=128), xt[:], reads=[xn], writes=[])

    def stage_merge(g, l, Xsrc, Xdst):
        G = GR[g]
        T = G["T"]
        UT, OT = D["UT_" + g], D["OT_" + g]
        with ExitStack() as st:
            Wb = b.sb(st, "Wb", [128, 4, 4, 1024], BF16)
            Wo = b.sb(st, "Wo", [128, 8, 1024], BF16)
            S.dma("pool", Wb[:], D["w_branch"][l].rearrange("n (kc p) c -> p n kc c", p=128), writes=["Wb"])
            S.dma("pool", Wo[:], D["w_out"][l].rearrange("(kc p) c -> p kc c", p=128), writes=["Wo"])
            ot = b.sb(st, "ot", [128, 4, 4, 512], BF16)
            yp = b.sb(st, "yp", [128, 8, 512], BF16)
            gts = [b.sb(st, "gt%d" % i, [128, 512]) for i in range(3)]
            accf = b.sb(st, "accf", [128, 512]); tm2 = b.sb(st, "tm2", [128, 512])
            z = b.sb(st, "z", [128, 8, 512]); xt = b.sb(st, "xt", [128, 8, 512])
            wkn = (b.sb(st, "e_sq", [128, 8, 512], BF16), b.sb(st, "e_rs", [128, 512]), b.sb(st, "e_tmp", [128, 512]))
            ng = 0
            for tt in range(T // 512):
                cs = slice(tt * 512, (tt + 1) * 512)
                S.dma("sp", ot[:], OT[:, :, cs].rearrange("n (kc p) t -> p n kc t", p=128), writes=["ot"])
                S.dma("act", xt[:], Xsrc[:, cs].rearrange("(kc p) t -> p kc t", p=128), writes=["xt"])
                for dmc in range(8):
                    for n in range(4):
                        gt, gtn = gts[ng % 3], "gt%d" % (ng % 3)
                        ng += 1
                        r0 = OFF["gates"] + n * 1024 + dmc * 128
                        S.dma("sp", gt[:], UT[r0:r0 + 128, cs], writes=[gtn])
                        S.op("act", lambda e: e.activation(gt[:], gt[:], AF.Sigmoid), reads=[gtn], writes=[gtn])
                        pt, pn = b.ps()
                        for kc in range(4):
                            S.op("pe", lambda e: e.matmul(pt[:], Wb[:, n, kc, dmc * 128:(dmc + 1) * 128], ot[:, n, kc, :], start=(kc == 0), stop=(kc == 3)),
                                 reads=["Wb", "ot"], writes=[pn], pe_acc=True)
                        if n == 0:
                            S.op("dve", lambda e: e.tensor_tensor(accf[:], pt[:], gt[:], ALU.mult), reads=[pn, gtn], writes=["accf"])
                        elif n < 3:
                            S.op("dve", lambda e: e.tensor_tensor(tm2[:], pt[:], gt[:], ALU.mult), reads=[pn, gtn], writes=["tm2"])
                            S.op("dve", lambda e: e.tensor_tensor(accf[:], accf[:], tm2[:], ALU.add), reads=["tm2", "accf"], writes=["accf"])
                        else:
                            S.op("dve", lambda e: e.tensor_tensor(tm2[:], pt[:], gt[:], ALU.mult), reads=[pn, gtn], writes=["tm2"])
                            S.op("dve", lambda e: e.tensor_tensor(yp[:, dmc, :], accf[:], tm2[:], ALU.add), reads=["tm2", "accf"], writes=["yp"])
                for oc in range(8):
                    pt, pn = b.ps()
                    for kc in range(8):
                        S.op("pe", lambda e: e.matmul(pt[:], Wo[:, kc, oc * 128:(oc + 1) * 128], yp[:, kc, :], start=(kc == 0), stop=(kc == 7)),
                             reads=["Wo", "yp"], writes=[pn], pe_acc=True)
                    S.op("act", lambda e: e.copy(z[:, oc, :], pt[:]), reads=[pn], writes=["z"])
                epilogue(st, z, "z", xt, "xt", coef[:, l, G["cond"], 2], Xdst, tt * 512, wkn)
            S.barrier()

    def stage_mlp(g, l, Xsrc, Xdst):
        G = GR[g]
        T = G["T"]
        with ExitStack() as st:
            h2 = stage_h(st, g, l, Xsrc, 3, 4)
            W2 = [b.sb(st, "W2_%d" % i, [128, 8, 1024], BF16) for i in range(2)]
            n2 = 0
            w1 = [b.sb(st, "w1_%d" % i, [128, 8, 512], BF16) for i in range(2)]
            hid = b.sb(st, "hid", [128, 32, 512], BF16)
            rl = [b.sb(st, "rl%d" % i, [128, 512]) for i in range(2)]
            z = b.sb(st, "z", [128, 8, 512]); xt = b.sb(st, "xt", [128, 8, 512])
            wkn = (b.sb(st, "e_sq", [128, 8, 512], BF16), b.sb(st, "e_rs", [128, 512]), b.sb(st, "e_tmp", [128, 512]))
            nw = 0
            nr = 0
            for tt in range(T // 512):
                cs = slice(tt * 512, (tt + 1) * 512)
                S.dma("act", xt[:], Xsrc[:, cs].rearrange("(kc p) t -> p kc t", p=128), writes=["xt"])
                for cg in range(8):
                    w, wn = w1[nw % 2], "w1_%d" % (nw % 2)
                    nw += 1
                    S.dma("pool", w[:], D["w_mlp_in"][l, :, cg * 512:(cg + 1) * 512].rearrange("(kc p) c -> p kc c", p=128), writes=[wn])
                    for cc in range(4):
                        pt, pn = b.ps()
                        for kc in range(8):
                            S.op("pe", lambda e: e.matmul(pt[:], w[:, kc, cc * 128:(cc + 1) * 128], h2[:, kc, cs], start=(kc == 0), stop=(kc == 7)),
                                 reads=[wn, "hT"], writes=[pn], pe_acc=True)
                        r, rn = rl[nr % 2], "rl%d" % (nr % 2)
                        nr += 1
                        S.op("act", lambda e: e.activation(r[:], pt[:], AF.Relu), reads=[pn], writes=[rn])
                        S.op("dve", lambda e: e.tensor_tensor(hid[:, cg * 4 + cc, :], r[:], r[:], ALU.mult), reads=[rn], writes=["hid"])
                for q4 in range(4):
                    w2, w2n = W2[n2 % 2], "W2_%d" % (n2 % 2)
                    n2 += 1
                    S.dma("pool", w2[:], D["w_mlp_out"][l, q4 * 1024:(q4 + 1) * 1024, :].rearrange("(kc p) c -> p kc c", p=128), writes=[w2n])
                    for oc in range(8):
                        pt, pn = b.ps()
                        for fc in range(8):
                            S.op("pe", lambda e: e.matmul(pt[:], w2[:, fc, oc * 128:(oc + 1) * 128], hid[:, q4 * 8 + fc, :], start=(fc == 0), stop=(fc == 7)),
                                 reads=[w2n, "hid"], writes=[pn], pe_acc=True)
                        if q4 == 0:
                            S.op("act", lambda e: e.copy(z[:, oc, :], pt[:]), reads=[pn], writes=["z%d" % oc, "z"])
                        else:
                            S.op("dve", lambda e: e.tensor_tensor(z[:, oc, :], z[:, oc, :], pt[:], ALU.add), reads=[pn, "z%d" % oc], writes=["z%d" % oc, "z"])
                epilogue(st, z, "z", xt, "xt", coef[:, l, G["cond"], 5], Xdst, tt * 512, wkn)
            S.barrier()

    for g in ("s", "p"):
        X = D["xT_" + g]
        for l in range(DEPTH):
            if "proj" in STAGES:
                with ExitStack() as st:
                    hT = stage_h(st, g, l, X, 0, 1)
                    stage_proj(g, l, hT)
            if "hgrn" in STAGES:
                stage_hgrn(g, l)
            if "swa" in STAGES:
                stage_gqa_like(g, l, "swa")
            if "mla" in STAGES:
                stage_mla(g, l)
            if "gqa" in STAGES:
                stage_gqa_like(g, l, "gqa")
            if "merge" in STAGES:
                stage_merge(g, l, X, D["X1_" + g])
            Xn = D["yT_" + g] if l == DEPTH - 1 else D["X2_%d_%s" % (l, g)]
            if "mlp" in STAGES:
                stage_mlp(g, l, D["X1_" + g], Xn)
            X = Xn
    S.barrier()
    return nc, b


_CACHE = {}


def _consts():
    nf = 16
    t = np.arange(2048)
    row, col = (t // 64).astype(np.float32), (t % 64).astype(np.float32)
    rope = np.zeros((4, 96, 2048), np.float32)
    def fill(ci, si, r0, nf):
        inv = (10000.0 ** (-np.arange(nf, dtype=np.float32) / nf)).astype(np.float32)
        ar = (row[None, :] * inv[:, None]).astype(np.float32)
        ac = (col[None, :] * inv[:, None]).astype(np.float32)
        for k, a in enumerate((ar, ac)):
            b0 = r0 + k * 2 * nf
            rope[ci, b0:b0 + nf] = np.cos(a); rope[ci, b0 + nf:b0 + 2 * nf] = np.cos(a)
            rope[si, b0:b0 + nf] = -np.sin(a); rope[si, b0 + nf:b0 + 2 * nf] = np.sin(a)
    fill(0, 1, 0, 16)
    fill(2, 3, 64, 8)
    s_ = np.arange(128)[:, None]; t_ = np.arange(128)[None, :]
    same = (s_ // 32) == (t_ // 32)
    cm = np.zeros((6, 128, 128), np.float32)
    cm[0] = np.eye(128)
    cm[1] = same & (s_ <= t_)
    cm[2] = same & (s_ >= t_)
    cm[3] = same & (s_ > t_)
    cm[4] = same & (s_ < t_)
    cm[5][:, 0:4] = (s_ // 32) == np.arange(4)[None, :]
    j = np.arange(128)[:, None]; i = (np.arange(512) % 128)[None, :]
    swm = np.stack([(j >= i), (j <= i)]).astype(np.float32).astype(ml_dtypes.bfloat16)
    return rope, cm, swm


def _perm(nd):
    q = nd // 4
    return np.concatenate([np.arange(q, 2 * q), np.arange(0, q), np.arange(3 * q, 4 * q), np.arange(2 * q, 3 * q)])


def kernel(**inp):
    f = lambda a: np.ascontiguousarray(np.asarray(a, dtype=np.float32))
    I = {k: f(v) for k, v in inp.items()}
    if "prog" not in _CACHE:
        _CACHE["prog"] = build_program()
    nc, b = _CACHE["prog"]
    rope, cm, swm = _consts()
    fm = lambda v, n: f(v.reshape(v.shape[0], n, 128).transpose(0, 2, 1))
    shared = dict(
        w_ada=I["w_ada"], b_adaT=fm(I["b_ada"], 48),
        gains=f(np.stack([fm(I[k], 8) for k in ("norm_mix_pre", "norm_mix_post", "norm_mlp_pre", "norm_mlp_post")], axis=2)),
        w_in=I["w_in"], lbT=f(np.stack([I["hgrn_lb_fwd"], I["hgrn_lb_bwd"]], axis=1)),
        hgrn_normT=fm(I["hgrn_norm"], 4), sink=f(I["swa_sink"][:, None, :]),
        mla_qnT=fm(I["mla_q_norm"], 2), mla_kvn=f(I["mla_kv_norm"][:, :, None]),
        w_uq=I["mla_w_uq"], w_ukv=I["mla_w_ukv"],
        gqa_qn=f(np.stack([I["gqa_q_norm"], I["gqa_q_norm"][:, _perm(64)]], axis=2)),
        gqa_kn=f(np.stack([I["gqa_k_norm"], I["gqa_k_norm"][:, _perm(64)]], axis=2)),
        w_branch=I["w_branch"], w_out=I["w_out"], w_mlp_in=I["w_mlp_in"], w_mlp_out=I["w_mlp_out"],
        rope64=rope, cmat=cm, swamask=swm,
    )
    wsw = I["mla_w_uq"].reshape(DEPTH, 256, 8, 96).copy()
    wsw[..., 64:96] = wsw[..., 64:96][..., _perm(32)]
    shared["w_uq_sw"] = f(wsw.reshape(DEPTH, 256, 768))
    in_maps = []
    for i in range(NCORE):
        bb = i % 2
        m = dict(shared)
        m["xT_s"] = f(I["x_sample"][bb].T)
        m["xT_p"] = f(I["x_prompt"][4 * i:4 * i + 4].reshape(1024, DM).T)
        cc = np.stack([I["c_ctx"], I["c"][bb]], axis=1)
        m["cT"] = f(cc.reshape(8, 128, 2).transpose(1, 0, 2))
        m["st_hgrn"] = f(I["state_hgrn"][bb])
        m["c_swa_kT"] = f(I["cache_swa_k"][bb].transpose(0, 2, 3, 1))
        m["c_swa_v"] = f(I["cache_swa_v"][bb].reshape(DEPTH, 512, 128))
        m["c_ckvT"] = f(I["cache_mla_ckv"][bb].transpose(0, 2, 1))
        m["c_krT"] = f(I["cache_mla_kr"][bb].transpose(0, 2, 1))
        m["c_gqa_kT"] = f(I["cache_gqa_k"][bb].transpose(0, 2, 3, 1))
        m["c_gqa_v"] = f(I["cache_gqa_v"][bb].reshape(DEPTH, 512, 128))
        in_maps.append(m)
    res = run_bass_kernel_spmd(nc, in_maps, core_ids=list(range(NCORE)))
    R = res.results
    y_prompt = np.concatenate([R[i]["yT_p"].T.reshape(4, 256, DM) for i in range(NCORE)], axis=0)
    y_sample = np.stack([R[0]["yT_s"].T, R[1]["yT_s"].T], axis=0)
    cat = lambda fn: np.ascontiguousarray(np.concatenate([fn(R[i]) for i in range(NCORE)], axis=0).astype(np.float32))
    n_hgrn = cat(lambda r: r["o_hgrn"].transpose(1, 0, 2, 3, 4, 5))
    kT = lambda a: a.reshape(DEPTH, 2, 64, 4, 256).transpose(3, 0, 4, 1, 2)
    vv = lambda a: a.reshape(DEPTH, 4, 256, 2, 64).transpose(1, 0, 2, 3, 4)
    n_swa_k = cat(lambda r: kT(r["o_swa_kT"]))
    n_swa_v = cat(lambda r: vv(r["o_swa_v"]))
    n_ckv = cat(lambda r: r["o_ckvT"].reshape(DEPTH, 128, 4, 256).transpose(2, 0, 3, 1))
    n_kr = cat(lambda r: r["o_krT"].reshape(DEPTH, 32, 4, 256).transpose(2, 0, 3, 1))
    n_gqa_k = cat(lambda r: kT(r["o_gqa_kT"]))
    n_gqa_v = cat(lambda r: vv(r["o_gqa_v"]))
    return (np.ascontiguousarray(y_prompt.astype(np.float32)), np.ascontiguousarray(y_sample.astype(np.float32)),
            n_hgrn, n_swa_k, n_swa_v, n_ckv, n_kr, n_gqa_k, n_gqa_v)
```

```python
import numpy as np
from contextlib import ExitStack
import ml_dtypes
import concourse.bass as bass
import concourse.mybir as mybir
from concourse.bass_utils import run_bass_kernel_spmd

F32 = mybir.dt.float32
BF16 = mybir.dt.bfloat16
AF = mybir.ActivationFunctionType
ALU = mybir.AluOpType

DM = 1024
DEPTH = 2
NCORE = 8
EPS = 1e-6
OFF = dict(hq=0, ff=512, fb=1024, hi=1536, hg=2048, sq=2560, sk=3072, sv=3200, cq=3328, ckv=3584, kr=3712,
           gq=3744, gk=4256, gv=4384, gates=4512)
D_IN = 8608
UTM_COLS = 1792
STAGES = {"proj", "hgrn", "swa", "mla", "gqa", "merge", "mlp"}


class Sched:
    NDMA = 24

    def __init__(self, nc, es):
        self.nc = nc
        self.eng = {"pe": nc.tensor, "act": nc.scalar, "dve": nc.vector, "pool": nc.gpsimd, "sp": nc.sync}
        self.sem = {k: es.enter_context(nc.semaphore("s_" + k)) for k in ("pe", "act", "dve", "pool")}
        self.cnt = {k: 0 for k in self.sem}
        self.dsem = [es.enter_context(nc.semaphore("d%d" % i)) for i in range(self.NDMA)]
        self.dcnt = [0] * self.NDMA
        self.dnext = 0
        self.seen = {k: {} for k in self.eng}
        self.lastw = {}
        self.reads = {}
        self.n_instr = 0

    def _sem_of(self, key):
        return self.sem[key] if isinstance(key, str) else self.dsem[key[1]]

    def _wait(self, e, tok):
        key, val = tok
        if self.seen[e].get(key, 0) >= val:
            return
        self.eng[e].wait_ge(self._sem_of(key), val)
        self.seen[e][key] = val

    def _deps(self, e, reads, writes, pe_acc=False):
        best = {}
        for r in reads:
            t = self.lastw.get(r)
            if t is not None and best.get(t[0], 0) < t[1]:
                best[t[0]] = t[1]
        for w in writes:
            t = self.lastw.get(w)
            if t is not None and best.get(t[0], 0) < t[1]:
                if not (pe_acc and t[0] == "pe"):
                    best[t[0]] = t[1]
            for t in self.reads.get(w, ()):
                if best.get(t[0], 0) < t[1]:
                    best[t[0]] = t[1]
        for key, val in best.items():
            self._wait(e, (key, val))

    def _record(self, tok, reads, writes):
        for r in reads:
            lst = self.reads.setdefault(r, [])
            lst[:] = [t for t in lst if t[0] != tok[0]]
            lst.append(tok)
        for w in writes:
            self.lastw[w] = tok
            self.reads[w] = []

    def op(self, e, fn, reads=(), writes=(), pe_acc=False):
        self._deps(e, reads, writes, pe_acc)
        ins = fn(self.eng[e])
        self.cnt[e] += 1
        ins.then_inc(self.sem[e], 1)
        self._record((e, self.cnt[e]), reads, writes)
        self.n_instr += 1
        return ins

    def dma(self, q, out, in_, reads=(), writes=()):
        i = self.dnext
        self.dnext = (self.dnext + 1) % self.NDMA
        if self.dcnt[i] > 0:
            self._wait(q, (("d", i), self.dcnt[i]))
        self._deps(q, reads, writes)
        ins = self.eng[q].dma_start(out=out, in_=in_)
        self.dcnt[i] += 16
        ins.then_inc(self.dsem[i], 16)
        self._record((("d", i), self.dcnt[i]), reads, writes)
        self.n_instr += 1
        return ins

    def barrier(self):
        best = {}
        for k in self.cnt:
            if self.cnt[k]:
                best[k] = self.cnt[k]
        for i in range(self.NDMA):
            if self.dcnt[i]:
                best[("d", i)] = self.dcnt[i]
        for e in self.eng:
            for key, val in best.items():
                self._wait(e, (key, val))
        self.lastw = {}
        self.reads = {}


class B:
    def __init__(self, nc, es):
        self.nc, self.es = nc, es
        self.S = Sched(nc, es)
        self.D = {}
        self.psn = 0

    def din(self, name, shape, dt=F32):
        self.D[name] = self.nc.dram_tensor(name, list(shape), dt, kind="ExternalInput").ap()
        return self.D[name]

    def dout(self, name, shape, dt=F32):
        self.D[name] = self.nc.dram_tensor(name, list(shape), dt, kind="ExternalOutput").ap()
        return self.D[name]

    def dscr(self, name, shape, dt=F32):
        self.D[name] = self.nc.dram_tensor(name, list(shape), dt, kind="Internal").ap()
        return self.D[name]

    def sb(self, st, name, shape, dt=F32):
        self.uid = getattr(self, "uid", 0) + 1
        return st.enter_context(self.nc.sbuf_tensor("sb%d_%s" % (self.uid, name), list(shape), dt))

    def ps(self):
        i = self.psn % self.nrot
        self.psn += 1
        return self.psum[i], "ps%d" % i

    nrot = 6

    def acc(self, i):
        k = (6, 7, 4, 5)[i]
        return self.psum[k], "ps%d" % k


def build_program():
    nc = bass.Bass("TRN2", target_bir_lowering=False)
    es = ExitStack()
    b = B(nc, es)
    S = b.S
    D = b.D
    GR = {
        "s": dict(T=2048, nseq=1, L=2048, P=512, rope=True, cond=1),
        "p": dict(T=1024, nseq=4, L=256, P=0, rope=False, cond=0),
    }
    for g, G in GR.items():
        T = G["T"]
        b.din("xT_" + g, [DM, T])
        b.dout("yT_" + g, [DM, T])
        b.dscr("X1_" + g, [DM, T])
        for l in range(DEPTH - 1):
            b.dscr("X2_%d_%s" % (l, g), [DM, T])
        b.dscr("UT_" + g, [68 * 128, T])
        b.dscr("UTM_" + g, [T, UTM_COLS])
        b.dscr("OT_" + g, [4, 512, T], BF16)
        b.dscr("YP_" + g, [DM, T], BF16)
    b.din("cT", [128, 8, 2])
    b.din("w_ada", [DEPTH, DM, 6 * DM])
    b.din("b_adaT", [DEPTH, 128, 48])
    b.din("gains", [DEPTH, 128, 4, 8])
    b.din("w_in", [DEPTH, DM, D_IN])
    b.din("lbT", [DEPTH, 2, 512])
    b.din("hgrn_normT", [DEPTH, 128, 4])
    b.din("sink", [DEPTH, 1, 8])
    b.din("mla_qnT", [DEPTH, 128, 2])
    b.din("mla_kvn", [DEPTH, 128, 1])
    b.din("w_uq", [DEPTH, 256, 768])
    b.din("w_uq_sw", [DEPTH, 256, 768])
    b.din("w_ukv", [DEPTH, 128, 1024])
    b.din("gqa_qn", [DEPTH, 128, 2])
    b.din("gqa_kn", [DEPTH, 128, 2])
    b.din("w_branch", [DEPTH, 4, 512, DM])
    b.din("w_out", [DEPTH, DM, DM])
    b.din("w_mlp_in", [DEPTH, DM, 4 * DM])
    b.din("w_mlp_out", [DEPTH, 4 * DM, DM])
    b.din("st_hgrn", [DEPTH, 2, 4, 128, 128])
    b.din("c_swa_kT", [DEPTH, 2, 64, 512])
    b.din("c_swa_v", [DEPTH, 512, 128])
    b.din("c_ckvT", [DEPTH, 128, 512])
    b.din("c_krT", [DEPTH, 32, 512])
    b.din("c_gqa_kT", [DEPTH, 2, 64, 512])
    b.din("c_gqa_v", [DEPTH, 512, 128])
    b.din("rope64", [4, 128, 2048])
    b.din("cmat", [6, 128, 128])
    b.din("swamask", [2, 128, 512], BF16)
    b.dout("o_hgrn", [DEPTH, 4, 2, 4, 128, 128])
    b.dout("o_swa_kT", [DEPTH, 128, 1024])
    b.dout("o_swa_v", [DEPTH, 1024, 128])
    b.dout("o_ckvT", [DEPTH, 128, 1024])
    b.dout("o_krT", [DEPTH, 32, 1024])
    b.dout("o_gqa_kT", [DEPTH, 128, 1024])
    b.dout("o_gqa_v", [DEPTH, 1024, 128])

    b.psum = [es.enter_context(nc.psum_tensor("psb%d" % i, [128, 512], F32)) for i in range(8)]

    cst = ExitStack()
    es.enter_context(cst)
    ones_bf = b.sb(cst, "ones_bf", [128, 128], BF16)
    ones_f = b.sb(cst, "ones_f", [128, 128], F32)
    cm = b.sb(cst, "cm", [128, 6, 128], F32)
    modT = b.sb(cst, "modT", [128, DEPTH, 48, 2], F32)
    gains = b.sb(cst, "gains", [128, DEPTH, 4, 8], F32)
    coef = b.sb(cst, "coef", [128, DEPTH, 2, 6, 8], F32)
    S.op("pool", lambda e: e.memset(ones_bf[:], 1.0), writes=["ones_bf"])
    onesAB = b.sb(cst, "onesAB", [128, 2, 128], BF16)
    S.op("pool", lambda e: e.memset(onesAB[:], 0.0), writes=["onesAB"])
    S.op("pool", lambda e: e.memset(onesAB[:, 0, 0:64], 1.0), writes=["onesAB"])
    S.op("pool", lambda e: e.memset(onesAB[:, 1, 64:128], 1.0), writes=["onesAB"])
    S.op("pool", lambda e: e.memset(ones_f[:], 1.0), writes=["ones_f"])
    S.dma("sp", cm[:], D["cmat"].rearrange("a p c -> p a c"), writes=["cm"])
    S.dma("sp", gains[:], D["gains"].rearrange("l p a c -> p l a c"), writes=["gains"])

    with ExitStack() as st:
        cT = b.sb(st, "cT", [128, 8, 2])
        scT = b.sb(st, "scT", [128, 8, 2])
        badaT = b.sb(st, "badaT", [128, DEPTH, 48])
        wa = [b.sb(st, "wa%d" % i, [128, 8, 768]) for i in range(2)]
        S.dma("sp", cT[:], D["cT"], writes=["cT"])
        S.dma("sp", badaT[:], D["b_adaT"].rearrange("l p j -> p l j"), writes=["badaT"])
        S.op("act", lambda e: e.activation(scT[:], cT[:], AF.Silu), reads=["cT"], writes=["scT"])
        n = 0
        for l in range(DEPTH):
            pt, pn = b.ps()
            for cg in range(8):
                w = wa[n % 2]
                wn = "wa%d" % (n % 2)
                n += 1
                S.dma("sp" if cg % 2 == 0 else "act", w[:],
                      D["w_ada"][l, :, cg * 768:(cg + 1) * 768].rearrange("(kc p) c -> p kc c", p=128), writes=[wn])
                for jj in range(6):
                    j = cg * 6 + jj
                    for kc in range(8):
                        S.op("pe", lambda e: e.matmul(pt[:, 2 * j:2 * j + 2], w[:, kc, jj * 128:(jj + 1) * 128], scT[:, kc, :],
                                                      start=(kc == 0), stop=(kc == 7)),
                             reads=[wn, "scT"], writes=[pn], pe_acc=True)
            S.op("dve", lambda e: e.tensor_tensor(modT[:, l], pt[:, 0:96].rearrange("p (j c) -> p j c", c=2),
                                                  badaT[:, l].unsqueeze(2).to_broadcast([128, 48, 2]), ALU.add),
                 reads=[pn, "badaT"], writes=["modT"])
        for l in range(DEPTH):
            for c in range(2):
                m = lambda i: modT[:, l, i * 8:(i + 1) * 8, c]
                S.op("dve", lambda e: e.scalar_tensor_tensor(coef[:, l, c, 0], m(1), 1.0, gains[:, l, 0], ALU.add, ALU.mult),
                     reads=["modT", "gains"], writes=["coef"])
                S.op("dve", lambda e: e.tensor_copy(coef[:, l, c, 1], m(0)), reads=["modT"], writes=["coef"])
                S.op("dve", lambda e: e.tensor_tensor(coef[:, l, c, 2], m(2), gains[:, l, 1], ALU.mult), reads=["modT", "gains"], writes=["coef"])
                S.op("dve", lambda e: e.scalar_tensor_tensor(coef[:, l, c, 3], m(4), 1.0, gains[:, l, 2], ALU.add, ALU.mult),
                     reads=["modT", "gains"], writes=["coef"])
                S.op("dve", lambda e: e.tensor_copy(coef[:, l, c, 4], m(3)), reads=["modT"], writes=["coef"])
                S.op("dve", lambda e: e.tensor_tensor(coef[:, l, c, 5], m(5), gains[:, l, 3], ALU.mult), reads=["modT", "gains"], writes=["coef"])
        S.barrier()

    def rstd_from_sumsq(pt, pn, out, outn, npart, ncol, inv_n):
        S.op("act", lambda e: e.activation(out[0:npart, 0:ncol], pt[0:npart, 0:ncol], AF.Sqrt, bias=epsb[0:npart, :], scale=inv_n),
             reads=[pn, "epsb"], writes=[outn])
        S.op("dve", lambda e: e.reciprocal(out[0:npart, 0:ncol], out[0:npart, 0:ncol]), reads=[outn], writes=[outn])

    epsb = b.sb(cst, "epsb", [128, 1], F32)
    S.op("pool", lambda e: e.memset(epsb[:], EPS), writes=["epsb"])

    def norm_mod_tile(st_unused, xt, xn, hT, hn, t0, a_ap, sh_ap, tmp, sq, rs):
        for kc in range(8):
            S.op("act", lambda e: e.activation(sq[:, kc, :], xt[:, kc, :], AF.Square), reads=[xn], writes=["sq"])
        pt, pn = b.ps()
        for kc in range(8):
            S.op("pe", lambda e: e.matmul(pt[:], ones_bf[:], sq[:, kc, :], start=(kc == 0), stop=(kc == 7)),
                 reads=["sq", "ones_bf"], writes=[pn], pe_acc=True)
        rstd_from_sumsq(pt, pn, rs, "rs", 128, 512, 1.0 / DM)
        for kc in range(8):
            S.op("dve", lambda e: e.tensor_tensor(tmp[:], xt[:, kc, :], rs[:], ALU.mult), reads=[xn, "rs"], writes=["tmp"])
            S.op("act", lambda e: e.activation(hT[:, kc, t0:t0 + 512], tmp[:], AF.Identity, bias=sh_ap[:, kc:kc + 1], scale=a_ap[:, kc:kc + 1]),
                 reads=["tmp", "coef"], writes=[hn])

    def stage_h(st, g, l, Xsrc, ia, ish):
        G = GR[g]
        T = G["T"]
        hT = b.sb(st, "hT", [128, 8, T], BF16)
        with ExitStack() as s2:
            xts = [b.sb(s2, "xt%d" % i, [128, 8, 512]) for i in range(2)]
            tmp = b.sb(s2, "tmp", [128, 512])
            sq = b.sb(s2, "sq", [128, 8, 512], BF16)
            rs = b.sb(s2, "rs", [128, 512])
            for tt in range(T // 512):
                xt, xn = xts[tt % 2], "xt%d" % (tt % 2)
                S.dma("sp", xt[:], Xsrc[:, tt * 512:(tt + 1) * 512].rearrange("(kc p) t -> p kc t", p=128), writes=[xn])
                norm_mod_tile(None, xt, xn, hT, "hT", tt * 512, coef[:, l, G["cond"], ia], coef[:, l, G["cond"], ish], tmp, sq, rs)
            S.barrier()
        return hT

    def stage_proj(g, l, hT):
        G = GR[g]
        T = G["T"]
        UT, UTM = D["UT_" + g], D["UTM_" + g]
        tm_groups = {1: [(0, 512, 0)], 2: [(0, 512, 512)], 3: [(0, 512, 1024)], 6: [(128, 128, 1536)], 8: [(288, 128, 1664)]}
        with ExitStack() as st:
            wb = [b.sb(st, "wb%d" % i, [128, 8, 512], BF16) for i in range(2)]
            ev = [b.sb(st, "ev%d" % i, [128, 512]) for i in range(4)]
            nev = 0
            for cg in range(17):
                ncol = 512 if cg < 16 else D_IN - 8192
                w, wn = wb[cg % 2], "wb%d" % (cg % 2)
                S.dma("pool", w[:, :, 0:ncol], D["w_in"][l, :, cg * 512:cg * 512 + ncol].rearrange("(kc p) c -> p kc c", p=128), writes=[wn])
                for tt in range(T // 512):
                    for cc in range((ncol + 127) // 128):
                        m = min(128, ncol - cc * 128)
                        pt, pn = b.ps()
                        for kc in range(8):
                            S.op("pe", lambda e: e.matmul(pt[0:m, :], w[:, kc, cc * 128:cc * 128 + m], hT[:, kc, tt * 512:(tt + 1) * 512],
                                                          start=(kc == 0), stop=(kc == 7)), reads=[wn, "hT"], writes=[pn], pe_acc=True)
                        e_, en = ev[nev % 4], "ev%d" % (nev % 4)
                        eng = "act" if nev % 2 == 0 else "dve"
                        nev += 1
                        if eng == "act":
                            S.op("act", lambda e: e.copy(e_[0:m, :], pt[0:m, :]), reads=[pn], writes=[en])
                        else:
                            S.op("dve", lambda e: e.tensor_copy(e_[0:m, :], pt[0:m, :]), reads=[pn], writes=[en])
                        r0 = cg * 512 + cc * 128
                        S.dma("sp", UT[r0:r0 + m, tt * 512:(tt + 1) * 512], e_[0:m, :], reads=[en], writes=[])
                for (c0, cn, dst) in tm_groups.get(cg, []):
                    for t4 in range(T // 128):
                        pt, pn = b.ps()
                        for kc in range(8):
                            S.op("pe", lambda e: e.matmul(pt[:, 0:cn], hT[:, kc, t4 * 128:(t4 + 1) * 128], w[:, kc, c0:c0 + cn],
                                                          start=(kc == 0), stop=(kc == 7)), reads=[wn, "hT"], writes=[pn], pe_acc=True)
                        e_, en = ev[nev % 4], "ev%d" % (nev % 4)
                        eng = "act" if nev % 2 == 0 else "dve"
                        nev += 1
                        if eng == "act":
                            S.op("act", lambda e: e.copy(e_[:, 0:cn], pt[:, 0:cn]), reads=[pn], writes=[en])
                        else:
                            S.op("dve", lambda e: e.tensor_copy(e_[:, 0:cn], pt[:, 0:cn]), reads=[pn], writes=[en])
                        S.dma("sp", UTM[t4 * 128:(t4 + 1) * 128, dst:dst + cn], e_[:, 0:cn], reads=[en], writes=[])
            S.barrier()

    def attn_unit(qT, qn, Kd, Nq, chunks, scale, o_out, on, wk, sink=None):
        po, pon = b.acc(0)
        pd, pdn = b.acc(1)
        nch = len(chunks)

        def emit_st(i):
            kT, kn, V, vn, nk, mask = chunks[i]
            pst, psn_ = b.ps()
            S.op("pe", lambda e: e.matmul(pst[0:nk, 0:Nq], kT, qT, start=True, stop=True), reads=[kn] + (qn if isinstance(qn, list) else [qn]), writes=[psn_])
            return pst, psn_

        cur = emit_st(0)
        for i, (kT, kn, V, vn, nk, mask) in enumerate(chunks):
            nxt = emit_st(i + 1) if i + 1 < nch else None
            pst, psn_ = cur
            E, En = wk["E"][i % 3], "E%d" % (i % 3)
            S.op("act", lambda e: e.activation(E[0:nk, 0:Nq], pst[0:nk, 0:Nq], AF.Exp, scale=scale), reads=[psn_], writes=[En])
            if mask is not None:
                S.op("pool", lambda e: e.tensor_tensor(E[0:nk, 0:Nq], E[0:nk, 0:Nq], mask, ALU.mult), reads=[En, "swamask"], writes=[En])
            last = (i == nch - 1) and sink is None
            S.op("pe", lambda e: e.matmul(po[0:64, 0:Nq], V, E[0:nk, 0:Nq], start=(i == 0), stop=(i == nch - 1)),
                 reads=[vn, En], writes=[pon], pe_acc=(i > 0))
            S.op("pe", lambda e: e.matmul(pd[0:64, 0:Nq], ones_bf[0:nk, 0:64], E[0:nk, 0:Nq], start=(i == 0), stop=last),
                 reads=["ones_bf", En], writes=[pdn], pe_acc=(i > 0))
            cur = nxt
        if sink is not None:
            S.op("pe", lambda e: e.matmul(pd[0:64, 0:Nq], ones_bf[0:1, 0:64], sink, start=False, stop=True),
                 reads=["ones_bf", "sinkrow"], writes=[pdn], pe_acc=True)
        rec = wk["rec"]
        S.op("dve", lambda e: e.reciprocal(rec[0:64, 0:Nq], pd[0:64, 0:Nq]), reads=[pdn], writes=["rec"])
        S.op("dve", lambda e: e.tensor_tensor(o_out, po[0:64, 0:Nq], rec[0:64, 0:Nq], ALU.mult), reads=[pon, "rec"], writes=[on])

    def attn_pair(qa, qan, qb, qbn, Nq, chunks, scale, o_out, on, wk, sink=None):
        po, pon = b.acc(0)
        pd, pdn = b.acc(1)
        nch = len(chunks)
        E = wk["E"]
        ne = len(E)

        def emit_st(i):
            kTa, kTb, kn, Va, Vb, vn, nk, mask = chunks[i]
            p1, n1 = b.ps()
            S.op("pe", lambda e: e.matmul(p1[0:nk, 0:Nq], kTa, qa, start=True, stop=True), reads=[kn] + qan, writes=[n1])
            p2, n2 = b.ps()
            S.op("pe", lambda e: e.matmul(p2[0:nk, 0:Nq], kTb, qb, start=True, stop=True), reads=[kn] + qbn, writes=[n2])
            return (p1, n1, p2, n2)

        cur = emit_st(0)
        for i, (kTa, kTb, kn, Va, Vb, vn, nk, mask) in enumerate(chunks):
            nxt = emit_st(i + 1) if i + 1 < nch else None
            p1, n1, p2, n2 = cur
            Ea, Ean = E[(2 * i) % ne], "E%d" % ((2 * i) % ne)
            Eb, Ebn = E[(2 * i + 1) % ne], "E%d" % ((2 * i + 1) % ne)
            S.op("act", lambda e: e.activation(Ea[0:nk, 0:Nq], p1[0:nk, 0:Nq], AF.Exp, scale=scale), reads=[n1], writes=[Ean])
            S.op("act", lambda e: e.activation(Eb[0:nk, 0:Nq], p2[0:nk, 0:Nq], AF.Exp, scale=scale), reads=[n2], writes=[Ebn])
            if mask is not None:
                S.op("pool", lambda e: e.tensor_tensor(Ea[0:nk, 0:Nq], Ea[0:nk, 0:Nq], mask, ALU.mult), reads=[Ean, "swamask"], writes=[Ean])
                S.op("dve", lambda e: e.tensor_tensor(Eb[0:nk, 0:Nq], Eb[0:nk, 0:Nq], mask, ALU.mult), reads=[Ebn, "swamask"], writes=[Ebn])
            last = (i == nch - 1)
            S.op("pe", lambda e: e.matmul(po[:, 0:Nq], Va, Ea[0:nk, 0:Nq], start=(i == 0), stop=False), reads=[vn, Ean], writes=[pon], pe_acc=(i > 0))
            S.op("pe", lambda e: e.matmul(po[:, 0:Nq], Vb, Eb[0:nk, 0:Nq], start=False, stop=last), reads=[vn, Ebn], writes=[pon], pe_acc=True)
            S.op("pe", lambda e: e.matmul(pd[:, 0:Nq], onesAB[0:nk, 0, :], Ea[0:nk, 0:Nq], start=(i == 0), stop=False), reads=["onesAB", Ean], writes=[pdn], pe_acc=(i > 0))
            S.op("pe", lambda e: e.matmul(pd[:, 0:Nq], onesAB[0:nk, 1, :], Eb[0:nk, 0:Nq], start=False, stop=(last and sink is None)), reads=["onesAB", Ebn], writes=[pdn], pe_acc=True)
            cur = nxt
        if sink is not None:
            sa, sb_ = sink
            S.op("pe", lambda e: e.matmul(pd[:, 0:Nq], onesAB[0:1, 0, :], sa, start=False, stop=False), reads=["onesAB", "sinkrow"], writes=[pdn], pe_acc=True)
            S.op("pe", lambda e: e.matmul(pd[:, 0:Nq], onesAB[0:1, 1, :], sb_, start=False, stop=True), reads=["onesAB", "sinkrow"], writes=[pdn], pe_acc=True)
        rec = wk["rec"]
        S.op("dve", lambda e: e.reciprocal(rec[:, 0:Nq], pd[:, 0:Nq]), reads=[pdn], writes=["rec"])
        S.op("dve", lambda e: e.tensor_tensor(o_out, po[:, 0:Nq], rec[:, 0:Nq], ALU.mult), reads=[pon, "rec"], writes=[on])

    def rr_alloc(st, T, do_rope):
        sets = []
        for k in range(2):
            sets.append(dict(k=k, x=b.sb(st, "rr_x%d" % k, [128, T]), xs=(b.sb(st, "rr_xs%d" % k, [128, T]) if do_rope else None),
                             sq=b.sb(st, "rr_sq%d" % k, [128, 512], BF16), rs=b.sb(st, "rr_rs%d" % k, [128, T]), t1=b.sb(st, "rr_t1%d" % k, [128, 512])))
        return sets

    def rms_rope_rows(W_, p0, src_rows_fn, n_rows, T, gain2, gainn, do_norm, do_rope, rope_idx, out_bf, outn, out32=None, out32n=None,
                      out32_pre_rope=False):
        k = W_["k"]
        x, xs, sqb, rs, t1 = W_["x"], W_["xs"], W_["sq"], W_["rs"], W_["t1"]
        xn, xsn, sqn, rsn, t1n = ("rr_x%d" % k, "rr_xs%d" % k, "rr_sq%d" % k, "rr_rs%d" % k, "rr_t1%d" % k)
        for (r0, nr, ap) in src_rows_fn(False):
            S.dma("sp", x[p0 + r0:p0 + r0 + nr, 0:T], ap, writes=[xn])
        if do_rope:
            for (r0, nr, ap) in src_rows_fn(True):
                S.dma("act", xs[p0 + r0:p0 + r0 + nr, 0:T], ap, writes=[xsn])
        nr = n_rows
        pr = slice(p0, p0 + nr)
        W = min(512, T)
        for c in range(T // W):
            cs = slice(c * W, (c + 1) * W)
            if do_norm:
                S.op("act", lambda e: e.activation(sqb[pr, 0:W], x[pr, cs], AF.Square), reads=[xn], writes=[sqn])
                pt, pn = b.ps()
                S.op("pe", lambda e: e.matmul(pt[:, 0:W], ones_bf[pr, :], sqb[pr, 0:W], start=True, stop=True),
                     reads=[sqn, "ones_bf"], writes=[pn])
                S.op("act", lambda e: e.activation(rs[pr, cs], pt[pr, 0:W], AF.Sqrt, bias=epsb[pr, :], scale=1.0 / nr), reads=[pn, "epsb"], writes=[rsn])
                S.op("dve", lambda e: e.reciprocal(rs[pr, cs], rs[pr, cs]), reads=[rsn], writes=[rsn])
                S.op("dve", lambda e: e.scalar_tensor_tensor(x[pr, cs], x[pr, cs], gain2[pr, 0:1], rs[pr, cs], ALU.mult, ALU.mult),
                     reads=[xn, rsn, gainn], writes=[xn])
                if do_rope:
                    S.op("dve", lambda e: e.scalar_tensor_tensor(xs[pr, cs], xs[pr, cs], gain2[pr, 1:2], rs[pr, cs], ALU.mult, ALU.mult),
                         reads=[xsn, rsn, gainn], writes=[xsn])
            if out32 is not None and out32_pre_rope:
                S.op("pool", lambda e: e.tensor_copy(out32[pr, cs], x[pr, cs]), reads=[xn], writes=[out32n])
            if do_rope:
                S.op("dve", lambda e: e.tensor_tensor(x[pr, cs], x[pr, cs], rope[pr, rope_idx, cs], ALU.mult), reads=[xn, "rope"], writes=[xn])
                S.op("pool", lambda e: e.tensor_tensor(t1[pr, 0:W], xs[pr, cs], rope[pr, rope_idx + 1, cs], ALU.mult), reads=[xsn, "rope"], writes=[t1n])
                S.op("dve", lambda e: e.tensor_tensor(out_bf[:, cs], x[pr, cs], t1[pr, 0:W], ALU.add), reads=[xn, t1n], writes=[outn])
            else:
                S.op("act", lambda e: e.copy(out_bf[:, cs], x[pr, cs]), reads=[xn], writes=[outn])

    def swap_rows(base, T0, T1, UT, nd):
        q = nd // 4
        def f(swapped):
            if not swapped:
                return [(0, nd, UT[base:base + nd, T0:T1])]
            return [(0, q, UT[base + q:base + 2 * q, T0:T1]), (q, q, UT[base:base + q, T0:T1]),
                    (2 * q, q, UT[base + 3 * q:base + 4 * q, T0:T1]), (3 * q, q, UT[base + 2 * q:base + 3 * q, T0:T1])]
        return f

    rope = b.sb(cst, "rope", [128, 4, 2048], BF16)
    S.dma("pool", rope[:], D["rope64"].rearrange("a p t -> p a t"), writes=["rope"])

    def stage_gqa_like(g, l, kind):
        G = GR[g]
        T, L, P, nseq, do_rope = G["T"], G["L"], G["P"], G["nseq"], G["rope"]
        UT, UTM, OT = D["UT_" + g], D["UTM_" + g], D["OT_" + g]
        qo, ko = (OFF["sq"], OFF["sk"]) if kind == "swa" else (OFF["gq"], OFF["gk"])
        vcol = 1536 if kind == "swa" else 1664
        bi = 1 if kind == "swa" else 3
        do_norm = kind == "gqa"
        scale = 64 ** -0.5
        nkc_ctx = P // 128
        with ExitStack() as st:
            kT = b.sb(st, "kT", [128, P + T], BF16)
            Vt = b.sb(st, "Vt", [128, (P + T) // 128, 2, 128], BF16)
            qTh = b.sb(st, "qTh", [128, 8, T], BF16)
            gq2 = b.sb(st, "gq2", [128, 2]); gk2 = b.sb(st, "gk2", [128, 2])
            sinkrow = b.sb(st, "sinkrow", [1, 8, 128], BF16)
            sk32 = b.sb(st, "sk32", [1, 8])
            wk = dict(E=[b.sb(st, "E%d" % i, [128, 512], BF16) for i in range(6)], rec=b.sb(st, "rec", [128, 512]))
            obs = [b.sb(st, "ob%d" % i, [128, 512], BF16) for i in range(2)]
            swm = b.sb(st, "swamask", [128, 2, 512], BF16)
            S.dma("sp", swm[:], D["swamask"].rearrange("a p c -> p a c"), writes=["swamask"])
            S.op("pool", lambda e: e.memset(qTh[64:128, 0:4, :], 0.0), writes=["qTh%d" % hh for hh in range(4)])
            S.op("pool", lambda e: e.memset(qTh[0:64, 4:8, :], 0.0), writes=["qTh%d" % hh for hh in range(4, 8)])
            S.op("pool", lambda e: e.memset(Vt[:], 0.0), writes=["Vt"])
            if do_norm:
                S.dma("sp", gq2[:], D["gqa_qn"][l], writes=["gq2"])
                S.dma("sp", gk2[:], D["gqa_kn"][l], writes=["gk2"])
            else:
                S.dma("sp", sk32[:], D["sink"][l], writes=["sk32"])
                S.op("act", lambda e: e.activation(sk32[:], sk32[:], AF.Exp), reads=["sk32"], writes=["sk32"])
                S.op("dve", lambda e: e.tensor_copy(sinkrow[:], sk32[:].unsqueeze(2).to_broadcast([1, 8, 128])), reads=["sk32"], writes=["sinkrow"])
            with ExitStack() as s2:
                RR = rr_alloc(s2, T, do_rope)
                k32 = b.sb(s2, "k32", [128, T]) if g == "p" else None
                v32 = b.sb(s2, "v32", [128, (P + T) // 128, 128])
                if P:
                    src_k = D["c_swa_kT"] if kind == "swa" else D["c_gqa_kT"]
                    src_v = D["c_swa_v"] if kind == "swa" else D["c_gqa_v"]
                    S.dma("pool", kT[:, 0:P], src_k[l].rearrange("h d t -> (h d) t"), writes=["kT"])
                    S.dma("sp", v32[:, 0:P // 128, :], src_v[l].rearrange("(c p) f -> p c f", p=128), writes=["v32"])
                S.dma("sp", v32[:, P // 128:, :], UTM[:, vcol:vcol + 128].rearrange("(c p) f -> p c f", p=128), writes=["v32"])
                S.op("dve", lambda e: e.tensor_copy(Vt[:, :, 0, 0:64], v32[:, :, 0:64]), reads=["v32"], writes=["Vt"])
                S.op("dve", lambda e: e.tensor_copy(Vt[:, :, 1, 64:128], v32[:, :, 64:128]), reads=["v32"], writes=["Vt"])
                if g == "p":
                    dst = D["o_swa_v"] if kind == "swa" else D["o_gqa_v"]
                    S.dma("act", dst[l].rearrange("(c p) f -> p c f", p=128), v32[:], reads=["v32"], writes=[])
                nrr = 0
                for kvh in range(2):
                    p0 = kvh * 64
                    rms_rope_rows(RR[nrr % 2], p0, swap_rows(ko + kvh * 64, 0, T, UT, 64), 64, T, gk2, "gk2", do_norm, do_rope, 0,
                                  kT[p0:p0 + 64, P:P + T], "kT", out32=k32, out32n="k32", out32_pre_rope=True)
                    nrr += 1
                if g == "p":
                    dst = D["o_swa_kT"] if kind == "swa" else D["o_gqa_kT"]
                    S.dma("sp", dst[l], k32[:], reads=["k32"], writes=[])
                for h in range(8):
                    p0 = (h // 4) * 64
                    rms_rope_rows(RR[nrr % 2], p0, swap_rows(qo + h * 64, 0, T, UT, 64), 64, T, gq2, "gq2", do_norm, do_rope, 0,
                                  qTh[p0:p0 + 64, h, :], "qTh%d" % h)
                    nrr += 1
                nu = 0
                qna = ["qTh%d" % hh for hh in range(4)]
                qnb = ["qTh%d" % hh for hh in range(4, 8)]
                for sq_ in range(nseq):
                    T0 = sq_ * L
                    kb0 = P + T0
                    for qb in range(L // 128):
                        cols = [(c * 128, None) for c in range(nkc_ctx)]
                        if kind == "swa" and P:
                            cols += [(kb0 + kb * 128, mi) for (kb, mi) in ((qb - 1, 0), (qb, None), (qb + 1, 1)) if 0 <= kb < L // 128]
                        else:
                            cols += [(kb0 + kb * 128, None) for kb in range(L // 128)]
                        chunks = [(kT[:, c0:c0 + 128], kT[:, c0:c0 + 128], "kT", Vt[:, c0 // 128, 0, :], Vt[:, c0 // 128, 1, :], "Vt", 128,
                                   None if mi is None else swm[:, mi, :]) for (c0, mi) in cols]
                        ob, obn = obs[nu % 2], "ob%d" % (nu % 2)
                        nu += 1
                        q0 = T0 + qb * 128
                        attn_pair(qTh[:, 0:4, q0:q0 + 128], qna, qTh[:, 4:8, q0:q0 + 128], qnb, 512, chunks, scale, ob[:], obn, wk,
                                  sink=((sinkrow[0:1, 0:4, :], sinkrow[0:1, 4:8, :]) if kind == "swa" else None))
                        for kvh in range(2):
                            S.dma("sp" if kvh == 0 else "act", OT[bi, kvh * 256:(kvh + 1) * 256, q0:q0 + 128].rearrange("(h d) t -> d h t", d=64),
                                  ob[kvh * 64:(kvh + 1) * 64, :].rearrange("d (h t) -> d h t", h=4), reads=[obn], writes=[])
                S.barrier()

    def stage_mla(g, l):
        G = GR[g]
        T, L, P, nseq, do_rope = G["T"], G["L"], G["P"], G["nseq"], G["rope"]
        UT, OT = D["UT_" + g], D["OT_" + g]
        scale = 96 ** -0.5
        NK = P + L
        for sq_ in range(nseq):
            T0, T1 = sq_ * L, (sq_ + 1) * L
            with ExitStack() as st:
                ckvT = b.sb(st, "ckvT", [128, NK], BF16)
                krT = b.sb(st, "krT", [96, NK], BF16)
                cqn = b.sb(st, "cqn", [128, 2, L], BF16)
                wuq = b.sb(st, "wuq", [128, 2, 768], BF16); wuqs = b.sb(st, "wuqs", [128, 2, 768], BF16)
                wukv = b.sb(st, "wukv", [128, 1024], BF16)
                g_q = b.sb(st, "g_q", [128, 2]); g_kv = b.sb(st, "g_kv", [128, 1])
                S.dma("pool", wuq[:], D["w_uq"][l].rearrange("(kc p) c -> p kc c", p=128), writes=["wuq"])
                S.dma("pool", wuqs[:], D["w_uq_sw"][l].rearrange("(kc p) c -> p kc c", p=128), writes=["wuqs"])
                S.dma("pool", wukv[:], D["w_ukv"][l], writes=["wukv"])
                S.dma("sp", g_q[:], D["mla_qnT"][l], writes=["g_q"])
                S.dma("sp", g_kv[:], D["mla_kvn"][l], writes=["g_kv"])
                with ExitStack() as s2:
                    x = b.sb(s2, "m_x", [128, 2, L]); sqb = b.sb(s2, "m_sq", [128, 2, 512], BF16); rs = b.sb(s2, "m_rs", [128, 512])
                    xk = b.sb(s2, "m_xk", [128, L]); kr32 = b.sb(s2, "m_kr", [96, L]); krs = b.sb(s2, "m_krs", [96, L]); t1 = b.sb(s2, "m_t1", [96, 512])
                    if P:
                        S.dma("pool", ckvT[:, 0:P], D["c_ckvT"][l], writes=["ckvT"])
                        S.dma("pool", krT[64:96, 0:P], D["c_krT"][l], writes=["krT"])
                    S.dma("sp", x[:], UT[OFF["cq"]:OFF["cq"] + 256, T0:T1].rearrange("(kc p) t -> p kc t", p=128), writes=["m_x"])
                    S.dma("sp", xk[:], UT[OFF["ckv"]:OFF["ckv"] + 128, T0:T1], writes=["m_xk"])
                    S.dma("sp", kr32[64:96, :], UT[OFF["kr"]:OFF["kr"] + 32, T0:T1], writes=["m_kr"])
                    if do_rope:
                        for (r0, nr, ap) in swap_rows(OFF["kr"], T0, T1, UT, 32)(True):
                            S.dma("act", krs[64 + r0:64 + r0 + nr, :], ap, writes=["m_krs"])
                    for c in range(L // 512 if L >= 512 else 1):
                        w = min(512, L)
                        cs = slice(c * w, (c + 1) * w)
                        pt, pn = b.ps()
                        for kc in range(2):
                            S.op("act", lambda e: e.activation(sqb[:, kc, 0:w], x[:, kc, cs], AF.Square), reads=["m_x"], writes=["m_sq"])
                        for kc in range(2):
                            S.op("pe", lambda e: e.matmul(pt[:, 0:w], ones_bf[:], sqb[:, kc, 0:w], start=(kc == 0), stop=(kc == 1)),
                                 reads=["m_sq", "ones_bf"], writes=[pn], pe_acc=True)
                        rstd_from_sumsq(pt, pn, rs, "m_rs", 128, w, 1.0 / 256)
                        for kc in range(2):
                            S.op("dve", lambda e: e.scalar_tensor_tensor(cqn[:, kc, cs], x[:, kc, cs], g_q[:, kc:kc + 1], rs[:, 0:w], ALU.mult, ALU.mult),
                                 reads=["m_x", "m_rs", "g_q"], writes=["cqn"])
                        pt, pn = b.ps()
                        S.op("act", lambda e: e.activation(sqb[:, 0, 0:w], xk[:, cs], AF.Square), reads=["m_xk"], writes=["m_sq"])
                        S.op("pe", lambda e: e.matmul(pt[:, 0:w], ones_bf[:], sqb[:, 0, 0:w], start=True, stop=True), reads=["m_sq", "ones_bf"], writes=[pn])
                        rstd_from_sumsq(pt, pn, rs, "m_rs", 128, w, 1.0 / 128)
                        S.op("dve", lambda e: e.scalar_tensor_tensor(xk[:, cs], xk[:, cs], g_kv[:, 0:1], rs[:, 0:w], ALU.mult, ALU.mult),
                             reads=["m_xk", "m_rs", "g_kv"], writes=["m_xk"])
                        S.op("pool", lambda e: e.tensor_copy(ckvT[:, P + c * w:P + (c + 1) * w], xk[:, cs]), reads=["m_xk"], writes=["ckvT"])
                        if do_rope:
                            S.op("dve", lambda e: e.tensor_tensor(t1[64:96, 0:w], kr32[64:96, cs], rope[64:96, 2, cs], ALU.mult), reads=["m_kr", "rope"], writes=["m_t1"])
                            S.op("pool", lambda e: e.tensor_tensor(krs[64:96, cs], krs[64:96, cs], rope[64:96, 3, cs], ALU.mult), reads=["m_krs", "rope"], writes=["m_krs"])
                            S.op("dve", lambda e: e.tensor_tensor(krT[64:96, P + c * w:P + (c + 1) * w], t1[64:96, 0:w], krs[64:96, cs], ALU.add),
                                 reads=["m_t1", "m_krs"], writes=["krT"])
                        else:
                            S.op("dve", lambda e: e.tensor_copy(krT[64:96, P + c * w:P + (c + 1) * w], kr32[64:96, cs]), reads=["m_kr"], writes=["krT"])
                    if g == "p":
                        S.dma("sp", D["o_ckvT"][l, :, T0:T1], xk[:], reads=["m_xk"], writes=[])
                        S.dma("sp", D["o_krT"][l, :, T0:T1], kr32[64:96, :], reads=["m_kr"], writes=[])
                    S.barrier()
                Vall = b.sb(st, "Vall", [128, NK // 128, 512], BF16)
                for c in range(NK // 128):
                    pt, pn = b.ps()
                    S.op("pe", lambda e: e.matmul(pt[:, :], ckvT[:, c * 128:(c + 1) * 128],
                                                  wukv[:].rearrange("p (h x) -> p h x", x=128)[:, :, 64:128], start=True, stop=True),
                         reads=["ckvT", "wukv"], writes=[pn])
                    if c % 2 == 0:
                        S.op("act", lambda e: e.copy(Vall[:, c, :], pt[:, :]), reads=[pn], writes=["Vall"])
                    else:
                        S.op("dve", lambda e: e.tensor_copy(Vall[:, c, :], pt[:, :]), reads=[pn], writes=["Vall"])
                S.barrier()
                with ExitStack() as s2:
                    wk = dict(E=[b.sb(s2, "E%d" % i, [128, 512], BF16) for i in range(3)], rec=b.sb(s2, "rec", [64, 512]))
                    kTh = [b.sb(s2, "kTh%d" % i, [96, NK], BF16) for i in range(2)]
                    qh = [b.sb(s2, "qh%d" % i, [96, L], BF16) for i in range(2)]
                    obs = [b.sb(s2, "ob%d" % i, [64, 512], BF16) for i in range(2)]
                    t2 = b.sb(s2, "m_t2", [96, 512]); t3 = b.sb(s2, "m_t3", [96, 512])
                    nu = 0
                    W = min(512, L)
                    for h in range(8):
                        kt, ktn = kTh[h % 2], "kTh%d" % (h % 2)
                        q_, q_n = qh[h % 2], "qh%d" % (h % 2)
                        for c in range((NK + 511) // 512):
                            w = min(512, NK - c * 512)
                            pt, pn = b.ps()
                            S.op("pe", lambda e: e.matmul(pt[0:64, 0:w], wukv[:, h * 128:h * 128 + 64], ckvT[:, c * 512:c * 512 + w], start=True, stop=True),
                                 reads=["wukv", "ckvT"], writes=[pn])
                            S.op("act", lambda e: e.copy(kt[0:64, c * 512:c * 512 + w], pt[0:64, 0:w]), reads=[pn], writes=[ktn])
                        S.op("pool", lambda e: e.tensor_copy(kt[64:96, :], krT[64:96, :]), reads=["krT"], writes=[ktn])
                        for c in range(L // W):
                            cs = slice(c * W, (c + 1) * W)
                            pt, pn = b.ps()
                            for kc in range(2):
                                S.op("pe", lambda e: e.matmul(pt[0:96, 0:W], wuq[:, kc, h * 96:(h + 1) * 96], cqn[:, kc, cs], start=(kc == 0), stop=(kc == 1)),
                                     reads=["wuq", "cqn"], writes=[pn], pe_acc=True)
                            S.op("act", lambda e: e.copy(q_[0:64, cs], pt[0:64, 0:W]), reads=[pn], writes=[q_n])
                            if do_rope:
                                pt2, pn2 = b.ps()
                                for kc in range(2):
                                    S.op("pe", lambda e: e.matmul(pt2[0:96, 0:W], wuqs[:, kc, h * 96:(h + 1) * 96], cqn[:, kc, cs], start=(kc == 0), stop=(kc == 1)),
                                         reads=["wuqs", "cqn"], writes=[pn2], pe_acc=True)
                                S.op("dve", lambda e: e.tensor_tensor(t2[64:96, 0:W], pt[64:96, 0:W], rope[64:96, 2, cs], ALU.mult), reads=[pn, "rope"], writes=["m_t2"])
                                S.op("dve", lambda e: e.tensor_tensor(t3[64:96, 0:W], pt2[64:96, 0:W], rope[64:96, 3, cs], ALU.mult), reads=[pn2, "rope"], writes=["m_t3"])
                                S.op("pool", lambda e: e.tensor_tensor(q_[64:96, cs], t2[64:96, 0:W], t3[64:96, 0:W], ALU.add), reads=["m_t2", "m_t3"], writes=[q_n])
                            else:
                                S.op("dve", lambda e: e.tensor_copy(q_[64:96, cs], pt[64:96, 0:W]), reads=[pn], writes=[q_n])
                        for c in range(L // W):
                            chunks = [(kt[:, kc * 128:(kc + 1) * 128], ktn, Vall[:, kc, h * 64:(h + 1) * 64], "Vall", 128, None) for kc in range(NK // 128)]
                            ob, obn = obs[nu % 2], "ob%d" % (nu % 2)
                            nu += 1
                            attn_unit(q_[:, c * W:(c + 1) * W], q_n, 96, W, chunks, scale, ob[:, 0:W], obn, wk)
                            S.dma("sp", OT[2, h * 64:(h + 1) * 64, T0 + c * W:T0 + (c + 1) * W], ob[:, 0:W], reads=[obn], writes=[])
                    S.barrier()

    def stage_hgrn(g, l):
        G = GR[g]
        T, L, P, nseq = G["T"], G["L"], G["P"], G["nseq"]
        UT, UTM, OT = D["UT_" + g], D["UTM_" + g], D["OT_" + g]
        NTL = L // 128
        ident, triU, triL, sL, sU, csel = (cm[:, i, :] for i in range(6))
        with ExitStack() as st:
            lbb = b.sb(st, "lbb", [128, 2, 512]); oml = b.sb(st, "oml", [128, 2, 512])
            with ExitStack() as s2:
                lbr = b.sb(s2, "lbr", [128, DEPTH, 2, 512]); den = b.sb(s2, "lden", [128, 2, 512])
                S.dma("sp", lbr[:], D["lbT"].partition_broadcast(128), writes=["lbr"])
                S.op("act", lambda e: e.activation(lbr[:], lbr[:], AF.Exp), reads=["lbr"], writes=["lbr"])
                S.op("dve", lambda e: e.tensor_tensor(den[:], lbr[:, 0], lbr[:, 1], ALU.add), reads=["lbr"], writes=["lden"])
                S.op("dve", lambda e: e.reciprocal(den[:], den[:]), reads=["lden"], writes=["lden"])
                if l == 0:
                    S.op("pool", lambda e: e.memset(lbb[:], 0.0), writes=["lbb"])
                else:
                    S.op("dve", lambda e: e.tensor_tensor(lbb[:], lbr[:, 1], den[:], ALU.mult), reads=["lbr", "lden"], writes=["lbb"])
                S.op("dve", lambda e: e.tensor_scalar(oml[:], lbb[:], -1.0, 1.0, ALU.mult, ALU.add), reads=["lbb"], writes=["oml"])
                S.barrier()
            hn = b.sb(st, "hn", [128, 4]); S.dma("sp", hn[:], D["hgrn_normT"][l], writes=["hn"])
            Sst = b.sb(st, "Sst", [128, 2, 4, 128])
            SSTN = ["Sst%d%d" % (d_, h_) for d_ in range(2) for h_ in range(4)]
            oall = b.sb(st, "oall", [128, 2, 4, L], BF16)
            W_ = {}
            for d in range(2):
                for k in range(2):
                    sfx = "%d%d" % (d, k)
                    W_[d, k] = dict(
                        sfx=sfx,
                        tt=b.sb(st, "h_t" + sfx, [128, 512]), vv=b.sb(st, "h_v" + sfx, [128, 512]), gg=b.sb(st, "h_g" + sfx, [128, 512]),
                        kk=b.sb(st, "h_k" + sfx, [128, 512]), kt=b.sb(st, "h_kt" + sfx, [128, 512]), kh=b.sb(st, "h_kh" + sfx, [128, 512]),
                        khm=b.sb(st, "h_khm" + sfx, [128, 4, 4, 128]), qT=b.sb(st, "h_qT" + sfx, [128, 4, 128]), qt=b.sb(st, "h_qt" + sfx, [128, 4, 128]),
                        eb=b.sb(st, "h_eb" + sfx, [128, 4, 132]), ktT=b.sb(st, "h_ktT" + sfx, [128, 4, 128]), AT=b.sb(st, "h_AT" + sfx, [128, 4, 128]))
            fin = dict(os=b.sb(st, "f_os", [128, 4, 256]), gT=b.sb(st, "f_gT", [128, 4, 256]), sq=b.sb(st, "f_sq", [128, 4, 256], BF16),
                       rs=b.sb(st, "f_rs", [128, 4, 256]), ob=b.sb(st, "f_ob", [128, 4, 256], BF16))
            it = 0
            b.nrot = 4
            tmpo = [b.sb(st, "h_tmpo%d" % d_, [128, 512]) for d_ in range(2)]
            for sq_ in range(nseq):
                T0 = sq_ * L
                if P:
                    S.dma("sp", Sst[:], D["st_hgrn"][l].rearrange("d h k v -> k d h v"), writes=SSTN)
                else:
                    S.op("pool", lambda e: e.memset(Sst[:], 0.0), writes=SSTN)
                for i in range(NTL):
                    k = it % 2
                    it += 1
                    tis = (i, NTL - 1 - i)
                    pos = [b.acc(0), b.acc(1)]
                    pis = [b.acc(2), b.acc(3)]
                    for d in range(2):
                        w = W_[d, k]
                        x = w["sfx"]
                        t0 = T0 + tis[d] * 128
                        M_in, M_ex = (triU, sL) if d == 0 else (triL, sU)
                        tt_, vv, gg, kk, kt_, kh, khm, qT, qt, eb, ktT, AT = (w[n_] for n_ in ("tt", "vv", "gg", "kk", "kt", "kh", "khm", "qT", "qt", "eb", "ktT", "AT"))
                        S.dma("sp", tt_[:], UTM[t0:t0 + 128, d * 512:(d + 1) * 512], writes=["h_t" + x])
                        S.dma("sp", vv[:], UTM[t0:t0 + 128, 1024:1536], writes=["h_v" + x])
                        S.dma("act", qT[:], UT[0:512, t0:t0 + 128].rearrange("(h p) t -> p h t", p=128), writes=["h_qT" + x])
                        S.op("act", lambda e: e.activation(qT[:], qT[:], AF.Silu), reads=["h_qT" + x], writes=["h_qT" + x])
                        S.op("act", lambda e: e.activation(tt_[:], tt_[:], AF.Sigmoid), reads=["h_t" + x], writes=["h_t" + x])
                        S.op("dve", lambda e: e.tensor_tensor(tt_[:], tt_[:], oml[:, d], ALU.mult), reads=["h_t" + x, "oml"], writes=["h_t" + x])
                        S.op("dve", lambda e: e.scalar_tensor_tensor(gg[:], tt_[:], 1e-30, lbb[:, d], ALU.max, ALU.add), reads=["h_t" + x, "lbb"], writes=["h_g" + x])
                        S.op("act", lambda e: e.activation(gg[:], gg[:], AF.Ln), reads=["h_g" + x], writes=["h_g" + x])
                        S.op("dve", lambda e: e.tensor_tensor(kk[:], oml[:, d], tt_[:], ALU.subtract), reads=["h_t" + x, "oml"], writes=["h_k" + x])
                        pb, pbn = b.ps()
                        S.op("pe", lambda e: e.matmul(pb[:], M_in, gg[:], start=True, stop=True), reads=["cm", "h_g" + x], writes=[pbn])
                        S.op("act", lambda e: e.activation(kt_[:], pb[:], AF.Exp, scale=-1.0), reads=[pbn], writes=["h_kt" + x])
                        S.op("dve", lambda e: e.tensor_tensor(kt_[:], kt_[:], kk[:], ALU.mult), reads=["h_kt" + x, "h_k" + x], writes=["h_kt" + x])
                        pr, prn = b.ps()
                        S.op("pe", lambda e: e.matmul(pr[:], M_ex, gg[:], start=True, stop=True), reads=["cm", "h_g" + x], writes=[prn])
                        S.op("act", lambda e: e.activation(kh[:], pr[:], AF.Exp), reads=[prn], writes=["h_kh" + x])
                        S.op("dve", lambda e: e.tensor_tensor(kh[:], kh[:], kk[:], ALU.mult), reads=["h_kh" + x, "h_k" + x], writes=["h_kh" + x])
                        po_, pon_ = pos[d]
                        for h in range(4):
                            hs = slice(h * 128, (h + 1) * 128)
                            pbt, pbtn = b.ps()
                            S.op("pe", lambda e: e.matmul(pbt[:, 0:128], gg[:, hs], M_in, start=True, stop=True), reads=["h_g" + x, "cm"], writes=[pbtn])
                            S.op("pe", lambda e: e.matmul(pbt[:, 128:132], gg[:, hs], csel[:, 0:4], start=True, stop=True), reads=["h_g" + x, "cm"], writes=[pbtn])
                            S.op("act", lambda e: e.activation(eb[:, h, :], pbt[:, 0:132], AF.Exp), reads=[pbtn], writes=["h_eb" + x])
                            S.op("dve", lambda e: e.tensor_tensor(qt[:, h, :], qT[:, h, :], eb[:, h, 0:128], ALU.mult), reads=["h_qT" + x, "h_eb" + x], writes=["h_qt" + x])
                            pk, pkn = b.ps()
                            S.op("pe", lambda e: e.matmul(pk[:, 0:128], kt_[:, hs], ident, start=True, stop=True), reads=["h_kt" + x, "cm"], writes=[pkn])
                            S.op("act", lambda e: e.copy(ktT[:, h, :], pk[:, 0:128]), reads=[pkn], writes=["h_ktT" + x])
                            pa, pan = b.ps()
                            S.op("pe", lambda e: e.matmul(pa[:, 0:128], ktT[:, h, :], qt[:, h, :], start=True, stop=True), reads=["h_ktT" + x, "h_qt" + x], writes=[pan])
                            S.op("dve", lambda e: e.tensor_tensor(AT[:, h, :], pa[:, 0:128], M_in, ALU.mult), reads=[pan, "cm"], writes=["h_AT" + x])
                            for j in range(4):
                                S.op("pool", lambda e: e.tensor_scalar(khm[:, h, j, :], kh[:, hs], csel[:, j:j + 1], None, ALU.mult),
                                     reads=["h_kh" + x, "cm"], writes=["h_khm" + x])
                            S.op("pe", lambda e: e.matmul(po_[:, hs], vv[:, hs], AT[:, h, :], start=True, stop=True), reads=["h_v" + x, "h_AT" + x], writes=[pon_], pe_acc=(h > 0))
                    for n_ in range(4):
                        pss = []
                        for d in range(2):
                            w = W_[d, k]
                            x = w["sfx"]
                            j = n_ if d == 0 else 3 - n_
                            js = slice(j * 32, (j + 1) * 32)
                            po_, pon_ = pis[d]
                            ps_, psn2 = b.ps()
                            pss.append((ps_, psn2, j))
                            for h in range(4):
                                hs = slice(h * 128, (h + 1) * 128)
                                S.op("pe", lambda e: e.matmul(po_[:, h * 128 + j * 32:h * 128 + (j + 1) * 32], Sst[:, d, h, :], w["qt"][:, h, js], start=True, stop=True),
                                     reads=["Sst%d%d" % (d, h), "h_qt" + x], writes=[pon_], pe_acc=(n_ > 0 or h > 0))
                                S.op("pe", lambda e: e.matmul(ps_[:, hs], w["khm"][:, h, j, :], w["vv"][:, hs], start=True, stop=True),
                                     reads=["h_khm" + x, "h_v" + x], writes=[psn2], pe_acc=True)
                        for d in range(2):
                            w = W_[d, k]
                            x = w["sfx"]
                            ps_, psn2, j = pss[d]
                            for h in range(4):
                                hs = slice(h * 128, (h + 1) * 128)
                                S.op("dve", lambda e: e.scalar_tensor_tensor(Sst[:, d, h, :], Sst[:, d, h, :], w["eb"][:, h, 128 + j:129 + j], ps_[:, hs], ALU.mult, ALU.add),
                                     reads=[psn2, "h_eb" + x, "Sst%d%d" % (d, h)], writes=["Sst%d%d" % (d, h)])
                    for d in range(2):
                        po_, pon_ = pos[d]
                        pi_, pin_ = pis[d]
                        tl = tis[d] * 128
                        S.op("act", lambda e: e.copy(tmpo[d][:], po_[:]), reads=[pon_], writes=["h_tmpo%d" % d])
                        S.op("dve", lambda e: e.tensor_tensor(oall[:, d, :, tl:tl + 128], tmpo[d][:].rearrange("p (h t) -> p h t", h=4),
                                                              pi_[:].rearrange("p (h t) -> p h t", h=4), ALU.add),
                             reads=[pin_, "h_tmpo%d" % d], writes=["oall"])
                for c in range(L // 256):
                    cs = slice(c * 256, (c + 1) * 256)
                    os_, gT, sq, rs, ob = fin["os"], fin["gT"], fin["sq"], fin["rs"], fin["ob"]
                    S.dma("act", gT[:], UT[OFF["hg"]:OFF["hg"] + 512, T0 + c * 256:T0 + (c + 1) * 256].rearrange("(h p) t -> p h t", p=128), writes=["f_gT"])
                    S.op("act", lambda e: e.activation(gT[:], gT[:], AF.Silu), reads=["f_gT"], writes=["f_gT"])
                    S.op("dve", lambda e: e.tensor_tensor(os_[:], oall[:, 0, :, cs], oall[:, 1, :, cs], ALU.add), reads=["oall"], writes=["f_os"])
                    S.op("act", lambda e: e.activation(sq[:], os_[:], AF.Square), reads=["f_os"], writes=["f_sq"])
                    for hh in range(2):
                        pn_, pnn = b.ps()
                        for h2 in range(2):
                            h = hh * 2 + h2
                            S.op("pe", lambda e: e.matmul(pn_[:, h2 * 256:(h2 + 1) * 256], ones_bf[:], sq[:, h, :], start=True, stop=True),
                                 reads=["f_sq", "ones_bf"], writes=[pnn])
                        rstd_from_sumsq(pn_, pnn, rs[:, hh * 2:hh * 2 + 2, :].rearrange("p a t -> p (a t)"), "f_rs", 128, 512, 1.0 / 128)
                    for h in range(4):
                        S.op("dve", lambda e: e.scalar_tensor_tensor(os_[:, h, :], os_[:, h, :], hn[:, h:h + 1], rs[:, h, :], ALU.mult, ALU.mult),
                             reads=["f_os", "hn", "f_rs"], writes=["f_os"])
                    S.op("dve", lambda e: e.tensor_tensor(ob[:], os_[:], gT[:], ALU.mult), reads=["f_os", "f_gT"], writes=["f_ob"])
                    S.dma("sp", OT[0, :, T0 + c * 256:T0 + (c + 1) * 256].rearrange("(h p) t -> p h t", p=128), ob[:], reads=["f_ob"], writes=[])
                if g == "p":
                    S.dma("sp", D["o_hgrn"][l, sq_].rearrange("d h k v -> k d h v"), Sst[:], reads=SSTN, writes=[])
            b.nrot = 6
            S.barrier()

    def epilogue(st, z, zn, xt, xn, gco, Xdst, c0, wkn):
        sq, rs, tmp = wkn
        for kc in range(8):
            S.op("act", lambda e: e.activation(sq[:, kc, :], z[:, kc, :], AF.Square), reads=[zn], writes=["e_sq"])
        pt, pn = b.ps()
        for kc in range(8):
            S.op("pe", lambda e: e.matmul(pt[:], ones_bf[:], sq[:, kc, :], start=(kc == 0), stop=(kc == 7)), reads=["e_sq", "ones_bf"], writes=[pn], pe_acc=True)
        rstd_from_sumsq(pt, pn, rs, "e_rs", 128, 512, 1.0 / DM)
        for kc in range(8):
            S.op("dve", lambda e: e.tensor_tensor(tmp[:], z[:, kc, :], rs[:], ALU.mult), reads=[zn, "e_rs"], writes=["e_tmp"])
            S.op("dve", lambda e: e.scalar_tensor_tensor(xt[:, kc, :], tmp[:], gco[:, kc:kc + 1], xt[:, kc, :], ALU.mult, ALU.add),
                 reads=["e_tmp", "coef", xn], writes=[xn])
        S.dma("sp", Xdst[:, c0:c0 + 512].rearrange("(kc p) t -> p kc t", p=128), xt[:], reads=[xn], writes=[])

    def stage_merge(g, l, Xsrc, Xdst):
        G = GR[g]
        T = G["T"]
        UT, OT = D["UT_" + g], D["OT_" + g]
        with ExitStack() as st:
            Wb = b.sb(st, "Wb", [128, 4, 4, 1024], BF16)
            Wo = b.sb(st, "Wo", [128, 8, 1024], BF16)
            S.dma("pool", Wb[:], D["w_branch"][l].rearrange("n (kc p) c -> p n kc c", p=128), writes=["Wb"])
            S.dma("pool", Wo[:], D["w_out"][l].rearrange("(kc p) c -> p kc c", p=128), writes=["Wo"])
            ot = b.sb(st, "ot", [128, 4, 4, 512], BF16)
            yp = b.sb(st, "yp", [128, 8, 512], BF16)
            gts = [b.sb(st, "gt%d" % i, [128, 512]) for i in range(3)]
            accf = b.sb(st, "accf", [128, 512]); tm2 = b.sb(st, "tm2", [128, 512])
            z = b.sb(st, "z", [128, 8, 512]); xt = b.sb(st, "xt", [128, 8, 512])
            wkn = (b.sb(st, "e_sq", [128, 8, 512], BF16), b.sb(st, "e_rs", [128, 512]), b.sb(st, "e_tmp", [128, 512]))
            ng = 0
            for tt in range(T // 512):
                cs = slice(tt * 512, (tt + 1) * 512)
                S.dma("sp", ot[:], OT[:, :, cs].rearrange("n (kc p) t -> p n kc t", p=128), writes=["ot"])
                S.dma("act", xt[:], Xsrc[:, cs].rearrange("(kc p) t -> p kc t", p=128), writes=["xt"])
                for dmc in range(8):
                    for n in range(4):
                        gt, gtn = gts[ng % 3], "gt%d" % (ng % 3)
                        ng += 1
                        r0 = OFF["gates"] + n * 1024 + dmc * 128
                        S.dma("sp", gt[:], UT[r0:r0 + 128, cs], writes=[gtn])
                        S.op("act", lambda e: e.activation(gt[:], gt[:], AF.Sigmoid), reads=[gtn], writes=[gtn])
                        pt, pn = b.ps()
                        for kc in range(4):
                            S.op("pe", lambda e: e.matmul(pt[:], Wb[:, n, kc, dmc * 128:(dmc + 1) * 128], ot[:, n, kc, :], start=(kc == 0), stop=(kc == 3)),
                                 reads=["Wb", "ot"], writes=[pn], pe_acc=True)
                        if n == 0:
                            S.op("dve", lambda e: e.tensor_tensor(accf[:], pt[:], gt[:], ALU.mult), reads=[pn, gtn], writes=["accf"])
                        elif n < 3:
                            S.op("dve", lambda e: e.tensor_tensor(tm2[:], pt[:], gt[:], ALU.mult), reads=[pn, gtn], writes=["tm2"])
                            S.op("dve", lambda e: e.tensor_tensor(accf[:], accf[:], tm2[:], ALU.add), reads=["tm2", "accf"], writes=["accf"])
                        else:
                            S.op("dve", lambda e: e.tensor_tensor(tm2[:], pt[:], gt[:], ALU.mult), reads=[pn, gtn], writes=["tm2"])
                            S.op("dve", lambda e: e.tensor_tensor(yp[:, dmc, :], accf[:], tm2[:], ALU.add), reads=["tm2", "accf"], writes=["yp"])
                for oc in range(8):
                    pt, pn = b.ps()
                    for kc in range(8):
                        S.op("pe", lambda e: e.matmul(pt[:], Wo[:, kc, oc * 128:(oc + 1) * 128], yp[:, kc, :], start=(kc == 0), stop=(kc == 7)),
                             reads=["Wo", "yp"], writes=[pn], pe_acc=True)
                    S.op("act", lambda e: e.copy(z[:, oc, :], pt[:]), reads=[pn], writes=["z"])
                epilogue(st, z, "z", xt, "xt", coef[:, l, G["cond"], 2], Xdst, tt * 512, wkn)
            S.barrier()

    def stage_mlp(g, l, Xsrc, Xdst):
        G = GR[g]
        T = G["T"]
        with ExitStack() as st:
            h2 = stage_h(st, g, l, Xsrc, 3, 4)
            W2 = [b.sb(st, "W2_%d" % i, [128, 8, 1024], BF16) for i in range(2)]
            n2 = 0
            w1 = [b.sb(st, "w1_%d" % i, [128, 8, 512], BF16) for i in range(2)]
            hid = b.sb(st, "hid", [128, 32, 512], BF16)
            rl = [b.sb(st, "rl%d" % i, [128, 512]) for i in range(2)]
            z = b.sb(st, "z", [128, 8, 512]); xt = b.sb(st, "xt", [128, 8, 512])
            wkn = (b.sb(st, "e_sq", [128, 8, 512], BF16), b.sb(st, "e_rs", [128, 512]), b.sb(st, "e_tmp", [128, 512]))
            nw = 0
            nr = 0
            for tt in range(T // 512):
                cs = slice(tt * 512, (tt + 1) * 512)
                S.dma("act", xt[:], Xsrc[:, cs].rearrange("(kc p) t -> p kc t", p=128), writes=["xt"])
                for cg in range(8):
                    w, wn = w1[nw % 2], "w1_%d" % (nw % 2)
                    nw += 1
                    S.dma("pool", w[:], D["w_mlp_in"][l, :, cg * 512:(cg + 1) * 512].rearrange("(kc p) c -> p kc c", p=128), writes=[wn])
                    for cc in range(4):
                        pt, pn = b.ps()
                        for kc in range(8):
                            S.op("pe", lambda e: e.matmul(pt[:], w[:, kc, cc * 128:(cc + 1) * 128], h2[:, kc, cs], start=(kc == 0), stop=(kc == 7)),
                                 reads=[wn, "hT"], writes=[pn], pe_acc=True)
                        r, rn = rl[nr % 2], "rl%d" % (nr % 2)
                        nr += 1
                        S.op("act", lambda e: e.activation(r[:], pt[:], AF.Relu), reads=[pn], writes=[rn])
                        S.op("dve", lambda e: e.tensor_tensor(hid[:, cg * 4 + cc, :], r[:], r[:], ALU.mult), reads=[rn], writes=["hid"])
                for q4 in range(4):
                    w2, w2n = W2[n2 % 2], "W2_%d" % (n2 % 2)
                    n2 += 1
                    S.dma("pool", w2[:], D["w_mlp_out"][l, q4 * 1024:(q4 + 1) * 1024, :].rearrange("(kc p) c -> p kc c", p=128), writes=[w2n])
                    for oc in range(8):
                        pt, pn = b.ps()
                        for fc in range(8):
                            S.op("pe", lambda e: e.matmul(pt[:], w2[:, fc, oc * 128:(oc + 1) * 128], hid[:, q4 * 8 + fc, :], start=(fc == 0), stop=(fc == 7)),
                                 reads=[w2n, "hid"], writes=[pn], pe_acc=True)
                        if q4 == 0:
                            S.op("act", lambda e: e.copy(z[:, oc, :], pt[:]), reads=[pn], writes=["z%d" % oc, "z"])
                        else:
                            S.op("dve", lambda e: e.tensor_tensor(z[:, oc, :], z[:, oc, :], pt[:], ALU.add), reads=[pn, "z%d" % oc], writes=["z%d" % oc, "z"])
                epilogue(st, z, "z", xt, "xt", coef[:, l, G["cond"], 5], Xdst, tt * 512, wkn)
            S.barrier()

    for g in ("s", "p"):
        X = D["xT_" + g]
        for l in range(DEPTH):
            if "proj" in STAGES:
                with ExitStack() as st:
                    hT = stage_h(st, g, l, X, 0, 1)
                    stage_proj(g, l, hT)
            if "hgrn" in STAGES:
                stage_hgrn(g, l)
            if "swa" in STAGES:
                stage_gqa_like(g, l, "swa")
            if "mla" in STAGES:
                stage_mla(g, l)
            if "gqa" in STAGES:
                stage_gqa_like(g, l, "gqa")
            if "merge" in STAGES:
                stage_merge(g, l, X, D["X1_" + g])
            Xn = D["yT_" + g] if l == DEPTH - 1 else D["X2_%d_%s" % (l, g)]
            if "mlp" in STAGES:
                stage_mlp(g, l, D["X1_" + g], Xn)
            X = Xn
    S.barrier()
    return nc, b


_CACHE = {}


def _consts():
    nf = 16
    t = np.arange(2048)
    row, col = (t // 64).astype(np.float32), (t % 64).astype(np.float32)
    rope = np.zeros((4, 128, 2048), np.float32)
    def fill(ci, si, r0, nf):
        inv = (10000.0 ** (-np.arange(nf, dtype=np.float32) / nf)).astype(np.float32)
        ar = (row[None, :] * inv[:, None]).astype(np.float32)
        ac = (col[None, :] * inv[:, None]).astype(np.float32)
        for k, a in enumerate((ar, ac)):
            b0 = r0 + k * 2 * nf
            rope[ci, b0:b0 + nf] = np.cos(a); rope[ci, b0 + nf:b0 + 2 * nf] = np.cos(a)
            rope[si, b0:b0 + nf] = -np.sin(a); rope[si, b0 + nf:b0 + 2 * nf] = np.sin(a)
    fill(0, 1, 0, 16)
    fill(0, 1, 64, 16)
    fill(2, 3, 64, 8)
    s_ = np.arange(128)[:, None]; t_ = np.arange(128)[None, :]
    same = (s_ // 32) == (t_ // 32)
    cm = np.zeros((6, 128, 128), np.float32)
    cm[0] = np.eye(128)
    cm[1] = same & (s_ <= t_)
    cm[2] = same & (s_ >= t_)
    cm[3] = same & (s_ > t_)
    cm[4] = same & (s_ < t_)
    cm[5][:, 0:4] = (s_ // 32) == np.arange(4)[None, :]
    j = np.arange(128)[:, None]; i = (np.arange(512) % 128)[None, :]
    swm = np.stack([(j >= i), (j <= i)]).astype(np.float32).astype(ml_dtypes.bfloat16)
    return rope, cm, swm


def _perm(nd):
    q = nd // 4
    return np.concatenate([np.arange(q, 2 * q), np.arange(0, q), np.arange(3 * q, 4 * q), np.arange(2 * q, 3 * q)])


def kernel(**inp):
    f = lambda a: np.ascontiguousarray(np.asarray(a, dtype=np.float32))
    I = {k: f(v) for k, v in inp.items()}
    if "prog" not in _CACHE:
        _CACHE["prog"] = build_program()
    nc, b = _CACHE["prog"]
    rope, cm, swm = _consts()
    fm = lambda v, n: f(v.reshape(v.shape[0], n, 128).transpose(0, 2, 1))
    shared = dict(
        w_ada=I["w_ada"], b_adaT=fm(I["b_ada"], 48),
        gains=f(np.stack([fm(I[k], 8) for k in ("norm_mix_pre", "norm_mix_post", "norm_mlp_pre", "norm_mlp_post")], axis=2)),
        w_in=I["w_in"], lbT=f(np.stack([I["hgrn_lb_fwd"], I["hgrn_lb_bwd"]], axis=1)),
        hgrn_normT=fm(I["hgrn_norm"], 4), sink=f(I["swa_sink"][:, None, :]),
        mla_qnT=fm(I["mla_q_norm"], 2), mla_kvn=f(I["mla_kv_norm"][:, :, None]),
        w_uq=I["mla_w_uq"], w_ukv=I["mla_w_ukv"],
        gqa_qn=f(np.tile(np.stack([I["gqa_q_norm"], I["gqa_q_norm"][:, _perm(64)]], axis=2), (1, 2, 1))),
        gqa_kn=f(np.tile(np.stack([I["gqa_k_norm"], I["gqa_k_norm"][:, _perm(64)]], axis=2), (1, 2, 1))),
        w_branch=I["w_branch"], w_out=I["w_out"], w_mlp_in=I["w_mlp_in"], w_mlp_out=I["w_mlp_out"],
        rope64=rope, cmat=cm, swamask=swm,
    )
    wsw = I["mla_w_uq"].reshape(DEPTH, 256, 8, 96).copy()
    wsw[..., 64:96] = wsw[..., 64:96][..., _perm(32)]
    shared["w_uq_sw"] = f(wsw.reshape(DEPTH, 256, 768))
    in_maps = []
    for i in range(NCORE):
        bb = i % 2
        m = dict(shared)
        m["xT_s"] = f(I["x_sample"][bb].T)
        m["xT_p"] = f(I["x_prompt"][4 * i:4 * i + 4].reshape(1024, DM).T)
        cc = np.stack([I["c_ctx"], I["c"][bb]], axis=1)
        m["cT"] = f(cc.reshape(8, 128, 2).transpose(1, 0, 2))
        m["st_hgrn"] = f(I["state_hgrn"][bb])
        m["c_swa_kT"] = f(I["cache_swa_k"][bb].transpose(0, 2, 3, 1))
        m["c_swa_v"] = f(I["cache_swa_v"][bb].reshape(DEPTH, 512, 128))
        m["c_ckvT"] = f(I["cache_mla_ckv"][bb].transpose(0, 2, 1))
        m["c_krT"] = f(I["cache_mla_kr"][bb].transpose(0, 2, 1))
        m["c_gqa_kT"] = f(I["cache_gqa_k"][bb].transpose(0, 2, 3, 1))
        m["c_gqa_v"] = f(I["cache_gqa_v"][bb].reshape(DEPTH, 512, 128))
        in_maps.append(m)
    res = run_bass_kernel_spmd(nc, in_maps, core_ids=list(range(NCORE)))
    R = res.results
    y_prompt = np.concatenate([R[i]["yT_p"].T.reshape(4, 256, DM) for i in range(NCORE)], axis=0)
    y_sample = np.stack([R[0]["yT_s"].T, R[1]["yT_s"].T], axis=0)
    cat = lambda fn: np.ascontiguousarray(np.concatenate([fn(R[i]) for i in range(NCORE)], axis=0).astype(np.float32))
    n_hgrn = cat(lambda r: r["o_hgrn"].transpose(1, 0, 2, 3, 4, 5))
    kT = lambda a: a.reshape(DEPTH, 2, 64, 4, 256).transpose(3, 0, 4, 1, 2)
    vv = lambda a: a.reshape(DEPTH, 4, 256, 2, 64).transpose(1, 0, 2, 3, 4)
    n_swa_k = cat(lambda r: kT(r["o_swa_kT"]))
    n_swa_v = cat(lambda r: vv(r["o_swa_v"]))
    n_ckv = cat(lambda r: r["o_ckvT"].reshape(DEPTH, 128, 4, 256).transpose(2, 0, 3, 1))
    n_kr = cat(lambda r: r["o_krT"].reshape(DEPTH, 32, 4, 256).transpose(2, 0, 3, 1))
    n_gqa_k = cat(lambda r: kT(r["o_gqa_kT"]))
    n_gqa_v = cat(lambda r: vv(r["o_gqa_v"]))
    return (np.ascontiguousarray(y_prompt.astype(np.float32)), np.ascontiguousarray(y_sample.astype(np.float32)),
            n_hgrn, n_swa_k, n_swa_v, n_ckv, n_kr, n_gqa_k, n_gqa_v)
```

```python
import numpy as np
from contextlib import ExitStack
import ml_dtypes
import concourse.bass as bass
import concourse.mybir as mybir
from concourse.bass_utils import run_bass_kernel_spmd

F32 = mybir.dt.float32
BF16 = mybir.dt.bfloat16
AF = mybir.ActivationFunctionType
ALU = mybir.AluOpType

DM = 1024
DEPTH = 2
NCORE = 8
EPS = 1e-6
OFF = dict(hq=0, ff=512, fb=1024, hi=1536, hg=2048, sq=2560, sk=3072, sv=3200, cq=3328, ckv=3584, kr=3712,
           gq=3744, gk=4256, gv=4384, gates=4512)
D_IN = 8608
UTM_COLS = 1792
STAGES = {"proj", "hgrn", "swa", "mla", "gqa", "merge", "mlp"}


class Sched:
    NDMA = 24

    def __init__(self, nc, es):
        self.nc = nc
        self.eng = {"pe": nc.tensor, "act": nc.scalar, "dve": nc.vector, "pool": nc.gpsimd, "sp": nc.sync}
        self.sem = {k: es.enter_context(nc.semaphore("s_" + k)) for k in ("pe", "act", "dve", "pool")}
        self.cnt = {k: 0 for k in self.sem}
        self.dsem = [es.enter_context(nc.semaphore("d%d" % i)) for i in range(self.NDMA)]
        self.dcnt = [0] * self.NDMA
        self.dnext = 0
        self.seen = {k: {} for k in self.eng}
        self.lastw = {}
        self.reads = {}
        self.n_instr = 0

    def _sem_of(self, key):
        return self.sem[key] if isinstance(key, str) else self.dsem[key[1]]

    def _wait(self, e, tok):
        key, val = tok
        if self.seen[e].get(key, 0) >= val:
            return
        self.eng[e].wait_ge(self._sem_of(key), val)
        self.seen[e][key] = val

    def _deps(self, e, reads, writes, pe_acc=False):
        best = {}
        for r in reads:
            t = self.lastw.get(r)
            if t is not None and best.get(t[0], 0) < t[1]:
                best[t[0]] = t[1]
        for w in writes:
            t = self.lastw.get(w)
            if t is not None and best.get(t[0], 0) < t[1]:
                if not (pe_acc and t[0] == "pe"):
                    best[t[0]] = t[1]
            for t in self.reads.get(w, ()):
                if best.get(t[0], 0) < t[1]:
                    best[t[0]] = t[1]
        for key, val in best.items():
            self._wait(e, (key, val))

    def _record(self, tok, reads, writes):
        for r in reads:
            lst = self.reads.setdefault(r, [])
            lst[:] = [t for t in lst if t[0] != tok[0]]
            lst.append(tok)
        for w in writes:
            self.lastw[w] = tok
            self.reads[w] = []

    def op(self, e, fn, reads=(), writes=(), pe_acc=False):
        self._deps(e, reads, writes, pe_acc)
        ins = fn(self.eng[e])
        self.cnt[e] += 1
        ins.then_inc(self.sem[e], 1)
        self._record((e, self.cnt[e]), reads, writes)
        self.n_instr += 1
        return ins

    def dma(self, q, out, in_, reads=(), writes=()):
        i = self.dnext
        self.dnext = (self.dnext + 1) % self.NDMA
        if self.dcnt[i] > 0:
            self._wait(q, (("d", i), self.dcnt[i]))
        self._deps(q, reads, writes)
        ins = self.eng[q].dma_start(out=out, in_=in_)
        self.dcnt[i] += 16
        ins.then_inc(self.dsem[i], 16)
        self._record((("d", i), self.dcnt[i]), reads, writes)
        self.n_instr += 1
        return ins

    def barrier(self):
        best = {}
        for k in self.cnt:
            if self.cnt[k]:
                best[k] = self.cnt[k]
        for i in range(self.NDMA):
            if self.dcnt[i]:
                best[("d", i)] = self.dcnt[i]
        for e in self.eng:
            for key, val in best.items():
                self._wait(e, (key, val))
        self.lastw = {}
        self.reads = {}


class B:
    def __init__(self, nc, es):
        self.nc, self.es = nc, es
        self.S = Sched(nc, es)
        self.D = {}
        self.psn = 0

    def din(self, name, shape, dt=F32):
        self.D[name] = self.nc.dram_tensor(name, list(shape), dt, kind="ExternalInput").ap()
        return self.D[name]

    def dout(self, name, shape, dt=F32):
        self.D[name] = self.nc.dram_tensor(name, list(shape), dt, kind="ExternalOutput").ap()
        return self.D[name]

    def dscr(self, name, shape, dt=F32):
        self.D[name] = self.nc.dram_tensor(name, list(shape), dt, kind="Internal").ap()
        return self.D[name]

    def sb(self, st, name, shape, dt=F32):
        self.uid = getattr(self, "uid", 0) + 1
        return st.enter_context(self.nc.sbuf_tensor("sb%d_%s" % (self.uid, name), list(shape), dt))

    def ps(self):
        i = self.psn % self.nrot
        self.psn += 1
        return self.psum[i], "ps%d" % i

    nrot = 6

    def acc(self, i):
        k = (6, 7, 4, 5)[i]
        return self.psum[k], "ps%d" % k


def build_program():
    nc = bass.Bass("TRN2", target_bir_lowering=False)
    es = ExitStack()
    b = B(nc, es)
    S = b.S
    D = b.D
    GR = {
        "s": dict(T=2048, nseq=1, L=2048, P=512, rope=True, cond=1),
        "p": dict(T=1024, nseq=4, L=256, P=0, rope=False, cond=0),
    }
    for g, G in GR.items():
        T = G["T"]
        b.din("xT_" + g, [DM, T])
        b.dout("yT_" + g, [DM, T])
        b.dscr("X1_" + g, [DM, T])
        for l in range(DEPTH - 1):
            b.dscr("X2_%d_%s" % (l, g), [DM, T])
        b.dscr("UT_" + g, [68 * 128, T])
        b.dscr("UTM_" + g, [T, UTM_COLS])
        b.dscr("OT_" + g, [4, 512, T], BF16)
        b.dscr("YP_" + g, [DM, T], BF16)
    b.din("cT", [128, 8, 2])
    b.din("w_ada", [DEPTH, DM, 6 * DM])
    b.din("b_adaT", [DEPTH, 128, 48])
    b.din("gains", [DEPTH, 128, 4, 8])
    b.din("w_in", [DEPTH, DM, D_IN])
    b.din("lbT", [DEPTH, 2, 512])
    b.din("hgrn_normT", [DEPTH, 128, 4])
    b.din("sink", [DEPTH, 1, 8])
    b.din("mla_qnT", [DEPTH, 128, 2])
    b.din("mla_kvn", [DEPTH, 128, 1])
    b.din("w_uq", [DEPTH, 256, 768])
    b.din("w_uq_sw", [DEPTH, 256, 768])
    b.din("w_ukv", [DEPTH, 128, 1024])
    b.din("gqa_qn", [DEPTH, 128, 2])
    b.din("gqa_kn", [DEPTH, 128, 2])
    b.din("w_branch", [DEPTH, 4, 512, DM])
    b.din("w_out", [DEPTH, DM, DM])
    b.din("w_mlp_in", [DEPTH, DM, 4 * DM])
    b.din("w_mlp_out", [DEPTH, 4 * DM, DM])
    b.din("st_hgrn", [DEPTH, 2, 4, 128, 128])
    b.din("c_swa_kT", [DEPTH, 2, 64, 512])
    b.din("c_swa_v", [DEPTH, 512, 128])
    b.din("c_ckvT", [DEPTH, 128, 512])
    b.din("c_krT", [DEPTH, 32, 512])
    b.din("c_gqa_kT", [DEPTH, 2, 64, 512])
    b.din("c_gqa_v", [DEPTH, 512, 128])
    b.din("rope64", [4, 128, 2048])
    b.din("cmat", [6, 128, 128])
    b.din("swamask", [2, 128, 512], BF16)
    b.dout("o_hgrn", [DEPTH, 4, 2, 4, 128, 128])
    b.dout("o_swa_kT", [DEPTH, 128, 1024])
    b.dout("o_swa_v", [DEPTH, 1024, 128])
    b.dout("o_ckvT", [DEPTH, 128, 1024])
    b.dout("o_krT", [DEPTH, 32, 1024])
    b.dout("o_gqa_kT", [DEPTH, 128, 1024])
    b.dout("o_gqa_v", [DEPTH, 1024, 128])

    b.psum = [es.enter_context(nc.psum_tensor("psb%d" % i, [128, 512], F32)) for i in range(8)]

    cst = ExitStack()
    es.enter_context(cst)
    ones_bf = b.sb(cst, "ones_bf", [128, 128], BF16)
    ones_f = b.sb(cst, "ones_f", [128, 128], F32)
    cm = b.sb(cst, "cm", [128, 6, 128], F32)
    modT = b.sb(cst, "modT", [128, DEPTH, 48, 2], F32)
    gains = b.sb(cst, "gains", [128, DEPTH, 4, 8], F32)
    coef = b.sb(cst, "coef", [128, DEPTH, 2, 6, 8], F32)
    S.op("pool", lambda e: e.memset(ones_bf[:], 1.0), writes=["ones_bf"])
    onesAB = b.sb(cst, "onesAB", [128, 2, 128], BF16)
    S.op("pool", lambda e: e.memset(onesAB[:], 0.0), writes=["onesAB"])
    S.op("pool", lambda e: e.memset(onesAB[:, 0, 0:64], 1.0), writes=["onesAB"])
    S.op("pool", lambda e: e.memset(onesAB[:, 1, 64:128], 1.0), writes=["onesAB"])
    S.op("pool", lambda e: e.memset(ones_f[:], 1.0), writes=["ones_f"])
    S.dma("sp", cm[:], D["cmat"].rearrange("a p c -> p a c"), writes=["cm"])
    S.dma("sp", gains[:], D["gains"].rearrange("l p a c -> p l a c"), writes=["gains"])

    with ExitStack() as st:
        cT = b.sb(st, "cT", [128, 8, 2])
        scT = b.sb(st, "scT", [128, 8, 2])
        badaT = b.sb(st, "badaT", [128, DEPTH, 48])
        wa = [b.sb(st, "wa%d" % i, [128, 8, 768]) for i in range(2)]
        S.dma("sp", cT[:], D["cT"], writes=["cT"])
        S.dma("sp", badaT[:], D["b_adaT"].rearrange("l p j -> p l j"), writes=["badaT"])
        S.op("act", lambda e: e.activation(scT[:], cT[:], AF.Silu), reads=["cT"], writes=["scT"])
        n = 0
        for l in range(DEPTH):
            pt, pn = b.ps()
            for cg in range(8):
                w = wa[n % 2]
                wn = "wa%d" % (n % 2)
                n += 1
                S.dma("sp" if cg % 2 == 0 else "act", w[:],
                      D["w_ada"][l, :, cg * 768:(cg + 1) * 768].rearrange("(kc p) c -> p kc c", p=128), writes=[wn])
                for jj in range(6):
                    j = cg * 6 + jj
                    for kc in range(8):
                        S.op("pe", lambda e: e.matmul(pt[:, 2 * j:2 * j + 2], w[:, kc, jj * 128:(jj + 1) * 128], scT[:, kc, :],
                                                      start=(kc == 0), stop=(kc == 7)),
                             reads=[wn, "scT"], writes=[pn], pe_acc=True)
            S.op("dve", lambda e: e.tensor_tensor(modT[:, l], pt[:, 0:96].rearrange("p (j c) -> p j c", c=2),
                                                  badaT[:, l].unsqueeze(2).to_broadcast([128, 48, 2]), ALU.add),
                 reads=[pn, "badaT"], writes=["modT"])
        for l in range(DEPTH):
            for c in range(2):
                m = lambda i: modT[:, l, i * 8:(i + 1) * 8, c]
                S.op("dve", lambda e: e.scalar_tensor_tensor(coef[:, l, c, 0], m(1), 1.0, gains[:, l, 0], ALU.add, ALU.mult),
                     reads=["modT", "gains"], writes=["coef"])
                S.op("dve", lambda e: e.tensor_copy(coef[:, l, c, 1], m(0)), reads=["modT"], writes=["coef"])
                S.op("dve", lambda e: e.tensor_tensor(coef[:, l, c, 2], m(2), gains[:, l, 1], ALU.mult), reads=["modT", "gains"], writes=["coef"])
                S.op("dve", lambda e: e.scalar_tensor_tensor(coef[:, l, c, 3], m(4), 1.0, gains[:, l, 2], ALU.add, ALU.mult),
                     reads=["modT", "gains"], writes=["coef"])
                S.op("dve", lambda e: e.tensor_copy(coef[:, l, c, 4], m(3)), reads=["modT"], writes=["coef"])
                S.op("dve", lambda e: e.tensor_tensor(coef[:, l, c, 5], m(5), gains[:, l, 3], ALU.mult), reads=["modT", "gains"], writes=["coef"])
        S.barrier()

    def rstd_from_sumsq(pt, pn, out, outn, npart, ncol, inv_n):
        S.op("act", lambda e: e.activation(out[0:npart, 0:ncol], pt[0:npart, 0:ncol], AF.Sqrt, bias=epsb[0:npart, :], scale=inv_n),
             reads=[pn, "epsb"], writes=[outn])
        S.op("dve", lambda e: e.reciprocal(out[0:npart, 0:ncol], out[0:npart, 0:ncol]), reads=[outn], writes=[outn])

    epsb = b.sb(cst, "epsb", [128, 1], F32)
    S.op("pool", lambda e: e.memset(epsb[:], EPS), writes=["epsb"])

    def norm_mod_tile(st_unused, xt, xn, hT, hn, t0, a_ap, sh_ap, tmp, sq, rs):
        for kc in range(8):
            S.op("act", lambda e: e.activation(sq[:, kc, :], xt[:, kc, :], AF.Square), reads=[xn], writes=["sq"])
        pt, pn = b.ps()
        for kc in range(8):
            S.op("pe", lambda e: e.matmul(pt[:], ones_bf[:], sq[:, kc, :], start=(kc == 0), stop=(kc == 7)),
                 reads=["sq", "ones_bf"], writes=[pn], pe_acc=True)
        rstd_from_sumsq(pt, pn, rs, "rs", 128, 512, 1.0 / DM)
        for kc in range(8):
            S.op("dve", lambda e: e.tensor_tensor(tmp[:], xt[:, kc, :], rs[:], ALU.mult), reads=[xn, "rs"], writes=["tmp"])
            S.op("act", lambda e: e.activation(hT[:, kc, t0:t0 + 512], tmp[:], AF.Identity, bias=sh_ap[:, kc:kc + 1], scale=a_ap[:, kc:kc + 1]),
                 reads=["tmp", "coef"], writes=[hn])

    def stage_h(st, g, l, Xsrc, ia, ish):
        G = GR[g]
        T = G["T"]
        hT = b.sb(st, "hT", [128, 8, T], BF16)
        with ExitStack() as s2:
            xts = [b.sb(s2, "xt%d" % i, [128, 8, 512]) for i in range(2)]
            tmp = b.sb(s2, "tmp", [128, 512])
            sq = b.sb(s2, "sq", [128, 8, 512], BF16)
            rs = b.sb(s2, "rs", [128, 512])
            for tt in range(T // 512):
                xt, xn = xts[tt % 2], "xt%d" % (tt % 2)
                S.dma("sp", xt[:], Xsrc[:, tt * 512:(tt + 1) * 512].rearrange("(kc p) t -> p kc t", p=128), writes=[xn])
                norm_mod_tile(None, xt, xn, hT, "hT", tt * 512, coef[:, l, G["cond"], ia], coef[:, l, G["cond"], ish], tmp, sq, rs)
            S.barrier()
        return hT

    def stage_proj(g, l, hT):
        G = GR[g]
        T = G["T"]
        UT, UTM = D["UT_" + g], D["UTM_" + g]
        tm_groups = {1: [(0, 512, 0)], 2: [(0, 512, 512)], 3: [(0, 512, 1024)], 6: [(128, 128, 1536)], 8: [(288, 128, 1664)]}
        with ExitStack() as st:
            wb = [b.sb(st, "wb%d" % i, [128, 8, 512], BF16) for i in range(2)]
            ev = [b.sb(st, "ev%d" % i, [128, 512]) for i in range(4)]
            nev = 0
            for cg in range(17):
                ncol = 512 if cg < 16 else D_IN - 8192
                w, wn = wb[cg % 2], "wb%d" % (cg % 2)
                S.dma("pool", w[:, :, 0:ncol], D["w_in"][l, :, cg * 512:cg * 512 + ncol].rearrange("(kc p) c -> p kc c", p=128), writes=[wn])
                for tt in range(T // 512):
                    for cc in range((ncol + 127) // 128):
                        m = min(128, ncol - cc * 128)
                        pt, pn = b.ps()
                        for kc in range(8):
                            S.op("pe", lambda e: e.matmul(pt[0:m, :], w[:, kc, cc * 128:cc * 128 + m], hT[:, kc, tt * 512:(tt + 1) * 512],
                                                          start=(kc == 0), stop=(kc == 7)), reads=[wn, "hT"], writes=[pn], pe_acc=True)
                        e_, en = ev[nev % 4], "ev%d" % (nev % 4)
                        eng = "act" if nev % 2 == 0 else "dve"
                        nev += 1
                        if eng == "act":
                            S.op("act", lambda e: e.copy(e_[0:m, :], pt[0:m, :]), reads=[pn], writes=[en])
                        else:
                            S.op("dve", lambda e: e.tensor_copy(e_[0:m, :], pt[0:m, :]), reads=[pn], writes=[en])
                        r0 = cg * 512 + cc * 128
                        S.dma("sp", UT[r0:r0 + m, tt * 512:(tt + 1) * 512], e_[0:m, :], reads=[en], writes=[])
                for (c0, cn, dst) in tm_groups.get(cg, []):
                    for t4 in range(T // 128):
                        pt, pn = b.ps()
                        for kc in range(8):
                            S.op("pe", lambda e: e.matmul(pt[:, 0:cn], hT[:, kc, t4 * 128:(t4 + 1) * 128], w[:, kc, c0:c0 + cn],
                                                          start=(kc == 0), stop=(kc == 7)), reads=[wn, "hT"], writes=[pn], pe_acc=True)
                        e_, en = ev[nev % 4], "ev%d" % (nev % 4)
                        eng = "act" if nev % 2 == 0 else "dve"
                        nev += 1
                        if eng == "act":
                            S.op("act", lambda e: e.copy(e_[:, 0:cn], pt[:, 0:cn]), reads=[pn], writes=[en])
                        else:
                            S.op("dve", lambda e: e.tensor_copy(e_[:, 0:cn], pt[:, 0:cn]), reads=[pn], writes=[en])
                        S.dma("sp", UTM[t4 * 128:(t4 + 1) * 128, dst:dst + cn], e_[:, 0:cn], reads=[en], writes=[])
            S.barrier()

    def attn_unit(qT, qn, Kd, Nq, chunks, scale, o_out, on, wk, sink=None):
        po, pon = b.acc(0)
        pd, pdn = b.acc(1)
        nch = len(chunks)

        def emit_st(i):
            kT, kn, V, vn, nk, mask = chunks[i]
            pst, psn_ = b.ps()
            S.op("pe", lambda e: e.matmul(pst[0:nk, 0:Nq], kT, qT, start=True, stop=True), reads=[kn] + (qn if isinstance(qn, list) else [qn]), writes=[psn_])
            return pst, psn_

        cur = emit_st(0)
        for i, (kT, kn, V, vn, nk, mask) in enumerate(chunks):
            nxt = emit_st(i + 1) if i + 1 < nch else None
            pst, psn_ = cur
            E, En = wk["E"][i % 3], "E%d" % (i % 3)
            S.op("act", lambda e: e.activation(E[0:nk, 0:Nq], pst[0:nk, 0:Nq], AF.Exp, scale=scale), reads=[psn_], writes=[En])
            if mask is not None:
                S.op("pool", lambda e: e.tensor_tensor(E[0:nk, 0:Nq], E[0:nk, 0:Nq], mask, ALU.mult), reads=[En, "swamask"], writes=[En])
            last = (i == nch - 1) and sink is None
            S.op("pe", lambda e: e.matmul(po[0:64, 0:Nq], V, E[0:nk, 0:Nq], start=(i == 0), stop=(i == nch - 1)),
                 reads=[vn, En], writes=[pon], pe_acc=(i > 0))
            S.op("pe", lambda e: e.matmul(pd[0:64, 0:Nq], ones_bf[0:nk, 0:64], E[0:nk, 0:Nq], start=(i == 0), stop=last),
                 reads=["ones_bf", En], writes=[pdn], pe_acc=(i > 0))
            cur = nxt
        if sink is not None:
            S.op("pe", lambda e: e.matmul(pd[0:64, 0:Nq], ones_bf[0:1, 0:64], sink, start=False, stop=True),
                 reads=["ones_bf", "sinkrow"], writes=[pdn], pe_acc=True)
        rec = wk["rec"]
        S.op("dve", lambda e: e.reciprocal(rec[0:64, 0:Nq], pd[0:64, 0:Nq]), reads=[pdn], writes=["rec"])
        S.op("dve", lambda e: e.tensor_tensor(o_out, po[0:64, 0:Nq], rec[0:64, 0:Nq], ALU.mult), reads=[pon, "rec"], writes=[on])

    def attn_pair(qa, qan, qb, qbn, Nq, chunks, scale, o_out, on, wk, sink=None):
        po, pon = b.acc(0)
        pd, pdn = b.acc(1)
        nch = len(chunks)
        E = wk["E"]
        ne = len(E)

        def emit_st(i):
            kTa, kTb, kn, Va, Vb, vn, nk, mask = chunks[i]
            p1, n1 = b.ps()
            kna, knb = kn if isinstance(kn, tuple) else (kn, kn)
            S.op("pe", lambda e: e.matmul(p1[0:nk, 0:Nq], kTa, qa, start=True, stop=True), reads=[kna] + qan, writes=[n1])
            p2, n2 = b.ps()
            S.op("pe", lambda e: e.matmul(p2[0:nk, 0:Nq], kTb, qb, start=True, stop=True), reads=[knb] + qbn, writes=[n2])
            return (p1, n1, p2, n2)

        cur = emit_st(0)
        for i, (kTa, kTb, kn, Va, Vb, vn, nk, mask) in enumerate(chunks):
            nxt = emit_st(i + 1) if i + 1 < nch else None
            p1, n1, p2, n2 = cur
            Ea, Ean = E[(2 * i) % ne], "E%d" % ((2 * i) % ne)
            Eb, Ebn = E[(2 * i + 1) % ne], "E%d" % ((2 * i + 1) % ne)
            S.op("act", lambda e: e.activation(Ea[0:nk, 0:Nq], p1[0:nk, 0:Nq], AF.Exp, scale=scale), reads=[n1], writes=[Ean])
            S.op("act", lambda e: e.activation(Eb[0:nk, 0:Nq], p2[0:nk, 0:Nq], AF.Exp, scale=scale), reads=[n2], writes=[Ebn])
            if mask is not None:
                S.op("pool", lambda e: e.tensor_tensor(Ea[0:nk, 0:Nq], Ea[0:nk, 0:Nq], mask, ALU.mult), reads=[Ean, "swamask"], writes=[Ean])
                S.op("dve", lambda e: e.tensor_tensor(Eb[0:nk, 0:Nq], Eb[0:nk, 0:Nq], mask, ALU.mult), reads=[Ebn, "swamask"], writes=[Ebn])
            last = (i == nch - 1)
            S.op("pe", lambda e: e.matmul(po[:, 0:Nq], Va, Ea[0:nk, 0:Nq], start=(i == 0), stop=False), reads=[vn, Ean], writes=[pon], pe_acc=(i > 0))
            S.op("pe", lambda e: e.matmul(po[:, 0:Nq], Vb, Eb[0:nk, 0:Nq], start=False, stop=last), reads=[vn, Ebn], writes=[pon], pe_acc=True)
            S.op("pe", lambda e: e.matmul(pd[:, 0:Nq], onesAB[0:nk, 0, :], Ea[0:nk, 0:Nq], start=(i == 0), stop=False), reads=["onesAB", Ean], writes=[pdn], pe_acc=(i > 0))
            S.op("pe", lambda e: e.matmul(pd[:, 0:Nq], onesAB[0:nk, 1, :], Eb[0:nk, 0:Nq], start=False, stop=(last and sink is None)), reads=["onesAB", Ebn], writes=[pdn], pe_acc=True)
            cur = nxt
        if sink is not None:
            sa, sb_ = sink
            S.op("pe", lambda e: e.matmul(pd[:, 0:Nq], onesAB[0:1, 0, :], sa, start=False, stop=False), reads=["onesAB", "sinkrow"], writes=[pdn], pe_acc=True)
            S.op("pe", lambda e: e.matmul(pd[:, 0:Nq], onesAB[0:1, 1, :], sb_, start=False, stop=True), reads=["onesAB", "sinkrow"], writes=[pdn], pe_acc=True)
        rec = wk["rec"]
        S.op("dve", lambda e: e.reciprocal(rec[:, 0:Nq], pd[:, 0:Nq]), reads=[pdn], writes=["rec"])
        S.op("dve", lambda e: e.tensor_tensor(o_out, po[:, 0:Nq], rec[:, 0:Nq], ALU.mult), reads=[pon, "rec"], writes=[on])

    def rr_alloc(st, T, do_rope):
        sets = []
        for k in range(2):
            sets.append(dict(k=k, x=b.sb(st, "rr_x%d" % k, [128, T]), xs=(b.sb(st, "rr_xs%d" % k, [128, T]) if do_rope else None),
                             sq=b.sb(st, "rr_sq%d" % k, [128, 512], BF16), rs=b.sb(st, "rr_rs%d" % k, [128, T]), t1=b.sb(st, "rr_t1%d" % k, [128, 512])))
        return sets

    def rms_rope_rows(W_, p0, src_rows_fn, n_rows, T, gain2, gainn, do_norm, do_rope, rope_idx, out_bf, outn, out32=None, out32n=None,
                      out32_pre_rope=False):
        k = W_["k"]
        x, xs, sqb, rs, t1 = W_["x"], W_["xs"], W_["sq"], W_["rs"], W_["t1"]
        xn, xsn, sqn, rsn, t1n = ("rr_x%d" % k, "rr_xs%d" % k, "rr_sq%d" % k, "rr_rs%d" % k, "rr_t1%d" % k)
        for (r0, nr, ap) in src_rows_fn(False):
            S.dma("sp", x[p0 + r0:p0 + r0 + nr, 0:T], ap, writes=[xn])
        if do_rope:
            for (r0, nr, ap) in src_rows_fn(True):
                S.dma("act", xs[p0 + r0:p0 + r0 + nr, 0:T], ap, writes=[xsn])
        nr = n_rows
        pr = slice(p0, p0 + nr)
        W = min(512, T)
        for c in range(T // W):
            cs = slice(c * W, (c + 1) * W)
            if do_norm:
                S.op("act", lambda e: e.activation(sqb[pr, 0:W], x[pr, cs], AF.Square), reads=[xn], writes=[sqn])
                pt, pn = b.ps()
                S.op("pe", lambda e: e.matmul(pt[:, 0:W], ones_bf[pr, :], sqb[pr, 0:W], start=True, stop=True),
                     reads=[sqn, "ones_bf"], writes=[pn])
                S.op("act", lambda e: e.activation(rs[pr, cs], pt[pr, 0:W], AF.Sqrt, bias=epsb[pr, :], scale=1.0 / nr), reads=[pn, "epsb"], writes=[rsn])
                S.op("dve", lambda e: e.reciprocal(rs[pr, cs], rs[pr, cs]), reads=[rsn], writes=[rsn])
                S.op("dve", lambda e: e.scalar_tensor_tensor(x[pr, cs], x[pr, cs], gain2[pr, 0:1], rs[pr, cs], ALU.mult, ALU.mult),
                     reads=[xn, rsn, gainn], writes=[xn])
                if do_rope:
                    S.op("dve", lambda e: e.scalar_tensor_tensor(xs[pr, cs], xs[pr, cs], gain2[pr, 1:2], rs[pr, cs], ALU.mult, ALU.mult),
                         reads=[xsn, rsn, gainn], writes=[xsn])
            if out32 is not None and out32_pre_rope:
                S.op("pool", lambda e: e.tensor_copy(out32[pr, cs], x[pr, cs]), reads=[xn], writes=[out32n])
            if do_rope:
                S.op("dve", lambda e: e.tensor_tensor(x[pr, cs], x[pr, cs], rope[pr, rope_idx, cs], ALU.mult), reads=[xn, "rope"], writes=[xn])
                S.op("pool", lambda e: e.tensor_tensor(t1[pr, 0:W], xs[pr, cs], rope[pr, rope_idx + 1, cs], ALU.mult), reads=[xsn, "rope"], writes=[t1n])
                S.op("dve", lambda e: e.tensor_tensor(out_bf[:, cs], x[pr, cs], t1[pr, 0:W], ALU.add), reads=[xn, t1n], writes=[outn])
            else:
                S.op("act", lambda e: e.copy(out_bf[:, cs], x[pr, cs]), reads=[xn], writes=[outn])

    def swap_rows(base, T0, T1, UT, nd):
        q = nd // 4
        def f(swapped):
            if not swapped:
                return [(0, nd, UT[base:base + nd, T0:T1])]
            return [(0, q, UT[base + q:base + 2 * q, T0:T1]), (q, q, UT[base:base + q, T0:T1]),
                    (2 * q, q, UT[base + 3 * q:base + 4 * q, T0:T1]), (3 * q, q, UT[base + 2 * q:base + 3 * q, T0:T1])]
        return f

    rope = b.sb(cst, "rope", [128, 4, 2048], BF16)
    S.dma("pool", rope[:], D["rope64"].rearrange("a p t -> p a t"), writes=["rope"])

    def stage_gqa_like(g, l, kind):
        G = GR[g]
        T, L, P, nseq, do_rope = G["T"], G["L"], G["P"], G["nseq"], G["rope"]
        UT, UTM, OT = D["UT_" + g], D["UTM_" + g], D["OT_" + g]
        qo, ko = (OFF["sq"], OFF["sk"]) if kind == "swa" else (OFF["gq"], OFF["gk"])
        vcol = 1536 if kind == "swa" else 1664
        bi = 1 if kind == "swa" else 3
        do_norm = kind == "gqa"
        scale = 64 ** -0.5
        nkc_ctx = P // 128
        with ExitStack() as st:
            kT = b.sb(st, "kT", [128, P + T], BF16)
            Vt = b.sb(st, "Vt", [128, (P + T) // 128, 2, 128], BF16)
            qTh = b.sb(st, "qTh", [128, 8, T], BF16)
            gq2 = b.sb(st, "gq2", [128, 2]); gk2 = b.sb(st, "gk2", [128, 2])
            sinkrow = b.sb(st, "sinkrow", [1, 8, 128], BF16)
            sk32 = b.sb(st, "sk32", [1, 8])
            wk = dict(E=[b.sb(st, "E%d" % i, [128, 512], BF16) for i in range(6)], rec=b.sb(st, "rec", [128, 512]))
            obs = [b.sb(st, "ob%d" % i, [128, 512], BF16) for i in range(2)]
            swm = b.sb(st, "swamask", [128, 2, 512], BF16)
            S.dma("sp", swm[:], D["swamask"].rearrange("a p c -> p a c"), writes=["swamask"])
            S.op("pool", lambda e: e.memset(qTh[64:128, 0:4, :], 0.0), writes=["qTh%d" % hh for hh in range(4)])
            S.op("pool", lambda e: e.memset(qTh[0:64, 4:8, :], 0.0), writes=["qTh%d" % hh for hh in range(4, 8)])
            S.op("pool", lambda e: e.memset(Vt[:], 0.0), writes=["Vt"])
            if do_norm:
                S.dma("sp", gq2[:], D["gqa_qn"][l], writes=["gq2"])
                S.dma("sp", gk2[:], D["gqa_kn"][l], writes=["gk2"])
            else:
                S.dma("sp", sk32[:], D["sink"][l], writes=["sk32"])
                S.op("act", lambda e: e.activation(sk32[:], sk32[:], AF.Exp), reads=["sk32"], writes=["sk32"])
                S.op("dve", lambda e: e.tensor_copy(sinkrow[:], sk32[:].unsqueeze(2).to_broadcast([1, 8, 128])), reads=["sk32"], writes=["sinkrow"])
            with ExitStack() as s2:
                RR = rr_alloc(s2, T, do_rope)
                k32 = b.sb(s2, "k32", [128, T]) if g == "p" else None
                v32 = b.sb(s2, "v32", [128, (P + T) // 128, 128])
                if P:
                    src_k = D["c_swa_kT"] if kind == "swa" else D["c_gqa_kT"]
                    src_v = D["c_swa_v"] if kind == "swa" else D["c_gqa_v"]
                    S.dma("pool", kT[:, 0:P], src_k[l].rearrange("h d t -> (h d) t"), writes=["kT"])
                    S.dma("sp", v32[:, 0:P // 128, :], src_v[l].rearrange("(c p) f -> p c f", p=128), writes=["v32"])
                S.dma("sp", v32[:, P // 128:, :], UTM[:, vcol:vcol + 128].rearrange("(c p) f -> p c f", p=128), writes=["v32"])
                S.op("dve", lambda e: e.tensor_copy(Vt[:, :, 0, 0:64], v32[:, :, 0:64]), reads=["v32"], writes=["Vt"])
                S.op("dve", lambda e: e.tensor_copy(Vt[:, :, 1, 64:128], v32[:, :, 64:128]), reads=["v32"], writes=["Vt"])
                if g == "p":
                    dst = D["o_swa_v"] if kind == "swa" else D["o_gqa_v"]
                    S.dma("act", dst[l].rearrange("(c p) f -> p c f", p=128), v32[:], reads=["v32"], writes=[])
                nrr = 0
                for kvh in range(2):
                    p0 = kvh * 64
                    rms_rope_rows(RR[nrr % 2], p0, swap_rows(ko + kvh * 64, 0, T, UT, 64), 64, T, gk2, "gk2", do_norm, do_rope, 0,
                                  kT[p0:p0 + 64, P:P + T], "kT", out32=k32, out32n="k32", out32_pre_rope=True)
                    nrr += 1
                if g == "p":
                    dst = D["o_swa_kT"] if kind == "swa" else D["o_gqa_kT"]
                    S.dma("sp", dst[l], k32[:], reads=["k32"], writes=[])
                for h in range(8):
                    p0 = (h // 4) * 64
                    rms_rope_rows(RR[nrr % 2], p0, swap_rows(qo + h * 64, 0, T, UT, 64), 64, T, gq2, "gq2", do_norm, do_rope, 0,
                                  qTh[p0:p0 + 64, h, :], "qTh%d" % h)
                    nrr += 1
                nu = 0
                qna = ["qTh%d" % hh for hh in range(4)]
                qnb = ["qTh%d" % hh for hh in range(4, 8)]
                for sq_ in range(nseq):
                    T0 = sq_ * L
                    kb0 = P + T0
                    for qb in range(L // 128):
                        cols = [(c * 128, None) for c in range(nkc_ctx)]
                        if kind == "swa" and P:
                            cols += [(kb0 + kb * 128, mi) for (kb, mi) in ((qb - 1, 0), (qb, None), (qb + 1, 1)) if 0 <= kb < L // 128]
                        else:
                            cols += [(kb0 + kb * 128, None) for kb in range(L // 128)]
                        chunks = [(kT[:, c0:c0 + 128], kT[:, c0:c0 + 128], "kT", Vt[:, c0 // 128, 0, :], Vt[:, c0 // 128, 1, :], "Vt", 128,
                                   None if mi is None else swm[:, mi, :]) for (c0, mi) in cols]
                        ob, obn = obs[nu % 2], "ob%d" % (nu % 2)
                        nu += 1
                        q0 = T0 + qb * 128
                        attn_pair(qTh[:, 0:4, q0:q0 + 128], qna, qTh[:, 4:8, q0:q0 + 128], qnb, 512, chunks, scale, ob[:], obn, wk,
                                  sink=((sinkrow[0:1, 0:4, :], sinkrow[0:1, 4:8, :]) if kind == "swa" else None))
                        for kvh in range(2):
                            S.dma("sp" if kvh == 0 else "act", OT[bi, kvh * 256:(kvh + 1) * 256, q0:q0 + 128].rearrange("(h d) t -> d h t", d=64),
                                  ob[kvh * 64:(kvh + 1) * 64, :].rearrange("d (h t) -> d h t", h=4), reads=[obn], writes=[])
                S.barrier()

    def stage_mla(g, l):
        G = GR[g]
        T, L, P, nseq, do_rope = G["T"], G["L"], G["P"], G["nseq"], G["rope"]
        UT, OT = D["UT_" + g], D["OT_" + g]
        scale = 96 ** -0.5
        NK = P + L
        for sq_ in range(nseq):
            T0, T1 = sq_ * L, (sq_ + 1) * L
            with ExitStack() as st:
                ckvT = b.sb(st, "ckvT", [128, NK], BF16)
                krT = b.sb(st, "krT", [96, NK], BF16)
                cqn = b.sb(st, "cqn", [128, 2, L], BF16)
                wuq = b.sb(st, "wuq", [128, 2, 768], BF16); wuqs = b.sb(st, "wuqs", [128, 2, 768], BF16)
                wukv = b.sb(st, "wukv", [128, 1024], BF16)
                g_q = b.sb(st, "g_q", [128, 2]); g_kv = b.sb(st, "g_kv", [128, 1])
                S.dma("pool", wuq[:], D["w_uq"][l].rearrange("(kc p) c -> p kc c", p=128), writes=["wuq"])
                S.dma("pool", wuqs[:], D["w_uq_sw"][l].rearrange("(kc p) c -> p kc c", p=128), writes=["wuqs"])
                S.dma("pool", wukv[:], D["w_ukv"][l], writes=["wukv"])
                S.dma("sp", g_q[:], D["mla_qnT"][l], writes=["g_q"])
                S.dma("sp", g_kv[:], D["mla_kvn"][l], writes=["g_kv"])
                with ExitStack() as s2:
                    x = b.sb(s2, "m_x", [128, 2, L]); sqb = b.sb(s2, "m_sq", [128, 2, 512], BF16); rs = b.sb(s2, "m_rs", [128, 512])
                    xk = b.sb(s2, "m_xk", [128, L]); kr32 = b.sb(s2, "m_kr", [96, L]); krs = b.sb(s2, "m_krs", [96, L]); t1 = b.sb(s2, "m_t1", [96, 512])
                    if P:
                        S.dma("pool", ckvT[:, 0:P], D["c_ckvT"][l], writes=["ckvT"])
                        S.dma("pool", krT[64:96, 0:P], D["c_krT"][l], writes=["krT"])
                    S.dma("sp", x[:], UT[OFF["cq"]:OFF["cq"] + 256, T0:T1].rearrange("(kc p) t -> p kc t", p=128), writes=["m_x"])
                    S.dma("sp", xk[:], UT[OFF["ckv"]:OFF["ckv"] + 128, T0:T1], writes=["m_xk"])
                    S.dma("sp", kr32[64:96, :], UT[OFF["kr"]:OFF["kr"] + 32, T0:T1], writes=["m_kr"])
                    if do_rope:
                        for (r0, nr, ap) in swap_rows(OFF["kr"], T0, T1, UT, 32)(True):
                            S.dma("act", krs[64 + r0:64 + r0 + nr, :], ap, writes=["m_krs"])
                    for c in range(L // 512 if L >= 512 else 1):
                        w = min(512, L)
                        cs = slice(c * w, (c + 1) * w)
                        pt, pn = b.ps()
                        for kc in range(2):
                            S.op("act", lambda e: e.activation(sqb[:, kc, 0:w], x[:, kc, cs], AF.Square), reads=["m_x"], writes=["m_sq"])
                        for kc in range(2):
                            S.op("pe", lambda e: e.matmul(pt[:, 0:w], ones_bf[:], sqb[:, kc, 0:w], start=(kc == 0), stop=(kc == 1)),
                                 reads=["m_sq", "ones_bf"], writes=[pn], pe_acc=True)
                        rstd_from_sumsq(pt, pn, rs, "m_rs", 128, w, 1.0 / 256)
                        for kc in range(2):
                            S.op("dve", lambda e: e.scalar_tensor_tensor(cqn[:, kc, cs], x[:, kc, cs], g_q[:, kc:kc + 1], rs[:, 0:w], ALU.mult, ALU.mult),
                                 reads=["m_x", "m_rs", "g_q"], writes=["cqn"])
                        pt, pn = b.ps()
                        S.op("act", lambda e: e.activation(sqb[:, 0, 0:w], xk[:, cs], AF.Square), reads=["m_xk"], writes=["m_sq"])
                        S.op("pe", lambda e: e.matmul(pt[:, 0:w], ones_bf[:], sqb[:, 0, 0:w], start=True, stop=True), reads=["m_sq", "ones_bf"], writes=[pn])
                        rstd_from_sumsq(pt, pn, rs, "m_rs", 128, w, 1.0 / 128)
                        S.op("dve", lambda e: e.scalar_tensor_tensor(xk[:, cs], xk[:, cs], g_kv[:, 0:1], rs[:, 0:w], ALU.mult, ALU.mult),
                             reads=["m_xk", "m_rs", "g_kv"], writes=["m_xk"])
                        S.op("pool", lambda e: e.tensor_copy(ckvT[:, P + c * w:P + (c + 1) * w], xk[:, cs]), reads=["m_xk"], writes=["ckvT"])
                        if do_rope:
                            S.op("dve", lambda e: e.tensor_tensor(t1[64:96, 0:w], kr32[64:96, cs], rope[64:96, 2, cs], ALU.mult), reads=["m_kr", "rope"], writes=["m_t1"])
                            S.op("pool", lambda e: e.tensor_tensor(krs[64:96, cs], krs[64:96, cs], rope[64:96, 3, cs], ALU.mult), reads=["m_krs", "rope"], writes=["m_krs"])
                            S.op("dve", lambda e: e.tensor_tensor(krT[64:96, P + c * w:P + (c + 1) * w], t1[64:96, 0:w], krs[64:96, cs], ALU.add),
                                 reads=["m_t1", "m_krs"], writes=["krT"])
                        else:
                            S.op("dve", lambda e: e.tensor_copy(krT[64:96, P + c * w:P + (c + 1) * w], kr32[64:96, cs]), reads=["m_kr"], writes=["krT"])
                    if g == "p":
                        S.dma("sp", D["o_ckvT"][l, :, T0:T1], xk[:], reads=["m_xk"], writes=[])
                        S.dma("sp", D["o_krT"][l, :, T0:T1], kr32[64:96, :], reads=["m_kr"], writes=[])
                    S.barrier()
                Vall = b.sb(st, "Vall", [128, NK // 128, 8, 128], BF16)
                S.op("pool", lambda e: e.memset(Vall[:], 0.0), writes=["Vall"])
                for c in range(NK // 128):
                    pt, pn = b.ps()
                    S.op("pe", lambda e: e.matmul(pt[:, :], ckvT[:, c * 128:(c + 1) * 128],
                                                  wukv[:].rearrange("p (h x) -> p h x", x=128)[:, :, 64:128], start=True, stop=True),
                         reads=["ckvT", "wukv"], writes=[pn])
                    pv = pt[:, :].rearrange("p (h two d) -> p h two d", two=2, d=64)
                    S.op("act", lambda e: e.copy(Vall[:, c, 0::2, 0:64], pv[:, :, 0, :]), reads=[pn], writes=["Vall"])
                    S.op("dve", lambda e: e.tensor_copy(Vall[:, c, 1::2, 64:128], pv[:, :, 1, :]), reads=[pn], writes=["Vall"])
                S.barrier()
                with ExitStack() as s2:
                    wk = dict(E=[b.sb(s2, "E%d" % i, [128, 512], BF16) for i in range(6)], rec=b.sb(s2, "rec", [128, 512]))
                    kTh = [b.sb(s2, "kTh%d" % i, [128, NK], BF16) for i in range(4)]
                    qh = [b.sb(s2, "qh%d" % i, [128, L], BF16) for i in range(4)]
                    obs = [b.sb(s2, "ob%d" % i, [128, 512], BF16) for i in range(2)]
                    t2 = b.sb(s2, "m_t2", [96, 512]); t3 = b.sb(s2, "m_t3", [96, 512])
                    for i in range(4):
                        S.op("pool", lambda e: e.memset(kTh[i][96:128, :], 0.0), writes=["kTh%d" % i])
                        S.op("pool", lambda e: e.memset(qh[i][96:128, :], 0.0), writes=["qh%d" % i])
                    nu = 0
                    W = min(512, L)
                    for hp in range(4):
                        bufs = []
                        for h2 in range(2):
                            h = hp * 2 + h2
                            bi_ = (hp % 2) * 2 + h2
                            kt, ktn = kTh[bi_], "kTh%d" % bi_
                            q_, q_n = qh[bi_], "qh%d" % bi_
                            bufs.append((kt, ktn, q_, q_n))
                            for c in range((NK + 511) // 512):
                                w = min(512, NK - c * 512)
                                pt, pn = b.ps()
                                S.op("pe", lambda e: e.matmul(pt[:, 0:w], wukv[:, h * 128:(h + 1) * 128], ckvT[:, c * 512:c * 512 + w], start=True, stop=True),
                                     reads=["wukv", "ckvT"], writes=[pn])
                                S.op("act", lambda e: e.copy(kt[0:64, c * 512:c * 512 + w], pt[0:64, 0:w]), reads=[pn], writes=[ktn])
                            S.op("pool", lambda e: e.tensor_copy(kt[64:96, :], krT[64:96, :]), reads=["krT"], writes=[ktn])
                            for c in range(L // W):
                                cs = slice(c * W, (c + 1) * W)
                                pt, pn = b.ps()
                                for kc in range(2):
                                    S.op("pe", lambda e: e.matmul(pt[0:96, 0:W], wuq[:, kc, h * 96:(h + 1) * 96], cqn[:, kc, cs], start=(kc == 0), stop=(kc == 1)),
                                         reads=["wuq", "cqn"], writes=[pn], pe_acc=(kc > 0))
                                S.op("act", lambda e: e.copy(q_[0:64, cs], pt[0:64, 0:W]), reads=[pn], writes=[q_n])
                                if do_rope:
                                    pt2, pn2 = b.ps()
                                    for kc in range(2):
                                        S.op("pe", lambda e: e.matmul(pt2[0:96, 0:W], wuqs[:, kc, h * 96:(h + 1) * 96], cqn[:, kc, cs], start=(kc == 0), stop=(kc == 1)),
                                             reads=["wuqs", "cqn"], writes=[pn2], pe_acc=(kc > 0))
                                    S.op("dve", lambda e: e.tensor_tensor(t2[64:96, 0:W], pt[64:96, 0:W], rope[64:96, 2, cs], ALU.mult), reads=[pn, "rope"], writes=["m_t2"])
                                    S.op("dve", lambda e: e.tensor_tensor(t3[64:96, 0:W], pt2[64:96, 0:W], rope[64:96, 3, cs], ALU.mult), reads=[pn2, "rope"], writes=["m_t3"])
                                    S.op("pool", lambda e: e.tensor_tensor(q_[64:96, cs], t2[64:96, 0:W], t3[64:96, 0:W], ALU.add), reads=["m_t2", "m_t3"], writes=[q_n])
                                else:
                                    S.op("dve", lambda e: e.tensor_copy(q_[64:96, cs], pt[64:96, 0:W]), reads=[pn], writes=[q_n])
                        (kta, ktan, qa, qan), (ktb, ktbn, qb_, qbn) = bufs
                        for c in range(L // W):
                            chunks = [(kta[:, kc * 128:(kc + 1) * 128], ktb[:, kc * 128:(kc + 1) * 128], (ktan, ktbn), Vall[:, kc, hp * 2, :], Vall[:, kc, hp * 2 + 1, :], "Vall", 128, None)
                                      for kc in range(NK // 128)]
                            ob, obn = obs[nu % 2], "ob%d" % (nu % 2)
                            nu += 1
                            attn_pair(qa[:, c * W:(c + 1) * W], [qan], qb_[:, c * W:(c + 1) * W], [qbn], W, chunks, scale, ob[:, 0:W], obn, wk)
                            S.dma("sp", OT[2, hp * 128:(hp + 1) * 128, T0 + c * W:T0 + (c + 1) * W], ob[:, 0:W], reads=[obn], writes=[])
                    S.barrier()

    def stage_hgrn(g, l):
        G = GR[g]
        T, L, P, nseq = G["T"], G["L"], G["P"], G["nseq"]
        UT, UTM, OT = D["UT_" + g], D["UTM_" + g], D["OT_" + g]
        NTL = L // 128
        ident, triU, triL, sL, sU, csel = (cm[:, i, :] for i in range(6))
        with ExitStack() as st:
            lbb = b.sb(st, "lbb", [128, 2, 512]); oml = b.sb(st, "oml", [128, 2, 512])
            with ExitStack() as s2:
                lbr = b.sb(s2, "lbr", [128, DEPTH, 2, 512]); den = b.sb(s2, "lden", [128, 2, 512])
                S.dma("sp", lbr[:], D["lbT"].partition_broadcast(128), writes=["lbr"])
                S.op("act", lambda e: e.activation(lbr[:], lbr[:], AF.Exp), reads=["lbr"], writes=["lbr"])
                S.op("dve", lambda e: e.tensor_tensor(den[:], lbr[:, 0], lbr[:, 1], ALU.add), reads=["lbr"], writes=["lden"])
                S.op("dve", lambda e: e.reciprocal(den[:], den[:]), reads=["lden"], writes=["lden"])
                if l == 0:
                    S.op("pool", lambda e: e.memset(lbb[:], 0.0), writes=["lbb"])
                else:
                    S.op("dve", lambda e: e.tensor_tensor(lbb[:], lbr[:, 1], den[:], ALU.mult), reads=["lbr", "lden"], writes=["lbb"])
                S.op("dve", lambda e: e.tensor_scalar(oml[:], lbb[:], -1.0, 1.0, ALU.mult, ALU.add), reads=["lbb"], writes=["oml"])
                S.barrier()
            hn = b.sb(st, "hn", [128, 4]); S.dma("sp", hn[:], D["hgrn_normT"][l], writes=["hn"])
            Sst = b.sb(st, "Sst", [128, 2, 4, 128])
            SSTN = ["Sst%d%d" % (d_, h_) for d_ in range(2) for h_ in range(4)]
            oall = b.sb(st, "oall", [128, 2, 4, L], BF16)
            W_ = {}
            for d in range(2):
                for k in range(2):
                    sfx = "%d%d" % (d, k)
                    W_[d, k] = dict(
                        sfx=sfx,
                        tt=b.sb(st, "h_t" + sfx, [128, 512]), vv=b.sb(st, "h_v" + sfx, [128, 512]), gg=b.sb(st, "h_g" + sfx, [128, 512]),
                        kk=b.sb(st, "h_k" + sfx, [128, 512]), kt=b.sb(st, "h_kt" + sfx, [128, 512]), kh=b.sb(st, "h_kh" + sfx, [128, 512]),
                        khm=b.sb(st, "h_khm" + sfx, [128, 4, 4, 128]), qT=b.sb(st, "h_qT" + sfx, [128, 4, 128]), qt=b.sb(st, "h_qt" + sfx, [128, 4, 128]),
                        eb=b.sb(st, "h_eb" + sfx, [128, 4, 132]), ktT=b.sb(st, "h_ktT" + sfx, [128, 4, 128]), AT=b.sb(st, "h_AT" + sfx, [128, 4, 128]))
            fin = dict(os=b.sb(st, "f_os", [128, 4, 256]), gT=b.sb(st, "f_gT", [128, 4, 256]), sq=b.sb(st, "f_sq", [128, 4, 256], BF16),
                       rs=b.sb(st, "f_rs", [128, 4, 256]), ob=b.sb(st, "f_ob", [128, 4, 256], BF16))
            it = 0
            b.nrot = 4
            tmpo = [b.sb(st, "h_tmpo%d" % d_, [128, 512]) for d_ in range(2)]
            for sq_ in range(nseq):
                T0 = sq_ * L
                if P:
                    S.dma("sp", Sst[:], D["st_hgrn"][l].rearrange("d h k v -> k d h v"), writes=SSTN)
                else:
                    S.op("pool", lambda e: e.memset(Sst[:], 0.0), writes=SSTN)
                for i in range(NTL):
                    k = it % 2
                    it += 1
                    tis = (i, NTL - 1 - i)
                    pos = [b.acc(0), b.acc(1)]
                    pis = [b.acc(2), b.acc(3)]
                    for d in range(2):
                        w = W_[d, k]
                        x = w["sfx"]
                        t0 = T0 + tis[d] * 128
                        M_in, M_ex = (triU, sL) if d == 0 else (triL, sU)
                        tt_, vv, gg, kk, kt_, kh, khm, qT, qt, eb, ktT, AT = (w[n_] for n_ in ("tt", "vv", "gg", "kk", "kt", "kh", "khm", "qT", "qt", "eb", "ktT", "AT"))
                        S.dma("sp", tt_[:], UTM[t0:t0 + 128, d * 512:(d + 1) * 512], writes=["h_t" + x])
                        S.dma("sp", vv[:], UTM[t0:t0 + 128, 1024:1536], writes=["h_v" + x])
                        S.dma("act", qT[:], UT[0:512, t0:t0 + 128].rearrange("(h p) t -> p h t", p=128), writes=["h_qT" + x])
                        S.op("act", lambda e: e.activation(qT[:], qT[:], AF.Silu), reads=["h_qT" + x], writes=["h_qT" + x])
                        S.op("act", lambda e: e.activation(tt_[:], tt_[:], AF.Sigmoid), reads=["h_t" + x], writes=["h_t" + x])
                        S.op("dve", lambda e: e.tensor_tensor(tt_[:], tt_[:], oml[:, d], ALU.mult), reads=["h_t" + x, "oml"], writes=["h_t" + x])
                        S.op("dve", lambda e: e.scalar_tensor_tensor(gg[:], tt_[:], 1e-30, lbb[:, d], ALU.max, ALU.add), reads=["h_t" + x, "lbb"], writes=["h_g" + x])
                        S.op("act", lambda e: e.activation(gg[:], gg[:], AF.Ln), reads=["h_g" + x], writes=["h_g" + x])
                        S.op("dve", lambda e: e.tensor_tensor(kk[:], oml[:, d], tt_[:], ALU.subtract), reads=["h_t" + x, "oml"], writes=["h_k" + x])
                        pb, pbn = b.ps()
                        S.op("pe", lambda e: e.matmul(pb[:], M_in, gg[:], start=True, stop=True), reads=["cm", "h_g" + x], writes=[pbn])
                        S.op("act", lambda e: e.activation(kt_[:], pb[:], AF.Exp, scale=-1.0), reads=[pbn], writes=["h_kt" + x])
                        S.op("dve", lambda e: e.tensor_tensor(kt_[:], kt_[:], kk[:], ALU.mult), reads=["h_kt" + x, "h_k" + x], writes=["h_kt" + x])
                        pr, prn = b.ps()
                        S.op("pe", lambda e: e.matmul(pr[:], M_ex, gg[:], start=True, stop=True), reads=["cm", "h_g" + x], writes=[prn])
                        S.op("act", lambda e: e.activation(kh[:], pr[:], AF.Exp), reads=[prn], writes=["h_kh" + x])
                        S.op("dve", lambda e: e.tensor_tensor(kh[:], kh[:], kk[:], ALU.mult), reads=["h_kh" + x, "h_k" + x], writes=["h_kh" + x])
                        po_, pon_ = pos[d]
                        for h in range(4):
                            hs = slice(h * 128, (h + 1) * 128)
                            pbt, pbtn = b.ps()
                            S.op("pe", lambda e: e.matmul(pbt[:, 0:128], gg[:, hs], M_in, start=True, stop=True), reads=["h_g" + x, "cm"], writes=[pbtn])
                            S.op("pe", lambda e: e.matmul(pbt[:, 128:132], gg[:, hs], csel[:, 0:4], start=True, stop=True), reads=["h_g" + x, "cm"], writes=[pbtn])
                            S.op("act", lambda e: e.activation(eb[:, h, :], pbt[:, 0:132], AF.Exp), reads=[pbtn], writes=["h_eb" + x])
                            S.op("dve", lambda e: e.tensor_tensor(qt[:, h, :], qT[:, h, :], eb[:, h, 0:128], ALU.mult), reads=["h_qT" + x, "h_eb" + x], writes=["h_qt" + x])
                            pk, pkn = b.ps()
                            S.op("pe", lambda e: e.matmul(pk[:, 0:128], kt_[:, hs], ident, start=True, stop=True), reads=["h_kt" + x, "cm"], writes=[pkn])
                            S.op("act", lambda e: e.copy(ktT[:, h, :], pk[:, 0:128]), reads=[pkn], writes=["h_ktT" + x])
                            pa, pan = b.ps()
                            S.op("pe", lambda e: e.matmul(pa[:, 0:128], ktT[:, h, :], qt[:, h, :], start=True, stop=True), reads=["h_ktT" + x, "h_qt" + x], writes=[pan])
                            S.op("dve", lambda e: e.tensor_tensor(AT[:, h, :], pa[:, 0:128], M_in, ALU.mult), reads=[pan, "cm"], writes=["h_AT" + x])
                            for j in range(4):
                                S.op("pool", lambda e: e.tensor_scalar(khm[:, h, j, :], kh[:, hs], csel[:, j:j + 1], None, ALU.mult),
                                     reads=["h_kh" + x, "cm"], writes=["h_khm" + x])
                            S.op("pe", lambda e: e.matmul(po_[:, hs], vv[:, hs], AT[:, h, :], start=True, stop=True), reads=["h_v" + x, "h_AT" + x], writes=[pon_], pe_acc=(h > 0))
                    for n_ in range(4):
                        pss = []
                        for d in range(2):
                            w = W_[d, k]
                            x = w["sfx"]
                            j = n_ if d == 0 else 3 - n_
                            js = slice(j * 32, (j + 1) * 32)
                            po_, pon_ = pis[d]
                            ps_, psn2 = b.ps()
                            pss.append((ps_, psn2, j))
                            for h in range(4):
                                hs = slice(h * 128, (h + 1) * 128)
                                S.op("pe", lambda e: e.matmul(po_[:, h * 128 + j * 32:h * 128 + (j + 1) * 32], Sst[:, d, h, :], w["qt"][:, h, js], start=True, stop=True),
                                     reads=["Sst%d%d" % (d, h), "h_qt" + x], writes=[pon_], pe_acc=(n_ > 0 or h > 0))
                                S.op("pe", lambda e: e.matmul(ps_[:, hs], w["khm"][:, h, j, :], w["vv"][:, hs], start=True, stop=True),
                                     reads=["h_khm" + x, "h_v" + x], writes=[psn2], pe_acc=True)
                        for d in range(2):
                            w = W_[d, k]
                            x = w["sfx"]
                            ps_, psn2, j = pss[d]
                            for h in range(4):
                                hs = slice(h * 128, (h + 1) * 128)
                                S.op("dve", lambda e: e.scalar_tensor_tensor(Sst[:, d, h, :], Sst[:, d, h, :], w["eb"][:, h, 128 + j:129 + j], ps_[:, hs], ALU.mult, ALU.add),
                                     reads=[psn2, "h_eb" + x, "Sst%d%d" % (d, h)], writes=["Sst%d%d" % (d, h)])
                    for d in range(2):
                        po_, pon_ = pos[d]
                        pi_, pin_ = pis[d]
                        tl = tis[d] * 128
                        S.op("act", lambda e: e.copy(tmpo[d][:], po_[:]), reads=[pon_], writes=["h_tmpo%d" % d])
                        S.op("dve", lambda e: e.tensor_tensor(oall[:, d, :, tl:tl + 128], tmpo[d][:].rearrange("p (h t) -> p h t", h=4),
                                                              pi_[:].rearrange("p (h t) -> p h t", h=4), ALU.add),
                             reads=[pin_, "h_tmpo%d" % d], writes=["oall"])
                for c in range(L // 256):
                    cs = slice(c * 256, (c + 1) * 256)
                    os_, gT, sq, rs, ob = fin["os"], fin["gT"], fin["sq"], fin["rs"], fin["ob"]
                    S.dma("act", gT[:], UT[OFF["hg"]:OFF["hg"] + 512, T0 + c * 256:T0 + (c + 1) * 256].rearrange("(h p) t -> p h t", p=128), writes=["f_gT"])
                    S.op("act", lambda e: e.activation(gT[:], gT[:], AF.Silu), reads=["f_gT"], writes=["f_gT"])
                    S.op("dve", lambda e: e.tensor_tensor(os_[:], oall[:, 0, :, cs], oall[:, 1, :, cs], ALU.add), reads=["oall"], writes=["f_os"])
                    S.op("act", lambda e: e.activation(sq[:], os_[:], AF.Square), reads=["f_os"], writes=["f_sq"])
                    for hh in range(2):
                        pn_, pnn = b.ps()
                        for h2 in range(2):
                            h = hh * 2 + h2
                            S.op("pe", lambda e: e.matmul(pn_[:, h2 * 256:(h2 + 1) * 256], ones_bf[:], sq[:, h, :], start=True, stop=True),
                                 reads=["f_sq", "ones_bf"], writes=[pnn])
                        rstd_from_sumsq(pn_, pnn, rs[:, hh * 2:hh * 2 + 2, :].rearrange("p a t -> p (a t)"), "f_rs", 128, 512, 1.0 / 128)
                    for h in range(4):
                        S.op("dve", lambda e: e.scalar_tensor_tensor(os_[:, h, :], os_[:, h, :], hn[:, h:h + 1], rs[:, h, :], ALU.mult, ALU.mult),
                             reads=["f_os", "hn", "f_rs"], writes=["f_os"])
                    S.op("dve", lambda e: e.tensor_tensor(ob[:], os_[:], gT[:], ALU.mult), reads=["f_os", "f_gT"], writes=["f_ob"])
                    S.dma("sp", OT[0, :, T0 + c * 256:T0 + (c + 1) * 256].rearrange("(h p) t -> p h t", p=128), ob[:], reads=["f_ob"], writes=[])
                if g == "p":
                    S.dma("sp", D["o_hgrn"][l, sq_].rearrange("d h k v -> k d h v"), Sst[:], reads=SSTN, writes=[])
            b.nrot = 6
            S.barrier()

    def epilogue(st, z, zn, xt, xn, gco, Xdst, c0, wkn):
        sq, rs, tmp = wkn
        for kc in range(8):
            S.op("act", lambda e: e.activation(sq[:, kc, :], z[:, kc, :], AF.Square), reads=[zn], writes=["e_sq"])
        pt, pn = b.ps()
        for kc in range(8):
            S.op("pe", lambda e: e.matmul(pt[:], ones_bf[:], sq[:, kc, :], start=(kc == 0), stop=(kc == 7)), reads=["e_sq", "ones_bf"], writes=[pn], pe_acc=True)
        rstd_from_sumsq(pt, pn, rs, "e_rs", 128, 512, 1.0 / DM)
        for kc in range(8):
            S.op("dve", lambda e: e.tensor_tensor(tmp[:], z[:, kc, :], rs[:], ALU.mult), reads=[zn, "e_rs"], writes=["e_tmp"])
            S.op("dve", lambda e: e.scalar_tensor_tensor(xt[:, kc, :], tmp[:], gco[:, kc:kc + 1], xt[:, kc, :], ALU.mult, ALU.add),
                 reads=["e_tmp", "coef", xn], writes=[xn])
        S.dma("sp", Xdst[:, c0:c0 + 512].rearrange("(kc p) t -> p kc t", p=128), xt[:], reads=[xn], writes=[])

    def stage_merge(g, l, Xsrc, Xdst):
        G = GR[g]
        T = G["T"]
        UT, OT = D["UT_" + g], D["OT_" + g]
        with ExitStack() as st:
            Wb = b.sb(st, "Wb", [128, 4, 4, 1024], BF16)
            Wo = b.sb(st, "Wo", [128, 8, 1024], BF16)
            S.dma("pool", Wb[:], D["w_branch"][l].rearrange("n (kc p) c -> p n kc c", p=128), writes=["Wb"])
            S.dma("pool", Wo[:], D["w_out"][l].rearrange("(kc p) c -> p kc c", p=128), writes=["Wo"])
            ot = b.sb(st, "ot", [128, 4, 4, 512], BF16)
            yp = b.sb(st, "yp", [128, 8, 512], BF16)
            gts = [b.sb(st, "gt%d" % i, [128, 512]) for i in range(3)]
            accf = b.sb(st, "accf", [128, 512]); tm2 = b.sb(st, "tm2", [128, 512])
            z = b.sb(st, "z", [128, 8, 512]); xt = b.sb(st, "xt", [128, 8, 512])
            wkn = (b.sb(st, "e_sq", [128, 8, 512], BF16), b.sb(st, "e_rs", [128, 512]), b.sb(st, "e_tmp", [128, 512]))
            ng = 0
            for tt in range(T // 512):
                cs = slice(tt * 512, (tt + 1) * 512)
                S.dma("sp", ot[:], OT[:, :, cs].rearrange("n (kc p) t -> p n kc t", p=128), writes=["ot"])
                S.dma("act", xt[:], Xsrc[:, cs].rearrange("(kc p) t -> p kc t", p=128), writes=["xt"])
                for dmc in range(8):
                    for n in range(4):
                        gt, gtn = gts[ng % 3], "gt%d" % (ng % 3)
                        ng += 1
                        r0 = OFF["gates"] + n * 1024 + dmc * 128
                        S.dma("sp", gt[:], UT[r0:r0 + 128, cs], writes=[gtn])
                        S.op("act", lambda e: e.activation(gt[:], gt[:], AF.Sigmoid), reads=[gtn], writes=[gtn])
                        pt, pn = b.ps()
                        for kc in range(4):
                            S.op("pe", lambda e: e.matmul(pt[:], Wb[:, n, kc, dmc * 128:(dmc + 1) * 128], ot[:, n, kc, :], start=(kc == 0), stop=(kc == 3)),
                                 reads=["Wb", "ot"], writes=[pn], pe_acc=True)
                        if n == 0:
                            S.op("dve", lambda e: e.tensor_tensor(accf[:], pt[:], gt[:], ALU.mult), reads=[pn, gtn], writes=["accf"])
                        elif n < 3:
                            S.op("dve", lambda e: e.tensor_tensor(tm2[:], pt[:], gt[:], ALU.mult), reads=[pn, gtn], writes=["tm2"])
                            S.op("dve", lambda e: e.tensor_tensor(accf[:], accf[:], tm2[:], ALU.add), reads=["tm2", "accf"], writes=["accf"])
                        else:
                            S.op("dve", lambda e: e.tensor_tensor(tm2[:], pt[:], gt[:], ALU.mult), reads=[pn, gtn], writes=["tm2"])
                            S.op("dve", lambda e: e.tensor_tensor(yp[:, dmc, :], accf[:], tm2[:], ALU.add), reads=["tm2", "accf"], writes=["yp"])
                for oc in range(8):
                    pt, pn = b.ps()
                    for kc in range(8):
                        S.op("pe", lambda e: e.matmul(pt[:], Wo[:, kc, oc * 128:(oc + 1) * 128], yp[:, kc, :], start=(kc == 0), stop=(kc == 7)),
                             reads=["Wo", "yp"], writes=[pn], pe_acc=True)
                    S.op("act", lambda e: e.copy(z[:, oc, :], pt[:]), reads=[pn], writes=["z"])
                epilogue(st, z, "z", xt, "xt", coef[:, l, G["cond"], 2], Xdst, tt * 512, wkn)
            S.barrier()

    def stage_mlp(g, l, Xsrc, Xdst):
        G = GR[g]
        T = G["T"]
        with ExitStack() as st:
            h2 = stage_h(st, g, l, Xsrc, 3, 4)
            W2 = [b.sb(st, "W2_%d" % i, [128, 8, 1024], BF16) for i in range(2)]
            n2 = 0
            w1 = [b.sb(st, "w1_%d" % i, [128, 8, 512], BF16) for i in range(2)]
            hid = b.sb(st, "hid", [128, 32, 512], BF16)
            rl = [b.sb(st, "rl%d" % i, [128, 512]) for i in range(2)]
            z = b.sb(st, "z", [128, 8, 512]); xt = b.sb(st, "xt", [128, 8, 512])
            wkn = (b.sb(st, "e_sq", [128, 8, 512], BF16), b.sb(st, "e_rs", [128, 512]), b.sb(st, "e_tmp", [128, 512]))
            nw = 0
            nr = 0
            for tt in range(T // 512):
                cs = slice(tt * 512, (tt + 1) * 512)
                S.dma("act", xt[:], Xsrc[:, cs].rearrange("(kc p) t -> p kc t", p=128), writes=["xt"])
                for cg in range(8):
                    w, wn = w1[nw % 2], "w1_%d" % (nw % 2)
                    nw += 1
                    S.dma("pool", w[:], D["w_mlp_in"][l, :, cg * 512:(cg + 1) * 512].rearrange("(kc p) c -> p kc c", p=128), writes=[wn])
                    for cc in range(4):
                        pt, pn = b.ps()
                        for kc in range(8):
                            S.op("pe", lambda e: e.matmul(pt[:], w[:, kc, cc * 128:(cc + 1) * 128], h2[:, kc, cs], start=(kc == 0), stop=(kc == 7)),
                                 reads=[wn, "hT"], writes=[pn], pe_acc=True)
                        r, rn = rl[nr % 2], "rl%d" % (nr % 2)
                        nr += 1
                        S.op("act", lambda e: e.activation(r[:], pt[:], AF.Relu), reads=[pn], writes=[rn])
                        S.op("dve", lambda e: e.tensor_tensor(hid[:, cg * 4 + cc, :], r[:], r[:], ALU.mult), reads=[rn], writes=["hid"])
                for q4 in range(4):
                    w2, w2n = W2[n2 % 2], "W2_%d" % (n2 % 2)
                    n2 += 1
                    S.dma("pool", w2[:], D["w_mlp_out"][l, q4 * 1024:(q4 + 1) * 1024, :].rearrange("(kc p) c -> p kc c", p=128), writes=[w2n])
                    for oc in range(8):
                        pt, pn = b.ps()
                        for fc in range(8):
                            S.op("pe", lambda e: e.matmul(pt[:], w2[:, fc, oc * 128:(oc + 1) * 128], hid[:, q4 * 8 + fc, :], start=(fc == 0), stop=(fc == 7)),
                                 reads=[w2n, "hid"], writes=[pn], pe_acc=True)
                        if q4 == 0:
                            S.op("act", lambda e: e.copy(z[:, oc, :], pt[:]), reads=[pn], writes=["z%d" % oc, "z"])
                        else:
                            S.op("dve", lambda e: e.tensor_tensor(z[:, oc, :], z[:, oc, :], pt[:], ALU.add), reads=[pn, "z%d" % oc], writes=["z%d" % oc, "z"])
                epilogue(st, z, "z", xt, "xt", coef[:, l, G["cond"], 5], Xdst, tt * 512, wkn)
            S.barrier()

    for g in ("s", "p"):
        X = D["xT_" + g]
        for l in range(DEPTH):
            if "proj" in STAGES:
                with ExitStack() as st:
                    hT = stage_h(st, g, l, X, 0, 1)
                    stage_proj(g, l, hT)
            if "hgrn" in STAGES:
                stage_hgrn(g, l)
            if "swa" in STAGES:
                stage_gqa_like(g, l, "swa")
            if "mla" in STAGES:
                stage_mla(g, l)
            if "gqa" in STAGES:
                stage_gqa_like(g, l, "gqa")
            if "merge" in STAGES:
                stage_merge(g, l, X, D["X1_" + g])
            Xn = D["yT_" + g] if l == DEPTH - 1 else D["X2_%d_%s" % (l, g)]
            if "mlp" in STAGES:
                stage_mlp(g, l, D["X1_" + g], Xn)
            X = Xn
    S.barrier()
    return nc, b


_CACHE = {}


def _consts():
    nf = 16
    t = np.arange(2048)
    row, col = (t // 64).astype(np.float32), (t % 64).astype(np.float32)
    rope = np.zeros((4, 128, 2048), np.float32)
    def fill(ci, si, r0, nf):
        inv = (10000.0 ** (-np.arange(nf, dtype=np.float32) / nf)).astype(np.float32)
        ar = (row[None, :] * inv[:, None]).astype(np.float32)
        ac = (col[None, :] * inv[:, None]).astype(np.float32)
        for k, a in enumerate((ar, ac)):
            b0 = r0 + k * 2 * nf
            rope[ci, b0:b0 + nf] = np.cos(a); rope[ci, b0 + nf:b0 + 2 * nf] = np.cos(a)
            rope[si, b0:b0 + nf] = -np.sin(a); rope[si, b0 + nf:b0 + 2 * nf] = np.sin(a)
    fill(0, 1, 0, 16)
    fill(0, 1, 64, 16)
    fill(2, 3, 64, 8)
    s_ = np.arange(128)[:, None]; t_ = np.arange(128)[None, :]
    same = (s_ // 32) == (t_ // 32)
    cm = np.zeros((6, 128, 128), np.float32)
    cm[0] = np.eye(128)
    cm[1] = same & (s_ <= t_)
    cm[2] = same & (s_ >= t_)
    cm[3] = same & (s_ > t_)
    cm[4] = same & (s_ < t_)
    cm[5][:, 0:4] = (s_ // 32) == np.arange(4)[None, :]
    j = np.arange(128)[:, None]; i = (np.arange(512) % 128)[None, :]
    swm = np.stack([(j >= i), (j <= i)]).astype(np.float32).astype(ml_dtypes.bfloat16)
    return rope, cm, swm


def _perm(nd):
    q = nd // 4
    return np.concatenate([np.arange(q, 2 * q), np.arange(0, q), np.arange(3 * q, 4 * q), np.arange(2 * q, 3 * q)])


def kernel(**inp):
    f = lambda a: np.ascontiguousarray(np.asarray(a, dtype=np.float32))
    I = {k: f(v) for k, v in inp.items()}
    if "prog" not in _CACHE:
        _CACHE["prog"] = build_program()
    nc, b = _CACHE["prog"]
    rope, cm, swm = _consts()
    fm = lambda v, n: f(v.reshape(v.shape[0], n, 128).transpose(0, 2, 1))
    shared = dict(
        w_ada=I["w_ada"], b_adaT=fm(I["b_ada"], 48),
        gains=f(np.stack([fm(I[k], 8) for k in ("norm_mix_pre", "norm_mix_post", "norm_mlp_pre", "norm_mlp_post")], axis=2)),
        w_in=I["w_in"], lbT=f(np.stack([I["hgrn_lb_fwd"], I["hgrn_lb_bwd"]], axis=1)),
        hgrn_normT=fm(I["hgrn_norm"], 4), sink=f(I["swa_sink"][:, None, :]),
        mla_qnT=fm(I["mla_q_norm"], 2), mla_kvn=f(I["mla_kv_norm"][:, :, None]),
        w_uq=I["mla_w_uq"], w_ukv=I["mla_w_ukv"],
        gqa_qn=f(np.tile(np.stack([I["gqa_q_norm"], I["gqa_q_norm"][:, _perm(64)]], axis=2), (1, 2, 1))),
        gqa_kn=f(np.tile(np.stack([I["gqa_k_norm"], I["gqa_k_norm"][:, _perm(64)]], axis=2), (1, 2, 1))),
        w_branch=I["w_branch"], w_out=I["w_out"], w_mlp_in=I["w_mlp_in"], w_mlp_out=I["w_mlp_out"],
        rope64=rope, cmat=cm, swamask=swm,
    )
    wsw = I["mla_w_uq"].reshape(DEPTH, 256, 8, 96).copy()
    wsw[..., 64:96] = wsw[..., 64:96][..., _perm(32)]
    shared["w_uq_sw"] = f(wsw.reshape(DEPTH, 256, 768))
    in_maps = []
    for i in range(NCORE):
        bb = i % 2
        m = dict(shared)
        m["xT_s"] = f(I["x_sample"][bb].T)
        m["xT_p"] = f(I["x_prompt"][4 * i:4 * i + 4].reshape(1024, DM).T)
        cc = np.stack([I["c_ctx"], I["c"][bb]], axis=1)
        m["cT"] = f(cc.reshape(8, 128, 2).transpose(1, 0, 2))
        m["st_hgrn"] = f(I["state_hgrn"][bb])
        m["c_swa_kT"] = f(I["cache_swa_k"][bb].transpose(0, 2, 3, 1))
        m["c_swa_v"] = f(I["cache_swa_v"][bb].reshape(DEPTH, 512, 128))
        m["c_ckvT"] = f(I["cache_mla_ckv"][bb].transpose(0, 2, 1))
        m["c_krT"] = f(I["cache_mla_kr"][bb].transpose(0, 2, 1))
        m["c_gqa_kT"] = f(I["cache_gqa_k"][bb].transpose(0, 2, 3, 1))
        m["c_gqa_v"] = f(I["cache_gqa_v"][bb].reshape(DEPTH, 512, 128))
        in_maps.append(m)
    res = run_bass_kernel_spmd(nc, in_maps, core_ids=list(range(NCORE)))
    R = res.results
    y_prompt = np.concatenate([R[i]["yT_p"].T.reshape(4, 256, DM) for i in range(NCORE)], axis=0)
    y_sample = np.stack([R[0]["yT_s"].T, R[1]["yT_s"].T], axis=0)
    cat = lambda fn: np.ascontiguousarray(np.concatenate([fn(R[i]) for i in range(NCORE)], axis=0).astype(np.float32))
    n_hgrn = cat(lambda r: r["o_hgrn"].transpose(1, 0, 2, 3, 4, 5))
    kT = lambda a: a.reshape(DEPTH, 2, 64, 4, 256).transpose(3, 0, 4, 1, 2)
    vv = lambda a: a.reshape(DEPTH, 4, 256, 2, 64).transpose(1, 0, 2, 3, 4)
    n_swa_k = cat(lambda r: kT(r["o_swa_kT"]))
    n_swa_v = cat(lambda r: vv(r["o_swa_v"]))
    n_ckv = cat(lambda r: r["o_ckvT"].reshape(DEPTH, 128, 4, 256).transpose(2, 0, 3, 1))
    n_kr = cat(lambda r: r["o_krT"].reshape(DEPTH, 32, 4, 256).transpose(2, 0, 3, 1))
    n_gqa_k = cat(lambda r: kT(r["o_gqa_kT"]))
    n_gqa_v = cat(lambda r: vv(r["o_gqa_v"]))
    return (np.ascontiguousarray(y_prompt.astype(np.float32)), np.ascontiguousarray(y_sample.astype(np.float32)),
            n_hgrn, n_swa_k, n_swa_v, n_ckv, n_kr, n_gqa_k, n_gqa_v)
```

```python
import numpy as np
from contextlib import ExitStack
import ml_dtypes
import concourse.bass as bass
import concourse.mybir as mybir
from concourse.bass_utils import run_bass_kernel_spmd

F32 = mybir.dt.float32
BF16 = mybir.dt.bfloat16
AF = mybir.ActivationFunctionType
ALU = mybir.AluOpType

DM = 1024
DEPTH = 2
NCORE = 8
EPS = 1e-6
OFF = dict(hq=0, ff=512, fb=1024, hi=1536, hg=2048, sq=2560, sk=3072, sv=3200, cq=3328, ckv=3584, kr=3712,
           gq=3744, gk=4256, gv=4384, gates=4512)
D_IN = 8608
UTM_COLS = 1792
STAGES = {"proj", "hgrn", "swa", "mla", "gqa", "merge", "mlp"}


class Sched:
    NDMA = 24

    def __init__(self, nc, es):
        self.nc = nc
        self.eng = {"pe": nc.tensor, "act": nc.scalar, "dve": nc.vector, "pool": nc.gpsimd, "sp": nc.sync}
        self.sem = {k: es.enter_context(nc.semaphore("s_" + k)) for k in ("pe", "act", "dve", "pool")}
        self.cnt = {k: 0 for k in self.sem}
        self.dsem = [es.enter_context(nc.semaphore("d%d" % i)) for i in range(self.NDMA)]
        self.dcnt = [0] * self.NDMA
        self.dnext = 0
        self.seen = {k: {} for k in self.eng}
        self.lastw = {}
        self.reads = {}
        self.n_instr = 0

    def _sem_of(self, key):
        return self.sem[key] if isinstance(key, str) else self.dsem[key[1]]

    def _wait(self, e, tok):
        key, val = tok
        if self.seen[e].get(key, 0) >= val:
            return
        self.eng[e].wait_ge(self._sem_of(key), val)
        self.seen[e][key] = val

    def _deps(self, e, reads, writes, pe_acc=False):
        best = {}
        for r in reads:
            t = self.lastw.get(r)
            if t is not None and best.get(t[0], 0) < t[1]:
                best[t[0]] = t[1]
        for w in writes:
            t = self.lastw.get(w)
            if t is not None and best.get(t[0], 0) < t[1]:
                if not (pe_acc and t[0] == "pe"):
                    best[t[0]] = t[1]
            for t in self.reads.get(w, ()):
                if best.get(t[0], 0) < t[1]:
                    best[t[0]] = t[1]
        for key, val in best.items():
            self._wait(e, (key, val))

    def _record(self, tok, reads, writes):
        for r in reads:
            lst = self.reads.setdefault(r, [])
            lst[:] = [t for t in lst if t[0] != tok[0]]
            lst.append(tok)
        for w in writes:
            self.lastw[w] = tok
            self.reads[w] = []

    def op(self, e, fn, reads=(), writes=(), pe_acc=False):
        self._deps(e, reads, writes, pe_acc)
        ins = fn(self.eng[e])
        self.cnt[e] += 1
        ins.then_inc(self.sem[e], 1)
        self._record((e, self.cnt[e]), reads, writes)
        self.n_instr += 1
        return ins

    def dma(self, q, out, in_, reads=(), writes=()):
        i = self.dnext
        self.dnext = (self.dnext + 1) % self.NDMA
        if self.dcnt[i] > 0:
            self._wait(q, (("d", i), self.dcnt[i]))
        self._deps(q, reads, writes)
        ins = self.eng[q].dma_start(out=out, in_=in_)
        self.dcnt[i] += 16
        ins.then_inc(self.dsem[i], 16)
        self._record((("d", i), self.dcnt[i]), reads, writes)
        self.n_instr += 1
        return ins

    def barrier(self):
        best = {}
        for k in self.cnt:
            if self.cnt[k]:
                best[k] = self.cnt[k]
        for i in range(self.NDMA):
            if self.dcnt[i]:
                best[("d", i)] = self.dcnt[i]
        for e in self.eng:
            for key, val in best.items():
                self._wait(e, (key, val))
        self.lastw = {}
        self.reads = {}


class B:
    def __init__(self, nc, es):
        self.nc, self.es = nc, es
        self.S = Sched(nc, es)
        self.D = {}
        self.psn = 0

    def din(self, name, shape, dt=F32):
        self.D[name] = self.nc.dram_tensor(name, list(shape), dt, kind="ExternalInput").ap()
        return self.D[name]

    def dout(self, name, shape, dt=F32):
        self.D[name] = self.nc.dram_tensor(name, list(shape), dt, kind="ExternalOutput").ap()
        return self.D[name]

    def dscr(self, name, shape, dt=F32):
        self.D[name] = self.nc.dram_tensor(name, list(shape), dt, kind="Internal").ap()
        return self.D[name]

    def sb(self, st, name, shape, dt=F32):
        self.uid = getattr(self, "uid", 0) + 1
        return st.enter_context(self.nc.sbuf_tensor("sb%d_%s" % (self.uid, name), list(shape), dt))

    def ps(self):
        i = self.psn % self.nrot
        self.psn += 1
        return self.psum[i], "ps%d" % i

    nrot = 6

    def acc(self, i):
        k = (6, 7, 4, 5)[i]
        return self.psum[k], "ps%d" % k


def build_program():
    nc = bass.Bass("TRN2", target_bir_lowering=False)
    es = ExitStack()
    b = B(nc, es)
    S = b.S
    D = b.D
    GR = {
        "s": dict(T=2048, nseq=1, L=2048, P=512, rope=True, cond=1),
        "p": dict(T=1024, nseq=4, L=256, P=0, rope=False, cond=0),
    }
    for g, G in GR.items():
        T = G["T"]
        b.din("xT_" + g, [DM, T])
        b.dout("yT_" + g, [DM, T])
        b.dscr("X1_" + g, [DM, T])
        for l in range(DEPTH - 1):
            b.dscr("X2_%d_%s" % (l, g), [DM, T])
        b.dscr("UT_" + g, [68 * 128, T])
        b.dscr("UTM_" + g, [T, UTM_COLS])
        b.dscr("OT_" + g, [4, 512, T], BF16)
        b.dscr("YP_" + g, [DM, T], BF16)
    b.din("cT", [128, 8, 2])
    b.din("w_ada", [DEPTH, DM, 6 * DM])
    b.din("b_adaT", [DEPTH, 128, 48])
    b.din("gains", [DEPTH, 128, 4, 8])
    b.din("w_in", [DEPTH, DM, D_IN])
    b.din("lbT", [DEPTH, 2, 512])
    b.din("hgrn_normT", [DEPTH, 128, 4])
    b.din("sink", [DEPTH, 1, 8])
    b.din("mla_qnT", [DEPTH, 128, 2])
    b.din("mla_kvn", [DEPTH, 128, 1])
    b.din("w_uq", [DEPTH, 256, 768])
    b.din("w_uq_sw", [DEPTH, 256, 768])
    b.din("w_ukv", [DEPTH, 128, 1024])
    b.din("gqa_qn", [DEPTH, 128, 2])
    b.din("gqa_kn", [DEPTH, 128, 2])
    b.din("w_branch", [DEPTH, 4, 512, DM])
    b.din("w_out", [DEPTH, DM, DM])
    b.din("w_mlp_in", [DEPTH, DM, 4 * DM])
    b.din("w_mlp_out", [DEPTH, 4 * DM, DM])
    b.din("st_hgrn", [DEPTH, 2, 4, 128, 128])
    b.din("c_swa_kT", [DEPTH, 2, 64, 512])
    b.din("c_swa_v", [DEPTH, 512, 128])
    b.din("c_ckvT", [DEPTH, 128, 512])
    b.din("c_krT", [DEPTH, 32, 512])
    b.din("c_gqa_kT", [DEPTH, 2, 64, 512])
    b.din("c_gqa_v", [DEPTH, 512, 128])
    b.din("rope64", [4, 128, 2048])
    b.din("cmat", [6, 128, 128])
    b.din("swamask", [2, 128, 512], BF16)
    b.dout("o_hgrn", [DEPTH, 4, 2, 4, 128, 128])
    b.dout("o_swa_kT", [DEPTH, 128, 1024])
    b.dout("o_swa_v", [DEPTH, 1024, 128])
    b.dout("o_ckvT", [DEPTH, 128, 1024])
    b.dout("o_krT", [DEPTH, 32, 1024])
    b.dout("o_gqa_kT", [DEPTH, 128, 1024])
    b.dout("o_gqa_v", [DEPTH, 1024, 128])

    b.psum = [es.enter_context(nc.psum_tensor("psb%d" % i, [128, 512], F32)) for i in range(8)]

    cst = ExitStack()
    es.enter_context(cst)
    ones_bf = b.sb(cst, "ones_bf", [128, 128], BF16)
    ones_f = b.sb(cst, "ones_f", [128, 128], F32)
    cm = b.sb(cst, "cm", [128, 6, 128], F32)
    modT = b.sb(cst, "modT", [128, DEPTH, 48, 2], F32)
    gains = b.sb(cst, "gains", [128, DEPTH, 4, 8], F32)
    coef = b.sb(cst, "coef", [128, DEPTH, 2, 6, 8], F32)
    S.op("pool", lambda e: e.memset(ones_bf[:], 1.0), writes=["ones_bf"])
    onesAB = b.sb(cst, "onesAB", [128, 2, 128], BF16)
    S.op("pool", lambda e: e.memset(onesAB[:], 0.0), writes=["onesAB"])
    S.op("pool", lambda e: e.memset(onesAB[:, 0, 0:64], 1.0), writes=["onesAB"])
    S.op("pool", lambda e: e.memset(onesAB[:, 1, 64:128], 1.0), writes=["onesAB"])
    S.op("pool", lambda e: e.memset(ones_f[:], 1.0), writes=["ones_f"])
    S.dma("sp", cm[:], D["cmat"].rearrange("a p c -> p a c"), writes=["cm"])
    S.dma("sp", gains[:], D["gains"].rearrange("l p a c -> p l a c"), writes=["gains"])

    with ExitStack() as st:
        cT = b.sb(st, "cT", [128, 8, 2])
        scT = b.sb(st, "scT", [128, 8, 2])
        badaT = b.sb(st, "badaT", [128, DEPTH, 48])
        wa = [b.sb(st, "wa%d" % i, [128, 8, 768]) for i in range(2)]
        S.dma("sp", cT[:], D["cT"], writes=["cT"])
        S.dma("sp", badaT[:], D["b_adaT"].rearrange("l p j -> p l j"), writes=["badaT"])
        S.op("act", lambda e: e.activation(scT[:], cT[:], AF.Silu), reads=["cT"], writes=["scT"])
        n = 0
        for l in range(DEPTH):
            pt, pn = b.ps()
            for cg in range(8):
                w = wa[n % 2]
                wn = "wa%d" % (n % 2)
                n += 1
                S.dma("sp" if cg % 2 == 0 else "act", w[:],
                      D["w_ada"][l, :, cg * 768:(cg + 1) * 768].rearrange("(kc p) c -> p kc c", p=128), writes=[wn])
                for jj in range(6):
                    j = cg * 6 + jj
                    for kc in range(8):
                        S.op("pe", lambda e: e.matmul(pt[:, 2 * j:2 * j + 2], w[:, kc, jj * 128:(jj + 1) * 128], scT[:, kc, :],
                                                      start=(kc == 0), stop=(kc == 7)),
                             reads=[wn, "scT"], writes=[pn], pe_acc=True)
            S.op("dve", lambda e: e.tensor_tensor(modT[:, l], pt[:, 0:96].rearrange("p (j c) -> p j c", c=2),
                                                  badaT[:, l].unsqueeze(2).to_broadcast([128, 48, 2]), ALU.add),
                 reads=[pn, "badaT"], writes=["modT"])
        for l in range(DEPTH):
            for c in range(2):
                m = lambda i: modT[:, l, i * 8:(i + 1) * 8, c]
                S.op("dve", lambda e: e.scalar_tensor_tensor(coef[:, l, c, 0], m(1), 1.0, gains[:, l, 0], ALU.add, ALU.mult),
                     reads=["modT", "gains"], writes=["coef"])
                S.op("dve", lambda e: e.tensor_copy(coef[:, l, c, 1], m(0)), reads=["modT"], writes=["coef"])
                S.op("dve", lambda e: e.tensor_tensor(coef[:, l, c, 2], m(2), gains[:, l, 1], ALU.mult), reads=["modT", "gains"], writes=["coef"])
                S.op("dve", lambda e: e.scalar_tensor_tensor(coef[:, l, c, 3], m(4), 1.0, gains[:, l, 2], ALU.add, ALU.mult),
                     reads=["modT", "gains"], writes=["coef"])
                S.op("dve", lambda e: e.tensor_copy(coef[:, l, c, 4], m(3)), reads=["modT"], writes=["coef"])
                S.op("dve", lambda e: e.tensor_tensor(coef[:, l, c, 5], m(5), gains[:, l, 3], ALU.mult), reads=["modT", "gains"], writes=["coef"])
        S.barrier()

    def rstd_from_sumsq(pt, pn, out, outn, npart, ncol, inv_n):
        S.op("act", lambda e: e.activation(out[0:npart, 0:ncol], pt[0:npart, 0:ncol], AF.Sqrt, bias=epsb[0:npart, :], scale=inv_n),
             reads=[pn, "epsb"], writes=[outn])
        S.op("dve", lambda e: e.reciprocal(out[0:npart, 0:ncol], out[0:npart, 0:ncol]), reads=[outn], writes=[outn])

    epsb = b.sb(cst, "epsb", [128, 1], F32)
    S.op("pool", lambda e: e.memset(epsb[:], EPS), writes=["epsb"])

    def norm_mod_tile(st_unused, xt, xn, hT, hn, t0, a_ap, sh_ap, tmp, sq, rs):
        for kc in range(8):
            S.op("act", lambda e: e.activation(sq[:, kc, :], xt[:, kc, :], AF.Square), reads=[xn], writes=["sq"])
        pt, pn = b.ps()
        for kc in range(8):
            S.op("pe", lambda e: e.matmul(pt[:], ones_bf[:], sq[:, kc, :], start=(kc == 0), stop=(kc == 7)),
                 reads=["sq", "ones_bf"], writes=[pn], pe_acc=True)
        rstd_from_sumsq(pt, pn, rs, "rs", 128, 512, 1.0 / DM)
        for kc in range(8):
            S.op("dve", lambda e: e.tensor_tensor(tmp[:], xt[:, kc, :], rs[:], ALU.mult), reads=[xn, "rs"], writes=["tmp"])
            S.op("act", lambda e: e.activation(hT[:, kc, t0:t0 + 512], tmp[:], AF.Identity, bias=sh_ap[:, kc:kc + 1], scale=a_ap[:, kc:kc + 1]),
                 reads=["tmp", "coef"], writes=[hn])

    def stage_h(st, g, l, Xsrc, ia, ish):
        G = GR[g]
        T = G["T"]
        hT = b.sb(st, "hT", [128, 8, T], BF16)
        with ExitStack() as s2:
            xts = [b.sb(s2, "xt%d" % i, [128, 8, 512]) for i in range(2)]
            tmp = b.sb(s2, "tmp", [128, 512])
            sq = b.sb(s2, "sq", [128, 8, 512], BF16)
            rs = b.sb(s2, "rs", [128, 512])
            for tt in range(T // 512):
                xt, xn = xts[tt % 2], "xt%d" % (tt % 2)
                S.dma("sp", xt[:], Xsrc[:, tt * 512:(tt + 1) * 512].rearrange("(kc p) t -> p kc t", p=128), writes=[xn])
                norm_mod_tile(None, xt, xn, hT, "hT", tt * 512, coef[:, l, G["cond"], ia], coef[:, l, G["cond"], ish], tmp, sq, rs)
            S.barrier()
        return hT

    def stage_proj(g, l, hT):
        G = GR[g]
        T = G["T"]
        UT, UTM = D["UT_" + g], D["UTM_" + g]
        tm_groups = {1: [(0, 512, 0)], 2: [(0, 512, 512)], 3: [(0, 512, 1024)], 6: [(128, 128, 1536)], 8: [(288, 128, 1664)]}
        with ExitStack() as st:
            wb = [b.sb(st, "wb%d" % i, [128, 8, 512], BF16) for i in range(2)]
            ev = [b.sb(st, "ev%d" % i, [128, 512]) for i in range(4)]
            nev = 0
            for cg in range(17):
                ncol = 512 if cg < 16 else D_IN - 8192
                w, wn = wb[cg % 2], "wb%d" % (cg % 2)
                S.dma("pool", w[:, :, 0:ncol], D["w_in"][l, :, cg * 512:cg * 512 + ncol].rearrange("(kc p) c -> p kc c", p=128), writes=[wn])
                for tt in range(T // 512):
                    for cc in range((ncol + 127) // 128):
                        m = min(128, ncol - cc * 128)
                        pt, pn = b.ps()
                        for kc in range(8):
                            S.op("pe", lambda e: e.matmul(pt[0:m, :], w[:, kc, cc * 128:cc * 128 + m], hT[:, kc, tt * 512:(tt + 1) * 512],
                                                          start=(kc == 0), stop=(kc == 7)), reads=[wn, "hT"], writes=[pn], pe_acc=True)
                        e_, en = ev[nev % 4], "ev%d" % (nev % 4)
                        eng = "act" if nev % 2 == 0 else "dve"
                        nev += 1
                        if eng == "act":
                            S.op("act", lambda e: e.copy(e_[0:m, :], pt[0:m, :]), reads=[pn], writes=[en])
                        else:
                            S.op("dve", lambda e: e.tensor_copy(e_[0:m, :], pt[0:m, :]), reads=[pn], writes=[en])
                        r0 = cg * 512 + cc * 128
                        S.dma("sp", UT[r0:r0 + m, tt * 512:(tt + 1) * 512], e_[0:m, :], reads=[en], writes=[])
                for (c0, cn, dst) in tm_groups.get(cg, []):
                    for t4 in range(T // 128):
                        pt, pn = b.ps()
                        for kc in range(8):
                            S.op("pe", lambda e: e.matmul(pt[:, 0:cn], hT[:, kc, t4 * 128:(t4 + 1) * 128], w[:, kc, c0:c0 + cn],
                                                          start=(kc == 0), stop=(kc == 7)), reads=[wn, "hT"], writes=[pn], pe_acc=True)
                        e_, en = ev[nev % 4], "ev%d" % (nev % 4)
                        eng = "act" if nev % 2 == 0 else "dve"
                        nev += 1
                        if eng == "act":
                            S.op("act", lambda e: e.copy(e_[:, 0:cn], pt[:, 0:cn]), reads=[pn], writes=[en])
                        else:
                            S.op("dve", lambda e: e.tensor_copy(e_[:, 0:cn], pt[:, 0:cn]), reads=[pn], writes=[en])
                        S.dma("sp", UTM[t4 * 128:(t4 + 1) * 128, dst:dst + cn], e_[:, 0:cn], reads=[en], writes=[])
            S.barrier()

    def attn_unit(qT, qn, Kd, Nq, chunks, scale, o_out, on, wk, sink=None):
        po, pon = b.acc(0)
        pd, pdn = b.acc(1)
        nch = len(chunks)

        def emit_st(i):
            kT, kn, V, vn, nk, mask = chunks[i]
            pst, psn_ = b.ps()
            S.op("pe", lambda e: e.matmul(pst[0:nk, 0:Nq], kT, qT, start=True, stop=True), reads=[kn] + (qn if isinstance(qn, list) else [qn]), writes=[psn_])
            return pst, psn_

        cur = emit_st(0)
        for i, (kT, kn, V, vn, nk, mask) in enumerate(chunks):
            nxt = emit_st(i + 1) if i + 1 < nch else None
            pst, psn_ = cur
            E, En = wk["E"][i % 3], "E%d" % (i % 3)
            S.op("act", lambda e: e.activation(E[0:nk, 0:Nq], pst[0:nk, 0:Nq], AF.Exp, scale=scale), reads=[psn_], writes=[En])
            if mask is not None:
                S.op("pool", lambda e: e.tensor_tensor(E[0:nk, 0:Nq], E[0:nk, 0:Nq], mask, ALU.mult), reads=[En, "swamask"], writes=[En])
            last = (i == nch - 1) and sink is None
            S.op("pe", lambda e: e.matmul(po[0:64, 0:Nq], V, E[0:nk, 0:Nq], start=(i == 0), stop=(i == nch - 1)),
                 reads=[vn, En], writes=[pon], pe_acc=(i > 0))
            S.op("pe", lambda e: e.matmul(pd[0:64, 0:Nq], ones_bf[0:nk, 0:64], E[0:nk, 0:Nq], start=(i == 0), stop=last),
                 reads=["ones_bf", En], writes=[pdn], pe_acc=(i > 0))
            cur = nxt
        if sink is not None:
            S.op("pe", lambda e: e.matmul(pd[0:64, 0:Nq], ones_bf[0:1, 0:64], sink, start=False, stop=True),
                 reads=["ones_bf", "sinkrow"], writes=[pdn], pe_acc=True)
        rec = wk["rec"]
        S.op("dve", lambda e: e.reciprocal(rec[0:64, 0:Nq], pd[0:64, 0:Nq]), reads=[pdn], writes=["rec"])
        S.op("dve", lambda e: e.tensor_tensor(o_out, po[0:64, 0:Nq], rec[0:64, 0:Nq], ALU.mult), reads=[pon, "rec"], writes=[on])

    def attn_pair(qa, qan, qb, qbn, Nq, chunks, scale, o_out, on, wk, sink=None):
        po, pon = b.acc(0)
        pd, pdn = b.acc(1)
        nch = len(chunks)
        E = wk["E"]
        ne = len(E)

        def emit_st(i):
            kTa, kTb, kn, Va, Vb, vn, nk, mask = chunks[i]
            p1, n1 = b.ps()
            kna, knb = kn if isinstance(kn, tuple) else (kn, kn)
            S.op("pe", lambda e: e.matmul(p1[0:nk, 0:Nq], kTa, qa, start=True, stop=True), reads=[kna] + qan, writes=[n1])
            p2, n2 = b.ps()
            S.op("pe", lambda e: e.matmul(p2[0:nk, 0:Nq], kTb, qb, start=True, stop=True), reads=[knb] + qbn, writes=[n2])
            return (p1, n1, p2, n2)

        cur = emit_st(0)
        for i, (kTa, kTb, kn, Va, Vb, vn, nk, mask) in enumerate(chunks):
            nxt = emit_st(i + 1) if i + 1 < nch else None
            p1, n1, p2, n2 = cur
            Ea, Ean = E[(2 * i) % ne], "E%d" % ((2 * i) % ne)
            Eb, Ebn = E[(2 * i + 1) % ne], "E%d" % ((2 * i + 1) % ne)
            S.op("act", lambda e: e.activation(Ea[0:nk, 0:Nq], p1[0:nk, 0:Nq], AF.Exp, scale=scale), reads=[n1], writes=[Ean])
            S.op("act", lambda e: e.activation(Eb[0:nk, 0:Nq], p2[0:nk, 0:Nq], AF.Exp, scale=scale), reads=[n2], writes=[Ebn])
            if mask is not None:
                S.op("pool", lambda e: e.tensor_tensor(Ea[0:nk, 0:Nq], Ea[0:nk, 0:Nq], mask, ALU.mult), reads=[Ean, "swamask"], writes=[Ean])
                S.op("dve", lambda e: e.tensor_tensor(Eb[0:nk, 0:Nq], Eb[0:nk, 0:Nq], mask, ALU.mult), reads=[Ebn, "swamask"], writes=[Ebn])
            last = (i == nch - 1)
            S.op("pe", lambda e: e.matmul(po[:, 0:Nq], Va, Ea[0:nk, 0:Nq], start=(i == 0), stop=False), reads=[vn, Ean], writes=[pon], pe_acc=(i > 0))
            S.op("pe", lambda e: e.matmul(po[:, 0:Nq], Vb, Eb[0:nk, 0:Nq], start=False, stop=last), reads=[vn, Ebn], writes=[pon], pe_acc=True)
            S.op("pe", lambda e: e.matmul(pd[:, 0:Nq], onesAB[0:nk, 0, :], Ea[0:nk, 0:Nq], start=(i == 0), stop=False), reads=["onesAB", Ean], writes=[pdn], pe_acc=(i > 0))
            S.op("pe", lambda e: e.matmul(pd[:, 0:Nq], onesAB[0:nk, 1, :], Eb[0:nk, 0:Nq], start=False, stop=(last and sink is None)), reads=["onesAB", Ebn], writes=[pdn], pe_acc=True)
            cur = nxt
        if sink is not None:
            sa, sb_ = sink
            S.op("pe", lambda e: e.matmul(pd[:, 0:Nq], onesAB[0:1, 0, :], sa, start=False, stop=False), reads=["onesAB", "sinkrow"], writes=[pdn], pe_acc=True)
            S.op("pe", lambda e: e.matmul(pd[:, 0:Nq], onesAB[0:1, 1, :], sb_, start=False, stop=True), reads=["onesAB", "sinkrow"], writes=[pdn], pe_acc=True)
        rec = wk["rec"]
        S.op("dve", lambda e: e.reciprocal(rec[:, 0:Nq], pd[:, 0:Nq]), reads=[pdn], writes=["rec"])
        S.op("dve", lambda e: e.tensor_tensor(o_out, po[:, 0:Nq], rec[:, 0:Nq], ALU.mult), reads=[pon, "rec"], writes=[on])

    def rr_alloc(st, T, do_rope):
        sets = []
        for k in range(2):
            sets.append(dict(k=k, x=b.sb(st, "rr_x%d" % k, [128, T]), xs=(b.sb(st, "rr_xs%d" % k, [128, T]) if do_rope else None),
                             sq=b.sb(st, "rr_sq%d" % k, [128, 512], BF16), rs=b.sb(st, "rr_rs%d" % k, [128, T]), t1=b.sb(st, "rr_t1%d" % k, [128, 512])))
        return sets

    def rms_rope_rows(W_, p0, src_rows_fn, n_rows, T, gain2, gainn, do_norm, do_rope, rope_idx, out_bf, outn, out32=None, out32n=None,
                      out32_pre_rope=False):
        k = W_["k"]
        x, xs, sqb, rs, t1 = W_["x"], W_["xs"], W_["sq"], W_["rs"], W_["t1"]
        xn, xsn, sqn, rsn, t1n = ("rr_x%d" % k, "rr_xs%d" % k, "rr_sq%d" % k, "rr_rs%d" % k, "rr_t1%d" % k)
        for (r0, nr, ap) in src_rows_fn(False):
            S.dma("sp", x[p0 + r0:p0 + r0 + nr, 0:T], ap, writes=[xn])
        if do_rope:
            for (r0, nr, ap) in src_rows_fn(True):
                S.dma("act", xs[p0 + r0:p0 + r0 + nr, 0:T], ap, writes=[xsn])
        nr = n_rows
        pr = slice(p0, p0 + nr)
        W = min(512, T)
        for c in range(T // W):
            cs = slice(c * W, (c + 1) * W)
            if do_norm:
                S.op("act", lambda e: e.activation(sqb[pr, 0:W], x[pr, cs], AF.Square), reads=[xn], writes=[sqn])
                pt, pn = b.ps()
                S.op("pe", lambda e: e.matmul(pt[:, 0:W], ones_bf[pr, :], sqb[pr, 0:W], start=True, stop=True),
                     reads=[sqn, "ones_bf"], writes=[pn])
                S.op("act", lambda e: e.activation(rs[pr, cs], pt[pr, 0:W], AF.Sqrt, bias=epsb[pr, :], scale=1.0 / nr), reads=[pn, "epsb"], writes=[rsn])
                S.op("dve", lambda e: e.reciprocal(rs[pr, cs], rs[pr, cs]), reads=[rsn], writes=[rsn])
                S.op("dve", lambda e: e.scalar_tensor_tensor(x[pr, cs], x[pr, cs], gain2[pr, 0:1], rs[pr, cs], ALU.mult, ALU.mult),
                     reads=[xn, rsn, gainn], writes=[xn])
                if do_rope:
                    S.op("dve", lambda e: e.scalar_tensor_tensor(xs[pr, cs], xs[pr, cs], gain2[pr, 1:2], rs[pr, cs], ALU.mult, ALU.mult),
                         reads=[xsn, rsn, gainn], writes=[xsn])
            if out32 is not None and out32_pre_rope:
                S.op("pool", lambda e: e.tensor_copy(out32[pr, cs], x[pr, cs]), reads=[xn], writes=[out32n])
            if do_rope:
                S.op("dve", lambda e: e.tensor_tensor(x[pr, cs], x[pr, cs], rope[pr, rope_idx, cs], ALU.mult), reads=[xn, "rope"], writes=[xn])
                S.op("pool", lambda e: e.tensor_tensor(t1[pr, 0:W], xs[pr, cs], rope[pr, rope_idx + 1, cs], ALU.mult), reads=[xsn, "rope"], writes=[t1n])
                S.op("dve", lambda e: e.tensor_tensor(out_bf[:, cs], x[pr, cs], t1[pr, 0:W], ALU.add), reads=[xn, t1n], writes=[outn])
            else:
                S.op("act", lambda e: e.copy(out_bf[:, cs], x[pr, cs]), reads=[xn], writes=[outn])

    def swap_rows(base, T0, T1, UT, nd):
        q = nd // 4
        def f(swapped):
            if not swapped:
                return [(0, nd, UT[base:base + nd, T0:T1])]
            return [(0, q, UT[base + q:base + 2 * q, T0:T1]), (q, q, UT[base:base + q, T0:T1]),
                    (2 * q, q, UT[base + 3 * q:base + 4 * q, T0:T1]), (3 * q, q, UT[base + 2 * q:base + 3 * q, T0:T1])]
        return f

    rope = b.sb(cst, "rope", [128, 4, 2048], BF16)
    S.dma("pool", rope[:], D["rope64"].rearrange("a p t -> p a t"), writes=["rope"])

    def stage_gqa_like(g, l, kind):
        G = GR[g]
        T, L, P, nseq, do_rope = G["T"], G["L"], G["P"], G["nseq"], G["rope"]
        UT, UTM, OT = D["UT_" + g], D["UTM_" + g], D["OT_" + g]
        qo, ko = (OFF["sq"], OFF["sk"]) if kind == "swa" else (OFF["gq"], OFF["gk"])
        vcol = 1536 if kind == "swa" else 1664
        bi = 1 if kind == "swa" else 3
        do_norm = kind == "gqa"
        scale = 64 ** -0.5
        nkc_ctx = P // 128
        with ExitStack() as st:
            kT = b.sb(st, "kT", [128, P + T], BF16)
            Vt = b.sb(st, "Vt", [128, (P + T) // 128, 2, 128], BF16)
            qTh = b.sb(st, "qTh", [128, 8, T], BF16)
            gq2 = b.sb(st, "gq2", [128, 2]); gk2 = b.sb(st, "gk2", [128, 2])
            sinkrow = b.sb(st, "sinkrow", [1, 8, 128], BF16)
            sk32 = b.sb(st, "sk32", [1, 8])
            wk = dict(E=[b.sb(st, "E%d" % i, [128, 512], BF16) for i in range(6)], rec=b.sb(st, "rec", [128, 512]))
            obs = [b.sb(st, "ob%d" % i, [128, 512], BF16) for i in range(2)]
            swm = b.sb(st, "swamask", [128, 2, 512], BF16)
            S.dma("sp", swm[:], D["swamask"].rearrange("a p c -> p a c"), writes=["swamask"])
            S.op("pool", lambda e: e.memset(qTh[64:128, 0:4, :], 0.0), writes=["qTh%d" % hh for hh in range(4)])
            S.op("pool", lambda e: e.memset(qTh[0:64, 4:8, :], 0.0), writes=["qTh%d" % hh for hh in range(4, 8)])
            S.op("pool", lambda e: e.memset(Vt[:], 0.0), writes=["Vt"])
            if do_norm:
                S.dma("sp", gq2[:], D["gqa_qn"][l], writes=["gq2"])
                S.dma("sp", gk2[:], D["gqa_kn"][l], writes=["gk2"])
            else:
                S.dma("sp", sk32[:], D["sink"][l], writes=["sk32"])
                S.op("act", lambda e: e.activation(sk32[:], sk32[:], AF.Exp), reads=["sk32"], writes=["sk32"])
                S.op("dve", lambda e: e.tensor_copy(sinkrow[:], sk32[:].unsqueeze(2).to_broadcast([1, 8, 128])), reads=["sk32"], writes=["sinkrow"])
            with ExitStack() as s2:
                RR = rr_alloc(s2, T, do_rope)
                k32 = b.sb(s2, "k32", [128, T]) if g == "p" else None
                v32 = b.sb(s2, "v32", [128, (P + T) // 128, 128])
                if P:
                    src_k = D["c_swa_kT"] if kind == "swa" else D["c_gqa_kT"]
                    src_v = D["c_swa_v"] if kind == "swa" else D["c_gqa_v"]
                    S.dma("pool", kT[:, 0:P], src_k[l].rearrange("h d t -> (h d) t"), writes=["kT"])
                    S.dma("sp", v32[:, 0:P // 128, :], src_v[l].rearrange("(c p) f -> p c f", p=128), writes=["v32"])
                S.dma("sp", v32[:, P // 128:, :], UTM[:, vcol:vcol + 128].rearrange("(c p) f -> p c f", p=128), writes=["v32"])
                S.op("dve", lambda e: e.tensor_copy(Vt[:, :, 0, 0:64], v32[:, :, 0:64]), reads=["v32"], writes=["Vt"])
                S.op("dve", lambda e: e.tensor_copy(Vt[:, :, 1, 64:128], v32[:, :, 64:128]), reads=["v32"], writes=["Vt"])
                if g == "p":
                    dst = D["o_swa_v"] if kind == "swa" else D["o_gqa_v"]
                    S.dma("act", dst[l].rearrange("(c p) f -> p c f", p=128), v32[:], reads=["v32"], writes=[])
                nrr = 0
                for kvh in range(2):
                    p0 = kvh * 64
                    rms_rope_rows(RR[nrr % 2], p0, swap_rows(ko + kvh * 64, 0, T, UT, 64), 64, T, gk2, "gk2", do_norm, do_rope, 0,
                                  kT[p0:p0 + 64, P:P + T], "kT", out32=k32, out32n="k32", out32_pre_rope=True)
                    nrr += 1
                if g == "p":
                    dst = D["o_swa_kT"] if kind == "swa" else D["o_gqa_kT"]
                    S.dma("sp", dst[l], k32[:], reads=["k32"], writes=[])
                for h in range(8):
                    p0 = (h // 4) * 64
                    rms_rope_rows(RR[nrr % 2], p0, swap_rows(qo + h * 64, 0, T, UT, 64), 64, T, gq2, "gq2", do_norm, do_rope, 0,
                                  qTh[p0:p0 + 64, h, :], "qTh%d" % h)
                    nrr += 1
                nu = 0
                qna = ["qTh%d" % hh for hh in range(4)]
                qnb = ["qTh%d" % hh for hh in range(4, 8)]
                for sq_ in range(nseq):
                    T0 = sq_ * L
                    kb0 = P + T0
                    for qb in range(L // 128):
                        cols = [(c * 128, None) for c in range(nkc_ctx)]
                        if kind == "swa" and P:
                            cols += [(kb0 + kb * 128, mi) for (kb, mi) in ((qb - 1, 0), (qb, None), (qb + 1, 1)) if 0 <= kb < L // 128]
                        else:
                            cols += [(kb0 + kb * 128, None) for kb in range(L // 128)]
                        chunks = [(kT[:, c0:c0 + 128], kT[:, c0:c0 + 128], "kT", Vt[:, c0 // 128, 0, :], Vt[:, c0 // 128, 1, :], "Vt", 128,
                                   None if mi is None else swm[:, mi, :]) for (c0, mi) in cols]
                        ob, obn = obs[nu % 2], "ob%d" % (nu % 2)
                        nu += 1
                        q0 = T0 + qb * 128
                        attn_pair(qTh[:, 0:4, q0:q0 + 128], qna, qTh[:, 4:8, q0:q0 + 128], qnb, 512, chunks, scale, ob[:], obn, wk,
                                  sink=((sinkrow[0:1, 0:4, :], sinkrow[0:1, 4:8, :]) if kind == "swa" else None))
                        for kvh in range(2):
                            S.dma("sp" if kvh == 0 else "act", OT[bi, kvh * 256:(kvh + 1) * 256, q0:q0 + 128].rearrange("(h d) t -> d h t", d=64),
                                  ob[kvh * 64:(kvh + 1) * 64, :].rearrange("d (h t) -> d h t", h=4), reads=[obn], writes=[])
                S.barrier()

    def stage_mla(g, l):
        G = GR[g]
        T, L, P, nseq, do_rope = G["T"], G["L"], G["P"], G["nseq"], G["rope"]
        UT, OT = D["UT_" + g], D["OT_" + g]
        scale = 96 ** -0.5
        NK = P + L
        for sq_ in range(nseq):
            T0, T1 = sq_ * L, (sq_ + 1) * L
            with ExitStack() as st:
                ckvT = b.sb(st, "ckvT", [128, NK], BF16)
                krT = b.sb(st, "krT", [96, NK], BF16)
                cqn = b.sb(st, "cqn", [128, 2, L], BF16)
                wuq = b.sb(st, "wuq", [128, 2, 768], BF16); wuqs = b.sb(st, "wuqs", [128, 2, 768], BF16)
                wukv = b.sb(st, "wukv", [128, 1024], BF16)
                g_q = b.sb(st, "g_q", [128, 2]); g_kv = b.sb(st, "g_kv", [128, 1])
                S.dma("pool", wuq[:], D["w_uq"][l].rearrange("(kc p) c -> p kc c", p=128), writes=["wuq"])
                S.dma("pool", wuqs[:], D["w_uq_sw"][l].rearrange("(kc p) c -> p kc c", p=128), writes=["wuqs"])
                S.dma("pool", wukv[:], D["w_ukv"][l], writes=["wukv"])
                S.dma("sp", g_q[:], D["mla_qnT"][l], writes=["g_q"])
                S.dma("sp", g_kv[:], D["mla_kvn"][l], writes=["g_kv"])
                with ExitStack() as s2:
                    x = b.sb(s2, "m_x", [128, 2, L]); sqb = b.sb(s2, "m_sq", [128, 2, 512], BF16); rs = b.sb(s2, "m_rs", [128, 512])
                    xk = b.sb(s2, "m_xk", [128, L]); kr32 = b.sb(s2, "m_kr", [96, L]); krs = b.sb(s2, "m_krs", [96, L]); t1 = b.sb(s2, "m_t1", [96, 512])
                    if P:
                        S.dma("pool", ckvT[:, 0:P], D["c_ckvT"][l], writes=["ckvT"])
                        S.dma("pool", krT[64:96, 0:P], D["c_krT"][l], writes=["krT"])
                    S.dma("sp", x[:], UT[OFF["cq"]:OFF["cq"] + 256, T0:T1].rearrange("(kc p) t -> p kc t", p=128), writes=["m_x"])
                    S.dma("sp", xk[:], UT[OFF["ckv"]:OFF["ckv"] + 128, T0:T1], writes=["m_xk"])
                    S.dma("sp", kr32[64:96, :], UT[OFF["kr"]:OFF["kr"] + 32, T0:T1], writes=["m_kr"])
                    if do_rope:
                        for (r0, nr, ap) in swap_rows(OFF["kr"], T0, T1, UT, 32)(True):
                            S.dma("act", krs[64 + r0:64 + r0 + nr, :], ap, writes=["m_krs"])
                    for c in range(L // 512 if L >= 512 else 1):
                        w = min(512, L)
                        cs = slice(c * w, (c + 1) * w)
                        pt, pn = b.ps()
                        for kc in range(2):
                            S.op("act", lambda e: e.activation(sqb[:, kc, 0:w], x[:, kc, cs], AF.Square), reads=["m_x"], writes=["m_sq"])
                        for kc in range(2):
                            S.op("pe", lambda e: e.matmul(pt[:, 0:w], ones_bf[:], sqb[:, kc, 0:w], start=(kc == 0), stop=(kc == 1)),
                                 reads=["m_sq", "ones_bf"], writes=[pn], pe_acc=True)
                        rstd_from_sumsq(pt, pn, rs, "m_rs", 128, w, 1.0 / 256)
                        for kc in range(2):
                            S.op("dve", lambda e: e.scalar_tensor_tensor(cqn[:, kc, cs], x[:, kc, cs], g_q[:, kc:kc + 1], rs[:, 0:w], ALU.mult, ALU.mult),
                                 reads=["m_x", "m_rs", "g_q"], writes=["cqn"])
                        pt, pn = b.ps()
                        S.op("act", lambda e: e.activation(sqb[:, 0, 0:w], xk[:, cs], AF.Square), reads=["m_xk"], writes=["m_sq"])
                        S.op("pe", lambda e: e.matmul(pt[:, 0:w], ones_bf[:], sqb[:, 0, 0:w], start=True, stop=True), reads=["m_sq", "ones_bf"], writes=[pn])
                        rstd_from_sumsq(pt, pn, rs, "m_rs", 128, w, 1.0 / 128)
                        S.op("dve", lambda e: e.scalar_tensor_tensor(xk[:, cs], xk[:, cs], g_kv[:, 0:1], rs[:, 0:w], ALU.mult, ALU.mult),
                             reads=["m_xk", "m_rs", "g_kv"], writes=["m_xk"])
                        S.op("pool", lambda e: e.tensor_copy(ckvT[:, P + c * w:P + (c + 1) * w], xk[:, cs]), reads=["m_xk"], writes=["ckvT"])
                        if do_rope:
                            S.op("dve", lambda e: e.tensor_tensor(t1[64:96, 0:w], kr32[64:96, cs], rope[64:96, 2, cs], ALU.mult), reads=["m_kr", "rope"], writes=["m_t1"])
                            S.op("pool", lambda e: e.tensor_tensor(krs[64:96, cs], krs[64:96, cs], rope[64:96, 3, cs], ALU.mult), reads=["m_krs", "rope"], writes=["m_krs"])
                            S.op("dve", lambda e: e.tensor_tensor(krT[64:96, P + c * w:P + (c + 1) * w], t1[64:96, 0:w], krs[64:96, cs], ALU.add),
                                 reads=["m_t1", "m_krs"], writes=["krT"])
                        else:
                            S.op("dve", lambda e: e.tensor_copy(krT[64:96, P + c * w:P + (c + 1) * w], kr32[64:96, cs]), reads=["m_kr"], writes=["krT"])
                    if g == "p":
                        S.dma("sp", D["o_ckvT"][l, :, T0:T1], xk[:], reads=["m_xk"], writes=[])
                        S.dma("sp", D["o_krT"][l, :, T0:T1], kr32[64:96, :], reads=["m_kr"], writes=[])
                    S.barrier()
                Vall = b.sb(st, "Vall", [128, NK // 128, 8, 128], BF16)
                S.op("pool", lambda e: e.memset(Vall[:], 0.0), writes=["Vall"])
                for c in range(NK // 128):
                    pt, pn = b.ps()
                    S.op("pe", lambda e: e.matmul(pt[:, :], ckvT[:, c * 128:(c + 1) * 128],
                                                  wukv[:].rearrange("p (h x) -> p h x", x=128)[:, :, 64:128], start=True, stop=True),
                         reads=["ckvT", "wukv"], writes=[pn])
                    pv = pt[:, :].rearrange("p (h two d) -> p h two d", two=2, d=64)
                    S.op("act", lambda e: e.copy(Vall[:, c, 0::2, 0:64], pv[:, :, 0, :]), reads=[pn], writes=["Vall"])
                    S.op("dve", lambda e: e.tensor_copy(Vall[:, c, 1::2, 64:128], pv[:, :, 1, :]), reads=[pn], writes=["Vall"])
                S.barrier()
                with ExitStack() as s2:
                    wk = dict(E=[b.sb(s2, "E%d" % i, [128, 512], BF16) for i in range(6)], rec=b.sb(s2, "rec", [128, 512]))
                    kTh = [b.sb(s2, "kTh%d" % i, [128, NK], BF16) for i in range(4)]
                    qh = [b.sb(s2, "qh%d" % i, [128, L], BF16) for i in range(4)]
                    obs = [b.sb(s2, "ob%d" % i, [128, 512], BF16) for i in range(2)]
                    t2 = b.sb(s2, "m_t2", [96, 512]); t3 = b.sb(s2, "m_t3", [96, 512])
                    for i in range(4):
                        S.op("pool", lambda e: e.memset(kTh[i][96:128, :], 0.0), writes=["kTh%d" % i])
                        S.op("pool", lambda e: e.memset(qh[i][96:128, :], 0.0), writes=["qh%d" % i])
                    nu = 0
                    W = min(512, L)
                    for hp in range(4):
                        bufs = []
                        for h2 in range(2):
                            h = hp * 2 + h2
                            bi_ = (hp % 2) * 2 + h2
                            kt, ktn = kTh[bi_], "kTh%d" % bi_
                            q_, q_n = qh[bi_], "qh%d" % bi_
                            bufs.append((kt, ktn, q_, q_n))
                            for c in range((NK + 511) // 512):
                                w = min(512, NK - c * 512)
                                pt, pn = b.ps()
                                S.op("pe", lambda e: e.matmul(pt[:, 0:w], wukv[:, h * 128:(h + 1) * 128], ckvT[:, c * 512:c * 512 + w], start=True, stop=True),
                                     reads=["wukv", "ckvT"], writes=[pn])
                                S.op("act", lambda e: e.copy(kt[0:64, c * 512:c * 512 + w], pt[0:64, 0:w]), reads=[pn], writes=[ktn])
                            S.op("pool", lambda e: e.tensor_copy(kt[64:96, :], krT[64:96, :]), reads=["krT"], writes=[ktn])
                            for c in range(L // W):
                                cs = slice(c * W, (c + 1) * W)
                                pt, pn = b.ps()
                                for kc in range(2):
                                    S.op("pe", lambda e: e.matmul(pt[0:96, 0:W], wuq[:, kc, h * 96:(h + 1) * 96], cqn[:, kc, cs], start=(kc == 0), stop=(kc == 1)),
                                         reads=["wuq", "cqn"], writes=[pn], pe_acc=(kc > 0))
                                S.op("act", lambda e: e.copy(q_[0:64, cs], pt[0:64, 0:W]), reads=[pn], writes=[q_n])
                                if do_rope:
                                    pt2, pn2 = b.ps()
                                    for kc in range(2):
                                        S.op("pe", lambda e: e.matmul(pt2[0:96, 0:W], wuqs[:, kc, h * 96:(h + 1) * 96], cqn[:, kc, cs], start=(kc == 0), stop=(kc == 1)),
                                             reads=["wuqs", "cqn"], writes=[pn2], pe_acc=(kc > 0))
                                    S.op("dve", lambda e: e.tensor_tensor(t2[64:96, 0:W], pt[64:96, 0:W], rope[64:96, 2, cs], ALU.mult), reads=[pn, "rope"], writes=["m_t2"])
                                    S.op("dve", lambda e: e.tensor_tensor(t3[64:96, 0:W], pt2[64:96, 0:W], rope[64:96, 3, cs], ALU.mult), reads=[pn2, "rope"], writes=["m_t3"])
                                    S.op("pool", lambda e: e.tensor_tensor(q_[64:96, cs], t2[64:96, 0:W], t3[64:96, 0:W], ALU.add), reads=["m_t2", "m_t3"], writes=[q_n])
                                else:
                                    S.op("dve", lambda e: e.tensor_copy(q_[64:96, cs], pt[64:96, 0:W]), reads=[pn], writes=[q_n])
                        (kta, ktan, qa, qan), (ktb, ktbn, qb_, qbn) = bufs
                        for c in range(L // W):
                            chunks = [(kta[:, kc * 128:(kc + 1) * 128], ktb[:, kc * 128:(kc + 1) * 128], (ktan, ktbn), Vall[:, kc, hp * 2, :], Vall[:, kc, hp * 2 + 1, :], "Vall", 128, None)
                                      for kc in range(NK // 128)]
                            ob, obn = obs[nu % 2], "ob%d" % (nu % 2)
                            nu += 1
                            attn_pair(qa[:, c * W:(c + 1) * W], [qan], qb_[:, c * W:(c + 1) * W], [qbn], W, chunks, scale, ob[:, 0:W], obn, wk)
                            S.dma("sp", OT[2, hp * 128:(hp + 1) * 128, T0 + c * W:T0 + (c + 1) * W], ob[:, 0:W], reads=[obn], writes=[])
                    S.barrier()

    def stage_hgrn(g, l):
        G = GR[g]
        T, L, P, nseq = G["T"], G["L"], G["P"], G["nseq"]
        UT, UTM, OT = D["UT_" + g], D["UTM_" + g], D["OT_" + g]
        NTL = L // 128
        ident, triU, triL, sL, sU, csel = (cm[:, i, :] for i in range(6))
        with ExitStack() as st:
            lbb = b.sb(st, "lbb", [128, 2, 512]); oml = b.sb(st, "oml", [128, 2, 512])
            with ExitStack() as s2:
                lbr = b.sb(s2, "lbr", [128, DEPTH, 2, 512]); den = b.sb(s2, "lden", [128, 2, 512])
                S.dma("sp", lbr[:], D["lbT"].partition_broadcast(128), writes=["lbr"])
                S.op("act", lambda e: e.activation(lbr[:], lbr[:], AF.Exp), reads=["lbr"], writes=["lbr"])
                S.op("dve", lambda e: e.tensor_tensor(den[:], lbr[:, 0], lbr[:, 1], ALU.add), reads=["lbr"], writes=["lden"])
                S.op("dve", lambda e: e.reciprocal(den[:], den[:]), reads=["lden"], writes=["lden"])
                if l == 0:
                    S.op("pool", lambda e: e.memset(lbb[:], 0.0), writes=["lbb"])
                else:
                    S.op("dve", lambda e: e.tensor_tensor(lbb[:], lbr[:, 1], den[:], ALU.mult), reads=["lbr", "lden"], writes=["lbb"])
                S.op("dve", lambda e: e.tensor_scalar(oml[:], lbb[:], -1.0, 1.0, ALU.mult, ALU.add), reads=["lbb"], writes=["oml"])
                S.barrier()
            hn = b.sb(st, "hn", [128, 4]); S.dma("sp", hn[:], D["hgrn_normT"][l], writes=["hn"])
            Sst = [b.sb(st, "Sst%d" % i, [128, 2, 4, 128]) for i in range(2)]
            oall = b.sb(st, "oall", [128, 2, 4, L], BF16)
            W_ = {}
            for d in range(2):
                for k in range(2):
                    sfx = "%d%d" % (d, k)
                    W_[d, k] = dict(
                        sfx=sfx,
                        tt=b.sb(st, "h_t" + sfx, [128, 512]), vv=b.sb(st, "h_v" + sfx, [128, 512]), gg=b.sb(st, "h_g" + sfx, [128, 512]),
                        kt=b.sb(st, "h_kt" + sfx, [128, 512]), kh=b.sb(st, "h_kh" + sfx, [128, 512]),
                        khm=b.sb(st, "h_khm" + sfx, [128, 4, 4, 128]), qT=b.sb(st, "h_qT" + sfx, [128, 4, 128]), qt=b.sb(st, "h_qt" + sfx, [128, 4, 128]),
                        eb=b.sb(st, "h_eb" + sfx, [128, 4, 128]), ktT=b.sb(st, "h_ktT" + sfx, [128, 4, 128]), AT=b.sb(st, "h_AT" + sfx, [128, 4, 128]),
                        tmpo=b.sb(st, "h_tmpo" + sfx, [128, 512]))
            etot = [b.sb(st, "h_etot%d" % k, [128, 2, 4, 4]) for k in range(2)]
            fin = dict(os=b.sb(st, "f_os", [128, 4, 256]), gT=b.sb(st, "f_gT", [128, 4, 256]), sq=b.sb(st, "f_sq", [128, 4, 256], BF16),
                       rs=b.sb(st, "f_rs", [128, 4, 256]), ob=b.sb(st, "f_ob", [128, 4, 256], BF16))
            b.nrot = 4
            M4 = lambda M: M.unsqueeze(1).to_broadcast([128, 4, 128])

            def sstn(sp, d, h):
                return "Sst%d_%d%d" % (sp, d, h)

            def partA(rec):
                k = rec["k"]
                for d in range(2):
                    w = W_[d, k]; x = w["sfx"]
                    t0 = rec["T0"] + rec["ti"][d] * 128
                    S.dma("sp", w["tt"][:], UTM[t0:t0 + 128, d * 512:(d + 1) * 512], writes=["h_t" + x])
                    S.dma("sp", w["vv"][:], UTM[t0:t0 + 128, 1024:1536], writes=["h_v" + x])
                    S.dma("act", w["qT"][:], UT[0:512, t0:t0 + 128].rearrange("(h p) t -> p h t", p=128), writes=["h_qT" + x])
                for d in range(2):
                    w = W_[d, k]; x = w["sfx"]
                    S.op("act", lambda e: e.activation(w["tt"][:], w["tt"][:], AF.Sigmoid), reads=["h_t" + x], writes=["h_t" + x])
                for d in range(2):
                    w = W_[d, k]; x = w["sfx"]
                    S.op("act", lambda e: e.activation(w["qT"][:], w["qT"][:], AF.Silu), reads=["h_qT" + x], writes=["h_qT" + x])
                for d in range(2):
                    w = W_[d, k]; x = w["sfx"]
                    S.op("dve", lambda e: e.tensor_tensor(w["tt"][:], w["tt"][:], oml[:, d], ALU.mult), reads=["h_t" + x, "oml"], writes=["h_t" + x])
                    S.op("dve", lambda e: e.scalar_tensor_tensor(w["gg"][:], w["tt"][:], 1e-30, lbb[:, d], ALU.max, ALU.add), reads=["h_t" + x, "lbb"], writes=["h_g" + x])
                for d in range(2):
                    w = W_[d, k]; x = w["sfx"]
                    S.op("act", lambda e: e.activation(w["gg"][:], w["gg"][:], AF.Ln), reads=["h_g" + x], writes=["h_g" + x])
                for d in range(2):
                    w = W_[d, k]; x = w["sfx"]
                    S.op("dve", lambda e: e.tensor_tensor(w["tt"][:], oml[:, d], w["tt"][:], ALU.subtract), reads=["h_t" + x, "oml"], writes=["h_t" + x])

            def partB(rec, stage):
                k = rec["k"]
                Ms = [(triU, sL), (triL, sU)]
                if stage == 0:
                    pbs = []
                    for d in range(2):
                        w = W_[d, k]; x = w["sfx"]
                        pb, pbn = b.ps()
                        S.op("pe", lambda e: e.matmul(pb[:], Ms[d][0], w["gg"][:], start=True, stop=True), reads=["cm", "h_g" + x], writes=[pbn])
                        pr, prn = b.ps()
                        S.op("pe", lambda e: e.matmul(pr[:], Ms[d][1], w["gg"][:], start=True, stop=True), reads=["cm", "h_g" + x], writes=[prn])
                        pbs.append((pb, pbn, pr, prn))
                    for d in range(2):
                        w = W_[d, k]; x = w["sfx"]
                        pb, pbn, pr, prn = pbs[d]
                        S.op("act", lambda e: e.activation(w["kt"][:], pb[:], AF.Exp, scale=-1.0), reads=[pbn], writes=["h_kt" + x])
                        S.op("act", lambda e: e.activation(w["kh"][:], pr[:], AF.Exp), reads=[prn], writes=["h_kh" + x])
                    pts = []
                    ptot, ptotn = b.ps()
                    for d in range(2):
                        w = W_[d, k]; x = w["sfx"]
                        pbt, pbtn = b.ps()
                        for h in range(4):
                            hs = slice(h * 128, (h + 1) * 128)
                            S.op("pe", lambda e: e.matmul(pbt[:, hs], w["gg"][:, hs], Ms[d][0], start=True, stop=True), reads=["h_g" + x, "cm"], writes=[pbtn], pe_acc=(h > 0))
                            S.op("pe", lambda e: e.matmul(ptot[:, d * 16 + h * 4:d * 16 + h * 4 + 4], w["gg"][:, hs], csel[:, 0:4], start=True, stop=True),
                                 reads=["h_g" + x, "cm"], writes=[ptotn], pe_acc=(d > 0 or h > 0))
                        pts.append((pbt, pbtn))
                    for d in range(2):
                        w = W_[d, k]; x = w["sfx"]
                        S.op("dve", lambda e: e.tensor_tensor(w["kt"][:], w["kt"][:], w["tt"][:], ALU.mult), reads=["h_kt" + x, "h_t" + x], writes=["h_kt" + x])
                        S.op("dve", lambda e: e.tensor_tensor(w["kh"][:], w["kh"][:], w["tt"][:], ALU.mult), reads=["h_kh" + x, "h_t" + x], writes=["h_kh" + x])
                    for d in range(2):
                        w = W_[d, k]; x = w["sfx"]
                        pbt, pbtn = pts[d]
                        S.op("act", lambda e: e.activation(w["eb"][:], pbt[:].rearrange("p (h t) -> p h t", h=4), AF.Exp), reads=[pbtn], writes=["h_eb" + x])
                    S.op("act", lambda e: e.activation(etot[k][:], ptot[:, 0:32].rearrange("p (d h j) -> p d h j", d=2, h=4), AF.Exp), reads=[ptotn], writes=["h_etot%d" % k])
                    for d in range(2):
                        w = W_[d, k]; x = w["sfx"]
                        S.op("dve", lambda e: e.tensor_tensor(w["qt"][:], w["qT"][:], w["eb"][:], ALU.mult), reads=["h_qT" + x, "h_eb" + x], writes=["h_qt" + x])
                    for d in range(2):
                        w = W_[d, k]; x = w["sfx"]
                        for h in range(4):
                            for j in range(4):
                                S.op("pool", lambda e: e.tensor_scalar(w["khm"][:, h, j, :], w["kh"][:, h * 128:(h + 1) * 128], csel[:, j:j + 1], None, ALU.mult),
                                     reads=["h_kh" + x, "cm"], writes=["h_khm" + x])
                elif stage == 1:
                    for d in range(2):
                        w = W_[d, k]; x = w["sfx"]
                        pk, pkn = b.ps()
                        for h in range(4):
                            hs = slice(h * 128, (h + 1) * 128)
                            S.op("pe", lambda e: e.matmul(pk[:, hs], w["kt"][:, hs], ident, start=True, stop=True), reads=["h_kt" + x, "cm"], writes=[pkn], pe_acc=(h > 0))
                        S.op("act", lambda e: e.copy(w["ktT"][:], pk[:].rearrange("p (h t) -> p h t", h=4)), reads=[pkn], writes=["h_ktT" + x])
                elif stage == 2:
                    for d in range(2):
                        w = W_[d, k]; x = w["sfx"]
                        pa, pan = b.ps()
                        for h in range(4):
                            hs = slice(h * 128, (h + 1) * 128)
                            S.op("pe", lambda e: e.matmul(pa[:, hs], w["ktT"][:, h, :], w["qt"][:, h, :], start=True, stop=True), reads=["h_ktT" + x, "h_qt" + x], writes=[pan], pe_acc=(h > 0))
                        S.op("dve", lambda e: e.tensor_tensor(w["AT"][:], pa[:].rearrange("p (h t) -> p h t", h=4), M4(Ms[d][0]), ALU.mult), reads=[pan, "cm"], writes=["h_AT" + x])
                else:
                    for d in range(2):
                        w = W_[d, k]; x = w["sfx"]
                        po_, pon_ = b.acc(d)
                        for h in range(4):
                            hs = slice(h * 128, (h + 1) * 128)
                            S.op("pe", lambda e: e.matmul(po_[:, hs], w["vv"][:, hs], w["AT"][:, h, :], start=True, stop=True), reads=["h_v" + x, "h_AT" + x], writes=[pon_], pe_acc=(h > 0))
                        S.op("act", lambda e: e.copy(w["tmpo"][:], po_[:]), reads=[pon_], writes=["h_tmpo" + x])

            def chunk_step(rec, n_):
                k = rec["k"]; sp = rec["sp"]
                pss = []
                for d in range(2):
                    w = W_[d, k]; x = w["sfx"]
                    j = n_ if d == 0 else 3 - n_
                    js = slice(j * 32, (j + 1) * 32)
                    pi_, pin_ = b.acc(2 + d)
                    ps_, psn2 = b.ps()
                    pss.append((ps_, psn2, j))
                    for h in range(4):
                        hs = slice(h * 128, (h + 1) * 128)
                        S.op("pe", lambda e: e.matmul(pi_[:, h * 128 + j * 32:h * 128 + (j + 1) * 32], Sst[sp][:, d, h, :], w["qt"][:, h, js], start=True, stop=True),
                             reads=[sstn(sp, d, h), "h_qt" + x], writes=[pin_], pe_acc=(n_ > 0 or h > 0))
                        S.op("pe", lambda e: e.matmul(ps_[:, hs], w["khm"][:, h, j, :], w["vv"][:, hs], start=True, stop=True),
                             reads=["h_khm" + x, "h_v" + x], writes=[psn2], pe_acc=(h > 0))
                for d in range(2):
                    w = W_[d, k]; x = w["sfx"]
                    ps_, psn2, j = pss[d]
                    for h in range(4):
                        hs = slice(h * 128, (h + 1) * 128)
                        S.op("dve", lambda e: e.scalar_tensor_tensor(Sst[sp][:, d, h, :], Sst[sp][:, d, h, :], etot[k][:, d, h, j:j + 1], ps_[:, hs], ALU.mult, ALU.add),
                             reads=[psn2, "h_etot%d" % k, sstn(sp, d, h)], writes=[sstn(sp, d, h)])

            def finish(rec):
                k = rec["k"]
                for d in range(2):
                    w = W_[d, k]; x = w["sfx"]
                    pi_, pin_ = b.acc(2 + d)
                    tl = rec["ti"][d] * 128
                    S.op("dve", lambda e: e.tensor_tensor(oall[:, d, :, tl:tl + 128], w["tmpo"][:].rearrange("p (h t) -> p h t", h=4),
                                                          pi_[:].rearrange("p (h t) -> p h t", h=4), ALU.add),
                         reads=[pin_, "h_tmpo" + x], writes=["oall"])

            def seq_final(sq_):
                T0 = sq_ * L
                sp = sq_ % 2
                for c in range(L // 256):
                    cs = slice(c * 256, (c + 1) * 256)
                    os_, gT, sq, rs, ob = fin["os"], fin["gT"], fin["sq"], fin["rs"], fin["ob"]
                    S.dma("act", gT[:], UT[OFF["hg"]:OFF["hg"] + 512, T0 + c * 256:T0 + (c + 1) * 256].rearrange("(h p) t -> p h t", p=128), writes=["f_gT"])
                    S.op("act", lambda e: e.activation(gT[:], gT[:], AF.Silu), reads=["f_gT"], writes=["f_gT"])
                    S.op("dve", lambda e: e.tensor_tensor(os_[:], oall[:, 0, :, cs], oall[:, 1, :, cs], ALU.add), reads=["oall"], writes=["f_os"])
                    S.op("act", lambda e: e.activation(sq[:], os_[:], AF.Square), reads=["f_os"], writes=["f_sq"])
                    for hh in range(2):
                        pn_, pnn = b.ps()
                        for h2 in range(2):
                            h = hh * 2 + h2
                            S.op("pe", lambda e: e.matmul(pn_[:, h2 * 256:(h2 + 1) * 256], ones_bf[:], sq[:, h, :], start=True, stop=True),
                                 reads=["f_sq", "ones_bf"], writes=[pnn], pe_acc=(h2 > 0))
                        rstd_from_sumsq(pn_, pnn, rs[:, hh * 2:hh * 2 + 2, :].rearrange("p a t -> p (a t)"), "f_rs", 128, 512, 1.0 / 128)
                    for h in range(4):
                        S.op("dve", lambda e: e.scalar_tensor_tensor(os_[:, h, :], os_[:, h, :], hn[:, h:h + 1], rs[:, h, :], ALU.mult, ALU.mult),
                             reads=["f_os", "hn", "f_rs"], writes=["f_os"])
                    S.op("dve", lambda e: e.tensor_tensor(ob[:], os_[:], gT[:], ALU.mult), reads=["f_os", "f_gT"], writes=["f_ob"])
                    S.dma("sp", OT[0, :, T0 + c * 256:T0 + (c + 1) * 256].rearrange("(h p) t -> p h t", p=128), ob[:], reads=["f_ob"], writes=[])
                if g == "p":
                    S.dma("sp", D["o_hgrn"][l, sq_].rearrange("d h k v -> k d h v"), Sst[sp][:], reads=[sstn(sp, d_, h_) for d_ in range(2) for h_ in range(4)], writes=[])

            recs = []
            for sq_ in range(nseq):
                for i in range(NTL):
                    recs.append(dict(sq=sq_, sp=sq_ % 2, T0=sq_ * L, i=i, ti=(i, NTL - 1 - i), k=len(recs) % 2))
            partA(recs[0])
            for stg in range(4):
                partB(recs[0], stg)
            for n, rec in enumerate(recs):
                nxt = recs[n + 1] if n + 1 < len(recs) else None
                if rec["i"] == 0:
                    sp = rec["sp"]
                    names = [sstn(sp, d_, h_) for d_ in range(2) for h_ in range(4)]
                    if P:
                        S.dma("sp", Sst[sp][:], D["st_hgrn"][l].rearrange("d h k v -> k d h v"), writes=names)
                    else:
                        S.op("pool", lambda e: e.memset(Sst[sp][:], 0.0), writes=names)
                if nxt is not None:
                    partA(nxt)
                for n_ in range(4):
                    chunk_step(rec, n_)
                    if nxt is not None:
                        partB(nxt, n_)
                finish(rec)
                if rec["i"] == NTL - 1:
                    seq_final(rec["sq"])
            b.nrot = 6
            S.barrier()

    def epilogue(st, z, zn, xt, xn, gco, Xdst, c0, wkn):
        sq, rs, tmp = wkn
        for kc in range(8):
            S.op("act", lambda e: e.activation(sq[:, kc, :], z[:, kc, :], AF.Square), reads=[zn], writes=["e_sq"])
        pt, pn = b.ps()
        for kc in range(8):
            S.op("pe", lambda e: e.matmul(pt[:], ones_bf[:], sq[:, kc, :], start=(kc == 0), stop=(kc == 7)), reads=["e_sq", "ones_bf"], writes=[pn], pe_acc=True)
        rstd_from_sumsq(pt, pn, rs, "e_rs", 128, 512, 1.0 / DM)
        for kc in range(8):
            S.op("dve", lambda e: e.tensor_tensor(tmp[:], z[:, kc, :], rs[:], ALU.mult), reads=[zn, "e_rs"], writes=["e_tmp"])
            S.op("dve", lambda e: e.scalar_tensor_tensor(xt[:, kc, :], tmp[:], gco[:, kc:kc + 1], xt[:, kc, :], ALU.mult, ALU.add),
                 reads=["e_tmp", "coef", xn], writes=[xn])
        S.dma("sp", Xdst[:, c0:c0 + 512].rearrange("(kc p) t -> p kc t", p=128), xt[:], reads=[xn], writes=[])

    def stage_merge(g, l, Xsrc, Xdst):
        G = GR[g]
        T = G["T"]
        UT, OT = D["UT_" + g], D["OT_" + g]
        with ExitStack() as st:
            Wb = b.sb(st, "Wb", [128, 4, 4, 1024], BF16)
            Wo = b.sb(st, "Wo", [128, 8, 1024], BF16)
            S.dma("pool", Wb[:], D["w_branch"][l].rearrange("n (kc p) c -> p n kc c", p=128), writes=["Wb"])
            S.dma("pool", Wo[:], D["w_out"][l].rearrange("(kc p) c -> p kc c", p=128), writes=["Wo"])
            ot = b.sb(st, "ot", [128, 4, 4, 512], BF16)
            yp = b.sb(st, "yp", [128, 8, 512], BF16)
            gts = [b.sb(st, "gt%d" % i, [128, 512]) for i in range(3)]
            accf = b.sb(st, "accf", [128, 512]); tm2 = b.sb(st, "tm2", [128, 512])
            z = b.sb(st, "z", [128, 8, 512]); xt = b.sb(st, "xt", [128, 8, 512])
            wkn = (b.sb(st, "e_sq", [128, 8, 512], BF16), b.sb(st, "e_rs", [128, 512]), b.sb(st, "e_tmp", [128, 512]))
            ng = 0
            for tt in range(T // 512):
                cs = slice(tt * 512, (tt + 1) * 512)
                S.dma("sp", ot[:], OT[:, :, cs].rearrange("n (kc p) t -> p n kc t", p=128), writes=["ot"])
                S.dma("act", xt[:], Xsrc[:, cs].rearrange("(kc p) t -> p kc t", p=128), writes=["xt"])
                for dmc in range(8):
                    for n in range(4):
                        gt, gtn = gts[ng % 3], "gt%d" % (ng % 3)
                        ng += 1
                        r0 = OFF["gates"] + n * 1024 + dmc * 128
                        S.dma("sp", gt[:], UT[r0:r0 + 128, cs], writes=[gtn])
                        S.op("act", lambda e: e.activation(gt[:], gt[:], AF.Sigmoid), reads=[gtn], writes=[gtn])
                        pt, pn = b.ps()
                        for kc in range(4):
                            S.op("pe", lambda e: e.matmul(pt[:], Wb[:, n, kc, dmc * 128:(dmc + 1) * 128], ot[:, n, kc, :], start=(kc == 0), stop=(kc == 3)),
                                 reads=["Wb", "ot"], writes=[pn], pe_acc=True)
                        if n == 0:
                            S.op("dve", lambda e: e.tensor_tensor(accf[:], pt[:], gt[:], ALU.mult), reads=[pn, gtn], writes=["accf"])
                        elif n < 3:
                            S.op("dve", lambda e: e.tensor_tensor(tm2[:], pt[:], gt[:], ALU.mult), reads=[pn, gtn], writes=["tm2"])
                            S.op("dve", lambda e: e.tensor_tensor(accf[:], accf[:], tm2[:], ALU.add), reads=["tm2", "accf"], writes=["accf"])
                        else:
                            S.op("dve", lambda e: e.tensor_tensor(tm2[:], pt[:], gt[:], ALU.mult), reads=[pn, gtn], writes=["tm2"])
                            S.op("dve", lambda e: e.tensor_tensor(yp[:, dmc, :], accf[:], tm2[:], ALU.add), reads=["tm2", "accf"], writes=["yp"])
                for oc in range(8):
                    pt, pn = b.ps()
                    for kc in range(8):
                        S.op("pe", lambda e: e.matmul(pt[:], Wo[:, kc, oc * 128:(oc + 1) * 128], yp[:, kc, :], start=(kc == 0), stop=(kc == 7)),
                             reads=["Wo", "yp"], writes=[pn], pe_acc=True)
                    S.op("act", lambda e: e.copy(z[:, oc, :], pt[:]), reads=[pn], writes=["z"])
                epilogue(st, z, "z", xt, "xt", coef[:, l, G["cond"], 2], Xdst, tt * 512, wkn)
            S.barrier()

    def stage_mlp(g, l, Xsrc, Xdst):
        G = GR[g]
        T = G["T"]
        with ExitStack() as st:
            h2 = stage_h(st, g, l, Xsrc, 3, 4)
            W2 = [b.sb(st, "W2_%d" % i, [128, 8, 1024], BF16) for i in range(2)]
            n2 = 0
            w1 = [b.sb(st, "w1_%d" % i, [128, 8, 512], BF16) for i in range(2)]
            hid = b.sb(st, "hid", [128, 32, 512], BF16)
            rl = [b.sb(st, "rl%d" % i, [128, 512]) for i in range(2)]
            z = b.sb(st, "z", [128, 8, 512]); xt = b.sb(st, "xt", [128, 8, 512])
            wkn = (b.sb(st, "e_sq", [128, 8, 512], BF16), b.sb(st, "e_rs", [128, 512]), b.sb(st, "e_tmp", [128, 512]))
            nw = 0
            nr = 0
            for tt in range(T // 512):
                cs = slice(tt * 512, (tt + 1) * 512)
                S.dma("act", xt[:], Xsrc[:, cs].rearrange("(kc p) t -> p kc t", p=128), writes=["xt"])
                for cg in range(8):
                    w, wn = w1[nw % 2], "w1_%d" % (nw % 2)
                    nw += 1
                    S.dma("pool", w[:], D["w_mlp_in"][l, :, cg * 512:(cg + 1) * 512].rearrange("(kc p) c -> p kc c", p=128), writes=[wn])
                    for cc in range(4):
                        pt, pn = b.ps()
                        for kc in range(8):
                            S.op("pe", lambda e: e.matmul(pt[:], w[:, kc, cc * 128:(cc + 1) * 128], h2[:, kc, cs], start=(kc == 0), stop=(kc == 7)),
                                 reads=[wn, "hT"], writes=[pn], pe_acc=True)
                        r, rn = rl[nr % 2], "rl%d" % (nr % 2)
                        nr += 1
                        S.op("act", lambda e: e.activation(r[:], pt[:], AF.Relu), reads=[pn], writes=[rn])
                        S.op("dve", lambda e: e.tensor_tensor(hid[:, cg * 4 + cc, :], r[:], r[:], ALU.mult), reads=[rn], writes=["hid"])
                for q4 in range(4):
                    w2, w2n = W2[n2 % 2], "W2_%d" % (n2 % 2)
                    n2 += 1
                    S.dma("pool", w2[:], D["w_mlp_out"][l, q4 * 1024:(q4 + 1) * 1024, :].rearrange("(kc p) c -> p kc c", p=128), writes=[w2n])
                    for oc in range(8):
                        pt, pn = b.ps()
                        for fc in range(8):
                            S.op("pe", lambda e: e.matmul(pt[:], w2[:, fc, oc * 128:(oc + 1) * 128], hid[:, q4 * 8 + fc, :], start=(fc == 0), stop=(fc == 7)),
                                 reads=[w2n, "hid"], writes=[pn], pe_acc=True)
                        if q4 == 0:
                            S.op("act", lambda e: e.copy(z[:, oc, :], pt[:]), reads=[pn], writes=["z%d" % oc, "z"])
                        else:
                            S.op("dve", lambda e: e.tensor_tensor(z[:, oc, :], z[:, oc, :], pt[:], ALU.add), reads=[pn, "z%d" % oc], writes=["z%d" % oc, "z"])
                epilogue(st, z, "z", xt, "xt", coef[:, l, G["cond"], 5], Xdst, tt * 512, wkn)
            S.barrier()

    for g in ("s", "p"):
        X = D["xT_" + g]
        for l in range(DEPTH):
            if "proj" in STAGES:
                with ExitStack() as st:
                    hT = stage_h(st, g, l, X, 0, 1)
                    stage_proj(g, l, hT)
            if "hgrn" in STAGES:
                stage_hgrn(g, l)
            if "swa" in STAGES:
                stage_gqa_like(g, l, "swa")
            if "mla" in STAGES:
                stage_mla(g, l)
            if "gqa" in STAGES:
                stage_gqa_like(g, l, "gqa")
            if "merge" in STAGES:
                stage_merge(g, l, X, D["X1_" + g])
            Xn = D["yT_" + g] if l == DEPTH - 1 else D["X2_%d_%s" % (l, g)]
            if "mlp" in STAGES:
                stage_mlp(g, l, D["X1_" + g], Xn)
            X = Xn
    S.barrier()
    return nc, b


_CACHE = {}


def _consts():
    nf = 16
    t = np.arange(2048)
    row, col = (t // 64).astype(np.float32), (t % 64).astype(np.float32)
    rope = np.zeros((4, 128, 2048), np.float32)
    def fill(ci, si, r0, nf):
        inv = (10000.0 ** (-np.arange(nf, dtype=np.float32) / nf)).astype(np.float32)
        ar = (row[None, :] * inv[:, None]).astype(np.float32)
        ac = (col[None, :] * inv[:, None]).astype(np.float32)
        for k, a in enumerate((ar, ac)):
            b0 = r0 + k * 2 * nf
            rope[ci, b0:b0 + nf] = np.cos(a); rope[ci, b0 + nf:b0 + 2 * nf] = np.cos(a)
            rope[si, b0:b0 + nf] = -np.sin(a); rope[si, b0 + nf:b0 + 2 * nf] = np.sin(a)
    fill(0, 1, 0, 16)
    fill(0, 1, 64, 16)
    fill(2, 3, 64, 8)
    s_ = np.arange(128)[:, None]; t_ = np.arange(128)[None, :]
    same = (s_ // 32) == (t_ // 32)
    cm = np.zeros((6, 128, 128), np.float32)
    cm[0] = np.eye(128)
    cm[1] = same & (s_ <= t_)
    cm[2] = same & (s_ >= t_)
    cm[3] = same & (s_ > t_)
    cm[4] = same & (s_ < t_)
    cm[5][:, 0:4] = (s_ // 32) == np.arange(4)[None, :]
    j = np.arange(128)[:, None]; i = (np.arange(512) % 128)[None, :]
    swm = np.stack([(j >= i), (j <= i)]).astype(np.float32).astype(ml_dtypes.bfloat16)
    return rope, cm, swm


def _perm(nd):
    q = nd // 4
    return np.concatenate([np.arange(q, 2 * q), np.arange(0, q), np.arange(3 * q, 4 * q), np.arange(2 * q, 3 * q)])


def kernel(**inp):
    f = lambda a: np.ascontiguousarray(np.asarray(a, dtype=np.float32))
    I = {k: f(v) for k, v in inp.items()}
    if "prog" not in _CACHE:
        _CACHE["prog"] = build_program()
    nc, b = _CACHE["prog"]
    rope, cm, swm = _consts()
    fm = lambda v, n: f(v.reshape(v.shape[0], n, 128).transpose(0, 2, 1))
    shared = dict(
        w_ada=I["w_ada"], b_adaT=fm(I["b_ada"], 48),
        gains=f(np.stack([fm(I[k], 8) for k in ("norm_mix_pre", "norm_mix_post", "norm_mlp_pre", "norm_mlp_post")], axis=2)),
        w_in=I["w_in"], lbT=f(np.stack([I["hgrn_lb_fwd"], I["hgrn_lb_bwd"]], axis=1)),
        hgrn_normT=fm(I["hgrn_norm"], 4), sink=f(I["swa_sink"][:, None, :]),
        mla_qnT=fm(I["mla_q_norm"], 2), mla_kvn=f(I["mla_kv_norm"][:, :, None]),
        w_uq=I["mla_w_uq"], w_ukv=I["mla_w_ukv"],
        gqa_qn=f(np.tile(np.stack([I["gqa_q_norm"], I["gqa_q_norm"][:, _perm(64)]], axis=2), (1, 2, 1))),
        gqa_kn=f(np.tile(np.stack([I["gqa_k_norm"], I["gqa_k_norm"][:, _perm(64)]], axis=2), (1, 2, 1))),
        w_branch=I["w_branch"], w_out=I["w_out"], w_mlp_in=I["w_mlp_in"], w_mlp_out=I["w_mlp_out"],
        rope64=rope, cmat=cm, swamask=swm,
    )
    wsw = I["mla_w_uq"].reshape(DEPTH, 256, 8, 96).copy()
    wsw[..., 64:96] = wsw[..., 64:96][..., _perm(32)]
    shared["w_uq_sw"] = f(wsw.reshape(DEPTH, 256, 768))
    in_maps = []
    for i in range(NCORE):
        bb = i % 2
        m = dict(shared)
        m["xT_s"] = f(I["x_sample"][bb].T)
        m["xT_p"] = f(I["x_prompt"][4 * i:4 * i + 4].reshape(1024, DM).T)
        cc = np.stack([I["c_ctx"], I["c"][bb]], axis=1)
        m["cT"] = f(cc.reshape(8, 128, 2).transpose(1, 0, 2))
        m["st_hgrn"] = f(I["state_hgrn"][bb])
        m["c_swa_kT"] = f(I["cache_swa_k"][bb].transpose(0, 2, 3, 1))
        m["c_swa_v"] = f(I["cache_swa_v"][bb].reshape(DEPTH, 512, 128))
        m["c_ckvT"] = f(I["cache_mla_ckv"][bb].transpose(0, 2, 1))
        m["c_krT"] = f(I["cache_mla_kr"][bb].transpose(0, 2, 1))
        m["c_gqa_kT"] = f(I["cache_gqa_k"][bb].transpose(0, 2, 3, 1))
        m["c_gqa_v"] = f(I["cache_gqa_v"][bb].reshape(DEPTH, 512, 128))
        in_maps.append(m)
    res = run_bass_kernel_spmd(nc, in_maps, core_ids=list(range(NCORE)))
    R = res.results
    y_prompt = np.concatenate([R[i]["yT_p"].T.reshape(4, 256, DM) for i in range(NCORE)], axis=0)
    y_sample = np.stack([R[0]["yT_s"].T, R[1]["yT_s"].T], axis=0)
    cat = lambda fn: np.ascontiguousarray(np.concatenate([fn(R[i]) for i in range(NCORE)], axis=0).astype(np.float32))
    n_hgrn = cat(lambda r: r["o_hgrn"].transpose(1, 0, 2, 3, 4, 5))
    kT = lambda a: a.reshape(DEPTH, 2, 64, 4, 256).transpose(3, 0, 4, 1, 2)
    vv = lambda a: a.reshape(DEPTH, 4, 256, 2, 64).transpose(1, 0, 2, 3, 4)
    n_swa_k = cat(lambda r: kT(r["o_swa_kT"]))
    n_swa_v = cat(lambda r: vv(r["o_swa_v"]))
    n_ckv = cat(lambda r: r["o_ckvT"].reshape(DEPTH, 128, 4, 256).transpose(2, 0, 3, 1))
    n_kr = cat(lambda r: r["o_krT"].reshape(DEPTH, 32, 4, 256).transpose(2, 0, 3, 1))
    n_gqa_k = cat(lambda r: kT(r["o_gqa_kT"]))
    n_gqa_v = cat(lambda r: vv(r["o_gqa_v"]))
    return (np.ascontiguousarray(y_prompt.astype(np.float32)), np.ascontiguousarray(y_sample.astype(np.float32)),
            n_hgrn, n_swa_k, n_swa_v, n_ckv, n_kr, n_gqa_k, n_gqa_v)
```

```python
import numpy as np
from contextlib import ExitStack
import ml_dtypes
import concourse.bass as bass
import concourse.mybir as mybir
from concourse.bass_utils import run_bass_kernel_spmd

F32 = mybir.dt.float32
BF16 = mybir.dt.bfloat16
AF = mybir.ActivationFunctionType
ALU = mybir.AluOpType

DM = 1024
DEPTH = 2
NCORE = 8
EPS = 1e-6
OFF = dict(hq=0, ff=512, fb=1024, hi=1536, hg=2048, sq=2560, sk=3072, sv=3200, cq=3328, ckv=3584, kr=3712,
           gq=3744, gk=4256, gv=4384, gates=4512)
D_IN = 8608
UTM_COLS = 1792
STAGES = {"proj", "hgrn", "swa", "mla", "gqa", "merge", "mlp"}


class Sched:
    NDMA = 24

    def __init__(self, nc, es):
        self.nc = nc
        self.eng = {"pe": nc.tensor, "act": nc.scalar, "dve": nc.vector, "pool": nc.gpsimd, "sp": nc.sync}
        self.sem = {k: es.enter_context(nc.semaphore("s_" + k)) for k in ("pe", "act", "dve", "pool")}
        self.cnt = {k: 0 for k in self.sem}
        self.dsem = [es.enter_context(nc.semaphore("d%d" % i)) for i in range(self.NDMA)]
        self.dcnt = [0] * self.NDMA
        self.dnext = 0
        self.seen = {k: {} for k in self.eng}
        self.lastw = {}
        self.reads = {}
        self.n_instr = 0

    def _sem_of(self, key):
        return self.sem[key] if isinstance(key, str) else self.dsem[key[1]]

    def _wait(self, e, tok):
        key, val = tok
        if self.seen[e].get(key, 0) >= val:
            return
        self.eng[e].wait_ge(self._sem_of(key), val)
        self.seen[e][key] = val

    def _deps(self, e, reads, writes, pe_acc=False):
        best = {}
        for r in reads:
            t = self.lastw.get(r)
            if t is not None and best.get(t[0], 0) < t[1]:
                best[t[0]] = t[1]
        for w in writes:
            t = self.lastw.get(w)
            if t is not None and best.get(t[0], 0) < t[1]:
                if not (pe_acc and t[0] == "pe"):
                    best[t[0]] = t[1]
            for t in self.reads.get(w, ()):
                if best.get(t[0], 0) < t[1]:
                    best[t[0]] = t[1]
        for key, val in best.items():
            self._wait(e, (key, val))

    def _record(self, tok, reads, writes):
        for r in reads:
            lst = self.reads.setdefault(r, [])
            lst[:] = [t for t in lst if t[0] != tok[0]]
            lst.append(tok)
        for w in writes:
            self.lastw[w] = tok
            self.reads[w] = []

    def op(self, e, fn, reads=(), writes=(), pe_acc=False):
        self._deps(e, reads, writes, pe_acc)
        ins = fn(self.eng[e])
        self.cnt[e] += 1
        ins.then_inc(self.sem[e], 1)
        self._record((e, self.cnt[e]), reads, writes)
        self.n_instr += 1
        return ins

    def dma(self, q, out, in_, reads=(), writes=()):
        i = self.dnext
        self.dnext = (self.dnext + 1) % self.NDMA
        if self.dcnt[i] > 0:
            self._wait(q, (("d", i), self.dcnt[i]))
        self._deps(q, reads, writes)
        ins = self.eng[q].dma_start(out=out, in_=in_)
        self.dcnt[i] += 16
        ins.then_inc(self.dsem[i], 16)
        self._record((("d", i), self.dcnt[i]), reads, writes)
        self.n_instr += 1
        return ins

    def barrier(self):
        best = {}
        for k in self.cnt:
            if self.cnt[k]:
                best[k] = self.cnt[k]
        for i in range(self.NDMA):
            if self.dcnt[i]:
                best[("d", i)] = self.dcnt[i]
        for e in self.eng:
            for key, val in best.items():
                self._wait(e, (key, val))
        self.lastw = {}
        self.reads = {}


class B:
    def __init__(self, nc, es):
        self.nc, self.es = nc, es
        self.S = Sched(nc, es)
        self.D = {}
        self.psn = 0

    def din(self, name, shape, dt=F32):
        self.D[name] = self.nc.dram_tensor(name, list(shape), dt, kind="ExternalInput").ap()
        return self.D[name]

    def dout(self, name, shape, dt=F32):
        self.D[name] = self.nc.dram_tensor(name, list(shape), dt, kind="ExternalOutput").ap()
        return self.D[name]

    def dscr(self, name, shape, dt=F32):
        self.D[name] = self.nc.dram_tensor(name, list(shape), dt, kind="Internal").ap()
        return self.D[name]

    def sb(self, st, name, shape, dt=F32):
        self.uid = getattr(self, "uid", 0) + 1
        return st.enter_context(self.nc.sbuf_tensor("sb%d_%s" % (self.uid, name), list(shape), dt))

    def ps(self):
        i = self.psn % self.nrot
        self.psn += 1
        return self.psum[i], "ps%d" % i

    nrot = 6

    def acc(self, i):
        k = (6, 7, 4, 5)[i]
        return self.psum[k], "ps%d" % k


def build_program():
    nc = bass.Bass("TRN2", target_bir_lowering=False)
    es = ExitStack()
    b = B(nc, es)
    S = b.S
    D = b.D
    GR = {
        "s": dict(T=2048, nseq=1, L=2048, P=512, rope=True, cond=1),
        "p": dict(T=1024, nseq=4, L=256, P=0, rope=False, cond=0),
    }
    for g, G in GR.items():
        T = G["T"]
        b.din("xT_" + g, [DM, T])
        b.dout("yT_" + g, [DM, T])
        b.dscr("X1_" + g, [DM, T])
        for l in range(DEPTH - 1):
            b.dscr("X2_%d_%s" % (l, g), [DM, T])
        b.dscr("UT_" + g, [68 * 128, T])
        b.dscr("UTM_" + g, [T, UTM_COLS])
        b.dscr("OT_" + g, [4, 512, T], BF16)
        b.dscr("YP_" + g, [DM, T], BF16)
    b.din("cT", [128, 8, 2])
    b.din("w_ada", [DEPTH, DM, 6 * DM])
    b.din("b_adaT", [DEPTH, 128, 48])
    b.din("gains", [DEPTH, 128, 4, 8])
    b.din("w_in", [DEPTH, DM, D_IN])
    b.din("lbT", [DEPTH, 2, 512])
    b.din("hgrn_normT", [DEPTH, 128, 4])
    b.din("sink", [DEPTH, 1, 8])
    b.din("mla_qnT", [DEPTH, 128, 2])
    b.din("mla_kvn", [DEPTH, 128, 1])
    b.din("w_uq", [DEPTH, 256, 768])
    b.din("w_uq_sw", [DEPTH, 256, 768])
    b.din("w_ukv", [DEPTH, 128, 1024])
    b.din("gqa_qn", [DEPTH, 128, 2])
    b.din("gqa_kn", [DEPTH, 128, 2])
    b.din("w_branch", [DEPTH, 4, 512, DM])
    b.din("w_out", [DEPTH, DM, DM])
    b.din("w_mlp_in", [DEPTH, DM, 4 * DM])
    b.din("w_mlp_out", [DEPTH, 4 * DM, DM])
    b.din("st_hgrn", [DEPTH, 2, 4, 128, 128])
    b.din("c_swa_kT", [DEPTH, 2, 64, 512])
    b.din("c_swa_v", [DEPTH, 512, 128])
    b.din("c_ckvT", [DEPTH, 128, 512])
    b.din("c_krT", [DEPTH, 32, 512])
    b.din("c_gqa_kT", [DEPTH, 2, 64, 512])
    b.din("c_gqa_v", [DEPTH, 512, 128])
    b.din("rope64", [4, 128, 2048])
    b.din("cmat", [6, 128, 128])
    b.din("swamask", [2, 128, 512], BF16)
    b.dout("o_hgrn", [DEPTH, 4, 2, 4, 128, 128])
    b.dout("o_swa_kT", [DEPTH, 128, 1024])
    b.dout("o_swa_v", [DEPTH, 1024, 128])
    b.dout("o_ckvT", [DEPTH, 128, 1024])
    b.dout("o_krT", [DEPTH, 32, 1024])
    b.dout("o_gqa_kT", [DEPTH, 128, 1024])
    b.dout("o_gqa_v", [DEPTH, 1024, 128])

    b.psum = [es.enter_context(nc.psum_tensor("psb%d" % i, [128, 512], F32)) for i in range(8)]

    cst = ExitStack()
    es.enter_context(cst)
    ones_bf = b.sb(cst, "ones_bf", [128, 128], BF16)
    ones_f = b.sb(cst, "ones_f", [128, 128], F32)
    cm = b.sb(cst, "cm", [128, 6, 128], F32)
    modT = b.sb(cst, "modT", [128, DEPTH, 48, 2], F32)
    gains = b.sb(cst, "gains", [128, DEPTH, 4, 8], F32)
    coef = b.sb(cst, "coef", [128, DEPTH, 2, 6, 8], F32)
    S.op("pool", lambda e: e.memset(ones_bf[:], 1.0), writes=["ones_bf"])
    onesAB = b.sb(cst, "onesAB", [128, 2, 128], BF16)
    S.op("pool", lambda e: e.memset(onesAB[:], 0.0), writes=["onesAB"])
    S.op("pool", lambda e: e.memset(onesAB[:, 0, 0:64], 1.0), writes=["onesAB"])
    S.op("pool", lambda e: e.memset(onesAB[:, 1, 64:128], 1.0), writes=["onesAB"])
    S.op("pool", lambda e: e.memset(ones_f[:], 1.0), writes=["ones_f"])
    S.dma("sp", cm[:], D["cmat"].rearrange("a p c -> p a c"), writes=["cm"])
    S.dma("sp", gains[:], D["gains"].rearrange("l p a c -> p l a c"), writes=["gains"])

    with ExitStack() as st:
        cT = b.sb(st, "cT", [128, 8, 2])
        scT = b.sb(st, "scT", [128, 8, 2])
        badaT = b.sb(st, "badaT", [128, DEPTH, 48])
        wa = [b.sb(st, "wa%d" % i, [128, 8, 768]) for i in range(2)]
        S.dma("sp", cT[:], D["cT"], writes=["cT"])
        S.dma("sp", badaT[:], D["b_adaT"].rearrange("l p j -> p l j"), writes=["badaT"])
        S.op("act", lambda e: e.activation(scT[:], cT[:], AF.Silu), reads=["cT"], writes=["scT"])
        n = 0
        for l in range(DEPTH):
            pt, pn = b.ps()
            for cg in range(8):
                w = wa[n % 2]
                wn = "wa%d" % (n % 2)
                n += 1
                S.dma("sp" if cg % 2 == 0 else "act", w[:],
                      D["w_ada"][l, :, cg * 768:(cg + 1) * 768].rearrange("(kc p) c -> p kc c", p=128), writes=[wn])
                for jj in range(6):
                    j = cg * 6 + jj
                    for kc in range(8):
                        S.op("pe", lambda e: e.matmul(pt[:, 2 * j:2 * j + 2], w[:, kc, jj * 128:(jj + 1) * 128], scT[:, kc, :],
                                                      start=(kc == 0), stop=(kc == 7)),
                             reads=[wn, "scT"], writes=[pn], pe_acc=True)
            S.op("dve", lambda e: e.tensor_tensor(modT[:, l], pt[:, 0:96].rearrange("p (j c) -> p j c", c=2),
                                                  badaT[:, l].unsqueeze(2).to_broadcast([128, 48, 2]), ALU.add),
                 reads=[pn, "badaT"], writes=["modT"])
        for l in range(DEPTH):
            for c in range(2):
                m = lambda i: modT[:, l, i * 8:(i + 1) * 8, c]
                S.op("dve", lambda e: e.scalar_tensor_tensor(coef[:, l, c, 0], m(1), 1.0, gains[:, l, 0], ALU.add, ALU.mult),
                     reads=["modT", "gains"], writes=["coef"])
                S.op("dve", lambda e: e.tensor_copy(coef[:, l, c, 1], m(0)), reads=["modT"], writes=["coef"])
                S.op("dve", lambda e: e.tensor_tensor(coef[:, l, c, 2], m(2), gains[:, l, 1], ALU.mult), reads=["modT", "gains"], writes=["coef"])
                S.op("dve", lambda e: e.scalar_tensor_tensor(coef[:, l, c, 3], m(4), 1.0, gains[:, l, 2], ALU.add, ALU.mult),
                     reads=["modT", "gains"], writes=["coef"])
                S.op("dve", lambda e: e.tensor_copy(coef[:, l, c, 4], m(3)), reads=["modT"], writes=["coef"])
                S.op("dve", lambda e: e.tensor_tensor(coef[:, l, c, 5], m(5), gains[:, l, 3], ALU.mult), reads=["modT", "gains"], writes=["coef"])
        S.barrier()

    def rstd_from_sumsq(pt, pn, out, outn, npart, ncol, inv_n):
        S.op("act", lambda e: e.activation(out[0:npart, 0:ncol], pt[0:npart, 0:ncol], AF.Sqrt, bias=epsb[0:npart, :], scale=inv_n),
             reads=[pn, "epsb"], writes=[outn])
        S.op("dve", lambda e: e.reciprocal(out[0:npart, 0:ncol], out[0:npart, 0:ncol]), reads=[outn], writes=[outn])

    epsb = b.sb(cst, "epsb", [128, 1], F32)
    S.op("pool", lambda e: e.memset(epsb[:], EPS), writes=["epsb"])

    def norm_mod_tile(st_unused, xt, xn, hT, hn, t0, a_ap, sh_ap, tmp, sq, rs):
        for kc in range(8):
            S.op("act", lambda e: e.activation(sq[:, kc, :], xt[:, kc, :], AF.Square), reads=[xn], writes=["sq"])
        pt, pn = b.ps()
        for kc in range(8):
            S.op("pe", lambda e: e.matmul(pt[:], ones_bf[:], sq[:, kc, :], start=(kc == 0), stop=(kc == 7)),
                 reads=["sq", "ones_bf"], writes=[pn], pe_acc=True)
        rstd_from_sumsq(pt, pn, rs, "rs", 128, 512, 1.0 / DM)
        for kc in range(8):
            S.op("dve", lambda e: e.tensor_tensor(tmp[:], xt[:, kc, :], rs[:], ALU.mult), reads=[xn, "rs"], writes=["tmp"])
            S.op("act", lambda e: e.activation(hT[:, kc, t0:t0 + 512], tmp[:], AF.Identity, bias=sh_ap[:, kc:kc + 1], scale=a_ap[:, kc:kc + 1]),
                 reads=["tmp", "coef"], writes=[hn])

    def stage_h(st, g, l, Xsrc, ia, ish):
        G = GR[g]
        T = G["T"]
        hT = b.sb(st, "hT", [128, 8, T], BF16)
        with ExitStack() as s2:
            xts = [b.sb(s2, "xt%d" % i, [128, 8, 512]) for i in range(2)]
            tmp = b.sb(s2, "tmp", [128, 512])
            sq = b.sb(s2, "sq", [128, 8, 512], BF16)
            rs = b.sb(s2, "rs", [128, 512])
            for tt in range(T // 512):
                xt, xn = xts[tt % 2], "xt%d" % (tt % 2)
                S.dma("sp", xt[:], Xsrc[:, tt * 512:(tt + 1) * 512].rearrange("(kc p) t -> p kc t", p=128), writes=[xn])
                norm_mod_tile(None, xt, xn, hT, "hT", tt * 512, coef[:, l, G["cond"], ia], coef[:, l, G["cond"], ish], tmp, sq, rs)
            S.barrier()
        return hT

    def stage_proj(g, l, hT):
        G = GR[g]
        T = G["T"]
        UT, UTM = D["UT_" + g], D["UTM_" + g]
        tm_groups = {1: [(0, 512, 0)], 2: [(0, 512, 512)], 3: [(0, 512, 1024)], 6: [(128, 128, 1536)], 8: [(288, 128, 1664)]}
        with ExitStack() as st:
            wb = [b.sb(st, "wb%d" % i, [128, 8, 512], BF16) for i in range(2)]
            ev = [b.sb(st, "ev%d" % i, [128, 512]) for i in range(4)]
            nev = 0
            for cg in range(17):
                ncol = 512 if cg < 16 else D_IN - 8192
                w, wn = wb[cg % 2], "wb%d" % (cg % 2)
                S.dma("pool", w[:, :, 0:ncol], D["w_in"][l, :, cg * 512:cg * 512 + ncol].rearrange("(kc p) c -> p kc c", p=128), writes=[wn])
                for tt in range(T // 512):
                    for cc in range((ncol + 127) // 128):
                        m = min(128, ncol - cc * 128)
                        pt, pn = b.ps()
                        for kc in range(8):
                            S.op("pe", lambda e: e.matmul(pt[0:m, :], w[:, kc, cc * 128:cc * 128 + m], hT[:, kc, tt * 512:(tt + 1) * 512],
                                                          start=(kc == 0), stop=(kc == 7)), reads=[wn, "hT"], writes=[pn], pe_acc=True)
                        e_, en = ev[nev % 4], "ev%d" % (nev % 4)
                        eng = "act" if nev % 2 == 0 else "dve"
                        nev += 1
                        if eng == "act":
                            S.op("act", lambda e: e.copy(e_[0:m, :], pt[0:m, :]), reads=[pn], writes=[en])
                        else:
                            S.op("dve", lambda e: e.tensor_copy(e_[0:m, :], pt[0:m, :]), reads=[pn], writes=[en])
                        r0 = cg * 512 + cc * 128
                        S.dma("sp", UT[r0:r0 + m, tt * 512:(tt + 1) * 512], e_[0:m, :], reads=[en], writes=[])
                for (c0, cn, dst) in tm_groups.get(cg, []):
                    for t4 in range(T // 128):
                        pt, pn = b.ps()
                        for kc in range(8):
                            S.op("pe", lambda e: e.matmul(pt[:, 0:cn], hT[:, kc, t4 * 128:(t4 + 1) * 128], w[:, kc, c0:c0 + cn],
                                                          start=(kc == 0), stop=(kc == 7)), reads=[wn, "hT"], writes=[pn], pe_acc=True)
                        e_, en = ev[nev % 4], "ev%d" % (nev % 4)
                        eng = "act" if nev % 2 == 0 else "dve"
                        nev += 1
                        if eng == "act":
                            S.op("act", lambda e: e.copy(e_[:, 0:cn], pt[:, 0:cn]), reads=[pn], writes=[en])
                        else:
                            S.op("dve", lambda e: e.tensor_copy(e_[:, 0:cn], pt[:, 0:cn]), reads=[pn], writes=[en])
                        S.dma("sp", UTM[t4 * 128:(t4 + 1) * 128, dst:dst + cn], e_[:, 0:cn], reads=[en], writes=[])
            S.barrier()

    def attn_unit(qT, qn, Kd, Nq, chunks, scale, o_out, on, wk, sink=None):
        po, pon = b.acc(0)
        pd, pdn = b.acc(1)
        nch = len(chunks)

        def emit_st(i):
            kT, kn, V, vn, nk, mask = chunks[i]
            pst, psn_ = b.ps()
            S.op("pe", lambda e: e.matmul(pst[0:nk, 0:Nq], kT, qT, start=True, stop=True), reads=[kn] + (qn if isinstance(qn, list) else [qn]), writes=[psn_])
            return pst, psn_

        cur = emit_st(0)
        for i, (kT, kn, V, vn, nk, mask) in enumerate(chunks):
            nxt = emit_st(i + 1) if i + 1 < nch else None
            pst, psn_ = cur
            E, En = wk["E"][i % 3], "E%d" % (i % 3)
            S.op("act", lambda e: e.activation(E[0:nk, 0:Nq], pst[0:nk, 0:Nq], AF.Exp, scale=scale), reads=[psn_], writes=[En])
            if mask is not None:
                S.op("pool", lambda e: e.tensor_tensor(E[0:nk, 0:Nq], E[0:nk, 0:Nq], mask, ALU.mult), reads=[En, "swamask"], writes=[En])
            last = (i == nch - 1) and sink is None
            S.op("pe", lambda e: e.matmul(po[0:64, 0:Nq], V, E[0:nk, 0:Nq], start=(i == 0), stop=(i == nch - 1)),
                 reads=[vn, En], writes=[pon], pe_acc=(i > 0))
            S.op("pe", lambda e: e.matmul(pd[0:64, 0:Nq], ones_bf[0:nk, 0:64], E[0:nk, 0:Nq], start=(i == 0), stop=last),
                 reads=["ones_bf", En], writes=[pdn], pe_acc=(i > 0))
            cur = nxt
        if sink is not None:
            S.op("pe", lambda e: e.matmul(pd[0:64, 0:Nq], ones_bf[0:1, 0:64], sink, start=False, stop=True),
                 reads=["ones_bf", "sinkrow"], writes=[pdn], pe_acc=True)
        rec = wk["rec"]
        S.op("dve", lambda e: e.reciprocal(rec[0:64, 0:Nq], pd[0:64, 0:Nq]), reads=[pdn], writes=["rec"])
        S.op("dve", lambda e: e.tensor_tensor(o_out, po[0:64, 0:Nq], rec[0:64, 0:Nq], ALU.mult), reads=[pon, "rec"], writes=[on])

    def attn_pair(qa, qan, qb, qbn, Nq, chunks, scale, o_out, on, wk, sink=None):
        po, pon = b.acc(0)
        pd, pdn = b.acc(1)
        nch = len(chunks)
        E = wk["E"]
        ne = len(E)

        def emit_st(i):
            kTa, kTb, kn, Va, Vb, vn, nk, mask = chunks[i]
            p1, n1 = b.ps()
            kna, knb = kn if isinstance(kn, tuple) else (kn, kn)
            S.op("pe", lambda e: e.matmul(p1[0:nk, 0:Nq], kTa, qa, start=True, stop=True), reads=[kna] + qan, writes=[n1])
            p2, n2 = b.ps()
            S.op("pe", lambda e: e.matmul(p2[0:nk, 0:Nq], kTb, qb, start=True, stop=True), reads=[knb] + qbn, writes=[n2])
            return (p1, n1, p2, n2)

        cur = emit_st(0)
        for i, (kTa, kTb, kn, Va, Vb, vn, nk, mask) in enumerate(chunks):
            nxt = emit_st(i + 1) if i + 1 < nch else None
            p1, n1, p2, n2 = cur
            Ea, Ean = E[(2 * i) % ne], "E%d" % ((2 * i) % ne)
            Eb, Ebn = E[(2 * i + 1) % ne], "E%d" % ((2 * i + 1) % ne)
            S.op("act", lambda e: e.activation(Ea[0:nk, 0:Nq], p1[0:nk, 0:Nq], AF.Exp, scale=scale), reads=[n1], writes=[Ean])
            S.op("act", lambda e: e.activation(Eb[0:nk, 0:Nq], p2[0:nk, 0:Nq], AF.Exp, scale=scale), reads=[n2], writes=[Ebn])
            if mask is not None:
                S.op("pool", lambda e: e.tensor_tensor(Ea[0:nk, 0:Nq], Ea[0:nk, 0:Nq], mask, ALU.mult), reads=[Ean, "swamask"], writes=[Ean])
                S.op("dve", lambda e: e.tensor_tensor(Eb[0:nk, 0:Nq], Eb[0:nk, 0:Nq], mask, ALU.mult), reads=[Ebn, "swamask"], writes=[Ebn])
            last = (i == nch - 1)
            S.op("pe", lambda e: e.matmul(po[:, 0:Nq], Va, Ea[0:nk, 0:Nq], start=(i == 0), stop=False), reads=[vn, Ean], writes=[pon], pe_acc=(i > 0))
            S.op("pe", lambda e: e.matmul(po[:, 0:Nq], Vb, Eb[0:nk, 0:Nq], start=False, stop=last), reads=[vn, Ebn], writes=[pon], pe_acc=True)
            S.op("pe", lambda e: e.matmul(pd[:, 0:Nq], onesAB[0:nk, 0, :], Ea[0:nk, 0:Nq], start=(i == 0), stop=False), reads=["onesAB", Ean], writes=[pdn], pe_acc=(i > 0))
            S.op("pe", lambda e: e.matmul(pd[:, 0:Nq], onesAB[0:nk, 1, :], Eb[0:nk, 0:Nq], start=False, stop=(last and sink is None)), reads=["onesAB", Ebn], writes=[pdn], pe_acc=True)
            cur = nxt
        if sink is not None:
            sa, sb_ = sink
            S.op("pe", lambda e: e.matmul(pd[:, 0:Nq], onesAB[0:1, 0, :], sa, start=False, stop=False), reads=["onesAB", "sinkrow"], writes=[pdn], pe_acc=True)
            S.op("pe", lambda e: e.matmul(pd[:, 0:Nq], onesAB[0:1, 1, :], sb_, start=False, stop=True), reads=["onesAB", "sinkrow"], writes=[pdn], pe_acc=True)
        rec = wk["rec"]
        S.op("dve", lambda e: e.reciprocal(rec[:, 0:Nq], pd[:, 0:Nq]), reads=[pdn], writes=["rec"])
        S.op("dve", lambda e: e.tensor_tensor(o_out, po[:, 0:Nq], rec[:, 0:Nq], ALU.mult), reads=[pon, "rec"], writes=[on])

    def rr_alloc(st, T, do_rope):
        sets = []
        for k in range(2):
            sets.append(dict(k=k, x=b.sb(st, "rr_x%d" % k, [128, T]), xs=(b.sb(st, "rr_xs%d" % k, [128, T]) if do_rope else None),
                             sq=b.sb(st, "rr_sq%d" % k, [128, 512], BF16), rs=b.sb(st, "rr_rs%d" % k, [128, T]), t1=b.sb(st, "rr_t1%d" % k, [128, 512])))
        return sets

    def rms_rope_rows(W_, p0, src_rows_fn, n_rows, T, gain2, gainn, do_norm, do_rope, rope_idx, out_bf, outn, out32=None, out32n=None,
                      out32_pre_rope=False):
        k = W_["k"]
        x, xs, sqb, rs, t1 = W_["x"], W_["xs"], W_["sq"], W_["rs"], W_["t1"]
        xn, xsn, sqn, rsn, t1n = ("rr_x%d" % k, "rr_xs%d" % k, "rr_sq%d" % k, "rr_rs%d" % k, "rr_t1%d" % k)
        for (r0, nr, ap) in src_rows_fn(False):
            S.dma("sp", x[p0 + r0:p0 + r0 + nr, 0:T], ap, writes=[xn])
        if do_rope:
            for (r0, nr, ap) in src_rows_fn(True):
                S.dma("act", xs[p0 + r0:p0 + r0 + nr, 0:T], ap, writes=[xsn])
        nr = n_rows
        pr = slice(p0, p0 + nr)
        W = min(512, T)
        for c in range(T // W):
            cs = slice(c * W, (c + 1) * W)
            if do_norm:
                S.op("act", lambda e: e.activation(sqb[pr, 0:W], x[pr, cs], AF.Square), reads=[xn], writes=[sqn])
                pt, pn = b.ps()
                S.op("pe", lambda e: e.matmul(pt[:, 0:W], ones_bf[pr, :], sqb[pr, 0:W], start=True, stop=True),
                     reads=[sqn, "ones_bf"], writes=[pn])
                S.op("act", lambda e: e.activation(rs[pr, cs], pt[pr, 0:W], AF.Sqrt, bias=epsb[pr, :], scale=1.0 / nr), reads=[pn, "epsb"], writes=[rsn])
                S.op("dve", lambda e: e.reciprocal(rs[pr, cs], rs[pr, cs]), reads=[rsn], writes=[rsn])
                S.op("dve", lambda e: e.scalar_tensor_tensor(x[pr, cs], x[pr, cs], gain2[pr, 0:1], rs[pr, cs], ALU.mult, ALU.mult),
                     reads=[xn, rsn, gainn], writes=[xn])
                if do_rope:
                    S.op("dve", lambda e: e.scalar_tensor_tensor(xs[pr, cs], xs[pr, cs], gain2[pr, 1:2], rs[pr, cs], ALU.mult, ALU.mult),
                         reads=[xsn, rsn, gainn], writes=[xsn])
            if out32 is not None and out32_pre_rope:
                S.op("pool", lambda e: e.tensor_copy(out32[pr, cs], x[pr, cs]), reads=[xn], writes=[out32n])
            if do_rope:
                S.op("dve", lambda e: e.tensor_tensor(x[pr, cs], x[pr, cs], rope[pr, rope_idx, cs], ALU.mult), reads=[xn, "rope"], writes=[xn])
                S.op("pool", lambda e: e.tensor_tensor(t1[pr, 0:W], xs[pr, cs], rope[pr, rope_idx + 1, cs], ALU.mult), reads=[xsn, "rope"], writes=[t1n])
                S.op("dve", lambda e: e.tensor_tensor(out_bf[:, cs], x[pr, cs], t1[pr, 0:W], ALU.add), reads=[xn, t1n], writes=[outn])
            else:
                S.op("act", lambda e: e.copy(out_bf[:, cs], x[pr, cs]), reads=[xn], writes=[outn])

    def swap_rows(base, T0, T1, UT, nd):
        q = nd // 4
        def f(swapped):
            if not swapped:
                return [(0, nd, UT[base:base + nd, T0:T1])]
            return [(0, q, UT[base + q:base + 2 * q, T0:T1]), (q, q, UT[base:base + q, T0:T1]),
                    (2 * q, q, UT[base + 3 * q:base + 4 * q, T0:T1]), (3 * q, q, UT[base + 2 * q:base + 3 * q, T0:T1])]
        return f

    rope = b.sb(cst, "rope", [128, 4, 2048], BF16)
    S.dma("pool", rope[:], D["rope64"].rearrange("a p t -> p a t"), writes=["rope"])

    def stage_gqa_like(g, l, kind):
        G = GR[g]
        T, L, P, nseq, do_rope = G["T"], G["L"], G["P"], G["nseq"], G["rope"]
        UT, UTM, OT = D["UT_" + g], D["UTM_" + g], D["OT_" + g]
        qo, ko = (OFF["sq"], OFF["sk"]) if kind == "swa" else (OFF["gq"], OFF["gk"])
        vcol = 1536 if kind == "swa" else 1664
        bi = 1 if kind == "swa" else 3
        do_norm = kind == "gqa"
        scale = 64 ** -0.5
        nkc_ctx = P // 128
        with ExitStack() as st:
            kT = b.sb(st, "kT", [128, P + T], BF16)
            Vt = b.sb(st, "Vt", [128, (P + T) // 128, 2, 128], BF16)
            qTh = b.sb(st, "qTh", [128, 8, T], BF16)
            gq2 = b.sb(st, "gq2", [128, 2]); gk2 = b.sb(st, "gk2", [128, 2])
            sinkrow = b.sb(st, "sinkrow", [1, 8, 128], BF16)
            sk32 = b.sb(st, "sk32", [1, 8])
            wk = dict(E=[b.sb(st, "E%d" % i, [128, 512], BF16) for i in range(6)], rec=b.sb(st, "rec", [128, 512]))
            obs = [b.sb(st, "ob%d" % i, [128, 512], BF16) for i in range(2)]
            swm = b.sb(st, "swamask", [128, 2, 512], BF16)
            S.dma("sp", swm[:], D["swamask"].rearrange("a p c -> p a c"), writes=["swamask"])
            S.op("pool", lambda e: e.memset(qTh[64:128, 0:4, :], 0.0), writes=["qTh%d" % hh for hh in range(4)])
            S.op("pool", lambda e: e.memset(qTh[0:64, 4:8, :], 0.0), writes=["qTh%d" % hh for hh in range(4, 8)])
            S.op("pool", lambda e: e.memset(Vt[:], 0.0), writes=["Vt"])
            if do_norm:
                S.dma("sp", gq2[:], D["gqa_qn"][l], writes=["gq2"])
                S.dma("sp", gk2[:], D["gqa_kn"][l], writes=["gk2"])
            else:
                S.dma("sp", sk32[:], D["sink"][l], writes=["sk32"])
                S.op("act", lambda e: e.activation(sk32[:], sk32[:], AF.Exp), reads=["sk32"], writes=["sk32"])
                S.op("dve", lambda e: e.tensor_copy(sinkrow[:], sk32[:].unsqueeze(2).to_broadcast([1, 8, 128])), reads=["sk32"], writes=["sinkrow"])
            with ExitStack() as s2:
                RR = rr_alloc(s2, T, do_rope)
                k32 = b.sb(s2, "k32", [128, T]) if g == "p" else None
                v32 = b.sb(s2, "v32", [128, (P + T) // 128, 128])
                if P:
                    src_k = D["c_swa_kT"] if kind == "swa" else D["c_gqa_kT"]
                    src_v = D["c_swa_v"] if kind == "swa" else D["c_gqa_v"]
                    S.dma("pool", kT[:, 0:P], src_k[l].rearrange("h d t -> (h d) t"), writes=["kT"])
                    S.dma("sp", v32[:, 0:P // 128, :], src_v[l].rearrange("(c p) f -> p c f", p=128), writes=["v32"])
                S.dma("sp", v32[:, P // 128:, :], UTM[:, vcol:vcol + 128].rearrange("(c p) f -> p c f", p=128), writes=["v32"])
                S.op("dve", lambda e: e.tensor_copy(Vt[:, :, 0, 0:64], v32[:, :, 0:64]), reads=["v32"], writes=["Vt"])
                S.op("dve", lambda e: e.tensor_copy(Vt[:, :, 1, 64:128], v32[:, :, 64:128]), reads=["v32"], writes=["Vt"])
                if g == "p":
                    dst = D["o_swa_v"] if kind == "swa" else D["o_gqa_v"]
                    S.dma("act", dst[l].rearrange("(c p) f -> p c f", p=128), v32[:], reads=["v32"], writes=[])
                nrr = 0
                for kvh in range(2):
                    p0 = kvh * 64
                    rms_rope_rows(RR[nrr % 2], p0, swap_rows(ko + kvh * 64, 0, T, UT, 64), 64, T, gk2, "gk2", do_norm, do_rope, 0,
                                  kT[p0:p0 + 64, P:P + T], "kT", out32=k32, out32n="k32", out32_pre_rope=True)
                    nrr += 1
                if g == "p":
                    dst = D["o_swa_kT"] if kind == "swa" else D["o_gqa_kT"]
                    S.dma("sp", dst[l], k32[:], reads=["k32"], writes=[])
                for h in range(8):
                    p0 = (h // 4) * 64
                    rms_rope_rows(RR[nrr % 2], p0, swap_rows(qo + h * 64, 0, T, UT, 64), 64, T, gq2, "gq2", do_norm, do_rope, 0,
                                  qTh[p0:p0 + 64, h, :], "qTh%d" % h)
                    nrr += 1
                nu = 0
                qna = ["qTh%d" % hh for hh in range(4)]
                qnb = ["qTh%d" % hh for hh in range(4, 8)]
                for sq_ in range(nseq):
                    T0 = sq_ * L
                    kb0 = P + T0
                    for qb in range(L // 128):
                        cols = [(c * 128, None) for c in range(nkc_ctx)]
                        if kind == "swa" and P:
                            cols += [(kb0 + kb * 128, mi) for (kb, mi) in ((qb - 1, 0), (qb, None), (qb + 1, 1)) if 0 <= kb < L // 128]
                        else:
                            cols += [(kb0 + kb * 128, None) for kb in range(L // 128)]
                        chunks = [(kT[:, c0:c0 + 128], kT[:, c0:c0 + 128], "kT", Vt[:, c0 // 128, 0, :], Vt[:, c0 // 128, 1, :], "Vt", 128,
                                   None if mi is None else swm[:, mi, :]) for (c0, mi) in cols]
                        ob, obn = obs[nu % 2], "ob%d" % (nu % 2)
                        nu += 1
                        q0 = T0 + qb * 128
                        attn_pair(qTh[:, 0:4, q0:q0 + 128], qna, qTh[:, 4:8, q0:q0 + 128], qnb, 512, chunks, scale, ob[:], obn, wk,
                                  sink=((sinkrow[0:1, 0:4, :], sinkrow[0:1, 4:8, :]) if kind == "swa" else None))
                        for kvh in range(2):
                            S.dma("sp" if kvh == 0 else "act", OT[bi, kvh * 256:(kvh + 1) * 256, q0:q0 + 128].rearrange("(h d) t -> d h t", d=64),
                                  ob[kvh * 64:(kvh + 1) * 64, :].rearrange("d (h t) -> d h t", h=4), reads=[obn], writes=[])
                S.barrier()

    def stage_mla(g, l):
        G = GR[g]
        T, L, P, nseq, do_rope = G["T"], G["L"], G["P"], G["nseq"], G["rope"]
        UT, OT = D["UT_" + g], D["OT_" + g]
        scale = 96 ** -0.5
        NK = P + L
        for sq_ in range(nseq):
            T0, T1 = sq_ * L, (sq_ + 1) * L
            with ExitStack() as st:
                ckvT = b.sb(st, "ckvT", [128, NK], BF16)
                krT = b.sb(st, "krT", [96, NK], BF16)
                cqn = b.sb(st, "cqn", [128, 2, L], BF16)
                wuq = b.sb(st, "wuq", [128, 2, 768], BF16); wuqs = b.sb(st, "wuqs", [128, 2, 768], BF16)
                wukv = b.sb(st, "wukv", [128, 1024], BF16)
                g_q = b.sb(st, "g_q", [128, 2]); g_kv = b.sb(st, "g_kv", [128, 1])
                S.dma("pool", wuq[:], D["w_uq"][l].rearrange("(kc p) c -> p kc c", p=128), writes=["wuq"])
                S.dma("pool", wuqs[:], D["w_uq_sw"][l].rearrange("(kc p) c -> p kc c", p=128), writes=["wuqs"])
                S.dma("pool", wukv[:], D["w_ukv"][l], writes=["wukv"])
                S.dma("sp", g_q[:], D["mla_qnT"][l], writes=["g_q"])
                S.dma("sp", g_kv[:], D["mla_kvn"][l], writes=["g_kv"])
                with ExitStack() as s2:
                    x = b.sb(s2, "m_x", [128, 2, L]); sqb = b.sb(s2, "m_sq", [128, 2, 512], BF16); rs = b.sb(s2, "m_rs", [128, 512])
                    xk = b.sb(s2, "m_xk", [128, L]); kr32 = b.sb(s2, "m_kr", [96, L]); krs = b.sb(s2, "m_krs", [96, L]); t1 = b.sb(s2, "m_t1", [96, 512])
                    if P:
                        S.dma("pool", ckvT[:, 0:P], D["c_ckvT"][l], writes=["ckvT"])
                        S.dma("pool", krT[64:96, 0:P], D["c_krT"][l], writes=["krT"])
                    S.dma("sp", x[:], UT[OFF["cq"]:OFF["cq"] + 256, T0:T1].rearrange("(kc p) t -> p kc t", p=128), writes=["m_x"])
                    S.dma("sp", xk[:], UT[OFF["ckv"]:OFF["ckv"] + 128, T0:T1], writes=["m_xk"])
                    S.dma("sp", kr32[64:96, :], UT[OFF["kr"]:OFF["kr"] + 32, T0:T1], writes=["m_kr"])
                    if do_rope:
                        for (r0, nr, ap) in swap_rows(OFF["kr"], T0, T1, UT, 32)(True):
                            S.dma("act", krs[64 + r0:64 + r0 + nr, :], ap, writes=["m_krs"])
                    for c in range(L // 512 if L >= 512 else 1):
                        w = min(512, L)
                        cs = slice(c * w, (c + 1) * w)
                        pt, pn = b.ps()
                        for kc in range(2):
                            S.op("act", lambda e: e.activation(sqb[:, kc, 0:w], x[:, kc, cs], AF.Square), reads=["m_x"], writes=["m_sq"])
                        for kc in range(2):
                            S.op("pe", lambda e: e.matmul(pt[:, 0:w], ones_bf[:], sqb[:, kc, 0:w], start=(kc == 0), stop=(kc == 1)),
                                 reads=["m_sq", "ones_bf"], writes=[pn], pe_acc=True)
                        rstd_from_sumsq(pt, pn, rs, "m_rs", 128, w, 1.0 / 256)
                        for kc in range(2):
                            S.op("dve", lambda e: e.scalar_tensor_tensor(cqn[:, kc, cs], x[:, kc, cs], g_q[:, kc:kc + 1], rs[:, 0:w], ALU.mult, ALU.mult),
                                 reads=["m_x", "m_rs", "g_q"], writes=["cqn"])
                        pt, pn = b.ps()
                        S.op("act", lambda e: e.activation(sqb[:, 0, 0:w], xk[:, cs], AF.Square), reads=["m_xk"], writes=["m_sq"])
                        S.op("pe", lambda e: e.matmul(pt[:, 0:w], ones_bf[:], sqb[:, 0, 0:w], start=True, stop=True), reads=["m_sq", "ones_bf"], writes=[pn])
                        rstd_from_sumsq(pt, pn, rs, "m_rs", 128, w, 1.0 / 128)
                        S.op("dve", lambda e: e.scalar_tensor_tensor(xk[:, cs], xk[:, cs], g_kv[:, 0:1], rs[:, 0:w], ALU.mult, ALU.mult),
                             reads=["m_xk", "m_rs", "g_kv"], writes=["m_xk"])
                        S.op("pool", lambda e: e.tensor_copy(ckvT[:, P + c * w:P + (c + 1) * w], xk[:, cs]), reads=["m_xk"], writes=["ckvT"])
                        if do_rope:
                            S.op("dve", lambda e: e.tensor_tensor(t1[64:96, 0:w], kr32[64:96, cs], rope[64:96, 2, cs], ALU.mult), reads=["m_kr", "rope"], writes=["m_t1"])
                            S.op("pool", lambda e: e.tensor_tensor(krs[64:96, cs], krs[64:96, cs], rope[64:96, 3, cs], ALU.mult), reads=["m_krs", "rope"], writes=["m_krs"])
                            S.op("dve", lambda e: e.tensor_tensor(krT[64:96, P + c * w:P + (c + 1) * w], t1[64:96, 0:w], krs[64:96, cs], ALU.add),
                                 reads=["m_t1", "m_krs"], writes=["krT"])
                        else:
                            S.op("dve", lambda e: e.tensor_copy(krT[64:96, P + c * w:P + (c + 1) * w], kr32[64:96, cs]), reads=["m_kr"], writes=["krT"])
                    if g == "p":
                        S.dma("sp", D["o_ckvT"][l, :, T0:T1], xk[:], reads=["m_xk"], writes=[])
                        S.dma("sp", D["o_krT"][l, :, T0:T1], kr32[64:96, :], reads=["m_kr"], writes=[])
                    S.barrier()
                Vall = b.sb(st, "Vall", [128, NK // 128, 8, 128], BF16)
                S.op("pool", lambda e: e.memset(Vall[:], 0.0), writes=["Vall"])
                for c in range(NK // 128):
                    pt, pn = b.ps()
                    S.op("pe", lambda e: e.matmul(pt[:, :], ckvT[:, c * 128:(c + 1) * 128],
                                                  wukv[:].rearrange("p (h x) -> p h x", x=128)[:, :, 64:128], start=True, stop=True),
                         reads=["ckvT", "wukv"], writes=[pn])
                    pv = pt[:, :].rearrange("p (h two d) -> p h two d", two=2, d=64)
                    S.op("act", lambda e: e.copy(Vall[:, c, 0::2, 0:64], pv[:, :, 0, :]), reads=[pn], writes=["Vall"])
                    S.op("dve", lambda e: e.tensor_copy(Vall[:, c, 1::2, 64:128], pv[:, :, 1, :]), reads=[pn], writes=["Vall"])
                S.barrier()
                with ExitStack() as s2:
                    wk = dict(E=[b.sb(s2, "E%d" % i, [128, 512], BF16) for i in range(6)], rec=b.sb(s2, "rec", [128, 512]))
                    kTh = [b.sb(s2, "kTh%d" % i, [128, NK], BF16) for i in range(4)]
                    qh = [b.sb(s2, "qh%d" % i, [128, L], BF16) for i in range(4)]
                    obs = [b.sb(s2, "ob%d" % i, [128, 512], BF16) for i in range(2)]
                    t2 = b.sb(s2, "m_t2", [96, 512]); t3 = b.sb(s2, "m_t3", [96, 512])
                    for i in range(4):
                        S.op("pool", lambda e: e.memset(kTh[i][96:128, :], 0.0), writes=["kTh%d" % i])
                        S.op("pool", lambda e: e.memset(qh[i][96:128, :], 0.0), writes=["qh%d" % i])
                    nu = 0
                    W = min(512, L)
                    for hp in range(4):
                        bufs = []
                        for h2 in range(2):
                            h = hp * 2 + h2
                            bi_ = (hp % 2) * 2 + h2
                            kt, ktn = kTh[bi_], "kTh%d" % bi_
                            q_, q_n = qh[bi_], "qh%d" % bi_
                            bufs.append((kt, ktn, q_, q_n))
                            for c in range((NK + 511) // 512):
                                w = min(512, NK - c * 512)
                                pt, pn = b.ps()
                                S.op("pe", lambda e: e.matmul(pt[:, 0:w], wukv[:, h * 128:(h + 1) * 128], ckvT[:, c * 512:c * 512 + w], start=True, stop=True),
                                     reads=["wukv", "ckvT"], writes=[pn])
                                S.op("act", lambda e: e.copy(kt[0:64, c * 512:c * 512 + w], pt[0:64, 0:w]), reads=[pn], writes=[ktn])
                            S.op("pool", lambda e: e.tensor_copy(kt[64:96, :], krT[64:96, :]), reads=["krT"], writes=[ktn])
                            for c in range(L // W):
                                cs = slice(c * W, (c + 1) * W)
                                pt, pn = b.ps()
                                for kc in range(2):
                                    S.op("pe", lambda e: e.matmul(pt[0:96, 0:W], wuq[:, kc, h * 96:(h + 1) * 96], cqn[:, kc, cs], start=(kc == 0), stop=(kc == 1)),
                                         reads=["wuq", "cqn"], writes=[pn], pe_acc=(kc > 0))
                                S.op("act", lambda e: e.copy(q_[0:64, cs], pt[0:64, 0:W]), reads=[pn], writes=[q_n])
                                if do_rope:
                                    pt2, pn2 = b.ps()
                                    for kc in range(2):
                                        S.op("pe", lambda e: e.matmul(pt2[0:96, 0:W], wuqs[:, kc, h * 96:(h + 1) * 96], cqn[:, kc, cs], start=(kc == 0), stop=(kc == 1)),
                                             reads=["wuqs", "cqn"], writes=[pn2], pe_acc=(kc > 0))
                                    S.op("dve", lambda e: e.tensor_tensor(t2[64:96, 0:W], pt[64:96, 0:W], rope[64:96, 2, cs], ALU.mult), reads=[pn, "rope"], writes=["m_t2"])
                                    S.op("dve", lambda e: e.tensor_tensor(t3[64:96, 0:W], pt2[64:96, 0:W], rope[64:96, 3, cs], ALU.mult), reads=[pn2, "rope"], writes=["m_t3"])
                                    S.op("pool", lambda e: e.tensor_tensor(q_[64:96, cs], t2[64:96, 0:W], t3[64:96, 0:W], ALU.add), reads=["m_t2", "m_t3"], writes=[q_n])
                                else:
                                    S.op("dve", lambda e: e.tensor_copy(q_[64:96, cs], pt[64:96, 0:W]), reads=[pn], writes=[q_n])
                        (kta, ktan, qa, qan), (ktb, ktbn, qb_, qbn) = bufs
                        for c in range(L // W):
                            chunks = [(kta[:, kc * 128:(kc + 1) * 128], ktb[:, kc * 128:(kc + 1) * 128], (ktan, ktbn), Vall[:, kc, hp * 2, :], Vall[:, kc, hp * 2 + 1, :], "Vall", 128, None)
                                      for kc in range(NK // 128)]
                            ob, obn = obs[nu % 2], "ob%d" % (nu % 2)
                            nu += 1
                            attn_pair(qa[:, c * W:(c + 1) * W], [qan], qb_[:, c * W:(c + 1) * W], [qbn], W, chunks, scale, ob[:, 0:W], obn, wk)
                            S.dma("sp", OT[2, hp * 128:(hp + 1) * 128, T0 + c * W:T0 + (c + 1) * W], ob[:, 0:W], reads=[obn], writes=[])
                    S.barrier()

    def stage_hgrn(g, l):
        G = GR[g]
        T, L, P, nseq = G["T"], G["L"], G["P"], G["nseq"]
        UT, UTM, OT = D["UT_" + g], D["UTM_" + g], D["OT_" + g]
        NTL = L // 128
        ident, triU, triL, sL, sU, csel = (cm[:, i, :] for i in range(6))
        with ExitStack() as st:
            lbb = b.sb(st, "lbb", [128, 2, 512]); oml = b.sb(st, "oml", [128, 2, 512])
            with ExitStack() as s2:
                lbr = b.sb(s2, "lbr", [128, DEPTH, 2, 512]); den = b.sb(s2, "lden", [128, 2, 512])
                S.dma("sp", lbr[:], D["lbT"].partition_broadcast(128), writes=["lbr"])
                S.op("act", lambda e: e.activation(lbr[:], lbr[:], AF.Exp), reads=["lbr"], writes=["lbr"])
                S.op("dve", lambda e: e.tensor_tensor(den[:], lbr[:, 0], lbr[:, 1], ALU.add), reads=["lbr"], writes=["lden"])
                S.op("dve", lambda e: e.reciprocal(den[:], den[:]), reads=["lden"], writes=["lden"])
                if l == 0:
                    S.op("pool", lambda e: e.memset(lbb[:], 0.0), writes=["lbb"])
                else:
                    S.op("dve", lambda e: e.tensor_tensor(lbb[:], lbr[:, 1], den[:], ALU.mult), reads=["lbr", "lden"], writes=["lbb"])
                S.op("dve", lambda e: e.tensor_scalar(oml[:], lbb[:], -1.0, 1.0, ALU.mult, ALU.add), reads=["lbb"], writes=["oml"])
                S.barrier()
            hn = b.sb(st, "hn", [128, 4]); S.dma("sp", hn[:], D["hgrn_normT"][l], writes=["hn"])
            Sst = [b.sb(st, "Sst%d" % i, [128, 2, 4, 128]) for i in range(2)]
            oall = b.sb(st, "oall", [128, 2, 4, L], BF16)
            W_ = {}
            for d in range(2):
                for k in range(2):
                    sfx = "%d%d" % (d, k)
                    W_[d, k] = dict(
                        sfx=sfx,
                        tt=b.sb(st, "h_t" + sfx, [128, 512]), vv=b.sb(st, "h_v" + sfx, [128, 512]), gg=b.sb(st, "h_g" + sfx, [128, 512]),
                        kt=b.sb(st, "h_kt" + sfx, [128, 512]), kh=b.sb(st, "h_kh" + sfx, [128, 512]),
                        khm=b.sb(st, "h_khm" + sfx, [128, 4, 4, 128]), qT=b.sb(st, "h_qT" + sfx, [128, 4, 128]), qt=b.sb(st, "h_qt" + sfx, [128, 4, 128]),
                        eb=b.sb(st, "h_eb" + sfx, [128, 4, 128]), ktT=b.sb(st, "h_ktT" + sfx, [128, 4, 128]), AT=b.sb(st, "h_AT" + sfx, [128, 4, 128]),
                        tmpo=b.sb(st, "h_tmpo" + sfx, [128, 512]))
            etot = [b.sb(st, "h_etot%d" % k, [128, 2, 4, 4]) for k in range(2)]
            fin = dict(os=b.sb(st, "f_os", [128, 4, 256]), gT=b.sb(st, "f_gT", [128, 4, 256]), sq=b.sb(st, "f_sq", [128, 4, 256], BF16),
                       rs=b.sb(st, "f_rs", [128, 4, 256]), ob=b.sb(st, "f_ob", [128, 4, 256], BF16))
            b.nrot = 4
            M4 = lambda M: M.unsqueeze(1).to_broadcast([128, 4, 128])

            def sstn(sp, d, h):
                return "Sst%d_%d%d" % (sp, d, h)

            def partA(rec):
                k = rec["k"]
                for d in range(2):
                    w = W_[d, k]; x = w["sfx"]
                    t0 = rec["T0"] + rec["ti"][d] * 128
                    S.dma("sp", w["tt"][:], UTM[t0:t0 + 128, d * 512:(d + 1) * 512], writes=["h_t" + x])
                    S.dma("sp", w["vv"][:], UTM[t0:t0 + 128, 1024:1536], writes=["h_v" + x])
                    S.dma("act", w["qT"][:], UT[0:512, t0:t0 + 128].rearrange("(h p) t -> p h t", p=128), writes=["h_qT" + x])
                for d in range(2):
                    w = W_[d, k]; x = w["sfx"]
                    S.op("act", lambda e: e.activation(w["tt"][:], w["tt"][:], AF.Sigmoid), reads=["h_t" + x], writes=["h_t" + x])
                for d in range(2):
                    w = W_[d, k]; x = w["sfx"]
                    S.op("act", lambda e: e.activation(w["qT"][:], w["qT"][:], AF.Silu), reads=["h_qT" + x], writes=["h_qT" + x])
                for d in range(2):
                    w = W_[d, k]; x = w["sfx"]
                    S.op("dve", lambda e: e.tensor_tensor(w["tt"][:], w["tt"][:], oml[:, d], ALU.mult), reads=["h_t" + x, "oml"], writes=["h_t" + x])
                    S.op("dve", lambda e: e.scalar_tensor_tensor(w["gg"][:], w["tt"][:], 1e-30, lbb[:, d], ALU.max, ALU.add), reads=["h_t" + x, "lbb"], writes=["h_g" + x])
                for d in range(2):
                    w = W_[d, k]; x = w["sfx"]
                    S.op("act", lambda e: e.activation(w["gg"][:], w["gg"][:], AF.Ln), reads=["h_g" + x], writes=["h_g" + x])
                for d in range(2):
                    w = W_[d, k]; x = w["sfx"]
                    S.op("dve", lambda e: e.tensor_tensor(w["tt"][:], oml[:, d], w["tt"][:], ALU.subtract), reads=["h_t" + x, "oml"], writes=["h_t" + x])

            def partB(rec, stage):
                k = rec["k"]
                Ms = [(triU, sL), (triL, sU)]
                if stage == 0:
                    pbs = []
                    for d in range(2):
                        w = W_[d, k]; x = w["sfx"]
                        pb, pbn = b.ps()
                        S.op("pe", lambda e: e.matmul(pb[:], Ms[d][0], w["gg"][:], start=True, stop=True), reads=["cm", "h_g" + x], writes=[pbn])
                        pr, prn = b.ps()
                        S.op("pe", lambda e: e.matmul(pr[:], Ms[d][1], w["gg"][:], start=True, stop=True), reads=["cm", "h_g" + x], writes=[prn])
                        pbs.append((pb, pbn, pr, prn))
                    for d in range(2):
                        w = W_[d, k]; x = w["sfx"]
                        pb, pbn, pr, prn = pbs[d]
                        S.op("act", lambda e: e.activation(w["kt"][:], pb[:], AF.Exp, scale=-1.0), reads=[pbn], writes=["h_kt" + x])
                        S.op("act", lambda e: e.activation(w["kh"][:], pr[:], AF.Exp), reads=[prn], writes=["h_kh" + x])
                    pts = []
                    ptot, ptotn = b.ps()
                    for d in range(2):
                        w = W_[d, k]; x = w["sfx"]
                        pbt, pbtn = b.ps()
                        for h in range(4):
                            hs = slice(h * 128, (h + 1) * 128)
                            S.op("pe", lambda e: e.matmul(pbt[:, hs], w["gg"][:, hs], Ms[d][0], start=True, stop=True), reads=["h_g" + x, "cm"], writes=[pbtn], pe_acc=(h > 0))
                            S.op("pe", lambda e: e.matmul(ptot[:, d * 16 + h * 4:d * 16 + h * 4 + 4], w["gg"][:, hs], csel[:, 0:4], start=True, stop=True),
                                 reads=["h_g" + x, "cm"], writes=[ptotn], pe_acc=(d > 0 or h > 0))
                        pts.append((pbt, pbtn))
                    for d in range(2):
                        w = W_[d, k]; x = w["sfx"]
                        S.op("dve", lambda e: e.tensor_tensor(w["kt"][:], w["kt"][:], w["tt"][:], ALU.mult), reads=["h_kt" + x, "h_t" + x], writes=["h_kt" + x])
                        S.op("dve", lambda e: e.tensor_tensor(w["kh"][:], w["kh"][:], w["tt"][:], ALU.mult), reads=["h_kh" + x, "h_t" + x], writes=["h_kh" + x])
                    for d in range(2):
                        w = W_[d, k]; x = w["sfx"]
                        pbt, pbtn = pts[d]
                        S.op("act", lambda e: e.activation(w["eb"][:], pbt[:].rearrange("p (h t) -> p h t", h=4), AF.Exp), reads=[pbtn], writes=["h_eb" + x])
                    S.op("act", lambda e: e.activation(etot[k][:], ptot[:, 0:32].rearrange("p (d h j) -> p d h j", d=2, h=4), AF.Exp), reads=[ptotn], writes=["h_etot%d" % k])
                    for d in range(2):
                        w = W_[d, k]; x = w["sfx"]
                        S.op("dve", lambda e: e.tensor_tensor(w["qt"][:], w["qT"][:], w["eb"][:], ALU.mult), reads=["h_qT" + x, "h_eb" + x], writes=["h_qt" + x])
                    for d in range(2):
                        w = W_[d, k]; x = w["sfx"]
                        for j in range(4):
                            eng = "pool" if j % 2 == 0 else "dve"
                            S.op(eng, lambda e: e.tensor_scalar(w["khm"][:, :, j, :], w["kh"][:].rearrange("p (h c) -> p h c", h=4), csel[:, j:j + 1], None, ALU.mult),
                                 reads=["h_kh" + x, "cm"], writes=["h_khm" + x])
                elif stage == 1:
                    for d in range(2):
                        w = W_[d, k]; x = w["sfx"]
                        pk, pkn = b.ps()
                        for h in range(4):
                            hs = slice(h * 128, (h + 1) * 128)
                            S.op("pe", lambda e: e.matmul(pk[:, hs], w["kt"][:, hs], ident, start=True, stop=True), reads=["h_kt" + x, "cm"], writes=[pkn], pe_acc=(h > 0))
                        S.op("act", lambda e: e.copy(w["ktT"][:], pk[:].rearrange("p (h t) -> p h t", h=4)), reads=[pkn], writes=["h_ktT" + x])
                elif stage == 2:
                    for d in range(2):
                        w = W_[d, k]; x = w["sfx"]
                        pa, pan = b.ps()
                        for h in range(4):
                            hs = slice(h * 128, (h + 1) * 128)
                            S.op("pe", lambda e: e.matmul(pa[:, hs], w["ktT"][:, h, :], w["qt"][:, h, :], start=True, stop=True), reads=["h_ktT" + x, "h_qt" + x], writes=[pan], pe_acc=(h > 0))
                        S.op("dve", lambda e: e.tensor_tensor(w["AT"][:], pa[:].rearrange("p (h t) -> p h t", h=4), M4(Ms[d][0]), ALU.mult), reads=[pan, "cm"], writes=["h_AT" + x])
                else:
                    for d in range(2):
                        w = W_[d, k]; x = w["sfx"]
                        po_, pon_ = b.acc(d)
                        for h in range(4):
                            hs = slice(h * 128, (h + 1) * 128)
                            S.op("pe", lambda e: e.matmul(po_[:, hs], w["vv"][:, hs], w["AT"][:, h, :], start=True, stop=True), reads=["h_v" + x, "h_AT" + x], writes=[pon_], pe_acc=(h > 0))
                        S.op("act", lambda e: e.copy(w["tmpo"][:], po_[:]), reads=[pon_], writes=["h_tmpo" + x])

            def chunk_step(rec, n_):
                k = rec["k"]; sp = rec["sp"]
                pss = []
                for d in range(2):
                    w = W_[d, k]; x = w["sfx"]
                    j = n_ if d == 0 else 3 - n_
                    js = slice(j * 32, (j + 1) * 32)
                    pi_, pin_ = b.acc(2 + d)
                    ps_, psn2 = b.ps()
                    pss.append((ps_, psn2, j))
                    for h in range(4):
                        hs = slice(h * 128, (h + 1) * 128)
                        S.op("pe", lambda e: e.matmul(pi_[:, h * 128 + j * 32:h * 128 + (j + 1) * 32], Sst[sp][:, d, h, :], w["qt"][:, h, js], start=True, stop=True),
                             reads=[sstn(sp, d, h), "h_qt" + x], writes=[pin_], pe_acc=(n_ > 0 or h > 0))
                        S.op("pe", lambda e: e.matmul(ps_[:, hs], w["khm"][:, h, j, :], w["vv"][:, hs], start=True, stop=True),
                             reads=["h_khm" + x, "h_v" + x], writes=[psn2], pe_acc=(h > 0))
                for d in range(2):
                    w = W_[d, k]; x = w["sfx"]
                    ps_, psn2, j = pss[d]
                    for h in range(4):
                        hs = slice(h * 128, (h + 1) * 128)
                        S.op("dve", lambda e: e.scalar_tensor_tensor(Sst[sp][:, d, h, :], Sst[sp][:, d, h, :], etot[k][:, d, h, j:j + 1], ps_[:, hs], ALU.mult, ALU.add),
                             reads=[psn2, "h_etot%d" % k, sstn(sp, d, h)], writes=[sstn(sp, d, h)])

            def finish(rec):
                k = rec["k"]
                for d in range(2):
                    w = W_[d, k]; x = w["sfx"]
                    pi_, pin_ = b.acc(2 + d)
                    tl = rec["ti"][d] * 128
                    S.op("dve", lambda e: e.tensor_tensor(oall[:, d, :, tl:tl + 128], w["tmpo"][:].rearrange("p (h t) -> p h t", h=4),
                                                          pi_[:].rearrange("p (h t) -> p h t", h=4), ALU.add),
                         reads=[pin_, "h_tmpo" + x], writes=["oall"])

            def seq_final(sq_):
                T0 = sq_ * L
                sp = sq_ % 2
                for c in range(L // 256):
                    cs = slice(c * 256, (c + 1) * 256)
                    os_, gT, sq, rs, ob = fin["os"], fin["gT"], fin["sq"], fin["rs"], fin["ob"]
                    S.dma("act", gT[:], UT[OFF["hg"]:OFF["hg"] + 512, T0 + c * 256:T0 + (c + 1) * 256].rearrange("(h p) t -> p h t", p=128), writes=["f_gT"])
                    S.op("act", lambda e: e.activation(gT[:], gT[:], AF.Silu), reads=["f_gT"], writes=["f_gT"])
                    S.op("dve", lambda e: e.tensor_tensor(os_[:], oall[:, 0, :, cs], oall[:, 1, :, cs], ALU.add), reads=["oall"], writes=["f_os"])
                    S.op("act", lambda e: e.activation(sq[:], os_[:], AF.Square), reads=["f_os"], writes=["f_sq"])
                    for hh in range(2):
                        pn_, pnn = b.ps()
                        for h2 in range(2):
                            h = hh * 2 + h2
                            S.op("pe", lambda e: e.matmul(pn_[:, h2 * 256:(h2 + 1) * 256], ones_bf[:], sq[:, h, :], start=True, stop=True),
                                 reads=["f_sq", "ones_bf"], writes=[pnn], pe_acc=(h2 > 0))
                        rstd_from_sumsq(pn_, pnn, rs[:, hh * 2:hh * 2 + 2, :].rearrange("p a t -> p (a t)"), "f_rs", 128, 512, 1.0 / 128)
                    for h in range(4):
                        S.op("dve", lambda e: e.scalar_tensor_tensor(os_[:, h, :], os_[:, h, :], hn[:, h:h + 1], rs[:, h, :], ALU.mult, ALU.mult),
                             reads=["f_os", "hn", "f_rs"], writes=["f_os"])
                    S.op("dve", lambda e: e.tensor_tensor(ob[:], os_[:], gT[:], ALU.mult), reads=["f_os", "f_gT"], writes=["f_ob"])
                    S.dma("sp", OT[0, :, T0 + c * 256:T0 + (c + 1) * 256].rearrange("(h p) t -> p h t", p=128), ob[:], reads=["f_ob"], writes=[])
                if g == "p":
                    S.dma("sp", D["o_hgrn"][l, sq_].rearrange("d h k v -> k d h v"), Sst[sp][:], reads=[sstn(sp, d_, h_) for d_ in range(2) for h_ in range(4)], writes=[])

            recs = []
            for sq_ in range(nseq):
                for i in range(NTL):
                    recs.append(dict(sq=sq_, sp=sq_ % 2, T0=sq_ * L, i=i, ti=(i, NTL - 1 - i), k=len(recs) % 2))
            partA(recs[0])
            for stg in range(4):
                partB(recs[0], stg)
            for n, rec in enumerate(recs):
                nxt = recs[n + 1] if n + 1 < len(recs) else None
                if rec["i"] == 0:
                    sp = rec["sp"]
                    names = [sstn(sp, d_, h_) for d_ in range(2) for h_ in range(4)]
                    if P:
                        S.dma("sp", Sst[sp][:], D["st_hgrn"][l].rearrange("d h k v -> k d h v"), writes=names)
                    else:
                        S.op("pool", lambda e: e.memset(Sst[sp][:], 0.0), writes=names)
                if nxt is not None:
                    partA(nxt)
                for n_ in range(4):
                    chunk_step(rec, n_)
                    if nxt is not None:
                        partB(nxt, n_)
                finish(rec)
                if rec["i"] == NTL - 1:
                    seq_final(rec["sq"])
            b.nrot = 6
            S.barrier()

    def epilogue(st, z, zn, xt, xn, gco, Xdst, c0, wkn):
        sq, rs, tmp = wkn
        for kc in range(8):
            S.op("act", lambda e: e.activation(sq[:, kc, :], z[:, kc, :], AF.Square), reads=[zn], writes=["e_sq"])
        pt, pn = b.ps()
        for kc in range(8):
            S.op("pe", lambda e: e.matmul(pt[:], ones_bf[:], sq[:, kc, :], start=(kc == 0), stop=(kc == 7)), reads=["e_sq", "ones_bf"], writes=[pn], pe_acc=True)
        rstd_from_sumsq(pt, pn, rs, "e_rs", 128, 512, 1.0 / DM)
        for kc in range(8):
            S.op("dve", lambda e: e.tensor_tensor(tmp[:], z[:, kc, :], rs[:], ALU.mult), reads=[zn, "e_rs"], writes=["e_tmp"])
            S.op("dve", lambda e: e.scalar_tensor_tensor(xt[:, kc, :], tmp[:], gco[:, kc:kc + 1], xt[:, kc, :], ALU.mult, ALU.add),
                 reads=["e_tmp", "coef", xn], writes=[xn])
        S.dma("sp", Xdst[:, c0:c0 + 512].rearrange("(kc p) t -> p kc t", p=128), xt[:], reads=[xn], writes=[])

    def stage_merge(g, l, Xsrc, Xdst):
        G = GR[g]
        T = G["T"]
        UT, OT = D["UT_" + g], D["OT_" + g]
        with ExitStack() as st:
            Wb = b.sb(st, "Wb", [128, 4, 4, 1024], BF16)
            Wo = b.sb(st, "Wo", [128, 8, 1024], BF16)
            S.dma("pool", Wb[:], D["w_branch"][l].rearrange("n (kc p) c -> p n kc c", p=128), writes=["Wb"])
            S.dma("pool", Wo[:], D["w_out"][l].rearrange("(kc p) c -> p kc c", p=128), writes=["Wo"])
            ot = b.sb(st, "ot", [128, 4, 4, 512], BF16)
            yp = b.sb(st, "yp", [128, 8, 512], BF16)
            gts = [b.sb(st, "gt%d" % i, [128, 512]) for i in range(3)]
            accf = b.sb(st, "accf", [128, 512]); tm2 = b.sb(st, "tm2", [128, 512])
            z = b.sb(st, "z", [128, 8, 512]); xt = b.sb(st, "xt", [128, 8, 512])
            wkn = (b.sb(st, "e_sq", [128, 8, 512], BF16), b.sb(st, "e_rs", [128, 512]), b.sb(st, "e_tmp", [128, 512]))
            ng = 0
            for tt in range(T // 512):
                cs = slice(tt * 512, (tt + 1) * 512)
                S.dma("sp", ot[:], OT[:, :, cs].rearrange("n (kc p) t -> p n kc t", p=128), writes=["ot"])
                S.dma("act", xt[:], Xsrc[:, cs].rearrange("(kc p) t -> p kc t", p=128), writes=["xt"])
                for dmc in range(8):
                    for n in range(4):
                        gt, gtn = gts[ng % 3], "gt%d" % (ng % 3)
                        ng += 1
                        r0 = OFF["gates"] + n * 1024 + dmc * 128
                        S.dma("sp", gt[:], UT[r0:r0 + 128, cs], writes=[gtn])
                        S.op("act", lambda e: e.activation(gt[:], gt[:], AF.Sigmoid), reads=[gtn], writes=[gtn])
                        pt, pn = b.ps()
                        for kc in range(4):
                            S.op("pe", lambda e: e.matmul(pt[:], Wb[:, n, kc, dmc * 128:(dmc + 1) * 128], ot[:, n, kc, :], start=(kc == 0), stop=(kc == 3)),
                                 reads=["Wb", "ot"], writes=[pn], pe_acc=True)
                        if n == 0:
                            S.op("dve", lambda e: e.tensor_tensor(accf[:], pt[:], gt[:], ALU.mult), reads=[pn, gtn], writes=["accf"])
                        elif n < 3:
                            S.op("dve", lambda e: e.tensor_tensor(tm2[:], pt[:], gt[:], ALU.mult), reads=[pn, gtn], writes=["tm2"])
                            S.op("dve", lambda e: e.tensor_tensor(accf[:], accf[:], tm2[:], ALU.add), reads=["tm2", "accf"], writes=["accf"])
                        else:
                            S.op("dve", lambda e: e.tensor_tensor(tm2[:], pt[:], gt[:], ALU.mult), reads=[pn, gtn], writes=["tm2"])
                            S.op("dve", lambda e: e.tensor_tensor(yp[:, dmc, :], accf[:], tm2[:], ALU.add), reads=["tm2", "accf"], writes=["yp"])
                for oc in range(8):
                    pt, pn = b.ps()
                    for kc in range(8):
                        S.op("pe", lambda e: e.matmul(pt[:], Wo[:, kc, oc * 128:(oc + 1) * 128], yp[:, kc, :], start=(kc == 0), stop=(kc == 7)),
                             reads=["Wo", "yp"], writes=[pn], pe_acc=True)
                    S.op("act", lambda e: e.copy(z[:, oc, :], pt[:]), reads=[pn], writes=["z"])
                epilogue(st, z, "z", xt, "xt", coef[:, l, G["cond"], 2], Xdst, tt * 512, wkn)
            S.barrier()

    def stage_mlp(g, l, Xsrc, Xdst):
        G = GR[g]
        T = G["T"]
        with ExitStack() as st:
            h2 = stage_h(st, g, l, Xsrc, 3, 4)
            W2 = [b.sb(st, "W2_%d" % i, [128, 8, 1024], BF16) for i in range(2)]
            n2 = 0
            w1 = [b.sb(st, "w1_%d" % i, [128, 8, 512], BF16) for i in range(2)]
            hid = b.sb(st, "hid", [128, 32, 512], BF16)
            rl = [b.sb(st, "rl%d" % i, [128, 512]) for i in range(2)]
            z = b.sb(st, "z", [128, 8, 512]); xt = b.sb(st, "xt", [128, 8, 512])
            wkn = (b.sb(st, "e_sq", [128, 8, 512], BF16), b.sb(st, "e_rs", [128, 512]), b.sb(st, "e_tmp", [128, 512]))
            nw = 0
            nr = 0
            for tt in range(T // 512):
                cs = slice(tt * 512, (tt + 1) * 512)
                S.dma("act", xt[:], Xsrc[:, cs].rearrange("(kc p) t -> p kc t", p=128), writes=["xt"])
                for cg in range(8):
                    w, wn = w1[nw % 2], "w1_%d" % (nw % 2)
                    nw += 1
                    S.dma("pool", w[:], D["w_mlp_in"][l, :, cg * 512:(cg + 1) * 512].rearrange("(kc p) c -> p kc c", p=128), writes=[wn])
                    for cc in range(4):
                        pt, pn = b.ps()
                        for kc in range(8):
                            S.op("pe", lambda e: e.matmul(pt[:], w[:, kc, cc * 128:(cc + 1) * 128], h2[:, kc, cs], start=(kc == 0), stop=(kc == 7)),
                                 reads=[wn, "hT"], writes=[pn], pe_acc=True)
                        r, rn = rl[nr % 2], "rl%d" % (nr % 2)
                        nr += 1
                        S.op("act", lambda e: e.activation(r[:], pt[:], AF.Relu), reads=[pn], writes=[rn])
                        S.op("dve", lambda e: e.tensor_tensor(hid[:, cg * 4 + cc, :], r[:], r[:], ALU.mult), reads=[rn], writes=["hid"])
                for q4 in range(4):
                    w2, w2n = W2[n2 % 2], "W2_%d" % (n2 % 2)
                    n2 += 1
                    S.dma("pool", w2[:], D["w_mlp_out"][l, q4 * 1024:(q4 + 1) * 1024, :].rearrange("(kc p) c -> p kc c", p=128), writes=[w2n])
                    for oc in range(8):
                        pt, pn = b.ps()
                        for fc in range(8):
                            S.op("pe", lambda e: e.matmul(pt[:], w2[:, fc, oc * 128:(oc + 1) * 128], hid[:, q4 * 8 + fc, :], start=(fc == 0), stop=(fc == 7)),
                                 reads=[w2n, "hid"], writes=[pn], pe_acc=True)
                        if q4 == 0:
                            S.op("act", lambda e: e.copy(z[:, oc, :], pt[:]), reads=[pn], writes=["z%d" % oc, "z"])
                        else:
                            S.op("dve", lambda e: e.tensor_tensor(z[:, oc, :], z[:, oc, :], pt[:], ALU.add), reads=[pn, "z%d" % oc], writes=["z%d" % oc, "z"])
                epilogue(st, z, "z", xt, "xt", coef[:, l, G["cond"], 5], Xdst, tt * 512, wkn)
            S.barrier()

    for g in ("s", "p"):
        X = D["xT_" + g]
        for l in range(DEPTH):
            if "proj" in STAGES:
                with ExitStack() as st:
                    hT = stage_h(st, g, l, X, 0, 1)
                    stage_proj(g, l, hT)
            if "hgrn" in STAGES:
                stage_hgrn(g, l)
            if "swa" in STAGES:
                stage_gqa_like(g, l, "swa")
            if "mla" in STAGES:
                stage_mla(g, l)
            if "gqa" in STAGES:
                stage_gqa_like(g, l, "gqa")
            if "merge" in STAGES:
                stage_merge(g, l, X, D["X1_" + g])
            Xn = D["yT_" + g] if l == DEPTH - 1 else D["X2_%d_%s" % (l, g)]
            if "mlp" in STAGES:
                stage_mlp(g, l, D["X1_" + g], Xn)
            X = Xn
    S.barrier()
    return nc, b


_CACHE = {}


def _consts():
    nf = 16
    t = np.arange(2048)
    row, col = (t // 64).astype(np.float32), (t % 64).astype(np.float32)
    rope = np.zeros((4, 128, 2048), np.float32)
    def fill(ci, si, r0, nf):
        inv = (10000.0 ** (-np.arange(nf, dtype=np.float32) / nf)).astype(np.float32)
        ar = (row[None, :] * inv[:, None]).astype(np.float32)
        ac = (col[None, :] * inv[:, None]).astype(np.float32)
        for k, a in enumerate((ar, ac)):
            b0 = r0 + k * 2 * nf
            rope[ci, b0:b0 + nf] = np.cos(a); rope[ci, b0 + nf:b0 + 2 * nf] = np.cos(a)
            rope[si, b0:b0 + nf] = -np.sin(a); rope[si, b0 + nf:b0 + 2 * nf] = np.sin(a)
    fill(0, 1, 0, 16)
    fill(0, 1, 64, 16)
    fill(2, 3, 64, 8)
    s_ = np.arange(128)[:, None]; t_ = np.arange(128)[None, :]
    same = (s_ // 32) == (t_ // 32)
    cm = np.zeros((6, 128, 128), np.float32)
    cm[0] = np.eye(128)
    cm[1] = same & (s_ <= t_)
    cm[2] = same & (s_ >= t_)
    cm[3] = same & (s_ > t_)
    cm[4] = same & (s_ < t_)
    cm[5][:, 0:4] = (s_ // 32) == np.arange(4)[None, :]
    j = np.arange(128)[:, None]; i = (np.arange(512) % 128)[None, :]
    swm = np.stack([(j >= i), (j <= i)]).astype(np.float32).astype(ml_dtypes.bfloat16)
    return rope, cm, swm


def _perm(nd):
    q = nd // 4
    return np.concatenate([np.arange(q, 2 * q), np.arange(0, q), np.arange(3 * q, 4 * q), np.arange(2 * q, 3 * q)])


def kernel(**inp):
    f = lambda a: np.ascontiguousarray(np.asarray(a, dtype=np.float32))
    I = {k: f(v) for k, v in inp.items()}
    if "prog" not in _CACHE:
        _CACHE["prog"] = build_program()
    nc, b = _CACHE["prog"]
    rope, cm, swm = _consts()
    fm = lambda v, n: f(v.reshape(v.shape[0], n, 128).transpose(0, 2, 1))
    shared = dict(
        w_ada=I["w_ada"], b_adaT=fm(I["b_ada"], 48),
        gains=f(np.stack([fm(I[k], 8) for k in ("norm_mix_pre", "norm_mix_post", "norm_mlp_pre", "norm_mlp_post")], axis=2)),
        w_in=I["w_in"], lbT=f(np.stack([I["hgrn_lb_fwd"], I["hgrn_lb_bwd"]], axis=1)),
        hgrn_normT=fm(I["hgrn_norm"], 4), sink=f(I["swa_sink"][:, None, :]),
        mla_qnT=fm(I["mla_q_norm"], 2), mla_kvn=f(I["mla_kv_norm"][:, :, None]),
        w_uq=I["mla_w_uq"], w_ukv=I["mla_w_ukv"],
        gqa_qn=f(np.tile(np.stack([I["gqa_q_norm"], I["gqa_q_norm"][:, _perm(64)]], axis=2), (1, 2, 1))),
        gqa_kn=f(np.tile(np.stack([I["gqa_k_norm"], I["gqa_k_norm"][:, _perm(64)]], axis=2), (1, 2, 1))),
        w_branch=I["w_branch"], w_out=I["w_out"], w_mlp_in=I["w_mlp_in"], w_mlp_out=I["w_mlp_out"],
        rope64=rope, cmat=cm, swamask=swm,
    )
    wsw = I["mla_w_uq"].reshape(DEPTH, 256, 8, 96).copy()
    wsw[..., 64:96] = wsw[..., 64:96][..., _perm(32)]
    shared["w_uq_sw"] = f(wsw.reshape(DEPTH, 256, 768))
    in_maps = []
    for i in range(NCORE):
        bb = i % 2
        m = dict(shared)
        m["xT_s"] = f(I["x_sample"][bb].T)
        m["xT_p"] = f(I["x_prompt"][4 * i:4 * i + 4].reshape(1024, DM).T)
        cc = np.stack([I["c_ctx"], I["c"][bb]], axis=1)
        m["cT"] = f(cc.reshape(8, 128, 2).transpose(1, 0, 2))
        m["st_hgrn"] = f(I["state_hgrn"][bb])
        m["c_swa_kT"] = f(I["cache_swa_k"][bb].transpose(0, 2, 3, 1))
        m["c_swa_v"] = f(I["cache_swa_v"][bb].reshape(DEPTH, 512, 128))
        m["c_ckvT"] = f(I["cache_mla_ckv"][bb].transpose(0, 2, 1))
        m["c_krT"] = f(I["cache_mla_kr"][bb].transpose(0, 2, 1))
        m["c_gqa_kT"] = f(I["cache_gqa_k"][bb].transpose(0, 2, 3, 1))
        m["c_gqa_v"] = f(I["cache_gqa_v"][bb].reshape(DEPTH, 512, 128))
        in_maps.append(m)
    res = run_bass_kernel_spmd(nc, in_maps, core_ids=list(range(NCORE)))
    R = res.results
    y_prompt = np.concatenate([R[i]["yT_p"].T.reshape(4, 256, DM) for i in range(NCORE)], axis=0)
    y_sample = np.stack([R[0]["yT_s"].T, R[1]["yT_s"].T], axis=0)
    cat = lambda fn: np.ascontiguousarray(np.concatenate([fn(R[i]) for i in range(NCORE)], axis=0).astype(np.float32))
    n_hgrn = cat(lambda r: r["o_hgrn"].transpose(1, 0, 2, 3, 4, 5))
    kT = lambda a: a.reshape(DEPTH, 2, 64, 4, 256).transpose(3, 0, 4, 1, 2)
    vv = lambda a: a.reshape(DEPTH, 4, 256, 2, 64).transpose(1, 0, 2, 3, 4)
    n_swa_k = cat(lambda r: kT(r["o_swa_kT"]))
    n_swa_v = cat(lambda r: vv(r["o_swa_v"]))
    n_ckv = cat(lambda r: r["o_ckvT"].reshape(DEPTH, 128, 4, 256).transpose(2, 0, 3, 1))
    n_kr = cat(lambda r: r["o_krT"].reshape(DEPTH, 32, 4, 256).transpose(2, 0, 3, 1))
    n_gqa_k = cat(lambda r: kT(r["o_gqa_kT"]))
    n_gqa_v = cat(lambda r: vv(r["o_gqa_v"]))
    return (np.ascontiguousarray(y_prompt.astype(np.float32)), np.ascontiguousarray(y_sample.astype(np.float32)),
            n_hgrn, n_swa_k, n_swa_v, n_ckv, n_kr, n_gqa_k, n_gqa_v)
```

```python
import numpy as np
from contextlib import ExitStack
import ml_dtypes
import concourse.bass as bass
import concourse.mybir as mybir
from concourse.bass_utils import run_bass_kernel_spmd

F32 = mybir.dt.float32
BF16 = mybir.dt.bfloat16
AF = mybir.ActivationFunctionType
ALU = mybir.AluOpType

DM = 1024
DEPTH = 2
NCORE = 8
EPS = 1e-6
OFF = dict(hq=0, ff=512, fb=1024, hi=1536, hg=2048, sq=2560, sk=3072, sv=3200, cq=3328, ckv=3584, kr=3712,
           gq=3744, gk=4256, gv=4384, gates=4512)
D_IN = 8608
UTM_COLS = 1792
STAGES = {"proj", "hgrn", "swa", "mla", "gqa", "merge", "mlp"}


class Sched:
    NDMA = 24

    def __init__(self, nc, es):
        self.nc = nc
        self.eng = {"pe": nc.tensor, "act": nc.scalar, "dve": nc.vector, "pool": nc.gpsimd, "sp": nc.sync}
        self.sem = {k: es.enter_context(nc.semaphore("s_" + k)) for k in ("pe", "act", "dve", "pool")}
        self.cnt = {k: 0 for k in self.sem}
        self.dsem = [es.enter_context(nc.semaphore("d%d" % i)) for i in range(self.NDMA)]
        self.dcnt = [0] * self.NDMA
        self.dnext = 0
        self.seen = {k: {} for k in self.eng}
        self.lastw = {}
        self.reads = {}
        self.n_instr = 0

    def _sem_of(self, key):
        return self.sem[key] if isinstance(key, str) else self.dsem[key[1]]

    def _wait(self, e, tok):
        key, val = tok
        if self.seen[e].get(key, 0) >= val:
            return
        self.eng[e].wait_ge(self._sem_of(key), val)
        self.seen[e][key] = val

    def _deps(self, e, reads, writes, pe_acc=False):
        best = {}
        for r in reads:
            t = self.lastw.get(r)
            if t is not None and best.get(t[0], 0) < t[1]:
                best[t[0]] = t[1]
        for w in writes:
            t = self.lastw.get(w)
            if t is not None and best.get(t[0], 0) < t[1]:
                if not (pe_acc and t[0] == "pe"):
                    best[t[0]] = t[1]
            for t in self.reads.get(w, ()):
                if best.get(t[0], 0) < t[1]:
                    best[t[0]] = t[1]
        for key, val in best.items():
            self._wait(e, (key, val))

    def _record(self, tok, reads, writes):
        for r in reads:
            lst = self.reads.setdefault(r, [])
            lst[:] = [t for t in lst if t[0] != tok[0]]
            lst.append(tok)
        for w in writes:
            self.lastw[w] = tok
            self.reads[w] = []

    def op(self, e, fn, reads=(), writes=(), pe_acc=False):
        self._deps(e, reads, writes, pe_acc)
        ins = fn(self.eng[e])
        self.cnt[e] += 1
        ins.then_inc(self.sem[e], 1)
        self._record((e, self.cnt[e]), reads, writes)
        self.n_instr += 1
        return ins

    def dma(self, q, out, in_, reads=(), writes=()):
        i = self.dnext
        self.dnext = (self.dnext + 1) % self.NDMA
        if self.dcnt[i] > 0:
            self._wait(q, (("d", i), self.dcnt[i]))
        self._deps(q, reads, writes)
        ins = self.eng[q].dma_start(out=out, in_=in_)
        self.dcnt[i] += 16
        ins.then_inc(self.dsem[i], 16)
        self._record((("d", i), self.dcnt[i]), reads, writes)
        self.n_instr += 1
        return ins

    def barrier(self):
        best = {}
        for k in self.cnt:
            if self.cnt[k]:
                best[k] = self.cnt[k]
        for i in range(self.NDMA):
            if self.dcnt[i]:
                best[("d", i)] = self.dcnt[i]
        for e in self.eng:
            for key, val in best.items():
                self._wait(e, (key, val))
        self.lastw = {}
        self.reads = {}


class B:
    def __init__(self, nc, es):
        self.nc, self.es = nc, es
        self.S = Sched(nc, es)
        self.D = {}
        self.psn = 0

    def din(self, name, shape, dt=F32):
        self.D[name] = self.nc.dram_tensor(name, list(shape), dt, kind="ExternalInput").ap()
        return self.D[name]

    def dout(self, name, shape, dt=F32):
        self.D[name] = self.nc.dram_tensor(name, list(shape), dt, kind="ExternalOutput").ap()
        return self.D[name]

    def dscr(self, name, shape, dt=F32):
        self.D[name] = self.nc.dram_tensor(name, list(shape), dt, kind="Internal").ap()
        return self.D[name]

    def sb(self, st, name, shape, dt=F32):
        self.uid = getattr(self, "uid", 0) + 1
        return st.enter_context(self.nc.sbuf_tensor("sb%d_%s" % (self.uid, name), list(shape), dt))

    def ps(self):
        i = self.psn % self.nrot
        self.psn += 1
        return self.psum[i], "ps%d" % i

    nrot = 6

    def acc(self, i):
        k = (6, 7, 4, 5)[i]
        return self.psum[k], "ps%d" % k


def build_program():
    nc = bass.Bass("TRN2", target_bir_lowering=False)
    es = ExitStack()
    b = B(nc, es)
    S = b.S
    D = b.D
    GR = {
        "s": dict(T=2048, nseq=1, L=2048, P=512, rope=True, cond=1),
        "p": dict(T=1024, nseq=4, L=256, P=0, rope=False, cond=0),
    }
    for g, G in GR.items():
        T = G["T"]
        b.din("xT_" + g, [DM, T])
        b.dout("yT_" + g, [DM, T])
        b.dscr("X1_" + g, [DM, T])
        for l in range(DEPTH - 1):
            b.dscr("X2_%d_%s" % (l, g), [DM, T])
        b.dscr("UT_" + g, [68 * 128, T])
        b.dscr("UTM_" + g, [T, UTM_COLS])
        b.dscr("OT_" + g, [4, 512, T], BF16)
        b.dscr("YP_" + g, [DM, T], BF16)
    b.din("cT", [128, 8, 2])
    b.din("w_ada", [DEPTH, DM, 6 * DM])
    b.din("b_adaT", [DEPTH, 128, 48])
    b.din("gains", [DEPTH, 128, 4, 8])
    b.din("w_in", [DEPTH, DM, D_IN])
    b.din("lbT", [DEPTH, 2, 512])
    b.din("hgrn_normT", [DEPTH, 128, 4])
    b.din("sink", [DEPTH, 1, 8])
    b.din("mla_qnT", [DEPTH, 128, 2])
    b.din("mla_kvn", [DEPTH, 128, 1])
    b.din("w_uq", [DEPTH, 256, 768])
    b.din("w_uq_sw", [DEPTH, 256, 768])
    b.din("w_ukv", [DEPTH, 128, 1024])
    b.din("gqa_qn", [DEPTH, 128, 2])
    b.din("gqa_kn", [DEPTH, 128, 2])
    b.din("w_branch", [DEPTH, 4, 512, DM])
    b.din("w_out", [DEPTH, DM, DM])
    b.din("w_mlp_in", [DEPTH, DM, 4 * DM])
    b.din("w_mlp_out", [DEPTH, 4 * DM, DM])
    b.din("st_hgrn", [DEPTH, 2, 4, 128, 128])
    b.din("c_swa_kT", [DEPTH, 2, 64, 512])
    b.din("c_swa_v", [DEPTH, 512, 128])
    b.din("c_ckvT", [DEPTH, 128, 512])
    b.din("c_krT", [DEPTH, 32, 512])
    b.din("c_gqa_kT", [DEPTH, 2, 64, 512])
    b.din("c_gqa_v", [DEPTH, 512, 128])
    b.din("rope64", [4, 128, 2048])
    b.din("cmat", [6, 128, 128])
    b.din("swamask", [2, 128, 512], BF16)
    b.dout("o_hgrn", [DEPTH, 4, 2, 4, 128, 128])
    b.dout("o_swa_kT", [DEPTH, 128, 1024])
    b.dout("o_swa_v", [DEPTH, 1024, 128])
    b.dout("o_ckvT", [DEPTH, 128, 1024])
    b.dout("o_krT", [DEPTH, 32, 1024])
    b.dout("o_gqa_kT", [DEPTH, 128, 1024])
    b.dout("o_gqa_v", [DEPTH, 1024, 128])

    b.psum = [es.enter_context(nc.psum_tensor("psb%d" % i, [128, 512], F32)) for i in range(8)]

    cst = ExitStack()
    es.enter_context(cst)
    ones_bf = b.sb(cst, "ones_bf", [128, 128], BF16)
    ones_f = b.sb(cst, "ones_f", [128, 128], F32)
    cm = b.sb(cst, "cm", [128, 6, 128], F32)
    modT = b.sb(cst, "modT", [128, DEPTH, 48, 2], F32)
    gains = b.sb(cst, "gains", [128, DEPTH, 4, 8], F32)
    coef = b.sb(cst, "coef", [128, DEPTH, 2, 6, 8], F32)
    S.op("pool", lambda e: e.memset(ones_bf[:], 1.0), writes=["ones_bf"])
    onesAB = b.sb(cst, "onesAB", [128, 2, 128], BF16)
    S.op("pool", lambda e: e.memset(onesAB[:], 0.0), writes=["onesAB"])
    S.op("pool", lambda e: e.memset(onesAB[:, 0, 0:64], 1.0), writes=["onesAB"])
    S.op("pool", lambda e: e.memset(onesAB[:, 1, 64:128], 1.0), writes=["onesAB"])
    S.op("pool", lambda e: e.memset(ones_f[:], 1.0), writes=["ones_f"])
    S.dma("sp", cm[:], D["cmat"].rearrange("a p c -> p a c"), writes=["cm"])
    S.dma("sp", gains[:], D["gains"].rearrange("l p a c -> p l a c"), writes=["gains"])

    with ExitStack() as st:
        cT = b.sb(st, "cT", [128, 8, 2])
        scT = b.sb(st, "scT", [128, 8, 2])
        badaT = b.sb(st, "badaT", [128, DEPTH, 48])
        wa = [b.sb(st, "wa%d" % i, [128, 8, 768]) for i in range(2)]
        S.dma("sp", cT[:], D["cT"], writes=["cT"])
        S.dma("sp", badaT[:], D["b_adaT"].rearrange("l p j -> p l j"), writes=["badaT"])
        S.op("act", lambda e: e.activation(scT[:], cT[:], AF.Silu), reads=["cT"], writes=["scT"])
        n = 0
        for l in range(DEPTH):
            pt, pn = b.ps()
            for cg in range(8):
                w = wa[n % 2]
                wn = "wa%d" % (n % 2)
                n += 1
                S.dma("sp" if cg % 2 == 0 else "act", w[:],
                      D["w_ada"][l, :, cg * 768:(cg + 1) * 768].rearrange("(kc p) c -> p kc c", p=128), writes=[wn])
                for jj in range(6):
                    j = cg * 6 + jj
                    for kc in range(8):
                        S.op("pe", lambda e: e.matmul(pt[:, 2 * j:2 * j + 2], w[:, kc, jj * 128:(jj + 1) * 128], scT[:, kc, :],
                                                      start=(kc == 0), stop=(kc == 7)),
                             reads=[wn, "scT"], writes=[pn], pe_acc=True)
            S.op("dve", lambda e: e.tensor_tensor(modT[:, l], pt[:, 0:96].rearrange("p (j c) -> p j c", c=2),
                                                  badaT[:, l].unsqueeze(2).to_broadcast([128, 48, 2]), ALU.add),
                 reads=[pn, "badaT"], writes=["modT"])
        for l in range(DEPTH):
            for c in range(2):
                m = lambda i: modT[:, l, i * 8:(i + 1) * 8, c]
                S.op("dve", lambda e: e.scalar_tensor_tensor(coef[:, l, c, 0], m(1), 1.0, gains[:, l, 0], ALU.add, ALU.mult),
                     reads=["modT", "gains"], writes=["coef"])
                S.op("dve", lambda e: e.tensor_copy(coef[:, l, c, 1], m(0)), reads=["modT"], writes=["coef"])
                S.op("dve", lambda e: e.tensor_tensor(coef[:, l, c, 2], m(2), gains[:, l, 1], ALU.mult), reads=["modT", "gains"], writes=["coef"])
                S.op("dve", lambda e: e.scalar_tensor_tensor(coef[:, l, c, 3], m(4), 1.0, gains[:, l, 2], ALU.add, ALU.mult),
                     reads=["modT", "gains"], writes=["coef"])
                S.op("dve", lambda e: e.tensor_copy(coef[:, l, c, 4], m(3)), reads=["modT"], writes=["coef"])
                S.op("dve", lambda e: e.tensor_tensor(coef[:, l, c, 5], m(5), gains[:, l, 3], ALU.mult), reads=["modT", "gains"], writes=["coef"])
        S.barrier()

    def rstd_from_sumsq(pt, pn, out, outn, npart, ncol, inv_n):
        S.op("act", lambda e: e.activation(out[0:npart, 0:ncol], pt[0:npart, 0:ncol], AF.Ln, bias=epsb[0:npart, :], scale=inv_n),
             reads=[pn, "epsb"], writes=[outn])
        S.op("act", lambda e: e.activation(out[0:npart, 0:ncol], out[0:npart, 0:ncol], AF.Exp, scale=-0.5), reads=[outn], writes=[outn])

    epsb = b.sb(cst, "epsb", [128, 1], F32)
    S.op("pool", lambda e: e.memset(epsb[:], EPS), writes=["epsb"])

    def norm_mod_tile(st_unused, xt, xn, hT, hn, t0, a_ap, sh_ap, tmp, sq, rs):
        for kc in range(8):
            S.op("act", lambda e: e.activation(sq[:, kc, :], xt[:, kc, :], AF.Square), reads=[xn], writes=["sq"])
        pt, pn = b.ps()
        for kc in range(8):
            S.op("pe", lambda e: e.matmul(pt[:], ones_bf[:], sq[:, kc, :], start=(kc == 0), stop=(kc == 7)),
                 reads=["sq", "ones_bf"], writes=[pn], pe_acc=True)
        rstd_from_sumsq(pt, pn, rs, "rs", 128, 512, 1.0 / DM)
        for kc in range(8):
            S.op("dve", lambda e: e.tensor_tensor(tmp[:], xt[:, kc, :], rs[:], ALU.mult), reads=[xn, "rs"], writes=["tmp"])
            S.op("act", lambda e: e.activation(hT[:, kc, t0:t0 + 512], tmp[:], AF.Identity, bias=sh_ap[:, kc:kc + 1], scale=a_ap[:, kc:kc + 1]),
                 reads=["tmp", "coef"], writes=[hn])

    def stage_h(st, g, l, Xsrc, ia, ish):
        G = GR[g]
        T = G["T"]
        hT = b.sb(st, "hT", [128, 8, T], BF16)
        with ExitStack() as s2:
            xts = [b.sb(s2, "xt%d" % i, [128, 8, 512]) for i in range(2)]
            tmp = b.sb(s2, "tmp", [128, 512])
            sq = b.sb(s2, "sq", [128, 8, 512], BF16)
            rs = b.sb(s2, "rs", [128, 512])
            for tt in range(T // 512):
                xt, xn = xts[tt % 2], "xt%d" % (tt % 2)
                S.dma("sp", xt[:], Xsrc[:, tt * 512:(tt + 1) * 512].rearrange("(kc p) t -> p kc t", p=128), writes=[xn])
                norm_mod_tile(None, xt, xn, hT, "hT", tt * 512, coef[:, l, G["cond"], ia], coef[:, l, G["cond"], ish], tmp, sq, rs)
            S.barrier()
        return hT

    def stage_proj(g, l, hT):
        G = GR[g]
        T = G["T"]
        UT, UTM = D["UT_" + g], D["UTM_" + g]
        tm_groups = {1: [(0, 512, 0)], 2: [(0, 512, 512)], 3: [(0, 512, 1024)], 6: [(128, 128, 1536)], 8: [(288, 128, 1664)]}
        with ExitStack() as st:
            wb = [b.sb(st, "wb%d" % i, [128, 8, 512], BF16) for i in range(2)]
            ev = [b.sb(st, "ev%d" % i, [128, 512]) for i in range(4)]
            nev = 0
            for cg in range(17):
                ncol = 512 if cg < 16 else D_IN - 8192
                w, wn = wb[cg % 2], "wb%d" % (cg % 2)
                S.dma("pool", w[:, :, 0:ncol], D["w_in"][l, :, cg * 512:cg * 512 + ncol].rearrange("(kc p) c -> p kc c", p=128), writes=[wn])
                for tt in range(T // 512):
                    for cc in range((ncol + 127) // 128):
                        m = min(128, ncol - cc * 128)
                        pt, pn = b.ps()
                        for kc in range(8):
                            S.op("pe", lambda e: e.matmul(pt[0:m, :], w[:, kc, cc * 128:cc * 128 + m], hT[:, kc, tt * 512:(tt + 1) * 512],
                                                          start=(kc == 0), stop=(kc == 7)), reads=[wn, "hT"], writes=[pn], pe_acc=True)
                        e_, en = ev[nev % 4], "ev%d" % (nev % 4)
                        eng = "act" if nev % 2 == 0 else "dve"
                        nev += 1
                        if eng == "act":
                            S.op("act", lambda e: e.copy(e_[0:m, :], pt[0:m, :]), reads=[pn], writes=[en])
                        else:
                            S.op("dve", lambda e: e.tensor_copy(e_[0:m, :], pt[0:m, :]), reads=[pn], writes=[en])
                        r0 = cg * 512 + cc * 128
                        S.dma("sp", UT[r0:r0 + m, tt * 512:(tt + 1) * 512], e_[0:m, :], reads=[en], writes=[])
                for (c0, cn, dst) in tm_groups.get(cg, []):
                    for t4 in range(T // 128):
                        pt, pn = b.ps()
                        for kc in range(8):
                            S.op("pe", lambda e: e.matmul(pt[:, 0:cn], hT[:, kc, t4 * 128:(t4 + 1) * 128], w[:, kc, c0:c0 + cn],
                                                          start=(kc == 0), stop=(kc == 7)), reads=[wn, "hT"], writes=[pn], pe_acc=True)
                        e_, en = ev[nev % 4], "ev%d" % (nev % 4)
                        eng = "act" if nev % 2 == 0 else "dve"
                        nev += 1
                        if eng == "act":
                            S.op("act", lambda e: e.copy(e_[:, 0:cn], pt[:, 0:cn]), reads=[pn], writes=[en])
                        else:
                            S.op("dve", lambda e: e.tensor_copy(e_[:, 0:cn], pt[:, 0:cn]), reads=[pn], writes=[en])
                        S.dma("sp", UTM[t4 * 128:(t4 + 1) * 128, dst:dst + cn], e_[:, 0:cn], reads=[en], writes=[])
            S.barrier()

    def attn_unit(qT, qn, Kd, Nq, chunks, scale, o_out, on, wk, sink=None):
        po, pon = b.acc(0)
        pd, pdn = b.acc(1)
        nch = len(chunks)

        def emit_st(i):
            kT, kn, V, vn, nk, mask = chunks[i]
            pst, psn_ = b.ps()
            S.op("pe", lambda e: e.matmul(pst[0:nk, 0:Nq], kT, qT, start=True, stop=True), reads=[kn] + (qn if isinstance(qn, list) else [qn]), writes=[psn_])
            return pst, psn_

        cur = emit_st(0)
        for i, (kT, kn, V, vn, nk, mask) in enumerate(chunks):
            nxt = emit_st(i + 1) if i + 1 < nch else None
            pst, psn_ = cur
            E, En = wk["E"][i % 3], "E%d" % (i % 3)
            S.op("act", lambda e: e.activation(E[0:nk, 0:Nq], pst[0:nk, 0:Nq], AF.Exp, scale=scale), reads=[psn_], writes=[En])
            if mask is not None:
                S.op("pool", lambda e: e.tensor_tensor(E[0:nk, 0:Nq], E[0:nk, 0:Nq], mask, ALU.mult), reads=[En, "swamask"], writes=[En])
            last = (i == nch - 1) and sink is None
            S.op("pe", lambda e: e.matmul(po[0:64, 0:Nq], V, E[0:nk, 0:Nq], start=(i == 0), stop=(i == nch - 1)),
                 reads=[vn, En], writes=[pon], pe_acc=(i > 0))
            S.op("pe", lambda e: e.matmul(pd[0:64, 0:Nq], ones_bf[0:nk, 0:64], E[0:nk, 0:Nq], start=(i == 0), stop=last),
                 reads=["ones_bf", En], writes=[pdn], pe_acc=(i > 0))
            cur = nxt
        if sink is not None:
            S.op("pe", lambda e: e.matmul(pd[0:64, 0:Nq], ones_bf[0:1, 0:64], sink, start=False, stop=True),
                 reads=["ones_bf", "sinkrow"], writes=[pdn], pe_acc=True)
        rec = wk["rec"]
        S.op("dve", lambda e: e.reciprocal(rec[0:64, 0:Nq], pd[0:64, 0:Nq]), reads=[pdn], writes=["rec"])
        S.op("dve", lambda e: e.tensor_tensor(o_out, po[0:64, 0:Nq], rec[0:64, 0:Nq], ALU.mult), reads=[pon, "rec"], writes=[on])

    def attn_pair(qa, qan, qb, qbn, Nq, chunks, scale, o_out, on, wk, sink=None):
        po, pon = b.acc(0)
        pd, pdn = b.acc(1)
        nch = len(chunks)
        E = wk["E"]
        ne = len(E)

        def emit_st(i):
            kTa, kTb, kn, Va, Vb, vn, nk, mask = chunks[i]
            p1, n1 = b.ps()
            kna, knb = kn if isinstance(kn, tuple) else (kn, kn)
            S.op("pe", lambda e: e.matmul(p1[0:nk, 0:Nq], kTa, qa, start=True, stop=True), reads=[kna] + qan, writes=[n1])
            p2, n2 = b.ps()
            S.op("pe", lambda e: e.matmul(p2[0:nk, 0:Nq], kTb, qb, start=True, stop=True), reads=[knb] + qbn, writes=[n2])
            return (p1, n1, p2, n2)

        cur = emit_st(0)
        for i, (kTa, kTb, kn, Va, Vb, vn, nk, mask) in enumerate(chunks):
            nxt = emit_st(i + 1) if i + 1 < nch else None
            p1, n1, p2, n2 = cur
            Ea, Ean = E[(2 * i) % ne], "E%d" % ((2 * i) % ne)
            Eb, Ebn = E[(2 * i + 1) % ne], "E%d" % ((2 * i + 1) % ne)
            S.op("act", lambda e: e.activation(Ea[0:nk, 0:Nq], p1[0:nk, 0:Nq], AF.Exp, scale=scale), reads=[n1], writes=[Ean])
            S.op("act", lambda e: e.activation(Eb[0:nk, 0:Nq], p2[0:nk, 0:Nq], AF.Exp, scale=scale), reads=[n2], writes=[Ebn])
            if mask is not None:
                S.op("pool", lambda e: e.tensor_tensor(Ea[0:nk, 0:Nq], Ea[0:nk, 0:Nq], mask, ALU.mult), reads=[Ean, "swamask"], writes=[Ean])
                S.op("dve", lambda e: e.tensor_tensor(Eb[0:nk, 0:Nq], Eb[0:nk, 0:Nq], mask, ALU.mult), reads=[Ebn, "swamask"], writes=[Ebn])
            last = (i == nch - 1)
            S.op("pe", lambda e: e.matmul(po[:, 0:Nq], Va, Ea[0:nk, 0:Nq], start=(i == 0), stop=False), reads=[vn, Ean], writes=[pon], pe_acc=(i > 0))
            S.op("pe", lambda e: e.matmul(po[:, 0:Nq], Vb, Eb[0:nk, 0:Nq], start=False, stop=last), reads=[vn, Ebn], writes=[pon], pe_acc=True)
            S.op("pe", lambda e: e.matmul(pd[:, 0:Nq], onesAB[0:nk, 0, :], Ea[0:nk, 0:Nq], start=(i == 0), stop=False), reads=["onesAB", Ean], writes=[pdn], pe_acc=(i > 0))
            S.op("pe", lambda e: e.matmul(pd[:, 0:Nq], onesAB[0:nk, 1, :], Eb[0:nk, 0:Nq], start=False, stop=(last and sink is None)), reads=["onesAB", Ebn], writes=[pdn], pe_acc=True)
            cur = nxt
        if sink is not None:
            sa, sb_ = sink
            S.op("pe", lambda e: e.matmul(pd[:, 0:Nq], onesAB[0:1, 0, :], sa, start=False, stop=False), reads=["onesAB", "sinkrow"], writes=[pdn], pe_acc=True)
            S.op("pe", lambda e: e.matmul(pd[:, 0:Nq], onesAB[0:1, 1, :], sb_, start=False, stop=True), reads=["onesAB", "sinkrow"], writes=[pdn], pe_acc=True)
        rec = wk["rec"]
        S.op("dve", lambda e: e.reciprocal(rec[:, 0:Nq], pd[:, 0:Nq]), reads=[pdn], writes=["rec"])
        S.op("dve", lambda e: e.tensor_tensor(o_out, po[:, 0:Nq], rec[:, 0:Nq], ALU.mult), reads=[pon, "rec"], writes=[on])

    def rr_alloc(st, T, do_rope):
        sets = []
        for k in range(2):
            sets.append(dict(k=k, x=b.sb(st, "rr_x%d" % k, [128, T]), xs=(b.sb(st, "rr_xs%d" % k, [128, T]) if do_rope else None),
                             sq=b.sb(st, "rr_sq%d" % k, [128, 512], BF16), rs=b.sb(st, "rr_rs%d" % k, [128, T]), t1=b.sb(st, "rr_t1%d" % k, [128, 512])))
        return sets

    def rms_rope_rows(W_, p0, src_rows_fn, n_rows, T, gain2, gainn, do_norm, do_rope, rope_idx, out_bf, outn, out32=None, out32n=None,
                      out32_pre_rope=False):
        k = W_["k"]
        x, xs, sqb, rs, t1 = W_["x"], W_["xs"], W_["sq"], W_["rs"], W_["t1"]
        xn, xsn, sqn, rsn, t1n = ("rr_x%d" % k, "rr_xs%d" % k, "rr_sq%d" % k, "rr_rs%d" % k, "rr_t1%d" % k)
        for (r0, nr, ap) in src_rows_fn(False):
            S.dma("sp", x[p0 + r0:p0 + r0 + nr, 0:T], ap, writes=[xn])
        if do_rope:
            for (r0, nr, ap) in src_rows_fn(True):
                S.dma("act", xs[p0 + r0:p0 + r0 + nr, 0:T], ap, writes=[xsn])
        nr = n_rows
        pr = slice(p0, p0 + nr)
        W = min(512, T)
        for c in range(T // W):
            cs = slice(c * W, (c + 1) * W)
            if do_norm:
                S.op("act", lambda e: e.activation(sqb[pr, 0:W], x[pr, cs], AF.Square), reads=[xn], writes=[sqn])
                pt, pn = b.ps()
                S.op("pe", lambda e: e.matmul(pt[:, 0:W], ones_bf[pr, :], sqb[pr, 0:W], start=True, stop=True),
                     reads=[sqn, "ones_bf"], writes=[pn])
                S.op("act", lambda e: e.activation(rs[pr, cs], pt[pr, 0:W], AF.Ln, bias=epsb[pr, :], scale=1.0 / nr), reads=[pn, "epsb"], writes=[rsn])
                S.op("act", lambda e: e.activation(rs[pr, cs], rs[pr, cs], AF.Exp, scale=-0.5), reads=[rsn], writes=[rsn])
                S.op("dve", lambda e: e.scalar_tensor_tensor(x[pr, cs], x[pr, cs], gain2[pr, 0:1], rs[pr, cs], ALU.mult, ALU.mult),
                     reads=[xn, rsn, gainn], writes=[xn])
                if do_rope:
                    S.op("dve", lambda e: e.scalar_tensor_tensor(xs[pr, cs], xs[pr, cs], gain2[pr, 1:2], rs[pr, cs], ALU.mult, ALU.mult),
                         reads=[xsn, rsn, gainn], writes=[xsn])
            if out32 is not None and out32_pre_rope:
                S.op("pool", lambda e: e.tensor_copy(out32[pr, cs], x[pr, cs]), reads=[xn], writes=[out32n])
            if do_rope:
                S.op("dve", lambda e: e.tensor_tensor(x[pr, cs], x[pr, cs], rope[pr, rope_idx, cs], ALU.mult), reads=[xn, "rope"], writes=[xn])
                S.op("pool", lambda e: e.tensor_tensor(t1[pr, 0:W], xs[pr, cs], rope[pr, rope_idx + 1, cs], ALU.mult), reads=[xsn, "rope"], writes=[t1n])
                S.op("dve", lambda e: e.tensor_tensor(out_bf[:, cs], x[pr, cs], t1[pr, 0:W], ALU.add), reads=[xn, t1n], writes=[outn])
            else:
                S.op("act", lambda e: e.copy(out_bf[:, cs], x[pr, cs]), reads=[xn], writes=[outn])

    def swap_rows(base, T0, T1, UT, nd):
        q = nd // 4
        def f(swapped):
            if not swapped:
                return [(0, nd, UT[base:base + nd, T0:T1])]
            return [(0, q, UT[base + q:base + 2 * q, T0:T1]), (q, q, UT[base:base + q, T0:T1]),
                    (2 * q, q, UT[base + 3 * q:base + 4 * q, T0:T1]), (3 * q, q, UT[base + 2 * q:base + 3 * q, T0:T1])]
        return f

    rope = b.sb(cst, "rope", [128, 4, 2048], BF16)
    S.dma("pool", rope[:], D["rope64"].rearrange("a p t -> p a t"), writes=["rope"])

    def stage_gqa_like(g, l, kind):
        G = GR[g]
        T, L, P, nseq, do_rope = G["T"], G["L"], G["P"], G["nseq"], G["rope"]
        UT, UTM, OT = D["UT_" + g], D["UTM_" + g], D["OT_" + g]
        qo, ko = (OFF["sq"], OFF["sk"]) if kind == "swa" else (OFF["gq"], OFF["gk"])
        vcol = 1536 if kind == "swa" else 1664
        bi = 1 if kind == "swa" else 3
        do_norm = kind == "gqa"
        scale = 64 ** -0.5
        nkc_ctx = P // 128
        with ExitStack() as st:
            kT = b.sb(st, "kT", [128, P + T], BF16)
            Vt = b.sb(st, "Vt", [128, (P + T) // 128, 2, 128], BF16)
            qTh = b.sb(st, "qTh", [128, 8, T], BF16)
            gq2 = b.sb(st, "gq2", [128, 2]); gk2 = b.sb(st, "gk2", [128, 2])
            sinkrow = b.sb(st, "sinkrow", [1, 8, 128], BF16)
            sk32 = b.sb(st, "sk32", [1, 8])
            wk = dict(E=[b.sb(st, "E%d" % i, [128, 512], BF16) for i in range(6)], rec=b.sb(st, "rec", [128, 512]))
            obs = [b.sb(st, "ob%d" % i, [128, 512], BF16) for i in range(2)]
            swm = b.sb(st, "swamask", [128, 2, 512], BF16)
            S.dma("sp", swm[:], D["swamask"].rearrange("a p c -> p a c"), writes=["swamask"])
            S.op("pool", lambda e: e.memset(qTh[64:128, 0:4, :], 0.0), writes=["qTh%d" % hh for hh in range(4)])
            S.op("pool", lambda e: e.memset(qTh[0:64, 4:8, :], 0.0), writes=["qTh%d" % hh for hh in range(4, 8)])
            S.op("pool", lambda e: e.memset(Vt[:], 0.0), writes=["Vt"])
            if do_norm:
                S.dma("sp", gq2[:], D["gqa_qn"][l], writes=["gq2"])
                S.dma("sp", gk2[:], D["gqa_kn"][l], writes=["gk2"])
            else:
                S.dma("sp", sk32[:], D["sink"][l], writes=["sk32"])
                S.op("act", lambda e: e.activation(sk32[:], sk32[:], AF.Exp), reads=["sk32"], writes=["sk32"])
                S.op("dve", lambda e: e.tensor_copy(sinkrow[:], sk32[:].unsqueeze(2).to_broadcast([1, 8, 128])), reads=["sk32"], writes=["sinkrow"])
            with ExitStack() as s2:
                RR = rr_alloc(s2, T, do_rope)
                k32 = b.sb(s2, "k32", [128, T]) if g == "p" else None
                v32 = b.sb(s2, "v32", [128, (P + T) // 128, 128])
                if P:
                    src_k = D["c_swa_kT"] if kind == "swa" else D["c_gqa_kT"]
                    src_v = D["c_swa_v"] if kind == "swa" else D["c_gqa_v"]
                    S.dma("pool", kT[:, 0:P], src_k[l].rearrange("h d t -> (h d) t"), writes=["kT"])
                    S.dma("sp", v32[:, 0:P // 128, :], src_v[l].rearrange("(c p) f -> p c f", p=128), writes=["v32"])
                S.dma("sp", v32[:, P // 128:, :], UTM[:, vcol:vcol + 128].rearrange("(c p) f -> p c f", p=128), writes=["v32"])
                S.op("dve", lambda e: e.tensor_copy(Vt[:, :, 0, 0:64], v32[:, :, 0:64]), reads=["v32"], writes=["Vt"])
                S.op("dve", lambda e: e.tensor_copy(Vt[:, :, 1, 64:128], v32[:, :, 64:128]), reads=["v32"], writes=["Vt"])
                if g == "p":
                    dst = D["o_swa_v"] if kind == "swa" else D["o_gqa_v"]
                    S.dma("act", dst[l].rearrange("(c p) f -> p c f", p=128), v32[:], reads=["v32"], writes=[])
                nrr = 0
                for kvh in range(2):
                    p0 = kvh * 64
                    rms_rope_rows(RR[nrr % 2], p0, swap_rows(ko + kvh * 64, 0, T, UT, 64), 64, T, gk2, "gk2", do_norm, do_rope, 0,
                                  kT[p0:p0 + 64, P:P + T], "kT", out32=k32, out32n="k32", out32_pre_rope=True)
                    nrr += 1
                if g == "p":
                    dst = D["o_swa_kT"] if kind == "swa" else D["o_gqa_kT"]
                    S.dma("sp", dst[l], k32[:], reads=["k32"], writes=[])
                for h in range(8):
                    p0 = (h // 4) * 64
                    rms_rope_rows(RR[nrr % 2], p0, swap_rows(qo + h * 64, 0, T, UT, 64), 64, T, gq2, "gq2", do_norm, do_rope, 0,
                                  qTh[p0:p0 + 64, h, :], "qTh%d" % h)
                    nrr += 1
                nu = 0
                qna = ["qTh%d" % hh for hh in range(4)]
                qnb = ["qTh%d" % hh for hh in range(4, 8)]
                for sq_ in range(nseq):
                    T0 = sq_ * L
                    kb0 = P + T0
                    for qb in range(L // 128):
                        cols = [(c * 128, None) for c in range(nkc_ctx)]
                        if kind == "swa" and P:
                            cols += [(kb0 + kb * 128, mi) for (kb, mi) in ((qb - 1, 0), (qb, None), (qb + 1, 1)) if 0 <= kb < L // 128]
                        else:
                            cols += [(kb0 + kb * 128, None) for kb in range(L // 128)]
                        chunks = [(kT[:, c0:c0 + 128], kT[:, c0:c0 + 128], "kT", Vt[:, c0 // 128, 0, :], Vt[:, c0 // 128, 1, :], "Vt", 128,
                                   None if mi is None else swm[:, mi, :]) for (c0, mi) in cols]
                        ob, obn = obs[nu % 2], "ob%d" % (nu % 2)
                        nu += 1
                        q0 = T0 + qb * 128
                        attn_pair(qTh[:, 0:4, q0:q0 + 128], qna, qTh[:, 4:8, q0:q0 + 128], qnb, 512, chunks, scale, ob[:], obn, wk,
                                  sink=((sinkrow[0:1, 0:4, :], sinkrow[0:1, 4:8, :]) if kind == "swa" else None))
                        for kvh in range(2):
                            S.dma("sp" if kvh == 0 else "act", OT[bi, kvh * 256:(kvh + 1) * 256, q0:q0 + 128].rearrange("(h d) t -> d h t", d=64),
                                  ob[kvh * 64:(kvh + 1) * 64, :].rearrange("d (h t) -> d h t", h=4), reads=[obn], writes=[])
                S.barrier()

    def stage_mla(g, l):
        G = GR[g]
        T, L, P, nseq, do_rope = G["T"], G["L"], G["P"], G["nseq"], G["rope"]
        UT, OT = D["UT_" + g], D["OT_" + g]
        scale = 96 ** -0.5
        NK = P + L
        for sq_ in range(nseq):
            T0, T1 = sq_ * L, (sq_ + 1) * L
            with ExitStack() as st:
                ckvT = b.sb(st, "ckvT", [128, NK], BF16)
                krT = b.sb(st, "krT", [96, NK], BF16)
                cqn = b.sb(st, "cqn", [128, 2, L], BF16)
                wuq = b.sb(st, "wuq", [128, 2, 768], BF16); wuqs = b.sb(st, "wuqs", [128, 2, 768], BF16)
                wukv = b.sb(st, "wukv", [128, 1024], BF16)
                g_q = b.sb(st, "g_q", [128, 2]); g_kv = b.sb(st, "g_kv", [128, 1])
                S.dma("pool", wuq[:], D["w_uq"][l].rearrange("(kc p) c -> p kc c", p=128), writes=["wuq"])
                S.dma("pool", wuqs[:], D["w_uq_sw"][l].rearrange("(kc p) c -> p kc c", p=128), writes=["wuqs"])
                S.dma("pool", wukv[:], D["w_ukv"][l], writes=["wukv"])
                S.dma("sp", g_q[:], D["mla_qnT"][l], writes=["g_q"])
                S.dma("sp", g_kv[:], D["mla_kvn"][l], writes=["g_kv"])
                with ExitStack() as s2:
                    x = b.sb(s2, "m_x", [128, 2, L]); sqb = b.sb(s2, "m_sq", [128, 2, 512], BF16); rs = b.sb(s2, "m_rs", [128, 512])
                    xk = b.sb(s2, "m_xk", [128, L]); kr32 = b.sb(s2, "m_kr", [96, L]); krs = b.sb(s2, "m_krs", [96, L]); t1 = b.sb(s2, "m_t1", [96, 512])
                    if P:
                        S.dma("pool", ckvT[:, 0:P], D["c_ckvT"][l], writes=["ckvT"])
                        S.dma("pool", krT[64:96, 0:P], D["c_krT"][l], writes=["krT"])
                    S.dma("sp", x[:], UT[OFF["cq"]:OFF["cq"] + 256, T0:T1].rearrange("(kc p) t -> p kc t", p=128), writes=["m_x"])
                    S.dma("sp", xk[:], UT[OFF["ckv"]:OFF["ckv"] + 128, T0:T1], writes=["m_xk"])
                    S.dma("sp", kr32[64:96, :], UT[OFF["kr"]:OFF["kr"] + 32, T0:T1], writes=["m_kr"])
                    if do_rope:
                        for (r0, nr, ap) in swap_rows(OFF["kr"], T0, T1, UT, 32)(True):
                            S.dma("act", krs[64 + r0:64 + r0 + nr, :], ap, writes=["m_krs"])
                    for c in range(L // 512 if L >= 512 else 1):
                        w = min(512, L)
                        cs = slice(c * w, (c + 1) * w)
                        pt, pn = b.ps()
                        for kc in range(2):
                            S.op("act", lambda e: e.activation(sqb[:, kc, 0:w], x[:, kc, cs], AF.Square), reads=["m_x"], writes=["m_sq"])
                        for kc in range(2):
                            S.op("pe", lambda e: e.matmul(pt[:, 0:w], ones_bf[:], sqb[:, kc, 0:w], start=(kc == 0), stop=(kc == 1)),
                                 reads=["m_sq", "ones_bf"], writes=[pn], pe_acc=True)
                        rstd_from_sumsq(pt, pn, rs, "m_rs", 128, w, 1.0 / 256)
                        for kc in range(2):
                            S.op("dve", lambda e: e.scalar_tensor_tensor(cqn[:, kc, cs], x[:, kc, cs], g_q[:, kc:kc + 1], rs[:, 0:w], ALU.mult, ALU.mult),
                                 reads=["m_x", "m_rs", "g_q"], writes=["cqn"])
                        pt, pn = b.ps()
                        S.op("act", lambda e: e.activation(sqb[:, 0, 0:w], xk[:, cs], AF.Square), reads=["m_xk"], writes=["m_sq"])
                        S.op("pe", lambda e: e.matmul(pt[:, 0:w], ones_bf[:], sqb[:, 0, 0:w], start=True, stop=True), reads=["m_sq", "ones_bf"], writes=[pn])
                        rstd_from_sumsq(pt, pn, rs, "m_rs", 128, w, 1.0 / 128)
                        S.op("dve", lambda e: e.scalar_tensor_tensor(xk[:, cs], xk[:, cs], g_kv[:, 0:1], rs[:, 0:w], ALU.mult, ALU.mult),
                             reads=["m_xk", "m_rs", "g_kv"], writes=["m_xk"])
                        S.op("pool", lambda e: e.tensor_copy(ckvT[:, P + c * w:P + (c + 1) * w], xk[:, cs]), reads=["m_xk"], writes=["ckvT"])
                        if do_rope:
                            S.op("dve", lambda e: e.tensor_tensor(t1[64:96, 0:w], kr32[64:96, cs], rope[64:96, 2, cs], ALU.mult), reads=["m_kr", "rope"], writes=["m_t1"])
                            S.op("pool", lambda e: e.tensor_tensor(krs[64:96, cs], krs[64:96, cs], rope[64:96, 3, cs], ALU.mult), reads=["m_krs", "rope"], writes=["m_krs"])
                            S.op("dve", lambda e: e.tensor_tensor(krT[64:96, P + c * w:P + (c + 1) * w], t1[64:96, 0:w], krs[64:96, cs], ALU.add),
                                 reads=["m_t1", "m_krs"], writes=["krT"])
                        else:
                            S.op("dve", lambda e: e.tensor_copy(krT[64:96, P + c * w:P + (c + 1) * w], kr32[64:96, cs]), reads=["m_kr"], writes=["krT"])
                    if g == "p":
                        S.dma("sp", D["o_ckvT"][l, :, T0:T1], xk[:], reads=["m_xk"], writes=[])
                        S.dma("sp", D["o_krT"][l, :, T0:T1], kr32[64:96, :], reads=["m_kr"], writes=[])
                    S.barrier()
                Vall = b.sb(st, "Vall", [128, NK // 128, 8, 128], BF16)
                S.op("pool", lambda e: e.memset(Vall[:], 0.0), writes=["Vall"])
                for c in range(NK // 128):
                    pt, pn = b.ps()
                    S.op("pe", lambda e: e.matmul(pt[:, :], ckvT[:, c * 128:(c + 1) * 128],
                                                  wukv[:].rearrange("p (h x) -> p h x", x=128)[:, :, 64:128], start=True, stop=True),
                         reads=["ckvT", "wukv"], writes=[pn])
                    pv = pt[:, :].rearrange("p (h two d) -> p h two d", two=2, d=64)
                    S.op("act", lambda e: e.copy(Vall[:, c, 0::2, 0:64], pv[:, :, 0, :]), reads=[pn], writes=["Vall"])
                    S.op("dve", lambda e: e.tensor_copy(Vall[:, c, 1::2, 64:128], pv[:, :, 1, :]), reads=[pn], writes=["Vall"])
                S.barrier()
                with ExitStack() as s2:
                    wk = dict(E=[b.sb(s2, "E%d" % i, [128, 512], BF16) for i in range(6)], rec=b.sb(s2, "rec", [128, 512]))
                    kTh = [b.sb(s2, "kTh%d" % i, [128, NK], BF16) for i in range(4)]
                    qh = [b.sb(s2, "qh%d" % i, [128, L], BF16) for i in range(4)]
                    obs = [b.sb(s2, "ob%d" % i, [128, 512], BF16) for i in range(2)]
                    t2 = b.sb(s2, "m_t2", [96, 512]); t3 = b.sb(s2, "m_t3", [96, 512])
                    for i in range(4):
                        S.op("pool", lambda e: e.memset(kTh[i][96:128, :], 0.0), writes=["kTh%d" % i])
                        S.op("pool", lambda e: e.memset(qh[i][96:128, :], 0.0), writes=["qh%d" % i])
                    nu = 0
                    W = min(512, L)
                    for hp in range(4):
                        bufs = []
                        for h2 in range(2):
                            h = hp * 2 + h2
                            bi_ = (hp % 2) * 2 + h2
                            kt, ktn = kTh[bi_], "kTh%d" % bi_
                            q_, q_n = qh[bi_], "qh%d" % bi_
                            bufs.append((kt, ktn, q_, q_n))
                            for c in range((NK + 511) // 512):
                                w = min(512, NK - c * 512)
                                pt, pn = b.ps()
                                S.op("pe", lambda e: e.matmul(pt[:, 0:w], wukv[:, h * 128:(h + 1) * 128], ckvT[:, c * 512:c * 512 + w], start=True, stop=True),
                                     reads=["wukv", "ckvT"], writes=[pn])
                                S.op("act", lambda e: e.copy(kt[0:64, c * 512:c * 512 + w], pt[0:64, 0:w]), reads=[pn], writes=[ktn])
                            S.op("pool", lambda e: e.tensor_copy(kt[64:96, :], krT[64:96, :]), reads=["krT"], writes=[ktn])
                            for c in range(L // W):
                                cs = slice(c * W, (c + 1) * W)
                                pt, pn = b.ps()
                                for kc in range(2):
                                    S.op("pe", lambda e: e.matmul(pt[0:96, 0:W], wuq[:, kc, h * 96:(h + 1) * 96], cqn[:, kc, cs], start=(kc == 0), stop=(kc == 1)),
                                         reads=["wuq", "cqn"], writes=[pn], pe_acc=(kc > 0))
                                S.op("act", lambda e: e.copy(q_[0:64, cs], pt[0:64, 0:W]), reads=[pn], writes=[q_n])
                                if do_rope:
                                    pt2, pn2 = b.ps()
                                    for kc in range(2):
                                        S.op("pe", lambda e: e.matmul(pt2[0:96, 0:W], wuqs[:, kc, h * 96:(h + 1) * 96], cqn[:, kc, cs], start=(kc == 0), stop=(kc == 1)),
                                             reads=["wuqs", "cqn"], writes=[pn2], pe_acc=(kc > 0))
                                    S.op("dve", lambda e: e.tensor_tensor(t2[64:96, 0:W], pt[64:96, 0:W], rope[64:96, 2, cs], ALU.mult), reads=[pn, "rope"], writes=["m_t2"])
                                    S.op("dve", lambda e: e.tensor_tensor(t3[64:96, 0:W], pt2[64:96, 0:W], rope[64:96, 3, cs], ALU.mult), reads=[pn2, "rope"], writes=["m_t3"])
                                    S.op("pool", lambda e: e.tensor_tensor(q_[64:96, cs], t2[64:96, 0:W], t3[64:96, 0:W], ALU.add), reads=["m_t2", "m_t3"], writes=[q_n])
                                else:
                                    S.op("dve", lambda e: e.tensor_copy(q_[64:96, cs], pt[64:96, 0:W]), reads=[pn], writes=[q_n])
                        (kta, ktan, qa, qan), (ktb, ktbn, qb_, qbn) = bufs
                        for c in range(L // W):
                            chunks = [(kta[:, kc * 128:(kc + 1) * 128], ktb[:, kc * 128:(kc + 1) * 128], (ktan, ktbn), Vall[:, kc, hp * 2, :], Vall[:, kc, hp * 2 + 1, :], "Vall", 128, None)
                                      for kc in range(NK // 128)]
                            ob, obn = obs[nu % 2], "ob%d" % (nu % 2)
                            nu += 1
                            attn_pair(qa[:, c * W:(c + 1) * W], [qan], qb_[:, c * W:(c + 1) * W], [qbn], W, chunks, scale, ob[:, 0:W], obn, wk)
                            S.dma("sp", OT[2, hp * 128:(hp + 1) * 128, T0 + c * W:T0 + (c + 1) * W], ob[:, 0:W], reads=[obn], writes=[])
                    S.barrier()

    def stage_hgrn(g, l):
        G = GR[g]
        T, L, P, nseq = G["T"], G["L"], G["P"], G["nseq"]
        UT, UTM, OT = D["UT_" + g], D["UTM_" + g], D["OT_" + g]
        NTL = L // 128
        ident, triU, triL, sL, sU, csel = (cm[:, i, :] for i in range(6))
        with ExitStack() as st:
            lbb = b.sb(st, "lbb", [128, 2, 512]); oml = b.sb(st, "oml", [128, 2, 512])
            with ExitStack() as s2:
                lbr = b.sb(s2, "lbr", [128, DEPTH, 2, 512]); den = b.sb(s2, "lden", [128, 2, 512])
                S.dma("sp", lbr[:], D["lbT"].partition_broadcast(128), writes=["lbr"])
                S.op("act", lambda e: e.activation(lbr[:], lbr[:], AF.Exp), reads=["lbr"], writes=["lbr"])
                S.op("dve", lambda e: e.tensor_tensor(den[:], lbr[:, 0], lbr[:, 1], ALU.add), reads=["lbr"], writes=["lden"])
                S.op("dve", lambda e: e.reciprocal(den[:], den[:]), reads=["lden"], writes=["lden"])
                if l == 0:
                    S.op("pool", lambda e: e.memset(lbb[:], 0.0), writes=["lbb"])
                else:
                    S.op("dve", lambda e: e.tensor_tensor(lbb[:], lbr[:, 1], den[:], ALU.mult), reads=["lbr", "lden"], writes=["lbb"])
                S.op("dve", lambda e: e.tensor_scalar(oml[:], lbb[:], -1.0, 1.0, ALU.mult, ALU.add), reads=["lbb"], writes=["oml"])
                S.barrier()
            hn = b.sb(st, "hn", [128, 4]); S.dma("sp", hn[:], D["hgrn_normT"][l], writes=["hn"])
            Sst = [b.sb(st, "Sst%d" % i, [128, 2, 4, 128]) for i in range(2)]
            oall = b.sb(st, "oall", [128, 2, 4, L], BF16)
            W_ = {}
            for d in range(2):
                for k in range(2):
                    sfx = "%d%d" % (d, k)
                    W_[d, k] = dict(
                        sfx=sfx,
                        tt=b.sb(st, "h_t" + sfx, [128, 512]), vv=b.sb(st, "h_v" + sfx, [128, 512]), gg=b.sb(st, "h_g" + sfx, [128, 512]),
                        kt=b.sb(st, "h_kt" + sfx, [128, 512]), kh=b.sb(st, "h_kh" + sfx, [128, 512]),
                        khm=b.sb(st, "h_khm" + sfx, [128, 4, 4, 128]), qT=b.sb(st, "h_qT" + sfx, [128, 4, 128]), qt=b.sb(st, "h_qt" + sfx, [128, 4, 128]),
                        eb=b.sb(st, "h_eb" + sfx, [128, 4, 128]), ktT=b.sb(st, "h_ktT" + sfx, [128, 4, 128]), AT=b.sb(st, "h_AT" + sfx, [128, 4, 128]),
                        tmpo=b.sb(st, "h_tmpo" + sfx, [128, 512]))
            etot = [b.sb(st, "h_etot%d" % k, [128, 2, 4, 4]) for k in range(2)]
            fin = dict(os=b.sb(st, "f_os", [128, 4, 256]), gT=b.sb(st, "f_gT", [128, 4, 256]), sq=b.sb(st, "f_sq", [128, 4, 256], BF16),
                       rs=b.sb(st, "f_rs", [128, 4, 256]), ob=b.sb(st, "f_ob", [128, 4, 256], BF16))
            b.nrot = 4
            M4 = lambda M: M.unsqueeze(1).to_broadcast([128, 4, 128])

            def sstn(sp, d, h):
                return "Sst%d_%d%d" % (sp, d, h)

            def partA(rec):
                k = rec["k"]
                for d in range(2):
                    w = W_[d, k]; x = w["sfx"]
                    t0 = rec["T0"] + rec["ti"][d] * 128
                    S.dma("sp", w["tt"][:], UTM[t0:t0 + 128, d * 512:(d + 1) * 512], writes=["h_t" + x])
                    S.dma("sp", w["vv"][:], UTM[t0:t0 + 128, 1024:1536], writes=["h_v" + x])
                    S.dma("act", w["qT"][:], UT[0:512, t0:t0 + 128].rearrange("(h p) t -> p h t", p=128), writes=["h_qT" + x])
                for d in range(2):
                    w = W_[d, k]; x = w["sfx"]
                    S.op("act", lambda e: e.activation(w["tt"][:], w["tt"][:], AF.Sigmoid), reads=["h_t" + x], writes=["h_t" + x])
                for d in range(2):
                    w = W_[d, k]; x = w["sfx"]
                    S.op("act", lambda e: e.activation(w["qT"][:], w["qT"][:], AF.Silu), reads=["h_qT" + x], writes=["h_qT" + x])
                for d in range(2):
                    w = W_[d, k]; x = w["sfx"]
                    S.op("dve", lambda e: e.tensor_tensor(w["tt"][:], w["tt"][:], oml[:, d], ALU.mult), reads=["h_t" + x, "oml"], writes=["h_t" + x])
                    S.op("dve", lambda e: e.scalar_tensor_tensor(w["gg"][:], w["tt"][:], 1e-30, lbb[:, d], ALU.max, ALU.add), reads=["h_t" + x, "lbb"], writes=["h_g" + x])
                for d in range(2):
                    w = W_[d, k]; x = w["sfx"]
                    S.op("act", lambda e: e.activation(w["gg"][:], w["gg"][:], AF.Ln), reads=["h_g" + x], writes=["h_g" + x])
                for d in range(2):
                    w = W_[d, k]; x = w["sfx"]
                    S.op("dve", lambda e: e.tensor_tensor(w["tt"][:], oml[:, d], w["tt"][:], ALU.subtract), reads=["h_t" + x, "oml"], writes=["h_t" + x])

            def partB(rec, stage):
                k = rec["k"]
                Ms = [(triU, sL), (triL, sU)]
                if stage == 0:
                    pbs = []
                    for d in range(2):
                        w = W_[d, k]; x = w["sfx"]
                        pb, pbn = b.ps()
                        S.op("pe", lambda e: e.matmul(pb[:], Ms[d][0], w["gg"][:], start=True, stop=True), reads=["cm", "h_g" + x], writes=[pbn])
                        pr, prn = b.ps()
                        S.op("pe", lambda e: e.matmul(pr[:], Ms[d][1], w["gg"][:], start=True, stop=True), reads=["cm", "h_g" + x], writes=[prn])
                        pbs.append((pb, pbn, pr, prn))
                    for d in range(2):
                        w = W_[d, k]; x = w["sfx"]
                        pb, pbn, pr, prn = pbs[d]
                        S.op("act", lambda e: e.activation(w["kt"][:], pb[:], AF.Exp, scale=-1.0), reads=[pbn], writes=["h_kt" + x])
                        S.op("act", lambda e: e.activation(w["kh"][:], pr[:], AF.Exp), reads=[prn], writes=["h_kh" + x])
                    pts = []
                    ptot, ptotn = b.ps()
                    for d in range(2):
                        w = W_[d, k]; x = w["sfx"]
                        pbt, pbtn = b.ps()
                        for h in range(4):
                            hs = slice(h * 128, (h + 1) * 128)
                            S.op("pe", lambda e: e.matmul(pbt[:, hs], w["gg"][:, hs], Ms[d][0], start=True, stop=True), reads=["h_g" + x, "cm"], writes=[pbtn], pe_acc=(h > 0))
                            S.op("pe", lambda e: e.matmul(ptot[:, d * 16 + h * 4:d * 16 + h * 4 + 4], w["gg"][:, hs], csel[:, 0:4], start=True, stop=True),
                                 reads=["h_g" + x, "cm"], writes=[ptotn], pe_acc=(d > 0 or h > 0))
                        pts.append((pbt, pbtn))
                    for d in range(2):
                        w = W_[d, k]; x = w["sfx"]
                        S.op("dve", lambda e: e.tensor_tensor(w["kt"][:], w["kt"][:], w["tt"][:], ALU.mult), reads=["h_kt" + x, "h_t" + x], writes=["h_kt" + x])
                        S.op("dve", lambda e: e.tensor_tensor(w["kh"][:], w["kh"][:], w["tt"][:], ALU.mult), reads=["h_kh" + x, "h_t" + x], writes=["h_kh" + x])
                    for d in range(2):
                        w = W_[d, k]; x = w["sfx"]
                        pbt, pbtn = pts[d]
                        S.op("act", lambda e: e.activation(w["eb"][:], pbt[:].rearrange("p (h t) -> p h t", h=4), AF.Exp), reads=[pbtn], writes=["h_eb" + x])
                    S.op("act", lambda e: e.activation(etot[k][:], ptot[:, 0:32].rearrange("p (d h j) -> p d h j", d=2, h=4), AF.Exp), reads=[ptotn], writes=["h_etot%d" % k])
                    for d in range(2):
                        w = W_[d, k]; x = w["sfx"]
                        S.op("dve", lambda e: e.tensor_tensor(w["qt"][:], w["qT"][:], w["eb"][:], ALU.mult), reads=["h_qT" + x, "h_eb" + x], writes=["h_qt" + x])
                    for d in range(2):
                        w = W_[d, k]; x = w["sfx"]
                        for j in range(4):
                            S.op("act", lambda e: e.activation(w["khm"][:, :, j, :], w["kh"][:].rearrange("p (h c) -> p h c", h=4), AF.Copy, scale=csel[:, j:j + 1]),
                                 reads=["h_kh" + x, "cm"], writes=["h_khm" + x])
                elif stage == 1:
                    for d in range(2):
                        w = W_[d, k]; x = w["sfx"]
                        pk, pkn = b.ps()
                        for h in range(4):
                            hs = slice(h * 128, (h + 1) * 128)
                            S.op("pe", lambda e: e.matmul(pk[:, hs], w["kt"][:, hs], ident, start=True, stop=True), reads=["h_kt" + x, "cm"], writes=[pkn], pe_acc=(h > 0))
                        S.op("act", lambda e: e.copy(w["ktT"][:], pk[:].rearrange("p (h t) -> p h t", h=4)), reads=[pkn], writes=["h_ktT" + x])
                elif stage == 2:
                    for d in range(2):
                        w = W_[d, k]; x = w["sfx"]
                        pa, pan = b.ps()
                        for h in range(4):
                            hs = slice(h * 128, (h + 1) * 128)
                            S.op("pe", lambda e: e.matmul(pa[:, hs], w["ktT"][:, h, :], w["qt"][:, h, :], start=True, stop=True), reads=["h_ktT" + x, "h_qt" + x], writes=[pan], pe_acc=(h > 0))
                        S.op("dve", lambda e: e.tensor_tensor(w["AT"][:], pa[:].rearrange("p (h t) -> p h t", h=4), M4(Ms[d][0]), ALU.mult), reads=[pan, "cm"], writes=["h_AT" + x])
                else:
                    for d in range(2):
                        w = W_[d, k]; x = w["sfx"]
                        po_, pon_ = b.acc(d)
                        for h in range(4):
                            hs = slice(h * 128, (h + 1) * 128)
                            S.op("pe", lambda e: e.matmul(po_[:, hs], w["vv"][:, hs], w["AT"][:, h, :], start=True, stop=True), reads=["h_v" + x, "h_AT" + x], writes=[pon_], pe_acc=(h > 0))
                        S.op("act", lambda e: e.copy(w["tmpo"][:], po_[:]), reads=[pon_], writes=["h_tmpo" + x])

            def chunk_step(rec, n_):
                k = rec["k"]; sp = rec["sp"]
                pss = []
                for d in range(2):
                    w = W_[d, k]; x = w["sfx"]
                    j = n_ if d == 0 else 3 - n_
                    js = slice(j * 32, (j + 1) * 32)
                    pi_, pin_ = b.acc(2 + d)
                    ps_, psn2 = b.ps()
                    pss.append((ps_, psn2, j))
                    for h in range(4):
                        hs = slice(h * 128, (h + 1) * 128)
                        S.op("pe", lambda e: e.matmul(pi_[:, h * 128 + j * 32:h * 128 + (j + 1) * 32], Sst[sp][:, d, h, :], w["qt"][:, h, js], start=True, stop=True),
                             reads=[sstn(sp, d, h), "h_qt" + x], writes=[pin_], pe_acc=(n_ > 0 or h > 0))
                        S.op("pe", lambda e: e.matmul(ps_[:, hs], w["khm"][:, h, j, :], w["vv"][:, hs], start=True, stop=True),
                             reads=["h_khm" + x, "h_v" + x], writes=[psn2], pe_acc=(h > 0))
                for d in range(2):
                    w = W_[d, k]; x = w["sfx"]
                    ps_, psn2, j = pss[d]
                    for h in range(4):
                        hs = slice(h * 128, (h + 1) * 128)
                        S.op("dve", lambda e: e.scalar_tensor_tensor(Sst[sp][:, d, h, :], Sst[sp][:, d, h, :], etot[k][:, d, h, j:j + 1], ps_[:, hs], ALU.mult, ALU.add),
                             reads=[psn2, "h_etot%d" % k, sstn(sp, d, h)], writes=[sstn(sp, d, h)])

            def finish(rec):
                k = rec["k"]
                for d in range(2):
                    w = W_[d, k]; x = w["sfx"]
                    pi_, pin_ = b.acc(2 + d)
                    tl = rec["ti"][d] * 128
                    S.op("dve", lambda e: e.tensor_tensor(oall[:, d, :, tl:tl + 128], w["tmpo"][:].rearrange("p (h t) -> p h t", h=4),
                                                          pi_[:].rearrange("p (h t) -> p h t", h=4), ALU.add),
                         reads=[pin_, "h_tmpo" + x], writes=["oall"])

            def seq_final(sq_):
                T0 = sq_ * L
                sp = sq_ % 2
                for c in range(L // 256):
                    cs = slice(c * 256, (c + 1) * 256)
                    os_, gT, sq, rs, ob = fin["os"], fin["gT"], fin["sq"], fin["rs"], fin["ob"]
                    S.dma("act", gT[:], UT[OFF["hg"]:OFF["hg"] + 512, T0 + c * 256:T0 + (c + 1) * 256].rearrange("(h p) t -> p h t", p=128), writes=["f_gT"])
                    S.op("act", lambda e: e.activation(gT[:], gT[:], AF.Silu), reads=["f_gT"], writes=["f_gT"])
                    S.op("dve", lambda e: e.tensor_tensor(os_[:], oall[:, 0, :, cs], oall[:, 1, :, cs], ALU.add), reads=["oall"], writes=["f_os"])
                    S.op("act", lambda e: e.activation(sq[:], os_[:], AF.Square), reads=["f_os"], writes=["f_sq"])
                    for hh in range(2):
                        pn_, pnn = b.ps()
                        for h2 in range(2):
                            h = hh * 2 + h2
                            S.op("pe", lambda e: e.matmul(pn_[:, h2 * 256:(h2 + 1) * 256], ones_bf[:], sq[:, h, :], start=True, stop=True),
                                 reads=["f_sq", "ones_bf"], writes=[pnn], pe_acc=(h2 > 0))
                        rstd_from_sumsq(pn_, pnn, rs[:, hh * 2:hh * 2 + 2, :].rearrange("p a t -> p (a t)"), "f_rs", 128, 512, 1.0 / 128)
                    for h in range(4):
                        S.op("dve", lambda e: e.scalar_tensor_tensor(os_[:, h, :], os_[:, h, :], hn[:, h:h + 1], rs[:, h, :], ALU.mult, ALU.mult),
                             reads=["f_os", "hn", "f_rs"], writes=["f_os"])
                    S.op("dve", lambda e: e.tensor_tensor(ob[:], os_[:], gT[:], ALU.mult), reads=["f_os", "f_gT"], writes=["f_ob"])
                    S.dma("sp", OT[0, :, T0 + c * 256:T0 + (c + 1) * 256].rearrange("(h p) t -> p h t", p=128), ob[:], reads=["f_ob"], writes=[])
                if g == "p":
                    S.dma("sp", D["o_hgrn"][l, sq_].rearrange("d h k v -> k d h v"), Sst[sp][:], reads=[sstn(sp, d_, h_) for d_ in range(2) for h_ in range(4)], writes=[])

            recs = []
            for sq_ in range(nseq):
                for i in range(NTL):
                    recs.append(dict(sq=sq_, sp=sq_ % 2, T0=sq_ * L, i=i, ti=(i, NTL - 1 - i), k=len(recs) % 2))
            partA(recs[0])
            for stg in range(4):
                partB(recs[0], stg)
            for n, rec in enumerate(recs):
                nxt = recs[n + 1] if n + 1 < len(recs) else None
                if rec["i"] == 0:
                    sp = rec["sp"]
                    names = [sstn(sp, d_, h_) for d_ in range(2) for h_ in range(4)]
                    if P:
                        S.dma("sp", Sst[sp][:], D["st_hgrn"][l].rearrange("d h k v -> k d h v"), writes=names)
                    else:
                        S.op("pool", lambda e: e.memset(Sst[sp][:], 0.0), writes=names)
                if nxt is not None:
                    partA(nxt)
                for n_ in range(4):
                    chunk_step(rec, n_)
                    if nxt is not None:
                        partB(nxt, n_)
                finish(rec)
                if rec["i"] == NTL - 1:
                    seq_final(rec["sq"])
            b.nrot = 6
            S.barrier()

    def epilogue(st, z, zn, xt, xn, gco, Xdst, c0, wkn):
        sq, rs, tmp = wkn
        for kc in range(8):
            S.op("act", lambda e: e.activation(sq[:, kc, :], z[:, kc, :], AF.Square), reads=[zn], writes=["e_sq"])
        pt, pn = b.ps()
        for kc in range(8):
            S.op("pe", lambda e: e.matmul(pt[:], ones_bf[:], sq[:, kc, :], start=(kc == 0), stop=(kc == 7)), reads=["e_sq", "ones_bf"], writes=[pn], pe_acc=True)
        rstd_from_sumsq(pt, pn, rs, "e_rs", 128, 512, 1.0 / DM)
        for kc in range(8):
            S.op("dve", lambda e: e.tensor_tensor(tmp[:], z[:, kc, :], rs[:], ALU.mult), reads=[zn, "e_rs"], writes=["e_tmp"])
            S.op("dve", lambda e: e.scalar_tensor_tensor(xt[:, kc, :], tmp[:], gco[:, kc:kc + 1], xt[:, kc, :], ALU.mult, ALU.add),
                 reads=["e_tmp", "coef", xn], writes=[xn])
        S.dma("sp", Xdst[:, c0:c0 + 512].rearrange("(kc p) t -> p kc t", p=128), xt[:], reads=[xn], writes=[])

    def stage_merge(g, l, Xsrc, Xdst):
        G = GR[g]
        T = G["T"]
        UT, OT = D["UT_" + g], D["OT_" + g]
        with ExitStack() as st:
            Wb = b.sb(st, "Wb", [128, 4, 4, 1024], BF16)
            Wo = b.sb(st, "Wo", [128, 8, 1024], BF16)
            S.dma("pool", Wb[:], D["w_branch"][l].rearrange("n (kc p) c -> p n kc c", p=128), writes=["Wb"])
            S.dma("pool", Wo[:], D["w_out"][l].rearrange("(kc p) c -> p kc c", p=128), writes=["Wo"])
            ot = b.sb(st, "ot", [128, 4, 4, 512], BF16)
            yp = b.sb(st, "yp", [128, 8, 512], BF16)
            gts = [b.sb(st, "gt%d" % i, [128, 512]) for i in range(3)]
            accf = b.sb(st, "accf", [128, 512]); tm2 = b.sb(st, "tm2", [128, 512])
            z = b.sb(st, "z", [128, 8, 512]); xt = b.sb(st, "xt", [128, 8, 512])
            wkn = (b.sb(st, "e_sq", [128, 8, 512], BF16), b.sb(st, "e_rs", [128, 512]), b.sb(st, "e_tmp", [128, 512]))
            ng = 0
            for tt in range(T // 512):
                cs = slice(tt * 512, (tt + 1) * 512)
                S.dma("sp", ot[:], OT[:, :, cs].rearrange("n (kc p) t -> p n kc t", p=128), writes=["ot"])
                S.dma("act", xt[:], Xsrc[:, cs].rearrange("(kc p) t -> p kc t", p=128), writes=["xt"])
                for dmc in range(8):
                    for n in range(4):
                        gt, gtn = gts[ng % 3], "gt%d" % (ng % 3)
                        ng += 1
                        r0 = OFF["gates"] + n * 1024 + dmc * 128
                        S.dma("sp", gt[:], UT[r0:r0 + 128, cs], writes=[gtn])
                        S.op("act", lambda e: e.activation(gt[:], gt[:], AF.Sigmoid), reads=[gtn], writes=[gtn])
                        pt, pn = b.ps()
                        for kc in range(4):
                            S.op("pe", lambda e: e.matmul(pt[:], Wb[:, n, kc, dmc * 128:(dmc + 1) * 128], ot[:, n, kc, :], start=(kc == 0), stop=(kc == 3)),
                                 reads=["Wb", "ot"], writes=[pn], pe_acc=True)
                        if n == 0:
                            S.op("dve", lambda e: e.tensor_tensor(accf[:], pt[:], gt[:], ALU.mult), reads=[pn, gtn], writes=["accf"])
                        elif n < 3:
                            S.op("dve", lambda e: e.tensor_tensor(tm2[:], pt[:], gt[:], ALU.mult), reads=[pn, gtn], writes=["tm2"])
                            S.op("dve", lambda e: e.tensor_tensor(accf[:], accf[:], tm2[:], ALU.add), reads=["tm2", "accf"], writes=["accf"])
                        else:
                            S.op("dve", lambda e: e.tensor_tensor(tm2[:], pt[:], gt[:], ALU.mult), reads=[pn, gtn], writes=["tm2"])
                            S.op("dve", lambda e: e.tensor_tensor(yp[:, dmc, :], accf[:], tm2[:], ALU.add), reads=["tm2", "accf"], writes=["yp"])
                for oc in range(8):
                    pt, pn = b.ps()
                    for kc in range(8):
                        S.op("pe", lambda e: e.matmul(pt[:], Wo[:, kc, oc * 128:(oc + 1) * 128], yp[:, kc, :], start=(kc == 0), stop=(kc == 7)),
                             reads=["Wo", "yp"], writes=[pn], pe_acc=True)
                    S.op("act", lambda e: e.copy(z[:, oc, :], pt[:]), reads=[pn], writes=["z"])
                epilogue(st, z, "z", xt, "xt", coef[:, l, G["cond"], 2], Xdst, tt * 512, wkn)
            S.barrier()

    def stage_mlp(g, l, Xsrc, Xdst):
        G = GR[g]
        T = G["T"]
        with ExitStack() as st:
            h2 = stage_h(st, g, l, Xsrc, 3, 4)
            W2 = [b.sb(st, "W2_%d" % i, [128, 8, 1024], BF16) for i in range(2)]
            n2 = 0
            w1 = [b.sb(st, "w1_%d" % i, [128, 8, 512], BF16) for i in range(2)]
            hid = b.sb(st, "hid", [128, 32, 512], BF16)
            rl = [b.sb(st, "rl%d" % i, [128, 512]) for i in range(2)]
            z = b.sb(st, "z", [128, 8, 512]); xt = b.sb(st, "xt", [128, 8, 512])
            wkn = (b.sb(st, "e_sq", [128, 8, 512], BF16), b.sb(st, "e_rs", [128, 512]), b.sb(st, "e_tmp", [128, 512]))
            nw = 0
            nr = 0
            for tt in range(T // 512):
                cs = slice(tt * 512, (tt + 1) * 512)
                S.dma("act", xt[:], Xsrc[:, cs].rearrange("(kc p) t -> p kc t", p=128), writes=["xt"])
                for cg in range(8):
                    w, wn = w1[nw % 2], "w1_%d" % (nw % 2)
                    nw += 1
                    S.dma("pool", w[:], D["w_mlp_in"][l, :, cg * 512:(cg + 1) * 512].rearrange("(kc p) c -> p kc c", p=128), writes=[wn])
                    for cc in range(4):
                        pt, pn = b.ps()
                        for kc in range(8):
                            S.op("pe", lambda e: e.matmul(pt[:], w[:, kc, cc * 128:(cc + 1) * 128], h2[:, kc, cs], start=(kc == 0), stop=(kc == 7)),
                                 reads=[wn, "hT"], writes=[pn], pe_acc=True)
                        r, rn = rl[nr % 2], "rl%d" % (nr % 2)
                        nr += 1
                        S.op("act", lambda e: e.activation(r[:], pt[:], AF.Relu), reads=[pn], writes=[rn])
                        S.op("dve", lambda e: e.tensor_tensor(hid[:, cg * 4 + cc, :], r[:], r[:], ALU.mult), reads=[rn], writes=["hid"])
                for q4 in range(4):
                    w2, w2n = W2[n2 % 2], "W2_%d" % (n2 % 2)
                    n2 += 1
                    S.dma("pool", w2[:], D["w_mlp_out"][l, q4 * 1024:(q4 + 1) * 1024, :].rearrange("(kc p) c -> p kc c", p=128), writes=[w2n])
                    for oc in range(8):
                        pt, pn = b.ps()
                        for fc in range(8):
                            S.op("pe", lambda e: e.matmul(pt[:], w2[:, fc, oc * 128:(oc + 1) * 128], hid[:, q4 * 8 + fc, :], start=(fc == 0), stop=(fc == 7)),
                                 reads=[w2n, "hid"], writes=[pn], pe_acc=True)
                        if q4 == 0:
                            S.op("act", lambda e: e.copy(z[:, oc, :], pt[:]), reads=[pn], writes=["z%d" % oc, "z"])
                        else:
                            S.op("dve", lambda e: e.tensor_tensor(z[:, oc, :], z[:, oc, :], pt[:], ALU.add), reads=[pn, "z%d" % oc], writes=["z%d" % oc, "z"])
                epilogue(st, z, "z", xt, "xt", coef[:, l, G["cond"], 5], Xdst, tt * 512, wkn)
            S.barrier()

    for g in ("s", "p"):
        X = D["xT_" + g]
        for l in range(DEPTH):
            if "proj" in STAGES:
                with ExitStack() as st:
                    hT = stage_h(st, g, l, X, 0, 1)
                    stage_proj(g, l, hT)
            if "hgrn" in STAGES:
                stage_hgrn(g, l)
            if "swa" in STAGES:
                stage_gqa_like(g, l, "swa")
            if "mla" in STAGES:
                stage_mla(g, l)
            if "gqa" in STAGES:
                stage_gqa_like(g, l, "gqa")
            if "merge" in STAGES:
                stage_merge(g, l, X, D["X1_" + g])
            Xn = D["yT_" + g] if l == DEPTH - 1 else D["X2_%d_%s" % (l, g)]
            if "mlp" in STAGES:
                stage_mlp(g, l, D["X1_" + g], Xn)
            X = Xn
    S.barrier()
    return nc, b


_CACHE = {}


def _consts():
    nf = 16
    t = np.arange(2048)
    row, col = (t // 64).astype(np.float32), (t % 64).astype(np.float32)
    rope = np.zeros((4, 128, 2048), np.float32)
    def fill(ci, si, r0, nf):
        inv = (10000.0 ** (-np.arange(nf, dtype=np.float32) / nf)).astype(np.float32)
        ar = (row[None, :] * inv[:, None]).astype(np.float32)
        ac = (col[None, :] * inv[:, None]).astype(np.float32)
        for k, a in enumerate((ar, ac)):
            b0 = r0 + k * 2 * nf
            rope[ci, b0:b0 + nf] = np.cos(a); rope[ci, b0 + nf:b0 + 2 * nf] = np.cos(a)
            rope[si, b0:b0 + nf] = -np.sin(a); rope[si, b0 + nf:b0 + 2 * nf] = np.sin(a)
    fill(0, 1, 0, 16)
    fill(0, 1, 64, 16)
    fill(2, 3, 64, 8)
    s_ = np.arange(128)[:, None]; t_ = np.arange(128)[None, :]
    same = (s_ // 32) == (t_ // 32)
    cm = np.zeros((6, 128, 128), np.float32)
    cm[0] = np.eye(128)
    cm[1] = same & (s_ <= t_)
    cm[2] = same & (s_ >= t_)
    cm[3] = same & (s_ > t_)
    cm[4] = same & (s_ < t_)
    cm[5][:, 0:4] = (s_ // 32) == np.arange(4)[None, :]
    j = np.arange(128)[:, None]; i = (np.arange(512) % 128)[None, :]
    swm = np.stack([(j >= i), (j <= i)]).astype(np.float32).astype(ml_dtypes.bfloat16)
    return rope, cm, swm


def _perm(nd):
    q = nd // 4
    return np.concatenate([np.arange(q, 2 * q), np.arange(0, q), np.arange(3 * q, 4 * q), np.arange(2 * q, 3 * q)])


def kernel(**inp):
    f = lambda a: np.ascontiguousarray(np.asarray(a, dtype=np.float32))
    I = {k: f(v) for k, v in inp.items()}
    if "prog" not in _CACHE:
        _CACHE["prog"] = build_program()
    nc, b = _CACHE["prog"]
    rope, cm, swm = _consts()
    fm = lambda v, n: f(v.reshape(v.shape[0], n, 128).transpose(0, 2, 1))
    shared = dict(
        w_ada=I["w_ada"], b_adaT=fm(I["b_ada"], 48),
        gains=f(np.stack([fm(I[k], 8) for k in ("norm_mix_pre", "norm_mix_post", "norm_mlp_pre", "norm_mlp_post")], axis=2)),
        w_in=I["w_in"], lbT=f(np.stack([I["hgrn_lb_fwd"], I["hgrn_lb_bwd"]], axis=1)),
        hgrn_normT=fm(I["hgrn_norm"], 4), sink=f(I["swa_sink"][:, None, :]),
        mla_qnT=fm(I["mla_q_norm"], 2), mla_kvn=f(I["mla_kv_norm"][:, :, None]),
        w_uq=I["mla_w_uq"], w_ukv=I["mla_w_ukv"],
        gqa_qn=f(np.tile(np.stack([I["gqa_q_norm"], I["gqa_q_norm"][:, _perm(64)]], axis=2), (1, 2, 1))),
        gqa_kn=f(np.tile(np.stack([I["gqa_k_norm"], I["gqa_k_norm"][:, _perm(64)]], axis=2), (1, 2, 1))),
        w_branch=I["w_branch"], w_out=I["w_out"], w_mlp_in=I["w_mlp_in"], w_mlp_out=I["w_mlp_out"],
        rope64=rope, cmat=cm, swamask=swm,
    )
    wsw = I["mla_w_uq"].reshape(DEPTH, 256, 8, 96).copy()
    wsw[..., 64:96] = wsw[..., 64:96][..., _perm(32)]
    shared["w_uq_sw"] = f(wsw.reshape(DEPTH, 256, 768))
    in_maps = []
    for i in range(NCORE):
        bb = i % 2
        m = dict(shared)
        m["xT_s"] = f(I["x_sample"][bb].T)
        m["xT_p"] = f(I["x_prompt"][4 * i:4 * i + 4].reshape(1024, DM).T)
        cc = np.stack([I["c_ctx"], I["c"][bb]], axis=1)
        m["cT"] = f(cc.reshape(8, 128, 2).transpose(1, 0, 2))
        m["st_hgrn"] = f(I["state_hgrn"][bb])
        m["c_swa_kT"] = f(I["cache_swa_k"][bb].transpose(0, 2, 3, 1))
        m["c_swa_v"] = f(I["cache_swa_v"][bb].reshape(DEPTH, 512, 128))
        m["c_ckvT"] = f(I["cache_mla_ckv"][bb].transpose(0, 2, 1))
        m["c_krT"] = f(I["cache_mla_kr"][bb].transpose(0, 2, 1))
        m["c_gqa_kT"] = f(I["cache_gqa_k"][bb].transpose(0, 2, 3, 1))
        m["c_gqa_v"] = f(I["cache_gqa_v"][bb].reshape(DEPTH, 512, 128))
        in_maps.append(m)
    res = run_bass_kernel_spmd(nc, in_maps, core_ids=list(range(NCORE)))
    R = res.results
    y_prompt = np.concatenate([R[i]["yT_p"].T.reshape(4, 256, DM) for i in range(NCORE)], axis=0)
    y_sample = np.stack([R[0]["yT_s"].T, R[1]["yT_s"].T], axis=0)
    cat = lambda fn: np.ascontiguousarray(np.concatenate([fn(R[i]) for i in range(NCORE)], axis=0).astype(np.float32))
    n_hgrn = cat(lambda r: r["o_hgrn"].transpose(1, 0, 2, 3, 4, 5))
    kT = lambda a: a.reshape(DEPTH, 2, 64, 4, 256).transpose(3, 0, 4, 1, 2)
    vv = lambda a: a.reshape(DEPTH, 4, 256, 2, 64).transpose(1, 0, 2, 3, 4)
    n_swa_k = cat(lambda r: kT(r["o_swa_kT"]))
    n_swa_v = cat(lambda r: vv(r["o_swa_v"]))
    n_ckv = cat(lambda r: r["o_ckvT"].reshape(DEPTH, 128, 4, 256).transpose(2, 0, 3, 1))
    n_kr = cat(lambda r: r["o_krT"].reshape(DEPTH, 32, 4, 256).transpose(2, 0, 3, 1))
    n_gqa_k = cat(lambda r: kT(r["o_gqa_kT"]))
    n_gqa_v = cat(lambda r: vv(r["o_gqa_v"]))
    return (np.ascontiguousarray(y_prompt.astype(np.float32)), np.ascontiguousarray(y_sample.astype(np.float32)),
            n_hgrn, n_swa_k, n_swa_v, n_ckv, n_kr, n_gqa_k, n_gqa_v)
```

```python
import numpy as np
from contextlib import ExitStack
import ml_dtypes
import concourse.bass as bass
import concourse.mybir as mybir
from concourse.bass_utils import run_bass_kernel_spmd

F32 = mybir.dt.float32
BF16 = mybir.dt.bfloat16
AF = mybir.ActivationFunctionType
ALU = mybir.AluOpType

DM = 1024
DEPTH = 2
NCORE = 8
EPS = 1e-6
OFF = dict(hq=0, ff=512, fb=1024, hi=1536, hg=2048, sq=2560, sk=3072, sv=3200, cq=3328, ckv=3584, kr=3712,
           gq=3744, gk=4256, gv=4384, gates=4512)
D_IN = 8608
UTM_COLS = 1792
STAGES = {"proj", "hgrn", "swa", "mla", "gqa", "merge", "mlp"}


class Sched:
    NDMA = 24

    def __init__(self, nc, es):
        self.nc = nc
        self.eng = {"pe": nc.tensor, "act": nc.scalar, "dve": nc.vector, "pool": nc.gpsimd, "sp": nc.sync}
        self.sem = {k: es.enter_context(nc.semaphore("s_" + k)) for k in ("pe", "act", "dve", "pool")}
        self.cnt = {k: 0 for k in self.sem}
        self.dsem = [es.enter_context(nc.semaphore("d%d" % i)) for i in range(self.NDMA)]
        self.dcnt = [0] * self.NDMA
        self.dnext = 0
        self.seen = {k: {} for k in self.eng}
        self.lastw = {}
        self.reads = {}
        self.n_instr = 0

    def _sem_of(self, key):
        return self.sem[key] if isinstance(key, str) else self.dsem[key[1]]

    def _wait(self, e, tok):
        key, val = tok
        if self.seen[e].get(key, 0) >= val:
            return
        self.eng[e].wait_ge(self._sem_of(key), val)
        self.seen[e][key] = val

    def _deps(self, e, reads, writes, pe_acc=False):
        best = {}
        for r in reads:
            t = self.lastw.get(r)
            if t is not None and best.get(t[0], 0) < t[1]:
                best[t[0]] = t[1]
        for w in writes:
            t = self.lastw.get(w)
            if t is not None and best.get(t[0], 0) < t[1]:
                if not (pe_acc and t[0] == "pe"):
                    best[t[0]] = t[1]
            for t in self.reads.get(w, ()):
                if best.get(t[0], 0) < t[1]:
                    best[t[0]] = t[1]
        for key, val in best.items():
            self._wait(e, (key, val))

    def _record(self, tok, reads, writes):
        for r in reads:
            lst = self.reads.setdefault(r, [])
            lst[:] = [t for t in lst if t[0] != tok[0]]
            lst.append(tok)
        for w in writes:
            self.lastw[w] = tok
            self.reads[w] = []

    def op(self, e, fn, reads=(), writes=(), pe_acc=False):
        self._deps(e, reads, writes, pe_acc)
        ins = fn(self.eng[e])
        self.cnt[e] += 1
        ins.then_inc(self.sem[e], 1)
        self._record((e, self.cnt[e]), reads, writes)
        self.n_instr += 1
        return ins

    def dma(self, q, out, in_, reads=(), writes=()):
        i = self.dnext
        self.dnext = (self.dnext + 1) % self.NDMA
        if self.dcnt[i] > 0:
            self._wait(q, (("d", i), self.dcnt[i]))
        self._deps(q, reads, writes)
        ins = self.eng[q].dma_start(out=out, in_=in_)
        self.dcnt[i] += 16
        ins.then_inc(self.dsem[i], 16)
        self._record((("d", i), self.dcnt[i]), reads, writes)
        self.n_instr += 1
        return ins

    def barrier(self):
        best = {}
        for k in self.cnt:
            if self.cnt[k]:
                best[k] = self.cnt[k]
        for i in range(self.NDMA):
            if self.dcnt[i]:
                best[("d", i)] = self.dcnt[i]
        for e in self.eng:
            for key, val in best.items():
                self._wait(e, (key, val))
        self.lastw = {}
        self.reads = {}


class B:
    def __init__(self, nc, es):
        self.nc, self.es = nc, es
        self.S = Sched(nc, es)
        self.D = {}
        self.psn = 0

    def din(self, name, shape, dt=F32):
        self.D[name] = self.nc.dram_tensor(name, list(shape), dt, kind="ExternalInput").ap()
        return self.D[name]

    def dout(self, name, shape, dt=F32):
        self.D[name] = self.nc.dram_tensor(name, list(shape), dt, kind="ExternalOutput").ap()
        return self.D[name]

    def dscr(self, name, shape, dt=F32):
        self.D[name] = self.nc.dram_tensor(name, list(shape), dt, kind="Internal").ap()
        return self.D[name]

    def sb(self, st, name, shape, dt=F32):
        self.uid = getattr(self, "uid", 0) + 1
        return st.enter_context(self.nc.sbuf_tensor("sb%d_%s" % (self.uid, name), list(shape), dt))

    def ps(self):
        i = self.psn % self.nrot
        self.psn += 1
        return self.psum[i], "ps%d" % i

    nrot = 6

    def acc(self, i):
        k = (6, 7, 4, 5)[i]
        return self.psum[k], "ps%d" % k


def build_program():
    nc = bass.Bass("TRN2", target_bir_lowering=False)
    es = ExitStack()
    b = B(nc, es)
    S = b.S
    D = b.D
    GR = {
        "s": dict(T=2048, nseq=1, L=2048, P=512, rope=True, cond=1),
        "p": dict(T=1024, nseq=4, L=256, P=0, rope=False, cond=0),
    }
    for g, G in GR.items():
        T = G["T"]
        b.din("xT_" + g, [DM, T])
        b.dout("yT_" + g, [DM, T])
        b.dscr("X1_" + g, [DM, T])
        for l in range(DEPTH - 1):
            b.dscr("X2_%d_%s" % (l, g), [DM, T])
        b.dscr("UT_" + g, [68 * 128, T])
        b.dscr("UTM_" + g, [T, UTM_COLS])
        b.dscr("OT_" + g, [4, 512, T], BF16)
        b.dscr("YP_" + g, [DM, T], BF16)
    b.din("cT", [128, 8, 2])
    b.din("w_ada", [DEPTH, DM, 6 * DM])
    b.din("b_adaT", [DEPTH, 128, 48])
    b.din("gains", [DEPTH, 128, 4, 8])
    b.din("w_in", [DEPTH, DM, D_IN])
    b.din("lbT", [DEPTH, 2, 512])
    b.din("hgrn_normT", [DEPTH, 128, 4])
    b.din("sink", [DEPTH, 1, 8])
    b.din("mla_qnT", [DEPTH, 128, 2])
    b.din("mla_kvn", [DEPTH, 128, 1])
    b.din("w_uq", [DEPTH, 256, 768])
    b.din("w_uq_sw", [DEPTH, 256, 768])
    b.din("w_ukv", [DEPTH, 128, 1024])
    b.din("gqa_qn", [DEPTH, 128, 2])
    b.din("gqa_kn", [DEPTH, 128, 2])
    b.din("w_branch", [DEPTH, 4, 512, DM])
    b.din("w_out", [DEPTH, DM, DM])
    b.din("w_mlp_in", [DEPTH, DM, 4 * DM])
    b.din("w_mlp_out", [DEPTH, 4 * DM, DM])
    b.din("st_hgrn", [DEPTH, 2, 4, 128, 128])
    b.din("c_swa_kT", [DEPTH, 2, 64, 512])
    b.din("c_swa_v", [DEPTH, 512, 128])
    b.din("c_ckvT", [DEPTH, 128, 512])
    b.din("c_krT", [DEPTH, 32, 512])
    b.din("c_gqa_kT", [DEPTH, 2, 64, 512])
    b.din("c_gqa_v", [DEPTH, 512, 128])
    b.din("rope64", [4, 128, 2048])
    b.din("cmat", [6, 128, 128])
    b.din("swamask", [2, 128, 512], BF16)
    b.dout("o_hgrn", [DEPTH, 4, 2, 4, 128, 128])
    b.dout("o_swa_kT", [DEPTH, 128, 1024])
    b.dout("o_swa_v", [DEPTH, 1024, 128])
    b.dout("o_ckvT", [DEPTH, 128, 1024])
    b.dout("o_krT", [DEPTH, 32, 1024])
    b.dout("o_gqa_kT", [DEPTH, 128, 1024])
    b.dout("o_gqa_v", [DEPTH, 1024, 128])

    b.psum = [es.enter_context(nc.psum_tensor("psb%d" % i, [128, 512], F32)) for i in range(8)]

    cst = ExitStack()
    es.enter_context(cst)
    ones_bf = b.sb(cst, "ones_bf", [128, 128], BF16)
    ones_f = b.sb(cst, "ones_f", [128, 128], F32)
    cm = b.sb(cst, "cm", [128, 6, 128], F32)
    modT = b.sb(cst, "modT", [128, DEPTH, 48, 2], F32)
    gains = b.sb(cst, "gains", [128, DEPTH, 4, 8], F32)
    coef = b.sb(cst, "coef", [128, DEPTH, 2, 6, 8], F32)
    S.op("pool", lambda e: e.memset(ones_bf[:], 1.0), writes=["ones_bf"])
    onesAB = b.sb(cst, "onesAB", [128, 2, 128], BF16)
    S.op("pool", lambda e: e.memset(onesAB[:], 0.0), writes=["onesAB"])
    S.op("pool", lambda e: e.memset(onesAB[:, 0, 0:64], 1.0), writes=["onesAB"])
    S.op("pool", lambda e: e.memset(onesAB[:, 1, 64:128], 1.0), writes=["onesAB"])
    S.op("pool", lambda e: e.memset(ones_f[:], 1.0), writes=["ones_f"])
    S.dma("sp", cm[:], D["cmat"].rearrange("a p c -> p a c"), writes=["cm"])
    S.dma("sp", gains[:], D["gains"].rearrange("l p a c -> p l a c"), writes=["gains"])

    with ExitStack() as st:
        cT = b.sb(st, "cT", [128, 8, 2])
        scT = b.sb(st, "scT", [128, 8, 2])
        badaT = b.sb(st, "badaT", [128, DEPTH, 48])
        wa = [b.sb(st, "wa%d" % i, [128, 8, 768]) for i in range(2)]
        S.dma("sp", cT[:], D["cT"], writes=["cT"])
        S.dma("sp", badaT[:], D["b_adaT"].rearrange("l p j -> p l j"), writes=["badaT"])
        S.op("act", lambda e: e.activation(scT[:], cT[:], AF.Silu), reads=["cT"], writes=["scT"])
        n = 0
        for l in range(DEPTH):
            pt, pn = b.ps()
            for cg in range(8):
                w = wa[n % 2]
                wn = "wa%d" % (n % 2)
                n += 1
                S.dma("sp" if cg % 2 == 0 else "act", w[:],
                      D["w_ada"][l, :, cg * 768:(cg + 1) * 768].rearrange("(kc p) c -> p kc c", p=128), writes=[wn])
                for jj in range(6):
                    j = cg * 6 + jj
                    for kc in range(8):
                        S.op("pe", lambda e: e.matmul(pt[:, 2 * j:2 * j + 2], w[:, kc, jj * 128:(jj + 1) * 128], scT[:, kc, :],
                                                      start=(kc == 0), stop=(kc == 7)),
                             reads=[wn, "scT"], writes=[pn], pe_acc=True)
            S.op("dve", lambda e: e.tensor_tensor(modT[:, l], pt[:, 0:96].rearrange("p (j c) -> p j c", c=2),
                                                  badaT[:, l].unsqueeze(2).to_broadcast([128, 48, 2]), ALU.add),
                 reads=[pn, "badaT"], writes=["modT"])
        for l in range(DEPTH):
            for c in range(2):
                m = lambda i: modT[:, l, i * 8:(i + 1) * 8, c]
                S.op("dve", lambda e: e.scalar_tensor_tensor(coef[:, l, c, 0], m(1), 1.0, gains[:, l, 0], ALU.add, ALU.mult),
                     reads=["modT", "gains"], writes=["coef"])
                S.op("dve", lambda e: e.tensor_copy(coef[:, l, c, 1], m(0)), reads=["modT"], writes=["coef"])
                S.op("dve", lambda e: e.tensor_tensor(coef[:, l, c, 2], m(2), gains[:, l, 1], ALU.mult), reads=["modT", "gains"], writes=["coef"])
                S.op("dve", lambda e: e.scalar_tensor_tensor(coef[:, l, c, 3], m(4), 1.0, gains[:, l, 2], ALU.add, ALU.mult),
                     reads=["modT", "gains"], writes=["coef"])
                S.op("dve", lambda e: e.tensor_copy(coef[:, l, c, 4], m(3)), reads=["modT"], writes=["coef"])
                S.op("dve", lambda e: e.tensor_tensor(coef[:, l, c, 5], m(5), gains[:, l, 3], ALU.mult), reads=["modT", "gains"], writes=["coef"])
        S.barrier()

    def rstd_from_sumsq(pt, pn, out, outn, npart, ncol, inv_n):
        S.op("act", lambda e: e.activation(out[0:npart, 0:ncol], pt[0:npart, 0:ncol], AF.Ln, bias=epsb[0:npart, :], scale=inv_n),
             reads=[pn, "epsb"], writes=[outn])
        S.op("act", lambda e: e.activation(out[0:npart, 0:ncol], out[0:npart, 0:ncol], AF.Exp, scale=-0.5), reads=[outn], writes=[outn])

    epsb = b.sb(cst, "epsb", [128, 1], F32)
    S.op("pool", lambda e: e.memset(epsb[:], EPS), writes=["epsb"])

    def norm_mod_tile(st_unused, xt, xn, hT, hn, t0, a_ap, sh_ap, tmp, sq, rs):
        for kc in range(8):
            S.op("act", lambda e: e.activation(sq[:, kc, :], xt[:, kc, :], AF.Square), reads=[xn], writes=["sq"])
        pt, pn = b.ps()
        for kc in range(8):
            S.op("pe", lambda e: e.matmul(pt[:], ones_bf[:], sq[:, kc, :], start=(kc == 0), stop=(kc == 7)),
                 reads=["sq", "ones_bf"], writes=[pn], pe_acc=True)
        rstd_from_sumsq(pt, pn, rs, "rs", 128, 512, 1.0 / DM)
        for kc in range(8):
            S.op("dve", lambda e: e.tensor_tensor(tmp[:], xt[:, kc, :], rs[:], ALU.mult), reads=[xn, "rs"], writes=["tmp"])
            S.op("act", lambda e: e.activation(hT[:, kc, t0:t0 + 512], tmp[:], AF.Identity, bias=sh_ap[:, kc:kc + 1], scale=a_ap[:, kc:kc + 1]),
                 reads=["tmp", "coef"], writes=[hn])

    def stage_h(st, g, l, Xsrc, ia, ish):
        G = GR[g]
        T = G["T"]
        hT = b.sb(st, "hT", [128, 8, T], BF16)
        with ExitStack() as s2:
            xts = [b.sb(s2, "xt%d" % i, [128, 8, 512]) for i in range(2)]
            tmp = b.sb(s2, "tmp", [128, 512])
            sq = b.sb(s2, "sq", [128, 8, 512], BF16)
            rs = b.sb(s2, "rs", [128, 512])
            for tt in range(T // 512):
                xt, xn = xts[tt % 2], "xt%d" % (tt % 2)
                S.dma("sp", xt[:], Xsrc[:, tt * 512:(tt + 1) * 512].rearrange("(kc p) t -> p kc t", p=128), writes=[xn])
                norm_mod_tile(None, xt, xn, hT, "hT", tt * 512, coef[:, l, G["cond"], ia], coef[:, l, G["cond"], ish], tmp, sq, rs)
            S.barrier()
        return hT

    def stage_proj(g, l, hT):
        G = GR[g]
        T = G["T"]
        UT, UTM = D["UT_" + g], D["UTM_" + g]
        tm_groups = {1: [(0, 512, 0)], 2: [(0, 512, 512)], 3: [(0, 512, 1024)], 6: [(128, 128, 1536)], 8: [(288, 128, 1664)]}
        with ExitStack() as st:
            wb = [b.sb(st, "wb%d" % i, [128, 8, 512], BF16) for i in range(2)]
            ev = [b.sb(st, "ev%d" % i, [128, 512]) for i in range(4)]
            nev = 0
            for cg in range(17):
                ncol = 512 if cg < 16 else D_IN - 8192
                w, wn = wb[cg % 2], "wb%d" % (cg % 2)
                S.dma("pool", w[:, :, 0:ncol], D["w_in"][l, :, cg * 512:cg * 512 + ncol].rearrange("(kc p) c -> p kc c", p=128), writes=[wn])
                for tt in range(T // 512):
                    for cc in range((ncol + 127) // 128):
                        m = min(128, ncol - cc * 128)
                        pt, pn = b.ps()
                        for kc in range(8):
                            S.op("pe", lambda e: e.matmul(pt[0:m, :], w[:, kc, cc * 128:cc * 128 + m], hT[:, kc, tt * 512:(tt + 1) * 512],
                                                          start=(kc == 0), stop=(kc == 7)), reads=[wn, "hT"], writes=[pn], pe_acc=True)
                        e_, en = ev[nev % 4], "ev%d" % (nev % 4)
                        eng = "act" if nev % 2 == 0 else "dve"
                        nev += 1
                        if eng == "act":
                            S.op("act", lambda e: e.copy(e_[0:m, :], pt[0:m, :]), reads=[pn], writes=[en])
                        else:
                            S.op("dve", lambda e: e.tensor_copy(e_[0:m, :], pt[0:m, :]), reads=[pn], writes=[en])
                        r0 = cg * 512 + cc * 128
                        S.dma("sp", UT[r0:r0 + m, tt * 512:(tt + 1) * 512], e_[0:m, :], reads=[en], writes=[])
                for (c0, cn, dst) in tm_groups.get(cg, []):
                    for t4 in range(T // 128):
                        pt, pn = b.ps()
                        for kc in range(8):
                            S.op("pe", lambda e: e.matmul(pt[:, 0:cn], hT[:, kc, t4 * 128:(t4 + 1) * 128], w[:, kc, c0:c0 + cn],
                                                          start=(kc == 0), stop=(kc == 7)), reads=[wn, "hT"], writes=[pn], pe_acc=True)
                        e_, en = ev[nev % 4], "ev%d" % (nev % 4)
                        eng = "act" if nev % 2 == 0 else "dve"
                        nev += 1
                        if eng == "act":
                            S.op("act", lambda e: e.copy(e_[:, 0:cn], pt[:, 0:cn]), reads=[pn], writes=[en])
                        else:
                            S.op("dve", lambda e: e.tensor_copy(e_[:, 0:cn], pt[:, 0:cn]), reads=[pn], writes=[en])
                        S.dma("sp", UTM[t4 * 128:(t4 + 1) * 128, dst:dst + cn], e_[:, 0:cn], reads=[en], writes=[])
            S.barrier()

    def attn_unit(qT, qn, Kd, Nq, chunks, scale, o_out, on, wk, sink=None):
        po, pon = b.acc(0)
        pd, pdn = b.acc(1)
        nch = len(chunks)

        def emit_st(i):
            kT, kn, V, vn, nk, mask = chunks[i]
            pst, psn_ = b.ps()
            S.op("pe", lambda e: e.matmul(pst[0:nk, 0:Nq], kT, qT, start=True, stop=True), reads=[kn] + (qn if isinstance(qn, list) else [qn]), writes=[psn_])
            return pst, psn_

        cur = emit_st(0)
        for i, (kT, kn, V, vn, nk, mask) in enumerate(chunks):
            nxt = emit_st(i + 1) if i + 1 < nch else None
            pst, psn_ = cur
            E, En = wk["E"][i % 3], "E%d" % (i % 3)
            S.op("act", lambda e: e.activation(E[0:nk, 0:Nq], pst[0:nk, 0:Nq], AF.Exp, scale=scale), reads=[psn_], writes=[En])
            if mask is not None:
                S.op("pool", lambda e: e.tensor_tensor(E[0:nk, 0:Nq], E[0:nk, 0:Nq], mask, ALU.mult), reads=[En, "swamask"], writes=[En])
            last = (i == nch - 1) and sink is None
            S.op("pe", lambda e: e.matmul(po[0:64, 0:Nq], V, E[0:nk, 0:Nq], start=(i == 0), stop=(i == nch - 1)),
                 reads=[vn, En], writes=[pon], pe_acc=(i > 0))
            S.op("pe", lambda e: e.matmul(pd[0:64, 0:Nq], ones_bf[0:nk, 0:64], E[0:nk, 0:Nq], start=(i == 0), stop=last),
                 reads=["ones_bf", En], writes=[pdn], pe_acc=(i > 0))
            cur = nxt
        if sink is not None:
            S.op("pe", lambda e: e.matmul(pd[0:64, 0:Nq], ones_bf[0:1, 0:64], sink, start=False, stop=True),
                 reads=["ones_bf", "sinkrow"], writes=[pdn], pe_acc=True)
        rec = wk["rec"]
        S.op("dve", lambda e: e.reciprocal(rec[0:64, 0:Nq], pd[0:64, 0:Nq]), reads=[pdn], writes=["rec"])
        S.op("dve", lambda e: e.tensor_tensor(o_out, po[0:64, 0:Nq], rec[0:64, 0:Nq], ALU.mult), reads=[pon, "rec"], writes=[on])

    def attn_pair(qa, qan, qb, qbn, Nq, chunks, scale, o_out, on, wk, sink=None):
        po, pon = b.acc(0)
        pd, pdn = b.acc(1)
        nch = len(chunks)
        E = wk["E"]
        ne = len(E)

        def emit_st(i):
            kTa, kTb, kn, Va, Vb, vn, nk, mask = chunks[i]
            p1, n1 = b.ps()
            kna, knb = kn if isinstance(kn, tuple) else (kn, kn)
            S.op("pe", lambda e: e.matmul(p1[0:nk, 0:Nq], kTa, qa, start=True, stop=True), reads=[kna] + qan, writes=[n1])
            p2, n2 = b.ps()
            S.op("pe", lambda e: e.matmul(p2[0:nk, 0:Nq], kTb, qb, start=True, stop=True), reads=[knb] + qbn, writes=[n2])
            return (p1, n1, p2, n2)

        cur = emit_st(0)
        for i, (kTa, kTb, kn, Va, Vb, vn, nk, mask) in enumerate(chunks):
            nxt = emit_st(i + 1) if i + 1 < nch else None
            p1, n1, p2, n2 = cur
            Ea, Ean = E[(2 * i) % ne], "E%d" % ((2 * i) % ne)
            Eb, Ebn = E[(2 * i + 1) % ne], "E%d" % ((2 * i + 1) % ne)
            S.op("act", lambda e: e.activation(Ea[0:nk, 0:Nq], p1[0:nk, 0:Nq], AF.Exp, scale=scale), reads=[n1], writes=[Ean])
            S.op("act", lambda e: e.activation(Eb[0:nk, 0:Nq], p2[0:nk, 0:Nq], AF.Exp, scale=scale), reads=[n2], writes=[Ebn])
            if mask is not None:
                S.op("pool", lambda e: e.tensor_tensor(Ea[0:nk, 0:Nq], Ea[0:nk, 0:Nq], mask, ALU.mult), reads=[Ean, "swamask"], writes=[Ean])
                S.op("dve", lambda e: e.tensor_tensor(Eb[0:nk, 0:Nq], Eb[0:nk, 0:Nq], mask, ALU.mult), reads=[Ebn, "swamask"], writes=[Ebn])
            last = (i == nch - 1)
            S.op("pe", lambda e: e.matmul(po[:, 0:Nq], Va, Ea[0:nk, 0:Nq], start=(i == 0), stop=False), reads=[vn, Ean], writes=[pon], pe_acc=(i > 0))
            S.op("pe", lambda e: e.matmul(po[:, 0:Nq], Vb, Eb[0:nk, 0:Nq], start=False, stop=last), reads=[vn, Ebn], writes=[pon], pe_acc=True)
            S.op("pe", lambda e: e.matmul(pd[:, 0:Nq], onesAB[0:nk, 0, :], Ea[0:nk, 0:Nq], start=(i == 0), stop=False), reads=["onesAB", Ean], writes=[pdn], pe_acc=(i > 0))
            S.op("pe", lambda e: e.matmul(pd[:, 0:Nq], onesAB[0:nk, 1, :], Eb[0:nk, 0:Nq], start=False, stop=(last and sink is None)), reads=["onesAB", Ebn], writes=[pdn], pe_acc=True)
            cur = nxt
        if sink is not None:
            sa, sb_ = sink
            S.op("pe", lambda e: e.matmul(pd[:, 0:Nq], onesAB[0:1, 0, :], sa, start=False, stop=False), reads=["onesAB", "sinkrow"], writes=[pdn], pe_acc=True)
            S.op("pe", lambda e: e.matmul(pd[:, 0:Nq], onesAB[0:1, 1, :], sb_, start=False, stop=True), reads=["onesAB", "sinkrow"], writes=[pdn], pe_acc=True)
        rec = wk["rec"]
        S.op("dve", lambda e: e.reciprocal(rec[:, 0:Nq], pd[:, 0:Nq]), reads=[pdn], writes=["rec"])
        S.op("dve", lambda e: e.tensor_tensor(o_out, po[:, 0:Nq], rec[:, 0:Nq], ALU.mult), reads=[pon, "rec"], writes=[on])

    def rr_alloc(st, T, do_rope):
        sets = []
        for k in range(2):
            sets.append(dict(k=k, x=b.sb(st, "rr_x%d" % k, [128, T]), xs=(b.sb(st, "rr_xs%d" % k, [128, T]) if do_rope else None),
                             sq=b.sb(st, "rr_sq%d" % k, [128, 512], BF16), rs=b.sb(st, "rr_rs%d" % k, [128, T]), t1=b.sb(st, "rr_t1%d" % k, [128, 512])))
        return sets

    def rms_rope_rows(W_, p0, src_rows_fn, n_rows, T, gain2, gainn, do_norm, do_rope, rope_idx, out_bf, outn, out32=None, out32n=None,
                      out32_pre_rope=False):
        k = W_["k"]
        x, xs, sqb, rs, t1 = W_["x"], W_["xs"], W_["sq"], W_["rs"], W_["t1"]
        xn, xsn, sqn, rsn, t1n = ("rr_x%d" % k, "rr_xs%d" % k, "rr_sq%d" % k, "rr_rs%d" % k, "rr_t1%d" % k)
        for (r0, nr, ap) in src_rows_fn(False):
            S.dma("sp", x[p0 + r0:p0 + r0 + nr, 0:T], ap, writes=[xn])
        if do_rope:
            for (r0, nr, ap) in src_rows_fn(True):
                S.dma("act", xs[p0 + r0:p0 + r0 + nr, 0:T], ap, writes=[xsn])
        nr = n_rows
        pr = slice(p0, p0 + nr)
        W = min(512, T)
        for c in range(T // W):
            cs = slice(c * W, (c + 1) * W)
            if do_norm:
                S.op("act", lambda e: e.activation(sqb[pr, 0:W], x[pr, cs], AF.Square), reads=[xn], writes=[sqn])
                pt, pn = b.ps()
                S.op("pe", lambda e: e.matmul(pt[:, 0:W], ones_bf[pr, :], sqb[pr, 0:W], start=True, stop=True),
                     reads=[sqn, "ones_bf"], writes=[pn])
                S.op("act", lambda e: e.activation(rs[pr, cs], pt[pr, 0:W], AF.Ln, bias=epsb[pr, :], scale=1.0 / nr), reads=[pn, "epsb"], writes=[rsn])
                S.op("act", lambda e: e.activation(rs[pr, cs], rs[pr, cs], AF.Exp, scale=-0.5), reads=[rsn], writes=[rsn])
                S.op("dve", lambda e: e.scalar_tensor_tensor(x[pr, cs], x[pr, cs], gain2[pr, 0:1], rs[pr, cs], ALU.mult, ALU.mult),
                     reads=[xn, rsn, gainn], writes=[xn])
                if do_rope:
                    S.op("dve", lambda e: e.scalar_tensor_tensor(xs[pr, cs], xs[pr, cs], gain2[pr, 1:2], rs[pr, cs], ALU.mult, ALU.mult),
                         reads=[xsn, rsn, gainn], writes=[xsn])
            if out32 is not None and out32_pre_rope:
                S.op("pool", lambda e: e.tensor_copy(out32[pr, cs], x[pr, cs]), reads=[xn], writes=[out32n])
            if do_rope:
                S.op("dve", lambda e: e.tensor_tensor(x[pr, cs], x[pr, cs], rope[pr, rope_idx, cs], ALU.mult), reads=[xn, "rope"], writes=[xn])
                S.op("pool", lambda e: e.tensor_tensor(t1[pr, 0:W], xs[pr, cs], rope[pr, rope_idx + 1, cs], ALU.mult), reads=[xsn, "rope"], writes=[t1n])
                S.op("dve", lambda e: e.tensor_tensor(out_bf[:, cs], x[pr, cs], t1[pr, 0:W], ALU.add), reads=[xn, t1n], writes=[outn])
            else:
                S.op("act", lambda e: e.copy(out_bf[:, cs], x[pr, cs]), reads=[xn], writes=[outn])

    def swap_rows(base, T0, T1, UT, nd):
        q = nd // 4
        def f(swapped):
            if not swapped:
                return [(0, nd, UT[base:base + nd, T0:T1])]
            return [(0, q, UT[base + q:base + 2 * q, T0:T1]), (q, q, UT[base:base + q, T0:T1]),
                    (2 * q, q, UT[base + 3 * q:base + 4 * q, T0:T1]), (3 * q, q, UT[base + 2 * q:base + 3 * q, T0:T1])]
        return f

    rope = b.sb(cst, "rope", [128, 4, 2048], BF16)
    S.dma("pool", rope[:], D["rope64"].rearrange("a p t -> p a t"), writes=["rope"])

    def stage_gqa_like(g, l, kind):
        G = GR[g]
        T, L, P, nseq, do_rope = G["T"], G["L"], G["P"], G["nseq"], G["rope"]
        UT, UTM, OT = D["UT_" + g], D["UTM_" + g], D["OT_" + g]
        qo, ko = (OFF["sq"], OFF["sk"]) if kind == "swa" else (OFF["gq"], OFF["gk"])
        vcol = 1536 if kind == "swa" else 1664
        bi = 1 if kind == "swa" else 3
        do_norm = kind == "gqa"
        scale = 64 ** -0.5
        nkc_ctx = P // 128
        with ExitStack() as st:
            kT = b.sb(st, "kT", [128, P + T], BF16)
            Vt = b.sb(st, "Vt", [128, (P + T) // 128, 2, 128], BF16)
            qTh = b.sb(st, "qTh", [128, 8, T], BF16)
            gq2 = b.sb(st, "gq2", [128, 2]); gk2 = b.sb(st, "gk2", [128, 2])
            sinkrow = b.sb(st, "sinkrow", [1, 8, 128], BF16)
            sk32 = b.sb(st, "sk32", [1, 8])
            wk = dict(E=[b.sb(st, "E%d" % i, [128, 512], BF16) for i in range(6)], rec=b.sb(st, "rec", [128, 512]))
            obs = [b.sb(st, "ob%d" % i, [128, 512], BF16) for i in range(2)]
            swm = b.sb(st, "swamask", [128, 2, 512], BF16)
            S.dma("sp", swm[:], D["swamask"].rearrange("a p c -> p a c"), writes=["swamask"])
            S.op("pool", lambda e: e.memset(qTh[64:128, 0:4, :], 0.0), writes=["qTh%d" % hh for hh in range(4)])
            S.op("pool", lambda e: e.memset(qTh[0:64, 4:8, :], 0.0), writes=["qTh%d" % hh for hh in range(4, 8)])
            S.op("pool", lambda e: e.memset(Vt[:], 0.0), writes=["Vt"])
            if do_norm:
                S.dma("sp", gq2[:], D["gqa_qn"][l], writes=["gq2"])
                S.dma("sp", gk2[:], D["gqa_kn"][l], writes=["gk2"])
            else:
                S.dma("sp", sk32[:], D["sink"][l], writes=["sk32"])
                S.op("act", lambda e: e.activation(sk32[:], sk32[:], AF.Exp), reads=["sk32"], writes=["sk32"])
                S.op("dve", lambda e: e.tensor_copy(sinkrow[:], sk32[:].unsqueeze(2).to_broadcast([1, 8, 128])), reads=["sk32"], writes=["sinkrow"])
            with ExitStack() as s2:
                RR = rr_alloc(s2, T, do_rope)
                k32 = b.sb(s2, "k32", [128, T]) if g == "p" else None
                v32 = b.sb(s2, "v32", [128, (P + T) // 128, 128])
                if P:
                    src_k = D["c_swa_kT"] if kind == "swa" else D["c_gqa_kT"]
                    src_v = D["c_swa_v"] if kind == "swa" else D["c_gqa_v"]
                    S.dma("pool", kT[:, 0:P], src_k[l].rearrange("h d t -> (h d) t"), writes=["kT"])
                    S.dma("sp", v32[:, 0:P // 128, :], src_v[l].rearrange("(c p) f -> p c f", p=128), writes=["v32"])
                S.dma("sp", v32[:, P // 128:, :], UTM[:, vcol:vcol + 128].rearrange("(c p) f -> p c f", p=128), writes=["v32"])
                S.op("dve", lambda e: e.tensor_copy(Vt[:, :, 0, 0:64], v32[:, :, 0:64]), reads=["v32"], writes=["Vt"])
                S.op("dve", lambda e: e.tensor_copy(Vt[:, :, 1, 64:128], v32[:, :, 64:128]), reads=["v32"], writes=["Vt"])
                if g == "p":
                    dst = D["o_swa_v"] if kind == "swa" else D["o_gqa_v"]
                    S.dma("act", dst[l].rearrange("(c p) f -> p c f", p=128), v32[:], reads=["v32"], writes=[])
                nrr = 0
                for kvh in range(2):
                    p0 = kvh * 64
                    rms_rope_rows(RR[nrr % 2], p0, swap_rows(ko + kvh * 64, 0, T, UT, 64), 64, T, gk2, "gk2", do_norm, do_rope, 0,
                                  kT[p0:p0 + 64, P:P + T], "kT", out32=k32, out32n="k32", out32_pre_rope=True)
                    nrr += 1
                if g == "p":
                    dst = D["o_swa_kT"] if kind == "swa" else D["o_gqa_kT"]
                    S.dma("sp", dst[l], k32[:], reads=["k32"], writes=[])
                for h in range(8):
                    p0 = (h // 4) * 64
                    rms_rope_rows(RR[nrr % 2], p0, swap_rows(qo + h * 64, 0, T, UT, 64), 64, T, gq2, "gq2", do_norm, do_rope, 0,
                                  qTh[p0:p0 + 64, h, :], "qTh%d" % h)
                    nrr += 1
                nu = 0
                qna = ["qTh%d" % hh for hh in range(4)]
                qnb = ["qTh%d" % hh for hh in range(4, 8)]
                for sq_ in range(nseq):
                    T0 = sq_ * L
                    kb0 = P + T0
                    for qb in range(L // 128):
                        cols = [(c * 128, None) for c in range(nkc_ctx)]
                        if kind == "swa" and P:
                            cols += [(kb0 + kb * 128, mi) for (kb, mi) in ((qb - 1, 0), (qb, None), (qb + 1, 1)) if 0 <= kb < L // 128]
                        else:
                            cols += [(kb0 + kb * 128, None) for kb in range(L // 128)]
                        chunks = [(kT[:, c0:c0 + 128], kT[:, c0:c0 + 128], "kT", Vt[:, c0 // 128, 0, :], Vt[:, c0 // 128, 1, :], "Vt", 128,
                                   None if mi is None else swm[:, mi, :]) for (c0, mi) in cols]
                        ob, obn = obs[nu % 2], "ob%d" % (nu % 2)
                        nu += 1
                        q0 = T0 + qb * 128
                        attn_pair(qTh[:, 0:4, q0:q0 + 128], qna, qTh[:, 4:8, q0:q0 + 128], qnb, 512, chunks, scale, ob[:], obn, wk,
                                  sink=((sinkrow[0:1, 0:4, :], sinkrow[0:1, 4:8, :]) if kind == "swa" else None))
                        for kvh in range(2):
                            S.dma("sp" if kvh == 0 else "act", OT[bi, kvh * 256:(kvh + 1) * 256, q0:q0 + 128].rearrange("(h d) t -> d h t", d=64),
                                  ob[kvh * 64:(kvh + 1) * 64, :].rearrange("d (h t) -> d h t", h=4), reads=[obn], writes=[])
                S.barrier()

    def stage_mla(g, l):
        G = GR[g]
        T, L, P, nseq, do_rope = G["T"], G["L"], G["P"], G["nseq"], G["rope"]
        UT, OT = D["UT_" + g], D["OT_" + g]
        scale = 96 ** -0.5
        NK = P + L
        for sq_ in range(nseq):
            T0, T1 = sq_ * L, (sq_ + 1) * L
            with ExitStack() as st:
                ckvT = b.sb(st, "ckvT", [128, NK], BF16)
                krT = b.sb(st, "krT", [96, NK], BF16)
                cqn = b.sb(st, "cqn", [128, 2, L], BF16)
                wuq = b.sb(st, "wuq", [128, 2, 768], BF16); wuqs = b.sb(st, "wuqs", [128, 2, 768], BF16)
                wukv = b.sb(st, "wukv", [128, 1024], BF16)
                g_q = b.sb(st, "g_q", [128, 2]); g_kv = b.sb(st, "g_kv", [128, 1])
                S.dma("pool", wuq[:], D["w_uq"][l].rearrange("(kc p) c -> p kc c", p=128), writes=["wuq"])
                S.dma("pool", wuqs[:], D["w_uq_sw"][l].rearrange("(kc p) c -> p kc c", p=128), writes=["wuqs"])
                S.dma("pool", wukv[:], D["w_ukv"][l], writes=["wukv"])
                S.dma("sp", g_q[:], D["mla_qnT"][l], writes=["g_q"])
                S.dma("sp", g_kv[:], D["mla_kvn"][l], writes=["g_kv"])
                with ExitStack() as s2:
                    x = b.sb(s2, "m_x", [128, 2, L]); sqb = b.sb(s2, "m_sq", [128, 2, 512], BF16); rs = b.sb(s2, "m_rs", [128, 512])
                    xk = b.sb(s2, "m_xk", [128, L]); kr32 = b.sb(s2, "m_kr", [96, L]); krs = b.sb(s2, "m_krs", [96, L]); t1 = b.sb(s2, "m_t1", [96, 512])
                    if P:
                        S.dma("pool", ckvT[:, 0:P], D["c_ckvT"][l], writes=["ckvT"])
                        S.dma("pool", krT[64:96, 0:P], D["c_krT"][l], writes=["krT"])
                    S.dma("sp", x[:], UT[OFF["cq"]:OFF["cq"] + 256, T0:T1].rearrange("(kc p) t -> p kc t", p=128), writes=["m_x"])
                    S.dma("sp", xk[:], UT[OFF["ckv"]:OFF["ckv"] + 128, T0:T1], writes=["m_xk"])
                    S.dma("sp", kr32[64:96, :], UT[OFF["kr"]:OFF["kr"] + 32, T0:T1], writes=["m_kr"])
                    if do_rope:
                        for (r0, nr, ap) in swap_rows(OFF["kr"], T0, T1, UT, 32)(True):
                            S.dma("act", krs[64 + r0:64 + r0 + nr, :], ap, writes=["m_krs"])
                    for c in range(L // 512 if L >= 512 else 1):
                        w = min(512, L)
                        cs = slice(c * w, (c + 1) * w)
                        pt, pn = b.ps()
                        for kc in range(2):
                            S.op("act", lambda e: e.activation(sqb[:, kc, 0:w], x[:, kc, cs], AF.Square), reads=["m_x"], writes=["m_sq"])
                        for kc in range(2):
                            S.op("pe", lambda e: e.matmul(pt[:, 0:w], ones_bf[:], sqb[:, kc, 0:w], start=(kc == 0), stop=(kc == 1)),
                                 reads=["m_sq", "ones_bf"], writes=[pn], pe_acc=True)
                        rstd_from_sumsq(pt, pn, rs, "m_rs", 128, w, 1.0 / 256)
                        for kc in range(2):
                            S.op("dve", lambda e: e.scalar_tensor_tensor(cqn[:, kc, cs], x[:, kc, cs], g_q[:, kc:kc + 1], rs[:, 0:w], ALU.mult, ALU.mult),
                                 reads=["m_x", "m_rs", "g_q"], writes=["cqn"])
                        pt, pn = b.ps()
                        S.op("act", lambda e: e.activation(sqb[:, 0, 0:w], xk[:, cs], AF.Square), reads=["m_xk"], writes=["m_sq"])
                        S.op("pe", lambda e: e.matmul(pt[:, 0:w], ones_bf[:], sqb[:, 0, 0:w], start=True, stop=True), reads=["m_sq", "ones_bf"], writes=[pn])
                        rstd_from_sumsq(pt, pn, rs, "m_rs", 128, w, 1.0 / 128)
                        S.op("dve", lambda e: e.scalar_tensor_tensor(xk[:, cs], xk[:, cs], g_kv[:, 0:1], rs[:, 0:w], ALU.mult, ALU.mult),
                             reads=["m_xk", "m_rs", "g_kv"], writes=["m_xk"])
                        S.op("pool", lambda e: e.tensor_copy(ckvT[:, P + c * w:P + (c + 1) * w], xk[:, cs]), reads=["m_xk"], writes=["ckvT"])
                        if do_rope:
                            S.op("dve", lambda e: e.tensor_tensor(t1[64:96, 0:w], kr32[64:96, cs], rope[64:96, 2, cs], ALU.mult), reads=["m_kr", "rope"], writes=["m_t1"])
                            S.op("pool", lambda e: e.tensor_tensor(krs[64:96, cs], krs[64:96, cs], rope[64:96, 3, cs], ALU.mult), reads=["m_krs", "rope"], writes=["m_krs"])
                            S.op("dve", lambda e: e.tensor_tensor(krT[64:96, P + c * w:P + (c + 1) * w], t1[64:96, 0:w], krs[64:96, cs], ALU.add),
                                 reads=["m_t1", "m_krs"], writes=["krT"])
                        else:
                            S.op("dve", lambda e: e.tensor_copy(krT[64:96, P + c * w:P + (c + 1) * w], kr32[64:96, cs]), reads=["m_kr"], writes=["krT"])
                    if g == "p":
                        S.dma("sp", D["o_ckvT"][l, :, T0:T1], xk[:], reads=["m_xk"], writes=[])
                        S.dma("sp", D["o_krT"][l, :, T0:T1], kr32[64:96, :], reads=["m_kr"], writes=[])
                    S.barrier()
                Vall = b.sb(st, "Vall", [128, NK // 128, 8, 128], BF16)
                S.op("pool", lambda e: e.memset(Vall[:], 0.0), writes=["Vall"])
                for c in range(NK // 128):
                    pt, pn = b.ps()
                    S.op("pe", lambda e: e.matmul(pt[:, :], ckvT[:, c * 128:(c + 1) * 128],
                                                  wukv[:].rearrange("p (h x) -> p h x", x=128)[:, :, 64:128], start=True, stop=True),
                         reads=["ckvT", "wukv"], writes=[pn])
                    pv = pt[:, :].rearrange("p (h two d) -> p h two d", two=2, d=64)
                    S.op("act", lambda e: e.copy(Vall[:, c, 0::2, 0:64], pv[:, :, 0, :]), reads=[pn], writes=["Vall"])
                    S.op("dve", lambda e: e.tensor_copy(Vall[:, c, 1::2, 64:128], pv[:, :, 1, :]), reads=[pn], writes=["Vall"])
                S.barrier()
                with ExitStack() as s2:
                    wk = dict(E=[b.sb(s2, "E%d" % i, [128, 512], BF16) for i in range(6)], rec=b.sb(s2, "rec", [128, 512]))
                    kTh = [b.sb(s2, "kTh%d" % i, [128, NK], BF16) for i in range(4)]
                    qh = [b.sb(s2, "qh%d" % i, [128, L], BF16) for i in range(4)]
                    obs = [b.sb(s2, "ob%d" % i, [128, 512], BF16) for i in range(2)]
                    t2 = b.sb(s2, "m_t2", [96, 512]); t3 = b.sb(s2, "m_t3", [96, 512])
                    for i in range(4):
                        S.op("pool", lambda e: e.memset(kTh[i][96:128, :], 0.0), writes=["kTh%d" % i])
                        S.op("pool", lambda e: e.memset(qh[i][96:128, :], 0.0), writes=["qh%d" % i])
                    nu = 0
                    W = min(512, L)
                    for hp in range(4):
                        bufs = []
                        for h2 in range(2):
                            h = hp * 2 + h2
                            bi_ = (hp % 2) * 2 + h2
                            kt, ktn = kTh[bi_], "kTh%d" % bi_
                            q_, q_n = qh[bi_], "qh%d" % bi_
                            bufs.append((kt, ktn, q_, q_n))
                            for c in range((NK + 511) // 512):
                                w = min(512, NK - c * 512)
                                pt, pn = b.ps()
                                S.op("pe", lambda e: e.matmul(pt[:, 0:w], wukv[:, h * 128:(h + 1) * 128], ckvT[:, c * 512:c * 512 + w], start=True, stop=True),
                                     reads=["wukv", "ckvT"], writes=[pn])
                                S.op("act", lambda e: e.copy(kt[0:64, c * 512:c * 512 + w], pt[0:64, 0:w]), reads=[pn], writes=[ktn])
                            S.op("pool", lambda e: e.tensor_copy(kt[64:96, :], krT[64:96, :]), reads=["krT"], writes=[ktn])
                            for c in range(L // W):
                                cs = slice(c * W, (c + 1) * W)
                                pt, pn = b.ps()
                                for kc in range(2):
                                    S.op("pe", lambda e: e.matmul(pt[0:96, 0:W], wuq[:, kc, h * 96:(h + 1) * 96], cqn[:, kc, cs], start=(kc == 0), stop=(kc == 1)),
                                         reads=["wuq", "cqn"], writes=[pn], pe_acc=(kc > 0))
                                S.op("act", lambda e: e.copy(q_[0:64, cs], pt[0:64, 0:W]), reads=[pn], writes=[q_n])
                                if do_rope:
                                    pt2, pn2 = b.ps()
                                    for kc in range(2):
                                        S.op("pe", lambda e: e.matmul(pt2[0:96, 0:W], wuqs[:, kc, h * 96:(h + 1) * 96], cqn[:, kc, cs], start=(kc == 0), stop=(kc == 1)),
                                             reads=["wuqs", "cqn"], writes=[pn2], pe_acc=(kc > 0))
                                    S.op("dve", lambda e: e.tensor_tensor(t2[64:96, 0:W], pt[64:96, 0:W], rope[64:96, 2, cs], ALU.mult), reads=[pn, "rope"], writes=["m_t2"])
                                    S.op("dve", lambda e: e.tensor_tensor(t3[64:96, 0:W], pt2[64:96, 0:W], rope[64:96, 3, cs], ALU.mult), reads=[pn2, "rope"], writes=["m_t3"])
                                    S.op("pool", lambda e: e.tensor_tensor(q_[64:96, cs], t2[64:96, 0:W], t3[64:96, 0:W], ALU.add), reads=["m_t2", "m_t3"], writes=[q_n])
                                else:
                                    S.op("dve", lambda e: e.tensor_copy(q_[64:96, cs], pt[64:96, 0:W]), reads=[pn], writes=[q_n])
                        (kta, ktan, qa, qan), (ktb, ktbn, qb_, qbn) = bufs
                        for c in range(L // W):
                            chunks = [(kta[:, kc * 128:(kc + 1) * 128], ktb[:, kc * 128:(kc + 1) * 128], (ktan, ktbn), Vall[:, kc, hp * 2, :], Vall[:, kc, hp * 2 + 1, :], "Vall", 128, None)
                                      for kc in range(NK // 128)]
                            ob, obn = obs[nu % 2], "ob%d" % (nu % 2)
                            nu += 1
                            attn_pair(qa[:, c * W:(c + 1) * W], [qan], qb_[:, c * W:(c + 1) * W], [qbn], W, chunks, scale, ob[:, 0:W], obn, wk)
                            S.dma("sp", OT[2, hp * 128:(hp + 1) * 128, T0 + c * W:T0 + (c + 1) * W], ob[:, 0:W], reads=[obn], writes=[])
                    S.barrier()

    def stage_hgrn(g, l):
        G = GR[g]
        T, L, P, nseq = G["T"], G["L"], G["P"], G["nseq"]
        UT, UTM, OT = D["UT_" + g], D["UTM_" + g], D["OT_" + g]
        NTL = L // 128
        ident, triU, triL, sL, sU, csel = (cm[:, i, :] for i in range(6))
        with ExitStack() as st:
            lbb = b.sb(st, "lbb", [128, 2, 512]); oml = b.sb(st, "oml", [128, 2, 512])
            with ExitStack() as s2:
                lbr = b.sb(s2, "lbr", [128, DEPTH, 2, 512]); den = b.sb(s2, "lden", [128, 2, 512])
                S.dma("sp", lbr[:], D["lbT"].partition_broadcast(128), writes=["lbr"])
                S.op("act", lambda e: e.activation(lbr[:], lbr[:], AF.Exp), reads=["lbr"], writes=["lbr"])
                S.op("dve", lambda e: e.tensor_tensor(den[:], lbr[:, 0], lbr[:, 1], ALU.add), reads=["lbr"], writes=["lden"])
                S.op("dve", lambda e: e.reciprocal(den[:], den[:]), reads=["lden"], writes=["lden"])
                if l == 0:
                    S.op("pool", lambda e: e.memset(lbb[:], 0.0), writes=["lbb"])
                else:
                    S.op("dve", lambda e: e.tensor_tensor(lbb[:], lbr[:, 1], den[:], ALU.mult), reads=["lbr", "lden"], writes=["lbb"])
                S.op("dve", lambda e: e.tensor_scalar(oml[:], lbb[:], -1.0, 1.0, ALU.mult, ALU.add), reads=["lbb"], writes=["oml"])
                S.barrier()
            hn = b.sb(st, "hn", [128, 4]); S.dma("sp", hn[:], D["hgrn_normT"][l], writes=["hn"])
            Sst = [b.sb(st, "Sst%d" % i, [128, 2, 4, 128]) for i in range(2)]
            oall = b.sb(st, "oall", [128, 2, 4, L], BF16)
            W_ = {}
            for d in range(2):
                for k in range(2):
                    sfx = "%d%d" % (d, k)
                    W_[d, k] = dict(
                        sfx=sfx,
                        tt=b.sb(st, "h_t" + sfx, [128, 512]), vv=b.sb(st, "h_v" + sfx, [128, 512]), gg=b.sb(st, "h_g" + sfx, [128, 512]),
                        kt=b.sb(st, "h_kt" + sfx, [128, 512]), kh=b.sb(st, "h_kh" + sfx, [128, 512]),
                        khm=b.sb(st, "h_khm" + sfx, [128, 4, 4, 128]), qT=b.sb(st, "h_qT" + sfx, [128, 4, 128]), qt=b.sb(st, "h_qt" + sfx, [128, 4, 128]),
                        eb=b.sb(st, "h_eb" + sfx, [128, 4, 128]), ktT=b.sb(st, "h_ktT" + sfx, [128, 4, 128]), AT=b.sb(st, "h_AT" + sfx, [128, 4, 128]),
                        tmpo=b.sb(st, "h_tmpo" + sfx, [128, 512]))
            etot = [b.sb(st, "h_etot%d" % k, [128, 2, 4, 4]) for k in range(2)]
            fin = dict(os=b.sb(st, "f_os", [128, 4, 256]), gT=b.sb(st, "f_gT", [128, 4, 256]), sq=b.sb(st, "f_sq", [128, 4, 256], BF16),
                       rs=b.sb(st, "f_rs", [128, 4, 256]), ob=b.sb(st, "f_ob", [128, 4, 256], BF16))
            b.nrot = 4
            M4 = lambda M: M.unsqueeze(1).to_broadcast([128, 4, 128])

            def sstn(sp, d, h):
                return "Sst%d_%d%d" % (sp, d, h)

            def partA(rec):
                k = rec["k"]
                for d in range(2):
                    w = W_[d, k]; x = w["sfx"]
                    t0 = rec["T0"] + rec["ti"][d] * 128
                    S.dma("sp", w["tt"][:], UTM[t0:t0 + 128, d * 512:(d + 1) * 512], writes=["h_t" + x])
                    S.dma("sp", w["vv"][:], UTM[t0:t0 + 128, 1024:1536], writes=["h_v" + x])
                    S.dma("act", w["qT"][:], UT[0:512, t0:t0 + 128].rearrange("(h p) t -> p h t", p=128), writes=["h_qT" + x])
                for d in range(2):
                    w = W_[d, k]; x = w["sfx"]
                    S.op("act", lambda e: e.activation(w["tt"][:], w["tt"][:], AF.Sigmoid), reads=["h_t" + x], writes=["h_t" + x])
                for d in range(2):
                    w = W_[d, k]; x = w["sfx"]
                    S.op("act", lambda e: e.activation(w["qT"][:], w["qT"][:], AF.Silu), reads=["h_qT" + x], writes=["h_qT" + x])
                for d in range(2):
                    w = W_[d, k]; x = w["sfx"]
                    S.op("dve", lambda e: e.tensor_tensor(w["tt"][:], w["tt"][:], oml[:, d], ALU.mult), reads=["h_t" + x, "oml"], writes=["h_t" + x])
                    S.op("dve", lambda e: e.scalar_tensor_tensor(w["gg"][:], w["tt"][:], 1e-30, lbb[:, d], ALU.max, ALU.add), reads=["h_t" + x, "lbb"], writes=["h_g" + x])
                for d in range(2):
                    w = W_[d, k]; x = w["sfx"]
                    S.op("act", lambda e: e.activation(w["gg"][:], w["gg"][:], AF.Ln), reads=["h_g" + x], writes=["h_g" + x])
                for d in range(2):
                    w = W_[d, k]; x = w["sfx"]
                    S.op("dve", lambda e: e.tensor_tensor(w["tt"][:], oml[:, d], w["tt"][:], ALU.subtract), reads=["h_t" + x, "oml"], writes=["h_t" + x])

            def partB(rec, stage):
                k = rec["k"]
                Ms = [(triU, sL), (triL, sU)]
                if stage == 0:
                    pbs = []
                    for d in range(2):
                        w = W_[d, k]; x = w["sfx"]
                        pb, pbn = b.ps()
                        S.op("pe", lambda e: e.matmul(pb[:], Ms[d][0], w["gg"][:], start=True, stop=True), reads=["cm", "h_g" + x], writes=[pbn])
                        pr, prn = b.ps()
                        S.op("pe", lambda e: e.matmul(pr[:], Ms[d][1], w["gg"][:], start=True, stop=True), reads=["cm", "h_g" + x], writes=[prn])
                        pbs.append((pb, pbn, pr, prn))
                    for d in range(2):
                        w = W_[d, k]; x = w["sfx"]
                        pb, pbn, pr, prn = pbs[d]
                        S.op("act", lambda e: e.activation(w["kt"][:], pb[:], AF.Exp, scale=-1.0), reads=[pbn], writes=["h_kt" + x])
                        S.op("act", lambda e: e.activation(w["kh"][:], pr[:], AF.Exp), reads=[prn], writes=["h_kh" + x])
                    pts = []
                    ptot, ptotn = b.ps()
                    for d in range(2):
                        w = W_[d, k]; x = w["sfx"]
                        pbt, pbtn = b.ps()
                        for h in range(4):
                            hs = slice(h * 128, (h + 1) * 128)
                            S.op("pe", lambda e: e.matmul(pbt[:, hs], w["gg"][:, hs], Ms[d][0], start=True, stop=True), reads=["h_g" + x, "cm"], writes=[pbtn], pe_acc=(h > 0))
                            S.op("pe", lambda e: e.matmul(ptot[:, d * 16 + h * 4:d * 16 + h * 4 + 4], w["gg"][:, hs], csel[:, 0:4], start=True, stop=True),
                                 reads=["h_g" + x, "cm"], writes=[ptotn], pe_acc=(d > 0 or h > 0))
                        pts.append((pbt, pbtn))
                    for d in range(2):
                        w = W_[d, k]; x = w["sfx"]
                        S.op("dve", lambda e: e.tensor_tensor(w["kt"][:], w["kt"][:], w["tt"][:], ALU.mult), reads=["h_kt" + x, "h_t" + x], writes=["h_kt" + x])
                        S.op("dve", lambda e: e.tensor_tensor(w["kh"][:], w["kh"][:], w["tt"][:], ALU.mult), reads=["h_kh" + x, "h_t" + x], writes=["h_kh" + x])
                    for d in range(2):
                        w = W_[d, k]; x = w["sfx"]
                        pbt, pbtn = pts[d]
                        S.op("act", lambda e: e.activation(w["eb"][:], pbt[:].rearrange("p (h t) -> p h t", h=4), AF.Exp), reads=[pbtn], writes=["h_eb" + x])
                    S.op("act", lambda e: e.activation(etot[k][:], ptot[:, 0:32].rearrange("p (d h j) -> p d h j", d=2, h=4), AF.Exp), reads=[ptotn], writes=["h_etot%d" % k])
                    for d in range(2):
                        w = W_[d, k]; x = w["sfx"]
                        S.op("dve", lambda e: e.tensor_tensor(w["qt"][:], w["qT"][:], w["eb"][:], ALU.mult), reads=["h_qT" + x, "h_eb" + x], writes=["h_qt" + x])
                    for d in range(2):
                        w = W_[d, k]; x = w["sfx"]
                        for j in range(4):
                            S.op("act", lambda e: e.activation(w["khm"][:, :, j, :], w["kh"][:].rearrange("p (h c) -> p h c", h=4), AF.Copy, scale=csel[:, j:j + 1]),
                                 reads=["h_kh" + x, "cm"], writes=["h_khm" + x])
                elif stage == 1:
                    for d in range(2):
                        w = W_[d, k]; x = w["sfx"]
                        pk, pkn = b.ps()
                        for h in range(4):
                            hs = slice(h * 128, (h + 1) * 128)
                            S.op("pe", lambda e: e.matmul(pk[:, hs], w["kt"][:, hs], ident, start=True, stop=True), reads=["h_kt" + x, "cm"], writes=[pkn], pe_acc=(h > 0))
                        S.op("act", lambda e: e.copy(w["ktT"][:], pk[:].rearrange("p (h t) -> p h t", h=4)), reads=[pkn], writes=["h_ktT" + x])
                elif stage == 2:
                    for d in range(2):
                        w = W_[d, k]; x = w["sfx"]
                        pa, pan = b.ps()
                        for h in range(4):
                            hs = slice(h * 128, (h + 1) * 128)
                            S.op("pe", lambda e: e.matmul(pa[:, hs], w["ktT"][:, h, :], w["qt"][:, h, :], start=True, stop=True), reads=["h_ktT" + x, "h_qt" + x], writes=[pan], pe_acc=(h > 0))
                        S.op("dve", lambda e: e.tensor_tensor(w["AT"][:], pa[:].rearrange("p (h t) -> p h t", h=4), M4(Ms[d][0]), ALU.mult), reads=[pan, "cm"], writes=["h_AT" + x])
                else:
                    for d in range(2):
                        w = W_[d, k]; x = w["sfx"]
                        po_, pon_ = b.acc(d)
                        for h in range(4):
                            hs = slice(h * 128, (h + 1) * 128)
                            S.op("pe", lambda e: e.matmul(po_[:, hs], w["vv"][:, hs], w["AT"][:, h, :], start=True, stop=True), reads=["h_v" + x, "h_AT" + x], writes=[pon_], pe_acc=(h > 0))
                        S.op("act", lambda e: e.copy(w["tmpo"][:], po_[:]), reads=[pon_], writes=["h_tmpo" + x])

            def chunk_step(rec, n_):
                k = rec["k"]; sp = rec["sp"]
                pss = []
                for d in range(2):
                    w = W_[d, k]; x = w["sfx"]
                    j = n_ if d == 0 else 3 - n_
                    js = slice(j * 32, (j + 1) * 32)
                    pi_, pin_ = b.acc(2 + d)
                    ps_, psn2 = b.ps()
                    pss.append((ps_, psn2, j))
                    for h in range(4):
                        hs = slice(h * 128, (h + 1) * 128)
                        S.op("pe", lambda e: e.matmul(pi_[:, h * 128 + j * 32:h * 128 + (j + 1) * 32], Sst[sp][:, d, h, :], w["qt"][:, h, js], start=True, stop=True),
                             reads=[sstn(sp, d, h), "h_qt" + x], writes=[pin_], pe_acc=(n_ > 0 or h > 0))
                        S.op("pe", lambda e: e.matmul(ps_[:, hs], w["khm"][:, h, j, :], w["vv"][:, hs], start=True, stop=True),
                             reads=["h_khm" + x, "h_v" + x], writes=[psn2], pe_acc=(h > 0))
                for d in range(2):
                    w = W_[d, k]; x = w["sfx"]
                    ps_, psn2, j = pss[d]
                    for h in range(4):
                        hs = slice(h * 128, (h + 1) * 128)
                        S.op("dve", lambda e: e.scalar_tensor_tensor(Sst[sp][:, d, h, :], Sst[sp][:, d, h, :], etot[k][:, d, h, j:j + 1], ps_[:, hs], ALU.mult, ALU.add),
                             reads=[psn2, "h_etot%d" % k, sstn(sp, d, h)], writes=[sstn(sp, d, h)])

            def finish(rec):
                k = rec["k"]
                for d in range(2):
                    w = W_[d, k]; x = w["sfx"]
                    pi_, pin_ = b.acc(2 + d)
                    tl = rec["ti"][d] * 128
                    S.op("dve", lambda e: e.tensor_tensor(oall[:, d, :, tl:tl + 128], w["tmpo"][:].rearrange("p (h t) -> p h t", h=4),
                                                          pi_[:].rearrange("p (h t) -> p h t", h=4), ALU.add),
                         reads=[pin_, "h_tmpo" + x], writes=["oall"])

            def seq_final(sq_):
                T0 = sq_ * L
                sp = sq_ % 2
                for c in range(L // 256):
                    cs = slice(c * 256, (c + 1) * 256)
                    os_, gT, sq, rs, ob = fin["os"], fin["gT"], fin["sq"], fin["rs"], fin["ob"]
                    S.dma("act", gT[:], UT[OFF["hg"]:OFF["hg"] + 512, T0 + c * 256:T0 + (c + 1) * 256].rearrange("(h p) t -> p h t", p=128), writes=["f_gT"])
                    S.op("act", lambda e: e.activation(gT[:], gT[:], AF.Silu), reads=["f_gT"], writes=["f_gT"])
                    S.op("dve", lambda e: e.tensor_tensor(os_[:], oall[:, 0, :, cs], oall[:, 1, :, cs], ALU.add), reads=["oall"], writes=["f_os"])
                    S.op("act", lambda e: e.activation(sq[:], os_[:], AF.Square), reads=["f_os"], writes=["f_sq"])
                    for hh in range(2):
                        pn_, pnn = b.ps()
                        for h2 in range(2):
                            h = hh * 2 + h2
                            S.op("pe", lambda e: e.matmul(pn_[:, h2 * 256:(h2 + 1) * 256], ones_bf[:], sq[:, h, :], start=True, stop=True),
                                 reads=["f_sq", "ones_bf"], writes=[pnn], pe_acc=(h2 > 0))
                        rstd_from_sumsq(pn_, pnn, rs[:, hh * 2:hh * 2 + 2, :].rearrange("p a t -> p (a t)"), "f_rs", 128, 512, 1.0 / 128)
                    for h in range(4):
                        S.op("dve", lambda e: e.scalar_tensor_tensor(os_[:, h, :], os_[:, h, :], hn[:, h:h + 1], rs[:, h, :], ALU.mult, ALU.mult),
                             reads=["f_os", "hn", "f_rs"], writes=["f_os"])
                    S.op("dve", lambda e: e.tensor_tensor(ob[:], os_[:], gT[:], ALU.mult), reads=["f_os", "f_gT"], writes=["f_ob"])
                    S.dma("sp", OT[0, :, T0 + c * 256:T0 + (c + 1) * 256].rearrange("(h p) t -> p h t", p=128), ob[:], reads=["f_ob"], writes=[])
                if g == "p":
                    S.dma("sp", D["o_hgrn"][l, sq_].rearrange("d h k v -> k d h v"), Sst[sp][:], reads=[sstn(sp, d_, h_) for d_ in range(2) for h_ in range(4)], writes=[])

            recs = []
            for sq_ in range(nseq):
                for i in range(NTL):
                    recs.append(dict(sq=sq_, sp=sq_ % 2, T0=sq_ * L, i=i, ti=(i, NTL - 1 - i), k=len(recs) % 2))
            partA(recs[0])
            for stg in range(4):
                partB(recs[0], stg)
            for n, rec in enumerate(recs):
                nxt = recs[n + 1] if n + 1 < len(recs) else None
                if rec["i"] == 0:
                    sp = rec["sp"]
                    names = [sstn(sp, d_, h_) for d_ in range(2) for h_ in range(4)]
                    if P:
                        S.dma("sp", Sst[sp][:], D["st_hgrn"][l].rearrange("d h k v -> k d h v"), writes=names)
                    else:
                        S.op("pool", lambda e: e.memset(Sst[sp][:], 0.0), writes=names)
                if nxt is not None:
                    partA(nxt)
                for n_ in range(4):
                    chunk_step(rec, n_)
                    if nxt is not None:
                        partB(nxt, n_)
                finish(rec)
                if rec["i"] == NTL - 1:
                    seq_final(rec["sq"])
            b.nrot = 6
            S.barrier()

    def epilogue(st, z, zn, xt, xn, gco, Xdst, c0, wkn):
        sq, rs, tmp = wkn
        for kc in range(8):
            S.op("act", lambda e: e.activation(sq[:, kc, :], z[:, kc, :], AF.Square), reads=(zn if isinstance(zn, list) else [zn]), writes=["e_sq"])
        pt, pn = b.ps()
        for kc in range(8):
            S.op("pe", lambda e: e.matmul(pt[:], ones_bf[:], sq[:, kc, :], start=(kc == 0), stop=(kc == 7)), reads=["e_sq", "ones_bf"], writes=[pn], pe_acc=True)
        rstd_from_sumsq(pt, pn, rs, "e_rs", 128, 512, 1.0 / DM)
        for kc in range(8):
            S.op("dve", lambda e: e.tensor_tensor(tmp[:], z[:, kc, :], rs[:], ALU.mult), reads=(zn if isinstance(zn, list) else [zn]) + ["e_rs"], writes=["e_tmp"])
            S.op("dve", lambda e: e.scalar_tensor_tensor(xt[:, kc, :], tmp[:], gco[:, kc:kc + 1], xt[:, kc, :], ALU.mult, ALU.add),
                 reads=["e_tmp", "coef", xn], writes=[xn])
        S.dma("sp", Xdst[:, c0:c0 + 512].rearrange("(kc p) t -> p kc t", p=128), xt[:], reads=[xn], writes=[])

    def stage_merge(g, l, Xsrc, Xdst):
        G = GR[g]
        T = G["T"]
        UT, OT = D["UT_" + g], D["OT_" + g]
        with ExitStack() as st:
            Wb = b.sb(st, "Wb", [128, 4, 4, 1024], BF16)
            Wo = b.sb(st, "Wo", [128, 8, 1024], BF16)
            S.dma("pool", Wb[:], D["w_branch"][l].rearrange("n (kc p) c -> p n kc c", p=128), writes=["Wb"])
            S.dma("pool", Wo[:], D["w_out"][l].rearrange("(kc p) c -> p kc c", p=128), writes=["Wo"])
            ot = b.sb(st, "ot", [128, 4, 4, 512], BF16)
            yp = b.sb(st, "yp", [128, 8, 512], BF16)
            gts = [b.sb(st, "gt%d" % i, [128, 512]) for i in range(3)]
            accf = b.sb(st, "accf", [128, 512]); tm2 = b.sb(st, "tm2", [128, 512])
            z = b.sb(st, "z", [128, 8, 512]); xt = b.sb(st, "xt", [128, 8, 512])
            wkn = (b.sb(st, "e_sq", [128, 8, 512], BF16), b.sb(st, "e_rs", [128, 512]), b.sb(st, "e_tmp", [128, 512]))
            ng = 0
            for tt in range(T // 512):
                cs = slice(tt * 512, (tt + 1) * 512)
                S.dma("sp", ot[:], OT[:, :, cs].rearrange("n (kc p) t -> p n kc t", p=128), writes=["ot"])
                S.dma("act", xt[:], Xsrc[:, cs].rearrange("(kc p) t -> p kc t", p=128), writes=["xt"])
                for dmc in range(8):
                    for n in range(4):
                        gt, gtn = gts[ng % 3], "gt%d" % (ng % 3)
                        ng += 1
                        r0 = OFF["gates"] + n * 1024 + dmc * 128
                        S.dma("sp", gt[:], UT[r0:r0 + 128, cs], writes=[gtn])
                        S.op("act", lambda e: e.activation(gt[:], gt[:], AF.Sigmoid), reads=[gtn], writes=[gtn])
                        pt, pn = b.ps()
                        for kc in range(4):
                            S.op("pe", lambda e: e.matmul(pt[:], Wb[:, n, kc, dmc * 128:(dmc + 1) * 128], ot[:, n, kc, :], start=(kc == 0), stop=(kc == 3)),
                                 reads=["Wb", "ot"], writes=[pn], pe_acc=True)
                        if n == 0:
                            S.op("dve", lambda e: e.tensor_tensor(accf[:], pt[:], gt[:], ALU.mult), reads=[pn, gtn], writes=["accf"])
                        elif n < 3:
                            S.op("dve", lambda e: e.tensor_tensor(tm2[:], pt[:], gt[:], ALU.mult), reads=[pn, gtn], writes=["tm2"])
                            S.op("dve", lambda e: e.tensor_tensor(accf[:], accf[:], tm2[:], ALU.add), reads=["tm2", "accf"], writes=["accf"])
                        else:
                            S.op("dve", lambda e: e.tensor_tensor(tm2[:], pt[:], gt[:], ALU.mult), reads=[pn, gtn], writes=["tm2"])
                            S.op("dve", lambda e: e.tensor_tensor(yp[:, dmc, :], accf[:], tm2[:], ALU.add), reads=["tm2", "accf"], writes=["yp"])
                for oc in range(8):
                    pt, pn = b.ps()
                    for kc in range(8):
                        S.op("pe", lambda e: e.matmul(pt[:], Wo[:, kc, oc * 128:(oc + 1) * 128], yp[:, kc, :], start=(kc == 0), stop=(kc == 7)),
                             reads=["Wo", "yp"], writes=[pn], pe_acc=True)
                    S.op("act", lambda e: e.copy(z[:, oc, :], pt[:]), reads=[pn], writes=["z"])
                epilogue(st, z, "z", xt, "xt", coef[:, l, G["cond"], 2], Xdst, tt * 512, wkn)
            S.barrier()

    def stage_mlp(g, l, Xsrc, Xdst):
        G = GR[g]
        T = G["T"]
        NT = T // 512
        with ExitStack() as st:
            h2 = stage_h(st, g, l, Xsrc, 3, 4)
            zall = b.sb(st, "zall", [128, 8, T])
            w1 = [b.sb(st, "w1_%d" % i, [128, 8, 512], BF16) for i in range(2)]
            w2 = [b.sb(st, "w2_%d" % i, [128, 4, 1024], BF16) for i in range(2)]
            hid = [b.sb(st, "hid%d" % i, [128, 4, 512], BF16) for i in range(2)]
            rl = [b.sb(st, "rl%d" % i, [128, 512]) for i in range(2)]
            xt = b.sb(st, "xt", [128, 8, 512])
            wkn = (b.sb(st, "e_sq", [128, 8, 512], BF16), b.sb(st, "e_rs", [128, 512]), b.sb(st, "e_tmp", [128, 512]))
            nh = 0
            nr = 0
            for cg in range(8):
                wa, wan = w1[cg % 2], "w1_%d" % (cg % 2)
                wb_, wbn = w2[cg % 2], "w2_%d" % (cg % 2)
                S.dma("pool", wa[:], D["w_mlp_in"][l, :, cg * 512:(cg + 1) * 512].rearrange("(kc p) c -> p kc c", p=128), writes=[wan])
                S.dma("pool", wb_[:], D["w_mlp_out"][l, cg * 512:(cg + 1) * 512, :].rearrange("(fc p) c -> p fc c", p=128), writes=[wbn])
                for tt in range(NT):
                    cs = slice(tt * 512, (tt + 1) * 512)
                    hd, hdn = hid[nh % 2], "hid%d" % (nh % 2)
                    nh += 1
                    for cc in range(4):
                        pt, pn = b.ps()
                        for kc in range(8):
                            S.op("pe", lambda e: e.matmul(pt[:], wa[:, kc, cc * 128:(cc + 1) * 128], h2[:, kc, cs], start=(kc == 0), stop=(kc == 7)),
                                 reads=[wan, "hT"], writes=[pn], pe_acc=(kc > 0))
                        r, rn = rl[nr % 2], "rl%d" % (nr % 2)
                        nr += 1
                        S.op("act", lambda e: e.activation(r[:], pt[:], AF.Relu), reads=[pn], writes=[rn])
                        S.op("dve", lambda e: e.tensor_tensor(hd[:, cc, :], r[:], r[:], ALU.mult), reads=[rn], writes=[hdn])
                    for oc in range(8):
                        pt, pn = b.ps()
                        for fc in range(4):
                            S.op("pe", lambda e: e.matmul(pt[:], wb_[:, fc, oc * 128:(oc + 1) * 128], hd[:, fc, :], start=(fc == 0), stop=(fc == 3)),
                                 reads=[wbn, hdn], writes=[pn], pe_acc=(fc > 0))
                        zn = "z%d_%d" % (tt, oc)
                        if cg == 0:
                            S.op("act", lambda e: e.copy(zall[:, oc, cs], pt[:]), reads=[pn], writes=[zn])
                        else:
                            S.op("dve", lambda e: e.tensor_tensor(zall[:, oc, cs], zall[:, oc, cs], pt[:], ALU.add), reads=[pn, zn], writes=[zn])
            for tt in range(NT):
                cs = slice(tt * 512, (tt + 1) * 512)
                S.dma("act", xt[:], Xsrc[:, cs].rearrange("(kc p) t -> p kc t", p=128), writes=["xt"])
                epilogue(st, zall[:, :, cs], ["z%d_%d" % (tt, oc) for oc in range(8)], xt, "xt", coef[:, l, G["cond"], 5], Xdst, tt * 512, wkn)
            S.barrier()

    for g in ("s", "p"):
        X = D["xT_" + g]
        for l in range(DEPTH):
            if "proj" in STAGES:
                with ExitStack() as st:
                    hT = stage_h(st, g, l, X, 0, 1)
                    stage_proj(g, l, hT)
            if "hgrn" in STAGES:
                stage_hgrn(g, l)
            if "swa" in STAGES:
                stage_gqa_like(g, l, "swa")
            if "mla" in STAGES:
                stage_mla(g, l)
            if "gqa" in STAGES:
                stage_gqa_like(g, l, "gqa")
            if "merge" in STAGES:
                stage_merge(g, l, X, D["X1_" + g])
            Xn = D["yT_" + g] if l == DEPTH - 1 else D["X2_%d_%s" % (l, g)]
            if "mlp" in STAGES:
                stage_mlp(g, l, D["X1_" + g], Xn)
            X = Xn
    S.barrier()
    return nc, b


_CACHE = {}


def _consts():
    nf = 16
    t = np.arange(2048)
    row, col = (t // 64).astype(np.float32), (t % 64).astype(np.float32)
    rope = np.zeros((4, 128, 2048), np.float32)
    def fill(ci, si, r0, nf):
        inv = (10000.0 ** (-np.arange(nf, dtype=np.float32) / nf)).astype(np.float32)
        ar = (row[None, :] * inv[:, None]).astype(np.float32)
        ac = (col[None, :] * inv[:, None]).astype(np.float32)
        for k, a in enumerate((ar, ac)):
            b0 = r0 + k * 2 * nf
            rope[ci, b0:b0 + nf] = np.cos(a); rope[ci, b0 + nf:b0 + 2 * nf] = np.cos(a)
            rope[si, b0:b0 + nf] = -np.sin(a); rope[si, b0 + nf:b0 + 2 * nf] = np.sin(a)
    fill(0, 1, 0, 16)
    fill(0, 1, 64, 16)
    fill(2, 3, 64, 8)
    s_ = np.arange(128)[:, None]; t_ = np.arange(128)[None, :]
    same = (s_ // 32) == (t_ // 32)
    cm = np.zeros((6, 128, 128), np.float32)
    cm[0] = np.eye(128)
    cm[1] = same & (s_ <= t_)
    cm[2] = same & (s_ >= t_)
    cm[3] = same & (s_ > t_)
    cm[4] = same & (s_ < t_)
    cm[5][:, 0:4] = (s_ // 32) == np.arange(4)[None, :]
    j = np.arange(128)[:, None]; i = (np.arange(512) % 128)[None, :]
    swm = np.stack([(j >= i), (j <= i)]).astype(np.float32).astype(ml_dtypes.bfloat16)
    return rope, cm, swm


def _perm(nd):
    q = nd // 4
    return np.concatenate([np.arange(q, 2 * q), np.arange(0, q), np.arange(3 * q, 4 * q), np.arange(2 * q, 3 * q)])


def kernel(**inp):
    f = lambda a: np.ascontiguousarray(np.asarray(a, dtype=np.float32))
    I = {k: f(v) for k, v in inp.items()}
    if "prog" not in _CACHE:
        _CACHE["prog"] = build_program()
    nc, b = _CACHE["prog"]
    rope, cm, swm = _consts()
    fm = lambda v, n: f(v.reshape(v.shape[0], n, 128).transpose(0, 2, 1))
    shared = dict(
        w_ada=I["w_ada"], b_adaT=fm(I["b_ada"], 48),
        gains=f(np.stack([fm(I[k], 8) for k in ("norm_mix_pre", "norm_mix_post", "norm_mlp_pre", "norm_mlp_post")], axis=2)),
        w_in=I["w_in"], lbT=f(np.stack([I["hgrn_lb_fwd"], I["hgrn_lb_bwd"]], axis=1)),
        hgrn_normT=fm(I["hgrn_norm"], 4), sink=f(I["swa_sink"][:, None, :]),
        mla_qnT=fm(I["mla_q_norm"], 2), mla_kvn=f(I["mla_kv_norm"][:, :, None]),
        w_uq=I["mla_w_uq"], w_ukv=I["mla_w_ukv"],
        gqa_qn=f(np.tile(np.stack([I["gqa_q_norm"], I["gqa_q_norm"][:, _perm(64)]], axis=2), (1, 2, 1))),
        gqa_kn=f(np.tile(np.stack([I["gqa_k_norm"], I["gqa_k_norm"][:, _perm(64)]], axis=2), (1, 2, 1))),
        w_branch=I["w_branch"], w_out=I["w_out"], w_mlp_in=I["w_mlp_in"], w_mlp_out=I["w_mlp_out"],
        rope64=rope, cmat=cm, swamask=swm,
    )
    wsw = I["mla_w_uq"].reshape(DEPTH, 256, 8, 96).copy()
    wsw[..., 64:96] = wsw[..., 64:96][..., _perm(32)]
    shared["w_uq_sw"] = f(wsw.reshape(DEPTH, 256, 768))
    in_maps = []
    for i in range(NCORE):
        bb = i % 2
        m = dict(shared)
        m["xT_s"] = f(I["x_sample"][bb].T)
        m["xT_p"] = f(I["x_prompt"][4 * i:4 * i + 4].reshape(1024, DM).T)
        cc = np.stack([I["c_ctx"], I["c"][bb]], axis=1)
        m["cT"] = f(cc.reshape(8, 128, 2).transpose(1, 0, 2))
        m["st_hgrn"] = f(I["state_hgrn"][bb])
        m["c_swa_kT"] = f(I["cache_swa_k"][bb].transpose(0, 2, 3, 1))
        m["c_swa_v"] = f(I["cache_swa_v"][bb].reshape(DEPTH, 512, 128))
        m["c_ckvT"] = f(I["cache_mla_ckv"][bb].transpose(0, 2, 1))
        m["c_krT"] = f(I["cache_mla_kr"][bb].transpose(0, 2, 1))
        m["c_gqa_kT"] = f(I["cache_gqa_k"][bb].transpose(0, 2, 3, 1))
        m["c_gqa_v"] = f(I["cache_gqa_v"][bb].reshape(DEPTH, 512, 128))
        in_maps.append(m)
    res = run_bass_kernel_spmd(nc, in_maps, core_ids=list(range(NCORE)))
    R = res.results
    y_prompt = np.concatenate([R[i]["yT_p"].T.reshape(4, 256, DM) for i in range(NCORE)], axis=0)
    y_sample = np.stack([R[0]["yT_s"].T, R[1]["yT_s"].T], axis=0)
    cat = lambda fn: np.ascontiguousarray(np.concatenate([fn(R[i]) for i in range(NCORE)], axis=0).astype(np.float32))
    n_hgrn = cat(lambda r: r["o_hgrn"].transpose(1, 0, 2, 3, 4, 5))
    kT = lambda a: a.reshape(DEPTH, 2, 64, 4, 256).transpose(3, 0, 4, 1, 2)
    vv = lambda a: a.reshape(DEPTH, 4, 256, 2, 64).transpose(1, 0, 2, 3, 4)
    n_swa_k = cat(lambda r: kT(r["o_swa_kT"]))
    n_swa_v = cat(lambda r: vv(r["o_swa_v"]))
    n_ckv = cat(lambda r: r["o_ckvT"].reshape(DEPTH, 128, 4, 256).transpose(2, 0, 3, 1))
    n_kr = cat(lambda r: r["o_krT"].reshape(DEPTH, 32, 4, 256).transpose(2, 0, 3, 1))
    n_gqa_k = cat(lambda r: kT(r["o_gqa_kT"]))
    n_gqa_v = cat(lambda r: vv(r["o_gqa_v"]))
    return (np.ascontiguousarray(y_prompt.astype(np.float32)), np.ascontiguousarray(y_sample.astype(np.float32)),
            n_hgrn, n_swa_k, n_swa_v, n_ckv, n_kr, n_gqa_k, n_gqa_v)
```

```python
import numpy as np
from contextlib import ExitStack
import ml_dtypes
import concourse.bass as bass
import concourse.mybir as mybir
from concourse.bass_utils import run_bass_kernel_spmd

F32 = mybir.dt.float32
BF16 = mybir.dt.bfloat16
AF = mybir.ActivationFunctionType
ALU = mybir.AluOpType

DM = 1024
DEPTH = 2
NCORE = 8
EPS = 1e-6
OFF = dict(hq=0, ff=512, fb=1024, hi=1536, hg=2048, sq=2560, sk=3072, sv=3200, cq=3328, ckv=3584, kr=3712,
           gq=3744, gk=4256, gv=4384, gates=4512)
D_IN = 8608
UTM_COLS = 1792
STAGES = {"proj", "hgrn", "swa", "mla", "gqa", "merge", "mlp"}


class Sched:
    NDMA = 24

    def __init__(self, nc, es):
        self.nc = nc
        self.eng = {"pe": nc.tensor, "act": nc.scalar, "dve": nc.vector, "pool": nc.gpsimd, "sp": nc.sync}
        self.sem = {k: es.enter_context(nc.semaphore("s_" + k)) for k in ("pe", "act", "dve", "pool")}
        self.cnt = {k: 0 for k in self.sem}
        self.dsem = [es.enter_context(nc.semaphore("d%d" % i)) for i in range(self.NDMA)]
        self.dcnt = [0] * self.NDMA
        self.dnext = 0
        self.seen = {k: {} for k in self.eng}
        self.lastw = {}
        self.reads = {}
        self.n_instr = 0

    def _sem_of(self, key):
        return self.sem[key] if isinstance(key, str) else self.dsem[key[1]]

    def _wait(self, e, tok):
        key, val = tok
        if self.seen[e].get(key, 0) >= val:
            return
        self.eng[e].wait_ge(self._sem_of(key), val)
        self.seen[e][key] = val

    def _deps(self, e, reads, writes, pe_acc=False):
        best = {}
        for r in reads:
            t = self.lastw.get(r)
            if t is not None and best.get(t[0], 0) < t[1]:
                best[t[0]] = t[1]
        for w in writes:
            t = self.lastw.get(w)
            if t is not None and best.get(t[0], 0) < t[1]:
                if not (pe_acc and t[0] == "pe"):
                    best[t[0]] = t[1]
            for t in self.reads.get(w, ()):
                if best.get(t[0], 0) < t[1]:
                    best[t[0]] = t[1]
        for key, val in best.items():
            self._wait(e, (key, val))

    def _record(self, tok, reads, writes):
        for r in reads:
            lst = self.reads.setdefault(r, [])
            lst[:] = [t for t in lst if t[0] != tok[0]]
            lst.append(tok)
        for w in writes:
            self.lastw[w] = tok
            self.reads[w] = []

    def op(self, e, fn, reads=(), writes=(), pe_acc=False):
        self._deps(e, reads, writes, pe_acc)
        ins = fn(self.eng[e])
        self.cnt[e] += 1
        ins.then_inc(self.sem[e], 1)
        self._record((e, self.cnt[e]), reads, writes)
        self.n_instr += 1
        return ins

    def dma(self, q, out, in_, reads=(), writes=()):
        i = self.dnext
        self.dnext = (self.dnext + 1) % self.NDMA
        if self.dcnt[i] > 0:
            self._wait(q, (("d", i), self.dcnt[i]))
        self._deps(q, reads, writes)
        ins = self.eng[q].dma_start(out=out, in_=in_)
        self.dcnt[i] += 16
        ins.then_inc(self.dsem[i], 16)
        self._record((("d", i), self.dcnt[i]), reads, writes)
        self.n_instr += 1
        return ins

    def barrier(self):
        best = {}
        for k in self.cnt:
            if self.cnt[k]:
                best[k] = self.cnt[k]
        for i in range(self.NDMA):
            if self.dcnt[i]:
                best[("d", i)] = self.dcnt[i]
        for e in self.eng:
            for key, val in best.items():
                self._wait(e, (key, val))
        self.lastw = {}
        self.reads = {}


class B:
    def __init__(self, nc, es):
        self.nc, self.es = nc, es
        self.S = Sched(nc, es)
        self.D = {}
        self.psn = 0

    def din(self, name, shape, dt=F32):
        self.D[name] = self.nc.dram_tensor(name, list(shape), dt, kind="ExternalInput").ap()
        return self.D[name]

    def dout(self, name, shape, dt=F32):
        self.D[name] = self.nc.dram_tensor(name, list(shape), dt, kind="ExternalOutput").ap()
        return self.D[name]

    def dscr(self, name, shape, dt=F32):
        self.D[name] = self.nc.dram_tensor(name, list(shape), dt, kind="Internal").ap()
        return self.D[name]

    def sb(self, st, name, shape, dt=F32):
        self.uid = getattr(self, "uid", 0) + 1
        return st.enter_context(self.nc.sbuf_tensor("sb%d_%s" % (self.uid, name), list(shape), dt))

    def ps(self):
        i = self.psn % self.nrot
        self.psn += 1
        return self.psum[i], "ps%d" % i

    nrot = 6

    def acc(self, i):
        k = (6, 7, 4, 5)[i]
        return self.psum[k], "ps%d" % k


def build_program():
    nc = bass.Bass("TRN2", target_bir_lowering=False)
    es = ExitStack()
    b = B(nc, es)
    S = b.S
    D = b.D
    GR = {
        "s": dict(T=2048, nseq=1, L=2048, P=512, rope=True, cond=1),
        "p": dict(T=1024, nseq=4, L=256, P=0, rope=False, cond=0),
    }
    for g, G in GR.items():
        T = G["T"]
        b.din("xT_" + g, [DM, T])
        b.dout("yT_" + g, [DM, T])
        b.dscr("X1_" + g, [DM, T])
        for l in range(DEPTH - 1):
            b.dscr("X2_%d_%s" % (l, g), [DM, T])
        b.dscr("UT_" + g, [68 * 128, T])
        b.dscr("UTM_" + g, [T, UTM_COLS])
        b.dscr("OT_" + g, [4, 512, T], BF16)
        b.dscr("YP_" + g, [DM, T], BF16)
    b.din("cT", [128, 8, 2])
    b.din("w_ada", [DEPTH, DM, 6 * DM])
    b.din("b_adaT", [DEPTH, 128, 48])
    b.din("gains", [DEPTH, 128, 4, 8])
    b.din("w_in", [DEPTH, DM, D_IN])
    b.din("lbT", [DEPTH, 2, 512])
    b.din("hgrn_normT", [DEPTH, 128, 4])
    b.din("sink", [DEPTH, 1, 8])
    b.din("mla_qnT", [DEPTH, 128, 2])
    b.din("mla_kvn", [DEPTH, 128, 1])
    b.din("w_uq", [DEPTH, 256, 768])
    b.din("w_uq_sw", [DEPTH, 256, 768])
    b.din("w_ukv", [DEPTH, 128, 1024])
    b.din("gqa_qn", [DEPTH, 128, 2])
    b.din("gqa_kn", [DEPTH, 128, 2])
    b.din("w_branch", [DEPTH, 4, 512, DM])
    b.din("w_out", [DEPTH, DM, DM])
    b.din("w_mlp_in", [DEPTH, DM, 4 * DM])
    b.din("w_mlp_out", [DEPTH, 4 * DM, DM])
    b.din("st_hgrn", [DEPTH, 2, 4, 128, 128])
    b.din("c_swa_kT", [DEPTH, 2, 64, 512])
    b.din("c_swa_v", [DEPTH, 512, 128])
    b.din("c_ckvT", [DEPTH, 128, 512])
    b.din("c_krT", [DEPTH, 32, 512])
    b.din("c_gqa_kT", [DEPTH, 2, 64, 512])
    b.din("c_gqa_v", [DEPTH, 512, 128])
    b.din("rope64", [4, 128, 2048])
    b.din("cmat", [6, 128, 128])
    b.din("swamask", [2, 128, 512], BF16)
    b.dout("o_hgrn", [DEPTH, 4, 2, 4, 128, 128])
    b.dout("o_swa_kT", [DEPTH, 128, 1024])
    b.dout("o_swa_v", [DEPTH, 1024, 128])
    b.dout("o_ckvT", [DEPTH, 128, 1024])
    b.dout("o_krT", [DEPTH, 32, 1024])
    b.dout("o_gqa_kT", [DEPTH, 128, 1024])
    b.dout("o_gqa_v", [DEPTH, 1024, 128])

    b.psum = [es.enter_context(nc.psum_tensor("psb%d" % i, [128, 512], F32)) for i in range(8)]

    cst = ExitStack()
    es.enter_context(cst)
    ones_bf = b.sb(cst, "ones_bf", [128, 128], BF16)
    ones_f = b.sb(cst, "ones_f", [128, 128], F32)
    cm = b.sb(cst, "cm", [128, 6, 128], F32)
    modT = b.sb(cst, "modT", [128, DEPTH, 48, 2], F32)
    gains = b.sb(cst, "gains", [128, DEPTH, 4, 8], F32)
    coef = b.sb(cst, "coef", [128, DEPTH, 2, 6, 8], F32)
    S.op("pool", lambda e: e.memset(ones_bf[:], 1.0), writes=["ones_bf"])
    onesAB = b.sb(cst, "onesAB", [128, 2, 128], BF16)
    S.op("pool", lambda e: e.memset(onesAB[:], 0.0), writes=["onesAB"])
    S.op("pool", lambda e: e.memset(onesAB[:, 0, 0:64], 1.0), writes=["onesAB"])
    S.op("pool", lambda e: e.memset(onesAB[:, 1, 64:128], 1.0), writes=["onesAB"])
    S.op("pool", lambda e: e.memset(ones_f[:], 1.0), writes=["ones_f"])
    S.dma("sp", cm[:], D["cmat"].rearrange("a p c -> p a c"), writes=["cm"])
    S.dma("sp", gains[:], D["gains"].rearrange("l p a c -> p l a c"), writes=["gains"])

    with ExitStack() as st:
        cT = b.sb(st, "cT", [128, 8, 2])
        scT = b.sb(st, "scT", [128, 8, 2])
        badaT = b.sb(st, "badaT", [128, DEPTH, 48])
        wa = [b.sb(st, "wa%d" % i, [128, 8, 768]) for i in range(2)]
        S.dma("sp", cT[:], D["cT"], writes=["cT"])
        S.dma("sp", badaT[:], D["b_adaT"].rearrange("l p j -> p l j"), writes=["badaT"])
        S.op("act", lambda e: e.activation(scT[:], cT[:], AF.Silu), reads=["cT"], writes=["scT"])
        n = 0
        for l in range(DEPTH):
            pt, pn = b.ps()
            for cg in range(8):
                w = wa[n % 2]
                wn = "wa%d" % (n % 2)
                n += 1
                S.dma("sp" if cg % 2 == 0 else "act", w[:],
                      D["w_ada"][l, :, cg * 768:(cg + 1) * 768].rearrange("(kc p) c -> p kc c", p=128), writes=[wn])
                for jj in range(6):
                    j = cg * 6 + jj
                    for kc in range(8):
                        S.op("pe", lambda e: e.matmul(pt[:, 2 * j:2 * j + 2], w[:, kc, jj * 128:(jj + 1) * 128], scT[:, kc, :],
                                                      start=(kc == 0), stop=(kc == 7)),
                             reads=[wn, "scT"], writes=[pn], pe_acc=True)
            S.op("dve", lambda e: e.tensor_tensor(modT[:, l], pt[:, 0:96].rearrange("p (j c) -> p j c", c=2),
                                                  badaT[:, l].unsqueeze(2).to_broadcast([128, 48, 2]), ALU.add),
                 reads=[pn, "badaT"], writes=["modT"])
        for l in range(DEPTH):
            for c in range(2):
                m = lambda i: modT[:, l, i * 8:(i + 1) * 8, c]
                S.op("dve", lambda e: e.scalar_tensor_tensor(coef[:, l, c, 0], m(1), 1.0, gains[:, l, 0], ALU.add, ALU.mult),
                     reads=["modT", "gains"], writes=["coef"])
                S.op("dve", lambda e: e.tensor_copy(coef[:, l, c, 1], m(0)), reads=["modT"], writes=["coef"])
                S.op("dve", lambda e: e.tensor_tensor(coef[:, l, c, 2], m(2), gains[:, l, 1], ALU.mult), reads=["modT", "gains"], writes=["coef"])
                S.op("dve", lambda e: e.scalar_tensor_tensor(coef[:, l, c, 3], m(4), 1.0, gains[:, l, 2], ALU.add, ALU.mult),
                     reads=["modT", "gains"], writes=["coef"])
                S.op("dve", lambda e: e.tensor_copy(coef[:, l, c, 4], m(3)), reads=["modT"], writes=["coef"])
                S.op("dve", lambda e: e.tensor_tensor(coef[:, l, c, 5], m(5), gains[:, l, 3], ALU.mult), reads=["modT", "gains"], writes=["coef"])
        S.barrier()

    def rstd_from_sumsq(pt, pn, out, outn, npart, ncol, inv_n):
        S.op("act", lambda e: e.activation(out[0:npart, 0:ncol], pt[0:npart, 0:ncol], AF.Ln, bias=epsb[0:npart, :], scale=inv_n),
             reads=[pn, "epsb"], writes=[outn])
        S.op("act", lambda e: e.activation(out[0:npart, 0:ncol], out[0:npart, 0:ncol], AF.Exp, scale=-0.5), reads=[outn], writes=[outn])

    epsb = b.sb(cst, "epsb", [128, 1], F32)
    S.op("pool", lambda e: e.memset(epsb[:], EPS), writes=["epsb"])

    def norm_mod_tile(st_unused, xt, xn, hT, hn, t0, a_ap, sh_ap, tmp, sq, rs):
        for kc in range(8):
            S.op("act", lambda e: e.activation(sq[:, kc, :], xt[:, kc, :], AF.Square), reads=[xn], writes=["sq"])
        pt, pn = b.ps()
        for kc in range(8):
            S.op("pe", lambda e: e.matmul(pt[:], ones_bf[:], sq[:, kc, :], start=(kc == 0), stop=(kc == 7)),
                 reads=["sq", "ones_bf"], writes=[pn], pe_acc=True)
        rstd_from_sumsq(pt, pn, rs, "rs", 128, 512, 1.0 / DM)
        for kc in range(8):
            S.op("dve", lambda e: e.tensor_tensor(tmp[:], xt[:, kc, :], rs[:], ALU.mult), reads=[xn, "rs"], writes=["tmp"])
            S.op("act", lambda e: e.activation(hT[:, kc, t0:t0 + 512], tmp[:], AF.Identity, bias=sh_ap[:, kc:kc + 1], scale=a_ap[:, kc:kc + 1]),
                 reads=["tmp", "coef"], writes=[hn])

    def stage_h(st, g, l, Xsrc, ia, ish):
        G = GR[g]
        T = G["T"]
        hT = b.sb(st, "hT", [128, 8, T], BF16)
        with ExitStack() as s2:
            xts = [b.sb(s2, "xt%d" % i, [128, 8, 512]) for i in range(2)]
            tmp = b.sb(s2, "tmp", [128, 512])
            sq = b.sb(s2, "sq", [128, 8, 512], BF16)
            rs = b.sb(s2, "rs", [128, 512])
            for tt in range(T // 512):
                xt, xn = xts[tt % 2], "xt%d" % (tt % 2)
                S.dma("sp", xt[:], Xsrc[:, tt * 512:(tt + 1) * 512].rearrange("(kc p) t -> p kc t", p=128), writes=[xn])
                norm_mod_tile(None, xt, xn, hT, "hT", tt * 512, coef[:, l, G["cond"], ia], coef[:, l, G["cond"], ish], tmp, sq, rs)
            S.barrier()
        return hT

    def stage_proj(g, l, hT):
        G = GR[g]
        T = G["T"]
        UT, UTM = D["UT_" + g], D["UTM_" + g]
        tm_groups = {1: [(0, 512, 0)], 2: [(0, 512, 512)], 3: [(0, 512, 1024)], 6: [(128, 128, 1536)], 8: [(288, 128, 1664)]}
        with ExitStack() as st:
            wb = [b.sb(st, "wb%d" % i, [128, 8, 512], BF16) for i in range(2)]
            ev = [b.sb(st, "ev%d" % i, [128, 512]) for i in range(4)]
            nev = 0
            for cg in range(17):
                ncol = 512 if cg < 16 else D_IN - 8192
                w, wn = wb[cg % 2], "wb%d" % (cg % 2)
                S.dma("pool", w[:, :, 0:ncol], D["w_in"][l, :, cg * 512:cg * 512 + ncol].rearrange("(kc p) c -> p kc c", p=128), writes=[wn])
                for tt in range(T // 512):
                    for cc in range((ncol + 127) // 128):
                        m = min(128, ncol - cc * 128)
                        pt, pn = b.ps()
                        for kc in range(8):
                            S.op("pe", lambda e: e.matmul(pt[0:m, :], w[:, kc, cc * 128:cc * 128 + m], hT[:, kc, tt * 512:(tt + 1) * 512],
                                                          start=(kc == 0), stop=(kc == 7)), reads=[wn, "hT"], writes=[pn], pe_acc=True)
                        e_, en = ev[nev % 4], "ev%d" % (nev % 4)
                        eng = "act" if nev % 2 == 0 else "dve"
                        nev += 1
                        if eng == "act":
                            S.op("act", lambda e: e.copy(e_[0:m, :], pt[0:m, :]), reads=[pn], writes=[en])
                        else:
                            S.op("dve", lambda e: e.tensor_copy(e_[0:m, :], pt[0:m, :]), reads=[pn], writes=[en])
                        r0 = cg * 512 + cc * 128
                        S.dma("sp", UT[r0:r0 + m, tt * 512:(tt + 1) * 512], e_[0:m, :], reads=[en], writes=[])
                for (c0, cn, dst) in tm_groups.get(cg, []):
                    for t4 in range(T // 128):
                        pt, pn = b.ps()
                        for kc in range(8):
                            S.op("pe", lambda e: e.matmul(pt[:, 0:cn], hT[:, kc, t4 * 128:(t4 + 1) * 128], w[:, kc, c0:c0 + cn],
                                                          start=(kc == 0), stop=(kc == 7)), reads=[wn, "hT"], writes=[pn], pe_acc=True)
                        e_, en = ev[nev % 4], "ev%d" % (nev % 4)
                        eng = "act" if nev % 2 == 0 else "dve"
                        nev += 1
                        if eng == "act":
                            S.op("act", lambda e: e.copy(e_[:, 0:cn], pt[:, 0:cn]), reads=[pn], writes=[en])
                        else:
                            S.op("dve", lambda e: e.tensor_copy(e_[:, 0:cn], pt[:, 0:cn]), reads=[pn], writes=[en])
                        S.dma("sp", UTM[t4 * 128:(t4 + 1) * 128, dst:dst + cn], e_[:, 0:cn], reads=[en], writes=[])
            S.barrier()

    def attn_unit(qT, qn, Kd, Nq, chunks, scale, o_out, on, wk, sink=None):
        po, pon = b.acc(0)
        pd, pdn = b.acc(1)
        nch = len(chunks)

        def emit_st(i):
            kT, kn, V, vn, nk, mask = chunks[i]
            pst, psn_ = b.ps()
            S.op("pe", lambda e: e.matmul(pst[0:nk, 0:Nq], kT, qT, start=True, stop=True), reads=[kn] + (qn if isinstance(qn, list) else [qn]), writes=[psn_])
            return pst, psn_

        cur = emit_st(0)
        for i, (kT, kn, V, vn, nk, mask) in enumerate(chunks):
            nxt = emit_st(i + 1) if i + 1 < nch else None
            pst, psn_ = cur
            E, En = wk["E"][i % 3], "E%d" % (i % 3)
            S.op("act", lambda e: e.activation(E[0:nk, 0:Nq], pst[0:nk, 0:Nq], AF.Exp, scale=scale), reads=[psn_], writes=[En])
            if mask is not None:
                S.op("pool", lambda e: e.tensor_tensor(E[0:nk, 0:Nq], E[0:nk, 0:Nq], mask, ALU.mult), reads=[En, "swamask"], writes=[En])
            last = (i == nch - 1) and sink is None
            S.op("pe", lambda e: e.matmul(po[0:64, 0:Nq], V, E[0:nk, 0:Nq], start=(i == 0), stop=(i == nch - 1)),
                 reads=[vn, En], writes=[pon], pe_acc=(i > 0))
            S.op("pe", lambda e: e.matmul(pd[0:64, 0:Nq], ones_bf[0:nk, 0:64], E[0:nk, 0:Nq], start=(i == 0), stop=last),
                 reads=["ones_bf", En], writes=[pdn], pe_acc=(i > 0))
            cur = nxt
        if sink is not None:
            S.op("pe", lambda e: e.matmul(pd[0:64, 0:Nq], ones_bf[0:1, 0:64], sink, start=False, stop=True),
                 reads=["ones_bf", "sinkrow"], writes=[pdn], pe_acc=True)
        rec = wk["rec"]
        S.op("dve", lambda e: e.reciprocal(rec[0:64, 0:Nq], pd[0:64, 0:Nq]), reads=[pdn], writes=["rec"])
        S.op("dve", lambda e: e.tensor_tensor(o_out, po[0:64, 0:Nq], rec[0:64, 0:Nq], ALU.mult), reads=[pon, "rec"], writes=[on])

    def attn_pair(qa, qan, qb, qbn, Nq, chunks, scale, o_out, on, wk, sink=None):
        po, pon = b.acc(0)
        pd, pdn = b.acc(1)
        nch = len(chunks)
        E = wk["E"]
        ne = len(E)

        def emit_st(i):
            kTa, kTb, kn, Va, Vb, vn, nk, mask = chunks[i]
            p1, n1 = b.ps()
            kna, knb = kn if isinstance(kn, tuple) else (kn, kn)
            S.op("pe", lambda e: e.matmul(p1[0:nk, 0:Nq], kTa, qa, start=True, stop=True), reads=[kna] + qan, writes=[n1])
            p2, n2 = b.ps()
            S.op("pe", lambda e: e.matmul(p2[0:nk, 0:Nq], kTb, qb, start=True, stop=True), reads=[knb] + qbn, writes=[n2])
            return (p1, n1, p2, n2)

        cur = emit_st(0)
        for i, (kTa, kTb, kn, Va, Vb, vn, nk, mask) in enumerate(chunks):
            nxt = emit_st(i + 1) if i + 1 < nch else None
            p1, n1, p2, n2 = cur
            Ea, Ean = E[(2 * i) % ne], "E%d" % ((2 * i) % ne)
            Eb, Ebn = E[(2 * i + 1) % ne], "E%d" % ((2 * i + 1) % ne)
            S.op("act", lambda e: e.activation(Ea[0:nk, 0:Nq], p1[0:nk, 0:Nq], AF.Exp, scale=scale), reads=[n1], writes=[Ean])
            S.op("act", lambda e: e.activation(Eb[0:nk, 0:Nq], p2[0:nk, 0:Nq], AF.Exp, scale=scale), reads=[n2], writes=[Ebn])
            if mask is not None:
                S.op("pool", lambda e: e.tensor_tensor(Ea[0:nk, 0:Nq], Ea[0:nk, 0:Nq], mask, ALU.mult), reads=[Ean, "swamask"], writes=[Ean])
                S.op("dve", lambda e: e.tensor_tensor(Eb[0:nk, 0:Nq], Eb[0:nk, 0:Nq], mask, ALU.mult), reads=[Ebn, "swamask"], writes=[Ebn])
            last = (i == nch - 1)
            S.op("pe", lambda e: e.matmul(po[:, 0:Nq], Va, Ea[0:nk, 0:Nq], start=(i == 0), stop=False), reads=[vn, Ean], writes=[pon], pe_acc=(i > 0))
            S.op("pe", lambda e: e.matmul(po[:, 0:Nq], Vb, Eb[0:nk, 0:Nq], start=False, stop=last), reads=[vn, Ebn], writes=[pon], pe_acc=True)
            S.op("pe", lambda e: e.matmul(pd[:, 0:Nq], onesAB[0:nk, 0, :], Ea[0:nk, 0:Nq], start=(i == 0), stop=False), reads=["onesAB", Ean], writes=[pdn], pe_acc=(i > 0))
            S.op("pe", lambda e: e.matmul(pd[:, 0:Nq], onesAB[0:nk, 1, :], Eb[0:nk, 0:Nq], start=False, stop=(last and sink is None)), reads=["onesAB", Ebn], writes=[pdn], pe_acc=True)
            cur = nxt
        if sink is not None:
            sa, sb_ = sink
            S.op("pe", lambda e: e.matmul(pd[:, 0:Nq], onesAB[0:1, 0, :], sa, start=False, stop=False), reads=["onesAB", "sinkrow"], writes=[pdn], pe_acc=True)
            S.op("pe", lambda e: e.matmul(pd[:, 0:Nq], onesAB[0:1, 1, :], sb_, start=False, stop=True), reads=["onesAB", "sinkrow"], writes=[pdn], pe_acc=True)
        rec = wk["rec"]
        S.op("dve", lambda e: e.reciprocal(rec[:, 0:Nq], pd[:, 0:Nq]), reads=[pdn], writes=["rec"])
        S.op("dve", lambda e: e.tensor_tensor(o_out, po[:, 0:Nq], rec[:, 0:Nq], ALU.mult), reads=[pon, "rec"], writes=[on])

    def rr_alloc(st, T, do_rope):
        sets = []
        for k in range(2):
            sets.append(dict(k=k, x=b.sb(st, "rr_x%d" % k, [128, T]), xs=(b.sb(st, "rr_xs%d" % k, [128, T]) if do_rope else None),
                             sq=b.sb(st, "rr_sq%d" % k, [128, T], BF16), rs=b.sb(st, "rr_rs%d" % k, [128, T]), t1=b.sb(st, "rr_t1%d" % k, [128, T])))
        return sets

    def rms_rope_rows(W_, p0, src_rows_fn, n_rows, T, gain2, gainn, do_norm, do_rope, rope_idx, out_bf, outn, out32=None, out32n=None,
                      out32_pre_rope=False):
        k = W_["k"]
        x, xs, sqb, rs, t1 = W_["x"], W_["xs"], W_["sq"], W_["rs"], W_["t1"]
        xn, xsn, sqn, rsn, t1n = ("rr_x%d" % k, "rr_xs%d" % k, "rr_sq%d" % k, "rr_rs%d" % k, "rr_t1%d" % k)
        for (r0, nr, ap) in src_rows_fn(False):
            S.dma("sp", x[p0 + r0:p0 + r0 + nr, 0:T], ap, writes=[xn])
        if do_rope:
            for (r0, nr, ap) in src_rows_fn(True):
                S.dma("act", xs[p0 + r0:p0 + r0 + nr, 0:T], ap, writes=[xsn])
        nr = n_rows
        pr = slice(p0, p0 + nr)
        W = min(512, T)
        CS = [slice(c * W, (c + 1) * W) for c in range(T // W)]
        if do_norm:
            S.op("act", lambda e: e.activation(sqb[pr, 0:T], x[pr, 0:T], AF.Square), reads=[xn], writes=[sqn])
            pts = []
            for cs in CS:
                pt, pn = b.ps()
                S.op("pe", lambda e: e.matmul(pt[:, 0:W], ones_bf[pr, :], sqb[pr, cs], start=True, stop=True), reads=[sqn, "ones_bf"], writes=[pn])
                pts.append((pt, pn))
            for cs, (pt, pn) in zip(CS, pts):
                S.op("act", lambda e: e.activation(rs[pr, cs], pt[pr, 0:W], AF.Ln, bias=epsb[pr, :], scale=1.0 / nr), reads=[pn, "epsb"], writes=[rsn])
            S.op("act", lambda e: e.activation(rs[pr, 0:T], rs[pr, 0:T], AF.Exp, scale=-0.5), reads=[rsn], writes=[rsn])
            S.op("dve", lambda e: e.scalar_tensor_tensor(x[pr, 0:T], x[pr, 0:T], gain2[pr, 0:1], rs[pr, 0:T], ALU.mult, ALU.mult),
                 reads=[xn, rsn, gainn], writes=[xn])
            if do_rope:
                S.op("dve", lambda e: e.scalar_tensor_tensor(xs[pr, 0:T], xs[pr, 0:T], gain2[pr, 1:2], rs[pr, 0:T], ALU.mult, ALU.mult),
                     reads=[xsn, rsn, gainn], writes=[xsn])
        if out32 is not None and out32_pre_rope:
            S.op("pool", lambda e: e.tensor_copy(out32[pr, 0:T], x[pr, 0:T]), reads=[xn], writes=[out32n])
        if do_rope:
            S.op("dve", lambda e: e.tensor_tensor(x[pr, 0:T], x[pr, 0:T], rope[pr, rope_idx, 0:T], ALU.mult), reads=[xn, "rope"], writes=[xn])
            S.op("pool", lambda e: e.tensor_tensor(t1[pr, 0:T], xs[pr, 0:T], rope[pr, rope_idx + 1, 0:T], ALU.mult), reads=[xsn, "rope"], writes=[t1n])
            S.op("dve", lambda e: e.tensor_tensor(out_bf[:, 0:T], x[pr, 0:T], t1[pr, 0:T], ALU.add), reads=[xn, t1n], writes=[outn])
        else:
            S.op("act", lambda e: e.copy(out_bf[:, 0:T], x[pr, 0:T]), reads=[xn], writes=[outn])

    def swap_rows(base, T0, T1, UT, nd):
        q = nd // 4
        def f(swapped):
            if not swapped:
                return [(0, nd, UT[base:base + nd, T0:T1])]
            return [(0, q, UT[base + q:base + 2 * q, T0:T1]), (q, q, UT[base:base + q, T0:T1]),
                    (2 * q, q, UT[base + 3 * q:base + 4 * q, T0:T1]), (3 * q, q, UT[base + 2 * q:base + 3 * q, T0:T1])]
        return f

    rope = b.sb(cst, "rope", [128, 4, 2048], BF16)
    S.dma("pool", rope[:], D["rope64"].rearrange("a p t -> p a t"), writes=["rope"])

    def stage_gqa_like(g, l, kind):
        G = GR[g]
        T, L, P, nseq, do_rope = G["T"], G["L"], G["P"], G["nseq"], G["rope"]
        UT, UTM, OT = D["UT_" + g], D["UTM_" + g], D["OT_" + g]
        qo, ko = (OFF["sq"], OFF["sk"]) if kind == "swa" else (OFF["gq"], OFF["gk"])
        vcol = 1536 if kind == "swa" else 1664
        bi = 1 if kind == "swa" else 3
        do_norm = kind == "gqa"
        scale = 64 ** -0.5
        nkc_ctx = P // 128
        with ExitStack() as st:
            kT = b.sb(st, "kT", [128, P + T], BF16)
            Vt = b.sb(st, "Vt", [128, (P + T) // 128, 2, 128], BF16)
            qTh = b.sb(st, "qTh", [128, 8, T], BF16)
            gq2 = b.sb(st, "gq2", [128, 2]); gk2 = b.sb(st, "gk2", [128, 2])
            sinkrow = b.sb(st, "sinkrow", [1, 8, 128], BF16)
            sk32 = b.sb(st, "sk32", [1, 8])
            wk = dict(E=[b.sb(st, "E%d" % i, [128, 512], BF16) for i in range(6)], rec=b.sb(st, "rec", [128, 512]))
            obs = [b.sb(st, "ob%d" % i, [128, 512], BF16) for i in range(2)]
            swm = b.sb(st, "swamask", [128, 2, 512], BF16)
            S.dma("sp", swm[:], D["swamask"].rearrange("a p c -> p a c"), writes=["swamask"])
            S.op("pool", lambda e: e.memset(qTh[64:128, 0:4, :], 0.0), writes=["qTh%d" % hh for hh in range(4)])
            S.op("pool", lambda e: e.memset(qTh[0:64, 4:8, :], 0.0), writes=["qTh%d" % hh for hh in range(4, 8)])
            S.op("pool", lambda e: e.memset(Vt[:], 0.0), writes=["Vt"])
            if do_norm:
                S.dma("sp", gq2[:], D["gqa_qn"][l], writes=["gq2"])
                S.dma("sp", gk2[:], D["gqa_kn"][l], writes=["gk2"])
            else:
                S.dma("sp", sk32[:], D["sink"][l], writes=["sk32"])
                S.op("act", lambda e: e.activation(sk32[:], sk32[:], AF.Exp), reads=["sk32"], writes=["sk32"])
                S.op("dve", lambda e: e.tensor_copy(sinkrow[:], sk32[:].unsqueeze(2).to_broadcast([1, 8, 128])), reads=["sk32"], writes=["sinkrow"])
            with ExitStack() as s2:
                RR = rr_alloc(s2, T, do_rope)
                k32 = b.sb(s2, "k32", [128, T]) if g == "p" else None
                v32 = b.sb(s2, "v32", [128, (P + T) // 128, 128])
                if P:
                    src_k = D["c_swa_kT"] if kind == "swa" else D["c_gqa_kT"]
                    src_v = D["c_swa_v"] if kind == "swa" else D["c_gqa_v"]
                    S.dma("pool", kT[:, 0:P], src_k[l].rearrange("h d t -> (h d) t"), writes=["kT"])
                    S.dma("sp", v32[:, 0:P // 128, :], src_v[l].rearrange("(c p) f -> p c f", p=128), writes=["v32"])
                S.dma("sp", v32[:, P // 128:, :], UTM[:, vcol:vcol + 128].rearrange("(c p) f -> p c f", p=128), writes=["v32"])
                S.op("dve", lambda e: e.tensor_copy(Vt[:, :, 0, 0:64], v32[:, :, 0:64]), reads=["v32"], writes=["Vt"])
                S.op("dve", lambda e: e.tensor_copy(Vt[:, :, 1, 64:128], v32[:, :, 64:128]), reads=["v32"], writes=["Vt"])
                if g == "p":
                    dst = D["o_swa_v"] if kind == "swa" else D["o_gqa_v"]
                    S.dma("act", dst[l].rearrange("(c p) f -> p c f", p=128), v32[:], reads=["v32"], writes=[])
                nrr = 0
                for kvh in range(2):
                    p0 = kvh * 64
                    rms_rope_rows(RR[nrr % 2], p0, swap_rows(ko + kvh * 64, 0, T, UT, 64), 64, T, gk2, "gk2", do_norm, do_rope, 0,
                                  kT[p0:p0 + 64, P:P + T], "kT", out32=k32, out32n="k32", out32_pre_rope=True)
                    nrr += 1
                if g == "p":
                    dst = D["o_swa_kT"] if kind == "swa" else D["o_gqa_kT"]
                    S.dma("sp", dst[l], k32[:], reads=["k32"], writes=[])
                for h in range(8):
                    p0 = (h // 4) * 64
                    rms_rope_rows(RR[nrr % 2], p0, swap_rows(qo + h * 64, 0, T, UT, 64), 64, T, gq2, "gq2", do_norm, do_rope, 0,
                                  qTh[p0:p0 + 64, h, :], "qTh%d" % h)
                    nrr += 1
                nu = 0
                qna = ["qTh%d" % hh for hh in range(4)]
                qnb = ["qTh%d" % hh for hh in range(4, 8)]
                for sq_ in range(nseq):
                    T0 = sq_ * L
                    kb0 = P + T0
                    for qb in range(L // 128):
                        cols = [(c * 128, None) for c in range(nkc_ctx)]
                        if kind == "swa" and P:
                            cols += [(kb0 + kb * 128, mi) for (kb, mi) in ((qb - 1, 0), (qb, None), (qb + 1, 1)) if 0 <= kb < L // 128]
                        else:
                            cols += [(kb0 + kb * 128, None) for kb in range(L // 128)]
                        chunks = [(kT[:, c0:c0 + 128], kT[:, c0:c0 + 128], "kT", Vt[:, c0 // 128, 0, :], Vt[:, c0 // 128, 1, :], "Vt", 128,
                                   None if mi is None else swm[:, mi, :]) for (c0, mi) in cols]
                        ob, obn = obs[nu % 2], "ob%d" % (nu % 2)
                        nu += 1
                        q0 = T0 + qb * 128
                        attn_pair(qTh[:, 0:4, q0:q0 + 128], qna, qTh[:, 4:8, q0:q0 + 128], qnb, 512, chunks, scale, ob[:], obn, wk,
                                  sink=((sinkrow[0:1, 0:4, :], sinkrow[0:1, 4:8, :]) if kind == "swa" else None))
                        for kvh in range(2):
                            S.dma("sp" if kvh == 0 else "act", OT[bi, kvh * 256:(kvh + 1) * 256, q0:q0 + 128].rearrange("(h d) t -> d h t", d=64),
                                  ob[kvh * 64:(kvh + 1) * 64, :].rearrange("d (h t) -> d h t", h=4), reads=[obn], writes=[])
                S.barrier()

    def stage_mla(g, l):
        G = GR[g]
        T, L, P, nseq, do_rope = G["T"], G["L"], G["P"], G["nseq"], G["rope"]
        UT, OT = D["UT_" + g], D["OT_" + g]
        scale = 96 ** -0.5
        NK = P + L
        for sq_ in range(nseq):
            T0, T1 = sq_ * L, (sq_ + 1) * L
            with ExitStack() as st:
                ckvT = b.sb(st, "ckvT", [128, NK], BF16)
                krT = b.sb(st, "krT", [96, NK], BF16)
                cqn = b.sb(st, "cqn", [128, 2, L], BF16)
                wuq = b.sb(st, "wuq", [128, 2, 768], BF16); wuqs = b.sb(st, "wuqs", [128, 2, 768], BF16)
                wukv = b.sb(st, "wukv", [128, 1024], BF16)
                g_q = b.sb(st, "g_q", [128, 2]); g_kv = b.sb(st, "g_kv", [128, 1])
                S.dma("pool", wuq[:], D["w_uq"][l].rearrange("(kc p) c -> p kc c", p=128), writes=["wuq"])
                S.dma("pool", wuqs[:], D["w_uq_sw"][l].rearrange("(kc p) c -> p kc c", p=128), writes=["wuqs"])
                S.dma("pool", wukv[:], D["w_ukv"][l], writes=["wukv"])
                S.dma("sp", g_q[:], D["mla_qnT"][l], writes=["g_q"])
                S.dma("sp", g_kv[:], D["mla_kvn"][l], writes=["g_kv"])
                with ExitStack() as s2:
                    x = b.sb(s2, "m_x", [128, 2, L]); sqb = b.sb(s2, "m_sq", [128, 2, 512], BF16); rs = b.sb(s2, "m_rs", [128, 512])
                    xk = b.sb(s2, "m_xk", [128, L]); kr32 = b.sb(s2, "m_kr", [96, L]); krs = b.sb(s2, "m_krs", [96, L]); t1 = b.sb(s2, "m_t1", [96, 512])
                    if P:
                        S.dma("pool", ckvT[:, 0:P], D["c_ckvT"][l], writes=["ckvT"])
                        S.dma("pool", krT[64:96, 0:P], D["c_krT"][l], writes=["krT"])
                    S.dma("sp", x[:], UT[OFF["cq"]:OFF["cq"] + 256, T0:T1].rearrange("(kc p) t -> p kc t", p=128), writes=["m_x"])
                    S.dma("sp", xk[:], UT[OFF["ckv"]:OFF["ckv"] + 128, T0:T1], writes=["m_xk"])
                    S.dma("sp", kr32[64:96, :], UT[OFF["kr"]:OFF["kr"] + 32, T0:T1], writes=["m_kr"])
                    if do_rope:
                        for (r0, nr, ap) in swap_rows(OFF["kr"], T0, T1, UT, 32)(True):
                            S.dma("act", krs[64 + r0:64 + r0 + nr, :], ap, writes=["m_krs"])
                    for c in range(L // 512 if L >= 512 else 1):
                        w = min(512, L)
                        cs = slice(c * w, (c + 1) * w)
                        pt, pn = b.ps()
                        for kc in range(2):
                            S.op("act", lambda e: e.activation(sqb[:, kc, 0:w], x[:, kc, cs], AF.Square), reads=["m_x"], writes=["m_sq"])
                        for kc in range(2):
                            S.op("pe", lambda e: e.matmul(pt[:, 0:w], ones_bf[:], sqb[:, kc, 0:w], start=(kc == 0), stop=(kc == 1)),
                                 reads=["m_sq", "ones_bf"], writes=[pn], pe_acc=True)
                        rstd_from_sumsq(pt, pn, rs, "m_rs", 128, w, 1.0 / 256)
                        for kc in range(2):
                            S.op("dve", lambda e: e.scalar_tensor_tensor(cqn[:, kc, cs], x[:, kc, cs], g_q[:, kc:kc + 1], rs[:, 0:w], ALU.mult, ALU.mult),
                                 reads=["m_x", "m_rs", "g_q"], writes=["cqn"])
                        pt, pn = b.ps()
                        S.op("act", lambda e: e.activation(sqb[:, 0, 0:w], xk[:, cs], AF.Square), reads=["m_xk"], writes=["m_sq"])
                        S.op("pe", lambda e: e.matmul(pt[:, 0:w], ones_bf[:], sqb[:, 0, 0:w], start=True, stop=True), reads=["m_sq", "ones_bf"], writes=[pn])
                        rstd_from_sumsq(pt, pn, rs, "m_rs", 128, w, 1.0 / 128)
                        S.op("dve", lambda e: e.scalar_tensor_tensor(xk[:, cs], xk[:, cs], g_kv[:, 0:1], rs[:, 0:w], ALU.mult, ALU.mult),
                             reads=["m_xk", "m_rs", "g_kv"], writes=["m_xk"])
                        S.op("pool", lambda e: e.tensor_copy(ckvT[:, P + c * w:P + (c + 1) * w], xk[:, cs]), reads=["m_xk"], writes=["ckvT"])
                        if do_rope:
                            S.op("dve", lambda e: e.tensor_tensor(t1[64:96, 0:w], kr32[64:96, cs], rope[64:96, 2, cs], ALU.mult), reads=["m_kr", "rope"], writes=["m_t1"])
                            S.op("pool", lambda e: e.tensor_tensor(krs[64:96, cs], krs[64:96, cs], rope[64:96, 3, cs], ALU.mult), reads=["m_krs", "rope"], writes=["m_krs"])
                            S.op("dve", lambda e: e.tensor_tensor(krT[64:96, P + c * w:P + (c + 1) * w], t1[64:96, 0:w], krs[64:96, cs], ALU.add),
                                 reads=["m_t1", "m_krs"], writes=["krT"])
                        else:
                            S.op("dve", lambda e: e.tensor_copy(krT[64:96, P + c * w:P + (c + 1) * w], kr32[64:96, cs]), reads=["m_kr"], writes=["krT"])
                    if g == "p":
                        S.dma("sp", D["o_ckvT"][l, :, T0:T1], xk[:], reads=["m_xk"], writes=[])
                        S.dma("sp", D["o_krT"][l, :, T0:T1], kr32[64:96, :], reads=["m_kr"], writes=[])
                    S.barrier()
                Vall = b.sb(st, "Vall", [128, NK // 128, 8, 128], BF16)
                S.op("pool", lambda e: e.memset(Vall[:], 0.0), writes=["Vall"])
                for c in range(NK // 128):
                    pt, pn = b.ps()
                    S.op("pe", lambda e: e.matmul(pt[:, :], ckvT[:, c * 128:(c + 1) * 128],
                                                  wukv[:].rearrange("p (h x) -> p h x", x=128)[:, :, 64:128], start=True, stop=True),
                         reads=["ckvT", "wukv"], writes=[pn])
                    pv = pt[:, :].rearrange("p (h two d) -> p h two d", two=2, d=64)
                    S.op("act", lambda e: e.copy(Vall[:, c, 0::2, 0:64], pv[:, :, 0, :]), reads=[pn], writes=["Vall"])
                    S.op("dve", lambda e: e.tensor_copy(Vall[:, c, 1::2, 64:128], pv[:, :, 1, :]), reads=[pn], writes=["Vall"])
                S.barrier()
                with ExitStack() as s2:
                    wk = dict(E=[b.sb(s2, "E%d" % i, [128, 512], BF16) for i in range(6)], rec=b.sb(s2, "rec", [128, 512]))
                    kTh = [b.sb(s2, "kTh%d" % i, [128, NK], BF16) for i in range(4)]
                    qh = [b.sb(s2, "qh%d" % i, [128, L], BF16) for i in range(4)]
                    obs = [b.sb(s2, "ob%d" % i, [128, 512], BF16) for i in range(2)]
                    t2 = b.sb(s2, "m_t2", [96, 512]); t3 = b.sb(s2, "m_t3", [96, 512])
                    for i in range(4):
                        S.op("pool", lambda e: e.memset(kTh[i][96:128, :], 0.0), writes=["kTh%d" % i])
                        S.op("pool", lambda e: e.memset(qh[i][96:128, :], 0.0), writes=["qh%d" % i])
                    nu = 0
                    W = min(512, L)
                    for hp in range(4):
                        bufs = []
                        for h2 in range(2):
                            h = hp * 2 + h2
                            bi_ = (hp % 2) * 2 + h2
                            kt, ktn = kTh[bi_], "kTh%d" % bi_
                            q_, q_n = qh[bi_], "qh%d" % bi_
                            bufs.append((kt, ktn, q_, q_n))
                            for c in range((NK + 511) // 512):
                                w = min(512, NK - c * 512)
                                pt, pn = b.ps()
                                S.op("pe", lambda e: e.matmul(pt[:, 0:w], wukv[:, h * 128:(h + 1) * 128], ckvT[:, c * 512:c * 512 + w], start=True, stop=True),
                                     reads=["wukv", "ckvT"], writes=[pn])
                                S.op("act", lambda e: e.copy(kt[0:64, c * 512:c * 512 + w], pt[0:64, 0:w]), reads=[pn], writes=[ktn])
                            S.op("pool", lambda e: e.tensor_copy(kt[64:96, :], krT[64:96, :]), reads=["krT"], writes=[ktn])
                            for c in range(L // W):
                                cs = slice(c * W, (c + 1) * W)
                                pt, pn = b.ps()
                                for kc in range(2):
                                    S.op("pe", lambda e: e.matmul(pt[0:96, 0:W], wuq[:, kc, h * 96:(h + 1) * 96], cqn[:, kc, cs], start=(kc == 0), stop=(kc == 1)),
                                         reads=["wuq", "cqn"], writes=[pn], pe_acc=(kc > 0))
                                S.op("act", lambda e: e.copy(q_[0:64, cs], pt[0:64, 0:W]), reads=[pn], writes=[q_n])
                                if do_rope:
                                    pt2, pn2 = b.ps()
                                    for kc in range(2):
                                        S.op("pe", lambda e: e.matmul(pt2[0:96, 0:W], wuqs[:, kc, h * 96:(h + 1) * 96], cqn[:, kc, cs], start=(kc == 0), stop=(kc == 1)),
                                             reads=["wuqs", "cqn"], writes=[pn2], pe_acc=(kc > 0))
                                    S.op("dve", lambda e: e.tensor_tensor(t2[64:96, 0:W], pt[64:96, 0:W], rope[64:96, 2, cs], ALU.mult), reads=[pn, "rope"], writes=["m_t2"])
                                    S.op("dve", lambda e: e.tensor_tensor(t3[64:96, 0:W], pt2[64:96, 0:W], rope[64:96, 3, cs], ALU.mult), reads=[pn2, "rope"], writes=["m_t3"])
                                    S.op("pool", lambda e: e.tensor_tensor(q_[64:96, cs], t2[64:96, 0:W], t3[64:96, 0:W], ALU.add), reads=["m_t2", "m_t3"], writes=[q_n])
                                else:
                                    S.op("dve", lambda e: e.tensor_copy(q_[64:96, cs], pt[64:96, 0:W]), reads=[pn], writes=[q_n])
                        (kta, ktan, qa, qan), (ktb, ktbn, qb_, qbn) = bufs
                        for c in range(L // W):
                            chunks = [(kta[:, kc * 128:(kc + 1) * 128], ktb[:, kc * 128:(kc + 1) * 128], (ktan, ktbn), Vall[:, kc, hp * 2, :], Vall[:, kc, hp * 2 + 1, :], "Vall", 128, None)
                                      for kc in range(NK // 128)]
                            ob, obn = obs[nu % 2], "ob%d" % (nu % 2)
                            nu += 1
                            attn_pair(qa[:, c * W:(c + 1) * W], [qan], qb_[:, c * W:(c + 1) * W], [qbn], W, chunks, scale, ob[:, 0:W], obn, wk)
                            S.dma("sp", OT[2, hp * 128:(hp + 1) * 128, T0 + c * W:T0 + (c + 1) * W], ob[:, 0:W], reads=[obn], writes=[])
                    S.barrier()

    def stage_hgrn(g, l):
        G = GR[g]
        T, L, P, nseq = G["T"], G["L"], G["P"], G["nseq"]
        UT, UTM, OT = D["UT_" + g], D["UTM_" + g], D["OT_" + g]
        NTL = L // 128
        ident, triU, triL, sL, sU, csel = (cm[:, i, :] for i in range(6))
        with ExitStack() as st:
            lbb = b.sb(st, "lbb", [128, 2, 512]); oml = b.sb(st, "oml", [128, 2, 512])
            with ExitStack() as s2:
                lbr = b.sb(s2, "lbr", [128, DEPTH, 2, 512]); den = b.sb(s2, "lden", [128, 2, 512])
                S.dma("sp", lbr[:], D["lbT"].partition_broadcast(128), writes=["lbr"])
                S.op("act", lambda e: e.activation(lbr[:], lbr[:], AF.Exp), reads=["lbr"], writes=["lbr"])
                S.op("dve", lambda e: e.tensor_tensor(den[:], lbr[:, 0], lbr[:, 1], ALU.add), reads=["lbr"], writes=["lden"])
                S.op("dve", lambda e: e.reciprocal(den[:], den[:]), reads=["lden"], writes=["lden"])
                if l == 0:
                    S.op("pool", lambda e: e.memset(lbb[:], 0.0), writes=["lbb"])
                else:
                    S.op("dve", lambda e: e.tensor_tensor(lbb[:], lbr[:, 1], den[:], ALU.mult), reads=["lbr", "lden"], writes=["lbb"])
                S.op("dve", lambda e: e.tensor_scalar(oml[:], lbb[:], -1.0, 1.0, ALU.mult, ALU.add), reads=["lbb"], writes=["oml"])
                S.barrier()
            hn = b.sb(st, "hn", [128, 4]); S.dma("sp", hn[:], D["hgrn_normT"][l], writes=["hn"])
            Sst = [b.sb(st, "Sst%d" % i, [128, 2, 4, 128]) for i in range(2)]
            oall = b.sb(st, "oall", [128, 2, 4, L], BF16)
            W_ = {}
            for d in range(2):
                for k in range(2):
                    sfx = "%d%d" % (d, k)
                    W_[d, k] = dict(
                        sfx=sfx,
                        tt=b.sb(st, "h_t" + sfx, [128, 512]), vv=b.sb(st, "h_v" + sfx, [128, 512]), gg=b.sb(st, "h_g" + sfx, [128, 512]),
                        kt=b.sb(st, "h_kt" + sfx, [128, 512]), kh=b.sb(st, "h_kh" + sfx, [128, 512]),
                        khm=b.sb(st, "h_khm" + sfx, [128, 4, 4, 128]), qT=b.sb(st, "h_qT" + sfx, [128, 4, 128]), qt=b.sb(st, "h_qt" + sfx, [128, 4, 128]),
                        eb=b.sb(st, "h_eb" + sfx, [128, 4, 128]), ktT=b.sb(st, "h_ktT" + sfx, [128, 4, 128]), AT=b.sb(st, "h_AT" + sfx, [128, 4, 128]),
                        tmpo=b.sb(st, "h_tmpo" + sfx, [128, 512]))
            etot = [b.sb(st, "h_etot%d" % k, [128, 2, 4, 4]) for k in range(2)]
            fin = dict(os=b.sb(st, "f_os", [128, 4, 256]), gT=b.sb(st, "f_gT", [128, 4, 256]), sq=b.sb(st, "f_sq", [128, 4, 256], BF16),
                       rs=b.sb(st, "f_rs", [128, 4, 256]), ob=b.sb(st, "f_ob", [128, 4, 256], BF16))
            b.nrot = 4
            M4 = lambda M: M.unsqueeze(1).to_broadcast([128, 4, 128])

            def sstn(sp, d, h):
                return "Sst%d_%d%d" % (sp, d, h)

            def partA(rec):
                k = rec["k"]
                for d in range(2):
                    w = W_[d, k]; x = w["sfx"]
                    t0 = rec["T0"] + rec["ti"][d] * 128
                    S.dma("sp", w["tt"][:], UTM[t0:t0 + 128, d * 512:(d + 1) * 512], writes=["h_t" + x])
                    S.dma("sp", w["vv"][:], UTM[t0:t0 + 128, 1024:1536], writes=["h_v" + x])
                    S.dma("act", w["qT"][:], UT[0:512, t0:t0 + 128].rearrange("(h p) t -> p h t", p=128), writes=["h_qT" + x])
                for d in range(2):
                    w = W_[d, k]; x = w["sfx"]
                    S.op("act", lambda e: e.activation(w["tt"][:], w["tt"][:], AF.Sigmoid), reads=["h_t" + x], writes=["h_t" + x])
                for d in range(2):
                    w = W_[d, k]; x = w["sfx"]
                    S.op("act", lambda e: e.activation(w["qT"][:], w["qT"][:], AF.Silu), reads=["h_qT" + x], writes=["h_qT" + x])
                for d in range(2):
                    w = W_[d, k]; x = w["sfx"]
                    S.op("dve", lambda e: e.tensor_tensor(w["tt"][:], w["tt"][:], oml[:, d], ALU.mult), reads=["h_t" + x, "oml"], writes=["h_t" + x])
                    S.op("dve", lambda e: e.scalar_tensor_tensor(w["gg"][:], w["tt"][:], 1e-30, lbb[:, d], ALU.max, ALU.add), reads=["h_t" + x, "lbb"], writes=["h_g" + x])
                for d in range(2):
                    w = W_[d, k]; x = w["sfx"]
                    S.op("act", lambda e: e.activation(w["gg"][:], w["gg"][:], AF.Ln), reads=["h_g" + x], writes=["h_g" + x])
                for d in range(2):
                    w = W_[d, k]; x = w["sfx"]
                    S.op("dve", lambda e: e.tensor_tensor(w["tt"][:], oml[:, d], w["tt"][:], ALU.subtract), reads=["h_t" + x, "oml"], writes=["h_t" + x])

            def partB(rec, stage):
                k = rec["k"]
                Ms = [(triU, sL), (triL, sU)]
                if stage == 0:
                    pbs = []
                    for d in range(2):
                        w = W_[d, k]; x = w["sfx"]
                        pb, pbn = b.ps()
                        S.op("pe", lambda e: e.matmul(pb[:], Ms[d][0], w["gg"][:], start=True, stop=True), reads=["cm", "h_g" + x], writes=[pbn])
                        pr, prn = b.ps()
                        S.op("pe", lambda e: e.matmul(pr[:], Ms[d][1], w["gg"][:], start=True, stop=True), reads=["cm", "h_g" + x], writes=[prn])
                        pbs.append((pb, pbn, pr, prn))
                    for d in range(2):
                        w = W_[d, k]; x = w["sfx"]
                        pb, pbn, pr, prn = pbs[d]
                        S.op("act", lambda e: e.activation(w["kt"][:], pb[:], AF.Exp, scale=-1.0), reads=[pbn], writes=["h_kt" + x])
                        S.op("act", lambda e: e.activation(w["kh"][:], pr[:], AF.Exp), reads=[prn], writes=["h_kh" + x])
                    pts = []
                    ptot, ptotn = b.ps()
                    for d in range(2):
                        w = W_[d, k]; x = w["sfx"]
                        pbt, pbtn = b.ps()
                        for h in range(4):
                            hs = slice(h * 128, (h + 1) * 128)
                            S.op("pe", lambda e: e.matmul(pbt[:, hs], w["gg"][:, hs], Ms[d][0], start=True, stop=True), reads=["h_g" + x, "cm"], writes=[pbtn], pe_acc=(h > 0))
                            S.op("pe", lambda e: e.matmul(ptot[:, d * 16 + h * 4:d * 16 + h * 4 + 4], w["gg"][:, hs], csel[:, 0:4], start=True, stop=True),
                                 reads=["h_g" + x, "cm"], writes=[ptotn], pe_acc=(d > 0 or h > 0))
                        pts.append((pbt, pbtn))
                    for d in range(2):
                        w = W_[d, k]; x = w["sfx"]
                        S.op("dve", lambda e: e.tensor_tensor(w["kt"][:], w["kt"][:], w["tt"][:], ALU.mult), reads=["h_kt" + x, "h_t" + x], writes=["h_kt" + x])
                        S.op("dve", lambda e: e.tensor_tensor(w["kh"][:], w["kh"][:], w["tt"][:], ALU.mult), reads=["h_kh" + x, "h_t" + x], writes=["h_kh" + x])
                    for d in range(2):
                        w = W_[d, k]; x = w["sfx"]
                        pbt, pbtn = pts[d]
                        S.op("act", lambda e: e.activation(w["eb"][:], pbt[:].rearrange("p (h t) -> p h t", h=4), AF.Exp), reads=[pbtn], writes=["h_eb" + x])
                    S.op("act", lambda e: e.activation(etot[k][:], ptot[:, 0:32].rearrange("p (d h j) -> p d h j", d=2, h=4), AF.Exp), reads=[ptotn], writes=["h_etot%d" % k])
                    for d in range(2):
                        w = W_[d, k]; x = w["sfx"]
                        S.op("dve", lambda e: e.tensor_tensor(w["qt"][:], w["qT"][:], w["eb"][:], ALU.mult), reads=["h_qT" + x, "h_eb" + x], writes=["h_qt" + x])
                    for d in range(2):
                        w = W_[d, k]; x = w["sfx"]
                        for j in range(4):
                            S.op("act", lambda e: e.activation(w["khm"][:, :, j, :], w["kh"][:].rearrange("p (h c) -> p h c", h=4), AF.Copy, scale=csel[:, j:j + 1]),
                                 reads=["h_kh" + x, "cm"], writes=["h_khm" + x])
                elif stage == 1:
                    for d in range(2):
                        w = W_[d, k]; x = w["sfx"]
                        pk, pkn = b.ps()
                        for h in range(4):
                            hs = slice(h * 128, (h + 1) * 128)
                            S.op("pe", lambda e: e.matmul(pk[:, hs], w["kt"][:, hs], ident, start=True, stop=True), reads=["h_kt" + x, "cm"], writes=[pkn], pe_acc=(h > 0))
                        S.op("act", lambda e: e.copy(w["ktT"][:], pk[:].rearrange("p (h t) -> p h t", h=4)), reads=[pkn], writes=["h_ktT" + x])
                elif stage == 2:
                    for d in range(2):
                        w = W_[d, k]; x = w["sfx"]
                        pa, pan = b.ps()
                        for h in range(4):
                            hs = slice(h * 128, (h + 1) * 128)
                            S.op("pe", lambda e: e.matmul(pa[:, hs], w["ktT"][:, h, :], w["qt"][:, h, :], start=True, stop=True), reads=["h_ktT" + x, "h_qt" + x], writes=[pan], pe_acc=(h > 0))
                        S.op("dve", lambda e: e.tensor_tensor(w["AT"][:], pa[:].rearrange("p (h t) -> p h t", h=4), M4(Ms[d][0]), ALU.mult), reads=[pan, "cm"], writes=["h_AT" + x])
                else:
                    for d in range(2):
                        w = W_[d, k]; x = w["sfx"]
                        po_, pon_ = b.acc(d)
                        for h in range(4):
                            hs = slice(h * 128, (h + 1) * 128)
                            S.op("pe", lambda e: e.matmul(po_[:, hs], w["vv"][:, hs], w["AT"][:, h, :], start=True, stop=True), reads=["h_v" + x, "h_AT" + x], writes=[pon_], pe_acc=(h > 0))
                        S.op("act", lambda e: e.copy(w["tmpo"][:], po_[:]), reads=[pon_], writes=["h_tmpo" + x])

            def chunk_step(rec, n_):
                k = rec["k"]; sp = rec["sp"]
                pss = []
                for d in range(2):
                    w = W_[d, k]; x = w["sfx"]
                    j = n_ if d == 0 else 3 - n_
                    js = slice(j * 32, (j + 1) * 32)
                    pi_, pin_ = b.acc(2 + d)
                    ps_, psn2 = b.ps()
                    pss.append((ps_, psn2, j))
                    for h in range(4):
                        hs = slice(h * 128, (h + 1) * 128)
                        S.op("pe", lambda e: e.matmul(pi_[:, h * 128 + j * 32:h * 128 + (j + 1) * 32], Sst[sp][:, d, h, :], w["qt"][:, h, js], start=True, stop=True),
                             reads=[sstn(sp, d, h), "h_qt" + x], writes=[pin_], pe_acc=(n_ > 0 or h > 0))
                        S.op("pe", lambda e: e.matmul(ps_[:, hs], w["khm"][:, h, j, :], w["vv"][:, hs], start=True, stop=True),
                             reads=["h_khm" + x, "h_v" + x], writes=[psn2], pe_acc=(h > 0))
                for d in range(2):
                    w = W_[d, k]; x = w["sfx"]
                    ps_, psn2, j = pss[d]
                    for h in range(4):
                        hs = slice(h * 128, (h + 1) * 128)
                        S.op("dve", lambda e: e.scalar_tensor_tensor(Sst[sp][:, d, h, :], Sst[sp][:, d, h, :], etot[k][:, d, h, j:j + 1], ps_[:, hs], ALU.mult, ALU.add),
                             reads=[psn2, "h_etot%d" % k, sstn(sp, d, h)], writes=[sstn(sp, d, h)])

            def finish(rec):
                k = rec["k"]
                for d in range(2):
                    w = W_[d, k]; x = w["sfx"]
                    pi_, pin_ = b.acc(2 + d)
                    tl = rec["ti"][d] * 128
                    S.op("dve", lambda e: e.tensor_tensor(oall[:, d, :, tl:tl + 128], w["tmpo"][:].rearrange("p (h t) -> p h t", h=4),
                                                          pi_[:].rearrange("p (h t) -> p h t", h=4), ALU.add),
                         reads=[pin_, "h_tmpo" + x], writes=["oall"])

            def seq_final(sq_):
                T0 = sq_ * L
                sp = sq_ % 2
                for c in range(L // 256):
                    cs = slice(c * 256, (c + 1) * 256)
                    os_, gT, sq, rs, ob = fin["os"], fin["gT"], fin["sq"], fin["rs"], fin["ob"]
                    S.dma("act", gT[:], UT[OFF["hg"]:OFF["hg"] + 512, T0 + c * 256:T0 + (c + 1) * 256].rearrange("(h p) t -> p h t", p=128), writes=["f_gT"])
                    S.op("act", lambda e: e.activation(gT[:], gT[:], AF.Silu), reads=["f_gT"], writes=["f_gT"])
                    S.op("dve", lambda e: e.tensor_tensor(os_[:], oall[:, 0, :, cs], oall[:, 1, :, cs], ALU.add), reads=["oall"], writes=["f_os"])
                    S.op("act", lambda e: e.activation(sq[:], os_[:], AF.Square), reads=["f_os"], writes=["f_sq"])
                    for hh in range(2):
                        pn_, pnn = b.ps()
                        for h2 in range(2):
                            h = hh * 2 + h2
                            S.op("pe", lambda e: e.matmul(pn_[:, h2 * 256:(h2 + 1) * 256], ones_bf[:], sq[:, h, :], start=True, stop=True),
                                 reads=["f_sq", "ones_bf"], writes=[pnn], pe_acc=(h2 > 0))
                        rstd_from_sumsq(pn_, pnn, rs[:, hh * 2:hh * 2 + 2, :].rearrange("p a t -> p (a t)"), "f_rs", 128, 512, 1.0 / 128)
                    for h in range(4):
                        S.op("dve", lambda e: e.scalar_tensor_tensor(os_[:, h, :], os_[:, h, :], hn[:, h:h + 1], rs[:, h, :], ALU.mult, ALU.mult),
                             reads=["f_os", "hn", "f_rs"], writes=["f_os"])
                    S.op("dve", lambda e: e.tensor_tensor(ob[:], os_[:], gT[:], ALU.mult), reads=["f_os", "f_gT"], writes=["f_ob"])
                    S.dma("sp", OT[0, :, T0 + c * 256:T0 + (c + 1) * 256].rearrange("(h p) t -> p h t", p=128), ob[:], reads=["f_ob"], writes=[])
                if g == "p":
                    S.dma("sp", D["o_hgrn"][l, sq_].rearrange("d h k v -> k d h v"), Sst[sp][:], reads=[sstn(sp, d_, h_) for d_ in range(2) for h_ in range(4)], writes=[])

            recs = []
            for sq_ in range(nseq):
                for i in range(NTL):
                    recs.append(dict(sq=sq_, sp=sq_ % 2, T0=sq_ * L, i=i, ti=(i, NTL - 1 - i), k=len(recs) % 2))
            partA(recs[0])
            for stg in range(4):
                partB(recs[0], stg)
            for n, rec in enumerate(recs):
                nxt = recs[n + 1] if n + 1 < len(recs) else None
                if rec["i"] == 0:
                    sp = rec["sp"]
                    names = [sstn(sp, d_, h_) for d_ in range(2) for h_ in range(4)]
                    if P:
                        S.dma("sp", Sst[sp][:], D["st_hgrn"][l].rearrange("d h k v -> k d h v"), writes=names)
                    else:
                        S.op("pool", lambda e: e.memset(Sst[sp][:], 0.0), writes=names)
                if nxt is not None:
                    partA(nxt)
                for n_ in range(4):
                    chunk_step(rec, n_)
                    if nxt is not None:
                        partB(nxt, n_)
                finish(rec)
                if rec["i"] == NTL - 1:
                    seq_final(rec["sq"])
            b.nrot = 6
            S.barrier()

    def epilogue(st, z, zn, xt, xn, gco, Xdst, c0, wkn):
        sq, rs, tmp = wkn
        for kc in range(8):
            S.op("act", lambda e: e.activation(sq[:, kc, :], z[:, kc, :], AF.Square), reads=(zn if isinstance(zn, list) else [zn]), writes=["e_sq"])
        pt, pn = b.ps()
        for kc in range(8):
            S.op("pe", lambda e: e.matmul(pt[:], ones_bf[:], sq[:, kc, :], start=(kc == 0), stop=(kc == 7)), reads=["e_sq", "ones_bf"], writes=[pn], pe_acc=True)
        rstd_from_sumsq(pt, pn, rs, "e_rs", 128, 512, 1.0 / DM)
        for kc in range(8):
            S.op("dve", lambda e: e.tensor_tensor(tmp[:], z[:, kc, :], rs[:], ALU.mult), reads=(zn if isinstance(zn, list) else [zn]) + ["e_rs"], writes=["e_tmp"])
            S.op("dve", lambda e: e.scalar_tensor_tensor(xt[:, kc, :], tmp[:], gco[:, kc:kc + 1], xt[:, kc, :], ALU.mult, ALU.add),
                 reads=["e_tmp", "coef", xn], writes=[xn])
        S.dma("sp", Xdst[:, c0:c0 + 512].rearrange("(kc p) t -> p kc t", p=128), xt[:], reads=[xn], writes=[])

    def stage_merge(g, l, Xsrc, Xdst):
        G = GR[g]
        T = G["T"]
        UT, OT = D["UT_" + g], D["OT_" + g]
        with ExitStack() as st:
            Wb = b.sb(st, "Wb", [128, 4, 4, 1024], BF16)
            Wo = b.sb(st, "Wo", [128, 8, 1024], BF16)
            S.dma("pool", Wb[:], D["w_branch"][l].rearrange("n (kc p) c -> p n kc c", p=128), writes=["Wb"])
            S.dma("pool", Wo[:], D["w_out"][l].rearrange("(kc p) c -> p kc c", p=128), writes=["Wo"])
            ot = b.sb(st, "ot", [128, 4, 4, 512], BF16)
            yp = b.sb(st, "yp", [128, 8, 512], BF16)
            gts = [b.sb(st, "gt%d" % i, [128, 512]) for i in range(3)]
            accf = b.sb(st, "accf", [128, 512]); tm2 = b.sb(st, "tm2", [128, 512])
            z = b.sb(st, "z", [128, 8, 512]); xt = b.sb(st, "xt", [128, 8, 512])
            wkn = (b.sb(st, "e_sq", [128, 8, 512], BF16), b.sb(st, "e_rs", [128, 512]), b.sb(st, "e_tmp", [128, 512]))
            ng = 0
            for tt in range(T // 512):
                cs = slice(tt * 512, (tt + 1) * 512)
                S.dma("sp", ot[:], OT[:, :, cs].rearrange("n (kc p) t -> p n kc t", p=128), writes=["ot"])
                S.dma("act", xt[:], Xsrc[:, cs].rearrange("(kc p) t -> p kc t", p=128), writes=["xt"])
                for dmc in range(8):
                    for n in range(4):
                        gt, gtn = gts[ng % 3], "gt%d" % (ng % 3)
                        ng += 1
                        r0 = OFF["gates"] + n * 1024 + dmc * 128
                        S.dma("sp", gt[:], UT[r0:r0 + 128, cs], writes=[gtn])
                        S.op("act", lambda e: e.activation(gt[:], gt[:], AF.Sigmoid), reads=[gtn], writes=[gtn])
                        pt, pn = b.ps()
                        for kc in range(4):
                            S.op("pe", lambda e: e.matmul(pt[:], Wb[:, n, kc, dmc * 128:(dmc + 1) * 128], ot[:, n, kc, :], start=(kc == 0), stop=(kc == 3)),
                                 reads=["Wb", "ot"], writes=[pn], pe_acc=True)
                        if n == 0:
                            S.op("dve", lambda e: e.tensor_tensor(accf[:], pt[:], gt[:], ALU.mult), reads=[pn, gtn], writes=["accf"])
                        elif n < 3:
                            S.op("dve", lambda e: e.tensor_tensor(tm2[:], pt[:], gt[:], ALU.mult), reads=[pn, gtn], writes=["tm2"])
                            S.op("dve", lambda e: e.tensor_tensor(accf[:], accf[:], tm2[:], ALU.add), reads=["tm2", "accf"], writes=["accf"])
                        else:
                            S.op("dve", lambda e: e.tensor_tensor(tm2[:], pt[:], gt[:], ALU.mult), reads=[pn, gtn], writes=["tm2"])
                            S.op("dve", lambda e: e.tensor_tensor(yp[:, dmc, :], accf[:], tm2[:], ALU.add), reads=["tm2", "accf"], writes=["yp"])
                for oc in range(8):
                    pt, pn = b.ps()
                    for kc in range(8):
                        S.op("pe", lambda e: e.matmul(pt[:], Wo[:, kc, oc * 128:(oc + 1) * 128], yp[:, kc, :], start=(kc == 0), stop=(kc == 7)),
                             reads=["Wo", "yp"], writes=[pn], pe_acc=True)
                    S.op("act", lambda e: e.copy(z[:, oc, :], pt[:]), reads=[pn], writes=["z"])
                epilogue(st, z, "z", xt, "xt", coef[:, l, G["cond"], 2], Xdst, tt * 512, wkn)
            S.barrier()

    def stage_mlp(g, l, Xsrc, Xdst):
        G = GR[g]
        T = G["T"]
        NT = T // 512
        with ExitStack() as st:
            h2 = stage_h(st, g, l, Xsrc, 3, 4)
            zall = b.sb(st, "zall", [128, 8, T])
            w1 = [b.sb(st, "w1_%d" % i, [128, 8, 512], BF16) for i in range(2)]
            w2 = [b.sb(st, "w2_%d" % i, [128, 4, 1024], BF16) for i in range(2)]
            hid = [b.sb(st, "hid%d" % i, [128, 4, 512], BF16) for i in range(2)]
            rl = [b.sb(st, "rl%d" % i, [128, 512]) for i in range(2)]
            xt = b.sb(st, "xt", [128, 8, 512])
            wkn = (b.sb(st, "e_sq", [128, 8, 512], BF16), b.sb(st, "e_rs", [128, 512]), b.sb(st, "e_tmp", [128, 512]))
            nh = 0
            nr = 0
            for cg in range(8):
                wa, wan = w1[cg % 2], "w1_%d" % (cg % 2)
                wb_, wbn = w2[cg % 2], "w2_%d" % (cg % 2)
                S.dma("pool", wa[:], D["w_mlp_in"][l, :, cg * 512:(cg + 1) * 512].rearrange("(kc p) c -> p kc c", p=128), writes=[wan])
                S.dma("pool", wb_[:], D["w_mlp_out"][l, cg * 512:(cg + 1) * 512, :].rearrange("(fc p) c -> p fc c", p=128), writes=[wbn])
                for tt in range(NT):
                    cs = slice(tt * 512, (tt + 1) * 512)
                    hd, hdn = hid[nh % 2], "hid%d" % (nh % 2)
                    nh += 1
                    for cc in range(4):
                        pt, pn = b.ps()
                        for kc in range(8):
                            S.op("pe", lambda e: e.matmul(pt[:], wa[:, kc, cc * 128:(cc + 1) * 128], h2[:, kc, cs], start=(kc == 0), stop=(kc == 7)),
                                 reads=[wan, "hT"], writes=[pn], pe_acc=(kc > 0))
                        r, rn = rl[nr % 2], "rl%d" % (nr % 2)
                        nr += 1
                        S.op("act", lambda e: e.activation(r[:], pt[:], AF.Relu), reads=[pn], writes=[rn])
                        S.op("dve", lambda e: e.tensor_tensor(hd[:, cc, :], r[:], r[:], ALU.mult), reads=[rn], writes=[hdn])
                    for oc in range(8):
                        pt, pn = b.ps()
                        for fc in range(4):
                            S.op("pe", lambda e: e.matmul(pt[:], wb_[:, fc, oc * 128:(oc + 1) * 128], hd[:, fc, :], start=(fc == 0), stop=(fc == 3)),
                                 reads=[wbn, hdn], writes=[pn], pe_acc=(fc > 0))
                        zn = "z%d_%d" % (tt, oc)
                        if cg == 0:
                            S.op("act", lambda e: e.copy(zall[:, oc, cs], pt[:]), reads=[pn], writes=[zn])
                        else:
                            S.op("dve", lambda e: e.tensor_tensor(zall[:, oc, cs], zall[:, oc, cs], pt[:], ALU.add), reads=[pn, zn], writes=[zn])
            for tt in range(NT):
                cs = slice(tt * 512, (tt + 1) * 512)
                S.dma("act", xt[:], Xsrc[:, cs].rearrange("(kc p) t -> p kc t", p=128), writes=["xt"])
                epilogue(st, zall[:, :, cs], ["z%d_%d" % (tt, oc) for oc in range(8)], xt, "xt", coef[:, l, G["cond"], 5], Xdst, tt * 512, wkn)
            S.barrier()

    for g in ("s", "p"):
        X = D["xT_" + g]
        for l in range(DEPTH):
            if "proj" in STAGES:
                with ExitStack() as st:
                    hT = stage_h(st, g, l, X, 0, 1)
                    stage_proj(g, l, hT)
            if "hgrn" in STAGES:
                stage_hgrn(g, l)
            if "swa" in STAGES:
                stage_gqa_like(g, l, "swa")
            if "mla" in STAGES:
                stage_mla(g, l)
            if "gqa" in STAGES:
                stage_gqa_like(g, l, "gqa")
            if "merge" in STAGES:
                stage_merge(g, l, X, D["X1_" + g])
            Xn = D["yT_" + g] if l == DEPTH - 1 else D["X2_%d_%s" % (l, g)]
            if "mlp" in STAGES:
                stage_mlp(g, l, D["X1_" + g], Xn)
            X = Xn
    S.barrier()
    return nc, b


_CACHE = {}


def _consts():
    nf = 16
    t = np.arange(2048)
    row, col = (t // 64).astype(np.float32), (t % 64).astype(np.float32)
    rope = np.zeros((4, 128, 2048), np.float32)
    def fill(ci, si, r0, nf):
        inv = (10000.0 ** (-np.arange(nf, dtype=np.float32) / nf)).astype(np.float32)
        ar = (row[None, :] * inv[:, None]).astype(np.float32)
        ac = (col[None, :] * inv[:, None]).astype(np.float32)
        for k, a in enumerate((ar, ac)):
            b0 = r0 + k * 2 * nf
            rope[ci, b0:b0 + nf] = np.cos(a); rope[ci, b0 + nf:b0 + 2 * nf] = np.cos(a)
            rope[si, b0:b0 + nf] = -np.sin(a); rope[si, b0 + nf:b0 + 2 * nf] = np.sin(a)
    fill(0, 1, 0, 16)
    fill(0, 1, 64, 16)
    fill(2, 3, 64, 8)
    s_ = np.arange(128)[:, None]; t_ = np.arange(128)[None, :]
    same = (s_ // 32) == (t_ // 32)
    cm = np.zeros((6, 128, 128), np.float32)
    cm[0] = np.eye(128)
    cm[1] = same & (s_ <= t_)
    cm[2] = same & (s_ >= t_)
    cm[3] = same & (s_ > t_)
    cm[4] = same & (s_ < t_)
    cm[5][:, 0:4] = (s_ // 32) == np.arange(4)[None, :]
    j = np.arange(128)[:, None]; i = (np.arange(512) % 128)[None, :]
    swm = np.stack([(j >= i), (j <= i)]).astype(np.float32).astype(ml_dtypes.bfloat16)
    return rope, cm, swm


def _perm(nd):
    q = nd // 4
    return np.concatenate([np.arange(q, 2 * q), np.arange(0, q), np.arange(3 * q, 4 * q), np.arange(2 * q, 3 * q)])


def kernel(**inp):
    f = lambda a: np.ascontiguousarray(np.asarray(a, dtype=np.float32))
    I = {k: f(v) for k, v in inp.items()}
    if "prog" not in _CACHE:
        _CACHE["prog"] = build_program()
    nc, b = _CACHE["prog"]
    rope, cm, swm = _consts()
    fm = lambda v, n: f(v.reshape(v.shape[0], n, 128).transpose(0, 2, 1))
    shared = dict(
        w_ada=I["w_ada"], b_adaT=fm(I["b_ada"], 48),
        gains=f(np.stack([fm(I[k], 8) for k in ("norm_mix_pre", "norm_mix_post", "norm_mlp_pre", "norm_mlp_post")], axis=2)),
        w_in=I["w_in"], lbT=f(np.stack([I["hgrn_lb_fwd"], I["hgrn_lb_bwd"]], axis=1)),
        hgrn_normT=fm(I["hgrn_norm"], 4), sink=f(I["swa_sink"][:, None, :]),
        mla_qnT=fm(I["mla_q_norm"], 2), mla_kvn=f(I["mla_kv_norm"][:, :, None]),
        w_uq=I["mla_w_uq"], w_ukv=I["mla_w_ukv"],
        gqa_qn=f(np.tile(np.stack([I["gqa_q_norm"], I["gqa_q_norm"][:, _perm(64)]], axis=2), (1, 2, 1))),
        gqa_kn=f(np.tile(np.stack([I["gqa_k_norm"], I["gqa_k_norm"][:, _perm(64)]], axis=2), (1, 2, 1))),
        w_branch=I["w_branch"], w_out=I["w_out"], w_mlp_in=I["w_mlp_in"], w_mlp_out=I["w_mlp_out"],
        rope64=rope, cmat=cm, swamask=swm,
    )
    wsw = I["mla_w_uq"].reshape(DEPTH, 256, 8, 96).copy()
    wsw[..., 64:96] = wsw[..., 64:96][..., _perm(32)]
    shared["w_uq_sw"] = f(wsw.reshape(DEPTH, 256, 768))
    in_maps = []
    for i in range(NCORE):
        bb = i % 2
        m = dict(shared)
        m["xT_s"] = f(I["x_sample"][bb].T)
        m["xT_p"] = f(I["x_prompt"][4 * i:4 * i + 4].reshape(1024, DM).T)
        cc = np.stack([I["c_ctx"], I["c"][bb]], axis=1)
        m["cT"] = f(cc.reshape(8, 128, 2).transpose(1, 0, 2))
        m["st_hgrn"] = f(I["state_hgrn"][bb])
        m["c_swa_kT"] = f(I["cache_swa_k"][bb].transpose(0, 2, 3, 1))
        m["c_swa_v"] = f(I["cache_swa_v"][bb].reshape(DEPTH, 512, 128))
        m["c_ckvT"] = f(I["cache_mla_ckv"][bb].transpose(0, 2, 1))
        m["c_krT"] = f(I["cache_mla_kr"][bb].transpose(0, 2, 1))
        m["c_gqa_kT"] = f(I["cache_gqa_k"][bb].transpose(0, 2, 3, 1))
        m["c_gqa_v"] = f(I["cache_gqa_v"][bb].reshape(DEPTH, 512, 128))
        in_maps.append(m)
    res = run_bass_kernel_spmd(nc, in_maps, core_ids=list(range(NCORE)))
    R = res.results
    y_prompt = np.concatenate([R[i]["yT_p"].T.reshape(4, 256, DM) for i in range(NCORE)], axis=0)
    y_sample = np.stack([R[0]["yT_s"].T, R[1]["yT_s"].T], axis=0)
    cat = lambda fn: np.ascontiguousarray(np.concatenate([fn(R[i]) for i in range(NCORE)], axis=0).astype(np.float32))
    n_hgrn = cat(lambda r: r["o_hgrn"].transpose(1, 0, 2, 3, 4, 5))
    kT = lambda a: a.reshape(DEPTH, 2, 64, 4, 256).transpose(3, 0, 4, 1, 2)
    vv = lambda a: a.reshape(DEPTH, 4, 256, 2, 64).transpose(1, 0, 2, 3, 4)
    n_swa_k = cat(lambda r: kT(r["o_swa_kT"]))
    n_swa_v = cat(lambda r: vv(r["o_swa_v"]))
    n_ckv = cat(lambda r: r["o_ckvT"].reshape(DEPTH, 128, 4, 256).transpose(2, 0, 3, 1))
    n_kr = cat(lambda r: r["o_krT"].reshape(DEPTH, 32, 4, 256).transpose(2, 0, 3, 1))
    n_gqa_k = cat(lambda r: kT(r["o_gqa_kT"]))
    n_gqa_v = cat(lambda r: vv(r["o_gqa_v"]))
    return (np.ascontiguousarray(y_prompt.astype(np.float32)), np.ascontiguousarray(y_sample.astype(np.float32)),
            n_hgrn, n_swa_k, n_swa_v, n_ckv, n_kr, n_gqa_k, n_gqa_v)
```

```python
import numpy as np
from contextlib import ExitStack
import ml_dtypes
import concourse.bass as bass
import concourse.mybir as mybir
from concourse.bass_utils import run_bass_kernel_spmd

F32 = mybir.dt.float32
BF16 = mybir.dt.bfloat16
AF = mybir.ActivationFunctionType
ALU = mybir.AluOpType

DM = 1024
DEPTH = 2
NCORE = 8
EPS = 1e-6
OFF = dict(hq=0, ff=512, fb=1024, hi=1536, hg=2048, sq=2560, sk=3072, sv=3200, cq=3328, ckv=3584, kr=3712,
           gq=3744, gk=4256, gv=4384, gates=4512)
D_IN = 8608
UTM_COLS = 1792
STAGES = {"proj", "hgrn", "swa", "mla", "gqa", "merge", "mlp"}


class Sched:
    NDMA = 24

    def __init__(self, nc, es):
        self.nc = nc
        self.eng = {"pe": nc.tensor, "act": nc.scalar, "dve": nc.vector, "pool": nc.gpsimd, "sp": nc.sync}
        self.sem = {k: es.enter_context(nc.semaphore("s_" + k)) for k in ("pe", "act", "dve", "pool")}
        self.cnt = {k: 0 for k in self.sem}
        self.dsem = [es.enter_context(nc.semaphore("d%d" % i)) for i in range(self.NDMA)]
        self.dcnt = [0] * self.NDMA
        self.dnext = 0
        self.seen = {k: {} for k in self.eng}
        self.lastw = {}
        self.reads = {}
        self.n_instr = 0

    def _sem_of(self, key):
        return self.sem[key] if isinstance(key, str) else self.dsem[key[1]]

    def _wait(self, e, tok):
        key, val = tok
        if self.seen[e].get(key, 0) >= val:
            return
        self.eng[e].wait_ge(self._sem_of(key), val)
        self.seen[e][key] = val

    def _deps(self, e, reads, writes, pe_acc=False):
        best = {}
        for r in reads:
            t = self.lastw.get(r)
            if t is not None and best.get(t[0], 0) < t[1]:
                best[t[0]] = t[1]
        for w in writes:
            t = self.lastw.get(w)
            if t is not None and best.get(t[0], 0) < t[1]:
                if not (pe_acc and t[0] == "pe"):
                    best[t[0]] = t[1]
            for t in self.reads.get(w, ()):
                if best.get(t[0], 0) < t[1]:
                    best[t[0]] = t[1]
        for key, val in best.items():
            self._wait(e, (key, val))

    def _record(self, tok, reads, writes):
        for r in reads:
            lst = self.reads.setdefault(r, [])
            lst[:] = [t for t in lst if t[0] != tok[0]]
            lst.append(tok)
        for w in writes:
            self.lastw[w] = tok
            self.reads[w] = []

    def op(self, e, fn, reads=(), writes=(), pe_acc=False):
        self._deps(e, reads, writes, pe_acc)
        ins = fn(self.eng[e])
        self.cnt[e] += 1
        ins.then_inc(self.sem[e], 1)
        self._record((e, self.cnt[e]), reads, writes)
        self.n_instr += 1
        return ins

    def dma(self, q, out, in_, reads=(), writes=()):
        i = self.dnext
        self.dnext = (self.dnext + 1) % self.NDMA
        if self.dcnt[i] > 0:
            self._wait(q, (("d", i), self.dcnt[i]))
        self._deps(q, reads, writes)
        ins = self.eng[q].dma_start(out=out, in_=in_)
        self.dcnt[i] += 16
        ins.then_inc(self.dsem[i], 16)
        self._record((("d", i), self.dcnt[i]), reads, writes)
        self.n_instr += 1
        return ins

    def barrier(self):
        best = {}
        for k in self.cnt:
            if self.cnt[k]:
                best[k] = self.cnt[k]
        for i in range(self.NDMA):
            if self.dcnt[i]:
                best[("d", i)] = self.dcnt[i]
        for e in self.eng:
            for key, val in best.items():
                self._wait(e, (key, val))
        self.lastw = {}
        self.reads = {}


class B:
    def __init__(self, nc, es):
        self.nc, self.es = nc, es
        self.S = Sched(nc, es)
        self.D = {}
        self.psn = 0

    def din(self, name, shape, dt=F32):
        self.D[name] = self.nc.dram_tensor(name, list(shape), dt, kind="ExternalInput").ap()
        return self.D[name]

    def dout(self, name, shape, dt=F32):
        self.D[name] = self.nc.dram_tensor(name, list(shape), dt, kind="ExternalOutput").ap()
        return self.D[name]

    def dscr(self, name, shape, dt=F32):
        self.D[name] = self.nc.dram_tensor(name, list(shape), dt, kind="Internal").ap()
        return self.D[name]

    def sb(self, st, name, shape, dt=F32):
        self.uid = getattr(self, "uid", 0) + 1
        return st.enter_context(self.nc.sbuf_tensor("sb%d_%s" % (self.uid, name), list(shape), dt))

    def ps(self):
        i = self.psn % self.nrot
        self.psn += 1
        return self.psum[i], "ps%d" % i

    nrot = 6

    def acc(self, i):
        k = (6, 7, 4, 5)[i]
        return self.psum[k], "ps%d" % k


def build_program():
    nc = bass.Bass("TRN2", target_bir_lowering=False)
    es = ExitStack()
    b = B(nc, es)
    S = b.S
    D = b.D
    GR = {
        "s": dict(T=2048, nseq=1, L=2048, P=512, rope=True, cond=1),
        "p": dict(T=1024, nseq=4, L=256, P=0, rope=False, cond=0),
    }
    for g, G in GR.items():
        T = G["T"]
        b.din("xT_" + g, [DM, T])
        b.dout("yT_" + g, [DM, T])
        b.dscr("X1_" + g, [DM, T])
        for l in range(DEPTH - 1):
            b.dscr("X2_%d_%s" % (l, g), [DM, T])
        b.dscr("UT_" + g, [68 * 128, T])
        b.dscr("UTM_" + g, [T, UTM_COLS])
        b.dscr("OT_" + g, [4, 512, T], BF16)
        b.dscr("YP_" + g, [DM, T], BF16)
    b.din("cT", [128, 8, 2])
    b.din("w_ada", [DEPTH, DM, 6 * DM])
    b.din("b_adaT", [DEPTH, 128, 48])
    b.din("gains", [DEPTH, 128, 4, 8])
    b.din("w_in", [DEPTH, DM, D_IN])
    b.din("lbT", [DEPTH, 2, 512])
    b.din("hgrn_normT", [DEPTH, 128, 4])
    b.din("sink", [DEPTH, 1, 8])
    b.din("mla_qnT", [DEPTH, 128, 2])
    b.din("mla_kvn", [DEPTH, 128, 1])
    b.din("w_uq", [DEPTH, 256, 768])
    b.din("w_uq_sw", [DEPTH, 256, 768])
    b.din("w_ukv", [DEPTH, 128, 1024])
    b.din("gqa_qn", [DEPTH, 128, 2])
    b.din("gqa_kn", [DEPTH, 128, 2])
    b.din("w_branch", [DEPTH, 4, 512, DM])
    b.din("w_out", [DEPTH, DM, DM])
    b.din("w_mlp_in", [DEPTH, DM, 4 * DM])
    b.din("w_mlp_out", [DEPTH, 4 * DM, DM])
    b.din("st_hgrn", [DEPTH, 2, 4, 128, 128])
    b.din("c_swa_kT", [DEPTH, 2, 64, 512])
    b.din("c_swa_v", [DEPTH, 512, 128])
    b.din("c_ckvT", [DEPTH, 128, 512])
    b.din("c_krT", [DEPTH, 32, 512])
    b.din("c_gqa_kT", [DEPTH, 2, 64, 512])
    b.din("c_gqa_v", [DEPTH, 512, 128])
    b.din("rope64", [4, 128, 2048])
    b.din("cmat", [6, 128, 128])
    b.din("swamask", [2, 128, 512], BF16)
    b.dout("o_hgrn", [DEPTH, 4, 2, 4, 128, 128])
    b.dout("o_swa_kT", [DEPTH, 128, 1024])
    b.dout("o_swa_v", [DEPTH, 1024, 128])
    b.dout("o_ckvT", [DEPTH, 128, 1024])
    b.dout("o_krT", [DEPTH, 32, 1024])
    b.dout("o_gqa_kT", [DEPTH, 128, 1024])
    b.dout("o_gqa_v", [DEPTH, 1024, 128])

    b.psum = [es.enter_context(nc.psum_tensor("psb%d" % i, [128, 512], F32)) for i in range(8)]

    cst = ExitStack()
    es.enter_context(cst)
    ones_bf = b.sb(cst, "ones_bf", [128, 128], BF16)
    ones_f = b.sb(cst, "ones_f", [128, 128], F32)
    cm = b.sb(cst, "cm", [128, 6, 128], F32)
    modT = b.sb(cst, "modT", [128, DEPTH, 48, 2], F32)
    gains = b.sb(cst, "gains", [128, DEPTH, 4, 8], F32)
    coef = b.sb(cst, "coef", [128, DEPTH, 2, 6, 8], F32)
    S.op("pool", lambda e: e.memset(ones_bf[:], 1.0), writes=["ones_bf"])
    onesAB = b.sb(cst, "onesAB", [128, 2, 128], BF16)
    S.op("pool", lambda e: e.memset(onesAB[:], 0.0), writes=["onesAB"])
    S.op("pool", lambda e: e.memset(onesAB[:, 0, 0:64], 1.0), writes=["onesAB"])
    S.op("pool", lambda e: e.memset(onesAB[:, 1, 64:128], 1.0), writes=["onesAB"])
    S.op("pool", lambda e: e.memset(ones_f[:], 1.0), writes=["ones_f"])
    S.dma("sp", cm[:], D["cmat"].rearrange("a p c -> p a c"), writes=["cm"])
    S.dma("sp", gains[:], D["gains"].rearrange("l p a c -> p l a c"), writes=["gains"])

    with ExitStack() as st:
        cT = b.sb(st, "cT", [128, 8, 2])
        scT = b.sb(st, "scT", [128, 8, 2])
        badaT = b.sb(st, "badaT", [128, DEPTH, 48])
        wa = [b.sb(st, "wa%d" % i, [128, 8, 768]) for i in range(2)]
        S.dma("sp", cT[:], D["cT"], writes=["cT"])
        S.dma("sp", badaT[:], D["b_adaT"].rearrange("l p j -> p l j"), writes=["badaT"])
        S.op("act", lambda e: e.activation(scT[:], cT[:], AF.Silu), reads=["cT"], writes=["scT"])
        n = 0
        for l in range(DEPTH):
            pt, pn = b.ps()
            for cg in range(8):
                w = wa[n % 2]
                wn = "wa%d" % (n % 2)
                n += 1
                S.dma("sp" if cg % 2 == 0 else "act", w[:],
                      D["w_ada"][l, :, cg * 768:(cg + 1) * 768].rearrange("(kc p) c -> p kc c", p=128), writes=[wn])
                for jj in range(6):
                    j = cg * 6 + jj
                    for kc in range(8):
                        S.op("pe", lambda e: e.matmul(pt[:, 2 * j:2 * j + 2], w[:, kc, jj * 128:(jj + 1) * 128], scT[:, kc, :],
                                                      start=(kc == 0), stop=(kc == 7)),
                             reads=[wn, "scT"], writes=[pn], pe_acc=True)
            S.op("dve", lambda e: e.tensor_tensor(modT[:, l], pt[:, 0:96].rearrange("p (j c) -> p j c", c=2),
                                                  badaT[:, l].unsqueeze(2).to_broadcast([128, 48, 2]), ALU.add),
                 reads=[pn, "badaT"], writes=["modT"])
        for l in range(DEPTH):
            for c in range(2):
                m = lambda i: modT[:, l, i * 8:(i + 1) * 8, c]
                S.op("dve", lambda e: e.scalar_tensor_tensor(coef[:, l, c, 0], m(1), 1.0, gains[:, l, 0], ALU.add, ALU.mult),
                     reads=["modT", "gains"], writes=["coef"])
                S.op("dve", lambda e: e.tensor_copy(coef[:, l, c, 1], m(0)), reads=["modT"], writes=["coef"])
                S.op("dve", lambda e: e.tensor_tensor(coef[:, l, c, 2], m(2), gains[:, l, 1], ALU.mult), reads=["modT", "gains"], writes=["coef"])
                S.op("dve", lambda e: e.scalar_tensor_tensor(coef[:, l, c, 3], m(4), 1.0, gains[:, l, 2], ALU.add, ALU.mult),
                     reads=["modT", "gains"], writes=["coef"])
                S.op("dve", lambda e: e.tensor_copy(coef[:, l, c, 4], m(3)), reads=["modT"], writes=["coef"])
                S.op("dve", lambda e: e.tensor_tensor(coef[:, l, c, 5], m(5), gains[:, l, 3], ALU.mult), reads=["modT", "gains"], writes=["coef"])
        S.barrier()

    def rstd_from_sumsq(pt, pn, out, outn, npart, ncol, inv_n):
        S.op("act", lambda e: e.activation(out[0:npart, 0:ncol], pt[0:npart, 0:ncol], AF.Ln, bias=epsb[0:npart, :], scale=inv_n),
             reads=[pn, "epsb"], writes=[outn])
        S.op("act", lambda e: e.activation(out[0:npart, 0:ncol], out[0:npart, 0:ncol], AF.Exp, scale=-0.5), reads=[outn], writes=[outn])

    epsb = b.sb(cst, "epsb", [128, 1], F32)
    S.op("pool", lambda e: e.memset(epsb[:], EPS), writes=["epsb"])

    def norm_mod_tile(st_unused, xt, xn, hT, hn, t0, a_ap, sh_ap, tmp, sq, rs):
        for kc in range(8):
            S.op("act", lambda e: e.activation(sq[:, kc, :], xt[:, kc, :], AF.Square), reads=[xn], writes=["sq"])
        pt, pn = b.ps()
        for kc in range(8):
            S.op("pe", lambda e: e.matmul(pt[:], ones_bf[:], sq[:, kc, :], start=(kc == 0), stop=(kc == 7)),
                 reads=["sq", "ones_bf"], writes=[pn], pe_acc=True)
        rstd_from_sumsq(pt, pn, rs, "rs", 128, 512, 1.0 / DM)
        for kc in range(8):
            S.op("dve", lambda e: e.tensor_tensor(tmp[:], xt[:, kc, :], rs[:], ALU.mult), reads=[xn, "rs"], writes=["tmp"])
            S.op("act", lambda e: e.activation(hT[:, kc, t0:t0 + 512], tmp[:], AF.Identity, bias=sh_ap[:, kc:kc + 1], scale=a_ap[:, kc:kc + 1]),
                 reads=["tmp", "coef"], writes=[hn])

    def stage_h(st, g, l, Xsrc, ia, ish, lazy=False):
        G = GR[g]
        T = G["T"]
        hT = b.sb(st, "hT", [128, 8, T], BF16)
        s2 = st if lazy else ExitStack()
        xts = [b.sb(s2, "xt%d" % i, [128, 8, 512]) for i in range(2)]
        tmp = b.sb(s2, "tmp", [128, 512])
        sq = b.sb(s2, "sq", [128, 8, 512], BF16)
        rs = b.sb(s2, "rs", [128, 512])

        def emit(tt):
            xt, xn = xts[tt % 2], "xt%d" % (tt % 2)
            S.dma("sp", xt[:], Xsrc[:, tt * 512:(tt + 1) * 512].rearrange("(kc p) t -> p kc t", p=128), writes=[xn])
            norm_mod_tile(None, xt, xn, hT, "hT%d" % tt, tt * 512, coef[:, l, G["cond"], ia], coef[:, l, G["cond"], ish], tmp, sq, rs)

        if lazy:
            return hT, [(lambda tt=tt: emit(tt)) for tt in range(T // 512)]
        for tt in range(T // 512):
            emit(tt)
        S.barrier()
        s2.close()
        return hT

    def stage_proj(g, l, hT, hemit):
        G = GR[g]
        T = G["T"]
        UT, UTM = D["UT_" + g], D["UTM_" + g]
        tm_groups = {1: [(0, 512, 0)], 2: [(0, 512, 512)], 3: [(0, 512, 1024)], 6: [(128, 128, 1536)], 8: [(288, 128, 1664)]}
        with ExitStack() as st:
            wb = [b.sb(st, "wb%d" % i, [128, 8, 512], BF16) for i in range(2)]
            ev = [b.sb(st, "ev%d" % i, [128, 512]) for i in range(4)]
            nev = 0
            for cg in range(17):
                ncol = 512 if cg < 16 else D_IN - 8192
                w, wn = wb[cg % 2], "wb%d" % (cg % 2)
                S.dma("pool", w[:, :, 0:ncol], D["w_in"][l, :, cg * 512:cg * 512 + ncol].rearrange("(kc p) c -> p kc c", p=128), writes=[wn])
                if cg == 0:
                    hemit[0]()
                for tt in range(T // 512):
                    if cg == 0 and tt + 1 < len(hemit):
                        hemit[tt + 1]()
                    for cc in range((ncol + 127) // 128):
                        m = min(128, ncol - cc * 128)
                        pt, pn = b.ps()
                        for kc in range(8):
                            S.op("pe", lambda e: e.matmul(pt[0:m, :], w[:, kc, cc * 128:cc * 128 + m], hT[:, kc, tt * 512:(tt + 1) * 512],
                                                          start=(kc == 0), stop=(kc == 7)), reads=[wn, "hT%d" % tt], writes=[pn], pe_acc=(kc > 0))
                        e_, en = ev[nev % 4], "ev%d" % (nev % 4)
                        eng = "act" if nev % 2 == 0 else "dve"
                        nev += 1
                        if eng == "act":
                            S.op("act", lambda e: e.copy(e_[0:m, :], pt[0:m, :]), reads=[pn], writes=[en])
                        else:
                            S.op("dve", lambda e: e.tensor_copy(e_[0:m, :], pt[0:m, :]), reads=[pn], writes=[en])
                        r0 = cg * 512 + cc * 128
                        S.dma("sp", UT[r0:r0 + m, tt * 512:(tt + 1) * 512], e_[0:m, :], reads=[en], writes=[])
                for (c0, cn, dst) in tm_groups.get(cg, []):
                    for t4 in range(T // 128):
                        pt, pn = b.ps()
                        for kc in range(8):
                            S.op("pe", lambda e: e.matmul(pt[:, 0:cn], hT[:, kc, t4 * 128:(t4 + 1) * 128], w[:, kc, c0:c0 + cn],
                                                          start=(kc == 0), stop=(kc == 7)), reads=[wn, "hT%d" % (t4 // 4)], writes=[pn], pe_acc=(kc > 0))
                        e_, en = ev[nev % 4], "ev%d" % (nev % 4)
                        eng = "act" if nev % 2 == 0 else "dve"
                        nev += 1
                        if eng == "act":
                            S.op("act", lambda e: e.copy(e_[:, 0:cn], pt[:, 0:cn]), reads=[pn], writes=[en])
                        else:
                            S.op("dve", lambda e: e.tensor_copy(e_[:, 0:cn], pt[:, 0:cn]), reads=[pn], writes=[en])
                        S.dma("sp", UTM[t4 * 128:(t4 + 1) * 128, dst:dst + cn], e_[:, 0:cn], reads=[en], writes=[])
            S.barrier()

    def attn_unit(qT, qn, Kd, Nq, chunks, scale, o_out, on, wk, sink=None):
        po, pon = b.acc(0)
        pd, pdn = b.acc(1)
        nch = len(chunks)

        def emit_st(i):
            kT, kn, V, vn, nk, mask = chunks[i]
            pst, psn_ = b.ps()
            S.op("pe", lambda e: e.matmul(pst[0:nk, 0:Nq], kT, qT, start=True, stop=True), reads=[kn] + (qn if isinstance(qn, list) else [qn]), writes=[psn_])
            return pst, psn_

        cur = emit_st(0)
        for i, (kT, kn, V, vn, nk, mask) in enumerate(chunks):
            nxt = emit_st(i + 1) if i + 1 < nch else None
            pst, psn_ = cur
            E, En = wk["E"][i % 3], "E%d" % (i % 3)
            S.op("act", lambda e: e.activation(E[0:nk, 0:Nq], pst[0:nk, 0:Nq], AF.Exp, scale=scale), reads=[psn_], writes=[En])
            if mask is not None:
                S.op("pool", lambda e: e.tensor_tensor(E[0:nk, 0:Nq], E[0:nk, 0:Nq], mask, ALU.mult), reads=[En, "swamask"], writes=[En])
            last = (i == nch - 1) and sink is None
            S.op("pe", lambda e: e.matmul(po[0:64, 0:Nq], V, E[0:nk, 0:Nq], start=(i == 0), stop=(i == nch - 1)),
                 reads=[vn, En], writes=[pon], pe_acc=(i > 0))
            S.op("pe", lambda e: e.matmul(pd[0:64, 0:Nq], ones_bf[0:nk, 0:64], E[0:nk, 0:Nq], start=(i == 0), stop=last),
                 reads=["ones_bf", En], writes=[pdn], pe_acc=(i > 0))
            cur = nxt
        if sink is not None:
            S.op("pe", lambda e: e.matmul(pd[0:64, 0:Nq], ones_bf[0:1, 0:64], sink, start=False, stop=True),
                 reads=["ones_bf", "sinkrow"], writes=[pdn], pe_acc=True)
        rec = wk["rec"]
        S.op("dve", lambda e: e.reciprocal(rec[0:64, 0:Nq], pd[0:64, 0:Nq]), reads=[pdn], writes=["rec"])
        S.op("dve", lambda e: e.tensor_tensor(o_out, po[0:64, 0:Nq], rec[0:64, 0:Nq], ALU.mult), reads=[pon, "rec"], writes=[on])

    def attn_pair(qa, qan, qb, qbn, Nq, chunks, scale, o_out, on, wk, sink=None):
        po, pon = b.acc(0)
        pd, pdn = b.acc(1)
        nch = len(chunks)
        E = wk["E"]
        ne = len(E)

        def emit_st(i):
            kTa, kTb, kn, Va, Vb, vn, nk, mask = chunks[i]
            p1, n1 = b.ps()
            kna, knb = kn if isinstance(kn, tuple) else (kn, kn)
            S.op("pe", lambda e: e.matmul(p1[0:nk, 0:Nq], kTa, qa, start=True, stop=True), reads=[kna] + qan, writes=[n1])
            p2, n2 = b.ps()
            S.op("pe", lambda e: e.matmul(p2[0:nk, 0:Nq], kTb, qb, start=True, stop=True), reads=[knb] + qbn, writes=[n2])
            return (p1, n1, p2, n2)

        cur = emit_st(0)
        for i, (kTa, kTb, kn, Va, Vb, vn, nk, mask) in enumerate(chunks):
            nxt = emit_st(i + 1) if i + 1 < nch else None
            p1, n1, p2, n2 = cur
            Ea, Ean = E[(2 * i) % ne], "E%d" % ((2 * i) % ne)
            Eb, Ebn = E[(2 * i + 1) % ne], "E%d" % ((2 * i + 1) % ne)
            S.op("act", lambda e: e.activation(Ea[0:nk, 0:Nq], p1[0:nk, 0:Nq], AF.Exp, scale=scale), reads=[n1], writes=[Ean])
            S.op("act", lambda e: e.activation(Eb[0:nk, 0:Nq], p2[0:nk, 0:Nq], AF.Exp, scale=scale), reads=[n2], writes=[Ebn])
            if mask is not None:
                S.op("pool", lambda e: e.tensor_tensor(Ea[0:nk, 0:Nq], Ea[0:nk, 0:Nq], mask, ALU.mult), reads=[Ean, "swamask"], writes=[Ean])
                S.op("dve", lambda e: e.tensor_tensor(Eb[0:nk, 0:Nq], Eb[0:nk, 0:Nq], mask, ALU.mult), reads=[Ebn, "swamask"], writes=[Ebn])
            last = (i == nch - 1)
            S.op("pe", lambda e: e.matmul(po[:, 0:Nq], Va, Ea[0:nk, 0:Nq], start=(i == 0), stop=False), reads=[vn, Ean], writes=[pon], pe_acc=(i > 0))
            S.op("pe", lambda e: e.matmul(po[:, 0:Nq], Vb, Eb[0:nk, 0:Nq], start=False, stop=last), reads=[vn, Ebn], writes=[pon], pe_acc=True)
            S.op("pe", lambda e: e.matmul(pd[:, 0:Nq], onesAB[0:nk, 0, :], Ea[0:nk, 0:Nq], start=(i == 0), stop=False), reads=["onesAB", Ean], writes=[pdn], pe_acc=(i > 0))
            S.op("pe", lambda e: e.matmul(pd[:, 0:Nq], onesAB[0:nk, 1, :], Eb[0:nk, 0:Nq], start=False, stop=(last and sink is None)), reads=["onesAB", Ebn], writes=[pdn], pe_acc=True)
            cur = nxt
        if sink is not None:
            sa, sb_ = sink
            S.op("pe", lambda e: e.matmul(pd[:, 0:Nq], onesAB[0:1, 0, :], sa, start=False, stop=False), reads=["onesAB", "sinkrow"], writes=[pdn], pe_acc=True)
            S.op("pe", lambda e: e.matmul(pd[:, 0:Nq], onesAB[0:1, 1, :], sb_, start=False, stop=True), reads=["onesAB", "sinkrow"], writes=[pdn], pe_acc=True)
        rec = wk["rec"]
        S.op("dve", lambda e: e.reciprocal(rec[:, 0:Nq], pd[:, 0:Nq]), reads=[pdn], writes=["rec"])
        S.op("dve", lambda e: e.tensor_tensor(o_out, po[:, 0:Nq], rec[:, 0:Nq], ALU.mult), reads=[pon, "rec"], writes=[on])

    def rr_alloc(st, T, do_rope):
        sets = []
        for k in range(2):
            sets.append(dict(k=k, x=b.sb(st, "rr_x%d" % k, [128, T]), xs=(b.sb(st, "rr_xs%d" % k, [128, T]) if do_rope else None),
                             sq=b.sb(st, "rr_sq%d" % k, [128, T], BF16), rs=b.sb(st, "rr_rs%d" % k, [128, T]), t1=b.sb(st, "rr_t1%d" % k, [128, T])))
        return sets

    def rms_rope_rows(W_, p0, src_rows_fn, n_rows, T, gain2, gainn, do_norm, do_rope, rope_idx, out_bf, outn, out32=None, out32n=None,
                      out32_pre_rope=False):
        k = W_["k"]
        x, xs, sqb, rs, t1 = W_["x"], W_["xs"], W_["sq"], W_["rs"], W_["t1"]
        xn, xsn, sqn, rsn, t1n = ("rr_x%d" % k, "rr_xs%d" % k, "rr_sq%d" % k, "rr_rs%d" % k, "rr_t1%d" % k)
        for (r0, nr, ap) in src_rows_fn(False):
            S.dma("sp", x[p0 + r0:p0 + r0 + nr, 0:T], ap, writes=[xn])
        if do_rope:
            for (r0, nr, ap) in src_rows_fn(True):
                S.dma("act", xs[p0 + r0:p0 + r0 + nr, 0:T], ap, writes=[xsn])
        nr = n_rows
        pr = slice(p0, p0 + nr)
        W = min(512, T)
        CS = [slice(c * W, (c + 1) * W) for c in range(T // W)]
        if do_norm:
            S.op("act", lambda e: e.activation(sqb[pr, 0:T], x[pr, 0:T], AF.Square), reads=[xn], writes=[sqn])
            pts = []
            for cs in CS:
                pt, pn = b.ps()
                S.op("pe", lambda e: e.matmul(pt[:, 0:W], ones_bf[pr, :], sqb[pr, cs], start=True, stop=True), reads=[sqn, "ones_bf"], writes=[pn])
                pts.append((pt, pn))
            for cs, (pt, pn) in zip(CS, pts):
                S.op("act", lambda e: e.activation(rs[pr, cs], pt[pr, 0:W], AF.Ln, bias=epsb[pr, :], scale=1.0 / nr), reads=[pn, "epsb"], writes=[rsn])
            S.op("act", lambda e: e.activation(rs[pr, 0:T], rs[pr, 0:T], AF.Exp, scale=-0.5), reads=[rsn], writes=[rsn])
            S.op("dve", lambda e: e.scalar_tensor_tensor(x[pr, 0:T], x[pr, 0:T], gain2[pr, 0:1], rs[pr, 0:T], ALU.mult, ALU.mult),
                 reads=[xn, rsn, gainn], writes=[xn])
            if do_rope:
                S.op("dve", lambda e: e.scalar_tensor_tensor(xs[pr, 0:T], xs[pr, 0:T], gain2[pr, 1:2], rs[pr, 0:T], ALU.mult, ALU.mult),
                     reads=[xsn, rsn, gainn], writes=[xsn])
        if out32 is not None and out32_pre_rope:
            S.op("pool", lambda e: e.tensor_copy(out32[pr, 0:T], x[pr, 0:T]), reads=[xn], writes=[out32n])
        if do_rope:
            S.op("dve", lambda e: e.tensor_tensor(x[pr, 0:T], x[pr, 0:T], rope[pr, rope_idx, 0:T], ALU.mult), reads=[xn, "rope"], writes=[xn])
            S.op("pool", lambda e: e.tensor_tensor(t1[pr, 0:T], xs[pr, 0:T], rope[pr, rope_idx + 1, 0:T], ALU.mult), reads=[xsn, "rope"], writes=[t1n])
            S.op("dve", lambda e: e.tensor_tensor(out_bf[:, 0:T], x[pr, 0:T], t1[pr, 0:T], ALU.add), reads=[xn, t1n], writes=[outn])
        else:
            S.op("act", lambda e: e.copy(out_bf[:, 0:T], x[pr, 0:T]), reads=[xn], writes=[outn])

    def swap_rows(base, T0, T1, UT, nd):
        q = nd // 4
        def f(swapped):
            if not swapped:
                return [(0, nd, UT[base:base + nd, T0:T1])]
            return [(0, q, UT[base + q:base + 2 * q, T0:T1]), (q, q, UT[base:base + q, T0:T1]),
                    (2 * q, q, UT[base + 3 * q:base + 4 * q, T0:T1]), (3 * q, q, UT[base + 2 * q:base + 3 * q, T0:T1])]
        return f

    rope = b.sb(cst, "rope", [128, 4, 2048], BF16)
    S.dma("pool", rope[:], D["rope64"].rearrange("a p t -> p a t"), writes=["rope"])

    def stage_gqa_like(g, l, kind):
        G = GR[g]
        T, L, P, nseq, do_rope = G["T"], G["L"], G["P"], G["nseq"], G["rope"]
        UT, UTM, OT = D["UT_" + g], D["UTM_" + g], D["OT_" + g]
        qo, ko = (OFF["sq"], OFF["sk"]) if kind == "swa" else (OFF["gq"], OFF["gk"])
        vcol = 1536 if kind == "swa" else 1664
        bi = 1 if kind == "swa" else 3
        do_norm = kind == "gqa"
        scale = 64 ** -0.5
        nkc_ctx = P // 128
        with ExitStack() as st:
            kT = b.sb(st, "kT", [128, P + T], BF16)
            Vt = b.sb(st, "Vt", [128, (P + T) // 128, 2, 128], BF16)
            qTh = b.sb(st, "qTh", [128, 8, T], BF16)
            gq2 = b.sb(st, "gq2", [128, 2]); gk2 = b.sb(st, "gk2", [128, 2])
            sinkrow = b.sb(st, "sinkrow", [1, 8, 128], BF16)
            sk32 = b.sb(st, "sk32", [1, 8])
            wk = dict(E=[b.sb(st, "E%d" % i, [128, 512], BF16) for i in range(6)], rec=b.sb(st, "rec", [128, 512]))
            obs = [b.sb(st, "ob%d" % i, [128, 512], BF16) for i in range(2)]
            swm = b.sb(st, "swamask", [128, 2, 512], BF16)
            S.dma("sp", swm[:], D["swamask"].rearrange("a p c -> p a c"), writes=["swamask"])
            S.op("pool", lambda e: e.memset(qTh[64:128, 0:4, :], 0.0), writes=["qTh%d" % hh for hh in range(4)])
            S.op("pool", lambda e: e.memset(qTh[0:64, 4:8, :], 0.0), writes=["qTh%d" % hh for hh in range(4, 8)])
            S.op("pool", lambda e: e.memset(Vt[:], 0.0), writes=["Vt"])
            if do_norm:
                S.dma("sp", gq2[:], D["gqa_qn"][l], writes=["gq2"])
                S.dma("sp", gk2[:], D["gqa_kn"][l], writes=["gk2"])
            else:
                S.dma("sp", sk32[:], D["sink"][l], writes=["sk32"])
                S.op("act", lambda e: e.activation(sk32[:], sk32[:], AF.Exp), reads=["sk32"], writes=["sk32"])
                S.op("dve", lambda e: e.tensor_copy(sinkrow[:], sk32[:].unsqueeze(2).to_broadcast([1, 8, 128])), reads=["sk32"], writes=["sinkrow"])
            with ExitStack() as s2:
                RR = rr_alloc(s2, T, do_rope)
                k32 = b.sb(s2, "k32", [128, T]) if g == "p" else None
                v32 = b.sb(s2, "v32", [128, (P + T) // 128, 128])
                if P:
                    src_k = D["c_swa_kT"] if kind == "swa" else D["c_gqa_kT"]
                    src_v = D["c_swa_v"] if kind == "swa" else D["c_gqa_v"]
                    S.dma("pool", kT[:, 0:P], src_k[l].rearrange("h d t -> (h d) t"), writes=["kT"])
                    S.dma("sp", v32[:, 0:P // 128, :], src_v[l].rearrange("(c p) f -> p c f", p=128), writes=["v32"])
                S.dma("sp", v32[:, P // 128:, :], UTM[:, vcol:vcol + 128].rearrange("(c p) f -> p c f", p=128), writes=["v32"])
                S.op("dve", lambda e: e.tensor_copy(Vt[:, :, 0, 0:64], v32[:, :, 0:64]), reads=["v32"], writes=["Vt"])
                S.op("dve", lambda e: e.tensor_copy(Vt[:, :, 1, 64:128], v32[:, :, 64:128]), reads=["v32"], writes=["Vt"])
                if g == "p":
                    dst = D["o_swa_v"] if kind == "swa" else D["o_gqa_v"]
                    S.dma("act", dst[l].rearrange("(c p) f -> p c f", p=128), v32[:], reads=["v32"], writes=[])
                nrr = 0
                for kvh in range(2):
                    p0 = kvh * 64
                    rms_rope_rows(RR[nrr % 2], p0, swap_rows(ko + kvh * 64, 0, T, UT, 64), 64, T, gk2, "gk2", do_norm, do_rope, 0,
                                  kT[p0:p0 + 64, P:P + T], "kT", out32=k32, out32n="k32", out32_pre_rope=True)
                    nrr += 1
                if g == "p":
                    dst = D["o_swa_kT"] if kind == "swa" else D["o_gqa_kT"]
                    S.dma("sp", dst[l], k32[:], reads=["k32"], writes=[])
                for h in range(8):
                    p0 = (h // 4) * 64
                    rms_rope_rows(RR[nrr % 2], p0, swap_rows(qo + h * 64, 0, T, UT, 64), 64, T, gq2, "gq2", do_norm, do_rope, 0,
                                  qTh[p0:p0 + 64, h, :], "qTh%d" % h)
                    nrr += 1
                nu = 0
                qna = ["qTh%d" % hh for hh in range(4)]
                qnb = ["qTh%d" % hh for hh in range(4, 8)]
                for sq_ in range(nseq):
                    T0 = sq_ * L
                    kb0 = P + T0
                    for qb in range(L // 128):
                        cols = [(c * 128, None) for c in range(nkc_ctx)]
                        if kind == "swa" and P:
                            cols += [(kb0 + kb * 128, mi) for (kb, mi) in ((qb - 1, 0), (qb, None), (qb + 1, 1)) if 0 <= kb < L // 128]
                        else:
                            cols += [(kb0 + kb * 128, None) for kb in range(L // 128)]
                        chunks = [(kT[:, c0:c0 + 128], kT[:, c0:c0 + 128], "kT", Vt[:, c0 // 128, 0, :], Vt[:, c0 // 128, 1, :], "Vt", 128,
                                   None if mi is None else swm[:, mi, :]) for (c0, mi) in cols]
                        ob, obn = obs[nu % 2], "ob%d" % (nu % 2)
                        nu += 1
                        q0 = T0 + qb * 128
                        attn_pair(qTh[:, 0:4, q0:q0 + 128], qna, qTh[:, 4:8, q0:q0 + 128], qnb, 512, chunks, scale, ob[:], obn, wk,
                                  sink=((sinkrow[0:1, 0:4, :], sinkrow[0:1, 4:8, :]) if kind == "swa" else None))
                        for kvh in range(2):
                            S.dma("sp" if kvh == 0 else "act", OT[bi, kvh * 256:(kvh + 1) * 256, q0:q0 + 128].rearrange("(h d) t -> d h t", d=64),
                                  ob[kvh * 64:(kvh + 1) * 64, :].rearrange("d (h t) -> d h t", h=4), reads=[obn], writes=[])
                S.barrier()

    def stage_mla(g, l):
        G = GR[g]
        T, L, P, nseq, do_rope = G["T"], G["L"], G["P"], G["nseq"], G["rope"]
        UT, OT = D["UT_" + g], D["OT_" + g]
        scale = 96 ** -0.5
        NK = P + L
        for sq_ in range(nseq):
            T0, T1 = sq_ * L, (sq_ + 1) * L
            with ExitStack() as st:
                ckvT = b.sb(st, "ckvT", [128, NK], BF16)
                krT = b.sb(st, "krT", [96, NK], BF16)
                cqn = b.sb(st, "cqn", [128, 2, L], BF16)
                wuq = b.sb(st, "wuq", [128, 2, 768], BF16); wuqs = b.sb(st, "wuqs", [128, 2, 768], BF16)
                wukv = b.sb(st, "wukv", [128, 1024], BF16)
                g_q = b.sb(st, "g_q", [128, 2]); g_kv = b.sb(st, "g_kv", [128, 1])
                S.dma("pool", wuq[:], D["w_uq"][l].rearrange("(kc p) c -> p kc c", p=128), writes=["wuq"])
                S.dma("pool", wuqs[:], D["w_uq_sw"][l].rearrange("(kc p) c -> p kc c", p=128), writes=["wuqs"])
                S.dma("pool", wukv[:], D["w_ukv"][l], writes=["wukv"])
                S.dma("sp", g_q[:], D["mla_qnT"][l], writes=["g_q"])
                S.dma("sp", g_kv[:], D["mla_kvn"][l], writes=["g_kv"])
                with ExitStack() as s2:
                    x = b.sb(s2, "m_x", [128, 2, L]); sqb = b.sb(s2, "m_sq", [128, 2, 512], BF16); rs = b.sb(s2, "m_rs", [128, 512])
                    xk = b.sb(s2, "m_xk", [128, L]); kr32 = b.sb(s2, "m_kr", [96, L]); krs = b.sb(s2, "m_krs", [96, L]); t1 = b.sb(s2, "m_t1", [96, 512])
                    if P:
                        S.dma("pool", ckvT[:, 0:P], D["c_ckvT"][l], writes=["ckvT"])
                        S.dma("pool", krT[64:96, 0:P], D["c_krT"][l], writes=["krT"])
                    S.dma("sp", x[:], UT[OFF["cq"]:OFF["cq"] + 256, T0:T1].rearrange("(kc p) t -> p kc t", p=128), writes=["m_x"])
                    S.dma("sp", xk[:], UT[OFF["ckv"]:OFF["ckv"] + 128, T0:T1], writes=["m_xk"])
                    S.dma("sp", kr32[64:96, :], UT[OFF["kr"]:OFF["kr"] + 32, T0:T1], writes=["m_kr"])
                    if do_rope:
                        for (r0, nr, ap) in swap_rows(OFF["kr"], T0, T1, UT, 32)(True):
                            S.dma("act", krs[64 + r0:64 + r0 + nr, :], ap, writes=["m_krs"])
                    for c in range(L // 512 if L >= 512 else 1):
                        w = min(512, L)
                        cs = slice(c * w, (c + 1) * w)
                        pt, pn = b.ps()
                        for kc in range(2):
                            S.op("act", lambda e: e.activation(sqb[:, kc, 0:w], x[:, kc, cs], AF.Square), reads=["m_x"], writes=["m_sq"])
                        for kc in range(2):
                            S.op("pe", lambda e: e.matmul(pt[:, 0:w], ones_bf[:], sqb[:, kc, 0:w], start=(kc == 0), stop=(kc == 1)),
                                 reads=["m_sq", "ones_bf"], writes=[pn], pe_acc=True)
                        rstd_from_sumsq(pt, pn, rs, "m_rs", 128, w, 1.0 / 256)
                        for kc in range(2):
                            S.op("dve", lambda e: e.scalar_tensor_tensor(cqn[:, kc, cs], x[:, kc, cs], g_q[:, kc:kc + 1], rs[:, 0:w], ALU.mult, ALU.mult),
                                 reads=["m_x", "m_rs", "g_q"], writes=["cqn"])
                        pt, pn = b.ps()
                        S.op("act", lambda e: e.activation(sqb[:, 0, 0:w], xk[:, cs], AF.Square), reads=["m_xk"], writes=["m_sq"])
                        S.op("pe", lambda e: e.matmul(pt[:, 0:w], ones_bf[:], sqb[:, 0, 0:w], start=True, stop=True), reads=["m_sq", "ones_bf"], writes=[pn])
                        rstd_from_sumsq(pt, pn, rs, "m_rs", 128, w, 1.0 / 128)
                        S.op("dve", lambda e: e.scalar_tensor_tensor(xk[:, cs], xk[:, cs], g_kv[:, 0:1], rs[:, 0:w], ALU.mult, ALU.mult),
                             reads=["m_xk", "m_rs", "g_kv"], writes=["m_xk"])
                        S.op("pool", lambda e: e.tensor_copy(ckvT[:, P + c * w:P + (c + 1) * w], xk[:, cs]), reads=["m_xk"], writes=["ckvT"])
                        if do_rope:
                            S.op("dve", lambda e: e.tensor_tensor(t1[64:96, 0:w], kr32[64:96, cs], rope[64:96, 2, cs], ALU.mult), reads=["m_kr", "rope"], writes=["m_t1"])
                            S.op("pool", lambda e: e.tensor_tensor(krs[64:96, cs], krs[64:96, cs], rope[64:96, 3, cs], ALU.mult), reads=["m_krs", "rope"], writes=["m_krs"])
                            S.op("dve", lambda e: e.tensor_tensor(krT[64:96, P + c * w:P + (c + 1) * w], t1[64:96, 0:w], krs[64:96, cs], ALU.add),
                                 reads=["m_t1", "m_krs"], writes=["krT"])
                        else:
                            S.op("dve", lambda e: e.tensor_copy(krT[64:96, P + c * w:P + (c + 1) * w], kr32[64:96, cs]), reads=["m_kr"], writes=["krT"])
                    if g == "p":
                        S.dma("sp", D["o_ckvT"][l, :, T0:T1], xk[:], reads=["m_xk"], writes=[])
                        S.dma("sp", D["o_krT"][l, :, T0:T1], kr32[64:96, :], reads=["m_kr"], writes=[])
                    S.barrier()
                Vall = b.sb(st, "Vall", [128, NK // 128, 8, 128], BF16)
                S.op("pool", lambda e: e.memset(Vall[:], 0.0), writes=["Vall"])
                for c in range(NK // 128):
                    pt, pn = b.ps()
                    S.op("pe", lambda e: e.matmul(pt[:, :], ckvT[:, c * 128:(c + 1) * 128],
                                                  wukv[:].rearrange("p (h x) -> p h x", x=128)[:, :, 64:128], start=True, stop=True),
                         reads=["ckvT", "wukv"], writes=[pn])
                    pv = pt[:, :].rearrange("p (h two d) -> p h two d", two=2, d=64)
                    S.op("act", lambda e: e.copy(Vall[:, c, 0::2, 0:64], pv[:, :, 0, :]), reads=[pn], writes=["Vall"])
                    S.op("dve", lambda e: e.tensor_copy(Vall[:, c, 1::2, 64:128], pv[:, :, 1, :]), reads=[pn], writes=["Vall"])
                S.barrier()
                with ExitStack() as s2:
                    wk = dict(E=[b.sb(s2, "E%d" % i, [128, 512], BF16) for i in range(6)], rec=b.sb(s2, "rec", [128, 512]))
                    kTh = [b.sb(s2, "kTh%d" % i, [128, NK], BF16) for i in range(4)]
                    qh = [b.sb(s2, "qh%d" % i, [128, L], BF16) for i in range(4)]
                    obs = [b.sb(s2, "ob%d" % i, [128, 512], BF16) for i in range(2)]
                    t2 = b.sb(s2, "m_t2", [96, 512]); t3 = b.sb(s2, "m_t3", [96, 512])
                    for i in range(4):
                        S.op("pool", lambda e: e.memset(kTh[i][96:128, :], 0.0), writes=["kTh%d" % i])
                        S.op("pool", lambda e: e.memset(qh[i][96:128, :], 0.0), writes=["qh%d" % i])
                    nu = 0
                    W = min(512, L)
                    for hp in range(4):
                        bufs = []
                        for h2 in range(2):
                            h = hp * 2 + h2
                            bi_ = (hp % 2) * 2 + h2
                            kt, ktn = kTh[bi_], "kTh%d" % bi_
                            q_, q_n = qh[bi_], "qh%d" % bi_
                            bufs.append((kt, ktn, q_, q_n))
                            for c in range((NK + 511) // 512):
                                w = min(512, NK - c * 512)
                                pt, pn = b.ps()
                                S.op("pe", lambda e: e.matmul(pt[:, 0:w], wukv[:, h * 128:(h + 1) * 128], ckvT[:, c * 512:c * 512 + w], start=True, stop=True),
                                     reads=["wukv", "ckvT"], writes=[pn])
                                S.op("act", lambda e: e.copy(kt[0:64, c * 512:c * 512 + w], pt[0:64, 0:w]), reads=[pn], writes=[ktn])
                            S.op("pool", lambda e: e.tensor_copy(kt[64:96, :], krT[64:96, :]), reads=["krT"], writes=[ktn])
                            for c in range(L // W):
                                cs = slice(c * W, (c + 1) * W)
                                pt, pn = b.ps()
                                for kc in range(2):
                                    S.op("pe", lambda e: e.matmul(pt[0:96, 0:W], wuq[:, kc, h * 96:(h + 1) * 96], cqn[:, kc, cs], start=(kc == 0), stop=(kc == 1)),
                                         reads=["wuq", "cqn"], writes=[pn], pe_acc=(kc > 0))
                                S.op("act", lambda e: e.copy(q_[0:64, cs], pt[0:64, 0:W]), reads=[pn], writes=[q_n])
                                if do_rope:
                                    pt2, pn2 = b.ps()
                                    for kc in range(2):
                                        S.op("pe", lambda e: e.matmul(pt2[0:96, 0:W], wuqs[:, kc, h * 96:(h + 1) * 96], cqn[:, kc, cs], start=(kc == 0), stop=(kc == 1)),
                                             reads=["wuqs", "cqn"], writes=[pn2], pe_acc=(kc > 0))
                                    S.op("dve", lambda e: e.tensor_tensor(t2[64:96, 0:W], pt[64:96, 0:W], rope[64:96, 2, cs], ALU.mult), reads=[pn, "rope"], writes=["m_t2"])
                                    S.op("dve", lambda e: e.tensor_tensor(t3[64:96, 0:W], pt2[64:96, 0:W], rope[64:96, 3, cs], ALU.mult), reads=[pn2, "rope"], writes=["m_t3"])
                                    S.op("pool", lambda e: e.tensor_tensor(q_[64:96, cs], t2[64:96, 0:W], t3[64:96, 0:W], ALU.add), reads=["m_t2", "m_t3"], writes=[q_n])
                                else:
                                    S.op("dve", lambda e: e.tensor_copy(q_[64:96, cs], pt[64:96, 0:W]), reads=[pn], writes=[q_n])
                        (kta, ktan, qa, qan), (ktb, ktbn, qb_, qbn) = bufs
                        for c in range(L // W):
                            chunks = [(kta[:, kc * 128:(kc + 1) * 128], ktb[:, kc * 128:(kc + 1) * 128], (ktan, ktbn), Vall[:, kc, hp * 2, :], Vall[:, kc, hp * 2 + 1, :], "Vall", 128, None)
                                      for kc in range(NK // 128)]
                            ob, obn = obs[nu % 2], "ob%d" % (nu % 2)
                            nu += 1
                            attn_pair(qa[:, c * W:(c + 1) * W], [qan], qb_[:, c * W:(c + 1) * W], [qbn], W, chunks, scale, ob[:, 0:W], obn, wk)
                            S.dma("sp", OT[2, hp * 128:(hp + 1) * 128, T0 + c * W:T0 + (c + 1) * W], ob[:, 0:W], reads=[obn], writes=[])
                    S.barrier()

    def stage_hgrn(g, l):
        G = GR[g]
        T, L, P, nseq = G["T"], G["L"], G["P"], G["nseq"]
        UT, UTM, OT = D["UT_" + g], D["UTM_" + g], D["OT_" + g]
        NTL = L // 128
        ident, triU, triL, sL, sU, csel = (cm[:, i, :] for i in range(6))
        with ExitStack() as st:
            lbb = b.sb(st, "lbb", [128, 2, 512]); oml = b.sb(st, "oml", [128, 2, 512])
            with ExitStack() as s2:
                lbr = b.sb(s2, "lbr", [128, DEPTH, 2, 512]); den = b.sb(s2, "lden", [128, 2, 512])
                S.dma("sp", lbr[:], D["lbT"].partition_broadcast(128), writes=["lbr"])
                S.op("act", lambda e: e.activation(lbr[:], lbr[:], AF.Exp), reads=["lbr"], writes=["lbr"])
                S.op("dve", lambda e: e.tensor_tensor(den[:], lbr[:, 0], lbr[:, 1], ALU.add), reads=["lbr"], writes=["lden"])
                S.op("dve", lambda e: e.reciprocal(den[:], den[:]), reads=["lden"], writes=["lden"])
                if l == 0:
                    S.op("pool", lambda e: e.memset(lbb[:], 0.0), writes=["lbb"])
                else:
                    S.op("dve", lambda e: e.tensor_tensor(lbb[:], lbr[:, 1], den[:], ALU.mult), reads=["lbr", "lden"], writes=["lbb"])
                S.op("dve", lambda e: e.tensor_scalar(oml[:], lbb[:], -1.0, 1.0, ALU.mult, ALU.add), reads=["lbb"], writes=["oml"])
                S.barrier()
            hn = b.sb(st, "hn", [128, 4]); S.dma("sp", hn[:], D["hgrn_normT"][l], writes=["hn"])
            Sst = [b.sb(st, "Sst%d" % i, [128, 2, 4, 128]) for i in range(2)]
            oall = b.sb(st, "oall", [128, 2, 4, L], BF16)
            W_ = {}
            for d in range(2):
                for k in range(2):
                    sfx = "%d%d" % (d, k)
                    W_[d, k] = dict(
                        sfx=sfx,
                        tt=b.sb(st, "h_t" + sfx, [128, 512]), vv=b.sb(st, "h_v" + sfx, [128, 512]), gg=b.sb(st, "h_g" + sfx, [128, 512]),
                        kt=b.sb(st, "h_kt" + sfx, [128, 512]), kh=b.sb(st, "h_kh" + sfx, [128, 512]),
                        khm=b.sb(st, "h_khm" + sfx, [128, 4, 4, 128]), qT=b.sb(st, "h_qT" + sfx, [128, 4, 128]), qt=b.sb(st, "h_qt" + sfx, [128, 4, 128]),
                        eb=b.sb(st, "h_eb" + sfx, [128, 4, 128]), ktT=b.sb(st, "h_ktT" + sfx, [128, 4, 128]), AT=b.sb(st, "h_AT" + sfx, [128, 4, 128]),
                        tmpo=b.sb(st, "h_tmpo" + sfx, [128, 512]))
            etot = [b.sb(st, "h_etot%d" % k, [128, 2, 4, 4]) for k in range(2)]
            fin = dict(os=b.sb(st, "f_os", [128, 4, 256]), gT=b.sb(st, "f_gT", [128, 4, 256]), sq=b.sb(st, "f_sq", [128, 4, 256], BF16),
                       rs=b.sb(st, "f_rs", [128, 4, 256]), ob=b.sb(st, "f_ob", [128, 4, 256], BF16))
            b.nrot = 4
            M4 = lambda M: M.unsqueeze(1).to_broadcast([128, 4, 128])

            def sstn(sp, d, h):
                return "Sst%d_%d%d" % (sp, d, h)

            def partA(rec):
                k = rec["k"]
                for d in range(2):
                    w = W_[d, k]; x = w["sfx"]
                    t0 = rec["T0"] + rec["ti"][d] * 128
                    S.dma("sp", w["tt"][:], UTM[t0:t0 + 128, d * 512:(d + 1) * 512], writes=["h_t" + x])
                    S.dma("sp", w["vv"][:], UTM[t0:t0 + 128, 1024:1536], writes=["h_v" + x])
                    S.dma("act", w["qT"][:], UT[0:512, t0:t0 + 128].rearrange("(h p) t -> p h t", p=128), writes=["h_qT" + x])
                for d in range(2):
                    w = W_[d, k]; x = w["sfx"]
                    S.op("act", lambda e: e.activation(w["tt"][:], w["tt"][:], AF.Sigmoid), reads=["h_t" + x], writes=["h_t" + x])
                for d in range(2):
                    w = W_[d, k]; x = w["sfx"]
                    S.op("act", lambda e: e.activation(w["qT"][:], w["qT"][:], AF.Silu), reads=["h_qT" + x], writes=["h_qT" + x])
                for d in range(2):
                    w = W_[d, k]; x = w["sfx"]
                    S.op("dve", lambda e: e.tensor_tensor(w["tt"][:], w["tt"][:], oml[:, d], ALU.mult), reads=["h_t" + x, "oml"], writes=["h_t" + x])
                    S.op("dve", lambda e: e.scalar_tensor_tensor(w["gg"][:], w["tt"][:], 1e-30, lbb[:, d], ALU.max, ALU.add), reads=["h_t" + x, "lbb"], writes=["h_g" + x])
                for d in range(2):
                    w = W_[d, k]; x = w["sfx"]
                    S.op("act", lambda e: e.activation(w["gg"][:], w["gg"][:], AF.Ln), reads=["h_g" + x], writes=["h_g" + x])
                for d in range(2):
                    w = W_[d, k]; x = w["sfx"]
                    S.op("dve", lambda e: e.tensor_tensor(w["tt"][:], oml[:, d], w["tt"][:], ALU.subtract), reads=["h_t" + x, "oml"], writes=["h_t" + x])

            def partB(rec, stage):
                k = rec["k"]
                Ms = [(triU, sL), (triL, sU)]
                if stage == 0:
                    pbs = []
                    for d in range(2):
                        w = W_[d, k]; x = w["sfx"]
                        pb, pbn = b.ps()
                        S.op("pe", lambda e: e.matmul(pb[:], Ms[d][0], w["gg"][:], start=True, stop=True), reads=["cm", "h_g" + x], writes=[pbn])
                        pr, prn = b.ps()
                        S.op("pe", lambda e: e.matmul(pr[:], Ms[d][1], w["gg"][:], start=True, stop=True), reads=["cm", "h_g" + x], writes=[prn])
                        pbs.append((pb, pbn, pr, prn))
                    for d in range(2):
                        w = W_[d, k]; x = w["sfx"]
                        pb, pbn, pr, prn = pbs[d]
                        S.op("act", lambda e: e.activation(w["kt"][:], pb[:], AF.Exp, scale=-1.0), reads=[pbn], writes=["h_kt" + x])
                        S.op("act", lambda e: e.activation(w["kh"][:], pr[:], AF.Exp), reads=[prn], writes=["h_kh" + x])
                    pts = []
                    ptot, ptotn = b.ps()
                    for d in range(2):
                        w = W_[d, k]; x = w["sfx"]
                        pbt, pbtn = b.ps()
                        for h in range(4):
                            hs = slice(h * 128, (h + 1) * 128)
                            S.op("pe", lambda e: e.matmul(pbt[:, hs], w["gg"][:, hs], Ms[d][0], start=True, stop=True), reads=["h_g" + x, "cm"], writes=[pbtn], pe_acc=(h > 0))
                            S.op("pe", lambda e: e.matmul(ptot[:, d * 16 + h * 4:d * 16 + h * 4 + 4], w["gg"][:, hs], csel[:, 0:4], start=True, stop=True),
                                 reads=["h_g" + x, "cm"], writes=[ptotn], pe_acc=(d > 0 or h > 0))
                        pts.append((pbt, pbtn))
                    for d in range(2):
                        w = W_[d, k]; x = w["sfx"]
                        S.op("dve", lambda e: e.tensor_tensor(w["kt"][:], w["kt"][:], w["tt"][:], ALU.mult), reads=["h_kt" + x, "h_t" + x], writes=["h_kt" + x])
                        S.op("dve", lambda e: e.tensor_tensor(w["kh"][:], w["kh"][:], w["tt"][:], ALU.mult), reads=["h_kh" + x, "h_t" + x], writes=["h_kh" + x])
                    for d in range(2):
                        w = W_[d, k]; x = w["sfx"]
                        pbt, pbtn = pts[d]
                        S.op("act", lambda e: e.activation(w["eb"][:], pbt[:].rearrange("p (h t) -> p h t", h=4), AF.Exp), reads=[pbtn], writes=["h_eb" + x])
                    S.op("act", lambda e: e.activation(etot[k][:], ptot[:, 0:32].rearrange("p (d h j) -> p d h j", d=2, h=4), AF.Exp), reads=[ptotn], writes=["h_etot%d" % k])
                    for d in range(2):
                        w = W_[d, k]; x = w["sfx"]
                        S.op("dve", lambda e: e.tensor_tensor(w["qt"][:], w["qT"][:], w["eb"][:], ALU.mult), reads=["h_qT" + x, "h_eb" + x], writes=["h_qt" + x])
                    for d in range(2):
                        w = W_[d, k]; x = w["sfx"]
                        for j in range(4):
                            S.op("act", lambda e: e.activation(w["khm"][:, :, j, :], w["kh"][:].rearrange("p (h c) -> p h c", h=4), AF.Copy, scale=csel[:, j:j + 1]),
                                 reads=["h_kh" + x, "cm"], writes=["h_khm" + x])
                elif stage == 1:
                    for d in range(2):
                        w = W_[d, k]; x = w["sfx"]
                        pk, pkn = b.ps()
                        for h in range(4):
                            hs = slice(h * 128, (h + 1) * 128)
                            S.op("pe", lambda e: e.matmul(pk[:, hs], w["kt"][:, hs], ident, start=True, stop=True), reads=["h_kt" + x, "cm"], writes=[pkn], pe_acc=(h > 0))
                        S.op("act", lambda e: e.copy(w["ktT"][:], pk[:].rearrange("p (h t) -> p h t", h=4)), reads=[pkn], writes=["h_ktT" + x])
                elif stage == 2:
                    for d in range(2):
                        w = W_[d, k]; x = w["sfx"]
                        pa, pan = b.ps()
                        for h in range(4):
                            hs = slice(h * 128, (h + 1) * 128)
                            S.op("pe", lambda e: e.matmul(pa[:, hs], w["ktT"][:, h, :], w["qt"][:, h, :], start=True, stop=True), reads=["h_ktT" + x, "h_qt" + x], writes=[pan], pe_acc=(h > 0))
                        S.op("dve", lambda e: e.tensor_tensor(w["AT"][:], pa[:].rearrange("p (h t) -> p h t", h=4), M4(Ms[d][0]), ALU.mult), reads=[pan, "cm"], writes=["h_AT" + x])
                else:
                    for d in range(2):
                        w = W_[d, k]; x = w["sfx"]
                        po_, pon_ = b.acc(d)
                        for h in range(4):
                            hs = slice(h * 128, (h + 1) * 128)
                            S.op("pe", lambda e: e.matmul(po_[:, hs], w["vv"][:, hs], w["AT"][:, h, :], start=True, stop=True), reads=["h_v" + x, "h_AT" + x], writes=[pon_], pe_acc=(h > 0))
                        S.op("act", lambda e: e.copy(w["tmpo"][:], po_[:]), reads=[pon_], writes=["h_tmpo" + x])

            def chunk_step(rec, n_):
                k = rec["k"]; sp = rec["sp"]
                pss = []
                for d in range(2):
                    w = W_[d, k]; x = w["sfx"]
                    j = n_ if d == 0 else 3 - n_
                    js = slice(j * 32, (j + 1) * 32)
                    pi_, pin_ = b.acc(2 + d)
                    ps_, psn2 = b.ps()
                    pss.append((ps_, psn2, j))
                    for h in range(4):
                        hs = slice(h * 128, (h + 1) * 128)
                        S.op("pe", lambda e: e.matmul(pi_[:, h * 128 + j * 32:h * 128 + (j + 1) * 32], Sst[sp][:, d, h, :], w["qt"][:, h, js], start=True, stop=True),
                             reads=[sstn(sp, d, h), "h_qt" + x], writes=[pin_], pe_acc=(n_ > 0 or h > 0))
                        S.op("pe", lambda e: e.matmul(ps_[:, hs], w["khm"][:, h, j, :], w["vv"][:, hs], start=True, stop=True),
                             reads=["h_khm" + x, "h_v" + x], writes=[psn2], pe_acc=(h > 0))
                for d in range(2):
                    w = W_[d, k]; x = w["sfx"]
                    ps_, psn2, j = pss[d]
                    for h in range(4):
                        hs = slice(h * 128, (h + 1) * 128)
                        S.op("dve", lambda e: e.scalar_tensor_tensor(Sst[sp][:, d, h, :], Sst[sp][:, d, h, :], etot[k][:, d, h, j:j + 1], ps_[:, hs], ALU.mult, ALU.add),
                             reads=[psn2, "h_etot%d" % k, sstn(sp, d, h)], writes=[sstn(sp, d, h)])

            def finish(rec):
                k = rec["k"]
                for d in range(2):
                    w = W_[d, k]; x = w["sfx"]
                    pi_, pin_ = b.acc(2 + d)
                    tl = rec["ti"][d] * 128
                    S.op("dve", lambda e: e.tensor_tensor(oall[:, d, :, tl:tl + 128], w["tmpo"][:].rearrange("p (h t) -> p h t", h=4),
                                                          pi_[:].rearrange("p (h t) -> p h t", h=4), ALU.add),
                         reads=[pin_, "h_tmpo" + x], writes=["oall"])

            def seq_final(sq_):
                T0 = sq_ * L
                sp = sq_ % 2
                for c in range(L // 256):
                    cs = slice(c * 256, (c + 1) * 256)
                    os_, gT, sq, rs, ob = fin["os"], fin["gT"], fin["sq"], fin["rs"], fin["ob"]
                    S.dma("act", gT[:], UT[OFF["hg"]:OFF["hg"] + 512, T0 + c * 256:T0 + (c + 1) * 256].rearrange("(h p) t -> p h t", p=128), writes=["f_gT"])
                    S.op("act", lambda e: e.activation(gT[:], gT[:], AF.Silu), reads=["f_gT"], writes=["f_gT"])
                    S.op("dve", lambda e: e.tensor_tensor(os_[:], oall[:, 0, :, cs], oall[:, 1, :, cs], ALU.add), reads=["oall"], writes=["f_os"])
                    S.op("act", lambda e: e.activation(sq[:], os_[:], AF.Square), reads=["f_os"], writes=["f_sq"])
                    for hh in range(2):
                        pn_, pnn = b.ps()
                        for h2 in range(2):
                            h = hh * 2 + h2
                            S.op("pe", lambda e: e.matmul(pn_[:, h2 * 256:(h2 + 1) * 256], ones_bf[:], sq[:, h, :], start=True, stop=True),
                                 reads=["f_sq", "ones_bf"], writes=[pnn], pe_acc=(h2 > 0))
                        rstd_from_sumsq(pn_, pnn, rs[:, hh * 2:hh * 2 + 2, :].rearrange("p a t -> p (a t)"), "f_rs", 128, 512, 1.0 / 128)
                    for h in range(4):
                        S.op("dve", lambda e: e.scalar_tensor_tensor(os_[:, h, :], os_[:, h, :], hn[:, h:h + 1], rs[:, h, :], ALU.mult, ALU.mult),
                             reads=["f_os", "hn", "f_rs"], writes=["f_os"])
                    S.op("dve", lambda e: e.tensor_tensor(ob[:], os_[:], gT[:], ALU.mult), reads=["f_os", "f_gT"], writes=["f_ob"])
                    S.dma("sp", OT[0, :, T0 + c * 256:T0 + (c + 1) * 256].rearrange("(h p) t -> p h t", p=128), ob[:], reads=["f_ob"], writes=[])
                if g == "p":
                    S.dma("sp", D["o_hgrn"][l, sq_].rearrange("d h k v -> k d h v"), Sst[sp][:], reads=[sstn(sp, d_, h_) for d_ in range(2) for h_ in range(4)], writes=[])

            recs = []
            for sq_ in range(nseq):
                for i in range(NTL):
                    recs.append(dict(sq=sq_, sp=sq_ % 2, T0=sq_ * L, i=i, ti=(i, NTL - 1 - i), k=len(recs) % 2))
            partA(recs[0])
            for stg in range(4):
                partB(recs[0], stg)
            for n, rec in enumerate(recs):
                nxt = recs[n + 1] if n + 1 < len(recs) else None
                if rec["i"] == 0:
                    sp = rec["sp"]
                    names = [sstn(sp, d_, h_) for d_ in range(2) for h_ in range(4)]
                    if P:
                        S.dma("sp", Sst[sp][:], D["st_hgrn"][l].rearrange("d h k v -> k d h v"), writes=names)
                    else:
                        S.op("pool", lambda e: e.memset(Sst[sp][:], 0.0), writes=names)
                if nxt is not None:
                    partA(nxt)
                for n_ in range(4):
                    chunk_step(rec, n_)
                    if nxt is not None:
                        partB(nxt, n_)
                finish(rec)
                if rec["i"] == NTL - 1:
                    seq_final(rec["sq"])
            b.nrot = 6
            S.barrier()

    def epilogue(st, z, zn, xt, xn, gco, Xdst, c0, wkn):
        sq, rs, tmp = wkn
        for kc in range(8):
            S.op("act", lambda e: e.activation(sq[:, kc, :], z[:, kc, :], AF.Square), reads=(zn if isinstance(zn, list) else [zn]), writes=["e_sq"])
        pt, pn = b.ps()
        for kc in range(8):
            S.op("pe", lambda e: e.matmul(pt[:], ones_bf[:], sq[:, kc, :], start=(kc == 0), stop=(kc == 7)), reads=["e_sq", "ones_bf"], writes=[pn], pe_acc=True)
        rstd_from_sumsq(pt, pn, rs, "e_rs", 128, 512, 1.0 / DM)
        for kc in range(8):
            S.op("dve", lambda e: e.tensor_tensor(tmp[:], z[:, kc, :], rs[:], ALU.mult), reads=(zn if isinstance(zn, list) else [zn]) + ["e_rs"], writes=["e_tmp"])
            S.op("dve", lambda e: e.scalar_tensor_tensor(xt[:, kc, :], tmp[:], gco[:, kc:kc + 1], xt[:, kc, :], ALU.mult, ALU.add),
                 reads=["e_tmp", "coef", xn], writes=[xn])
        S.dma("sp", Xdst[:, c0:c0 + 512].rearrange("(kc p) t -> p kc t", p=128), xt[:], reads=[xn], writes=[])

    def stage_merge(g, l, Xsrc, Xdst):
        G = GR[g]
        T = G["T"]
        UT, OT = D["UT_" + g], D["OT_" + g]
        with ExitStack() as st:
            Wb = b.sb(st, "Wb", [128, 4, 4, 1024], BF16)
            Wo = b.sb(st, "Wo", [128, 8, 1024], BF16)
            S.dma("pool", Wb[:], D["w_branch"][l].rearrange("n (kc p) c -> p n kc c", p=128), writes=["Wb"])
            S.dma("pool", Wo[:], D["w_out"][l].rearrange("(kc p) c -> p kc c", p=128), writes=["Wo"])
            ot = b.sb(st, "ot", [128, 4, 4, 512], BF16)
            yp = b.sb(st, "yp", [128, 8, 512], BF16)
            gts = [b.sb(st, "gt%d" % i, [128, 512]) for i in range(3)]
            accf = b.sb(st, "accf", [128, 512]); tm2 = b.sb(st, "tm2", [128, 512])
            z = b.sb(st, "z", [128, 8, 512]); xt = b.sb(st, "xt", [128, 8, 512])
            wkn = (b.sb(st, "e_sq", [128, 8, 512], BF16), b.sb(st, "e_rs", [128, 512]), b.sb(st, "e_tmp", [128, 512]))
            ng = 0
            for tt in range(T // 512):
                cs = slice(tt * 512, (tt + 1) * 512)
                S.dma("sp", ot[:], OT[:, :, cs].rearrange("n (kc p) t -> p n kc t", p=128), writes=["ot"])
                S.dma("act", xt[:], Xsrc[:, cs].rearrange("(kc p) t -> p kc t", p=128), writes=["xt"])
                for dmc in range(8):
                    for n in range(4):
                        gt, gtn = gts[ng % 3], "gt%d" % (ng % 3)
                        ng += 1
                        r0 = OFF["gates"] + n * 1024 + dmc * 128
                        S.dma("sp", gt[:], UT[r0:r0 + 128, cs], writes=[gtn])
                        S.op("act", lambda e: e.activation(gt[:], gt[:], AF.Sigmoid), reads=[gtn], writes=[gtn])
                        pt, pn = b.ps()
                        for kc in range(4):
                            S.op("pe", lambda e: e.matmul(pt[:], Wb[:, n, kc, dmc * 128:(dmc + 1) * 128], ot[:, n, kc, :], start=(kc == 0), stop=(kc == 3)),
                                 reads=["Wb", "ot"], writes=[pn], pe_acc=True)
                        if n == 0:
                            S.op("dve", lambda e: e.tensor_tensor(accf[:], pt[:], gt[:], ALU.mult), reads=[pn, gtn], writes=["accf"])
                        elif n < 3:
                            S.op("dve", lambda e: e.tensor_tensor(tm2[:], pt[:], gt[:], ALU.mult), reads=[pn, gtn], writes=["tm2"])
                            S.op("dve", lambda e: e.tensor_tensor(accf[:], accf[:], tm2[:], ALU.add), reads=["tm2", "accf"], writes=["accf"])
                        else:
                            S.op("dve", lambda e: e.tensor_tensor(tm2[:], pt[:], gt[:], ALU.mult), reads=[pn, gtn], writes=["tm2"])
                            S.op("dve", lambda e: e.tensor_tensor(yp[:, dmc, :], accf[:], tm2[:], ALU.add), reads=["tm2", "accf"], writes=["yp"])
                for oc in range(8):
                    pt, pn = b.ps()
                    for kc in range(8):
                        S.op("pe", lambda e: e.matmul(pt[:], Wo[:, kc, oc * 128:(oc + 1) * 128], yp[:, kc, :], start=(kc == 0), stop=(kc == 7)),
                             reads=["Wo", "yp"], writes=[pn], pe_acc=True)
                    S.op("act", lambda e: e.copy(z[:, oc, :], pt[:]), reads=[pn], writes=["z"])
                epilogue(st, z, "z", xt, "xt", coef[:, l, G["cond"], 2], Xdst, tt * 512, wkn)
            S.barrier()

    def stage_mlp(g, l, Xsrc, Xdst):
        G = GR[g]
        T = G["T"]
        NT = T // 512
        with ExitStack() as st:
            if T <= 1024:
                h2, hemit = stage_h(st, g, l, Xsrc, 3, 4, lazy=True)
            else:
                h2 = stage_h(st, g, l, Xsrc, 3, 4)
                hemit = [(lambda: None)] * NT
            zall = b.sb(st, "zall", [128, 8, T])
            w1 = [b.sb(st, "w1_%d" % i, [128, 8, 512], BF16) for i in range(2)]
            w2 = [b.sb(st, "w2_%d" % i, [128, 4, 1024], BF16) for i in range(2)]
            hid = [b.sb(st, "hid%d" % i, [128, 4, 512], BF16) for i in range(2)]
            rl = [b.sb(st, "rl%d" % i, [128, 512]) for i in range(2)]
            xt = b.sb(st, "xt", [128, 8, 512])
            wkn = (b.sb(st, "e_sq", [128, 8, 512], BF16), b.sb(st, "e_rs", [128, 512]), b.sb(st, "e_tmp", [128, 512]))
            nh = 0
            nr = 0
            for cg in range(8):
                wa, wan = w1[cg % 2], "w1_%d" % (cg % 2)
                wb_, wbn = w2[cg % 2], "w2_%d" % (cg % 2)
                S.dma("pool", wa[:], D["w_mlp_in"][l, :, cg * 512:(cg + 1) * 512].rearrange("(kc p) c -> p kc c", p=128), writes=[wan])
                S.dma("pool", wb_[:], D["w_mlp_out"][l, cg * 512:(cg + 1) * 512, :].rearrange("(fc p) c -> p fc c", p=128), writes=[wbn])
                if cg == 0:
                    hemit[0]()
                for tt in range(NT):
                    if cg == 0 and tt + 1 < NT:
                        hemit[tt + 1]()
                    cs = slice(tt * 512, (tt + 1) * 512)
                    hd, hdn = hid[nh % 2], "hid%d" % (nh % 2)
                    nh += 1
                    for cc in range(4):
                        pt, pn = b.ps()
                        for kc in range(8):
                            S.op("pe", lambda e: e.matmul(pt[:], wa[:, kc, cc * 128:(cc + 1) * 128], h2[:, kc, cs], start=(kc == 0), stop=(kc == 7)),
                                 reads=[wan, "hT%d" % tt], writes=[pn], pe_acc=(kc > 0))
                        r, rn = rl[nr % 2], "rl%d" % (nr % 2)
                        nr += 1
                        S.op("act", lambda e: e.activation(r[:], pt[:], AF.Relu), reads=[pn], writes=[rn])
                        S.op("dve", lambda e: e.tensor_tensor(hd[:, cc, :], r[:], r[:], ALU.mult), reads=[rn], writes=[hdn])
                    for oc in range(8):
                        pt, pn = b.ps()
                        for fc in range(4):
                            S.op("pe", lambda e: e.matmul(pt[:], wb_[:, fc, oc * 128:(oc + 1) * 128], hd[:, fc, :], start=(fc == 0), stop=(fc == 3)),
                                 reads=[wbn, hdn], writes=[pn], pe_acc=(fc > 0))
                        zn = "z%d_%d" % (tt, oc)
                        if cg == 0:
                            S.op("act", lambda e: e.copy(zall[:, oc, cs], pt[:]), reads=[pn], writes=[zn])
                        else:
                            S.op("dve", lambda e: e.tensor_tensor(zall[:, oc, cs], zall[:, oc, cs], pt[:], ALU.add), reads=[pn, zn], writes=[zn])
            for tt in range(NT):
                cs = slice(tt * 512, (tt + 1) * 512)
                S.dma("act", xt[:], Xsrc[:, cs].rearrange("(kc p) t -> p kc t", p=128), writes=["xt"])
                epilogue(st, zall[:, :, cs], ["z%d_%d" % (tt, oc) for oc in range(8)], xt, "xt", coef[:, l, G["cond"], 5], Xdst, tt * 512, wkn)
            S.barrier()

    for g in ("s", "p"):
        X = D["xT_" + g]
        for l in range(DEPTH):
            if "proj" in STAGES:
                with ExitStack() as st:
                    hT, hemit = stage_h(st, g, l, X, 0, 1, lazy=True)
                    stage_proj(g, l, hT, hemit)
            if "hgrn" in STAGES:
                stage_hgrn(g, l)
            if "swa" in STAGES:
                stage_gqa_like(g, l, "swa")
            if "mla" in STAGES:
                stage_mla(g, l)
            if "gqa" in STAGES:
                stage_gqa_like(g, l, "gqa")
            if "merge" in STAGES:
                stage_merge(g, l, X, D["X1_" + g])
            Xn = D["yT_" + g] if l == DEPTH - 1 else D["X2_%d_%s" % (l, g)]
            if "mlp" in STAGES:
                stage_mlp(g, l, D["X1_" + g], Xn)
            X = Xn
    S.barrier()
    return nc, b


_CACHE = {}


def _consts():
    nf = 16
    t = np.arange(2048)
    row, col = (t // 64).astype(np.float32), (t % 64).astype(np.float32)
    rope = np.zeros((4, 128, 2048), np.float32)
    def fill(ci, si, r0, nf):
        inv = (10000.0 ** (-np.arange(nf, dtype=np.float32) / nf)).astype(np.float32)
        ar = (row[None, :] * inv[:, None]).astype(np.float32)
        ac = (col[None, :] * inv[:, None]).astype(np.float32)
        for k, a in enumerate((ar, ac)):
            b0 = r0 + k * 2 * nf
            rope[ci, b0:b0 + nf] = np.cos(a); rope[ci, b0 + nf:b0 + 2 * nf] = np.cos(a)
            rope[si, b0:b0 + nf] = -np.sin(a); rope[si, b0 + nf:b0 + 2 * nf] = np.sin(a)
    fill(0, 1, 0, 16)
    fill(0, 1, 64, 16)
    fill(2, 3, 64, 8)
    s_ = np.arange(128)[:, None]; t_ = np.arange(128)[None, :]
    same = (s_ // 32) == (t_ // 32)
    cm = np.zeros((6, 128, 128), np.float32)
    cm[0] = np.eye(128)
    cm[1] = same & (s_ <= t_)
    cm[2] = same & (s_ >= t_)
    cm[3] = same & (s_ > t_)
    cm[4] = same & (s_ < t_)
    cm[5][:, 0:4] = (s_ // 32) == np.arange(4)[None, :]
    j = np.arange(128)[:, None]; i = (np.arange(512) % 128)[None, :]
    swm = np.stack([(j >= i), (j <= i)]).astype(np.float32).astype(ml_dtypes.bfloat16)
    return rope, cm, swm


def _perm(nd):
    q = nd // 4
    return np.concatenate([np.arange(q, 2 * q), np.arange(0, q), np.arange(3 * q, 4 * q), np.arange(2 * q, 3 * q)])


def kernel(**inp):
    f = lambda a: np.ascontiguousarray(np.asarray(a, dtype=np.float32))
    I = {k: f(v) for k, v in inp.items()}
    if "prog" not in _CACHE:
        _CACHE["prog"] = build_program()
    nc, b = _CACHE["prog"]
    rope, cm, swm = _consts()
    fm = lambda v, n: f(v.reshape(v.shape[0], n, 128).transpose(0, 2, 1))
    shared = dict(
        w_ada=I["w_ada"], b_adaT=fm(I["b_ada"], 48),
        gains=f(np.stack([fm(I[k], 8) for k in ("norm_mix_pre", "norm_mix_post", "norm_mlp_pre", "norm_mlp_post")], axis=2)),
        w_in=I["w_in"], lbT=f(np.stack([I["hgrn_lb_fwd"], I["hgrn_lb_bwd"]], axis=1)),
        hgrn_normT=fm(I["hgrn_norm"], 4), sink=f(I["swa_sink"][:, None, :]),
        mla_qnT=fm(I["mla_q_norm"], 2), mla_kvn=f(I["mla_kv_norm"][:, :, None]),
        w_uq=I["mla_w_uq"], w_ukv=I["mla_w_ukv"],
        gqa_qn=f(np.tile(np.stack([I["gqa_q_norm"], I["gqa_q_norm"][:, _perm(64)]], axis=2), (1, 2, 1))),
        gqa_kn=f(np.tile(np.stack([I["gqa_k_norm"], I["gqa_k_norm"][:, _perm(64)]], axis=2), (1, 2, 1))),
        w_branch=I["w_branch"], w_out=I["w_out"], w_mlp_in=I["w_mlp_in"], w_mlp_out=I["w_mlp_out"],
        rope64=rope, cmat=cm, swamask=swm,
    )
    wsw = I["mla_w_uq"].reshape(DEPTH, 256, 8, 96).copy()
    wsw[..., 64:96] = wsw[..., 64:96][..., _perm(32)]
    shared["w_uq_sw"] = f(wsw.reshape(DEPTH, 256, 768))
    in_maps = []
    for i in range(NCORE):
        bb = i % 2
        m = dict(shared)
        m["xT_s"] = f(I["x_sample"][bb].T)
        m["xT_p"] = f(I["x_prompt"][4 * i:4 * i + 4].reshape(1024, DM).T)
        cc = np.stack([I["c_ctx"], I["c"][bb]], axis=1)
        m["cT"] = f(cc.reshape(8, 128, 2).transpose(1, 0, 2))
        m["st_hgrn"] = f(I["state_hgrn"][bb])
        m["c_swa_kT"] = f(I["cache_swa_k"][bb].transpose(0, 2, 3, 1))
        m["c_swa_v"] = f(I["cache_swa_v"][bb].reshape(DEPTH, 512, 128))
        m["c_ckvT"] = f(I["cache_mla_ckv"][bb].transpose(0, 2, 1))
        m["c_krT"] = f(I["cache_mla_kr"][bb].transpose(0, 2, 1))
        m["c_gqa_kT"] = f(I["cache_gqa_k"][bb].transpose(0, 2, 3, 1))
        m["c_gqa_v"] = f(I["cache_gqa_v"][bb].reshape(DEPTH, 512, 128))
        in_maps.append(m)
    res = run_bass_kernel_spmd(nc, in_maps, core_ids=list(range(NCORE)))
    R = res.results
    y_prompt = np.concatenate([R[i]["yT_p"].T.reshape(4, 256, DM) for i in range(NCORE)], axis=0)
    y_sample = np.stack([R[0]["yT_s"].T, R[1]["yT_s"].T], axis=0)
    cat = lambda fn: np.ascontiguousarray(np.concatenate([fn(R[i]) for i in range(NCORE)], axis=0).astype(np.float32))
    n_hgrn = cat(lambda r: r["o_hgrn"].transpose(1, 0, 2, 3, 4, 5))
    kT = lambda a: a.reshape(DEPTH, 2, 64, 4, 256).transpose(3, 0, 4, 1, 2)
    vv = lambda a: a.reshape(DEPTH, 4, 256, 2, 64).transpose(1, 0, 2, 3, 4)
    n_swa_k = cat(lambda r: kT(r["o_swa_kT"]))
    n_swa_v = cat(lambda r: vv(r["o_swa_v"]))
    n_ckv = cat(lambda r: r["o_ckvT"].reshape(DEPTH, 128, 4, 256).transpose(2, 0, 3, 1))
    n_kr = cat(lambda r: r["o_krT"].reshape(DEPTH, 32, 4, 256).transpose(2, 0, 3, 1))
    n_gqa_k = cat(lambda r: kT(r["o_gqa_kT"]))
    n_gqa_v = cat(lambda r: vv(r["o_gqa_v"]))
    return (np.ascontiguousarray(y_prompt.astype(np.float32)), np.ascontiguousarray(y_sample.astype(np.float32)),
            n_hgrn, n_swa_k, n_swa_v, n_ckv, n_kr, n_gqa_k, n_gqa_v)
```

```python
import numpy as np
from contextlib import ExitStack
import ml_dtypes
import concourse.bass as bass
import concourse.mybir as mybir
from concourse.bass_utils import run_bass_kernel_spmd

F32 = mybir.dt.float32
BF16 = mybir.dt.bfloat16
AF = mybir.ActivationFunctionType
ALU = mybir.AluOpType

DM = 1024
DEPTH = 2
NCORE = 8
EPS = 1e-6
OFF = dict(hq=0, ff=512, fb=1024, hi=1536, hg=2048, sq=2560, sk=3072, sv=3200, cq=3328, ckv=3584, kr=3712,
           gq=3744, gk=4256, gv=4384, gates=4512)
D_IN = 8608
UTM_COLS = 1792
STAGES = {"proj", "hgrn", "swa", "mla", "gqa", "merge", "mlp"}


class Sched:
    NDMA = 24

    def __init__(self, nc, es):
        self.nc = nc
        self.eng = {"pe": nc.tensor, "act": nc.scalar, "dve": nc.vector, "pool": nc.gpsimd, "sp": nc.sync}
        self.sem = {k: es.enter_context(nc.semaphore("s_" + k)) for k in ("pe", "act", "dve", "pool")}
        self.cnt = {k: 0 for k in self.sem}
        self.dsem = [es.enter_context(nc.semaphore("d%d" % i)) for i in range(self.NDMA)]
        self.dcnt = [0] * self.NDMA
        self.dnext = 0
        self.seen = {k: {} for k in self.eng}
        self.lastw = {}
        self.reads = {}
        self.n_instr = 0

    def _sem_of(self, key):
        return self.sem[key] if isinstance(key, str) else self.dsem[key[1]]

    def _wait(self, e, tok):
        key, val = tok
        if self.seen[e].get(key, 0) >= val:
            return
        self.eng[e].wait_ge(self._sem_of(key), val)
        self.seen[e][key] = val

    def _deps(self, e, reads, writes, pe_acc=False):
        best = {}
        for r in reads:
            t = self.lastw.get(r)
            if t is not None and best.get(t[0], 0) < t[1]:
                best[t[0]] = t[1]
        for w in writes:
            t = self.lastw.get(w)
            if t is not None and best.get(t[0], 0) < t[1]:
                if not (pe_acc and t[0] == "pe"):
                    best[t[0]] = t[1]
            for t in self.reads.get(w, ()):
                if best.get(t[0], 0) < t[1]:
                    best[t[0]] = t[1]
        for key, val in best.items():
            self._wait(e, (key, val))

    def _record(self, tok, reads, writes):
        for r in reads:
            lst = self.reads.setdefault(r, [])
            lst[:] = [t for t in lst if t[0] != tok[0]]
            lst.append(tok)
        for w in writes:
            self.lastw[w] = tok
            self.reads[w] = []

    def op(self, e, fn, reads=(), writes=(), pe_acc=False):
        self._deps(e, reads, writes, pe_acc)
        ins = fn(self.eng[e])
        self.cnt[e] += 1
        ins.then_inc(self.sem[e], 1)
        self._record((e, self.cnt[e]), reads, writes)
        self.n_instr += 1
        return ins

    def dma(self, q, out, in_, reads=(), writes=()):
        i = self.dnext
        self.dnext = (self.dnext + 1) % self.NDMA
        if self.dcnt[i] > 0:
            self._wait(q, (("d", i), self.dcnt[i]))
        self._deps(q, reads, writes)
        ins = self.eng[q].dma_start(out=out, in_=in_)
        self.dcnt[i] += 16
        ins.then_inc(self.dsem[i], 16)
        self._record((("d", i), self.dcnt[i]), reads, writes)
        self.n_instr += 1
        return ins

    def barrier(self):
        best = {}
        for k in self.cnt:
            if self.cnt[k]:
                best[k] = self.cnt[k]
        for i in range(self.NDMA):
            if self.dcnt[i]:
                best[("d", i)] = self.dcnt[i]
        for e in self.eng:
            for key, val in best.items():
                self._wait(e, (key, val))
        self.lastw = {}
        self.reads = {}


class B:
    def __init__(self, nc, es):
        self.nc, self.es = nc, es
        self.S = Sched(nc, es)
        self.D = {}
        self.psn = 0

    def din(self, name, shape, dt=F32):
        self.D[name] = self.nc.dram_tensor(name, list(shape), dt, kind="ExternalInput").ap()
        return self.D[name]

    def dout(self, name, shape, dt=F32):
        self.D[name] = self.nc.dram_tensor(name, list(shape), dt, kind="ExternalOutput").ap()
        return self.D[name]

    def dscr(self, name, shape, dt=F32):
        self.D[name] = self.nc.dram_tensor(name, list(shape), dt, kind="Internal").ap()
        return self.D[name]

    def sb(self, st, name, shape, dt=F32):
        self.uid = getattr(self, "uid", 0) + 1
        return st.enter_context(self.nc.sbuf_tensor("sb%d_%s" % (self.uid, name), list(shape), dt))

    def ps(self):
        i = self.psn % self.nrot
        self.psn += 1
        return self.psum[i], "ps%d" % i

    nrot = 6

    def acc(self, i):
        k = (6, 7, 4, 5)[i]
        return self.psum[k], "ps%d" % k


def build_program():
    nc = bass.Bass("TRN2", target_bir_lowering=False)
    es = ExitStack()
    b = B(nc, es)
    S = b.S
    D = b.D
    GR = {
        "s": dict(T=2048, nseq=1, L=2048, P=512, rope=True, cond=1),
        "p": dict(T=1024, nseq=4, L=256, P=0, rope=False, cond=0),
    }
    for g, G in GR.items():
        T = G["T"]
        b.din("xT_" + g, [DM, T])
        b.dout("yT_" + g, [DM, T])
        b.dscr("X1_" + g, [DM, T])
        for l in range(DEPTH - 1):
            b.dscr("X2_%d_%s" % (l, g), [DM, T])
        b.dscr("UT_" + g, [68 * 128, T])
        b.dscr("UTM_" + g, [T, UTM_COLS])
        b.dscr("OT_" + g, [4, 512, T], BF16)
        b.dscr("YP_" + g, [DM, T], BF16)
    b.din("cT", [128, 8, 2])
    b.din("w_ada", [DEPTH, DM, 6 * DM])
    b.din("b_adaT", [DEPTH, 128, 48])
    b.din("gains", [DEPTH, 128, 4, 8])
    b.din("w_in", [DEPTH, DM, D_IN])
    b.din("lbT", [DEPTH, 2, 512])
    b.din("hgrn_normT", [DEPTH, 128, 4])
    b.din("sink", [DEPTH, 1, 8])
    b.din("mla_qnT", [DEPTH, 128, 2])
    b.din("mla_kvn", [DEPTH, 128, 1])
    b.din("w_uq", [DEPTH, 256, 768])
    b.din("w_uq_sw", [DEPTH, 256, 768])
    b.din("w_ukv", [DEPTH, 128, 1024])
    b.din("gqa_qn", [DEPTH, 128, 2])
    b.din("gqa_kn", [DEPTH, 128, 2])
    b.din("w_branch", [DEPTH, 4, 512, DM])
    b.din("w_out", [DEPTH, DM, DM])
    b.din("w_mlp_in", [DEPTH, DM, 4 * DM])
    b.din("w_mlp_out", [DEPTH, 4 * DM, DM])
    b.din("st_hgrn", [DEPTH, 2, 4, 128, 128])
    b.din("c_swa_kT", [DEPTH, 2, 64, 512])
    b.din("c_swa_v", [DEPTH, 512, 128])
    b.din("c_ckvT", [DEPTH, 128, 512])
    b.din("c_krT", [DEPTH, 32, 512])
    b.din("c_gqa_kT", [DEPTH, 2, 64, 512])
    b.din("c_gqa_v", [DEPTH, 512, 128])
    b.din("rope64", [4, 128, 2048])
    b.din("cmat", [6, 128, 128])
    b.din("swamask", [2, 128, 512], BF16)
    b.dout("o_hgrn", [DEPTH, 4, 2, 4, 128, 128])
    b.dout("o_swa_kT", [DEPTH, 128, 1024])
    b.dout("o_swa_v", [DEPTH, 1024, 128])
    b.dout("o_ckvT", [DEPTH, 128, 1024])
    b.dout("o_krT", [DEPTH, 32, 1024])
    b.dout("o_gqa_kT", [DEPTH, 128, 1024])
    b.dout("o_gqa_v", [DEPTH, 1024, 128])

    b.psum = [es.enter_context(nc.psum_tensor("psb%d" % i, [128, 512], F32)) for i in range(8)]

    cst = ExitStack()
    es.enter_context(cst)
    ones_bf = b.sb(cst, "ones_bf", [128, 128], BF16)
    ones_f = b.sb(cst, "ones_f", [128, 128], F32)
    cm = b.sb(cst, "cm", [128, 6, 128], F32)
    modT = b.sb(cst, "modT", [128, DEPTH, 48, 2], F32)
    gains = b.sb(cst, "gains", [128, DEPTH, 4, 8], F32)
    coef = b.sb(cst, "coef", [128, DEPTH, 2, 6, 8], F32)
    S.op("pool", lambda e: e.memset(ones_bf[:], 1.0), writes=["ones_bf"])
    onesAB = b.sb(cst, "onesAB", [128, 2, 128], BF16)
    S.op("pool", lambda e: e.memset(onesAB[:], 0.0), writes=["onesAB"])
    S.op("pool", lambda e: e.memset(onesAB[:, 0, 0:64], 1.0), writes=["onesAB"])
    S.op("pool", lambda e: e.memset(onesAB[:, 1, 64:128], 1.0), writes=["onesAB"])
    S.op("pool", lambda e: e.memset(ones_f[:], 1.0), writes=["ones_f"])
    S.dma("sp", cm[:], D["cmat"].rearrange("a p c -> p a c"), writes=["cm"])
    S.dma("sp", gains[:], D["gains"].rearrange("l p a c -> p l a c"), writes=["gains"])

    with ExitStack() as st:
        cT = b.sb(st, "cT", [128, 8, 2])
        scT = b.sb(st, "scT", [128, 8, 2])
        badaT = b.sb(st, "badaT", [128, DEPTH, 48])
        wa = [b.sb(st, "wa%d" % i, [128, 8, 768]) for i in range(2)]
        S.dma("sp", cT[:], D["cT"], writes=["cT"])
        S.dma("sp", badaT[:], D["b_adaT"].rearrange("l p j -> p l j"), writes=["badaT"])
        S.op("act", lambda e: e.activation(scT[:], cT[:], AF.Silu), reads=["cT"], writes=["scT"])
        n = 0
        for l in range(DEPTH):
            pt, pn = b.ps()
            for cg in range(8):
                w = wa[n % 2]
                wn = "wa%d" % (n % 2)
                n += 1
                S.dma("sp" if cg % 2 == 0 else "act", w[:],
                      D["w_ada"][l, :, cg * 768:(cg + 1) * 768].rearrange("(kc p) c -> p kc c", p=128), writes=[wn])
                for jj in range(6):
                    j = cg * 6 + jj
                    for kc in range(8):
                        S.op("pe", lambda e: e.matmul(pt[:, 2 * j:2 * j + 2], w[:, kc, jj * 128:(jj + 1) * 128], scT[:, kc, :],
                                                      start=(kc == 0), stop=(kc == 7)),
                             reads=[wn, "scT"], writes=[pn], pe_acc=True)
            S.op("dve", lambda e: e.tensor_tensor(modT[:, l], pt[:, 0:96].rearrange("p (j c) -> p j c", c=2),
                                                  badaT[:, l].unsqueeze(2).to_broadcast([128, 48, 2]), ALU.add),
                 reads=[pn, "badaT"], writes=["modT"])
        for l in range(DEPTH):
            for c in range(2):
                m = lambda i: modT[:, l, i * 8:(i + 1) * 8, c]
                S.op("dve", lambda e: e.scalar_tensor_tensor(coef[:, l, c, 0], m(1), 1.0, gains[:, l, 0], ALU.add, ALU.mult),
                     reads=["modT", "gains"], writes=["coef"])
                S.op("dve", lambda e: e.tensor_copy(coef[:, l, c, 1], m(0)), reads=["modT"], writes=["coef"])
                S.op("dve", lambda e: e.tensor_tensor(coef[:, l, c, 2], m(2), gains[:, l, 1], ALU.mult), reads=["modT", "gains"], writes=["coef"])
                S.op("dve", lambda e: e.scalar_tensor_tensor(coef[:, l, c, 3], m(4), 1.0, gains[:, l, 2], ALU.add, ALU.mult),
                     reads=["modT", "gains"], writes=["coef"])
                S.op("dve", lambda e: e.tensor_copy(coef[:, l, c, 4], m(3)), reads=["modT"], writes=["coef"])
                S.op("dve", lambda e: e.tensor_tensor(coef[:, l, c, 5], m(5), gains[:, l, 3], ALU.mult), reads=["modT", "gains"], writes=["coef"])
        S.barrier()

    def rstd_from_sumsq(pt, pn, out, outn, npart, ncol, inv_n):
        S.op("act", lambda e: e.activation(out[0:npart, 0:ncol], pt[0:npart, 0:ncol], AF.Ln, bias=epsb[0:npart, :], scale=inv_n),
             reads=[pn, "epsb"], writes=[outn])
        S.op("act", lambda e: e.activation(out[0:npart, 0:ncol], out[0:npart, 0:ncol], AF.Exp, scale=-0.5), reads=[outn], writes=[outn])

    epsb = b.sb(cst, "epsb", [128, 1], F32)
    S.op("pool", lambda e: e.memset(epsb[:], EPS), writes=["epsb"])

    def norm_mod_tile(st_unused, xt, xn, hT, hn, t0, a_ap, sh_ap, tmp, sq, rs):
        for kc in range(8):
            S.op("act", lambda e: e.activation(sq[:, kc, :], xt[:, kc, :], AF.Square), reads=[xn], writes=["sq"])
        pt, pn = b.ps()
        for kc in range(8):
            S.op("pe", lambda e: e.matmul(pt[:], ones_bf[:], sq[:, kc, :], start=(kc == 0), stop=(kc == 7)),
                 reads=["sq", "ones_bf"], writes=[pn], pe_acc=True)
        rstd_from_sumsq(pt, pn, rs, "rs", 128, 512, 1.0 / DM)
        for kc in range(8):
            S.op("dve", lambda e: e.tensor_tensor(tmp[:], xt[:, kc, :], rs[:], ALU.mult), reads=[xn, "rs"], writes=["tmp"])
            S.op("act", lambda e: e.activation(hT[:, kc, t0:t0 + 512], tmp[:], AF.Identity, bias=sh_ap[:, kc:kc + 1], scale=a_ap[:, kc:kc + 1]),
                 reads=["tmp", "coef"], writes=[hn])

    def stage_h(st, g, l, Xsrc, ia, ish, lazy=False):
        G = GR[g]
        T = G["T"]
        hT = b.sb(st, "hT", [128, 8, T], BF16)
        s2 = st if lazy else ExitStack()
        xts = [b.sb(s2, "xt%d" % i, [128, 8, 512]) for i in range(2)]
        tmp = b.sb(s2, "tmp", [128, 512])
        sq = b.sb(s2, "sq", [128, 8, 512], BF16)
        rs = b.sb(s2, "rs", [128, 512])

        def emit(tt):
            xt, xn = xts[tt % 2], "xt%d" % (tt % 2)
            S.dma("sp", xt[:], Xsrc[:, tt * 512:(tt + 1) * 512].rearrange("(kc p) t -> p kc t", p=128), writes=[xn])
            norm_mod_tile(None, xt, xn, hT, "hT%d" % tt, tt * 512, coef[:, l, G["cond"], ia], coef[:, l, G["cond"], ish], tmp, sq, rs)

        if lazy:
            return hT, [(lambda tt=tt: emit(tt)) for tt in range(T // 512)]
        for tt in range(T // 512):
            emit(tt)
        S.barrier()
        s2.close()
        return hT

    def stage_proj(g, l, hT, hemit):
        G = GR[g]
        T = G["T"]
        UT, UTM = D["UT_" + g], D["UTM_" + g]
        tm_groups = {1: [(0, 512, 0)], 2: [(0, 512, 512)], 3: [(0, 512, 1024)], 6: [(128, 128, 1536)], 8: [(288, 128, 1664)]}
        with ExitStack() as st:
            wb = [b.sb(st, "wb%d" % i, [128, 8, 512], BF16) for i in range(2)]
            ev = [b.sb(st, "ev%d" % i, [128, 512]) for i in range(4)]
            nev = 0
            for cg in range(17):
                ncol = 512 if cg < 16 else D_IN - 8192
                w, wn = wb[cg % 2], "wb%d" % (cg % 2)
                S.dma("pool", w[:, :, 0:ncol], D["w_in"][l, :, cg * 512:cg * 512 + ncol].rearrange("(kc p) c -> p kc c", p=128), writes=[wn])
                if cg == 0:
                    hemit[0]()
                for tt in range(T // 512):
                    if cg == 0 and tt + 1 < len(hemit):
                        hemit[tt + 1]()
                    for cc in range((ncol + 127) // 128):
                        m = min(128, ncol - cc * 128)
                        pt, pn = b.ps()
                        for kc in range(8):
                            S.op("pe", lambda e: e.matmul(pt[0:m, :], w[:, kc, cc * 128:cc * 128 + m], hT[:, kc, tt * 512:(tt + 1) * 512],
                                                          start=(kc == 0), stop=(kc == 7)), reads=[wn, "hT%d" % tt], writes=[pn], pe_acc=(kc > 0))
                        e_, en = ev[nev % 4], "ev%d" % (nev % 4)
                        eng = "act" if nev % 2 == 0 else "dve"
                        nev += 1
                        if eng == "act":
                            S.op("act", lambda e: e.copy(e_[0:m, :], pt[0:m, :]), reads=[pn], writes=[en])
                        else:
                            S.op("dve", lambda e: e.tensor_copy(e_[0:m, :], pt[0:m, :]), reads=[pn], writes=[en])
                        r0 = cg * 512 + cc * 128
                        S.dma("sp", UT[r0:r0 + m, tt * 512:(tt + 1) * 512], e_[0:m, :], reads=[en], writes=[])
                for (c0, cn, dst) in tm_groups.get(cg, []):
                    for t4 in range(T // 128):
                        pt, pn = b.ps()
                        for kc in range(8):
                            S.op("pe", lambda e: e.matmul(pt[:, 0:cn], hT[:, kc, t4 * 128:(t4 + 1) * 128], w[:, kc, c0:c0 + cn],
                                                          start=(kc == 0), stop=(kc == 7)), reads=[wn, "hT%d" % (t4 // 4)], writes=[pn], pe_acc=(kc > 0))
                        e_, en = ev[nev % 4], "ev%d" % (nev % 4)
                        eng = "act" if nev % 2 == 0 else "dve"
                        nev += 1
                        if eng == "act":
                            S.op("act", lambda e: e.copy(e_[:, 0:cn], pt[:, 0:cn]), reads=[pn], writes=[en])
                        else:
                            S.op("dve", lambda e: e.tensor_copy(e_[:, 0:cn], pt[:, 0:cn]), reads=[pn], writes=[en])
                        S.dma("sp", UTM[t4 * 128:(t4 + 1) * 128, dst:dst + cn], e_[:, 0:cn], reads=[en], writes=[])
            S.barrier()

    def attn_unit(qT, qn, Kd, Nq, chunks, scale, o_out, on, wk, sink=None):
        po, pon = b.acc(0)
        pd, pdn = b.acc(1)
        nch = len(chunks)

        def emit_st(i):
            kT, kn, V, vn, nk, mask = chunks[i]
            pst, psn_ = b.ps()
            S.op("pe", lambda e: e.matmul(pst[0:nk, 0:Nq], kT, qT, start=True, stop=True), reads=[kn] + (qn if isinstance(qn, list) else [qn]), writes=[psn_])
            return pst, psn_

        cur = emit_st(0)
        for i, (kT, kn, V, vn, nk, mask) in enumerate(chunks):
            nxt = emit_st(i + 1) if i + 1 < nch else None
            pst, psn_ = cur
            E, En = wk["E"][i % 3], "E%d" % (i % 3)
            S.op("act", lambda e: e.activation(E[0:nk, 0:Nq], pst[0:nk, 0:Nq], AF.Exp, scale=scale), reads=[psn_], writes=[En])
            if mask is not None:
                S.op("pool", lambda e: e.tensor_tensor(E[0:nk, 0:Nq], E[0:nk, 0:Nq], mask, ALU.mult), reads=[En, "swamask"], writes=[En])
            last = (i == nch - 1) and sink is None
            S.op("pe", lambda e: e.matmul(po[0:64, 0:Nq], V, E[0:nk, 0:Nq], start=(i == 0), stop=(i == nch - 1)),
                 reads=[vn, En], writes=[pon], pe_acc=(i > 0))
            S.op("pe", lambda e: e.matmul(pd[0:64, 0:Nq], ones_bf[0:nk, 0:64], E[0:nk, 0:Nq], start=(i == 0), stop=last),
                 reads=["ones_bf", En], writes=[pdn], pe_acc=(i > 0))
            cur = nxt
        if sink is not None:
            S.op("pe", lambda e: e.matmul(pd[0:64, 0:Nq], ones_bf[0:1, 0:64], sink, start=False, stop=True),
                 reads=["ones_bf", "sinkrow"], writes=[pdn], pe_acc=True)
        rec = wk["rec"]
        S.op("dve", lambda e: e.reciprocal(rec[0:64, 0:Nq], pd[0:64, 0:Nq]), reads=[pdn], writes=["rec"])
        S.op("dve", lambda e: e.tensor_tensor(o_out, po[0:64, 0:Nq], rec[0:64, 0:Nq], ALU.mult), reads=[pon, "rec"], writes=[on])

    def attn_pair(qa, qan, qb, qbn, Nq, chunks, scale, o_out, on, wk, sink=None):
        po, pon = b.acc(0)
        pd, pdn = b.acc(1)
        nch = len(chunks)
        E = wk["E"]
        ne = len(E)

        def emit_st(i):
            kTa, kTb, kn, Va, Vb, vn, nk, mask = chunks[i]
            p1, n1 = b.ps()
            kna, knb = kn if isinstance(kn, tuple) else (kn, kn)
            S.op("pe", lambda e: e.matmul(p1[0:nk, 0:Nq], kTa, qa, start=True, stop=True), reads=[kna] + qan, writes=[n1])
            p2, n2 = b.ps()
            S.op("pe", lambda e: e.matmul(p2[0:nk, 0:Nq], kTb, qb, start=True, stop=True), reads=[knb] + qbn, writes=[n2])
            return (p1, n1, p2, n2)

        cur = emit_st(0)
        for i, (kTa, kTb, kn, Va, Vb, vn, nk, mask) in enumerate(chunks):
            nxt = emit_st(i + 1) if i + 1 < nch else None
            p1, n1, p2, n2 = cur
            Ea, Ean = E[(2 * i) % ne], "E%d" % ((2 * i) % ne)
            Eb, Ebn = E[(2 * i + 1) % ne], "E%d" % ((2 * i + 1) % ne)
            S.op("act", lambda e: e.activation(Ea[0:nk, 0:Nq], p1[0:nk, 0:Nq], AF.Exp, scale=scale), reads=[n1], writes=[Ean])
            S.op("act", lambda e: e.activation(Eb[0:nk, 0:Nq], p2[0:nk, 0:Nq], AF.Exp, scale=scale), reads=[n2], writes=[Ebn])
            if mask is not None:
                S.op("pool", lambda e: e.tensor_tensor(Ea[0:nk, 0:Nq], Ea[0:nk, 0:Nq], mask, ALU.mult), reads=[Ean, "swamask"], writes=[Ean])
                S.op("dve", lambda e: e.tensor_tensor(Eb[0:nk, 0:Nq], Eb[0:nk, 0:Nq], mask, ALU.mult), reads=[Ebn, "swamask"], writes=[Ebn])
            last = (i == nch - 1)
            S.op("pe", lambda e: e.matmul(po[:, 0:Nq], Va, Ea[0:nk, 0:Nq], start=(i == 0), stop=False), reads=[vn, Ean], writes=[pon], pe_acc=(i > 0))
            S.op("pe", lambda e: e.matmul(po[:, 0:Nq], Vb, Eb[0:nk, 0:Nq], start=False, stop=last), reads=[vn, Ebn], writes=[pon], pe_acc=True)
            S.op("pe", lambda e: e.matmul(pd[:, 0:Nq], onesAB[0:nk, 0, :], Ea[0:nk, 0:Nq], start=(i == 0), stop=False), reads=["onesAB", Ean], writes=[pdn], pe_acc=(i > 0))
            S.op("pe", lambda e: e.matmul(pd[:, 0:Nq], onesAB[0:nk, 1, :], Eb[0:nk, 0:Nq], start=False, stop=(last and sink is None)), reads=["onesAB", Ebn], writes=[pdn], pe_acc=True)
            cur = nxt
        if sink is not None:
            sa, sb_ = sink
            S.op("pe", lambda e: e.matmul(pd[:, 0:Nq], onesAB[0:1, 0, :], sa, start=False, stop=False), reads=["onesAB", "sinkrow"], writes=[pdn], pe_acc=True)
            S.op("pe", lambda e: e.matmul(pd[:, 0:Nq], onesAB[0:1, 1, :], sb_, start=False, stop=True), reads=["onesAB", "sinkrow"], writes=[pdn], pe_acc=True)
        rec = wk["rec"]
        S.op("dve", lambda e: e.reciprocal(rec[:, 0:Nq], pd[:, 0:Nq]), reads=[pdn], writes=["rec"])
        S.op("dve", lambda e: e.tensor_tensor(o_out, po[:, 0:Nq], rec[:, 0:Nq], ALU.mult), reads=[pon, "rec"], writes=[on])

    def rr_alloc(st, T, do_rope):
        sets = []
        for k in range(2):
            sets.append(dict(k=k, x=b.sb(st, "rr_x%d" % k, [128, T]), xs=(b.sb(st, "rr_xs%d" % k, [128, T]) if do_rope else None),
                             sq=b.sb(st, "rr_sq%d" % k, [128, T], BF16), rs=b.sb(st, "rr_rs%d" % k, [128, T]), t1=b.sb(st, "rr_t1%d" % k, [128, T])))
        return sets

    def rms_rope_rows(W_, p0, src_rows_fn, n_rows, T, gain2, gainn, do_norm, do_rope, rope_idx, out_bf, outn, out32=None, out32n=None,
                      out32_pre_rope=False):
        k = W_["k"]
        x, xs, sqb, rs, t1 = W_["x"], W_["xs"], W_["sq"], W_["rs"], W_["t1"]
        xn, xsn, sqn, rsn, t1n = ("rr_x%d" % k, "rr_xs%d" % k, "rr_sq%d" % k, "rr_rs%d" % k, "rr_t1%d" % k)
        for (r0, nr, ap) in src_rows_fn(False):
            S.dma("sp", x[p0 + r0:p0 + r0 + nr, 0:T], ap, writes=[xn])
        if do_rope:
            for (r0, nr, ap) in src_rows_fn(True):
                S.dma("act", xs[p0 + r0:p0 + r0 + nr, 0:T], ap, writes=[xsn])
        nr = n_rows
        pr = slice(p0, p0 + nr)
        W = min(512, T)
        CS = [slice(c * W, (c + 1) * W) for c in range(T // W)]
        if do_norm:
            S.op("act", lambda e: e.activation(sqb[pr, 0:T], x[pr, 0:T], AF.Square), reads=[xn], writes=[sqn])
            pts = []
            for cs in CS:
                pt, pn = b.ps()
                S.op("pe", lambda e: e.matmul(pt[:, 0:W], ones_bf[pr, :], sqb[pr, cs], start=True, stop=True), reads=[sqn, "ones_bf"], writes=[pn])
                pts.append((pt, pn))
            for cs, (pt, pn) in zip(CS, pts):
                S.op("act", lambda e: e.activation(rs[pr, cs], pt[pr, 0:W], AF.Ln, bias=epsb[pr, :], scale=1.0 / nr), reads=[pn, "epsb"], writes=[rsn])
            S.op("act", lambda e: e.activation(rs[pr, 0:T], rs[pr, 0:T], AF.Exp, scale=-0.5), reads=[rsn], writes=[rsn])
            S.op("dve", lambda e: e.scalar_tensor_tensor(x[pr, 0:T], x[pr, 0:T], gain2[pr, 0:1], rs[pr, 0:T], ALU.mult, ALU.mult),
                 reads=[xn, rsn, gainn], writes=[xn])
            if do_rope:
                S.op("dve", lambda e: e.scalar_tensor_tensor(xs[pr, 0:T], xs[pr, 0:T], gain2[pr, 1:2], rs[pr, 0:T], ALU.mult, ALU.mult),
                     reads=[xsn, rsn, gainn], writes=[xsn])
        if out32 is not None and out32_pre_rope:
            S.op("pool", lambda e: e.tensor_copy(out32[pr, 0:T], x[pr, 0:T]), reads=[xn], writes=[out32n])
        if do_rope:
            S.op("dve", lambda e: e.tensor_tensor(x[pr, 0:T], x[pr, 0:T], rope[pr, rope_idx, 0:T], ALU.mult), reads=[xn, "rope"], writes=[xn])
            S.op("pool", lambda e: e.tensor_tensor(t1[pr, 0:T], xs[pr, 0:T], rope[pr, rope_idx + 1, 0:T], ALU.mult), reads=[xsn, "rope"], writes=[t1n])
            S.op("dve", lambda e: e.tensor_tensor(out_bf[:, 0:T], x[pr, 0:T], t1[pr, 0:T], ALU.add), reads=[xn, t1n], writes=[outn])
        else:
            S.op("act", lambda e: e.copy(out_bf[:, 0:T], x[pr, 0:T]), reads=[xn], writes=[outn])

    def swap_rows(base, T0, T1, UT, nd):
        q = nd // 4
        def f(swapped):
            if not swapped:
                return [(0, nd, UT[base:base + nd, T0:T1])]
            return [(0, q, UT[base + q:base + 2 * q, T0:T1]), (q, q, UT[base:base + q, T0:T1]),
                    (2 * q, q, UT[base + 3 * q:base + 4 * q, T0:T1]), (3 * q, q, UT[base + 2 * q:base + 3 * q, T0:T1])]
        return f

    rope = b.sb(cst, "rope", [128, 4, 2048], BF16)
    S.dma("pool", rope[:], D["rope64"].rearrange("a p t -> p a t"), writes=["rope"])

    def stage_gqa_like(g, l, kind):
        G = GR[g]
        T, L, P, nseq, do_rope = G["T"], G["L"], G["P"], G["nseq"], G["rope"]
        UT, UTM, OT = D["UT_" + g], D["UTM_" + g], D["OT_" + g]
        qo, ko = (OFF["sq"], OFF["sk"]) if kind == "swa" else (OFF["gq"], OFF["gk"])
        vcol = 1536 if kind == "swa" else 1664
        bi = 1 if kind == "swa" else 3
        do_norm = kind == "gqa"
        scale = 64 ** -0.5
        nkc_ctx = P // 128
        with ExitStack() as st:
            kT = b.sb(st, "kT", [128, P + T], BF16)
            Vt = b.sb(st, "Vt", [128, (P + T) // 128, 2, 128], BF16)
            qTh = b.sb(st, "qTh", [128, 8, T], BF16)
            gq2 = b.sb(st, "gq2", [128, 2]); gk2 = b.sb(st, "gk2", [128, 2])
            sinkrow = b.sb(st, "sinkrow", [1, 8, 128], BF16)
            sk32 = b.sb(st, "sk32", [1, 8])
            wk = dict(E=[b.sb(st, "E%d" % i, [128, 512], BF16) for i in range(6)], rec=b.sb(st, "rec", [128, 512]))
            obs = [b.sb(st, "ob%d" % i, [128, 512], BF16) for i in range(2)]
            swm = b.sb(st, "swamask", [128, 2, 512], BF16)
            S.dma("sp", swm[:], D["swamask"].rearrange("a p c -> p a c"), writes=["swamask"])
            S.op("pool", lambda e: e.memset(qTh[64:128, 0:4, :], 0.0), writes=["qTh%d" % hh for hh in range(4)])
            S.op("pool", lambda e: e.memset(qTh[0:64, 4:8, :], 0.0), writes=["qTh%d" % hh for hh in range(4, 8)])
            S.op("pool", lambda e: e.memset(Vt[:], 0.0), writes=["Vt"])
            if do_norm:
                S.dma("sp", gq2[:], D["gqa_qn"][l], writes=["gq2"])
                S.dma("sp", gk2[:], D["gqa_kn"][l], writes=["gk2"])
            else:
                S.dma("sp", sk32[:], D["sink"][l], writes=["sk32"])
                S.op("act", lambda e: e.activation(sk32[:], sk32[:], AF.Exp), reads=["sk32"], writes=["sk32"])
                S.op("dve", lambda e: e.tensor_copy(sinkrow[:], sk32[:].unsqueeze(2).to_broadcast([1, 8, 128])), reads=["sk32"], writes=["sinkrow"])
            with ExitStack() as s2:
                RR = rr_alloc(s2, T, do_rope)
                k32 = b.sb(s2, "k32", [128, T]) if g == "p" else None
                v32 = b.sb(s2, "v32", [128, (P + T) // 128, 128])
                if P:
                    src_k = D["c_swa_kT"] if kind == "swa" else D["c_gqa_kT"]
                    src_v = D["c_swa_v"] if kind == "swa" else D["c_gqa_v"]
                    S.dma("pool", kT[:, 0:P], src_k[l].rearrange("h d t -> (h d) t"), writes=["kT"])
                    S.dma("sp", v32[:, 0:P // 128, :], src_v[l].rearrange("(c p) f -> p c f", p=128), writes=["v32"])
                S.dma("sp", v32[:, P // 128:, :], UTM[:, vcol:vcol + 128].rearrange("(c p) f -> p c f", p=128), writes=["v32"])
                S.op("dve", lambda e: e.tensor_copy(Vt[:, :, 0, 0:64], v32[:, :, 0:64]), reads=["v32"], writes=["Vt"])
                S.op("dve", lambda e: e.tensor_copy(Vt[:, :, 1, 64:128], v32[:, :, 64:128]), reads=["v32"], writes=["Vt"])
                if g == "p":
                    dst = D["o_swa_v"] if kind == "swa" else D["o_gqa_v"]
                    S.dma("act", dst[l].rearrange("(c p) f -> p c f", p=128), v32[:], reads=["v32"], writes=[])
                nrr = 0
                for kvh in range(2):
                    p0 = kvh * 64
                    rms_rope_rows(RR[nrr % 2], p0, swap_rows(ko + kvh * 64, 0, T, UT, 64), 64, T, gk2, "gk2", do_norm, do_rope, 0,
                                  kT[p0:p0 + 64, P:P + T], "kT", out32=k32, out32n="k32", out32_pre_rope=True)
                    nrr += 1
                if g == "p":
                    dst = D["o_swa_kT"] if kind == "swa" else D["o_gqa_kT"]
                    S.dma("sp", dst[l], k32[:], reads=["k32"], writes=[])
                for h in range(8):
                    p0 = (h // 4) * 64
                    rms_rope_rows(RR[nrr % 2], p0, swap_rows(qo + h * 64, 0, T, UT, 64), 64, T, gq2, "gq2", do_norm, do_rope, 0,
                                  qTh[p0:p0 + 64, h, :], "qTh%d" % h)
                    nrr += 1
                nu = 0
                qna = ["qTh%d" % hh for hh in range(4)]
                qnb = ["qTh%d" % hh for hh in range(4, 8)]
                for sq_ in range(nseq):
                    T0 = sq_ * L
                    kb0 = P + T0
                    for qb in range(L // 128):
                        cols = [(c * 128, None) for c in range(nkc_ctx)]
                        if kind == "swa" and P:
                            cols += [(kb0 + kb * 128, mi) for (kb, mi) in ((qb - 1, 0), (qb, None), (qb + 1, 1)) if 0 <= kb < L // 128]
                        else:
                            cols += [(kb0 + kb * 128, None) for kb in range(L // 128)]
                        chunks = [(kT[:, c0:c0 + 128], kT[:, c0:c0 + 128], "kT", Vt[:, c0 // 128, 0, :], Vt[:, c0 // 128, 1, :], "Vt", 128,
                                   None if mi is None else swm[:, mi, :]) for (c0, mi) in cols]
                        ob, obn = obs[nu % 2], "ob%d" % (nu % 2)
                        nu += 1
                        q0 = T0 + qb * 128
                        attn_pair(qTh[:, 0:4, q0:q0 + 128], qna, qTh[:, 4:8, q0:q0 + 128], qnb, 512, chunks, scale, ob[:], obn, wk,
                                  sink=((sinkrow[0:1, 0:4, :], sinkrow[0:1, 4:8, :]) if kind == "swa" else None))
                        for kvh in range(2):
                            S.dma("sp" if kvh == 0 else "act", OT[bi, kvh * 256:(kvh + 1) * 256, q0:q0 + 128].rearrange("(h d) t -> d h t", d=64),
                                  ob[kvh * 64:(kvh + 1) * 64, :].rearrange("d (h t) -> d h t", h=4), reads=[obn], writes=[])
                S.barrier()

    def stage_mla(g, l):
        G = GR[g]
        T, L, P, nseq, do_rope = G["T"], G["L"], G["P"], G["nseq"], G["rope"]
        UT, OT = D["UT_" + g], D["OT_" + g]
        scale = 96 ** -0.5
        NK = P + L
        for sq_ in range(nseq):
            T0, T1 = sq_ * L, (sq_ + 1) * L
            with ExitStack() as st:
                ckvT = b.sb(st, "ckvT", [128, NK], BF16)
                krT = b.sb(st, "krT", [96, NK], BF16)
                cqn = b.sb(st, "cqn", [128, 2, L], BF16)
                wuq = b.sb(st, "wuq", [128, 2, 768], BF16); wuqs = b.sb(st, "wuqs", [128, 2, 768], BF16)
                wukv = b.sb(st, "wukv", [128, 1024], BF16)
                g_q = b.sb(st, "g_q", [128, 2]); g_kv = b.sb(st, "g_kv", [128, 1])
                S.dma("pool", wuq[:], D["w_uq"][l].rearrange("(kc p) c -> p kc c", p=128), writes=["wuq"])
                S.dma("pool", wuqs[:], D["w_uq_sw"][l].rearrange("(kc p) c -> p kc c", p=128), writes=["wuqs"])
                S.dma("pool", wukv[:], D["w_ukv"][l], writes=["wukv"])
                S.dma("sp", g_q[:], D["mla_qnT"][l], writes=["g_q"])
                S.dma("sp", g_kv[:], D["mla_kvn"][l], writes=["g_kv"])
                with ExitStack() as s2:
                    x = b.sb(s2, "m_x", [128, 2, L]); sqb = b.sb(s2, "m_sq", [128, 2, 512], BF16); rs = b.sb(s2, "m_rs", [128, 512])
                    xk = b.sb(s2, "m_xk", [128, L]); kr32 = b.sb(s2, "m_kr", [96, L]); krs = b.sb(s2, "m_krs", [96, L]); t1 = b.sb(s2, "m_t1", [96, 512])
                    if P:
                        S.dma("pool", ckvT[:, 0:P], D["c_ckvT"][l], writes=["ckvT"])
                        S.dma("pool", krT[64:96, 0:P], D["c_krT"][l], writes=["krT"])
                    S.dma("sp", x[:], UT[OFF["cq"]:OFF["cq"] + 256, T0:T1].rearrange("(kc p) t -> p kc t", p=128), writes=["m_x"])
                    S.dma("sp", xk[:], UT[OFF["ckv"]:OFF["ckv"] + 128, T0:T1], writes=["m_xk"])
                    S.dma("sp", kr32[64:96, :], UT[OFF["kr"]:OFF["kr"] + 32, T0:T1], writes=["m_kr"])
                    if do_rope:
                        for (r0, nr, ap) in swap_rows(OFF["kr"], T0, T1, UT, 32)(True):
                            S.dma("act", krs[64 + r0:64 + r0 + nr, :], ap, writes=["m_krs"])
                    for c in range(L // 512 if L >= 512 else 1):
                        w = min(512, L)
                        cs = slice(c * w, (c + 1) * w)
                        pt, pn = b.ps()
                        for kc in range(2):
                            S.op("act", lambda e: e.activation(sqb[:, kc, 0:w], x[:, kc, cs], AF.Square), reads=["m_x"], writes=["m_sq"])
                        for kc in range(2):
                            S.op("pe", lambda e: e.matmul(pt[:, 0:w], ones_bf[:], sqb[:, kc, 0:w], start=(kc == 0), stop=(kc == 1)),
                                 reads=["m_sq", "ones_bf"], writes=[pn], pe_acc=True)
                        rstd_from_sumsq(pt, pn, rs, "m_rs", 128, w, 1.0 / 256)
                        for kc in range(2):
                            S.op("dve", lambda e: e.scalar_tensor_tensor(cqn[:, kc, cs], x[:, kc, cs], g_q[:, kc:kc + 1], rs[:, 0:w], ALU.mult, ALU.mult),
                                 reads=["m_x", "m_rs", "g_q"], writes=["cqn"])
                        pt, pn = b.ps()
                        S.op("act", lambda e: e.activation(sqb[:, 0, 0:w], xk[:, cs], AF.Square), reads=["m_xk"], writes=["m_sq"])
                        S.op("pe", lambda e: e.matmul(pt[:, 0:w], ones_bf[:], sqb[:, 0, 0:w], start=True, stop=True), reads=["m_sq", "ones_bf"], writes=[pn])
                        rstd_from_sumsq(pt, pn, rs, "m_rs", 128, w, 1.0 / 128)
                        S.op("dve", lambda e: e.scalar_tensor_tensor(xk[:, cs], xk[:, cs], g_kv[:, 0:1], rs[:, 0:w], ALU.mult, ALU.mult),
                             reads=["m_xk", "m_rs", "g_kv"], writes=["m_xk"])
                        S.op("pool", lambda e: e.tensor_copy(ckvT[:, P + c * w:P + (c + 1) * w], xk[:, cs]), reads=["m_xk"], writes=["ckvT"])
                        if do_rope:
                            S.op("dve", lambda e: e.tensor_tensor(t1[64:96, 0:w], kr32[64:96, cs], rope[64:96, 2, cs], ALU.mult), reads=["m_kr", "rope"], writes=["m_t1"])
                            S.op("pool", lambda e: e.tensor_tensor(krs[64:96, cs], krs[64:96, cs], rope[64:96, 3, cs], ALU.mult), reads=["m_krs", "rope"], writes=["m_krs"])
                            S.op("dve", lambda e: e.tensor_tensor(krT[64:96, P + c * w:P + (c + 1) * w], t1[64:96, 0:w], krs[64:96, cs], ALU.add),
                                 reads=["m_t1", "m_krs"], writes=["krT"])
                        else:
                            S.op("dve", lambda e: e.tensor_copy(krT[64:96, P + c * w:P + (c + 1) * w], kr32[64:96, cs]), reads=["m_kr"], writes=["krT"])
                    if g == "p":
                        S.dma("sp", D["o_ckvT"][l, :, T0:T1], xk[:], reads=["m_xk"], writes=[])
                        S.dma("sp", D["o_krT"][l, :, T0:T1], kr32[64:96, :], reads=["m_kr"], writes=[])
                    S.barrier()
                Vall = b.sb(st, "Vall", [128, NK // 128, 8, 128], BF16)
                S.op("pool", lambda e: e.memset(Vall[:], 0.0), writes=["Vall"])
                for c in range(NK // 128):
                    pt, pn = b.ps()
                    S.op("pe", lambda e: e.matmul(pt[:, :], ckvT[:, c * 128:(c + 1) * 128],
                                                  wukv[:].rearrange("p (h x) -> p h x", x=128)[:, :, 64:128], start=True, stop=True),
                         reads=["ckvT", "wukv"], writes=[pn])
                    pv = pt[:, :].rearrange("p (h two d) -> p h two d", two=2, d=64)
                    S.op("act", lambda e: e.copy(Vall[:, c, 0::2, 0:64], pv[:, :, 0, :]), reads=[pn], writes=["Vall"])
                    S.op("dve", lambda e: e.tensor_copy(Vall[:, c, 1::2, 64:128], pv[:, :, 1, :]), reads=[pn], writes=["Vall"])
                S.barrier()
                with ExitStack() as s2:
                    wk = dict(E=[b.sb(s2, "E%d" % i, [128, 512], BF16) for i in range(6)], rec=b.sb(s2, "rec", [128, 512]))
                    kTh = [b.sb(s2, "kTh%d" % i, [128, NK], BF16) for i in range(4)]
                    qh = [b.sb(s2, "qh%d" % i, [128, L], BF16) for i in range(4)]
                    obs = [b.sb(s2, "ob%d" % i, [128, 512], BF16) for i in range(2)]
                    t2 = b.sb(s2, "m_t2", [96, 512]); t3 = b.sb(s2, "m_t3", [96, 512])
                    for i in range(4):
                        S.op("pool", lambda e: e.memset(kTh[i][96:128, :], 0.0), writes=["kTh%d" % i])
                        S.op("pool", lambda e: e.memset(qh[i][96:128, :], 0.0), writes=["qh%d" % i])
                    nu = 0
                    W = min(512, L)
                    for hp in range(4):
                        bufs = []
                        for h2 in range(2):
                            h = hp * 2 + h2
                            bi_ = (hp % 2) * 2 + h2
                            kt, ktn = kTh[bi_], "kTh%d" % bi_
                            q_, q_n = qh[bi_], "qh%d" % bi_
                            bufs.append((kt, ktn, q_, q_n))
                            for c in range((NK + 511) // 512):
                                w = min(512, NK - c * 512)
                                pt, pn = b.ps()
                                S.op("pe", lambda e: e.matmul(pt[:, 0:w], wukv[:, h * 128:(h + 1) * 128], ckvT[:, c * 512:c * 512 + w], start=True, stop=True),
                                     reads=["wukv", "ckvT"], writes=[pn])
                                S.op("act", lambda e: e.copy(kt[0:64, c * 512:c * 512 + w], pt[0:64, 0:w]), reads=[pn], writes=[ktn])
                            S.op("pool", lambda e: e.tensor_copy(kt[64:96, :], krT[64:96, :]), reads=["krT"], writes=[ktn])
                            for c in range(L // W):
                                cs = slice(c * W, (c + 1) * W)
                                pt, pn = b.ps()
                                for kc in range(2):
                                    S.op("pe", lambda e: e.matmul(pt[0:96, 0:W], wuq[:, kc, h * 96:(h + 1) * 96], cqn[:, kc, cs], start=(kc == 0), stop=(kc == 1)),
                                         reads=["wuq", "cqn"], writes=[pn], pe_acc=(kc > 0))
                                S.op("act", lambda e: e.copy(q_[0:64, cs], pt[0:64, 0:W]), reads=[pn], writes=[q_n])
                                if do_rope:
                                    pt2, pn2 = b.ps()
                                    for kc in range(2):
                                        S.op("pe", lambda e: e.matmul(pt2[0:96, 0:W], wuqs[:, kc, h * 96:(h + 1) * 96], cqn[:, kc, cs], start=(kc == 0), stop=(kc == 1)),
                                             reads=["wuqs", "cqn"], writes=[pn2], pe_acc=(kc > 0))
                                    S.op("dve", lambda e: e.tensor_tensor(t2[64:96, 0:W], pt[64:96, 0:W], rope[64:96, 2, cs], ALU.mult), reads=[pn, "rope"], writes=["m_t2"])
                                    S.op("dve", lambda e: e.tensor_tensor(t3[64:96, 0:W], pt2[64:96, 0:W], rope[64:96, 3, cs], ALU.mult), reads=[pn2, "rope"], writes=["m_t3"])
                                    S.op("pool", lambda e: e.tensor_tensor(q_[64:96, cs], t2[64:96, 0:W], t3[64:96, 0:W], ALU.add), reads=["m_t2", "m_t3"], writes=[q_n])
                                else:
                                    S.op("dve", lambda e: e.tensor_copy(q_[64:96, cs], pt[64:96, 0:W]), reads=[pn], writes=[q_n])
                        (kta, ktan, qa, qan), (ktb, ktbn, qb_, qbn) = bufs
                        for c in range(L // W):
                            chunks = [(kta[:, kc * 128:(kc + 1) * 128], ktb[:, kc * 128:(kc + 1) * 128], (ktan, ktbn), Vall[:, kc, hp * 2, :], Vall[:, kc, hp * 2 + 1, :], "Vall", 128, None)
                                      for kc in range(NK // 128)]
                            ob, obn = obs[nu % 2], "ob%d" % (nu % 2)
                            nu += 1
                            attn_pair(qa[:, c * W:(c + 1) * W], [qan], qb_[:, c * W:(c + 1) * W], [qbn], W, chunks, scale, ob[:, 0:W], obn, wk)
                            S.dma("sp", OT[2, hp * 128:(hp + 1) * 128, T0 + c * W:T0 + (c + 1) * W], ob[:, 0:W], reads=[obn], writes=[])
                    S.barrier()

    def stage_hgrn(g, l):
        G = GR[g]
        T, L, P, nseq = G["T"], G["L"], G["P"], G["nseq"]
        UT, UTM, OT = D["UT_" + g], D["UTM_" + g], D["OT_" + g]
        NTL = L // 128
        ident, triU, triL, sL, sU, csel = (cm[:, i, :] for i in range(6))
        with ExitStack() as st:
            lbb = b.sb(st, "lbb", [128, 2, 512]); oml = b.sb(st, "oml", [128, 2, 512])
            with ExitStack() as s2:
                lbr = b.sb(s2, "lbr", [128, DEPTH, 2, 512]); den = b.sb(s2, "lden", [128, 2, 512])
                S.dma("sp", lbr[:], D["lbT"].partition_broadcast(128), writes=["lbr"])
                S.op("act", lambda e: e.activation(lbr[:], lbr[:], AF.Exp), reads=["lbr"], writes=["lbr"])
                S.op("dve", lambda e: e.tensor_tensor(den[:], lbr[:, 0], lbr[:, 1], ALU.add), reads=["lbr"], writes=["lden"])
                S.op("dve", lambda e: e.reciprocal(den[:], den[:]), reads=["lden"], writes=["lden"])
                if l == 0:
                    S.op("pool", lambda e: e.memset(lbb[:], 0.0), writes=["lbb"])
                else:
                    S.op("dve", lambda e: e.tensor_tensor(lbb[:], lbr[:, 1], den[:], ALU.mult), reads=["lbr", "lden"], writes=["lbb"])
                S.op("dve", lambda e: e.tensor_scalar(oml[:], lbb[:], -1.0, 1.0, ALU.mult, ALU.add), reads=["lbb"], writes=["oml"])
                S.barrier()
            hn = b.sb(st, "hn", [128, 4]); S.dma("sp", hn[:], D["hgrn_normT"][l], writes=["hn"])
            Sst = [b.sb(st, "Sst%d" % i, [128, 2, 4, 128]) for i in range(2)]
            oall = b.sb(st, "oall", [128, 2, 4, L], BF16)
            W_ = {}
            for d in range(2):
                for k in range(2):
                    sfx = "%d%d" % (d, k)
                    W_[d, k] = dict(
                        sfx=sfx,
                        tt=b.sb(st, "h_t" + sfx, [128, 512]), vv=b.sb(st, "h_v" + sfx, [128, 512]), gg=b.sb(st, "h_g" + sfx, [128, 512]),
                        kt=b.sb(st, "h_kt" + sfx, [128, 512]), kh=b.sb(st, "h_kh" + sfx, [128, 512]),
                        khm=b.sb(st, "h_khm" + sfx, [128, 4, 4, 128]), qT=b.sb(st, "h_qT" + sfx, [128, 4, 128]), qt=b.sb(st, "h_qt" + sfx, [128, 4, 128]),
                        eb=b.sb(st, "h_eb" + sfx, [128, 4, 128]), ktT=b.sb(st, "h_ktT" + sfx, [128, 4, 128]), AT=b.sb(st, "h_AT" + sfx, [128, 4, 128]),
                        tmpo=b.sb(st, "h_tmpo" + sfx, [128, 512]))
            etot = [b.sb(st, "h_etot%d" % k, [128, 2, 4, 4]) for k in range(2)]
            fin = dict(os=b.sb(st, "f_os", [128, 4, 256]), gT=b.sb(st, "f_gT", [128, 4, 256]), sq=b.sb(st, "f_sq", [128, 4, 256], BF16),
                       rs=b.sb(st, "f_rs", [128, 4, 256]), ob=b.sb(st, "f_ob", [128, 4, 256], BF16))
            b.nrot = 4
            M4 = lambda M: M.unsqueeze(1).to_broadcast([128, 4, 128])

            def sstn(sp, d, h):
                return "Sst%d_%d%d" % (sp, d, h)

            def partA(rec):
                k = rec["k"]
                for d in range(2):
                    w = W_[d, k]; x = w["sfx"]
                    t0 = rec["T0"] + rec["ti"][d] * 128
                    S.dma("sp", w["tt"][:], UTM[t0:t0 + 128, d * 512:(d + 1) * 512], writes=["h_t" + x])
                    S.dma("sp", w["vv"][:], UTM[t0:t0 + 128, 1024:1536], writes=["h_v" + x])
                    S.dma("act", w["qT"][:], UT[0:512, t0:t0 + 128].rearrange("(h p) t -> p h t", p=128), writes=["h_qT" + x])
                for d in range(2):
                    w = W_[d, k]; x = w["sfx"]
                    S.op("act", lambda e: e.activation(w["tt"][:], w["tt"][:], AF.Sigmoid), reads=["h_t" + x], writes=["h_t" + x])
                for d in range(2):
                    w = W_[d, k]; x = w["sfx"]
                    S.op("act", lambda e: e.activation(w["qT"][:], w["qT"][:], AF.Silu), reads=["h_qT" + x], writes=["h_qT" + x])
                for d in range(2):
                    w = W_[d, k]; x = w["sfx"]
                    S.op("dve", lambda e: e.tensor_tensor(w["tt"][:], w["tt"][:], oml[:, d], ALU.mult), reads=["h_t" + x, "oml"], writes=["h_t" + x])
                    S.op("dve", lambda e: e.scalar_tensor_tensor(w["gg"][:], w["tt"][:], 1e-30, lbb[:, d], ALU.max, ALU.add), reads=["h_t" + x, "lbb"], writes=["h_g" + x])
                for d in range(2):
                    w = W_[d, k]; x = w["sfx"]
                    S.op("act", lambda e: e.activation(w["gg"][:], w["gg"][:], AF.Ln), reads=["h_g" + x], writes=["h_g" + x])
                for d in range(2):
                    w = W_[d, k]; x = w["sfx"]
                    S.op("dve", lambda e: e.tensor_tensor(w["tt"][:], oml[:, d], w["tt"][:], ALU.subtract), reads=["h_t" + x, "oml"], writes=["h_t" + x])

            def partB(rec, stage):
                k = rec["k"]
                Ms = [(triU, sL), (triL, sU)]
                if stage == 0:
                    pbs = []
                    for d in range(2):
                        w = W_[d, k]; x = w["sfx"]
                        pb, pbn = b.ps()
                        S.op("pe", lambda e: e.matmul(pb[:], Ms[d][0], w["gg"][:], start=True, stop=True), reads=["cm", "h_g" + x], writes=[pbn])
                        pr, prn = b.ps()
                        S.op("pe", lambda e: e.matmul(pr[:], Ms[d][1], w["gg"][:], start=True, stop=True), reads=["cm", "h_g" + x], writes=[prn])
                        pbs.append((pb, pbn, pr, prn))
                    for d in range(2):
                        w = W_[d, k]; x = w["sfx"]
                        pb, pbn, pr, prn = pbs[d]
                        S.op("act", lambda e: e.activation(w["kt"][:], pb[:], AF.Exp, scale=-1.0), reads=[pbn], writes=["h_kt" + x])
                        S.op("act", lambda e: e.activation(w["kh"][:], pr[:], AF.Exp), reads=[prn], writes=["h_kh" + x])
                    pts = []
                    ptot, ptotn = b.ps()
                    for d in range(2):
                        w = W_[d, k]; x = w["sfx"]
                        pbt, pbtn = b.ps()
                        for h in range(4):
                            hs = slice(h * 128, (h + 1) * 128)
                            S.op("pe", lambda e: e.matmul(pbt[:, hs], w["gg"][:, hs], Ms[d][0], start=True, stop=True), reads=["h_g" + x, "cm"], writes=[pbtn], pe_acc=(h > 0))
                            S.op("pe", lambda e: e.matmul(ptot[:, d * 16 + h * 4:d * 16 + h * 4 + 4], w["gg"][:, hs], csel[:, 0:4], start=True, stop=True),
                                 reads=["h_g" + x, "cm"], writes=[ptotn], pe_acc=(d > 0 or h > 0))
                        pts.append((pbt, pbtn))
                    for d in range(2):
                        w = W_[d, k]; x = w["sfx"]
                        S.op("dve", lambda e: e.tensor_tensor(w["kt"][:], w["kt"][:], w["tt"][:], ALU.mult), reads=["h_kt" + x, "h_t" + x], writes=["h_kt" + x])
                        S.op("dve", lambda e: e.tensor_tensor(w["kh"][:], w["kh"][:], w["tt"][:], ALU.mult), reads=["h_kh" + x, "h_t" + x], writes=["h_kh" + x])
                    for d in range(2):
                        w = W_[d, k]; x = w["sfx"]
                        pbt, pbtn = pts[d]
                        S.op("act", lambda e: e.activation(w["eb"][:], pbt[:].rearrange("p (h t) -> p h t", h=4), AF.Exp), reads=[pbtn], writes=["h_eb" + x])
                    S.op("act", lambda e: e.activation(etot[k][:], ptot[:, 0:32].rearrange("p (d h j) -> p d h j", d=2, h=4), AF.Exp), reads=[ptotn], writes=["h_etot%d" % k])
                    for d in range(2):
                        w = W_[d, k]; x = w["sfx"]
                        S.op("dve", lambda e: e.tensor_tensor(w["qt"][:], w["qT"][:], w["eb"][:], ALU.mult), reads=["h_qT" + x, "h_eb" + x], writes=["h_qt" + x])
                    for d in range(2):
                        w = W_[d, k]; x = w["sfx"]
                        for j in range(4):
                            S.op("act", lambda e: e.activation(w["khm"][:, :, j, :], w["kh"][:].rearrange("p (h c) -> p h c", h=4), AF.Copy, scale=csel[:, j:j + 1]),
                                 reads=["h_kh" + x, "cm"], writes=["h_khm" + x])
                elif stage == 1:
                    for d in range(2):
                        w = W_[d, k]; x = w["sfx"]
                        pk, pkn = b.ps()
                        for h in range(4):
                            hs = slice(h * 128, (h + 1) * 128)
                            S.op("pe", lambda e: e.matmul(pk[:, hs], w["kt"][:, hs], ident, start=True, stop=True), reads=["h_kt" + x, "cm"], writes=[pkn], pe_acc=(h > 0))
                        S.op("act", lambda e: e.copy(w["ktT"][:], pk[:].rearrange("p (h t) -> p h t", h=4)), reads=[pkn], writes=["h_ktT" + x])
                elif stage == 2:
                    for d in range(2):
                        w = W_[d, k]; x = w["sfx"]
                        pa, pan = b.ps()
                        for h in range(4):
                            hs = slice(h * 128, (h + 1) * 128)
                            S.op("pe", lambda e: e.matmul(pa[:, hs], w["ktT"][:, h, :], w["qt"][:, h, :], start=True, stop=True), reads=["h_ktT" + x, "h_qt" + x], writes=[pan], pe_acc=(h > 0))
                        S.op("dve", lambda e: e.tensor_tensor(w["AT"][:], pa[:].rearrange("p (h t) -> p h t", h=4), M4(Ms[d][0]), ALU.mult), reads=[pan, "cm"], writes=["h_AT" + x])
                else:
                    for d in range(2):
                        w = W_[d, k]; x = w["sfx"]
                        po_, pon_ = b.acc(d)
                        for h in range(4):
                            hs = slice(h * 128, (h + 1) * 128)
                            S.op("pe", lambda e: e.matmul(po_[:, hs], w["vv"][:, hs], w["AT"][:, h, :], start=True, stop=True), reads=["h_v" + x, "h_AT" + x], writes=[pon_], pe_acc=(h > 0))
                        S.op("act", lambda e: e.copy(w["tmpo"][:], po_[:]), reads=[pon_], writes=["h_tmpo" + x])

            def chunk_step(rec, n_):
                k = rec["k"]; sp = rec["sp"]
                pss = []
                for d in range(2):
                    w = W_[d, k]; x = w["sfx"]
                    j = n_ if d == 0 else 3 - n_
                    js = slice(j * 32, (j + 1) * 32)
                    pi_, pin_ = b.acc(2 + d)
                    ps_, psn2 = b.ps()
                    pss.append((ps_, psn2, j))
                    for h in range(4):
                        hs = slice(h * 128, (h + 1) * 128)
                        S.op("pe", lambda e: e.matmul(pi_[:, h * 128 + j * 32:h * 128 + (j + 1) * 32], Sst[sp][:, d, h, :], w["qt"][:, h, js], start=True, stop=True),
                             reads=[sstn(sp, d, h), "h_qt" + x], writes=[pin_], pe_acc=(n_ > 0 or h > 0))
                        S.op("pe", lambda e: e.matmul(ps_[:, hs], w["khm"][:, h, j, :], w["vv"][:, hs], start=True, stop=True),
                             reads=["h_khm" + x, "h_v" + x], writes=[psn2], pe_acc=(h > 0))
                for d in range(2):
                    w = W_[d, k]; x = w["sfx"]
                    ps_, psn2, j = pss[d]
                    for h in range(4):
                        hs = slice(h * 128, (h + 1) * 128)
                        S.op("dve", lambda e: e.scalar_tensor_tensor(Sst[sp][:, d, h, :], Sst[sp][:, d, h, :], etot[k][:, d, h, j:j + 1], ps_[:, hs], ALU.mult, ALU.add),
                             reads=[psn2, "h_etot%d" % k, sstn(sp, d, h)], writes=[sstn(sp, d, h)])

            def finish(rec):
                k = rec["k"]
                for d in range(2):
                    w = W_[d, k]; x = w["sfx"]
                    pi_, pin_ = b.acc(2 + d)
                    tl = rec["ti"][d] * 128
                    S.op("dve", lambda e: e.tensor_tensor(oall[:, d, :, tl:tl + 128], w["tmpo"][:].rearrange("p (h t) -> p h t", h=4),
                                                          pi_[:].rearrange("p (h t) -> p h t", h=4), ALU.add),
                         reads=[pin_, "h_tmpo" + x], writes=["oall"])

            def seq_final(sq_):
                T0 = sq_ * L
                sp = sq_ % 2
                for c in range(L // 256):
                    cs = slice(c * 256, (c + 1) * 256)
                    os_, gT, sq, rs, ob = fin["os"], fin["gT"], fin["sq"], fin["rs"], fin["ob"]
                    S.dma("act", gT[:], UT[OFF["hg"]:OFF["hg"] + 512, T0 + c * 256:T0 + (c + 1) * 256].rearrange("(h p) t -> p h t", p=128), writes=["f_gT"])
                    S.op("act", lambda e: e.activation(gT[:], gT[:], AF.Silu), reads=["f_gT"], writes=["f_gT"])
                    S.op("dve", lambda e: e.tensor_tensor(os_[:], oall[:, 0, :, cs], oall[:, 1, :, cs], ALU.add), reads=["oall"], writes=["f_os"])
                    S.op("act", lambda e: e.activation(sq[:], os_[:], AF.Square), reads=["f_os"], writes=["f_sq"])
                    for hh in range(2):
                        pn_, pnn = b.ps()
                        for h2 in range(2):
                            h = hh * 2 + h2
                            S.op("pe", lambda e: e.matmul(pn_[:, h2 * 256:(h2 + 1) * 256], ones_bf[:], sq[:, h, :], start=True, stop=True),
                                 reads=["f_sq", "ones_bf"], writes=[pnn], pe_acc=(h2 > 0))
                        rstd_from_sumsq(pn_, pnn, rs[:, hh * 2:hh * 2 + 2, :].rearrange("p a t -> p (a t)"), "f_rs", 128, 512, 1.0 / 128)
                    for h in range(4):
                        S.op("dve", lambda e: e.scalar_tensor_tensor(os_[:, h, :], os_[:, h, :], hn[:, h:h + 1], rs[:, h, :], ALU.mult, ALU.mult),
                             reads=["f_os", "hn", "f_rs"], writes=["f_os"])
                    S.op("dve", lambda e: e.tensor_tensor(ob[:], os_[:], gT[:], ALU.mult), reads=["f_os", "f_gT"], writes=["f_ob"])
                    S.dma("sp", OT[0, :, T0 + c * 256:T0 + (c + 1) * 256].rearrange("(h p) t -> p h t", p=128), ob[:], reads=["f_ob"], writes=[])
                if g == "p":
                    S.dma("sp", D["o_hgrn"][l, sq_].rearrange("d h k v -> k d h v"), Sst[sp][:], reads=[sstn(sp, d_, h_) for d_ in range(2) for h_ in range(4)], writes=[])

            recs = []
            for sq_ in range(nseq):
                for i in range(NTL):
                    recs.append(dict(sq=sq_, sp=sq_ % 2, T0=sq_ * L, i=i, ti=(i, NTL - 1 - i), k=len(recs) % 2))
            partA(recs[0])
            for stg in range(4):
                partB(recs[0], stg)
            for n, rec in enumerate(recs):
                nxt = recs[n + 1] if n + 1 < len(recs) else None
                if rec["i"] == 0:
                    sp = rec["sp"]
                    names = [sstn(sp, d_, h_) for d_ in range(2) for h_ in range(4)]
                    if P:
                        S.dma("sp", Sst[sp][:], D["st_hgrn"][l].rearrange("d h k v -> k d h v"), writes=names)
                    else:
                        S.op("pool", lambda e: e.memset(Sst[sp][:], 0.0), writes=names)
                if nxt is not None:
                    partA(nxt)
                for n_ in range(4):
                    chunk_step(rec, n_)
                    if nxt is not None:
                        partB(nxt, n_)
                finish(rec)
                if rec["i"] == NTL - 1:
                    seq_final(rec["sq"])
            b.nrot = 6
            S.barrier()

    def epilogue(st, z, zn, xt, xn, gco, Xdst, c0, wkn):
        sq, rs, tmp = wkn
        for kc in range(8):
            S.op("act", lambda e: e.activation(sq[:, kc, :], z[:, kc, :], AF.Square), reads=(zn if isinstance(zn, list) else [zn]), writes=["e_sq"])
        pt, pn = b.ps()
        for kc in range(8):
            S.op("pe", lambda e: e.matmul(pt[:], ones_bf[:], sq[:, kc, :], start=(kc == 0), stop=(kc == 7)), reads=["e_sq", "ones_bf"], writes=[pn], pe_acc=True)
        rstd_from_sumsq(pt, pn, rs, "e_rs", 128, 512, 1.0 / DM)
        for kc in range(8):
            S.op("dve", lambda e: e.tensor_tensor(tmp[:], z[:, kc, :], rs[:], ALU.mult), reads=(zn if isinstance(zn, list) else [zn]) + ["e_rs"], writes=["e_tmp"])
            S.op("dve", lambda e: e.scalar_tensor_tensor(xt[:, kc, :], tmp[:], gco[:, kc:kc + 1], xt[:, kc, :], ALU.mult, ALU.add),
                 reads=["e_tmp", "coef", xn], writes=[xn])
        S.dma("sp", Xdst[:, c0:c0 + 512].rearrange("(kc p) t -> p kc t", p=128), xt[:], reads=[xn], writes=[])

    def stage_merge(g, l, Xsrc, Xdst):
        G = GR[g]
        T = G["T"]
        UT, OT = D["UT_" + g], D["OT_" + g]
        with ExitStack() as st:
            Wb = b.sb(st, "Wb", [128, 4, 4, 1024], BF16)
            Wo = b.sb(st, "Wo", [128, 8, 1024], BF16)
            S.dma("pool", Wb[:], D["w_branch"][l].rearrange("n (kc p) c -> p n kc c", p=128), writes=["Wb"])
            S.dma("pool", Wo[:], D["w_out"][l].rearrange("(kc p) c -> p kc c", p=128), writes=["Wo"])
            ot = b.sb(st, "ot", [128, 4, 4, 512], BF16)
            yp = b.sb(st, "yp", [128, 8, 512], BF16)
            gts = [b.sb(st, "gt%d" % i, [128, 512]) for i in range(3)]
            accf = b.sb(st, "accf", [128, 512]); tm2 = b.sb(st, "tm2", [128, 512])
            z = b.sb(st, "z", [128, 8, 512]); xt = b.sb(st, "xt", [128, 8, 512])
            wkn = (b.sb(st, "e_sq", [128, 8, 512], BF16), b.sb(st, "e_rs", [128, 512]), b.sb(st, "e_tmp", [128, 512]))
            ng = 0
            for tt in range(T // 512):
                cs = slice(tt * 512, (tt + 1) * 512)
                S.dma("sp", ot[:], OT[:, :, cs].rearrange("n (kc p) t -> p n kc t", p=128), writes=["ot"])
                S.dma("act", xt[:], Xsrc[:, cs].rearrange("(kc p) t -> p kc t", p=128), writes=["xt"])
                for dmc in range(8):
                    for n in range(4):
                        gt, gtn = gts[ng % 3], "gt%d" % (ng % 3)
                        ng += 1
                        r0 = OFF["gates"] + n * 1024 + dmc * 128
                        S.dma("sp", gt[:], UT[r0:r0 + 128, cs], writes=[gtn])
                        S.op("act", lambda e: e.activation(gt[:], gt[:], AF.Sigmoid), reads=[gtn], writes=[gtn])
                        pt, pn = b.ps()
                        for kc in range(4):
                            S.op("pe", lambda e: e.matmul(pt[:], Wb[:, n, kc, dmc * 128:(dmc + 1) * 128], ot[:, n, kc, :], start=(kc == 0), stop=(kc == 3)),
                                 reads=["Wb", "ot"], writes=[pn], pe_acc=True)
                        if n == 0:
                            S.op("dve", lambda e: e.tensor_tensor(accf[:], pt[:], gt[:], ALU.mult), reads=[pn, gtn], writes=["accf"])
                        elif n < 3:
                            S.op("dve", lambda e: e.tensor_tensor(tm2[:], pt[:], gt[:], ALU.mult), reads=[pn, gtn], writes=["tm2"])
                            S.op("dve", lambda e: e.tensor_tensor(accf[:], accf[:], tm2[:], ALU.add), reads=["tm2", "accf"], writes=["accf"])
                        else:
                            S.op("dve", lambda e: e.tensor_tensor(tm2[:], pt[:], gt[:], ALU.mult), reads=[pn, gtn], writes=["tm2"])
                            S.op("dve", lambda e: e.tensor_tensor(yp[:, dmc, :], accf[:], tm2[:], ALU.add), reads=["tm2", "accf"], writes=["yp"])
                for oc in range(8):
                    pt, pn = b.ps()
                    for kc in range(8):
                        S.op("pe", lambda e: e.matmul(pt[:], Wo[:, kc, oc * 128:(oc + 1) * 128], yp[:, kc, :], start=(kc == 0), stop=(kc == 7)),
                             reads=["Wo", "yp"], writes=[pn], pe_acc=True)
                    S.op("act", lambda e: e.copy(z[:, oc, :], pt[:]), reads=[pn], writes=["z"])
                epilogue(st, z, "z", xt, "xt", coef[:, l, G["cond"], 2], Xdst, tt * 512, wkn)
            S.barrier()

    def stage_mlp(g, l, Xsrc, Xdst):
        G = GR[g]
        T = G["T"]
        NT = T // 512
        with ExitStack() as st:
            if T <= 1024:
                h2, hemit = stage_h(st, g, l, Xsrc, 3, 4, lazy=True)
            else:
                h2 = stage_h(st, g, l, Xsrc, 3, 4)
                hemit = [(lambda: None)] * NT
            zall = b.sb(st, "zall", [128, 8, T])
            w1 = [b.sb(st, "w1_%d" % i, [128, 8, 512], BF16) for i in range(2)]
            w2 = [b.sb(st, "w2_%d" % i, [128, 4, 1024], BF16) for i in range(2)]
            hid = [b.sb(st, "hid%d" % i, [128, 4, 512], BF16) for i in range(2)]
            rl = [b.sb(st, "rl%d" % i, [128, 512]) for i in range(2)]
            xt = b.sb(st, "xt", [128, 8, 512])
            wkn = (b.sb(st, "e_sq", [128, 8, 512], BF16), b.sb(st, "e_rs", [128, 512]), b.sb(st, "e_tmp", [128, 512]))
            nh = 0
            nr = 0

            def do_ep(tt):
                cs = slice(tt * 512, (tt + 1) * 512)
                S.dma("act", xt[:], Xsrc[:, cs].rearrange("(kc p) t -> p kc t", p=128), writes=["xt"])
                epilogue(st, zall[:, :, cs], ["z%d_%d" % (tt, oc) for oc in range(8)], xt, "xt", coef[:, l, G["cond"], 5], Xdst, tt * 512, wkn)

            for cg in range(8):
                wa, wan = w1[cg % 2], "w1_%d" % (cg % 2)
                wb_, wbn = w2[cg % 2], "w2_%d" % (cg % 2)
                S.dma("pool", wa[:], D["w_mlp_in"][l, :, cg * 512:(cg + 1) * 512].rearrange("(kc p) c -> p kc c", p=128), writes=[wan])
                S.dma("pool", wb_[:], D["w_mlp_out"][l, cg * 512:(cg + 1) * 512, :].rearrange("(fc p) c -> p fc c", p=128), writes=[wbn])
                if cg == 0:
                    hemit[0]()
                for tt in range(NT):
                    if cg == 0 and tt + 1 < NT:
                        hemit[tt + 1]()
                    cs = slice(tt * 512, (tt + 1) * 512)
                    hd, hdn = hid[nh % 2], "hid%d" % (nh % 2)
                    nh += 1
                    for cc in range(4):
                        pt, pn = b.ps()
                        for kc in range(8):
                            S.op("pe", lambda e: e.matmul(pt[:], wa[:, kc, cc * 128:(cc + 1) * 128], h2[:, kc, cs], start=(kc == 0), stop=(kc == 7)),
                                 reads=[wan, "hT%d" % tt], writes=[pn], pe_acc=(kc > 0))
                        r, rn = rl[nr % 2], "rl%d" % (nr % 2)
                        nr += 1
                        S.op("act", lambda e: e.activation(r[:], pt[:], AF.Relu), reads=[pn], writes=[rn])
                        S.op("dve", lambda e: e.tensor_tensor(hd[:, cc, :], r[:], r[:], ALU.mult), reads=[rn], writes=[hdn])
                    for oc in range(8):
                        pt, pn = b.ps()
                        for fc in range(4):
                            S.op("pe", lambda e: e.matmul(pt[:], wb_[:, fc, oc * 128:(oc + 1) * 128], hd[:, fc, :], start=(fc == 0), stop=(fc == 3)),
                                 reads=[wbn, hdn], writes=[pn], pe_acc=(fc > 0))
                        zn = "z%d_%d" % (tt, oc)
                        if cg == 0:
                            S.op("act", lambda e: e.copy(zall[:, oc, cs], pt[:]), reads=[pn], writes=[zn])
                        else:
                            S.op("dve", lambda e: e.tensor_tensor(zall[:, oc, cs], zall[:, oc, cs], pt[:], ALU.add), reads=[pn, zn], writes=[zn])
                    if cg == 7 and tt >= 1:
                        do_ep(tt - 1)
            do_ep(NT - 1)
            S.barrier()

    for g in ("s", "p"):
        X = D["xT_" + g]
        for l in range(DEPTH):
            if "proj" in STAGES:
                with ExitStack() as st:
                    hT, hemit = stage_h(st, g, l, X, 0, 1, lazy=True)
                    stage_proj(g, l, hT, hemit)
            if "hgrn" in STAGES:
                stage_hgrn(g, l)
            if "swa" in STAGES:
                stage_gqa_like(g, l, "swa")
            if "mla" in STAGES:
                stage_mla(g, l)
            if "gqa" in STAGES:
                stage_gqa_like(g, l, "gqa")
            if "merge" in STAGES:
                stage_merge(g, l, X, D["X1_" + g])
            Xn = D["yT_" + g] if l == DEPTH - 1 else D["X2_%d_%s" % (l, g)]
            if "mlp" in STAGES:
                stage_mlp(g, l, D["X1_" + g], Xn)
            X = Xn
    S.barrier()
    return nc, b


_CACHE = {}


def _consts():
    nf = 16
    t = np.arange(2048)
    row, col = (t // 64).astype(np.float32), (t % 64).astype(np.float32)
    rope = np.zeros((4, 128, 2048), np.float32)
    def fill(ci, si, r0, nf):
        inv = (10000.0 ** (-np.arange(nf, dtype=np.float32) / nf)).astype(np.float32)
        ar = (row[None, :] * inv[:, None]).astype(np.float32)
        ac = (col[None, :] * inv[:, None]).astype(np.float32)
        for k, a in enumerate((ar, ac)):
            b0 = r0 + k * 2 * nf
            rope[ci, b0:b0 + nf] = np.cos(a); rope[ci, b0 + nf:b0 + 2 * nf] = np.cos(a)
            rope[si, b0:b0 + nf] = -np.sin(a); rope[si, b0 + nf:b0 + 2 * nf] = np.sin(a)
    fill(0, 1, 0, 16)
    fill(0, 1, 64, 16)
    fill(2, 3, 64, 8)
    s_ = np.arange(128)[:, None]; t_ = np.arange(128)[None, :]
    same = (s_ // 32) == (t_ // 32)
    cm = np.zeros((6, 128, 128), np.float32)
    cm[0] = np.eye(128)
    cm[1] = same & (s_ <= t_)
    cm[2] = same & (s_ >= t_)
    cm[3] = same & (s_ > t_)
    cm[4] = same & (s_ < t_)
    cm[5][:, 0:4] = (s_ // 32) == np.arange(4)[None, :]
    j = np.arange(128)[:, None]; i = (np.arange(512) % 128)[None, :]
    swm = np.stack([(j >= i), (j <= i)]).astype(np.float32).astype(ml_dtypes.bfloat16)
    return rope, cm, swm


def _perm(nd):
    q = nd // 4
    return np.concatenate([np.arange(q, 2 * q), np.arange(0, q), np.arange(3 * q, 4 * q), np.arange(2 * q, 3 * q)])


def kernel(**inp):
    f = lambda a: np.ascontiguousarray(np.asarray(a, dtype=np.float32))
    I = {k: f(v) for k, v in inp.items()}
    if "prog" not in _CACHE:
        _CACHE["prog"] = build_program()
    nc, b = _CACHE["prog"]
    rope, cm, swm = _consts()
    fm = lambda v, n: f(v.reshape(v.shape[0], n, 128).transpose(0, 2, 1))
    shared = dict(
        w_ada=I["w_ada"], b_adaT=fm(I["b_ada"], 48),
        gains=f(np.stack([fm(I[k], 8) for k in ("norm_mix_pre", "norm_mix_post", "norm_mlp_pre", "norm_mlp_post")], axis=2)),
        w_in=I["w_in"], lbT=f(np.stack([I["hgrn_lb_fwd"], I["hgrn_lb_bwd"]], axis=1)),
        hgrn_normT=fm(I["hgrn_norm"], 4), sink=f(I["swa_sink"][:, None, :]),
        mla_qnT=fm(I["mla_q_norm"], 2), mla_kvn=f(I["mla_kv_norm"][:, :, None]),
        w_uq=I["mla_w_uq"], w_ukv=I["mla_w_ukv"],
        gqa_qn=f(np.tile(np.stack([I["gqa_q_norm"], I["gqa_q_norm"][:, _perm(64)]], axis=2), (1, 2, 1))),
        gqa_kn=f(np.tile(np.stack([I["gqa_k_norm"], I["gqa_k_norm"][:, _perm(64)]], axis=2), (1, 2, 1))),
        w_branch=I["w_branch"], w_out=I["w_out"], w_mlp_in=I["w_mlp_in"], w_mlp_out=I["w_mlp_out"],
        rope64=rope, cmat=cm, swamask=swm,
    )
    wsw = I["mla_w_uq"].reshape(DEPTH, 256, 8, 96).copy()
    wsw[..., 64:96] = wsw[..., 64:96][..., _perm(32)]
    shared["w_uq_sw"] = f(wsw.reshape(DEPTH, 256, 768))
    in_maps = []
    for i in range(NCORE):
        bb = i % 2
        m = dict(shared)
        m["xT_s"] = f(I["x_sample"][bb].T)
        m["xT_p"] = f(I["x_prompt"][4 * i:4 * i + 4].reshape(1024, DM).T)
        cc = np.stack([I["c_ctx"], I["c"][bb]], axis=1)
        m["cT"] = f(cc.reshape(8, 128, 2).transpose(1, 0, 2))
        m["st_hgrn"] = f(I["state_hgrn"][bb])
        m["c_swa_kT"] = f(I["cache_swa_k"][bb].transpose(0, 2, 3, 1))
        m["c_swa_v"] = f(I["cache_swa_v"][bb].reshape(DEPTH, 512, 128))
        m["c_ckvT"] = f(I["cache_mla_ckv"][bb].transpose(0, 2, 1))
        m["c_krT"] = f(I["cache_mla_kr"][bb].transpose(0, 2, 1))
        m["c_gqa_kT"] = f(I["cache_gqa_k"][bb].transpose(0, 2, 3, 1))
        m["c_gqa_v"] = f(I["cache_gqa_v"][bb].reshape(DEPTH, 512, 128))
        in_maps.append(m)
    res = run_bass_kernel_spmd(nc, in_maps, core_ids=list(range(NCORE)))
    R = res.results
    y_prompt = np.concatenate([R[i]["yT_p"].T.reshape(4, 256, DM) for i in range(NCORE)], axis=0)
    y_sample = np.stack([R[0]["yT_s"].T, R[1]["yT_s"].T], axis=0)
    cat = lambda fn: np.ascontiguousarray(np.concatenate([fn(R[i]) for i in range(NCORE)], axis=0).astype(np.float32))
    n_hgrn = cat(lambda r: r["o_hgrn"].transpose(1, 0, 2, 3, 4, 5))
    kT = lambda a: a.reshape(DEPTH, 2, 64, 4, 256).transpose(3, 0, 4, 1, 2)
    vv = lambda a: a.reshape(DEPTH, 4, 256, 2, 64).transpose(1, 0, 2, 3, 4)
    n_swa_k = cat(lambda r: kT(r["o_swa_kT"]))
    n_swa_v = cat(lambda r: vv(r["o_swa_v"]))
    n_ckv = cat(lambda r: r["o_ckvT"].reshape(DEPTH, 128, 4, 256).transpose(2, 0, 3, 1))
    n_kr = cat(lambda r: r["o_krT"].reshape(DEPTH, 32, 4, 256).transpose(2, 0, 3, 1))
    n_gqa_k = cat(lambda r: kT(r["o_gqa_kT"]))
    n_gqa_v = cat(lambda r: vv(r["o_gqa_v"]))
    return (np.ascontiguousarray(y_prompt.astype(np.float32)), np.ascontiguousarray(y_sample.astype(np.float32)),
            n_hgrn, n_swa_k, n_swa_v, n_ckv, n_kr, n_gqa_k, n_gqa_v)
```
